# Optimizing a Trainium2 kernel written in Bass

```python
import math
import jax, jax.numpy as jnp
from jax import lax
import numpy as np

D_MODEL = 1024
BATCH = 4
SEQ = 8192
DEPTH = 1

RWKV_HEAD = 64
RWKV_HEADS = D_MODEL // RWKV_HEAD
RWKV_WIDTH = RWKV_HEADS * RWKV_HEAD
LORA_DECAY = 64
LORA_ICLR = 64
LORA_GATE = 160
GN_EPS = 64e-5
ATTN_HEAD = 64
ATTN_HEADS_PER_GROUP = 8
ATTN_GROUPS = ((128, 1), (512, 4), (2048, 16))
N_ATTN_GROUPS = 3
ATTN_WIDTH = N_ATTN_GROUPS * ATTN_HEADS_PER_GROUP * ATTN_HEAD
ATTN_OUT = ATTN_HEADS_PER_GROUP * ATTN_HEAD
BLK = 128
FFN_HIDDEN = -(-8 * D_MODEL // (3 * 256)) * 256
RMS_EPS = 1e-6
PROJ_SIZES = (RWKV_WIDTH, RWKV_WIDTH, RWKV_WIDTH, ATTN_WIDTH, ATTN_WIDTH, ATTN_WIDTH, D_MODEL, D_MODEL)
PROJ_IN = 3 * RWKV_WIDTH + 3 * ATTN_WIDTH + 2 * D_MODEL

kernel_name = 'hybrid_rwkv7_dilated_alibi_block'


def _rmsnorm(t, g):
    t32 = t.astype(jnp.float32)
    n = t32 * lax.rsqrt(jnp.mean(t32 * t32, axis=-1, keepdims=True) + RMS_EPS)
    return n.astype(t.dtype) * g


def _shift(t):
    return jnp.pad(t, ((0, 0), (1, 0), (0, 0)))[:, :-1]


def _split_cols(t, sizes):
    idx, acc = [], 0
    for s in sizes[:-1]:
        acc += s
        idx.append(acc)
    return jnp.split(t, idx, axis=-1)


def _alibi_slopes(n):
    def pow2(m):
        start = 2.0 ** (-8.0 / m)
        return [start ** (i + 1) for i in range(m)]
    if math.log2(n).is_integer():
        s = pow2(n)
    else:
        p = 2 ** int(math.floor(math.log2(n)))
        s = pow2(p) + pow2(2 * p)[0::2][: n - p]
    return sorted(s, reverse=True)


def _rwkv7_recurrence(r, decay, k, v, a_vec, b_vec):
    B, T, H, N = r.shape

    def step(S, inp):
        r_t, w_t, k_t, v_t, a_t, b_t = inp
        Sa = jnp.einsum('bhij,bhj->bhi', S, a_t)
        S = S * w_t[:, :, None, :] + Sa[..., None] * b_t[:, :, None, :] + v_t[..., None] * k_t[:, :, None, :]
        return S, jnp.einsum('bhij,bhj->bhi', S, r_t)

    xs = tuple(jnp.moveaxis(t, 1, 0) for t in (r, decay, k, v, a_vec, b_vec))
    _, y = lax.scan(step, jnp.zeros((B, H, N, N), jnp.float32), xs)
    return jnp.moveaxis(y, 0, 1)


def _rwkv7_branch(h, p_r, p_k, p_v, mu_rkv, mu_lora, w0, w1, w2, a0, a1, a2, g1, g2, k_k, k_a, r_k, ln_x_w, ln_x_b, w_o):
    B, T, _ = h.shape
    f32 = jnp.float32
    r = p_r + (_shift(p_r) - p_r) * mu_rkv[0]
    k = p_k + (_shift(p_k) - p_k) * mu_rkv[1]
    v = p_v + (_shift(p_v) - p_v) * mu_rkv[2]
    dh = _shift(h) - h
    xw = h + dh * mu_lora[0]
    xa = h + dh * mu_lora[1]
    xg = h + dh * mu_lora[2]
    w_log = -jax.nn.softplus(-(w0 + jnp.tanh(xw @ w1) @ w2)) - 0.5
    decay = jnp.exp(-jnp.exp(w_log.astype(f32)))
    a = jax.nn.sigmoid(a0 + (xa @ a1) @ a2)
    g = jax.nn.sigmoid(xg @ g1) @ g2
    heads = lambda t: t.reshape(B, T, RWKV_HEADS, RWKV_HEAD)
    kk = heads(k * k_k).astype(f32)
    kk = kk / jnp.maximum(jnp.sqrt(jnp.sum(kk * kk, axis=-1, keepdims=True)), 1e-12)
    k = k * (1 + (a - 1) * k_a)
    r_h, k_h, v_h, a_h = heads(r), heads(k), heads(v), heads(a)
    y = _rwkv7_recurrence(r_h.astype(f32), heads(decay), k_h.astype(f32), v_h.astype(f32),
                          -kk, kk * a_h.astype(f32))
    mean = jnp.mean(y, axis=-1, keepdims=True)
    var = jnp.mean(jnp.square(y - mean), axis=-1, keepdims=True)
    y = ((y - mean) * lax.rsqrt(var + GN_EPS)).astype(h.dtype).reshape(B, T, RWKV_WIDTH) * ln_x_w + ln_x_b
    bonus = jnp.sum(r_h * k_h * r_k, axis=-1, keepdims=True) * v_h
    y = (y + bonus.reshape(B, T, RWKV_WIDTH)) * g
    return y @ w_o


def _dilated_window_attention(q, k, v, slopes, window, dilation):
    B, T, H, D = q.shape
    span = window // dilation
    L = T // dilation
    Lp = -(-L // BLK) * BLK
    nb = Lp // BLK

    def blocks(t):
        t = t.reshape(B, L, dilation, H, D)
        t = jnp.pad(t, ((0, 0), (0, Lp - L), (0, 0), (0, 0), (0, 0)))
        return t.reshape(B, nb, BLK, dilation, H, D)

    def band(t):
        prev = jnp.pad(t, ((0, 0), (1, 0), (0, 0), (0, 0), (0, 0), (0, 0)))[:, :-1]
        return jnp.concatenate([prev, t], axis=2)

    qb = blocks(q)
    kc, vc = band(blocks(k)), band(blocks(v))
    s = jnp.einsum('bnqrhd,bnkrhd->bnrhqk', qb, kc, preferred_element_type=jnp.float32) * (D ** -0.5)
    iq = jnp.arange(BLK)[:, None]
    jk = jnp.arange(2 * BLK)[None, :]
    delta = iq - jk + BLK
    nidx = jnp.arange(nb)[:, None, None]
    valid = ((delta >= 0) & (delta <= span))[None] & ((nidx > 0) | (jk[None] >= BLK))
    dist = (delta * dilation).astype(jnp.float32)
    s = s - slopes[:, None, None] * dist[None]
    s = jnp.where(valid[None, :, None, None], s, -jnp.inf)
    m = jnp.max(s, axis=-1, keepdims=True)
    p = jnp.exp(s - m)
    den = jnp.sum(p, axis=-1, keepdims=True)
    o = jnp.einsum('bnrhqk,bnkrhd->bnqrhd', p / den, vc.astype(jnp.float32))
    lse = jnp.transpose((m + jnp.log(den))[..., 0], (0, 1, 4, 2, 3))
    o = o.reshape(B, Lp, dilation, H, D)[:, :L].reshape(B, T, H, D)
    lse = lse.reshape(B, Lp, dilation, H)[:, :L].reshape(B, T, H)
    return o, lse


def setup_inputs(seed: int = 0) -> dict:
    key = jax.random.key(seed)
    ks = jax.random.split(key, 32)
    f32 = jnp.float32
    nrm = lambda k, shape, scale: jax.random.normal(k, shape, f32) * scale
    return {
        'x': nrm(ks[0], (BATCH, SEQ, D_MODEL), 1.0),
        'c': nrm(ks[1], (BATCH, D_MODEL), 1.0),
        'w_mod': nrm(ks[2], (DEPTH, D_MODEL, 6 * D_MODEL), 0.5 * D_MODEL ** -0.5),
        'b_mod': nrm(ks[3], (DEPTH, 6 * D_MODEL), 0.01),
        'g_pre_mix': 1.0 + nrm(ks[4], (DEPTH, D_MODEL), 0.05),
        'g_post_mix': 1.0 + nrm(ks[5], (DEPTH, D_MODEL), 0.05),
        'g_pre_ffn': 1.0 + nrm(ks[6], (DEPTH, D_MODEL), 0.05),
        'g_post_ffn': 1.0 + nrm(ks[7], (DEPTH, D_MODEL), 0.05),
        'w_in': nrm(ks[8], (DEPTH, D_MODEL, PROJ_IN), D_MODEL ** -0.5),
        'mu_rkv': jax.random.uniform(ks[9], (DEPTH, 3, RWKV_WIDTH), f32),
        'mu_lora': jax.random.uniform(ks[10], (DEPTH, 3, D_MODEL), f32),
        'w0': jax.random.uniform(ks[11], (DEPTH, RWKV_WIDTH), f32, -4.0, 1.0),
        'w1': nrm(ks[12], (DEPTH, D_MODEL, LORA_DECAY), D_MODEL ** -0.5),
        'w2': nrm(ks[13], (DEPTH, LORA_DECAY, RWKV_WIDTH), 0.5 * LORA_DECAY ** -0.5),
        'a0': nrm(ks[14], (DEPTH, RWKV_WIDTH), 0.1),
        'a1': nrm(ks[15], (DEPTH, D_MODEL, LORA_ICLR), D_MODEL ** -0.5),
        'a2': nrm(ks[16], (DEPTH, LORA_ICLR, RWKV_WIDTH), LORA_ICLR ** -0.5),
        'g1': nrm(ks[17], (DEPTH, D_MODEL, LORA_GATE), D_MODEL ** -0.5),
        'g2': nrm(ks[18], (DEPTH, LORA_GATE, RWKV_WIDTH), LORA_GATE ** -0.5),
        'k_k': 0.85 + nrm(ks[19], (DEPTH, RWKV_WIDTH), 0.05),
        'k_a': 1.0 + nrm(ks[20], (DEPTH, RWKV_WIDTH), 0.05),
        'r_k': nrm(ks[21], (DEPTH, RWKV_HEADS, RWKV_HEAD), 0.1),
        'ln_x_w': 1.0 + nrm(ks[22], (DEPTH, RWKV_WIDTH), 0.05),
        'ln_x_b': nrm(ks[23], (DEPTH, RWKV_WIDTH), 0.01),
        'w_o_rwkv': nrm(ks[24], (DEPTH, RWKV_WIDTH, D_MODEL), RWKV_WIDTH ** -0.5),
        'w_o_attn': nrm(ks[25], (DEPTH, ATTN_OUT, D_MODEL), ATTN_OUT ** -0.5),
        'w_out': nrm(ks[26], (DEPTH, D_MODEL, D_MODEL), D_MODEL ** -0.5),
        'w_ffn_in': nrm(ks[27], (DEPTH, D_MODEL, 2 * FFN_HIDDEN), D_MODEL ** -0.5),
        'w_ffn_out': nrm(ks[28], (DEPTH, FFN_HIDDEN, D_MODEL), FFN_HIDDEN ** -0.5),
    }


def reference(x, c, w_mod, b_mod, g_pre_mix, g_post_mix, g_pre_ffn, g_post_ffn, w_in, mu_rkv, mu_lora,
              w0, w1, w2, a0, a1, a2, g1, g2, k_k, k_a, r_k, ln_x_w, ln_x_b, w_o_rwkv, w_o_attn, w_out,
              w_ffn_in, w_ffn_out):
    B, T, _ = x.shape
    slopes = jnp.asarray(_alibi_slopes(N_ATTN_GROUPS * ATTN_HEADS_PER_GROUP), jnp.float32)
    slopes = slopes.reshape(N_ATTN_GROUPS, ATTN_HEADS_PER_GROUP)
    for l in range(DEPTH):
        mod = (c @ w_mod[l] + b_mod[l])[:, None, :]
        sh_m, sc_m, gt_m, sh_f, sc_f, gt_f = jnp.split(mod, 6, axis=-1)

        h = _rmsnorm(x, g_pre_mix[l]) * (1 + sc_m) + sh_m
        p_r, p_k, p_v, q_att, k_att, v_att, z_a, z_b = _split_cols(h @ w_in[l], PROJ_SIZES)

        y_rwkv = _rwkv7_branch(h, p_r, p_k, p_v, mu_rkv[l], mu_lora[l], w0[l], w1[l], w2[l], a0[l], a1[l],
                               a2[l], g1[l], g2[l], k_k[l], k_a[l], r_k[l], ln_x_w[l], ln_x_b[l], w_o_rwkv[l])

        grp = lambda t: t.reshape(B, T, N_ATTN_GROUPS, ATTN_HEADS_PER_GROUP, ATTN_HEAD)
        qg, kg, vg = grp(q_att), grp(k_att), grp(v_att)
        outs, lses = [], []
        for gi, (window, dilation) in enumerate(ATTN_GROUPS):
            o, lse = _dilated_window_attention(qg[:, :, gi], kg[:, :, gi], vg[:, :, gi], slopes[gi], window, dilation)
            outs.append(o)
            lses.append(lse)
        wts = jax.nn.softmax(jnp.stack(lses), axis=0)
        o = jnp.einsum('gbth,gbthd->bthd', wts, jnp.stack(outs))
        y_attn = o.reshape(B, T, ATTN_OUT).astype(x.dtype) @ w_o_attn[l]

        mixed = (jax.nn.sigmoid(z_a) * y_rwkv + jax.nn.sigmoid(z_b) * y_attn) @ w_out[l]
        x = x + gt_m * _rmsnorm(mixed, g_post_mix[l])

        h = _rmsnorm(x, g_pre_ffn[l]) * (1 + sc_f) + sh_f
        u_gate, u_up = jnp.split(h @ w_ffn_in[l], 2, axis=-1)
        y = (jax.nn.silu(u_gate) * u_up) @ w_ffn_out[l]
        x = x + gt_f * _rmsnorm(y, g_post_ffn[l])
    return x
```

```python
import math
from contextlib import ExitStack

import numpy as np
import concourse.bass as bass
import concourse.mybir as mybir
from concourse.bass_utils import run_bass_kernel_spmd

F32 = mybir.dt.float32
BF16 = mybir.dt.bfloat16
AF = mybir.ActivationFunctionType
ALU = mybir.AluOpType
AX = mybir.AxisListType

ENGS = ("tensor", "vector", "scalar", "gpsimd", "sync")

T = 8192
D = 1024
TT = 256
NT = T // TT
PB0 = NT // 2
NSUB = TT // 128
FH = 2816
C0 = math.exp(-0.5)
GN_EPS = 64e-5
RMS_EPS = 1e-6
NSLOT = 3


class Buf:
    def __init__(self, name, t):
        self.name = name
        self.t = t
        self.writer = None
        self.readers = []
        self.dsem = None
        self.dcnt = 0

    def __getitem__(self, idx):
        return View(self, self.t[idx])

    def v(self, ap):
        return View(self, ap)


class View:
    def __init__(self, buf, ap):
        self.buf = buf
        self.ap = ap

    def __getitem__(self, idx):
        return View(self.buf, self.ap[idx])

    def re(self, pat, **kw):
        return View(self.buf, self.ap.rearrange(pat, **kw))

    def bc(self, axis, shape):
        return View(self.buf, self.ap.unsqueeze(axis).to_broadcast(list(shape)))


def _unw(x):
    return x.ap if isinstance(x, View) else x


class Sched:
    def __init__(self, nc, stack):
        self.nc = nc
        self.stack = stack
        self.q = {e: [] for e in ENGS}
        self.waited = {e: {} for e in ENGS}
        self.dma_sems = []

    def sb(self, name, shape, dt):
        t = self.stack.enter_context(self.nc.sbuf_tensor("s_" + name, list(shape), dt))
        return Buf(name, t)

    def ps(self, name, shape, dt=F32):
        t = self.stack.enter_context(self.nc.psum_tensor("p_" + name, list(shape), dt))
        return Buf(name, t)

    def dram(self, name, shape, dt, kind):
        t = self.nc.dram_tensor(name, list(shape), dt, kind=kind).ap()
        return Buf(name, t)

    def _deps(self, eng, reads, writes):
        deps = {}

        def add(tok):
            if tok is None:
                return
            k, v = tok
            if deps.get(k, 0) < v:
                deps[k] = v

        for b in reads:
            add(b.writer)
        for b in writes:
            add(b.writer)
            for r in b.readers:
                add(r)
        waits = []
        for k, v in deps.items():
            if k == "tensor" and eng == "tensor":
                continue
            if self.waited[eng].get(k, 0) >= v:
                continue
            self.waited[eng][k] = v
            waits.append((k, v))
            if isinstance(k, str):
                self.q[k][v - 1][2] = True
        return waits

    def _commit(self, tok, reads, writes):
        for b in writes:
            b.writer = tok
            b.readers = []
        for b in reads:
            if b in writes:
                continue
            b.readers.append(tok)
            if len(b.readers) > 48:
                d = {}
                for k, v in b.readers:
                    if d.get(k, 0) < v:
                        d[k] = v
                b.readers = list(d.items())

    def op(self, eng, meth, **kw):
        writes, reads = [], []
        for k, v in kw.items():
            if isinstance(v, View):
                if k in ("out", "accum_out", "ap"):
                    if v.buf not in writes:
                        writes.append(v.buf)
                else:
                    if v.buf not in reads:
                        reads.append(v.buf)
        waits = self._deps(eng, reads, writes)
        args = {k: _unw(v) for k, v in kw.items()}
        fn = lambda e, m=meth, a=args: getattr(e, m)(**a)
        self.q[eng].append([waits, fn, False, None])
        tok = (eng, len(self.q[eng]))
        self._commit(tok, reads, writes)
        return tok

    def dma(self, eng, out, in_, **kw):
        sb = out.buf
        if sb.dsem is None:
            sb.dsem = ("dma", len(self.dma_sems))
            self.dma_sems.append(sb.name)
        waits = self._deps(eng, [in_.buf], [out.buf])
        sb.dcnt += 16
        tok = (sb.dsem, sb.dcnt)
        a = dict(out=out.ap, in_=in_.ap, **kw)
        fn = lambda e, a=a: e.dma_start(**a)
        self.q[eng].append([waits, fn, False, sb.dsem])
        self._commit(tok, [in_.buf], [out.buf])
        return tok

    def finish(self, final_bufs):
        waits = self._deps("sync", final_bufs, [])
        self.q["sync"].append([waits, None, False, None])

    def emit(self):
        nc = self.nc
        st = self.stack
        esem = {e: st.enter_context(nc.semaphore("es_" + e)) for e in ENGS}
        dsem = [st.enter_context(nc.semaphore("ds%d" % i)) for i in range(len(self.dma_sems))]
        cum = {}
        for e in ENGS:
            c = 0
            arr = []
            for it in self.q[e]:
                if it[2]:
                    c += 1
                arr.append(c)
            cum[e] = arr

        def semval(k, v):
            if isinstance(k, str):
                return esem[k], cum[k][v - 1]
            return dsem[k[1]], v

        block = st.enter_context(nc.Block())

        def run(e, eng):
            for waits, fn, sig, dk in self.q[e]:
                for k, v in waits:
                    s, val = semval(k, v)
                    eng.wait_ge(s, val)
                if fn is None:
                    continue
                ins = fn(eng)
                if dk is not None:
                    ins.then_inc(dsem[dk[1]], 16)
                elif sig:
                    ins.then_inc(esem[e], 1)

        @block.tensor
        def _(eng):
            run("tensor", eng)

        @block.vector
        def _(eng):
            run("vector", eng)

        @block.scalar
        def _(eng):
            run("scalar", eng)

        @block.gpsimd
        def _(eng):
            run("gpsimd", eng)

        @block.sync
        def _(eng):
            run("sync", eng)


def _alibi_slopes(n):
    def pow2(m):
        start = 2.0 ** (-8.0 / m)
        return [start ** (i + 1) for i in range(m)]
    if math.log2(n).is_integer():
        s = pow2(n)
    else:
        p = 2 ** int(math.floor(math.log2(n)))
        s = pow2(p) + pow2(2 * p)[0::2][: n - p]
    return sorted(s, reverse=True)


(PV_SHM, PV_SCM, PV_GTM, PV_SHF, PV_SCF, PV_GTF, PV_GPM, PV_GQM, PV_GPF, PV_GQF,
 PV_MUR, PV_MUK, PV_MUV, PV_MUW, PV_MUA, PV_MUG, PV_W0, PV_A0, PV_KK, PV_KA, PV_RK,
 PV_LNW, PV_LNB, PV_C) = range(24)
NPV = 24

CB_ID = 0
CB_ONESBD = 128
CB_ONES = 256
CB_MT4 = 320
CB_ML4 = 832
CB_E1 = 1344
CB_EA2 = CB_E1 + 8 * 256
CB_EB2 = CB_EA2 + 2 * 8 * 64
CB_EA3 = CB_EB2 + 8 * 64
CB_EB3 = CB_EA3 + 8 * 8 * 16
NCB = CB_EB3 + 8 * 16
CF_MSK = 0
CF_VM = 256
CF_EPS = CF_VM + 128
CF_IDF = CF_EPS + 4
NCF = CF_IDF + 128

CH_RW = 0
CH_AT = 8
CH_PB = 17
CH_WOUT = 25
CH_FF = 27
CH_FO = 38
NCH = 44


def _host_consts():
    sl = np.asarray(_alibi_slopes(24), np.float64).reshape(3, 8)
    cb = np.zeros((128, NCB), np.float32)
    p = np.arange(128)
    cb[:, CB_ID:CB_ID + 128] = np.eye(128)
    cb[:, CB_ONESBD:CB_ONESBD + 128] = (p[:, None] // 64 == p[None, :] // 64)
    cb[:, CB_ONES:CB_ONES + 64] = 1.0
    same = (p[:, None] // 64 == p[None, :] // 64)
    su = same & (p[:, None] < p[None, :])
    iu = same & (p[:, None] <= p[None, :])
    slo = same & (p[:, None] > p[None, :])
    cb[:, CB_MT4:CB_MT4 + 512] = np.concatenate([su, iu, su, iu], 1)
    cb[:, CB_ML4:CB_ML4 + 512] = np.concatenate([slo] * 4, 1)
    k = p[:, None].astype(np.float64)
    q = p[None, :].astype(np.float64)
    for h in range(8):
        dpv = q - k + 128
        e_prev = np.where(dpv <= 128, np.exp(-sl[0, h] * dpv), 0.0)
        dcu = q - k
        e_cur = np.where(dcu >= 0, np.exp(-sl[0, h] * np.maximum(dcu, 0)), 0.0)
        cb[:, CB_E1 + h * 256: CB_E1 + h * 256 + 128] = e_prev
        cb[:, CB_E1 + h * 256 + 128: CB_E1 + h * 256 + 256] = e_cur
    i64 = np.arange(64)[None, :].astype(np.float64)
    for rot in range(2):
        for h in range(8):
            j = p // 64
            pp = (p % 64).astype(np.float64)
            a = ((rot - j - 1) % 2) + 1
            dl = 64.0 * a[:, None] + i64 - pp[:, None]
            e = np.where(dl <= 128, np.exp(-sl[1, h] * 4.0 * dl), 0.0)
            o = CB_EA2 + (rot * 8 + h) * 64
            cb[:, o:o + 64] = e
    for h in range(8):
        kk = np.arange(64)[:, None].astype(np.float64)
        dl = i64 - kk
        e = np.where(dl >= 0, np.exp(-sl[1, h] * 4.0 * np.maximum(dl, 0)), 0.0)
        o = CB_EB2 + h * 64
        cb[0:64, o:o + 64] = e
    i16 = np.arange(16)[None, :].astype(np.float64)
    for rot in range(8):
        for h in range(8):
            j = p // 16
            pp = (p % 16).astype(np.float64)
            a = ((rot - j - 1) % 8) + 1
            dl = 16.0 * a[:, None] + i16 - pp[:, None]
            e = np.where(dl <= 128, np.exp(-sl[2, h] * 16.0 * dl), 0.0)
            o = CB_EA3 + (rot * 8 + h) * 16
            cb[:, o:o + 16] = e
    for h in range(8):
        kk = np.arange(16)[:, None].astype(np.float64)
        dl = i16 - kk
        e = np.where(dl >= 0, np.exp(-sl[2, h] * 16.0 * np.maximum(dl, 0)), 0.0)
        o = CB_EB3 + h * 16
        cb[0:16, o:o + 16] = e
    return cb


def _host_cf(hh):
    cf = np.zeros((128, NCF), np.float32)
    m = np.ones((128, 256), np.float32)
    m[:, 0::64] = 0.0
    cf[:, CF_MSK:CF_MSK + 256] = m
    valid = lambda t: 0.0 if t < 0 else (1.0 if (hh == 1 or t >= PB0) else 0.0)
    p = np.arange(128)
    for t in range(NT):
        cf[:, CF_VM + t] = valid(t)
        cf[:, CF_VM + 32 + t] = valid(t - 1)
        j = p // 64
        a = ((t - j - 1) % 2) + 1
        cf[:, CF_VM + 64 + t] = [valid(t - aa) for aa in a]
        j = p // 16
        a = ((t - j - 1) % 8) + 1
        cf[:, CF_VM + 96 + t] = [valid(t - aa) for aa in a]
    cf[:, CF_EPS] = RMS_EPS
    cf[:, CF_EPS + 1] = GN_EPS
    cf[:, CF_IDF:CF_IDF + 128] = np.eye(128)
    return cf


def _fm(v):
    return np.ascontiguousarray(v.reshape(8, 128).T)


def _wchunk(w, cols):
    return w[:, cols].reshape(8, 128, -1).transpose(1, 0, 2)


class _Stop(Exception):
    pass


def build(nt=NT, dbg=None, dbg_tile=0, dbg_c=0, stop=None):
    nc = bass.Bass("TRN2", target_bir_lowering=False)
    with ExitStack() as st:
        S = Sched(nc, st)
        finals = []

        def CK(name):
            if stop == name:
                raise _Stop()

        def DBG(name, view, m=None, c=None):
            if not dbg or name not in dbg:
                return
            if m is not None and m != dbg_tile:
                return
            if c is not None and c != dbg_c:
                return
            shp = list(view.ap.shape)
            dd = S.dram("dbg_" + name, shp, view.ap.dtype, "ExternalOutput")
            S.dma("gpsimd", out=dd[:], in_=view)
            finals.append(dd)
        xv = S.dram("xv", [T, D], F32, "ExternalInput")
        pfm_d = S.dram("pfm", [128, NPV * 8], F32, "ExternalInput")
        wmod_d = S.dram("wmod", [24, 128, 2048], F32, "ExternalInput")
        wsrc = S.dram("wsrc", [NCH, 128, 4096], F32, "ExternalInput")
        l1_d = S.dram("l1", [128, 8 * 288], F32, "ExternalInput")
        l2_d = S.dram("l2", [128, 3 * 1024], F32, "ExternalInput")
        cb_d = S.dram("cbt", [128, NCB], F32, "ExternalInput")
        cf_d = S.dram("cft", [128, NCF], F32, "ExternalInput")
        y_d = S.dram("y", [T // 2, D], F32, "ExternalOutput")
        wscr = S.dram("wscr", [NCH, 128, 4096], BF16, "Internal")

        cb = S.sb("cb", [128, NCB], BF16)
        cf = S.sb("cf", [128, NCF], F32)
        pf = S.sb("pf", [128, NPV * 8], F32)
        pd = S.sb("pd", [128, 12 * 8], F32)
        gmb = S.sb("gmb", [128, 1024], BF16)
        gfb = S.sb("gfb", [128, 1024], BF16)
        l1a = S.sb("l1a", [128, 8, 288], BF16)
        l1b = S.sb("l1b", [128, 8, 288], BF16)
        l2 = S.sb("l2", [128, 3, 1024], BF16)
        ring = [S.sb("ring%d" % i, [128, 4096], BF16) for i in range(NSLOT)]
        xt1 = S.sb("xt", [128, NSUB, 1024], F32)
        xt = [xt1, xt1]
        nb = S.sb("nb", [128, NSUB, 1024], BF16)
        junk = nb[:, 0, :]
        st4 = S.sb("st4", [128, 16], F32)
        hT = S.sb("hT", [128, 8, TT + 1], BF16)
        h2T = S.sb("h2T", [128, 8, TT], BF16)
        mixT = h2T
        Zf = S.sb("Zf", [128, 8, 64], F32)
        Zb = S.sb("Zb", [128, 8, 2, 64], BF16)
        hal = S.sb("hal", [128, 8, 3], F32)
        tp = [S.sb("tp%d" % i, [128, TT + 1], F32) for i in range(12)]
        tv = lambda i: tp[i][:, 0:TT]
        pj = [tp[0], tp[0], tp[0]]
        tmpd = tv(1)
        rkv = [tv(2), tv(3), tv(4)]
        sw = tv(5); asig = tv(6); gg = tv(7); cs = tv(0); cm = tv(1)
        Ep = tv(8); En = tv(9); Em = tv(10); rinv = tv(1); kkb = tv(11); ff = tv(0)
        kmod = tv(5); bv = tv(1); bon = tv(6); yln = tv(9); ysq = tv(10)
        sqb = S.sb("sqb", [128, TT], BF16)
        AR = S.sb("AR", [128, NSUB, 2, 128], BF16)
        Bt = S.sb("Bt", [128, TT], BF16)
        Kt = S.sb("Kt", [128, TT], BF16)
        vbf = S.sb("vbf", [128, TT], BF16)
        Bpad = S.sb("Bpad", [128, NSUB, 2, 128], BF16)
        Kpad = S.sb("Kpad", [128, NSUB, 2, 128], BF16)
        Vtm = S.sb("Vtm", [128, NSUB, 128], BF16)
        AM = S.sb("AM", [128, 4, 512], BF16)
        L0 = S.sb("L0", [128, 4, 128], BF16)
        LP = [S.sb("LP%d" % i, [128, 4, 128], BF16) for i in range(2)]
        LT = [S.sb("LT%d" % i, [128, 4, 128], BF16) for i in range(2)]
        SS = [S.sb("SS%d" % i, [128, 4, 128], BF16) for i in range(2)]
        Xb = S.sb("Xb", [128, 128], BF16)
        Ub = S.sb("Ub", [128, 128], BF16)
        ztmp = S.sb("ztmp", [128, 64], F32)
        Ytm = S.sb("Ytm", [128, NSUB, 128], F32)
        ynb = S.sb("ynb", [128, NSUB, 128], BF16)
        gst = S.sb("gst", [128, 32], F32)
        lw = S.sb("lw", [128, TT], BF16)
        lga = S.sb("lga", [128, TT], BF16)
        lgb = S.sb("lgb", [32, TT], BF16)
        yfin = S.sb("yfin", [128, 8, TT], BF16)
        Qa = [S.sb("Qa%d" % g, [128, 4, TT], BF16) for g in range(3)]
        K1 = S.sb("K1", [128, 4, 128 + TT], BF16)
        V1 = S.sb("V1", [128, 3, 512], BF16)
        K2c = S.sb("K2c", [128, 4, TT], BF16)
        K2r = S.sb("K2r", [128, 4, 4, 128], BF16)
        V2c = S.sb("V2c", [64, 4, 128], BF16)
        V2r = S.sb("V2r", [128, 4, 512], BF16)
        K3c = S.sb("K3c", [128, 4, TT], BF16)
        K3r = S.sb("K3r", [128, 4, 16, 128], BF16)
        V3c = S.sb("V3c", [16, 16, 128], BF16)
        V3r = S.sb("V3r", [128, 16, 512], BF16)
        VF = S.sb("VF", [128, 4, TT], BF16)
        pe = S.sb("pe", [128, 512], BF16)
        pp_ = S.sb("pp", [128, 512], BF16)
        peb = S.sb("peb", [64, 256], BF16)
        ppb = S.sb("ppb", [64, 256], BF16)
        accO = S.sb("accO", [64, TT], F32)
        accD = S.sb("accD", [64, TT], F32)
        oT = S.sb("oT", [64, 8, TT], BF16)
        sa = tv(2); sbb = tv(3); sg = tv(4); utmp = tv(10)
        actT = S.sb("actT", [128, 8, TT], BF16)
        VF2 = S.sb("VF2", [128, 4, TT], BF16)
        wst = xt1[:].re("p s c -> p (s c)")

        pA = S.ps("pA", [128, 512])
        pB = S.ps("pB", [128, 512])
        pS = S.ps("pS", [128, 1024])
        pO = S.ps("pO", [128, 512])
        pD = S.ps("pD", [128, 512])
        pT = S.ps("pT", [128, 1024], BF16)
        pC = S.ps("pC", [128, 512])
        pab = [pA, pB]
        pab_i = [0]

        def nextp():
            pab_i[0] ^= 1
            return pab[pab_i[0]]

        mm = lambda **kw: S.op("tensor", "matmul", **kw)
        tr = lambda **kw: S.op("tensor", "transpose", **kw)
        act = lambda **kw: S.op("scalar", "activation", **kw)
        vec = lambda m, **kw: S.op("vector", m, **kw)
        gps = lambda m, **kw: S.op("gpsimd", m, **kw)

        ident = cb[:, CB_ID:CB_ID + 128]
        identf = cf[:, CF_IDF:CF_IDF + 128]
        onesbd = cb[:, CB_ONESBD:CB_ONESBD + 128]
        eps_r = cf[:, CF_EPS:CF_EPS + 1]
        eps_g = cf[:, CF_EPS + 1:CF_EPS + 2]
        zero_c = cf[:, CF_EPS + 2:CF_EPS + 3]

        def pv(i, kc):
            return pf[:, i * 8 + kc: i * 8 + kc + 1]

        def pdv(i, kc):
            return pd[:, i * 8 + kc: i * 8 + kc + 1]
        PD_A1, PD_A2, PD_GM, PD_GF, PD_OMK, PD_OMR, PD_OMKm, PD_OMV = range(8)

        try:
            S.dma("gpsimd", out=cb[:, :], in_=cb_d[:, :])
            S.dma("sync", out=cf[:, :], in_=cf_d[:, :])
            S.dma("sync", out=pf[:, :], in_=pfm_d[:, :])
            for i in range(NCH):
                S.dma("gpsimd", out=wscr[i], in_=wsrc[i])
            CK('dma0')
            for b_ in (Zf, Zb, hal, Bpad, Kpad, K1, K2r, V2r, K3r, V3r, V1, hT, Xb, Ub, Vtm):
                gps("memset", ap=b_[:], constant=0.0)
            CK('memset')
            for half in range(2):
                S.dma("sync", out=wst[:, 0:4 * 288], in_=l1_d[:, half * 4 * 288:(half + 1) * 4 * 288])
                w1v = wst[:, 0:4 * 288].re("p (k c) -> p k c", c=288)
                for k4 in range(4):
                    kc = half * 4 + k4
                    for (lo, hi, mui) in ((0, 64, PV_MUW), (64, 128, PV_MUA), (128, 288, PV_MUG)):
                        vec("tensor_scalar", out=l1b[:, kc, lo:hi], in0=w1v[:, k4, lo:hi], scalar1=pv(mui, kc),
                            scalar2=None, op0=ALU.mult)
                        vec("tensor_tensor", out=l1a[:, kc, lo:hi], in0=w1v[:, k4, lo:hi], in1=l1b[:, kc, lo:hi],
                            op=ALU.subtract)
            for half in range(2):
                S.dma("sync", out=wst[:, 0:1536], in_=l2_d[:, half * 1536:(half + 1) * 1536])
                vec("tensor_copy", out=l2[:].re("p a c -> p (a c)")[:, half * 1536:(half + 1) * 1536], in_=wst[:, 0:1536])
            CK('lora0')
            for j in range(24):
                S.dma("sync", out=wst[:, 0:2048], in_=wmod_d[j])
                wv = wst[:, 0:2048].re("p (k c) -> p k c", c=256)
                for cc in range(2):
                    col = j * 2 + cc
                    for kc in range(8):
                        mm(out=pA[:, col:col + 1], lhsT=wv[:, kc, cc * 128:(cc + 1) * 128], rhs=pv(PV_C, kc),
                           start=(kc == 0), stop=(kc == 7))
            modf = S.sb("modf", [128, 48], F32)
            vec("tensor_tensor", out=modf[:, :], in0=pA[:, 0:48], in1=pf[:, 0:48], op=ALU.add)
            for kc in range(8):
                vec("scalar_tensor_tensor", out=pdv(PD_A1, kc), in0=modf[:, 8 + kc:9 + kc], scalar=1.0,
                    in1=pv(PV_GPM, kc), op0=ALU.add, op1=ALU.mult)
                vec("scalar_tensor_tensor", out=pdv(PD_A2, kc), in0=modf[:, 32 + kc:33 + kc], scalar=1.0,
                    in1=pv(PV_GPF, kc), op0=ALU.add, op1=ALU.mult)
                vec("tensor_tensor", out=pdv(PD_GM, kc), in0=modf[:, 16 + kc:17 + kc], in1=pv(PV_GQM, kc), op=ALU.mult)
                vec("tensor_tensor", out=pdv(PD_GF, kc), in0=modf[:, 40 + kc:41 + kc], in1=pv(PV_GQF, kc), op=ALU.mult)
                vec("tensor_scalar", out=pdv(PD_OMK, kc), in0=pv(PV_KA, kc), scalar1=-1.0, scalar2=1.0,
                    op0=ALU.mult, op1=ALU.add)
            dg = S.sb("dg", [128, 128], F32)
            onesf = S.sb("onesf", [128, 128], F32)
            gps("memset", ap=onesf[:, :], constant=1.0)
            for (pdi, dst) in ((PD_GM, gmb), (PD_GF, gfb)):
                for kc in range(8):
                    vec("tensor_scalar", out=dg[:, :], in0=identf, scalar1=pdv(pdi, kc), scalar2=None, op0=ALU.mult)
                    pz = nextp()
                    mm(out=pz[:, 0:128], lhsT=onesf[:, :], rhs=dg[:, :], start=True, stop=True)
                    act(out=dst[:, kc * 128:(kc + 1) * 128], in_=pz[:, 0:128], func=AF.Copy)

            CK('startup')
            ring_i = [0]

            def wload(ch):
                s = ring[ring_i[0] % NSLOT]
                ring_i[0] += 1
                S.dma("sync", out=s[:, :], in_=wscr[ch])
                return s

            def rms_rstd(src3, dst_cols, nsub=NSUB):
                for sub in range(nsub):
                    act(out=junk[:, :], in_=src3[:, sub, :], func=AF.Square,
                        accum_out=st4[:, 8 + sub:9 + sub])
                act(out=st4[:, 12:12 + nsub], in_=st4[:, 8:8 + nsub], func=AF.Sqrt, bias=eps_r, scale=1.0 / D)
                vec("reciprocal", out=st4[:, dst_cols:dst_cols + nsub], in_=st4[:, 12:12 + nsub])

            def norm_transpose(xsrc, rcol, dstT, a_idx, b_view_fn, halo):
                for sub in range(NSUB):
                    vec("tensor_scalar", out=nb[:, sub, :], in0=xsrc[:, sub, :], scalar1=st4[:, rcol + sub:rcol + sub + 1],
                        scalar2=None, op0=ALU.mult)
                for kc in range(8):
                    for sub in range(NSUB):
                        tr(out=pT[:, sub * 128:(sub + 1) * 128], in_=nb[:, sub, kc * 128:(kc + 1) * 128], identity=ident)
                    act(out=dstT[:, kc, halo:halo + TT], in_=pT[:, 0:TT], func=AF.Identity,
                        scale=pdv(a_idx, kc), bias=b_view_fn(kc))

            for m in range(nt):
                phaseB = m >= PB0
                xm = xt[m % 2]
                vcur = cf[:, CF_VM + m:CF_VM + m + 1]
                vprev = cf[:, CF_VM + 32 + m:CF_VM + 33 + m]
                vr2 = cf[:, CF_VM + 64 + m:CF_VM + 65 + m]
                vr3 = cf[:, CF_VM + 96 + m:CF_VM + 97 + m]
                S.dma("gpsimd", out=xm[:], in_=xv.v(xv.t[m * TT:(m + 1) * TT, :].rearrange("(s p) c -> p s c", p=128)))
                if m > 0:
                    vec("tensor_scalar", out=hT[:, :, 0:1], in0=hT[:, :, TT:TT + 1],
                        scalar1=cf[:, CF_VM + m - 1:CF_VM + m], scalar2=None, op0=ALU.mult)
                rms_rstd(xm, 0)
                norm_transpose(xm, 0, hT, PD_A1, lambda kc: pf[:, PV_SHM * 8 + kc:PV_SHM * 8 + kc + 1]
                               if False else modf[:, kc:kc + 1], 1)

                CK('stage1')
                pz = nextp()
                for kc in range(8):
                    mm(out=pz[:, 0:TT], lhsT=l1a[:, kc, 0:128], rhs=hT[:, kc, 1:TT + 1], start=(kc == 0), stop=False)
                    mm(out=pz[:, 0:TT], lhsT=l1b[:, kc, 0:128], rhs=hT[:, kc, 0:TT], start=False, stop=(kc == 7))
                act(out=lw[0:64, :], in_=pz[0:64, 0:TT], func=AF.Tanh)
                act(out=lw[64:128, :], in_=pz[64:128, 0:TT], func=AF.Copy)
                pz = nextp()
                for kc in range(8):
                    mm(out=pz[:, 0:TT], lhsT=l1a[:, kc, 128:256], rhs=hT[:, kc, 1:TT + 1], start=(kc == 0), stop=False)
                    mm(out=pz[:, 0:TT], lhsT=l1b[:, kc, 128:256], rhs=hT[:, kc, 0:TT], start=False, stop=(kc == 7))
                act(out=lga[:, :], in_=pz[:, 0:TT], func=AF.Sigmoid)
                pz = nextp()
                for kc in range(8):
                    mm(out=pz[0:32, 0:TT], lhsT=l1a[:, kc, 256:288], rhs=hT[:, kc, 1:TT + 1], start=(kc == 0), stop=False)
                    mm(out=pz[0:32, 0:TT], lhsT=l1b[:, kc, 256:288], rhs=hT[:, kc, 0:TT], start=False, stop=(kc == 7))
                act(out=lgb[0:32, :], in_=pz[0:32, 0:TT], func=AF.Sigmoid)

                CK('lora1')
                for c in range(8):
                    csl = slice(c * 128, (c + 1) * 128)
                    wr = wload(CH_RW + c)
                    wrv = wr[:, :].re("p (k c) -> p k c", c=512)
                    for j in range(3):
                        pz = nextp()
                        for kc in range(8):
                            mm(out=pz[:, 0:TT], lhsT=wrv[:, kc, j * 128:(j + 1) * 128], rhs=hT[:, kc, 1:TT + 1],
                               start=(kc == 0), stop=(kc == 7))
                        vec("tensor_copy", out=pj[j][:, 0:1], in_=hal[:, c, j:j + 1])
                        act(out=pj[j][:, 1:TT + 1], in_=pz[:, 0:TT], func=AF.Copy)
                        vec("tensor_scalar", out=hal[:, c, j:j + 1], in0=pj[j][:, TT:TT + 1], scalar1=vcur,
                            scalar2=None, op0=ALU.mult)
                        vec("tensor_tensor", out=tmpd[:, :], in0=pj[j][:, 0:TT], in1=pj[j][:, 1:TT + 1], op=ALU.subtract)
                        vec("scalar_tensor_tensor", out=rkv[j][:, :], in0=tmpd[:, :], scalar=pv(PV_MUR + j, c),
                            in1=pj[j][:, 1:TT + 1], op0=ALU.mult, op1=ALU.add)
                    r_, k_, v_ = rkv
                    vec("tensor_scalar", out=v_[:, :], in0=v_[:, :], scalar1=vcur, scalar2=None, op0=ALU.mult)
                    act(out=vbf[:, :], in_=v_[:, :], func=AF.Copy)
                    pz = nextp()
                    mm(out=pz[:, 0:TT], lhsT=l2[0:64, 0, csl], rhs=lw[0:64, :], start=True, stop=True)
                    act(out=sw[:, :], in_=pz[:, 0:TT], func=AF.Sigmoid, bias=pv(PV_W0, c))
                    pz = nextp()
                    mm(out=pz[:, 0:TT], lhsT=l2[64:128, 0, csl], rhs=lw[64:128, :], start=True, stop=True)
                    act(out=asig[:, :], in_=pz[:, 0:TT], func=AF.Sigmoid, bias=pv(PV_A0, c))
                    pz = nextp()
                    mm(out=pz[:, 0:TT], lhsT=l2[:, 1, csl], rhs=lga[:, :], start=True, stop=False)
                    mm(out=pz[:, 0:TT], lhsT=l2[0:32, 2, csl], rhs=lgb[0:32, :], start=False, stop=True)
                    act(out=gg[:, :], in_=pz[:, 0:TT], func=AF.Copy)
                    vec("tensor_tensor_scan", out=cs[:, :], data0=cf[:, CF_MSK:CF_MSK + TT], data1=sw[:, :],
                        initial=0.0, op0=ALU.mult, op1=ALU.add)
                    vec("tensor_tensor", out=cm[:, :], in0=cs[:, :], in1=sw[:, :], op=ALU.subtract)
                    act(out=Ep[:, :], in_=cs[:, :], func=AF.Exp, scale=-C0)
                    act(out=En[:, :], in_=cs[:, :], func=AF.Exp, scale=C0)
                    act(out=Em[:, :], in_=cm[:, :], func=AF.Exp, scale=-C0)
                    act(out=sqb[:, :], in_=k_[:, :], func=AF.Square, scale=pv(PV_KK, c))
                    pz = nextp()
                    mm(out=pz[:, 0:TT], lhsT=onesbd, rhs=sqb[:, :], start=True, stop=True)
                    act(out=rinv[:, :], in_=pz[:, 0:TT], func=AF.Sqrt)
                    vec("tensor_scalar", out=rinv[:, :], in0=rinv[:, :], scalar1=1e-12, scalar2=None, op0=ALU.max)
                    vec("reciprocal", out=rinv[:, :], in_=rinv[:, :])
                    vec("scalar_tensor_tensor", out=kkb[:, :], in0=k_[:, :], scalar=pv(PV_KK, c), in1=rinv[:, :],
                        op0=ALU.mult, op1=ALU.mult)
                    vec("tensor_scalar", out=ff[:, :], in0=asig[:, :], scalar1=pv(PV_KA, c), scalar2=pdv(PD_OMK, c),
                        op0=ALU.mult, op1=ALU.add)
                    vec("tensor_tensor", out=kmod[:, :], in0=k_[:, :], in1=ff[:, :], op=ALU.mult)
                    vec("tensor_tensor", out=bv[:, :], in0=kkb[:, :], in1=asig[:, :], op=ALU.mult)
                    vec("scalar_tensor_tensor", out=AR[:, :, 0, :], in0=kkb[:, :].re("p (s t) -> p s t", t=128), scalar=-1.0,
                        in1=Em[:, :].re("p (s t) -> p s t", t=128), op0=ALU.mult, op1=ALU.mult)
                    vec("tensor_tensor", out=AR[:, :, 1, :], in0=r_[:, :].re("p (s t) -> p s t", t=128),
                        in1=Ep[:, :].re("p (s t) -> p s t", t=128), op=ALU.mult)
                    vec("tensor_tensor", out=Bt[:, :], in0=bv[:, :], in1=En[:, :], op=ALU.mult)
                    vec("tensor_tensor", out=Kt[:, :], in0=kmod[:, :], in1=En[:, :], op=ALU.mult)
                    vec("tensor_tensor", out=tmpd[:, :], in0=r_[:, :], in1=kmod[:, :], op=ALU.mult)
                    act(out=sqb[:, :], in_=tmpd[:, :], func=AF.Copy, scale=pv(PV_RK, c))
                    pz = nextp()
                    mm(out=pz[:, 0:TT], lhsT=onesbd, rhs=sqb[:, :], start=True, stop=True)
                    vec("tensor_tensor", out=bon[:, :], in0=pz[:, 0:TT], in1=v_[:, :], op=ALU.mult)
                    for qi, (src, dst) in enumerate(((Bt, Bpad), (Kt, Kpad), (vbf, None))):
                        for sub in range(NSUB):
                            tr(out=pT[:, qi * 256 + sub * 128: qi * 256 + (sub + 1) * 128],
                               in_=src[:, sub * 128:(sub + 1) * 128], identity=ident)
                        srcv = pT[:, qi * 256:(qi + 1) * 256]
                        if dst is None:
                            act(out=Vtm[:].re("p s c -> p (s c)"), in_=srcv, func=AF.Copy)
                        else:
                            for h in range(2):
                                act(out=dst[:, :, h, h * 64:(h + 1) * 64],
                                    in_=srcv.re("p (s c) -> p s c", c=128)[:, :, h * 64:(h + 1) * 64], func=AF.Copy)
                    CK('rwkv_a')
                    for h in range(2):
                        hs = slice(h * 64, (h + 1) * 64)
                        for sub in range(NSUB):
                            u = h * NSUB + sub
                            tsl = slice(sub * 128, (sub + 1) * 128)
                            pz = nextp()
                            mm(out=pz[:, 0:256], lhsT=Bt[hs, tsl], rhs=AR[hs, sub, :, :].re("p a t -> p (a t)"),
                               start=True, stop=True)
                            mm(out=pz[:, 256:512], lhsT=Kt[hs, tsl], rhs=AR[hs, sub, :, :].re("p a t -> p (a t)"),
                               start=True, stop=True)
                            vec("tensor_tensor", out=AM[:, u, :], in0=pz[:, :], in1=cb[:, CB_MT4:CB_MT4 + 512], op=ALU.mult)
                            mm(out=pC[:, u * 128:(u + 1) * 128], lhsT=AR[hs, sub, 0, :], rhs=Bt[hs, tsl],
                               start=True, stop=True)
                    vec("tensor_tensor", out=L0[:].re("p u t -> p (u t)"), in0=pC[:, :], in1=cb[:, CB_ML4:CB_ML4 + 512],
                        op=ALU.mult)
                    CK('rwkv_b')
                    vec("tensor_tensor", out=SS[0][:], in0=AM[:, :, 0:128], in1=ident.bc(1, [128, 4, 128]), op=ALU.add)
                    lt_prev = lambda u: AM[:, u, 0:128]
                    lp_prev = lambda u: L0[:, u, :]
                    scur = 0
                    for lev in range(1, 6):
                        lpn = LP[lev % 2]
                        ltn = LT[lev % 2]
                        for u in range(4):
                            mm(out=pS[:, u * 128:(u + 1) * 128], lhsT=lt_prev(u), rhs=lp_prev(u), start=True, stop=True)
                        if lev <= 4:
                            for u in range(4):
                                mm(out=pS[:, 512 + u * 128:512 + (u + 1) * 128], lhsT=lp_prev(u), rhs=lt_prev(u),
                                   start=True, stop=True)
                        act(out=lpn[:].re("p u t -> p (u t)"), in_=pS[:, 0:512], func=AF.Copy)
                        if lev <= 4:
                            act(out=ltn[:].re("p u t -> p (u t)"), in_=pS[:, 512:1024], func=AF.Copy)
                        for u in range(4):
                            mm(out=pO[:, u * 128:(u + 1) * 128], lhsT=lpn[:, u, :], rhs=SS[scur][:, u, :], start=True, stop=True)
                        vec("tensor_tensor", out=SS[1 - scur][:].re("p u t -> p (u t)"), in0=pO[:, :],
                            in1=SS[scur][:].re("p u t -> p (u t)"), op=ALU.add)
                        scur = 1 - scur
                        lt_prev = (lambda b: (lambda u: b[:, u, :]))(ltn)
                        lp_prev = (lambda b: (lambda u: b[:, u, :]))(lpn)
                    TTm = SS[scur]
                    CK('rwkv_c')
                    for q in range(2 * NSUB):
                        sub, half = q // 2, q % 2
                        ps_ = slice(half * 64, half * 64 + 64)
                        tsl = slice(sub * 128, (sub + 1) * 128)
                        zi = q % 2
                        for h in range(2):
                            hs = slice(h * 64, (h + 1) * 64)
                            u = h * NSUB + sub
                            mm(out=pC[:, hs], lhsT=AR[hs, sub, 0, :], rhs=Zb[hs, c, zi, :], start=True, stop=False)
                            mm(out=pC[:, hs], lhsT=AM[:, u, 256:384], rhs=Vtm[:, sub, hs], start=False, stop=True)
                        act(out=Xb[ps_, :], in_=pC[ps_, 0:128], func=AF.Copy)
                        CK('c1')
                        for h in range(2):
                            hs = slice(h * 64, (h + 1) * 64)
                            u = h * NSUB + sub
                            mm(out=pC[:, 128 + h * 64:128 + (h + 1) * 64], lhsT=TTm[ps_, u, :], rhs=Xb[ps_, hs],
                               start=True, stop=True)
                        vec("tensor_copy", out=Ub[ps_, :], in_=pC[ps_, 128:256])
                        CK('c2')
                        for h in range(2):
                            hs = slice(h * 64, (h + 1) * 64)
                            u = h * NSUB + sub
                            o_ = slice(256 + h * 64, 256 + (h + 1) * 64)
                            mm(out=pC[:, o_], lhsT=AR[hs, sub, 1, :], rhs=Zb[hs, c, zi, :], start=True, stop=False)
                            mm(out=pC[:, o_], lhsT=AM[:, u, 128:256], rhs=Ub[:, hs], start=False, stop=False)
                            mm(out=pC[:, o_], lhsT=AM[:, u, 384:512], rhs=Vtm[:, sub, hs], start=False, stop=True)
                        act(out=Ytm[ps_, sub, :], in_=pC[ps_, 256:384], func=AF.Copy)
                        CK('c3')
                        for h in range(2):
                            hs = slice(h * 64, (h + 1) * 64)
                            mm(out=pC[:, 384:448], lhsT=Bpad[ps_, sub, h, :], rhs=Ub[ps_, hs], start=(h == 0), stop=False)
                            mm(out=pC[:, 384:448], lhsT=Kpad[ps_, sub, h, :], rhs=Vtm[ps_, sub, hs], start=False, stop=(h == 1))
                        CK('c4')
                        pcv = Ep[:, q * 64 + 63:q * 64 + 64]
                        vec("tensor_scalar", out=ztmp[:, :], in0=Zf[:, c, :], scalar1=pcv, scalar2=None, op0=ALU.mult)
                        vec("scalar_tensor_tensor", out=Zf[:, c, :], in0=pC[:, 384:448], scalar=pcv, in1=ztmp[:, :],
                            op0=ALU.mult, op1=ALU.add)
                        act(out=Zb[:, c, 1 - zi, :], in_=Zf[:, c, :], func=AF.Copy)
                    CK('rwkv_d')
                    yv = Ytm[:].re("p s (h i) -> p (s h) i", i=64)
                    vec("tensor_reduce", out=gst[:, 0:4], in_=yv, axis=AX.X, op=ALU.add)
                    act(out=ysq[:, :], in_=Ytm[:].re("p s c -> p (s c)"), func=AF.Square)
                    vec("tensor_reduce", out=gst[:, 4:8], in_=ysq[:, :].re("p (g i) -> p g i", i=64), axis=AX.X, op=ALU.add)
                    vec("tensor_scalar", out=gst[:, 8:12], in0=gst[:, 0:4], scalar1=1.0 / 64, scalar2=None, op0=ALU.mult)
                    vec("tensor_tensor", out=gst[:, 12:16], in0=gst[:, 8:12], in1=gst[:, 8:12], op=ALU.mult)
                    vec("scalar_tensor_tensor", out=gst[:, 16:20], in0=gst[:, 4:8], scalar=1.0 / 64, in1=gst[:, 12:16],
                        op0=ALU.mult, op1=ALU.subtract)
                    act(out=gst[:, 20:24], in_=gst[:, 16:20], func=AF.Sqrt, bias=eps_g, scale=1.0)
                    vec("reciprocal", out=gst[:, 24:28], in_=gst[:, 20:24])
                    ysv = ysq[:, :].re("p (g i) -> p g i", i=64)
                    vec("tensor_tensor", out=ysv, in0=yv, in1=gst[:, 8:12].bc(2, [128, 4, 64]), op=ALU.subtract)
                    vec("tensor_tensor", out=ynb[:].re("p s (h i) -> p (s h) i", i=64), in0=ysv,
                        in1=gst[:, 24:28].bc(2, [128, 4, 64]), op=ALU.mult)
                    for sub in range(NSUB):
                        tr(out=pT[:, 768 + sub * 128 - 768 * 0: 768 + (sub + 1) * 128 - 768 * 0] if False else
                           pT[:, sub * 128:(sub + 1) * 128], in_=ynb[:, sub, :], identity=ident)
                    act(out=yln[:, :], in_=pT[:, 0:TT], func=AF.Identity, scale=pv(PV_LNW, c), bias=pv(PV_LNB, c))
                    vec("tensor_tensor", out=yln[:, :], in0=yln[:, :], in1=bon[:, :], op=ALU.add)
                    vec("tensor_tensor", out=yfin[:, c, :], in0=yln[:, :], in1=gg[:, :], op=ALU.mult)

                CK('rwkv')
                j0_2 = m % 2
                j0_3 = m % 8
                for g in range(3):
                    kdst = (K1, K2c, K3c)[g]
                    for j in range(3):
                        wa = wload(CH_AT + g * 3 + j)
                        wav = wa[:, :].re("p (k c) -> p k c", c=512)
                        for cc in range(4):
                            pz = nextp()
                            for kc in range(8):
                                mm(out=pz[:, 0:TT], lhsT=wav[:, kc, cc * 128:(cc + 1) * 128], rhs=hT[:, kc, 1:TT + 1],
                                   start=(kc == 0), stop=(kc == 7))
                            if j == 0:
                                act(out=Qa[g][:, cc, :], in_=pz[:, 0:TT], func=AF.Copy, scale=0.125)
                            elif j == 1:
                                if g == 0:
                                    act(out=K1[:, cc, 128:128 + TT], in_=pz[:, 0:TT], func=AF.Copy)
                                else:
                                    act(out=kdst[:, cc, :], in_=pz[:, 0:TT], func=AF.Copy)
                            else:
                                act(out=VF[:, cc, :], in_=pz[:, 0:TT], func=AF.Copy)
                    if g == 0:
                        for blk in range(2):
                            for cc in range(4):
                                tr(out=pT[:, cc * 128:(cc + 1) * 128], in_=VF[:, cc, blk * 128:(blk + 1) * 128], identity=ident)
                            vec("tensor_copy", out=V1[:, 1 + blk, :], in_=pT[:, 0:512])
                    elif g == 1:
                        vec("tensor_copy", out=VF2[:], in_=VF[:])
                CK('attn_proj')
                for h in range(8):
                    cc, hp = h // 2, (h % 2) * 64
                    hs = slice(hp, hp + 64)
                    vs = slice(h * 64, (h + 1) * 64)
                    vl = slice(hp, hp + 64)
                    if h % 2 == 0:
                        for r in range(4):
                            tr(out=pT[0:64, r * 128:(r + 1) * 128],
                               in_=VF2[:, cc, :].re("p (i r) -> p r i", r=4)[:, r, :], identity=ident)
                        vec("tensor_copy", out=V2c[0:64, :, :].re("p r c -> p (r c)"), in_=pT[0:64, 0:512])
                        for r in range(16):
                            tr(out=pT[0:16, (r % 8) * 128:(r % 8 + 1) * 128],
                               in_=VF[:, cc, :].re("p (i r) -> p r i", r=16)[:, r, :], identity=ident)
                            if r % 8 == 7:
                                vec("tensor_copy", out=V3c[0:16, r - 7:r + 1, :].re("p a c -> p (a c)"), in_=pT[0:16, 0:1024])
                    for blk in range(2):
                        qv = Qa[0][hs, cc, blk * 128:(blk + 1) * 128]
                        mm(out=pS[:, (blk * 2) * 128:(blk * 2 + 1) * 128], lhsT=K1[hs, cc, blk * 128:(blk + 1) * 128],
                           rhs=qv, start=True, stop=True)
                        mm(out=pS[:, (blk * 2 + 1) * 128:(blk * 2 + 2) * 128],
                           lhsT=K1[hs, cc, 128 + blk * 128:128 + (blk + 1) * 128], rhs=qv, start=True, stop=True)
                    act(out=pe[:, :], in_=pS[:, 0:512], func=AF.Exp)
                    vec("tensor_tensor", out=pp_[:, :].re("p (b e) -> p b e", b=2), in0=pe[:, :].re("p (b e) -> p b e", b=2),
                        in1=cb[:, CB_E1 + h * 256:CB_E1 + (h + 1) * 256].bc(1, [128, 2, 256]), op=ALU.mult)
                    vec("tensor_scalar", out=pp_[:, 0:128], in0=pp_[:, 0:128], scalar1=vprev, scalar2=None, op0=ALU.mult)
                    for blk in range(2):
                        mm(out=pO[0:64, blk * 128:(blk + 1) * 128], lhsT=V1[:, blk, vs],
                           rhs=pp_[:, (blk * 2) * 128:(blk * 2 + 1) * 128], start=True, stop=False)
                        mm(out=pO[0:64, blk * 128:(blk + 1) * 128], lhsT=V1[:, blk + 1, vs],
                           rhs=pp_[:, (blk * 2 + 1) * 128:(blk * 2 + 2) * 128], start=False, stop=True)
                    ppv = pp_[:, :].re("p (b c q) -> p b c q", b=2, c=2)
                    mm(out=pD[0:64, 0:TT], lhsT=cb[:, CB_ONES:CB_ONES + 64], rhs=ppv[:, :, 0, :], start=True, stop=False)
                    mm(out=pD[0:64, 0:TT], lhsT=cb[:, CB_ONES:CB_ONES + 64], rhs=ppv[:, :, 1, :], start=False, stop=True)
                    act(out=accO[:, :], in_=pO[0:64, 0:TT], func=AF.Copy)
                    act(out=accD[:, :], in_=pD[0:64, 0:TT], func=AF.Copy)
                    for r in range(4):
                        qv = Qa[1][hs, cc, :].re("p (i r) -> p r i", r=4)[:, r, :]
                        mm(out=pS[:, r * 64:(r + 1) * 64], lhsT=K2r[hs, cc, r, :], rhs=qv, start=True, stop=True)
                        mm(out=pS[0:64, 512 + r * 64:512 + (r + 1) * 64],
                           lhsT=K2c[hs, cc, :].re("p (i r) -> p r i", r=4)[:, r, :], rhs=qv, start=True, stop=True)
                    act(out=pe[:, 0:256], in_=pS[:, 0:256], func=AF.Exp)
                    act(out=peb[0:64, :], in_=pS[0:64, 512:768], func=AF.Exp)
                    ea = cb[:, CB_EA2 + (j0_2 * 8 + h) * 64:CB_EA2 + (j0_2 * 8 + h + 1) * 64]
                    vec("scalar_tensor_tensor", out=pp_[:, 0:256].re("p (r i) -> p r i", r=4),
                        in0=pe[:, 0:256].re("p (r i) -> p r i", r=4), scalar=vr2, in1=ea.bc(1, [128, 4, 64]),
                        op0=ALU.mult, op1=ALU.mult)
                    eb = cb[0:64, CB_EB2 + h * 64:CB_EB2 + (h + 1) * 64]
                    vec("tensor_tensor", out=ppb[0:64, :].re("p (r i) -> p r i", r=4),
                        in0=peb[0:64, :].re("p (r i) -> p r i", r=4), in1=eb.bc(1, [64, 4, 64]), op=ALU.mult)
                    for r in range(4):
                        mm(out=pO[0:64, r * 64:(r + 1) * 64], lhsT=V2r[:, r, vs], rhs=pp_[:, r * 64:(r + 1) * 64],
                           start=True, stop=False)
                        mm(out=pO[0:64, r * 64:(r + 1) * 64], lhsT=V2c[0:64, r, vl], rhs=ppb[0:64, r * 64:(r + 1) * 64],
                           start=False, stop=True)
                    mm(out=pD[0:64, 0:TT], lhsT=cb[:, CB_ONES:CB_ONES + 64], rhs=pp_[:, 0:256], start=True, stop=False)
                    mm(out=pD[0:64, 0:TT], lhsT=cb[0:64, CB_ONES:CB_ONES + 64], rhs=ppb[0:64, :], start=False, stop=True)
                    vec("tensor_tensor", out=accO[:, :].re("p (i r) -> p r i", r=4), in0=accO[:, :].re("p (i r) -> p r i", r=4),
                        in1=pO[0:64, 0:TT].re("p (r i) -> p r i", r=4), op=ALU.add)
                    vec("tensor_tensor", out=accD[:, :].re("p (i r) -> p r i", r=4), in0=accD[:, :].re("p (i r) -> p r i", r=4),
                        in1=pD[0:64, 0:TT].re("p (r i) -> p r i", r=4), op=ALU.add)
                    for r in range(16):
                        qv = Qa[2][hs, cc, :].re("p (i r) -> p r i", r=16)[:, r, :]
                        mm(out=pS[:, r * 16:(r + 1) * 16], lhsT=K3r[hs, cc, r, :], rhs=qv, start=True, stop=True)
                        mm(out=pS[0:16, 512 + r * 16:512 + (r + 1) * 16],
                           lhsT=K3c[hs, cc, :].re("p (i r) -> p r i", r=16)[:, r, :], rhs=qv, start=True, stop=True)
                    act(out=pe[:, 0:256], in_=pS[:, 0:256], func=AF.Exp)
                    act(out=peb[0:16, :], in_=pS[0:16, 512:768], func=AF.Exp)
                    ea = cb[:, CB_EA3 + (j0_3 * 8 + h) * 16:CB_EA3 + (j0_3 * 8 + h + 1) * 16]
                    vec("scalar_tensor_tensor", out=pp_[:, 0:256].re("p (r i) -> p r i", r=16),
                        in0=pe[:, 0:256].re("p (r i) -> p r i", r=16), scalar=vr3, in1=ea.bc(1, [128, 16, 16]),
                        op0=ALU.mult, op1=ALU.mult)
                    eb = cb[0:16, CB_EB3 + h * 16:CB_EB3 + (h + 1) * 16]
                    vec("tensor_tensor", out=ppb[0:16, :].re("p (r i) -> p r i", r=16),
                        in0=peb[0:16, :].re("p (r i) -> p r i", r=16), in1=eb.bc(1, [16, 16, 16]), op=ALU.mult)
                    for r in range(16):
                        mm(out=pO[0:64, r * 16:(r + 1) * 16], lhsT=V3r[:, r, vs], rhs=pp_[:, r * 16:(r + 1) * 16],
                           start=True, stop=False)
                        mm(out=pO[0:64, r * 16:(r + 1) * 16], lhsT=V3c[0:16, r, vl], rhs=ppb[0:16, r * 16:(r + 1) * 16],
                           start=False, stop=True)
                    mm(out=pD[0:64, 0:TT], lhsT=cb[:, CB_ONES:CB_ONES + 64], rhs=pp_[:, 0:256], start=True, stop=False)
                    mm(out=pD[0:64, 0:TT], lhsT=cb[0:16, CB_ONES:CB_ONES + 64], rhs=ppb[0:16, :], start=False, stop=True)
                    if phaseB:
                        vec("tensor_tensor", out=accO[:, :].re("p (i r) -> p r i", r=16),
                            in0=accO[:, :].re("p (i r) -> p r i", r=16),
                            in1=pO[0:64, 0:TT].re("p (r i) -> p r i", r=16), op=ALU.add)
                        vec("tensor_tensor", out=accD[:, :].re("p (i r) -> p r i", r=16),
                            in0=accD[:, :].re("p (i r) -> p r i", r=16),
                            in1=pD[0:64, 0:TT].re("p (r i) -> p r i", r=16), op=ALU.add)
                        vec("reciprocal", out=accD[:, :], in_=accD[:, :])
                        vec("tensor_tensor", out=oT[:, h, :], in0=accO[:, :], in1=accD[:, :], op=ALU.mult)
                    if h % 2 == 1:
                        S.dma("gpsimd", out=V2r[j0_2 * 64:(j0_2 + 1) * 64, :, cc * 128:(cc + 1) * 128], in_=V2c[0:64, :, :])
                        S.dma("gpsimd", out=V3r[j0_3 * 16:(j0_3 + 1) * 16, :, cc * 128:(cc + 1) * 128], in_=V3c[0:16, :, :])
                CK('attn')
                vec("tensor_copy", out=K1[:, :, 0:128], in_=K1[:, :, TT:TT + 128])
                vec("tensor_copy", out=V1[:, 0, :], in_=V1[:, 2, :])
                vec("tensor_copy", out=K2r[:, :, :, j0_2 * 64:(j0_2 + 1) * 64],
                    in_=K2c[:].re("p c (i r) -> p c r i", r=4))
                vec("tensor_copy", out=K3r[:, :, :, j0_3 * 16:(j0_3 + 1) * 16],
                    in_=K3c[:].re("p c (i r) -> p c r i", r=16))

                if not phaseB:
                    continue
                for cc in range(8):
                    wpb = wload(CH_PB + cc)
                    wv = wpb[:, :].re("p (a k c) -> p a k c", a=4, c=128)
                    pz = nextp()
                    for kc in range(8):
                        mm(out=pz[:, 0:TT], lhsT=wv[:, 0, kc, :], rhs=hT[:, kc, 1:TT + 1], start=(kc == 0), stop=(kc == 7))
                    act(out=sa[:, :], in_=pz[:, 0:TT], func=AF.Sigmoid)
                    pz = nextp()
                    for kc in range(8):
                        mm(out=pz[:, 0:TT], lhsT=wv[:, 1, kc, :], rhs=hT[:, kc, 1:TT + 1], start=(kc == 0), stop=(kc == 7))
                    act(out=sbb[:, :], in_=pz[:, 0:TT], func=AF.Sigmoid)
                    pz = nextp()
                    for kc in range(8):
                        mm(out=pz[:, 0:TT], lhsT=wv[:, 2, kc, :], rhs=yfin[:, kc, :], start=(kc == 0), stop=(kc == 7))
                    vec("tensor_tensor", out=sa[:, :], in0=sa[:, :], in1=pz[:, 0:TT], op=ALU.mult)
                    pz = nextp()
                    for hh_ in range(8):
                        mm(out=pz[:, 0:TT], lhsT=wv[0:64, 3, hh_, :], rhs=oT[:, hh_, :], start=(hh_ == 0), stop=(hh_ == 7))
                    vec("tensor_tensor", out=sbb[:, :], in0=sbb[:, :], in1=pz[:, 0:TT], op=ALU.mult)
                    vec("tensor_tensor", out=mixT[:, cc, :], in0=sa[:, :], in1=sbb[:, :], op=ALU.add)

                def norm_residual(ps_views, gb):
                    for hf in range(2):
                        act(out=junk[:, 0:512], in_=ps_views[hf], func=AF.Square, accum_out=st4[:, 8 + hf:9 + hf])
                    vec("tensor_tensor", out=st4[:, 10:11], in0=st4[:, 8:9], in1=st4[:, 9:10], op=ALU.add)
                    act(out=st4[:, 11:12], in_=st4[:, 10:11], func=AF.Sqrt, bias=eps_r, scale=1.0 / D)
                    vec("reciprocal", out=st4[:, 4:5], in_=st4[:, 11:12])
                    for hf in range(2):
                        for qq in range(2):
                            cs_ = slice(hf * 512 + qq * 256, hf * 512 + (qq + 1) * 256)
                            vec("scalar_tensor_tensor", out=utmp[:, :], in0=ps_views[hf][:, qq * 256:(qq + 1) * 256],
                                scalar=st4[:, 4:5], in1=gb[:, cs_], op0=ALU.mult, op1=ALU.mult)
                            vec("tensor_tensor", out=xm[:, sub, cs_], in0=xm[:, sub, cs_], in1=utmp[:, :], op=ALU.add)

                wo = [wload(CH_WOUT + 0), wload(CH_WOUT + 1)]
                for sub in range(NSUB):
                    for hf in range(2):
                        wv = wo[hf][:, :].re("p (k c) -> p k c", c=512)
                        for kc in range(8):
                            mm(out=pS[:, hf * 512:(hf + 1) * 512], lhsT=mixT[:, kc, sub * 128:(sub + 1) * 128], rhs=wv[:, kc, :],
                               start=(kc == 0), stop=(kc == 7))
                    norm_residual([pS[:, 0:512], pS[:, 512:1024]], gmb)
                rms_rstd(xm, 2)
                norm_transpose(xm, 2, h2T, PD_A2, lambda kc: modf[:, 24 + kc:25 + kc], 0)
                accs = [[pS[:, 0:512], pS[:, 512:1024]], [pO[:, :], pD[:, :]]]
                for pg in range(3):
                    nk = 8 if pg < 2 else 6
                    for i4 in range(nk // 2):
                        i = pg * 4 + i4
                        wf_ = wload(CH_FF + i)
                        wv = wf_[:, :].re("p (k c) -> p k c", c=512)
                        for jj in range(2):
                            jl = i4 * 2 + jj
                            pg_ = nextp()
                            for kc in range(8):
                                mm(out=pg_[:, 0:TT], lhsT=wv[:, kc, jj * 128:(jj + 1) * 128], rhs=h2T[:, kc, :],
                                   start=(kc == 0), stop=(kc == 7))
                            act(out=sg[:, :], in_=pg_[:, 0:TT], func=AF.Silu)
                            pu = nextp()
                            for kc in range(8):
                                mm(out=pu[:, 0:TT], lhsT=wv[:, kc, 256 + jj * 128:256 + (jj + 1) * 128], rhs=h2T[:, kc, :],
                                   start=(kc == 0), stop=(kc == 7))
                            vec("tensor_tensor", out=actT[:, jl, :], in0=sg[:, :], in1=pu[:, 0:TT], op=ALU.mult)
                    for hf in range(2):
                        wf_ = wload(CH_FO + pg * 2 + hf)
                        wv = wf_[:, :].re("p (k c) -> p k c", c=512)
                        for sub in range(NSUB):
                            for kc in range(nk):
                                mm(out=accs[sub][hf], lhsT=actT[:, kc, sub * 128:(sub + 1) * 128], rhs=wv[:, kc, :],
                                   start=(pg == 0 and kc == 0), stop=(pg == 2 and kc == nk - 1))
                for sub in range(NSUB):
                    norm_residual(accs[sub], gfb)
                r0 = (m - PB0) * TT
                S.dma("gpsimd", out=y_d.v(y_d.t[r0:r0 + TT, :].rearrange("(s p) c -> p s c", p=128)), in_=xm[:])


        except _Stop:
            pass
        S.finish([y_d] + finals)
        S.emit()
    return nc


_CACHE = {}


def prep_inputs(x, c, w_mod, b_mod, g_pre_mix, g_post_mix, g_pre_ffn, g_post_ffn, w_in, mu_rkv, mu_lora,
           w0, w1, w2, a0, a1, a2, g1, g2, k_k, k_a, r_k, ln_x_w, ln_x_b, w_o_rwkv, w_o_attn, w_out,
           w_ffn_in, w_ffn_out):
    f = lambda a: np.asarray(a, np.float32)
    x = f(x); c = f(c)
    w_in = f(w_in)[0]; w_modm = f(w_mod)[0]
    bm = f(b_mod)[0].reshape(6, 1024)
    vecs = [bm[0], bm[1], bm[2], bm[3], bm[4], bm[5], f(g_pre_mix)[0], f(g_post_mix)[0], f(g_pre_ffn)[0],
            f(g_post_ffn)[0], f(mu_rkv)[0, 0], f(mu_rkv)[0, 1], f(mu_rkv)[0, 2], f(mu_lora)[0, 0], f(mu_lora)[0, 1],
            f(mu_lora)[0, 2], f(w0)[0], f(a0)[0], f(k_k)[0], f(k_a)[0], f(r_k)[0].reshape(-1), f(ln_x_w)[0],
            f(ln_x_b)[0]]
    wsrc = np.zeros((NCH, 128, 4096), np.float32)
    def put(i, arr3):
        P, K, C = arr3.shape
        v = wsrc[i].reshape(128, -1)
        tmp = np.zeros((128, K, 4096 // K if K in (8,) else C), np.float32) if False else None
        blk = np.zeros((128, K * C), np.float32)
        blk[:P] = arr3.reshape(P, K * C)
        v[:, :K * C] = blk
    for cch in range(8):
        a = np.zeros((128, 8, 512), np.float32)
        for j in range(3):
            a[:, :, j * 128:(j + 1) * 128] = _wchunk(w_in, slice(j * 1024 + cch * 128, j * 1024 + (cch + 1) * 128))
        put(CH_RW + cch, a)
    for g in range(3):
        for j in range(3):
            o = 3072 + j * 1536 + g * 512
            put(CH_AT + g * 3 + j, _wchunk(w_in, slice(o, o + 512)))
    wor = f(w_o_rwkv)[0]; woa = f(w_o_attn)[0]; wout = f(w_out)[0]
    for cc in range(8):
        cs_ = slice(cc * 128, (cc + 1) * 128)
        a = np.zeros((128, 4, 8, 128), np.float32)
        a[:, 0] = _wchunk(w_in, slice(7680 + cc * 128, 7680 + (cc + 1) * 128))
        a[:, 1] = _wchunk(w_in, slice(8704 + cc * 128, 8704 + (cc + 1) * 128))
        a[:, 2] = _wchunk(wor, cs_)
        a[0:64, 3] = woa[:, cs_].reshape(8, 64, 128).transpose(1, 0, 2)
        put(CH_PB + cc, a.reshape(128, 32, 128))
    for hf in range(2):
        put(CH_WOUT + hf, _wchunk(wout, slice(hf * 512, (hf + 1) * 512)))
    wfi = f(w_ffn_in)[0]; wfo = f(w_ffn_out)[0]
    for i in range(11):
        a = np.zeros((128, 8, 512), np.float32)
        a[:, :, 0:256] = _wchunk(wfi, slice(i * 256, (i + 1) * 256))
        a[:, :, 256:512] = _wchunk(wfi, slice(FH + i * 256, FH + (i + 1) * 256))
        put(CH_FF + i, a)
    for pg in range(3):
        nk = 8 if pg < 2 else 6
        for hf in range(2):
            blk = wfo[pg * 1024:pg * 1024 + nk * 128, hf * 512:(hf + 1) * 512]
            put(CH_FO + pg * 2 + hf, blk.reshape(nk, 128, 512).transpose(1, 0, 2))
    wmod = np.ascontiguousarray(
        w_modm.reshape(8, 128, 24, 256).transpose(2, 1, 0, 3).reshape(24, 128, 2048))
    l1 = np.concatenate([f(w1)[0], f(a1)[0], f(g1)[0]], 1)
    l1 = np.ascontiguousarray(l1.reshape(8, 128, 288).transpose(1, 0, 2).reshape(128, 8 * 288))
    l2 = np.zeros((128, 3, 1024), np.float32)
    l2[0:64, 0] = f(w2)[0]; l2[64:128, 0] = f(a2)[0]
    l2[:, 1] = f(g2)[0][0:128]; l2[0:32, 2] = f(g2)[0][128:160]
    l2 = l2.reshape(128, 3072)
    cbt = _host_consts()
    in_maps = []
    for core in range(8):
        b, hh = core // 2, core % 2
        pfm = np.concatenate([_fm(v) for v in vecs] + [_fm(c[b])], 1)
        if hh == 1:
            xvv = x[b]
        else:
            xvv = np.concatenate([np.zeros((T // 2, D), np.float32), x[b, :T // 2]], 0)
        in_maps.append({"xv": np.ascontiguousarray(xvv), "pfm": np.ascontiguousarray(pfm), "wmod": wmod,
                        "wsrc": wsrc, "l1": l1, "l2": l2, "cbt": cbt, "cft": _host_cf(hh)})
    return in_maps


def kernel(**inputs):
    in_maps = prep_inputs(**inputs)
    if "nc" not in _CACHE:
        _CACHE["nc"] = build()
    nc = _CACHE["nc"]
    res = run_bass_kernel_spmd(nc, in_maps, core_ids=list(range(8)))
    out = np.zeros((4, T, D), np.float32)
    for core in range(8):
        b, hh = core // 2, core % 2
        out[b, hh * (T // 2):(hh + 1) * (T // 2)] = res.results[core]["y"]
    return out
```

```python
import math
from contextlib import ExitStack

import numpy as np
import concourse.bass as bass
import concourse.mybir as mybir
from concourse.bass_utils import run_bass_kernel_spmd

F32 = mybir.dt.float32
BF16 = mybir.dt.bfloat16
AF = mybir.ActivationFunctionType
ALU = mybir.AluOpType
AX = mybir.AxisListType

ENGS = ("tensor", "vector", "scalar", "gpsimd", "sync")

T = 8192
D = 1024
TT = 256
NT = T // TT
PB0 = NT // 2
NSUB = TT // 128
FH = 2816
C0 = math.exp(-0.5)
GN_EPS = 64e-5
RMS_EPS = 1e-6
NSLOT = 4
import os
INTERLEAVE = os.environ.get('NOIL') is None


class Buf:
    def __init__(self, name, t):
        self.name = name
        self.t = t
        self.writer = None
        self.readers = []
        self.dsem = None
        self.dcnt = 0
        self.psum = False

    def __getitem__(self, idx):
        return View(self, self.t[idx])

    def v(self, ap):
        return View(self, ap)


class SubBuf:
    def __init__(self, buf, col0):
        self.buf = buf
        self.col0 = col0

    def __getitem__(self, idx):
        ps, cs = idx
        a = 0 if cs.start is None else cs.start
        assert cs.stop is not None
        return View(self.buf, self.buf.t[ps, self.col0 + a:self.col0 + cs.stop])


class View:
    def __init__(self, buf, ap):
        self.buf = buf
        self.ap = ap

    def __getitem__(self, idx):
        return View(self.buf, self.ap[idx])

    def re(self, pat, **kw):
        return View(self.buf, self.ap.rearrange(pat, **kw))

    def bc(self, axis, shape):
        return View(self.buf, self.ap.unsqueeze(axis).to_broadcast(list(shape)))


def _unw(x):
    return x.ap if isinstance(x, View) else x


class Sched:
    def __init__(self, nc, stack):
        self.nc = nc
        self.stack = stack
        self.q = {e: [] for e in ENGS}
        self.waited = {e: {} for e in ENGS}
        self.dma_sems = []

    def sb(self, name, shape, dt):
        t = self.stack.enter_context(self.nc.sbuf_tensor("s_" + name, list(shape), dt))
        return Buf(name, t)

    def ps(self, name, shape, dt=F32):
        t = self.stack.enter_context(self.nc.psum_tensor("p_" + name, list(shape), dt))
        return Buf(name, t)

    def dram(self, name, shape, dt, kind):
        t = self.nc.dram_tensor(name, list(shape), dt, kind=kind).ap()
        return Buf(name, t)

    def _deps(self, eng, reads, writes):
        deps = {}

        def add(tok):
            if tok is None:
                return
            k, v = tok
            if deps.get(k, 0) < v:
                deps[k] = v

        for b in reads:
            add(b.writer)
            if b.psum:
                for r in b.readers:
                    if r[0] != eng:
                        add(r)
        for b in writes:
            add(b.writer)
            for r in b.readers:
                add(r)
        waits = []
        for k, v in deps.items():
            if k == "tensor" and eng == "tensor":
                continue
            if self.waited[eng].get(k, 0) >= v:
                continue
            self.waited[eng][k] = v
            waits.append((k, v))
            if isinstance(k, str):
                self.q[k][v - 1][2] = True
        return waits

    def _commit(self, tok, reads, writes):
        for b in writes:
            b.writer = tok
            b.readers = []
        for b in reads:
            if b in writes:
                continue
            b.readers.append(tok)
            if len(b.readers) > 48:
                d = {}
                for k, v in b.readers:
                    if d.get(k, 0) < v:
                        d[k] = v
                b.readers = list(d.items())

    def op(self, eng, meth, **kw):
        writes, reads = [], []
        for k, v in kw.items():
            if isinstance(v, View):
                if k in ("out", "accum_out", "ap"):
                    if v.buf not in writes:
                        writes.append(v.buf)
                else:
                    if v.buf not in reads:
                        reads.append(v.buf)
        waits = self._deps(eng, reads, writes)
        if eng == "tensor":
            src = kw.get("lhsT", kw.get("in_"))
            lo = src.ap.base_partition()
            rows = (lo, lo + src.ap.partition_size())
            ob = kw["out"].buf
            prev = getattr(ob, "pe_rows", None)
            if prev is not None and ob.writer is not None and ob.writer[0] == "tensor" and \
                    (rows[1] <= prev[0] or prev[1] <= rows[0]):
                k, v = ob.writer
                if self.waited[eng].get(k, 0) < v:
                    self.waited[eng][k] = v
                    waits.append((k, v))
                    self.q[k][v - 1][2] = True
            ob.pe_rows = rows
        args = {k: _unw(v) for k, v in kw.items()}
        fn = lambda e, m=meth, a=args: getattr(e, m)(**a)
        self.q[eng].append([waits, fn, False, None])
        tok = (eng, len(self.q[eng]))
        self._commit(tok, reads, writes)
        return tok

    def dma(self, eng, out, in_, **kw):
        sb = out.buf
        if sb.dsem is None:
            sb.dsem = ("dma", len(self.dma_sems))
            self.dma_sems.append(sb.name)
        waits = self._deps(eng, [in_.buf], [out.buf])
        sb.dcnt += 16
        tok = (sb.dsem, sb.dcnt)
        a = dict(out=out.ap, in_=in_.ap, **kw)
        fn = lambda e, a=a: e.dma_start(**a)
        self.q[eng].append([waits, fn, False, sb.dsem])
        self._commit(tok, [in_.buf], [out.buf])
        return tok

    def finish(self, final_bufs):
        waits = self._deps("sync", final_bufs, [])
        self.q["sync"].append([waits, None, False, None])

    def emit(self):
        nc = self.nc
        st = self.stack
        esem = {e: st.enter_context(nc.semaphore("es_" + e)) for e in ENGS}
        dsem = [st.enter_context(nc.semaphore("ds%d" % i)) for i in range(len(self.dma_sems))]
        cum = {}
        for e in ENGS:
            c = 0
            arr = []
            for it in self.q[e]:
                if it[2]:
                    c += 1
                arr.append(c)
            cum[e] = arr

        def semval(k, v):
            if isinstance(k, str):
                return esem[k], cum[k][v - 1]
            return dsem[k[1]], v

        block = st.enter_context(nc.Block())

        def run(e, eng):
            for waits, fn, sig, dk in self.q[e]:
                for k, v in waits:
                    s, val = semval(k, v)
                    eng.wait_ge(s, val)
                if fn is None:
                    continue
                ins = fn(eng)
                if dk is not None:
                    ins.then_inc(dsem[dk[1]], 16)
                elif sig:
                    ins.then_inc(esem[e], 1)

        @block.tensor
        def _(eng):
            run("tensor", eng)

        @block.vector
        def _(eng):
            run("vector", eng)

        @block.scalar
        def _(eng):
            run("scalar", eng)

        @block.gpsimd
        def _(eng):
            run("gpsimd", eng)

        @block.sync
        def _(eng):
            run("sync", eng)


def _alibi_slopes(n):
    def pow2(m):
        start = 2.0 ** (-8.0 / m)
        return [start ** (i + 1) for i in range(m)]
    if math.log2(n).is_integer():
        s = pow2(n)
    else:
        p = 2 ** int(math.floor(math.log2(n)))
        s = pow2(p) + pow2(2 * p)[0::2][: n - p]
    return sorted(s, reverse=True)


(PV_SHM, PV_SCM, PV_GTM, PV_SHF, PV_SCF, PV_GTF, PV_GPM, PV_GQM, PV_GPF, PV_GQF,
 PV_MUR, PV_MUK, PV_MUV, PV_MUW, PV_MUA, PV_MUG, PV_W0, PV_A0, PV_KK, PV_KA, PV_RK,
 PV_LNW, PV_LNB, PV_C) = range(24)
NPV = 24

CB_ID = 0
CB_ONESBD = 128
CB_ONES = 256
CB_MT4 = 320
CB_ML4 = 832
CB_E1 = 1344
CB_EA2 = CB_E1 + 8 * 256
CB_EB2 = CB_EA2 + 2 * 8 * 64
CB_EA3 = CB_EB2 + 8 * 64
CB_EB3 = CB_EA3 + 8 * 8 * 16
NCB = CB_EB3 + 8 * 16
CF_MSK = 0
CF_VM = 256
CF_EPS = CF_VM + 128
CF_IDF = CF_EPS + 4
NCF = CF_IDF + 128

CH_RW = 0
CH_AT = 8
CH_PB = 17
CH_WOUT = 25
CH_FF = 27
CH_FO = 38
NCH = 44


def _host_consts():
    sl = np.asarray(_alibi_slopes(24), np.float64).reshape(3, 8)
    cb = np.zeros((128, NCB), np.float32)
    p = np.arange(128)
    cb[:, CB_ID:CB_ID + 128] = np.eye(128)
    cb[:, CB_ONESBD:CB_ONESBD + 128] = (p[:, None] // 64 == p[None, :] // 64)
    cb[:, CB_ONES:CB_ONES + 64] = 1.0
    same = (p[:, None] // 64 == p[None, :] // 64)
    su = same & (p[:, None] < p[None, :])
    iu = same & (p[:, None] <= p[None, :])
    slo = same & (p[:, None] > p[None, :])
    cb[:, CB_MT4:CB_MT4 + 512] = np.concatenate([su, iu, su, iu], 1)
    cb[:, CB_ML4:CB_ML4 + 512] = np.concatenate([slo] * 4, 1)
    k = p[:, None].astype(np.float64)
    q = p[None, :].astype(np.float64)
    for h in range(8):
        dpv = q - k + 128
        e_prev = np.where(dpv <= 128, np.exp(-sl[0, h] * dpv), 0.0)
        dcu = q - k
        e_cur = np.where(dcu >= 0, np.exp(-sl[0, h] * np.maximum(dcu, 0)), 0.0)
        cb[:, CB_E1 + h * 256: CB_E1 + h * 256 + 128] = e_prev
        cb[:, CB_E1 + h * 256 + 128: CB_E1 + h * 256 + 256] = e_cur
    i64 = np.arange(64)[None, :].astype(np.float64)
    for rot in range(2):
        for h in range(8):
            j = p // 64
            pp = (p % 64).astype(np.float64)
            a = ((rot - j - 1) % 2) + 1
            dl = 64.0 * a[:, None] + i64 - pp[:, None]
            e = np.where(dl <= 128, np.exp(-sl[1, h] * 4.0 * dl), 0.0)
            o = CB_EA2 + (rot * 8 + h) * 64
            cb[:, o:o + 64] = e
    for h in range(8):
        kk = np.arange(64)[:, None].astype(np.float64)
        dl = i64 - kk
        e = np.where(dl >= 0, np.exp(-sl[1, h] * 4.0 * np.maximum(dl, 0)), 0.0)
        o = CB_EB2 + h * 64
        cb[0:64, o:o + 64] = e
    i16 = np.arange(16)[None, :].astype(np.float64)
    for rot in range(8):
        for h in range(8):
            j = p // 16
            pp = (p % 16).astype(np.float64)
            a = ((rot - j - 1) % 8) + 1
            dl = 16.0 * a[:, None] + i16 - pp[:, None]
            e = np.where(dl <= 128, np.exp(-sl[2, h] * 16.0 * dl), 0.0)
            o = CB_EA3 + (rot * 8 + h) * 16
            cb[:, o:o + 16] = e
    for h in range(8):
        kk = np.arange(16)[:, None].astype(np.float64)
        dl = i16 - kk
        e = np.where(dl >= 0, np.exp(-sl[2, h] * 16.0 * np.maximum(dl, 0)), 0.0)
        o = CB_EB3 + h * 16
        cb[0:16, o:o + 16] = e
    return cb


def _host_cf(hh):
    cf = np.zeros((128, NCF), np.float32)
    m = np.ones((128, 256), np.float32)
    m[:, 0::64] = 0.0
    cf[:, CF_MSK:CF_MSK + 256] = m
    valid = lambda t: 0.0 if t < 0 else (1.0 if (hh == 1 or t >= PB0) else 0.0)
    p = np.arange(128)
    for t in range(NT):
        cf[:, CF_VM + t] = valid(t)
        cf[:, CF_VM + 32 + t] = valid(t - 1)
        j = p // 64
        a = ((t - j - 1) % 2) + 1
        cf[:, CF_VM + 64 + t] = [valid(t - aa) for aa in a]
        j = p // 16
        a = ((t - j - 1) % 8) + 1
        cf[:, CF_VM + 96 + t] = [valid(t - aa) for aa in a]
    cf[:, CF_EPS] = RMS_EPS
    cf[:, CF_EPS + 1] = GN_EPS
    cf[:, CF_EPS + 3] = 1.0
    cf[:, CF_IDF:CF_IDF + 128] = np.eye(128)
    return cf


def _fm(v):
    return np.ascontiguousarray(v.reshape(8, 128).T)


def _wchunk(w, cols):
    return w[:, cols].reshape(8, 128, -1).transpose(1, 0, 2)


class _Stop(Exception):
    pass


def build(nt=NT, dbg=None, dbg_tile=0, dbg_c=0, stop=None):
    nc = bass.Bass("TRN2", target_bir_lowering=False)
    with ExitStack() as st:
        S = Sched(nc, st)
        finals = []

        def CK(name):
            if stop == name:
                raise _Stop()

        def DBG(name, view, m=None, c=None):
            if not dbg or name not in dbg:
                return
            if m is not None and m != dbg_tile:
                return
            if c is not None and c != dbg_c:
                return
            shp = list(view.ap.shape)
            dd = S.dram("dbg_" + name, shp, view.ap.dtype, "ExternalOutput")
            S.dma("gpsimd", out=dd[:], in_=view)
            finals.append(dd)
        xv = S.dram("xv", [T, D], F32, "ExternalInput")
        pfm_d = S.dram("pfm", [128, NPV * 8], F32, "ExternalInput")
        wmod_d = S.dram("wmod", [24, 128, 2048], F32, "ExternalInput")
        wsrc = S.dram("wsrc", [NCH, 128, 4096], F32, "ExternalInput")
        l1_d = S.dram("l1", [128, 8 * 288], F32, "ExternalInput")
        l2_d = S.dram("l2", [128, 3 * 1024], F32, "ExternalInput")
        cb_d = S.dram("cbt", [128, NCB], F32, "ExternalInput")
        cf_d = S.dram("cft", [128, NCF], F32, "ExternalInput")
        y_d = S.dram("y", [T // 2, D], F32, "ExternalOutput")
        wscr = S.dram("wscr", [NCH, 128, 4096], BF16, "Internal")

        cb = S.sb("cb", [128, NCB], BF16)
        cf = S.sb("cf", [128, NCF], F32)
        pf = S.sb("pf", [128, NPV * 8], F32)
        pd = S.sb("pd", [128, 12 * 8], F32)
        gmb = S.sb("gmb", [128, 1024], BF16)
        gfb = S.sb("gfb", [128, 1024], BF16)
        l1a = S.sb("l1a", [128, 8, 288], BF16)
        l1b = S.sb("l1b", [128, 8, 288], BF16)
        l2 = S.sb("l2", [128, 3, 1024], BF16)
        ring = [S.sb("ring%d" % i, [128, 4096], BF16) for i in range(NSLOT)]
        xt1 = S.sb("xt", [128, NSUB, 1024], F32)
        xt = [xt1, xt1]
        nb = S.sb("nb", [128, NSUB, 1024], BF16)
        junk = nb[:, 0, :]
        st4 = S.sb("st4", [128, 16], F32)
        hT = S.sb("hT", [128, 8, TT + 1], BF16)
        h2T = S.sb("h2T", [128, 8, TT], BF16)
        mixT = h2T
        Zf = S.sb("Zf", [128, 8, 64], F32)
        Zb = S.sb("Zb", [128, 8, 2, 64], BF16)
        hal = S.sb("hal", [128, 8, 3], F32)
        tp = [S.sb("tp%d" % i, [128, TT + 1], F32) for i in range(12)]
        tv = lambda i: tp[i][:, 0:TT]
        pj = [tp[0], tp[0], tp[0]]
        tmpd = tv(1)
        rkv = [tv(2), tv(3), tv(4)]
        sw = tv(5); asig = tv(6); gg = tv(7); cs = tv(0); cm = tv(1)
        Ep = tv(8); En = tv(9); Em = tv(10); rinv = tv(1); kkb = tv(11); ff = tv(0)
        kmod = tv(5); bv = tv(1); bon = tv(6); yln = tv(9); ysq = tv(10)
        sqb = S.sb("sqb", [128, TT], BF16)
        AR = S.sb("AR", [128, NSUB, 2, 128], BF16)
        Bt = S.sb("Bt", [128, TT], BF16)
        Kt = S.sb("Kt", [128, TT], BF16)
        vbf = S.sb("vbf", [128, TT], BF16)
        Bpad = S.sb("Bpad", [128, NSUB, 2, 128], BF16)
        Kpad = S.sb("Kpad", [128, NSUB, 2, 128], BF16)
        Vtm = S.sb("Vtm", [128, NSUB, 128], BF16)
        AM = S.sb("AM", [128, 4, 512], BF16)
        L0 = S.sb("L0", [128, 4, 128], BF16)
        LP = [S.sb("LP%d" % i, [128, 4, 128], BF16) for i in range(2)]
        LT = [S.sb("LT%d" % i, [128, 4, 128], BF16) for i in range(2)]
        SS = [S.sb("SS%d" % i, [128, 4, 128], BF16) for i in range(2)]
        Xb = S.sb("Xb", [128, 128], BF16)
        Ub = S.sb("Ub", [128, 128], BF16)
        ztmp = S.sb("ztmp", [128, 64], F32)
        Ytm = S.sb("Ytm", [128, NSUB, 128], F32)
        ynb = S.sb("ynb", [128, NSUB, 128], BF16)
        gst = S.sb("gst", [128, 32], F32)
        lw = S.sb("lw", [128, TT], BF16)
        lga = S.sb("lga", [128, TT], BF16)
        lgb = S.sb("lgb", [32, TT], BF16)
        yfin = S.sb("yfin", [128, 8, TT], BF16)
        Qa = [S.sb("Qa%d" % g, [128, 4, TT], BF16) for g in range(3)]
        K1 = S.sb("K1", [128, 4, 128 + TT], BF16)
        V1 = S.sb("V1", [128, 3, 512], BF16)
        K2c = S.sb("K2c", [128, 4, TT], BF16)
        K2r = S.sb("K2r", [128, 4, 4, 128], BF16)
        V2c = S.sb("V2c", [64, 4, 128], BF16)
        V2r = S.sb("V2r", [128, 4, 512], BF16)
        K3c = S.sb("K3c", [128, 4, TT], BF16)
        K3r = S.sb("K3r", [128, 4, 16, 128], BF16)
        V3c = S.sb("V3c", [16, 16, 128], BF16)
        V3r = S.sb("V3r", [128, 16, 512], BF16)
        VF = S.sb("VF", [128, 4, TT], BF16)
        pe = S.sb("pe", [128, 512], BF16)
        pp_ = S.sb("pp", [128, 512], BF16)
        peb = S.sb("peb", [64, 256], BF16)
        ppb = S.sb("ppb", [64, 256], BF16)
        accO = S.sb("accO", [64, TT], F32)
        accD = S.sb("accD", [64, TT], F32)
        oT = S.sb("oT", [64, 8, TT], BF16)
        sa = tv(2); sbb = tv(3); sg = tv(4); utmp = tv(10)
        actT = S.sb("actT", [128, 8, TT], BF16)
        VF2 = S.sb("VF2", [128, 4, TT], BF16)
        wst = xt1[:].re("p s c -> p (s c)")

        _b0 = S.ps("b0", [128, 512])
        _b4 = S.ps("b4", [128, 512])
        _bS = S.ps("bS", [128, 1024])
        B3 = S.ps("b3", [128, 512])
        B5 = S.ps("b5", [128, 512])
        B6 = S.ps("b6", [128, 512])
        _pT = S.ps("pT", [128, 1024], BF16)
        R0 = SubBuf(_b0, 0); R1 = SubBuf(_b0, 256)
        Q0 = SubBuf(_b4, 0); Q1 = SubBuf(_b4, 256)
        B1 = Buf("B1", _bS.t[:, 0:512]); B2 = Buf("B2", _bS.t[:, 512:1024])
        pTr = SubBuf(_pT, 0); pTa = SubBuf(_pT, 512)
        for b_ in (_b0, _b4, B1, B2, B3, B5, B6, _pT):
            b_.psum = True
        pC = B3
        prot = {"r": [R0, R1], "a": [Q0, Q1], "x": [R0, R1, Q0, Q1]}
        prot_i = {"r": 0, "a": 0, "x": 0}

        def nextp(k="x"):
            prot_i[k] = (prot_i[k] + 1) % len(prot[k])
            return prot[k][prot_i[k]]

        mm = lambda **kw: S.op("tensor", "matmul", **kw)
        tr = lambda **kw: S.op("tensor", "transpose", **kw)
        act = lambda **kw: S.op("scalar", "activation", **kw)
        vec = lambda m, **kw: S.op("vector", m, **kw)
        gps = lambda m, **kw: S.op("gpsimd", m, **kw)

        def sigmoid_to(dst, src, nbias=None, scale=1.0):
            if nbias is None:
                act(out=dst, in_=src, func=AF.Exp, scale=-scale)
            else:
                act(out=dst, in_=src, func=AF.Exp, scale=-scale, bias=nbias)
            act(out=dst, in_=dst, func=AF.Ln, bias=one_c_for(dst))
            act(out=dst, in_=dst, func=AF.Exp, scale=-1.0)

        def one_c_for(v):
            lo = v.ap.base_partition()
            n = v.ap.partition_size()
            return cf[lo:lo + n, CF_EPS + 3:CF_EPS + 4]

        def rsqrt_to(dst, src, bias_ap, scale=1.0):
            act(out=dst, in_=src, func=AF.Ln, bias=bias_ap, scale=scale)
            act(out=dst, in_=dst, func=AF.Exp, scale=-0.5)

        ident = cb[:, CB_ID:CB_ID + 128]
        identf = cf[:, CF_IDF:CF_IDF + 128]
        onesbd = cb[:, CB_ONESBD:CB_ONESBD + 128]
        eps_r = cf[:, CF_EPS:CF_EPS + 1]
        eps_g = cf[:, CF_EPS + 1:CF_EPS + 2]
        zero_c = cf[:, CF_EPS + 2:CF_EPS + 3]
        one_c = cf[:, CF_EPS + 3:CF_EPS + 4]

        def pv(i, kc):
            return pf[:, i * 8 + kc: i * 8 + kc + 1]

        def pdv(i, kc):
            return pd[:, i * 8 + kc: i * 8 + kc + 1]
        PD_A1, PD_A2, PD_GM, PD_GF, PD_OMK, PD_OMR, PD_OMKm, PD_OMV = range(8)

        try:
            S.dma("gpsimd", out=cb[:, :], in_=cb_d[:, :])
            S.dma("sync", out=cf[:, :], in_=cf_d[:, :])
            S.dma("sync", out=pf[:, :], in_=pfm_d[:, :])
            for i in range(NCH):
                S.dma("gpsimd", out=wscr[i], in_=wsrc[i])
            CK('dma0')
            for b_ in (Zf, Zb, hal, Bpad, Kpad, K1, K2r, V2r, K3r, V3r, V1, hT, Xb, Ub, Vtm):
                gps("memset", ap=b_[:], constant=0.0)
            CK('memset')
            for half in range(2):
                S.dma("sync", out=wst[:, 0:4 * 288], in_=l1_d[:, half * 4 * 288:(half + 1) * 4 * 288])
                w1v = wst[:, 0:4 * 288].re("p (k c) -> p k c", c=288)
                for k4 in range(4):
                    kc = half * 4 + k4
                    for (lo, hi, mui) in ((0, 64, PV_MUW), (64, 128, PV_MUA), (128, 288, PV_MUG)):
                        vec("tensor_scalar", out=l1b[:, kc, lo:hi], in0=w1v[:, k4, lo:hi], scalar1=pv(mui, kc),
                            scalar2=None, op0=ALU.mult)
                        vec("tensor_tensor", out=l1a[:, kc, lo:hi], in0=w1v[:, k4, lo:hi], in1=l1b[:, kc, lo:hi],
                            op=ALU.subtract)
            for half in range(2):
                S.dma("sync", out=wst[:, 0:1536], in_=l2_d[:, half * 1536:(half + 1) * 1536])
                vec("tensor_copy", out=l2[:].re("p a c -> p (a c)")[:, half * 1536:(half + 1) * 1536], in_=wst[:, 0:1536])
            CK('lora0')
            for j in range(24):
                S.dma("sync", out=wst[:, 0:2048], in_=wmod_d[j])
                wv = wst[:, 0:2048].re("p (k c) -> p k c", c=256)
                for cc in range(2):
                    col = j * 2 + cc
                    for kc in range(8):
                        mm(out=B1[:, col:col + 1], lhsT=wv[:, kc, cc * 128:(cc + 1) * 128], rhs=pv(PV_C, kc),
                           start=(kc == 0), stop=(kc == 7))
            modf = S.sb("modf", [128, 48], F32)
            vec("tensor_tensor", out=modf[:, :], in0=B1[:, 0:48], in1=pf[:, 0:48], op=ALU.add)
            for kc in range(8):
                vec("scalar_tensor_tensor", out=pdv(PD_A1, kc), in0=modf[:, 8 + kc:9 + kc], scalar=1.0,
                    in1=pv(PV_GPM, kc), op0=ALU.add, op1=ALU.mult)
                vec("scalar_tensor_tensor", out=pdv(PD_A2, kc), in0=modf[:, 32 + kc:33 + kc], scalar=1.0,
                    in1=pv(PV_GPF, kc), op0=ALU.add, op1=ALU.mult)
                vec("tensor_tensor", out=pdv(PD_GM, kc), in0=modf[:, 16 + kc:17 + kc], in1=pv(PV_GQM, kc), op=ALU.mult)
                vec("tensor_tensor", out=pdv(PD_GF, kc), in0=modf[:, 40 + kc:41 + kc], in1=pv(PV_GQF, kc), op=ALU.mult)
                vec("tensor_scalar", out=pdv(PD_OMK, kc), in0=pv(PV_KA, kc), scalar1=-1.0, scalar2=1.0,
                    op0=ALU.mult, op1=ALU.add)
                vec("tensor_scalar", out=pdv(5, kc), in0=pv(PV_W0, kc), scalar1=-1.0, scalar2=None, op0=ALU.mult)
                vec("tensor_scalar", out=pdv(6, kc), in0=pv(PV_A0, kc), scalar1=-1.0, scalar2=None, op0=ALU.mult)
            dg = S.sb("dg", [128, 128], F32)
            onesf = S.sb("onesf", [128, 128], F32)
            gps("memset", ap=onesf[:, :], constant=1.0)
            for (pdi, dst) in ((PD_GM, gmb), (PD_GF, gfb)):
                for kc in range(8):
                    vec("tensor_scalar", out=dg[:, :], in0=identf, scalar1=pdv(pdi, kc), scalar2=None, op0=ALU.mult)
                    pz = nextp()
                    mm(out=pz[:, 0:128], lhsT=onesf[:, :], rhs=dg[:, :], start=True, stop=True)
                    act(out=dst[:, kc * 128:(kc + 1) * 128], in_=pz[:, 0:128], func=AF.Copy)

            CK('startup')
            ring_i = [0]

            ring_sets = {"r": ring[0:2], "a": ring[2:4], "x": ring}
            ring_k = {"r": 0, "a": 0, "x": 0}

            def wload(ch, k="x"):
                s = ring_sets[k][ring_k[k] % len(ring_sets[k])]
                ring_k[k] += 1
                S.dma("sync", out=s[:, :], in_=wscr[ch])
                return s

            def rms_rstd(src3, dst_cols, nsub=NSUB):
                for sub in range(nsub):
                    act(out=junk[:, :], in_=src3[:, sub, :], func=AF.Square,
                        accum_out=st4[:, 8 + sub:9 + sub])
                rsqrt_to(st4[:, dst_cols:dst_cols + nsub], st4[:, 8:8 + nsub], eps_r, 1.0 / D)

            def norm_transpose(xsrc, rcol, dstT, a_idx, b_view_fn, halo):
                for sub in range(NSUB):
                    vec("tensor_scalar", out=nb[:, sub, :], in0=xsrc[:, sub, :], scalar1=st4[:, rcol + sub:rcol + sub + 1],
                        scalar2=None, op0=ALU.mult)
                for kc in range(8):
                    for sub in range(NSUB):
                        tr(out=pTr[:, (kc % 2) * 256 + sub * 128:(kc % 2) * 256 + (sub + 1) * 128], in_=nb[:, sub, kc * 128:(kc + 1) * 128], identity=ident)
                    act(out=dstT[:, kc, halo:halo + TT], in_=pTr[:, (kc % 2) * 256:(kc % 2) * 256 + TT], func=AF.Identity,
                        scale=pdv(a_idx, kc), bias=b_view_fn(kc))

            for m in range(nt):
                phaseB = m >= PB0
                xm = xt[m % 2]
                vcur = cf[:, CF_VM + m:CF_VM + m + 1]
                vprev = cf[:, CF_VM + 32 + m:CF_VM + 33 + m]
                vr2 = cf[:, CF_VM + 64 + m:CF_VM + 65 + m]
                vr3 = cf[:, CF_VM + 96 + m:CF_VM + 97 + m]
                S.dma("gpsimd", out=xm[:], in_=xv.v(xv.t[m * TT:(m + 1) * TT, :].rearrange("(s p) c -> p s c", p=128)))
                if m > 0:
                    vec("tensor_scalar", out=hT[:, :, 0:1], in0=hT[:, :, TT:TT + 1],
                        scalar1=cf[:, CF_VM + m - 1:CF_VM + m], scalar2=None, op0=ALU.mult)
                rms_rstd(xm, 0)
                norm_transpose(xm, 0, hT, PD_A1, lambda kc: pf[:, PV_SHM * 8 + kc:PV_SHM * 8 + kc + 1]
                               if False else modf[:, kc:kc + 1], 1)

                CK('stage1')
                pz = nextp()
                for kc in range(8):
                    mm(out=pz[:, 0:TT], lhsT=l1a[:, kc, 0:128], rhs=hT[:, kc, 1:TT + 1], start=(kc == 0), stop=False)
                    mm(out=pz[:, 0:TT], lhsT=l1b[:, kc, 0:128], rhs=hT[:, kc, 0:TT], start=False, stop=(kc == 7))
                sigmoid_to(tp[11][0:64, 0:TT], pz[0:64, 0:TT], None, 2.0)
                vec("tensor_scalar", out=lw[0:64, :], in0=tp[11][0:64, 0:TT], scalar1=2.0, scalar2=-1.0, op0=ALU.mult, op1=ALU.add)
                act(out=lw[64:128, :], in_=pz[64:128, 0:TT], func=AF.Copy)
                pz = nextp()
                for kc in range(8):
                    mm(out=pz[:, 0:TT], lhsT=l1a[:, kc, 128:256], rhs=hT[:, kc, 1:TT + 1], start=(kc == 0), stop=False)
                    mm(out=pz[:, 0:TT], lhsT=l1b[:, kc, 128:256], rhs=hT[:, kc, 0:TT], start=False, stop=(kc == 7))
                sigmoid_to(tp[11][:, 0:TT], pz[:, 0:TT])
                act(out=lga[:, :], in_=tp[11][:, 0:TT], func=AF.Copy)
                pz = nextp()
                for kc in range(8):
                    mm(out=pz[0:32, 0:TT], lhsT=l1a[:, kc, 256:288], rhs=hT[:, kc, 1:TT + 1], start=(kc == 0), stop=False)
                    mm(out=pz[0:32, 0:TT], lhsT=l1b[:, kc, 256:288], rhs=hT[:, kc, 0:TT], start=False, stop=(kc == 7))
                sigmoid_to(tp[11][0:32, 0:TT], pz[0:32, 0:TT])
                act(out=lgb[0:32, :], in_=tp[11][0:32, 0:TT], func=AF.Copy)

                CK('lora1')
                def th_rwkv():
                    for c in range(8):
                        csl = slice(c * 128, (c + 1) * 128)
                        wr = wload(CH_RW + c, 'r')
                        wrv = wr[:, :].re("p (k c) -> p k c", c=512)
                        for j in range(3):
                            pz = nextp("r")
                            for kc in range(8):
                                mm(out=pz[:, 0:TT], lhsT=wrv[:, kc, j * 128:(j + 1) * 128], rhs=hT[:, kc, 1:TT + 1],
                                   start=(kc == 0), stop=(kc == 7))
                            vec("tensor_copy", out=pj[j][:, 0:1], in_=hal[:, c, j:j + 1])
                            act(out=pj[j][:, 1:TT + 1], in_=pz[:, 0:TT], func=AF.Copy)
                            vec("tensor_scalar", out=hal[:, c, j:j + 1], in0=pj[j][:, TT:TT + 1], scalar1=vcur,
                                scalar2=None, op0=ALU.mult)
                            vec("tensor_tensor", out=tmpd[:, :], in0=pj[j][:, 0:TT], in1=pj[j][:, 1:TT + 1], op=ALU.subtract)
                            vec("scalar_tensor_tensor", out=rkv[j][:, :], in0=tmpd[:, :], scalar=pv(PV_MUR + j, c),
                                in1=pj[j][:, 1:TT + 1], op0=ALU.mult, op1=ALU.add)
                            yield
                        r_, k_, v_ = rkv
                        vec("tensor_scalar", out=v_[:, :], in0=v_[:, :], scalar1=vcur, scalar2=None, op0=ALU.mult)
                        act(out=vbf[:, :], in_=v_[:, :], func=AF.Copy)
                        pz = nextp("r")
                        mm(out=pz[:, 0:TT], lhsT=l2[0:64, 0, csl], rhs=lw[0:64, :], start=True, stop=True)
                        sigmoid_to(sw[:, :], pz[:, 0:TT], pdv(5, c))
                        pz = nextp("r")
                        mm(out=pz[:, 0:TT], lhsT=l2[64:128, 0, csl], rhs=lw[64:128, :], start=True, stop=True)
                        sigmoid_to(asig[:, :], pz[:, 0:TT], pdv(6, c))
                        pz = nextp("r")
                        mm(out=pz[:, 0:TT], lhsT=l2[:, 1, csl], rhs=lga[:, :], start=True, stop=False)
                        mm(out=pz[:, 0:TT], lhsT=l2[0:32, 2, csl], rhs=lgb[0:32, :], start=False, stop=True)
                        act(out=gg[:, :], in_=pz[:, 0:TT], func=AF.Copy)
                        yield
                        vec("tensor_tensor_scan", out=cs[:, :], data0=cf[:, CF_MSK:CF_MSK + TT], data1=sw[:, :],
                            initial=0.0, op0=ALU.mult, op1=ALU.add)
                        vec("tensor_tensor", out=cm[:, :], in0=cs[:, :], in1=sw[:, :], op=ALU.subtract)
                        act(out=Ep[:, :], in_=cs[:, :], func=AF.Exp, scale=-C0)
                        act(out=En[:, :], in_=cs[:, :], func=AF.Exp, scale=C0)
                        act(out=Em[:, :], in_=cm[:, :], func=AF.Exp, scale=-C0)
                        yield
                        act(out=sqb[:, :], in_=k_[:, :], func=AF.Square, scale=pv(PV_KK, c))
                        pz = nextp("r")
                        mm(out=pz[:, 0:TT], lhsT=onesbd, rhs=sqb[:, :], start=True, stop=True)
                        vec("tensor_scalar", out=rinv[:, :], in0=pz[:, 0:TT], scalar1=1e-18, scalar2=None, op0=ALU.max)
                        act(out=rinv[:, :], in_=rinv[:, :], func=AF.Ln)
                        act(out=rinv[:, :], in_=rinv[:, :], func=AF.Exp, scale=-0.5)
                        vec("scalar_tensor_tensor", out=kkb[:, :], in0=k_[:, :], scalar=pv(PV_KK, c), in1=rinv[:, :],
                            op0=ALU.mult, op1=ALU.mult)
                        yield
                        vec("tensor_scalar", out=ff[:, :], in0=asig[:, :], scalar1=pv(PV_KA, c), scalar2=pdv(PD_OMK, c),
                            op0=ALU.mult, op1=ALU.add)
                        vec("tensor_tensor", out=kmod[:, :], in0=k_[:, :], in1=ff[:, :], op=ALU.mult)
                        vec("tensor_tensor", out=bv[:, :], in0=kkb[:, :], in1=asig[:, :], op=ALU.mult)
                        vec("scalar_tensor_tensor", out=AR[:, :, 0, :], in0=kkb[:, :].re("p (s t) -> p s t", t=128), scalar=-1.0,
                            in1=Em[:, :].re("p (s t) -> p s t", t=128), op0=ALU.mult, op1=ALU.mult)
                        vec("tensor_tensor", out=AR[:, :, 1, :], in0=r_[:, :].re("p (s t) -> p s t", t=128),
                            in1=Ep[:, :].re("p (s t) -> p s t", t=128), op=ALU.mult)
                        vec("tensor_tensor", out=Bt[:, :], in0=bv[:, :], in1=En[:, :], op=ALU.mult)
                        vec("tensor_tensor", out=Kt[:, :], in0=kmod[:, :], in1=En[:, :], op=ALU.mult)
                        yield
                        vec("tensor_tensor", out=tmpd[:, :], in0=r_[:, :], in1=kmod[:, :], op=ALU.mult)
                        act(out=sqb[:, :], in_=tmpd[:, :], func=AF.Copy, scale=pv(PV_RK, c))
                        pz = nextp("r")
                        mm(out=pz[:, 0:TT], lhsT=onesbd, rhs=sqb[:, :], start=True, stop=True)
                        vec("tensor_tensor", out=bon[:, :], in0=pz[:, 0:TT], in1=v_[:, :], op=ALU.mult)
                        yield
                        for qi, (src, dst) in enumerate(((Bt, Bpad), (Kt, Kpad), (vbf, None))):
                            for sub in range(NSUB):
                                tr(out=pTr[:, (qi % 2) * 256 + sub * 128: (qi % 2) * 256 + (sub + 1) * 128],
                                   in_=src[:, sub * 128:(sub + 1) * 128], identity=ident)
                            yield
                            srcv = pTr[:, (qi % 2) * 256:(qi % 2 + 1) * 256]
                            if dst is None:
                                act(out=Vtm[:].re("p s c -> p (s c)"), in_=srcv, func=AF.Copy)
                            else:
                                for h in range(2):
                                    act(out=dst[:, :, h, h * 64:(h + 1) * 64],
                                        in_=srcv.re("p (s c) -> p s c", c=128)[:, :, h * 64:(h + 1) * 64], func=AF.Copy)
                        CK('rwkv_a')
                        for h in range(2):
                            hs = slice(h * 64, (h + 1) * 64)
                            for sub in range(NSUB):
                                u = h * NSUB + sub
                                tsl = slice(sub * 128, (sub + 1) * 128)
                                pz = (B1, B2)[u % 2]
                                mm(out=pz[:, 0:256], lhsT=Bt[hs, tsl], rhs=AR[hs, sub, :, :].re("p a t -> p (a t)"),
                                   start=True, stop=True)
                                mm(out=pz[:, 256:512], lhsT=Kt[hs, tsl], rhs=AR[hs, sub, :, :].re("p a t -> p (a t)"),
                                   start=True, stop=True)
                                vec("tensor_tensor", out=AM[:, u, :], in0=pz[:, :], in1=cb[:, CB_MT4:CB_MT4 + 512], op=ALU.mult)
                                mm(out=pC[:, u * 128:(u + 1) * 128], lhsT=AR[hs, sub, 0, :], rhs=Bt[hs, tsl],
                                   start=True, stop=True)
                                yield
                        vec("tensor_tensor", out=L0[:].re("p u t -> p (u t)"), in0=pC[:, :], in1=cb[:, CB_ML4:CB_ML4 + 512],
                            op=ALU.mult)
                        yield
                        CK('rwkv_b')
                        vec("tensor_tensor", out=SS[0][:], in0=AM[:, :, 0:128], in1=ident.bc(1, [128, 4, 128]), op=ALU.add)
                        lt_prev = lambda u: AM[:, u, 0:128]
                        lp_prev = lambda u: L0[:, u, :]
                        scur = 0
                        for lev in range(1, 6):
                            lpn = LP[lev % 2]
                            ltn = LT[lev % 2]
                            for u in range(4):
                                mm(out=B1[:, u * 128:(u + 1) * 128], lhsT=lt_prev(u), rhs=lp_prev(u), start=True, stop=True)
                            if lev <= 4:
                                for u in range(4):
                                    mm(out=B2[:, u * 128:(u + 1) * 128], lhsT=lp_prev(u), rhs=lt_prev(u),
                                       start=True, stop=True)
                            act(out=lpn[:].re("p u t -> p (u t)"), in_=B1[:, :], func=AF.Copy)
                            if lev <= 4:
                                act(out=ltn[:].re("p u t -> p (u t)"), in_=B2[:, :], func=AF.Copy)
                            yield
                            for u in range(4):
                                mm(out=B3[:, u * 128:(u + 1) * 128], lhsT=lpn[:, u, :], rhs=SS[scur][:, u, :], start=True, stop=True)
                            vec("tensor_tensor", out=SS[1 - scur][:].re("p u t -> p (u t)"), in0=B3[:, :],
                                in1=SS[scur][:].re("p u t -> p (u t)"), op=ALU.add)
                            scur = 1 - scur
                            lt_prev = (lambda b: (lambda u: b[:, u, :]))(ltn)
                            yield
                            lp_prev = (lambda b: (lambda u: b[:, u, :]))(lpn)
                        TTm = SS[scur]
                        CK('rwkv_c')
                        for q in range(2 * NSUB):
                            sub, half = q // 2, q % 2
                            ps_ = slice(half * 64, half * 64 + 64)
                            tsl = slice(sub * 128, (sub + 1) * 128)
                            zi = q % 2
                            for h in range(2):
                                hs = slice(h * 64, (h + 1) * 64)
                                u = h * NSUB + sub
                                mm(out=pC[:, hs], lhsT=AR[hs, sub, 0, :], rhs=Zb[hs, c, zi, :], start=True, stop=False)
                                mm(out=pC[:, hs], lhsT=AM[:, u, 256:384], rhs=Vtm[:, sub, hs], start=False, stop=True)
                            act(out=Xb[ps_, :], in_=pC[ps_, 0:128], func=AF.Copy)
                            yield
                            CK('c1')
                            for h in range(2):
                                hs = slice(h * 64, (h + 1) * 64)
                                u = h * NSUB + sub
                                mm(out=pC[:, 128 + h * 64:128 + (h + 1) * 64], lhsT=TTm[ps_, u, :], rhs=Xb[ps_, hs],
                                   start=True, stop=True)
                            vec("tensor_copy", out=Ub[ps_, :], in_=pC[ps_, 128:256])
                            yield
                            CK('c2')
                            for h in range(2):
                                hs = slice(h * 64, (h + 1) * 64)
                                u = h * NSUB + sub
                                o_ = slice(256 + h * 64, 256 + (h + 1) * 64)
                                mm(out=pC[:, o_], lhsT=AR[hs, sub, 1, :], rhs=Zb[hs, c, zi, :], start=True, stop=False)
                                mm(out=pC[:, o_], lhsT=AM[:, u, 128:256], rhs=Ub[:, hs], start=False, stop=False)
                                mm(out=pC[:, o_], lhsT=AM[:, u, 384:512], rhs=Vtm[:, sub, hs], start=False, stop=True)
                            act(out=Ytm[ps_, sub, :], in_=pC[ps_, 256:384], func=AF.Copy)
                            yield
                            CK('c3')
                            for h in range(2):
                                hs = slice(h * 64, (h + 1) * 64)
                                mm(out=pC[:, 384:448], lhsT=Bpad[ps_, sub, h, :], rhs=Ub[ps_, hs], start=(h == 0), stop=False)
                                mm(out=pC[:, 384:448], lhsT=Kpad[ps_, sub, h, :], rhs=Vtm[ps_, sub, hs], start=False, stop=(h == 1))
                            CK('c4')
                            pcv = Ep[:, q * 64 + 63:q * 64 + 64]
                            vec("tensor_scalar", out=ztmp[:, :], in0=Zf[:, c, :], scalar1=pcv, scalar2=None, op0=ALU.mult)
                            vec("scalar_tensor_tensor", out=Zf[:, c, :], in0=pC[:, 384:448], scalar=pcv, in1=ztmp[:, :],
                                op0=ALU.mult, op1=ALU.add)
                            act(out=Zb[:, c, 1 - zi, :], in_=Zf[:, c, :], func=AF.Copy)
                            yield
                        CK('rwkv_d')
                        yv = Ytm[:].re("p s (h i) -> p (s h) i", i=64)
                        vec("tensor_reduce", out=gst[:, 0:4], in_=yv, axis=AX.X, op=ALU.add)
                        act(out=ysq[:, :], in_=Ytm[:].re("p s c -> p (s c)"), func=AF.Square)
                        vec("tensor_reduce", out=gst[:, 4:8], in_=ysq[:, :].re("p (g i) -> p g i", i=64), axis=AX.X, op=ALU.add)
                        vec("tensor_scalar", out=gst[:, 8:12], in0=gst[:, 0:4], scalar1=1.0 / 64, scalar2=None, op0=ALU.mult)
                        vec("tensor_tensor", out=gst[:, 12:16], in0=gst[:, 8:12], in1=gst[:, 8:12], op=ALU.mult)
                        vec("scalar_tensor_tensor", out=gst[:, 16:20], in0=gst[:, 4:8], scalar=1.0 / 64, in1=gst[:, 12:16],
                            op0=ALU.mult, op1=ALU.subtract)
                        rsqrt_to(gst[:, 24:28], gst[:, 16:20], eps_g, 1.0)
                        ysv = ysq[:, :].re("p (g i) -> p g i", i=64)
                        vec("tensor_tensor", out=ysv, in0=yv, in1=gst[:, 8:12].bc(2, [128, 4, 64]), op=ALU.subtract)
                        vec("tensor_tensor", out=ynb[:].re("p s (h i) -> p (s h) i", i=64), in0=ysv,
                            in1=gst[:, 24:28].bc(2, [128, 4, 64]), op=ALU.mult)
                        yield
                        for sub in range(NSUB):
                            tr(out=pTr[:, 256 + sub * 128:256 + (sub + 1) * 128], in_=ynb[:, sub, :], identity=ident)
                        act(out=yln[:, :], in_=pTr[:, 256:256 + TT], func=AF.Identity, scale=pv(PV_LNW, c), bias=pv(PV_LNB, c))
                        vec("tensor_tensor", out=yln[:, :], in0=yln[:, :], in1=bon[:, :], op=ALU.add)
                        vec("tensor_tensor", out=yfin[:, c, :], in0=yln[:, :], in1=gg[:, :], op=ALU.mult)
                        yield


                def th_attn():
                    j0_2 = m % 2
                    j0_3 = m % 8
                    for g in range(3):
                        kdst = (K1, K2c, K3c)[g]
                        for j in range(3):
                            wa = wload(CH_AT + g * 3 + j, 'a')
                            wav = wa[:, :].re("p (k c) -> p k c", c=512)
                            for cc in range(4):
                                pz = nextp("a")
                                for kc in range(8):
                                    mm(out=pz[:, 0:TT], lhsT=wav[:, kc, cc * 128:(cc + 1) * 128], rhs=hT[:, kc, 1:TT + 1],
                                       start=(kc == 0), stop=(kc == 7))
                                if j == 0:
                                    act(out=Qa[g][:, cc, :], in_=pz[:, 0:TT], func=AF.Copy, scale=0.125)
                                elif j == 1:
                                    if g == 0:
                                        act(out=K1[:, cc, 128:128 + TT], in_=pz[:, 0:TT], func=AF.Copy)
                                    else:
                                        act(out=kdst[:, cc, :], in_=pz[:, 0:TT], func=AF.Copy)
                                else:
                                    act(out=VF[:, cc, :], in_=pz[:, 0:TT], func=AF.Copy)
                            yield
                        if g == 0:
                            for blk in range(2):
                                for cc in range(4):
                                    tr(out=pTa[:, cc * 128:(cc + 1) * 128], in_=VF[:, cc, blk * 128:(blk + 1) * 128], identity=ident)
                                vec("tensor_copy", out=V1[:, 1 + blk, :], in_=pTa[:, 0:512])
                            yield
                        elif g == 1:
                            vec("tensor_copy", out=VF2[:], in_=VF[:])
                    CK('attn_proj')
                    for h in range(8):
                        cc, hp = h // 2, (h % 2) * 64
                        hs = slice(hp, hp + 64)
                        vs = slice(h * 64, (h + 1) * 64)
                        vl = slice(hp, hp + 64)
                        if h % 2 == 0:
                            for r in range(4):
                                tr(out=pTa[0:64, r * 128:(r + 1) * 128],
                                   in_=VF2[:, cc, :].re("p (i r) -> p r i", r=4)[:, r, :], identity=ident)
                            vec("tensor_copy", out=V2c[0:64, :, :].re("p r c -> p (r c)"), in_=pTa[0:64, 0:512])
                            for r in range(16):
                                tr(out=pTa[0:16, (r % 4) * 128:(r % 4 + 1) * 128],
                                   in_=VF[:, cc, :].re("p (i r) -> p r i", r=16)[:, r, :], identity=ident)
                                if r % 4 == 3:
                                    vec("tensor_copy", out=V3c[0:16, r - 3:r + 1, :].re("p a c -> p (a c)"), in_=pTa[0:16, 0:512])
                                yield
                        for blk in range(2):
                            qv = Qa[0][hs, cc, blk * 128:(blk + 1) * 128]
                            mm(out=B5[:, (blk * 2) * 128:(blk * 2 + 1) * 128], lhsT=K1[hs, cc, blk * 128:(blk + 1) * 128],
                               rhs=qv, start=True, stop=True)
                            mm(out=B5[:, (blk * 2 + 1) * 128:(blk * 2 + 2) * 128],
                               lhsT=K1[hs, cc, 128 + blk * 128:128 + (blk + 1) * 128], rhs=qv, start=True, stop=True)
                        act(out=pe[:, :], in_=B5[:, 0:512], func=AF.Exp)
                        yield
                        vec("tensor_tensor", out=pp_[:, :].re("p (b e) -> p b e", b=2), in0=pe[:, :].re("p (b e) -> p b e", b=2),
                            in1=cb[:, CB_E1 + h * 256:CB_E1 + (h + 1) * 256].bc(1, [128, 2, 256]), op=ALU.mult)
                        vec("tensor_scalar", out=pp_[:, 0:128], in0=pp_[:, 0:128], scalar1=vprev, scalar2=None, op0=ALU.mult)
                        yield
                        for blk in range(2):
                            mm(out=B6[0:64, blk * 128:(blk + 1) * 128], lhsT=V1[:, blk, vs],
                               rhs=pp_[:, (blk * 2) * 128:(blk * 2 + 1) * 128], start=True, stop=False)
                            mm(out=B6[0:64, blk * 128:(blk + 1) * 128], lhsT=V1[:, blk + 1, vs],
                               rhs=pp_[:, (blk * 2 + 1) * 128:(blk * 2 + 2) * 128], start=False, stop=True)
                        ppv = pp_[:, :].re("p (b c q) -> p b c q", b=2, c=2)
                        mm(out=B6[0:64, 256:512], lhsT=cb[:, CB_ONES:CB_ONES + 64], rhs=ppv[:, :, 0, :], start=True, stop=False)
                        mm(out=B6[0:64, 256:512], lhsT=cb[:, CB_ONES:CB_ONES + 64], rhs=ppv[:, :, 1, :], start=False, stop=True)
                        act(out=accO[:, :], in_=B6[0:64, 0:TT], func=AF.Copy)
                        act(out=accD[:, :], in_=B6[0:64, 256:512], func=AF.Copy)
                        yield
                        for r in range(4):
                            qv = Qa[1][hs, cc, :].re("p (i r) -> p r i", r=4)[:, r, :]
                            mm(out=B5[:, r * 64:(r + 1) * 64], lhsT=K2r[hs, cc, r, :], rhs=qv, start=True, stop=True)
                            mm(out=B5[0:64, 256 + r * 64:256 + (r + 1) * 64],
                               lhsT=K2c[hs, cc, :].re("p (i r) -> p r i", r=4)[:, r, :], rhs=qv, start=True, stop=True)
                        act(out=pe[:, 0:256], in_=B5[:, 0:256], func=AF.Exp)
                        act(out=peb[0:64, :], in_=B5[0:64, 256:512], func=AF.Exp)
                        yield
                        ea = cb[:, CB_EA2 + (j0_2 * 8 + h) * 64:CB_EA2 + (j0_2 * 8 + h + 1) * 64]
                        vec("scalar_tensor_tensor", out=pp_[:, 0:256].re("p (r i) -> p r i", r=4),
                            in0=pe[:, 0:256].re("p (r i) -> p r i", r=4), scalar=vr2, in1=ea.bc(1, [128, 4, 64]),
                            op0=ALU.mult, op1=ALU.mult)
                        eb = cb[0:64, CB_EB2 + h * 64:CB_EB2 + (h + 1) * 64]
                        vec("tensor_tensor", out=ppb[0:64, :].re("p (r i) -> p r i", r=4),
                            in0=peb[0:64, :].re("p (r i) -> p r i", r=4), in1=eb.bc(1, [64, 4, 64]), op=ALU.mult)
                        yield
                        for r in range(4):
                            mm(out=B6[0:64, r * 64:(r + 1) * 64], lhsT=V2r[:, r, vs], rhs=pp_[:, r * 64:(r + 1) * 64],
                               start=True, stop=False)
                            mm(out=B6[0:64, r * 64:(r + 1) * 64], lhsT=V2c[0:64, r, vl], rhs=ppb[0:64, r * 64:(r + 1) * 64],
                               start=False, stop=True)
                        mm(out=B6[0:64, 256:512], lhsT=cb[:, CB_ONES:CB_ONES + 64], rhs=pp_[:, 0:256], start=True, stop=False)
                        mm(out=B6[0:64, 256:512], lhsT=cb[0:64, CB_ONES:CB_ONES + 64], rhs=ppb[0:64, :], start=False, stop=True)
                        vec("tensor_tensor", out=accO[:, :].re("p (i r) -> p r i", r=4), in0=accO[:, :].re("p (i r) -> p r i", r=4),
                            in1=B6[0:64, 0:TT].re("p (r i) -> p r i", r=4), op=ALU.add)
                        vec("tensor_tensor", out=accD[:, :].re("p (i r) -> p r i", r=4), in0=accD[:, :].re("p (i r) -> p r i", r=4),
                            in1=B6[0:64, 256:512].re("p (r i) -> p r i", r=4), op=ALU.add)
                        yield
                        for r in range(16):
                            qv = Qa[2][hs, cc, :].re("p (i r) -> p r i", r=16)[:, r, :]
                            mm(out=B5[:, r * 16:(r + 1) * 16], lhsT=K3r[hs, cc, r, :], rhs=qv, start=True, stop=True)
                            mm(out=B5[0:16, 256 + r * 16:256 + (r + 1) * 16],
                               lhsT=K3c[hs, cc, :].re("p (i r) -> p r i", r=16)[:, r, :], rhs=qv, start=True, stop=True)
                        act(out=pe[:, 0:256], in_=B5[:, 0:256], func=AF.Exp)
                        act(out=peb[0:16, :], in_=B5[0:16, 256:512], func=AF.Exp)
                        yield
                        ea = cb[:, CB_EA3 + (j0_3 * 8 + h) * 16:CB_EA3 + (j0_3 * 8 + h + 1) * 16]
                        vec("scalar_tensor_tensor", out=pp_[:, 0:256].re("p (r i) -> p r i", r=16),
                            in0=pe[:, 0:256].re("p (r i) -> p r i", r=16), scalar=vr3, in1=ea.bc(1, [128, 16, 16]),
                            op0=ALU.mult, op1=ALU.mult)
                        eb = cb[0:16, CB_EB3 + h * 16:CB_EB3 + (h + 1) * 16]
                        vec("tensor_tensor", out=ppb[0:16, :].re("p (r i) -> p r i", r=16),
                            in0=peb[0:16, :].re("p (r i) -> p r i", r=16), in1=eb.bc(1, [16, 16, 16]), op=ALU.mult)
                        yield
                        for r in range(16):
                            mm(out=B6[0:64, r * 16:(r + 1) * 16], lhsT=V3r[:, r, vs], rhs=pp_[:, r * 16:(r + 1) * 16],
                               start=True, stop=False)
                            mm(out=B6[0:64, r * 16:(r + 1) * 16], lhsT=V3c[0:16, r, vl], rhs=ppb[0:16, r * 16:(r + 1) * 16],
                               start=False, stop=True)
                        mm(out=B6[0:64, 256:512], lhsT=cb[:, CB_ONES:CB_ONES + 64], rhs=pp_[:, 0:256], start=True, stop=False)
                        mm(out=B6[0:64, 256:512], lhsT=cb[0:16, CB_ONES:CB_ONES + 64], rhs=ppb[0:16, :], start=False, stop=True)
                        yield
                        if phaseB:
                            vec("tensor_tensor", out=accO[:, :].re("p (i r) -> p r i", r=16),
                                in0=accO[:, :].re("p (i r) -> p r i", r=16),
                                in1=B6[0:64, 0:TT].re("p (r i) -> p r i", r=16), op=ALU.add)
                            vec("tensor_tensor", out=accD[:, :].re("p (i r) -> p r i", r=16),
                                in0=accD[:, :].re("p (i r) -> p r i", r=16),
                                in1=B6[0:64, 256:512].re("p (r i) -> p r i", r=16), op=ALU.add)
                            vec("reciprocal", out=accD[:, :], in_=accD[:, :])
                            vec("tensor_tensor", out=oT[:, h, :], in0=accO[:, :], in1=accD[:, :], op=ALU.mult)
                        if h % 2 == 1:
                            S.dma("gpsimd", out=V2r[j0_2 * 64:(j0_2 + 1) * 64, :, cc * 128:(cc + 1) * 128], in_=V2c[0:64, :, :])
                            S.dma("gpsimd", out=V3r[j0_3 * 16:(j0_3 + 1) * 16, :, cc * 128:(cc + 1) * 128], in_=V3c[0:16, :, :])
                    CK('attn')
                    vec("tensor_copy", out=K1[:, :, 0:128], in_=K1[:, :, TT:TT + 128])
                    vec("tensor_copy", out=V1[:, 0, :], in_=V1[:, 2, :])
                    vec("tensor_copy", out=K2r[:, :, :, j0_2 * 64:(j0_2 + 1) * 64],
                        in_=K2c[:].re("p c (i r) -> p c r i", r=4))
                    vec("tensor_copy", out=K3r[:, :, :, j0_3 * 16:(j0_3 + 1) * 16],
                        in_=K3c[:].re("p c (i r) -> p c r i", r=16))

                ths = [th_rwkv(), th_attn()]
                if not INTERLEAVE:
                    for t_ in ths:
                        for _ in t_:
                            pass
                    ths = []
                while ths:
                    for t_ in list(ths):
                        try:
                            next(t_)
                        except StopIteration:
                            ths.remove(t_)

                if not phaseB:
                    continue
                for cc in range(8):
                    wpb = wload(CH_PB + cc)
                    wv = wpb[:, :].re("p (a k c) -> p a k c", a=4, c=128)
                    pz = nextp()
                    for kc in range(8):
                        mm(out=pz[:, 0:TT], lhsT=wv[:, 0, kc, :], rhs=hT[:, kc, 1:TT + 1], start=(kc == 0), stop=(kc == 7))
                    sigmoid_to(sa[:, :], pz[:, 0:TT])
                    pz = nextp()
                    for kc in range(8):
                        mm(out=pz[:, 0:TT], lhsT=wv[:, 1, kc, :], rhs=hT[:, kc, 1:TT + 1], start=(kc == 0), stop=(kc == 7))
                    sigmoid_to(sbb[:, :], pz[:, 0:TT])
                    pz = nextp()
                    for kc in range(8):
                        mm(out=pz[:, 0:TT], lhsT=wv[:, 2, kc, :], rhs=yfin[:, kc, :], start=(kc == 0), stop=(kc == 7))
                    vec("tensor_tensor", out=sa[:, :], in0=sa[:, :], in1=pz[:, 0:TT], op=ALU.mult)
                    pz = nextp()
                    for hh_ in range(8):
                        mm(out=pz[:, 0:TT], lhsT=wv[0:64, 3, hh_, :], rhs=oT[:, hh_, :], start=(hh_ == 0), stop=(hh_ == 7))
                    vec("tensor_tensor", out=sbb[:, :], in0=sbb[:, :], in1=pz[:, 0:TT], op=ALU.mult)
                    vec("tensor_tensor", out=mixT[:, cc, :], in0=sa[:, :], in1=sbb[:, :], op=ALU.add)

                def norm_residual(ps_views, gb):
                    for hf in range(2):
                        act(out=junk[:, 0:512], in_=ps_views[hf], func=AF.Square, accum_out=st4[:, 8 + hf:9 + hf])
                    vec("tensor_tensor", out=st4[:, 10:11], in0=st4[:, 8:9], in1=st4[:, 9:10], op=ALU.add)
                    rsqrt_to(st4[:, 4:5], st4[:, 10:11], eps_r, 1.0 / D)
                    for hf in range(2):
                        for qq in range(2):
                            cs_ = slice(hf * 512 + qq * 256, hf * 512 + (qq + 1) * 256)
                            vec("scalar_tensor_tensor", out=utmp[:, :], in0=ps_views[hf][:, qq * 256:(qq + 1) * 256],
                                scalar=st4[:, 4:5], in1=gb[:, cs_], op0=ALU.mult, op1=ALU.mult)
                            vec("tensor_tensor", out=xm[:, sub, cs_], in0=xm[:, sub, cs_], in1=utmp[:, :], op=ALU.add)

                wo = [wload(CH_WOUT + 0), wload(CH_WOUT + 1)]
                for sub in range(NSUB):
                    for hf in range(2):
                        wv = wo[hf][:, :].re("p (k c) -> p k c", c=512)
                        for kc in range(8):
                            mm(out=(B1, B2)[hf][:, :], lhsT=mixT[:, kc, sub * 128:(sub + 1) * 128], rhs=wv[:, kc, :],
                               start=(kc == 0), stop=(kc == 7))
                    norm_residual([B1[:, :], B2[:, :]], gmb)
                rms_rstd(xm, 2)
                norm_transpose(xm, 2, h2T, PD_A2, lambda kc: modf[:, 24 + kc:25 + kc], 0)
                accs = [[B1[:, :], B2[:, :]], [B3[:, :], B5[:, :]]]
                for pg in range(3):
                    nk = 8 if pg < 2 else 6
                    for i4 in range(nk // 2):
                        i = pg * 4 + i4
                        wf_ = wload(CH_FF + i)
                        wv = wf_[:, :].re("p (k c) -> p k c", c=512)
                        for jj in range(2):
                            jl = i4 * 2 + jj
                            pg_ = nextp()
                            for kc in range(8):
                                mm(out=pg_[:, 0:TT], lhsT=wv[:, kc, jj * 128:(jj + 1) * 128], rhs=h2T[:, kc, :],
                                   start=(kc == 0), stop=(kc == 7))
                            act(out=sg[:, :], in_=pg_[:, 0:TT], func=AF.Silu)
                            pu = nextp()
                            for kc in range(8):
                                mm(out=pu[:, 0:TT], lhsT=wv[:, kc, 256 + jj * 128:256 + (jj + 1) * 128], rhs=h2T[:, kc, :],
                                   start=(kc == 0), stop=(kc == 7))
                            vec("tensor_tensor", out=actT[:, jl, :], in0=sg[:, :], in1=pu[:, 0:TT], op=ALU.mult)
                    for hf in range(2):
                        wf_ = wload(CH_FO + pg * 2 + hf)
                        wv = wf_[:, :].re("p (k c) -> p k c", c=512)
                        for sub in range(NSUB):
                            for kc in range(nk):
                                mm(out=accs[sub][hf], lhsT=actT[:, kc, sub * 128:(sub + 1) * 128], rhs=wv[:, kc, :],
                                   start=(pg == 0 and kc == 0), stop=(pg == 2 and kc == nk - 1))
                for sub in range(NSUB):
                    norm_residual(accs[sub], gfb)
                r0 = (m - PB0) * TT
                S.dma("gpsimd", out=y_d.v(y_d.t[r0:r0 + TT, :].rearrange("(s p) c -> p s c", p=128)), in_=xm[:])


        except _Stop:
            pass
        S.finish([y_d] + finals)
        S.emit()
    return nc


_CACHE = {}


def prep_inputs(x, c, w_mod, b_mod, g_pre_mix, g_post_mix, g_pre_ffn, g_post_ffn, w_in, mu_rkv, mu_lora,
           w0, w1, w2, a0, a1, a2, g1, g2, k_k, k_a, r_k, ln_x_w, ln_x_b, w_o_rwkv, w_o_attn, w_out,
           w_ffn_in, w_ffn_out):
    f = lambda a: np.asarray(a, np.float32)
    x = f(x); c = f(c)
    w_in = f(w_in)[0]; w_modm = f(w_mod)[0]
    bm = f(b_mod)[0].reshape(6, 1024)
    vecs = [bm[0], bm[1], bm[2], bm[3], bm[4], bm[5], f(g_pre_mix)[0], f(g_post_mix)[0], f(g_pre_ffn)[0],
            f(g_post_ffn)[0], f(mu_rkv)[0, 0], f(mu_rkv)[0, 1], f(mu_rkv)[0, 2], f(mu_lora)[0, 0], f(mu_lora)[0, 1],
            f(mu_lora)[0, 2], f(w0)[0], f(a0)[0], f(k_k)[0], f(k_a)[0], f(r_k)[0].reshape(-1), f(ln_x_w)[0],
            f(ln_x_b)[0]]
    wsrc = np.zeros((NCH, 128, 4096), np.float32)
    def put(i, arr3):
        P, K, C = arr3.shape
        v = wsrc[i].reshape(128, -1)
        tmp = np.zeros((128, K, 4096 // K if K in (8,) else C), np.float32) if False else None
        blk = np.zeros((128, K * C), np.float32)
        blk[:P] = arr3.reshape(P, K * C)
        v[:, :K * C] = blk
    for cch in range(8):
        a = np.zeros((128, 8, 512), np.float32)
        for j in range(3):
            a[:, :, j * 128:(j + 1) * 128] = _wchunk(w_in, slice(j * 1024 + cch * 128, j * 1024 + (cch + 1) * 128))
        put(CH_RW + cch, a)
    for g in range(3):
        for j in range(3):
            o = 3072 + j * 1536 + g * 512
            put(CH_AT + g * 3 + j, _wchunk(w_in, slice(o, o + 512)))
    wor = f(w_o_rwkv)[0]; woa = f(w_o_attn)[0]; wout = f(w_out)[0]
    for cc in range(8):
        cs_ = slice(cc * 128, (cc + 1) * 128)
        a = np.zeros((128, 4, 8, 128), np.float32)
        a[:, 0] = _wchunk(w_in, slice(7680 + cc * 128, 7680 + (cc + 1) * 128))
        a[:, 1] = _wchunk(w_in, slice(8704 + cc * 128, 8704 + (cc + 1) * 128))
        a[:, 2] = _wchunk(wor, cs_)
        a[0:64, 3] = woa[:, cs_].reshape(8, 64, 128).transpose(1, 0, 2)
        put(CH_PB + cc, a.reshape(128, 32, 128))
    for hf in range(2):
        put(CH_WOUT + hf, _wchunk(wout, slice(hf * 512, (hf + 1) * 512)))
    wfi = f(w_ffn_in)[0]; wfo = f(w_ffn_out)[0]
    for i in range(11):
        a = np.zeros((128, 8, 512), np.float32)
        a[:, :, 0:256] = _wchunk(wfi, slice(i * 256, (i + 1) * 256))
        a[:, :, 256:512] = _wchunk(wfi, slice(FH + i * 256, FH + (i + 1) * 256))
        put(CH_FF + i, a)
    for pg in range(3):
        nk = 8 if pg < 2 else 6
        for hf in range(2):
            blk = wfo[pg * 1024:pg * 1024 + nk * 128, hf * 512:(hf + 1) * 512]
            put(CH_FO + pg * 2 + hf, blk.reshape(nk, 128, 512).transpose(1, 0, 2))
    wmod = np.ascontiguousarray(
        w_modm.reshape(8, 128, 24, 256).transpose(2, 1, 0, 3).reshape(24, 128, 2048))
    l1 = np.concatenate([f(w1)[0], f(a1)[0], f(g1)[0]], 1)
    l1 = np.ascontiguousarray(l1.reshape(8, 128, 288).transpose(1, 0, 2).reshape(128, 8 * 288))
    l2 = np.zeros((128, 3, 1024), np.float32)
    l2[0:64, 0] = f(w2)[0]; l2[64:128, 0] = f(a2)[0]
    l2[:, 1] = f(g2)[0][0:128]; l2[0:32, 2] = f(g2)[0][128:160]
    l2 = l2.reshape(128, 3072)
    cbt = _host_consts()
    in_maps = []
    for core in range(8):
        b, hh = core // 2, core % 2
        pfm = np.concatenate([_fm(v) for v in vecs] + [_fm(c[b])], 1)
        if hh == 1:
            xvv = x[b]
        else:
            xvv = np.concatenate([np.zeros((T // 2, D), np.float32), x[b, :T // 2]], 0)
        in_maps.append({"xv": np.ascontiguousarray(xvv), "pfm": np.ascontiguousarray(pfm), "wmod": wmod,
                        "wsrc": wsrc, "l1": l1, "l2": l2, "cbt": cbt, "cft": _host_cf(hh)})
    return in_maps


def kernel(**inputs):
    in_maps = prep_inputs(**inputs)
    if "nc" not in _CACHE:
        _CACHE["nc"] = build()
    nc = _CACHE["nc"]
    res = run_bass_kernel_spmd(nc, in_maps, core_ids=list(range(8)))
    out = np.zeros((4, T, D), np.float32)
    for core in range(8):
        b, hh = core // 2, core % 2
        out[b, hh * (T // 2):(hh + 1) * (T // 2)] = res.results[core]["y"]
    return out
```

```python
import math
from contextlib import ExitStack

import numpy as np
import concourse.bass as bass
import concourse.mybir as mybir
from concourse.bass_utils import run_bass_kernel_spmd

F32 = mybir.dt.float32
BF16 = mybir.dt.bfloat16
AF = mybir.ActivationFunctionType
ALU = mybir.AluOpType
AX = mybir.AxisListType

ENGS = ("tensor", "vector", "scalar", "gpsimd", "sync")

T = 8192
D = 1024
TT = 256
NT = T // TT
PB0 = NT // 2
NSUB = TT // 128
FH = 2816
C0 = math.exp(-0.5)
GN_EPS = 64e-5
RMS_EPS = 1e-6
NSLOT = 4
import os
INTERLEAVE = os.environ.get('NOIL') is None


class Buf:
    def __init__(self, name, t):
        self.name = name
        self.t = t
        self.writer = None
        self.readers = []
        self.dsem = None
        self.dcnt = 0
        self.psum = False

    def __getitem__(self, idx):
        return View(self, self.t[idx])

    def v(self, ap):
        return View(self, ap)


class SubBuf:
    def __init__(self, buf, col0):
        self.buf = buf
        self.col0 = col0

    def __getitem__(self, idx):
        ps, cs = idx
        a = 0 if cs.start is None else cs.start
        assert cs.stop is not None
        return View(self.buf, self.buf.t[ps, self.col0 + a:self.col0 + cs.stop])


class View:
    def __init__(self, buf, ap):
        self.buf = buf
        self.ap = ap

    def __getitem__(self, idx):
        return View(self.buf, self.ap[idx])

    def re(self, pat, **kw):
        return View(self.buf, self.ap.rearrange(pat, **kw))

    def bc(self, axis, shape):
        return View(self.buf, self.ap.unsqueeze(axis).to_broadcast(list(shape)))


def _unw(x):
    return x.ap if isinstance(x, View) else x


class Sched:
    def __init__(self, nc, stack):
        self.nc = nc
        self.stack = stack
        self.q = {e: [] for e in ENGS}
        self.waited = {e: {} for e in ENGS}
        self.dma_sems = []

    def sb(self, name, shape, dt):
        t = self.stack.enter_context(self.nc.sbuf_tensor("s_" + name, list(shape), dt))
        return Buf(name, t)

    def ps(self, name, shape, dt=F32):
        t = self.stack.enter_context(self.nc.psum_tensor("p_" + name, list(shape), dt))
        return Buf(name, t)

    def dram(self, name, shape, dt, kind):
        t = self.nc.dram_tensor(name, list(shape), dt, kind=kind).ap()
        return Buf(name, t)

    def _deps(self, eng, reads, writes):
        deps = {}

        def add(tok):
            if tok is None:
                return
            k, v = tok
            if deps.get(k, 0) < v:
                deps[k] = v

        for b in reads:
            add(b.writer)
            if b.psum:
                for r in b.readers:
                    if r[0] != eng:
                        add(r)
        for b in writes:
            add(b.writer)
            for r in b.readers:
                add(r)
        waits = []
        for k, v in deps.items():
            if k == "tensor" and eng == "tensor":
                continue
            if self.waited[eng].get(k, 0) >= v:
                continue
            self.waited[eng][k] = v
            waits.append((k, v))
            if isinstance(k, str):
                self.q[k][v - 1][2] = True
        return waits

    def _commit(self, tok, reads, writes):
        for b in writes:
            b.writer = tok
            b.readers = []
        for b in reads:
            if b in writes:
                continue
            b.readers.append(tok)
            if len(b.readers) > 48:
                d = {}
                for k, v in b.readers:
                    if d.get(k, 0) < v:
                        d[k] = v
                b.readers = list(d.items())

    def op(self, eng, meth, **kw):
        writes, reads = [], []
        for k, v in kw.items():
            if isinstance(v, View):
                if k in ("out", "accum_out", "ap"):
                    if v.buf not in writes:
                        writes.append(v.buf)
                else:
                    if v.buf not in reads:
                        reads.append(v.buf)
        waits = self._deps(eng, reads, writes)
        if eng == "tensor":
            src = kw.get("lhsT", kw.get("in_"))
            lo = src.ap.base_partition()
            rows = (lo, lo + src.ap.partition_size())
            ob = kw["out"].buf
            prev = getattr(ob, "pe_rows", None)
            if prev is not None and ob.writer is not None and ob.writer[0] == "tensor" and \
                    (rows[1] <= prev[0] or prev[1] <= rows[0]):
                k, v = ob.writer
                if self.waited[eng].get(k, 0) < v:
                    self.waited[eng][k] = v
                    waits.append((k, v))
                    self.q[k][v - 1][2] = True
            ob.pe_rows = rows
        args = {k: _unw(v) for k, v in kw.items()}
        fn = lambda e, m=meth, a=args: getattr(e, m)(**a)
        self.q[eng].append([waits, fn, False, None])
        tok = (eng, len(self.q[eng]))
        self._commit(tok, reads, writes)
        return tok

    def dma(self, eng, out, in_, **kw):
        sb = out.buf
        if sb.dsem is None:
            sb.dsem = ("dma", len(self.dma_sems))
            self.dma_sems.append(sb.name)
        waits = self._deps(eng, [in_.buf], [out.buf])
        sb.dcnt += 16
        tok = (sb.dsem, sb.dcnt)
        a = dict(out=out.ap, in_=in_.ap, **kw)
        fn = lambda e, a=a: e.dma_start(**a)
        self.q[eng].append([waits, fn, False, sb.dsem])
        self._commit(tok, [in_.buf], [out.buf])
        return tok

    def finish(self, final_bufs):
        waits = self._deps("sync", final_bufs, [])
        self.q["sync"].append([waits, None, False, None])

    def emit(self):
        nc = self.nc
        st = self.stack
        esem = {e: st.enter_context(nc.semaphore("es_" + e)) for e in ENGS}
        dsem = [st.enter_context(nc.semaphore("ds%d" % i)) for i in range(len(self.dma_sems))]
        cum = {}
        for e in ENGS:
            c = 0
            arr = []
            for it in self.q[e]:
                if it[2]:
                    c += 1
                arr.append(c)
            cum[e] = arr

        def semval(k, v):
            if isinstance(k, str):
                return esem[k], cum[k][v - 1]
            return dsem[k[1]], v

        block = st.enter_context(nc.Block())

        def run(e, eng):
            for waits, fn, sig, dk in self.q[e]:
                for k, v in waits:
                    s, val = semval(k, v)
                    eng.wait_ge(s, val)
                if fn is None:
                    continue
                ins = fn(eng)
                if dk is not None:
                    ins.then_inc(dsem[dk[1]], 16)
                elif sig:
                    ins.then_inc(esem[e], 1)

        @block.tensor
        def _(eng):
            run("tensor", eng)

        @block.vector
        def _(eng):
            run("vector", eng)

        @block.scalar
        def _(eng):
            run("scalar", eng)

        @block.gpsimd
        def _(eng):
            run("gpsimd", eng)

        @block.sync
        def _(eng):
            run("sync", eng)


def _alibi_slopes(n):
    def pow2(m):
        start = 2.0 ** (-8.0 / m)
        return [start ** (i + 1) for i in range(m)]
    if math.log2(n).is_integer():
        s = pow2(n)
    else:
        p = 2 ** int(math.floor(math.log2(n)))
        s = pow2(p) + pow2(2 * p)[0::2][: n - p]
    return sorted(s, reverse=True)


(PV_SHM, PV_SCM, PV_GTM, PV_SHF, PV_SCF, PV_GTF, PV_GPM, PV_GQM, PV_GPF, PV_GQF,
 PV_MUR, PV_MUK, PV_MUV, PV_MUW, PV_MUA, PV_MUG, PV_W0, PV_A0, PV_KK, PV_KA, PV_RK,
 PV_LNW, PV_LNB, PV_C) = range(24)
NPV = 24

CB_ID = 0
CB_ONESBD = 128
CB_ONES = 256
CB_MT4 = 320
CB_ML4 = 832
CB_E1 = 1344
CB_EA2 = CB_E1 + 8 * 256
CB_EB2 = CB_EA2 + 2 * 8 * 64
CB_EA3 = CB_EB2 + 8 * 64
CB_EB3 = CB_EA3 + 8 * 8 * 16
NCB = CB_EB3 + 8 * 16
CF_MSK = 0
CF_VM = 256
CF_EPS = CF_VM + 128
CF_IDF = CF_EPS + 4
NCF = CF_IDF + 128

CH_RW = 0
CH_AT = 8
CH_PB = 17
CH_WOUT = 25
CH_FF = 27
CH_FO = 38
NCH = 44


def _host_consts():
    sl = np.asarray(_alibi_slopes(24), np.float64).reshape(3, 8)
    cb = np.zeros((128, NCB), np.float32)
    p = np.arange(128)
    cb[:, CB_ID:CB_ID + 128] = np.eye(128)
    cb[:, CB_ONESBD:CB_ONESBD + 128] = (p[:, None] // 64 == p[None, :] // 64)
    cb[:, CB_ONES:CB_ONES + 64] = 1.0
    same = (p[:, None] // 64 == p[None, :] // 64)
    su = same & (p[:, None] < p[None, :])
    iu = same & (p[:, None] <= p[None, :])
    slo = same & (p[:, None] > p[None, :])
    cb[:, CB_MT4:CB_MT4 + 512] = np.concatenate([su, iu, su, iu], 1)
    cb[:, CB_ML4:CB_ML4 + 512] = np.concatenate([slo] * 4, 1)
    k = p[:, None].astype(np.float64)
    q = p[None, :].astype(np.float64)
    for h in range(8):
        dpv = q - k + 128
        e_prev = np.where(dpv <= 128, np.exp(-sl[0, h] * dpv), 0.0)
        dcu = q - k
        e_cur = np.where(dcu >= 0, np.exp(-sl[0, h] * np.maximum(dcu, 0)), 0.0)
        cb[:, CB_E1 + h * 256: CB_E1 + h * 256 + 128] = e_prev
        cb[:, CB_E1 + h * 256 + 128: CB_E1 + h * 256 + 256] = e_cur
    i64 = np.arange(64)[None, :].astype(np.float64)
    for rot in range(2):
        for h in range(8):
            j = p // 64
            pp = (p % 64).astype(np.float64)
            a = ((rot - j - 1) % 2) + 1
            dl = 64.0 * a[:, None] + i64 - pp[:, None]
            e = np.where(dl <= 128, np.exp(-sl[1, h] * 4.0 * dl), 0.0)
            o = CB_EA2 + (rot * 8 + h) * 64
            cb[:, o:o + 64] = e
    for h in range(8):
        kk = np.arange(64)[:, None].astype(np.float64)
        dl = i64 - kk
        e = np.where(dl >= 0, np.exp(-sl[1, h] * 4.0 * np.maximum(dl, 0)), 0.0)
        o = CB_EB2 + h * 64
        cb[0:64, o:o + 64] = e
    i16 = np.arange(16)[None, :].astype(np.float64)
    for rot in range(8):
        for h in range(8):
            j = p // 16
            pp = (p % 16).astype(np.float64)
            a = ((rot - j - 1) % 8) + 1
            dl = 16.0 * a[:, None] + i16 - pp[:, None]
            e = np.where(dl <= 128, np.exp(-sl[2, h] * 16.0 * dl), 0.0)
            o = CB_EA3 + (rot * 8 + h) * 16
            cb[:, o:o + 16] = e
    for h in range(8):
        kk = np.arange(16)[:, None].astype(np.float64)
        dl = i16 - kk
        e = np.where(dl >= 0, np.exp(-sl[2, h] * 16.0 * np.maximum(dl, 0)), 0.0)
        o = CB_EB3 + h * 16
        cb[0:16, o:o + 16] = e
    return cb


def _host_cf(hh):
    cf = np.zeros((128, NCF), np.float32)
    m = np.ones((128, 256), np.float32)
    m[:, 0::64] = 0.0
    cf[:, CF_MSK:CF_MSK + 256] = m
    valid = lambda t: 0.0 if t < 0 else (1.0 if (hh == 1 or t >= PB0) else 0.0)
    p = np.arange(128)
    for t in range(NT):
        cf[:, CF_VM + t] = valid(t)
        cf[:, CF_VM + 32 + t] = valid(t - 1)
        j = p // 64
        a = ((t - j - 1) % 2) + 1
        cf[:, CF_VM + 64 + t] = [valid(t - aa) for aa in a]
        j = p // 16
        a = ((t - j - 1) % 8) + 1
        cf[:, CF_VM + 96 + t] = [valid(t - aa) for aa in a]
    cf[:, CF_EPS] = RMS_EPS
    cf[:, CF_EPS + 1] = GN_EPS
    cf[:, CF_EPS + 3] = 1.0
    cf[:, CF_IDF:CF_IDF + 128] = np.eye(128)
    return cf


def _fm(v):
    return np.ascontiguousarray(v.reshape(8, 128).T)


def _wchunk(w, cols):
    return w[:, cols].reshape(8, 128, -1).transpose(1, 0, 2)


class _Stop(Exception):
    pass


def build(nt=NT, dbg=None, dbg_tile=0, dbg_c=0, stop=None):
    nc = bass.Bass("TRN2", target_bir_lowering=False)
    with ExitStack() as st:
        S = Sched(nc, st)
        finals = []

        def CK(name):
            if stop == name:
                raise _Stop()

        def DBG(name, view, m=None, c=None):
            if not dbg or name not in dbg:
                return
            if m is not None and m != dbg_tile:
                return
            if c is not None and c != dbg_c:
                return
            shp = list(view.ap.shape)
            dd = S.dram("dbg_" + name, shp, view.ap.dtype, "ExternalOutput")
            S.dma("gpsimd", out=dd[:], in_=view)
            finals.append(dd)
        xv = S.dram("xv", [T, D], F32, "ExternalInput")
        pfm_d = S.dram("pfm", [128, NPV * 8], F32, "ExternalInput")
        wmod_d = S.dram("wmod", [24, 128, 2048], F32, "ExternalInput")
        wsrc = S.dram("wsrc", [NCH, 128, 4096], F32, "ExternalInput")
        l1_d = S.dram("l1", [128, 8 * 288], F32, "ExternalInput")
        l2_d = S.dram("l2", [128, 3 * 1024], F32, "ExternalInput")
        cb_d = S.dram("cbt", [128, NCB], F32, "ExternalInput")
        cf_d = S.dram("cft", [128, NCF], F32, "ExternalInput")
        y_d = S.dram("y", [T // 2, D], F32, "ExternalOutput")
        wscr = S.dram("wscr", [NCH, 128, 4096], BF16, "Internal")

        cb = S.sb("cb", [128, NCB], BF16)
        cf = S.sb("cf", [128, NCF], F32)
        pf = S.sb("pf", [128, NPV * 8], F32)
        pd = S.sb("pd", [128, 12 * 8], F32)
        gmb = S.sb("gmb", [128, 1024], BF16)
        gfb = S.sb("gfb", [128, 1024], BF16)
        l1a = S.sb("l1a", [128, 8, 288], BF16)
        l1b = S.sb("l1b", [128, 8, 288], BF16)
        l2 = S.sb("l2", [128, 3, 1024], BF16)
        ring = [S.sb("ring%d" % i, [128, 4096], BF16) for i in range(NSLOT)]
        xt1 = S.sb("xt", [128, NSUB, 1024], F32)
        xt = [xt1, xt1]
        nb = S.sb("nb", [128, NSUB, 1024], BF16)
        junk = nb[:, 0, :]
        st4 = S.sb("st4", [128, 16], F32)
        hT = S.sb("hT", [128, 8, TT + 1], BF16)
        h2T = S.sb("h2T", [128, 8, TT], BF16)
        mixT = h2T
        Zf = S.sb("Zf", [128, 8, 64], F32)
        Zb = S.sb("Zb", [128, 8, 2, 64], BF16)
        hal = S.sb("hal", [128, 8, 3], F32)
        tp = [S.sb("tp%d" % i, [128, TT + 1], F32) for i in range(12)]
        tv = lambda i: tp[i][:, 0:TT]
        pj = [tp[0], tp[0], tp[0]]
        tmpd = tv(1)
        rkv = [tv(2), tv(3), tv(4)]
        sw = tv(5); asig = tv(6); gg = tv(7); cs = tv(0); cm = tv(1)
        Ep = tv(8); En = tv(9); Em = tv(10); rinv = tv(1); kkb = tv(11); ff = tv(0)
        kmod = tv(5); bv = tv(1); bon = tv(6); yln = tv(9); ysq = tv(10)
        sqb = S.sb("sqb", [128, TT], BF16)
        AR = S.sb("AR", [128, NSUB, 2, 128], BF16)
        Bt = S.sb("Bt", [128, TT], BF16)
        Kt = S.sb("Kt", [128, TT], BF16)
        vbf = S.sb("vbf", [128, TT], BF16)
        Bpad = S.sb("Bpad", [128, NSUB, 2, 128], BF16)
        Kpad = S.sb("Kpad", [128, NSUB, 2, 128], BF16)
        Vtm = S.sb("Vtm", [128, NSUB, 128], BF16)
        AM = S.sb("AM", [128, 4, 512], BF16)
        L0 = S.sb("L0", [128, 4, 128], BF16)
        LP = [S.sb("LP%d" % i, [128, 4, 128], BF16) for i in range(2)]
        LT = [S.sb("LT%d" % i, [128, 4, 128], BF16) for i in range(2)]
        SS = [S.sb("SS%d" % i, [128, 4, 128], BF16) for i in range(2)]
        Xb = S.sb("Xb", [128, 128], BF16)
        Ub = S.sb("Ub", [128, 128], BF16)
        ztmp = S.sb("ztmp", [128, 64], F32)
        Ytm = S.sb("Ytm", [128, NSUB, 128], F32)
        ynb = S.sb("ynb", [128, NSUB, 128], BF16)
        gst = S.sb("gst", [128, 32], F32)
        lw = S.sb("lw", [128, TT], BF16)
        lga = S.sb("lga", [128, TT], BF16)
        lgb = S.sb("lgb", [32, TT], BF16)
        yfin = S.sb("yfin", [128, 8, TT], BF16)
        Qa = [S.sb("Qa%d" % g, [128, 4, TT], BF16) for g in range(3)]
        K1 = S.sb("K1", [128, 4, 128 + TT], BF16)
        V1 = S.sb("V1", [128, 3, 512], BF16)
        K2c = S.sb("K2c", [128, 4, TT], BF16)
        K2r = S.sb("K2r", [128, 4, 4, 128], BF16)
        V2c = S.sb("V2c", [64, 4, 128], BF16)
        V2r = S.sb("V2r", [128, 4, 512], BF16)
        K3c = S.sb("K3c", [128, 4, TT], BF16)
        K3r = S.sb("K3r", [128, 4, 16, 128], BF16)
        V3c = S.sb("V3c", [16, 16, 128], BF16)
        V3r = S.sb("V3r", [128, 16, 512], BF16)
        VF = S.sb("VF", [128, 4, TT], BF16)
        pe = S.sb("pe", [128, 512], BF16)
        pp_ = S.sb("pp", [128, 512], BF16)
        peb = S.sb("peb", [64, 256], BF16)
        ppb = S.sb("ppb", [64, 256], BF16)
        accO = S.sb("accO", [64, TT], F32)
        accD = S.sb("accD", [64, TT], F32)
        oT = S.sb("oT", [64, 8, TT], BF16)
        sa = tv(2); sbb = tv(3); sg = tv(4); utmp = tv(10)
        actT = S.sb("actT", [128, 8, TT], BF16)
        VF2 = S.sb("VF2", [128, 4, TT], BF16)
        wst = xt1[:].re("p s c -> p (s c)")

        _b0 = S.ps("b0", [128, 512])
        _b4 = S.ps("b4", [128, 512])
        _bS = S.ps("bS", [128, 1024])
        B3 = S.ps("b3", [128, 512])
        B5 = S.ps("b5", [128, 512])
        B6 = S.ps("b6", [128, 512])
        _pT = S.ps("pT", [128, 1024], BF16)
        R0 = SubBuf(_b0, 0); R1 = SubBuf(_b0, 256)
        Q0 = SubBuf(_b4, 0); Q1 = SubBuf(_b4, 256)
        B1 = Buf("B1", _bS.t[:, 0:512]); B2 = Buf("B2", _bS.t[:, 512:1024])
        pTr = SubBuf(_pT, 0); pTa = SubBuf(_pT, 512)
        for b_ in (_b0, _b4, B1, B2, B3, B5, B6, _pT):
            b_.psum = True
        pC = B3
        prot = {"r": [R0, R1], "a": [Q0, Q1], "x": [R0, R1, Q0, Q1]}
        prot_i = {"r": 0, "a": 0, "x": 0}

        def nextp(k="x"):
            prot_i[k] = (prot_i[k] + 1) % len(prot[k])
            return prot[k][prot_i[k]]

        mm = lambda **kw: S.op("tensor", "matmul", **kw)
        tr = lambda **kw: S.op("tensor", "transpose", **kw)
        act = lambda **kw: S.op("scalar", "activation", **kw)
        vec = lambda m, **kw: S.op("vector", m, **kw)
        gps = lambda m, **kw: S.op("gpsimd", m, **kw)

        def sigmoid_to(dst, src, nbias=None, scale=1.0):
            if nbias is None:
                act(out=dst, in_=src, func=AF.Exp, scale=-scale)
            else:
                act(out=dst, in_=src, func=AF.Exp, scale=-scale, bias=nbias)
            act(out=dst, in_=dst, func=AF.Ln, bias=one_c_for(dst))
            act(out=dst, in_=dst, func=AF.Exp, scale=-1.0)

        def one_c_for(v):
            lo = v.ap.base_partition()
            n = v.ap.partition_size()
            return cf[lo:lo + n, CF_EPS + 3:CF_EPS + 4]

        def rsqrt_to(dst, src, bias_ap, scale=1.0):
            act(out=dst, in_=src, func=AF.Ln, bias=bias_ap, scale=scale)
            act(out=dst, in_=dst, func=AF.Exp, scale=-0.5)

        ident = cb[:, CB_ID:CB_ID + 128]
        identf = cf[:, CF_IDF:CF_IDF + 128]
        onesbd = cb[:, CB_ONESBD:CB_ONESBD + 128]
        eps_r = cf[:, CF_EPS:CF_EPS + 1]
        eps_g = cf[:, CF_EPS + 1:CF_EPS + 2]
        zero_c = cf[:, CF_EPS + 2:CF_EPS + 3]
        one_c = cf[:, CF_EPS + 3:CF_EPS + 4]

        def pv(i, kc):
            return pf[:, i * 8 + kc: i * 8 + kc + 1]

        def pdv(i, kc):
            return pd[:, i * 8 + kc: i * 8 + kc + 1]
        PD_A1, PD_A2, PD_GM, PD_GF, PD_OMK, PD_OMR, PD_OMKm, PD_OMV = range(8)

        try:
            S.dma("gpsimd", out=cb[:, :], in_=cb_d[:, :])
            S.dma("sync", out=cf[:, :], in_=cf_d[:, :])
            S.dma("sync", out=pf[:, :], in_=pfm_d[:, :])
            for i in range(NCH):
                S.dma("gpsimd", out=wscr[i], in_=wsrc[i])
            CK('dma0')
            for b_ in (Zf, Zb, hal, Bpad, Kpad, K1, K2r, V2r, K3r, V3r, V1, hT, Xb, Ub, Vtm):
                gps("memset", ap=b_[:], constant=0.0)
            CK('memset')
            for half in range(2):
                S.dma("sync", out=wst[:, 0:4 * 288], in_=l1_d[:, half * 4 * 288:(half + 1) * 4 * 288])
                w1v = wst[:, 0:4 * 288].re("p (k c) -> p k c", c=288)
                for k4 in range(4):
                    kc = half * 4 + k4
                    for (lo, hi, mui) in ((0, 64, PV_MUW), (64, 128, PV_MUA), (128, 288, PV_MUG)):
                        vec("tensor_scalar", out=l1b[:, kc, lo:hi], in0=w1v[:, k4, lo:hi], scalar1=pv(mui, kc),
                            scalar2=None, op0=ALU.mult)
                        vec("tensor_tensor", out=l1a[:, kc, lo:hi], in0=w1v[:, k4, lo:hi], in1=l1b[:, kc, lo:hi],
                            op=ALU.subtract)
            for half in range(2):
                S.dma("sync", out=wst[:, 0:1536], in_=l2_d[:, half * 1536:(half + 1) * 1536])
                vec("tensor_copy", out=l2[:].re("p a c -> p (a c)")[:, half * 1536:(half + 1) * 1536], in_=wst[:, 0:1536])
            CK('lora0')
            for j in range(24):
                S.dma("sync", out=wst[:, 0:2048], in_=wmod_d[j])
                wv = wst[:, 0:2048].re("p (k c) -> p k c", c=256)
                for cc in range(2):
                    col = j * 2 + cc
                    for kc in range(8):
                        mm(out=B1[:, col:col + 1], lhsT=wv[:, kc, cc * 128:(cc + 1) * 128], rhs=pv(PV_C, kc),
                           start=(kc == 0), stop=(kc == 7))
            modf = S.sb("modf", [128, 48], F32)
            vec("tensor_tensor", out=modf[:, :], in0=B1[:, 0:48], in1=pf[:, 0:48], op=ALU.add)
            for kc in range(8):
                vec("scalar_tensor_tensor", out=pdv(PD_A1, kc), in0=modf[:, 8 + kc:9 + kc], scalar=1.0,
                    in1=pv(PV_GPM, kc), op0=ALU.add, op1=ALU.mult)
                vec("scalar_tensor_tensor", out=pdv(PD_A2, kc), in0=modf[:, 32 + kc:33 + kc], scalar=1.0,
                    in1=pv(PV_GPF, kc), op0=ALU.add, op1=ALU.mult)
                vec("tensor_tensor", out=pdv(PD_GM, kc), in0=modf[:, 16 + kc:17 + kc], in1=pv(PV_GQM, kc), op=ALU.mult)
                vec("tensor_tensor", out=pdv(PD_GF, kc), in0=modf[:, 40 + kc:41 + kc], in1=pv(PV_GQF, kc), op=ALU.mult)
                vec("tensor_scalar", out=pdv(PD_OMK, kc), in0=pv(PV_KA, kc), scalar1=-1.0, scalar2=1.0,
                    op0=ALU.mult, op1=ALU.add)
                vec("tensor_scalar", out=pdv(5, kc), in0=pv(PV_W0, kc), scalar1=-1.0, scalar2=None, op0=ALU.mult)
                vec("tensor_scalar", out=pdv(6, kc), in0=pv(PV_A0, kc), scalar1=-1.0, scalar2=None, op0=ALU.mult)
            dg = S.sb("dg", [128, 128], F32)
            onesf = S.sb("onesf", [128, 128], F32)
            gps("memset", ap=onesf[:, :], constant=1.0)
            for (pdi, dst) in ((PD_GM, gmb), (PD_GF, gfb)):
                for kc in range(8):
                    vec("tensor_scalar", out=dg[:, :], in0=identf, scalar1=pdv(pdi, kc), scalar2=None, op0=ALU.mult)
                    pz = nextp()
                    mm(out=pz[:, 0:128], lhsT=onesf[:, :], rhs=dg[:, :], start=True, stop=True)
                    act(out=dst[:, kc * 128:(kc + 1) * 128], in_=pz[:, 0:128], func=AF.Copy)

            CK('startup')
            ring_i = [0]

            ring_sets = {"r": ring[0:2], "a": ring[2:4], "x": ring}
            ring_k = {"r": 0, "a": 0, "x": 0}

            def wload(ch, k="x"):
                s = ring_sets[k][ring_k[k] % len(ring_sets[k])]
                ring_k[k] += 1
                S.dma("sync", out=s[:, :], in_=wscr[ch])
                return s

            def rms_rstd(src3, dst_cols, nsub=NSUB):
                for sub in range(nsub):
                    act(out=junk[:, :], in_=src3[:, sub, :], func=AF.Square,
                        accum_out=st4[:, 8 + sub:9 + sub])
                rsqrt_to(st4[:, dst_cols:dst_cols + nsub], st4[:, 8:8 + nsub], eps_r, 1.0 / D)

            def norm_transpose(xsrc, rcol, dstT, a_idx, b_view_fn, halo):
                for sub in range(NSUB):
                    vec("tensor_scalar", out=nb[:, sub, :], in0=xsrc[:, sub, :], scalar1=st4[:, rcol + sub:rcol + sub + 1],
                        scalar2=None, op0=ALU.mult)
                for kc in range(8):
                    for sub in range(NSUB):
                        tr(out=pTr[:, (kc % 2) * 256 + sub * 128:(kc % 2) * 256 + (sub + 1) * 128], in_=nb[:, sub, kc * 128:(kc + 1) * 128], identity=ident)
                    act(out=dstT[:, kc, halo:halo + TT], in_=pTr[:, (kc % 2) * 256:(kc % 2) * 256 + TT], func=AF.Identity,
                        scale=pdv(a_idx, kc), bias=b_view_fn(kc))

            for m in range(nt):
                phaseB = m >= PB0
                xm = xt[m % 2]
                vcur = cf[:, CF_VM + m:CF_VM + m + 1]
                vprev = cf[:, CF_VM + 32 + m:CF_VM + 33 + m]
                vr2 = cf[:, CF_VM + 64 + m:CF_VM + 65 + m]
                vr3 = cf[:, CF_VM + 96 + m:CF_VM + 97 + m]
                S.dma("gpsimd", out=xm[:], in_=xv.v(xv.t[m * TT:(m + 1) * TT, :].rearrange("(s p) c -> p s c", p=128)))
                if m > 0:
                    vec("tensor_scalar", out=hT[:, :, 0:1], in0=hT[:, :, TT:TT + 1],
                        scalar1=cf[:, CF_VM + m - 1:CF_VM + m], scalar2=None, op0=ALU.mult)
                rms_rstd(xm, 0)
                norm_transpose(xm, 0, hT, PD_A1, lambda kc: pf[:, PV_SHM * 8 + kc:PV_SHM * 8 + kc + 1]
                               if False else modf[:, kc:kc + 1], 1)

                CK('stage1')
                pz = nextp()
                for kc in range(8):
                    mm(out=pz[:, 0:TT], lhsT=l1a[:, kc, 0:128], rhs=hT[:, kc, 1:TT + 1], start=(kc == 0), stop=False)
                    mm(out=pz[:, 0:TT], lhsT=l1b[:, kc, 0:128], rhs=hT[:, kc, 0:TT], start=False, stop=(kc == 7))
                sigmoid_to(tp[11][0:64, 0:TT], pz[0:64, 0:TT], None, 2.0)
                vec("tensor_scalar", out=lw[0:64, :], in0=tp[11][0:64, 0:TT], scalar1=2.0, scalar2=-1.0, op0=ALU.mult, op1=ALU.add)
                act(out=lw[64:128, :], in_=pz[64:128, 0:TT], func=AF.Copy)
                if phaseB:
                    pz = nextp()
                    for kc in range(8):
                        mm(out=pz[:, 0:TT], lhsT=l1a[:, kc, 128:256], rhs=hT[:, kc, 1:TT + 1], start=(kc == 0), stop=False)
                        mm(out=pz[:, 0:TT], lhsT=l1b[:, kc, 128:256], rhs=hT[:, kc, 0:TT], start=False, stop=(kc == 7))
                    sigmoid_to(tp[11][:, 0:TT], pz[:, 0:TT])
                    act(out=lga[:, :], in_=tp[11][:, 0:TT], func=AF.Copy)
                    pz = nextp()
                    for kc in range(8):
                        mm(out=pz[0:32, 0:TT], lhsT=l1a[:, kc, 256:288], rhs=hT[:, kc, 1:TT + 1], start=(kc == 0), stop=False)
                        mm(out=pz[0:32, 0:TT], lhsT=l1b[:, kc, 256:288], rhs=hT[:, kc, 0:TT], start=False, stop=(kc == 7))
                    sigmoid_to(tp[11][0:32, 0:TT], pz[0:32, 0:TT])
                    act(out=lgb[0:32, :], in_=tp[11][0:32, 0:TT], func=AF.Copy)

                CK('lora1')
                def th_rwkv():
                    for c in range(8):
                        csl = slice(c * 128, (c + 1) * 128)
                        wr = wload(CH_RW + c, 'r')
                        wrv = wr[:, :].re("p (k c) -> p k c", c=512)
                        for j in ((0, 1, 2) if m >= PB0 - 1 else (1, 2)):
                            pz = nextp("r")
                            for kc in range(8):
                                mm(out=pz[:, 0:TT], lhsT=wrv[:, kc, j * 128:(j + 1) * 128], rhs=hT[:, kc, 1:TT + 1],
                                   start=(kc == 0), stop=(kc == 7))
                            vec("tensor_copy", out=pj[j][:, 0:1], in_=hal[:, c, j:j + 1])
                            act(out=pj[j][:, 1:TT + 1], in_=pz[:, 0:TT], func=AF.Copy)
                            vec("tensor_scalar", out=hal[:, c, j:j + 1], in0=pj[j][:, TT:TT + 1], scalar1=vcur,
                                scalar2=None, op0=ALU.mult)
                            vec("tensor_tensor", out=tmpd[:, :], in0=pj[j][:, 0:TT], in1=pj[j][:, 1:TT + 1], op=ALU.subtract)
                            vec("scalar_tensor_tensor", out=rkv[j][:, :], in0=tmpd[:, :], scalar=pv(PV_MUR + j, c),
                                in1=pj[j][:, 1:TT + 1], op0=ALU.mult, op1=ALU.add)
                            yield
                        r_, k_, v_ = rkv
                        vec("tensor_scalar", out=v_[:, :], in0=v_[:, :], scalar1=vcur, scalar2=None, op0=ALU.mult)
                        act(out=vbf[:, :], in_=v_[:, :], func=AF.Copy)
                        pz = nextp("r")
                        mm(out=pz[:, 0:TT], lhsT=l2[0:64, 0, csl], rhs=lw[0:64, :], start=True, stop=True)
                        sigmoid_to(sw[:, :], pz[:, 0:TT], pdv(5, c))
                        pz = nextp("r")
                        mm(out=pz[:, 0:TT], lhsT=l2[64:128, 0, csl], rhs=lw[64:128, :], start=True, stop=True)
                        sigmoid_to(asig[:, :], pz[:, 0:TT], pdv(6, c))
                        pz = nextp("r")
                        if phaseB:
                            mm(out=pz[:, 0:TT], lhsT=l2[:, 1, csl], rhs=lga[:, :], start=True, stop=False)
                            mm(out=pz[:, 0:TT], lhsT=l2[0:32, 2, csl], rhs=lgb[0:32, :], start=False, stop=True)
                            act(out=gg[:, :], in_=pz[:, 0:TT], func=AF.Copy)
                        yield
                        vec("tensor_tensor_scan", out=cs[:, :], data0=cf[:, CF_MSK:CF_MSK + TT], data1=sw[:, :],
                            initial=0.0, op0=ALU.mult, op1=ALU.add)
                        vec("tensor_tensor", out=cm[:, :], in0=cs[:, :], in1=sw[:, :], op=ALU.subtract)
                        act(out=Ep[:, :], in_=cs[:, :], func=AF.Exp, scale=-C0)
                        act(out=En[:, :], in_=cs[:, :], func=AF.Exp, scale=C0)
                        act(out=Em[:, :], in_=cm[:, :], func=AF.Exp, scale=-C0)
                        yield
                        act(out=sqb[:, :], in_=k_[:, :], func=AF.Square, scale=pv(PV_KK, c))
                        pz = nextp("r")
                        mm(out=pz[:, 0:TT], lhsT=onesbd, rhs=sqb[:, :], start=True, stop=True)
                        vec("tensor_scalar", out=rinv[:, :], in0=pz[:, 0:TT], scalar1=1e-18, scalar2=None, op0=ALU.max)
                        act(out=rinv[:, :], in_=rinv[:, :], func=AF.Ln)
                        act(out=rinv[:, :], in_=rinv[:, :], func=AF.Exp, scale=-0.5)
                        vec("scalar_tensor_tensor", out=kkb[:, :], in0=k_[:, :], scalar=pv(PV_KK, c), in1=rinv[:, :],
                            op0=ALU.mult, op1=ALU.mult)
                        yield
                        vec("tensor_scalar", out=ff[:, :], in0=asig[:, :], scalar1=pv(PV_KA, c), scalar2=pdv(PD_OMK, c),
                            op0=ALU.mult, op1=ALU.add)
                        vec("tensor_tensor", out=kmod[:, :], in0=k_[:, :], in1=ff[:, :], op=ALU.mult)
                        vec("tensor_tensor", out=bv[:, :], in0=kkb[:, :], in1=asig[:, :], op=ALU.mult)
                        vec("scalar_tensor_tensor", out=AR[:, :, 0, :], in0=kkb[:, :].re("p (s t) -> p s t", t=128), scalar=-1.0,
                            in1=Em[:, :].re("p (s t) -> p s t", t=128), op0=ALU.mult, op1=ALU.mult)
                        if phaseB:
                            vec("tensor_tensor", out=AR[:, :, 1, :], in0=r_[:, :].re("p (s t) -> p s t", t=128),
                                in1=Ep[:, :].re("p (s t) -> p s t", t=128), op=ALU.mult)
                        vec("tensor_tensor", out=Bt[:, :], in0=bv[:, :], in1=En[:, :], op=ALU.mult)
                        vec("tensor_tensor", out=Kt[:, :], in0=kmod[:, :], in1=En[:, :], op=ALU.mult)
                        yield
                        if phaseB:
                            vec("tensor_tensor", out=tmpd[:, :], in0=r_[:, :], in1=kmod[:, :], op=ALU.mult)
                            act(out=sqb[:, :], in_=tmpd[:, :], func=AF.Copy, scale=pv(PV_RK, c))
                            pz = nextp("r")
                            mm(out=pz[:, 0:TT], lhsT=onesbd, rhs=sqb[:, :], start=True, stop=True)
                            vec("tensor_tensor", out=bon[:, :], in0=pz[:, 0:TT], in1=v_[:, :], op=ALU.mult)
                        yield
                        for qi, (src, dst) in enumerate(((Bt, Bpad), (Kt, Kpad), (vbf, None))):
                            for sub in range(NSUB):
                                tr(out=pTr[:, (qi % 2) * 256 + sub * 128: (qi % 2) * 256 + (sub + 1) * 128],
                                   in_=src[:, sub * 128:(sub + 1) * 128], identity=ident)
                            yield
                            srcv = pTr[:, (qi % 2) * 256:(qi % 2 + 1) * 256]
                            if dst is None:
                                act(out=Vtm[:].re("p s c -> p (s c)"), in_=srcv, func=AF.Copy)
                            else:
                                for h in range(2):
                                    act(out=dst[:, :, h, h * 64:(h + 1) * 64],
                                        in_=srcv.re("p (s c) -> p s c", c=128)[:, :, h * 64:(h + 1) * 64], func=AF.Copy)
                        CK('rwkv_a')
                        for h in range(2):
                            hs = slice(h * 64, (h + 1) * 64)
                            for sub in range(NSUB):
                                u = h * NSUB + sub
                                tsl = slice(sub * 128, (sub + 1) * 128)
                                pz = (B1, B2)[u % 2]
                                if phaseB:
                                    mm(out=pz[:, 0:256], lhsT=Bt[hs, tsl], rhs=AR[hs, sub, :, :].re("p a t -> p (a t)"),
                                       start=True, stop=True)
                                    mm(out=pz[:, 256:512], lhsT=Kt[hs, tsl], rhs=AR[hs, sub, :, :].re("p a t -> p (a t)"),
                                       start=True, stop=True)
                                    vec("tensor_tensor", out=AM[:, u, :], in0=pz[:, :], in1=cb[:, CB_MT4:CB_MT4 + 512], op=ALU.mult)
                                else:
                                    mm(out=pz[:, 0:128], lhsT=Bt[hs, tsl], rhs=AR[hs, sub, 0, :], start=True, stop=True)
                                    mm(out=pz[:, 256:384], lhsT=Kt[hs, tsl], rhs=AR[hs, sub, 0, :], start=True, stop=True)
                                    v4 = lambda ap_: ap_.re("p (a two b) -> p a two b", a=2, two=2)[:, :, 0, :]
                                    vec("tensor_tensor", out=v4(AM[:, u, :]), in0=v4(pz[:, :]),
                                        in1=v4(cb[:, CB_MT4:CB_MT4 + 512]), op=ALU.mult)
                                mm(out=pC[:, u * 128:(u + 1) * 128], lhsT=AR[hs, sub, 0, :], rhs=Bt[hs, tsl],
                                   start=True, stop=True)
                                yield
                        vec("tensor_tensor", out=L0[:].re("p u t -> p (u t)"), in0=pC[:, :], in1=cb[:, CB_ML4:CB_ML4 + 512],
                            op=ALU.mult)
                        yield
                        CK('rwkv_b')
                        vec("tensor_tensor", out=SS[0][:], in0=AM[:, :, 0:128], in1=ident.bc(1, [128, 4, 128]), op=ALU.add)
                        lt_prev = lambda u: AM[:, u, 0:128]
                        lp_prev = lambda u: L0[:, u, :]
                        scur = 0
                        for lev in range(1, 6):
                            lpn = LP[lev % 2]
                            ltn = LT[lev % 2]
                            for u in range(4):
                                mm(out=B1[:, u * 128:(u + 1) * 128], lhsT=lt_prev(u), rhs=lp_prev(u), start=True, stop=True)
                            if lev <= 4:
                                for u in range(4):
                                    mm(out=B2[:, u * 128:(u + 1) * 128], lhsT=lp_prev(u), rhs=lt_prev(u),
                                       start=True, stop=True)
                            act(out=lpn[:].re("p u t -> p (u t)"), in_=B1[:, :], func=AF.Copy)
                            if lev <= 4:
                                act(out=ltn[:].re("p u t -> p (u t)"), in_=B2[:, :], func=AF.Copy)
                            yield
                            for u in range(4):
                                mm(out=B3[:, u * 128:(u + 1) * 128], lhsT=lpn[:, u, :], rhs=SS[scur][:, u, :], start=True, stop=True)
                            vec("tensor_tensor", out=SS[1 - scur][:].re("p u t -> p (u t)"), in0=B3[:, :],
                                in1=SS[scur][:].re("p u t -> p (u t)"), op=ALU.add)
                            scur = 1 - scur
                            lt_prev = (lambda b: (lambda u: b[:, u, :]))(ltn)
                            yield
                            lp_prev = (lambda b: (lambda u: b[:, u, :]))(lpn)
                        TTm = SS[scur]
                        CK('rwkv_c')
                        for q in range(2 * NSUB):
                            sub, half = q // 2, q % 2
                            ps_ = slice(half * 64, half * 64 + 64)
                            tsl = slice(sub * 128, (sub + 1) * 128)
                            zi = q % 2
                            for h in range(2):
                                hs = slice(h * 64, (h + 1) * 64)
                                u = h * NSUB + sub
                                mm(out=pC[:, hs], lhsT=AR[hs, sub, 0, :], rhs=Zb[hs, c, zi, :], start=True, stop=False)
                                mm(out=pC[:, hs], lhsT=AM[:, u, 256:384], rhs=Vtm[:, sub, hs], start=False, stop=True)
                            act(out=Xb[ps_, :], in_=pC[ps_, 0:128], func=AF.Copy)
                            yield
                            CK('c1')
                            for h in range(2):
                                hs = slice(h * 64, (h + 1) * 64)
                                u = h * NSUB + sub
                                mm(out=pC[:, 128 + h * 64:128 + (h + 1) * 64], lhsT=TTm[ps_, u, :], rhs=Xb[ps_, hs],
                                   start=True, stop=True)
                            vec("tensor_copy", out=Ub[ps_, :], in_=pC[ps_, 128:256])
                            yield
                            CK('c2')
                            if phaseB:
                                for h in range(2):
                                    hs = slice(h * 64, (h + 1) * 64)
                                    u = h * NSUB + sub
                                    o_ = slice(256 + h * 64, 256 + (h + 1) * 64)
                                    mm(out=pC[:, o_], lhsT=AR[hs, sub, 1, :], rhs=Zb[hs, c, zi, :], start=True, stop=False)
                                    mm(out=pC[:, o_], lhsT=AM[:, u, 128:256], rhs=Ub[:, hs], start=False, stop=False)
                                    mm(out=pC[:, o_], lhsT=AM[:, u, 384:512], rhs=Vtm[:, sub, hs], start=False, stop=True)
                                act(out=Ytm[ps_, sub, :], in_=pC[ps_, 256:384], func=AF.Copy)
                                yield
                            CK('c3')
                            for h in range(2):
                                hs = slice(h * 64, (h + 1) * 64)
                                mm(out=pC[:, 384:448], lhsT=Bpad[ps_, sub, h, :], rhs=Ub[ps_, hs], start=(h == 0), stop=False)
                                mm(out=pC[:, 384:448], lhsT=Kpad[ps_, sub, h, :], rhs=Vtm[ps_, sub, hs], start=False, stop=(h == 1))
                            CK('c4')
                            pcv = Ep[:, q * 64 + 63:q * 64 + 64]
                            vec("tensor_scalar", out=ztmp[:, :], in0=Zf[:, c, :], scalar1=pcv, scalar2=None, op0=ALU.mult)
                            vec("scalar_tensor_tensor", out=Zf[:, c, :], in0=pC[:, 384:448], scalar=pcv, in1=ztmp[:, :],
                                op0=ALU.mult, op1=ALU.add)
                            act(out=Zb[:, c, 1 - zi, :], in_=Zf[:, c, :], func=AF.Copy)
                            yield
                        CK('rwkv_d')
                        if phaseB:
                            yv = Ytm[:].re("p s (h i) -> p (s h) i", i=64)
                            vec("tensor_reduce", out=gst[:, 0:4], in_=yv, axis=AX.X, op=ALU.add)
                            act(out=ysq[:, :], in_=Ytm[:].re("p s c -> p (s c)"), func=AF.Square)
                            vec("tensor_reduce", out=gst[:, 4:8], in_=ysq[:, :].re("p (g i) -> p g i", i=64), axis=AX.X, op=ALU.add)
                            vec("tensor_scalar", out=gst[:, 8:12], in0=gst[:, 0:4], scalar1=1.0 / 64, scalar2=None, op0=ALU.mult)
                            vec("tensor_tensor", out=gst[:, 12:16], in0=gst[:, 8:12], in1=gst[:, 8:12], op=ALU.mult)
                            vec("scalar_tensor_tensor", out=gst[:, 16:20], in0=gst[:, 4:8], scalar=1.0 / 64, in1=gst[:, 12:16],
                                op0=ALU.mult, op1=ALU.subtract)
                            rsqrt_to(gst[:, 24:28], gst[:, 16:20], eps_g, 1.0)
                            ysv = ysq[:, :].re("p (g i) -> p g i", i=64)
                            vec("tensor_tensor", out=ysv, in0=yv, in1=gst[:, 8:12].bc(2, [128, 4, 64]), op=ALU.subtract)
                            vec("tensor_tensor", out=ynb[:].re("p s (h i) -> p (s h) i", i=64), in0=ysv,
                                in1=gst[:, 24:28].bc(2, [128, 4, 64]), op=ALU.mult)
                            yield
                            for sub in range(NSUB):
                                tr(out=pTr[:, 256 + sub * 128:256 + (sub + 1) * 128], in_=ynb[:, sub, :], identity=ident)
                            act(out=yln[:, :], in_=pTr[:, 256:256 + TT], func=AF.Identity, scale=pv(PV_LNW, c), bias=pv(PV_LNB, c))
                            vec("tensor_tensor", out=yln[:, :], in0=yln[:, :], in1=bon[:, :], op=ALU.add)
                            vec("tensor_tensor", out=yfin[:, c, :], in0=yln[:, :], in1=gg[:, :], op=ALU.mult)
                            yield


                def th_attn():
                    if m < PB0 - 8:
                        return
                    j0_2 = m % 2
                    j0_3 = m % 8
                    for g in range(3):
                        kdst = (K1, K2c, K3c)[g]
                        for j in ((0, 1, 2) if phaseB else (1, 2)):
                            wa = wload(CH_AT + g * 3 + j, 'a')
                            wav = wa[:, :].re("p (k c) -> p k c", c=512)
                            for cc in range(4):
                                pz = nextp("a")
                                for kc in range(8):
                                    mm(out=pz[:, 0:TT], lhsT=wav[:, kc, cc * 128:(cc + 1) * 128], rhs=hT[:, kc, 1:TT + 1],
                                       start=(kc == 0), stop=(kc == 7))
                                if j == 0:
                                    act(out=Qa[g][:, cc, :], in_=pz[:, 0:TT], func=AF.Copy, scale=0.125)
                                elif j == 1:
                                    if g == 0:
                                        act(out=K1[:, cc, 128:128 + TT], in_=pz[:, 0:TT], func=AF.Copy)
                                    else:
                                        act(out=kdst[:, cc, :], in_=pz[:, 0:TT], func=AF.Copy)
                                else:
                                    act(out=VF[:, cc, :], in_=pz[:, 0:TT], func=AF.Copy)
                            yield
                        if g == 0:
                            for blk in range(2):
                                for cc in range(4):
                                    tr(out=pTa[:, cc * 128:(cc + 1) * 128], in_=VF[:, cc, blk * 128:(blk + 1) * 128], identity=ident)
                                vec("tensor_copy", out=V1[:, 1 + blk, :], in_=pTa[:, 0:512])
                            yield
                        elif g == 1:
                            vec("tensor_copy", out=VF2[:], in_=VF[:])
                    CK('attn_proj')
                    for h in range(8):
                        cc, hp = h // 2, (h % 2) * 64
                        hs = slice(hp, hp + 64)
                        vs = slice(h * 64, (h + 1) * 64)
                        vl = slice(hp, hp + 64)
                        if h % 2 == 0:
                            for r in range(4):
                                tr(out=pTa[0:64, r * 128:(r + 1) * 128],
                                   in_=VF2[:, cc, :].re("p (i r) -> p r i", r=4)[:, r, :], identity=ident)
                            vec("tensor_copy", out=V2c[0:64, :, :].re("p r c -> p (r c)"), in_=pTa[0:64, 0:512])
                            for r in range(16):
                                tr(out=pTa[0:16, (r % 4) * 128:(r % 4 + 1) * 128],
                                   in_=VF[:, cc, :].re("p (i r) -> p r i", r=16)[:, r, :], identity=ident)
                                if r % 4 == 3:
                                    vec("tensor_copy", out=V3c[0:16, r - 3:r + 1, :].re("p a c -> p (a c)"), in_=pTa[0:16, 0:512])
                                yield
                        if not phaseB:
                            if h % 2 == 1:
                                S.dma("gpsimd", out=V2r[j0_2 * 64:(j0_2 + 1) * 64, :, cc * 128:(cc + 1) * 128], in_=V2c[0:64, :, :])
                                S.dma("gpsimd", out=V3r[j0_3 * 16:(j0_3 + 1) * 16, :, cc * 128:(cc + 1) * 128], in_=V3c[0:16, :, :])
                            continue
                        for blk in range(2):
                            qv = Qa[0][hs, cc, blk * 128:(blk + 1) * 128]
                            mm(out=B5[:, (blk * 2) * 128:(blk * 2 + 1) * 128], lhsT=K1[hs, cc, blk * 128:(blk + 1) * 128],
                               rhs=qv, start=True, stop=True)
                            mm(out=B5[:, (blk * 2 + 1) * 128:(blk * 2 + 2) * 128],
                               lhsT=K1[hs, cc, 128 + blk * 128:128 + (blk + 1) * 128], rhs=qv, start=True, stop=True)
                        act(out=pe[:, :], in_=B5[:, 0:512], func=AF.Exp)
                        yield
                        vec("tensor_tensor", out=pp_[:, :].re("p (b e) -> p b e", b=2), in0=pe[:, :].re("p (b e) -> p b e", b=2),
                            in1=cb[:, CB_E1 + h * 256:CB_E1 + (h + 1) * 256].bc(1, [128, 2, 256]), op=ALU.mult)
                        vec("tensor_scalar", out=pp_[:, 0:128], in0=pp_[:, 0:128], scalar1=vprev, scalar2=None, op0=ALU.mult)
                        yield
                        for blk in range(2):
                            mm(out=B6[0:64, blk * 128:(blk + 1) * 128], lhsT=V1[:, blk, vs],
                               rhs=pp_[:, (blk * 2) * 128:(blk * 2 + 1) * 128], start=True, stop=False)
                            mm(out=B6[0:64, blk * 128:(blk + 1) * 128], lhsT=V1[:, blk + 1, vs],
                               rhs=pp_[:, (blk * 2 + 1) * 128:(blk * 2 + 2) * 128], start=False, stop=True)
                        ppv = pp_[:, :].re("p (b c q) -> p b c q", b=2, c=2)
                        mm(out=B6[0:64, 256:512], lhsT=cb[:, CB_ONES:CB_ONES + 64], rhs=ppv[:, :, 0, :], start=True, stop=False)
                        mm(out=B6[0:64, 256:512], lhsT=cb[:, CB_ONES:CB_ONES + 64], rhs=ppv[:, :, 1, :], start=False, stop=True)
                        act(out=accO[:, :], in_=B6[0:64, 0:TT], func=AF.Copy)
                        act(out=accD[:, :], in_=B6[0:64, 256:512], func=AF.Copy)
                        yield
                        for r in range(4):
                            qv = Qa[1][hs, cc, :].re("p (i r) -> p r i", r=4)[:, r, :]
                            mm(out=B5[:, r * 64:(r + 1) * 64], lhsT=K2r[hs, cc, r, :], rhs=qv, start=True, stop=True)
                            mm(out=B5[0:64, 256 + r * 64:256 + (r + 1) * 64],
                               lhsT=K2c[hs, cc, :].re("p (i r) -> p r i", r=4)[:, r, :], rhs=qv, start=True, stop=True)
                        act(out=pe[:, 0:256], in_=B5[:, 0:256], func=AF.Exp)
                        act(out=peb[0:64, :], in_=B5[0:64, 256:512], func=AF.Exp)
                        yield
                        ea = cb[:, CB_EA2 + (j0_2 * 8 + h) * 64:CB_EA2 + (j0_2 * 8 + h + 1) * 64]
                        vec("scalar_tensor_tensor", out=pp_[:, 0:256].re("p (r i) -> p r i", r=4),
                            in0=pe[:, 0:256].re("p (r i) -> p r i", r=4), scalar=vr2, in1=ea.bc(1, [128, 4, 64]),
                            op0=ALU.mult, op1=ALU.mult)
                        eb = cb[0:64, CB_EB2 + h * 64:CB_EB2 + (h + 1) * 64]
                        vec("tensor_tensor", out=ppb[0:64, :].re("p (r i) -> p r i", r=4),
                            in0=peb[0:64, :].re("p (r i) -> p r i", r=4), in1=eb.bc(1, [64, 4, 64]), op=ALU.mult)
                        yield
                        for r in range(4):
                            mm(out=B6[0:64, r * 64:(r + 1) * 64], lhsT=V2r[:, r, vs], rhs=pp_[:, r * 64:(r + 1) * 64],
                               start=True, stop=False)
                            mm(out=B6[0:64, r * 64:(r + 1) * 64], lhsT=V2c[0:64, r, vl], rhs=ppb[0:64, r * 64:(r + 1) * 64],
                               start=False, stop=True)
                        mm(out=B6[0:64, 256:512], lhsT=cb[:, CB_ONES:CB_ONES + 64], rhs=pp_[:, 0:256], start=True, stop=False)
                        mm(out=B6[0:64, 256:512], lhsT=cb[0:64, CB_ONES:CB_ONES + 64], rhs=ppb[0:64, :], start=False, stop=True)
                        vec("tensor_tensor", out=accO[:, :].re("p (i r) -> p r i", r=4), in0=accO[:, :].re("p (i r) -> p r i", r=4),
                            in1=B6[0:64, 0:TT].re("p (r i) -> p r i", r=4), op=ALU.add)
                        vec("tensor_tensor", out=accD[:, :].re("p (i r) -> p r i", r=4), in0=accD[:, :].re("p (i r) -> p r i", r=4),
                            in1=B6[0:64, 256:512].re("p (r i) -> p r i", r=4), op=ALU.add)
                        yield
                        for r in range(16):
                            qv = Qa[2][hs, cc, :].re("p (i r) -> p r i", r=16)[:, r, :]
                            mm(out=B5[:, r * 16:(r + 1) * 16], lhsT=K3r[hs, cc, r, :], rhs=qv, start=True, stop=True)
                            mm(out=B5[0:16, 256 + r * 16:256 + (r + 1) * 16],
                               lhsT=K3c[hs, cc, :].re("p (i r) -> p r i", r=16)[:, r, :], rhs=qv, start=True, stop=True)
                        act(out=pe[:, 0:256], in_=B5[:, 0:256], func=AF.Exp)
                        act(out=peb[0:16, :], in_=B5[0:16, 256:512], func=AF.Exp)
                        yield
                        ea = cb[:, CB_EA3 + (j0_3 * 8 + h) * 16:CB_EA3 + (j0_3 * 8 + h + 1) * 16]
                        vec("scalar_tensor_tensor", out=pp_[:, 0:256].re("p (r i) -> p r i", r=16),
                            in0=pe[:, 0:256].re("p (r i) -> p r i", r=16), scalar=vr3, in1=ea.bc(1, [128, 16, 16]),
                            op0=ALU.mult, op1=ALU.mult)
                        eb = cb[0:16, CB_EB3 + h * 16:CB_EB3 + (h + 1) * 16]
                        vec("tensor_tensor", out=ppb[0:16, :].re("p (r i) -> p r i", r=16),
                            in0=peb[0:16, :].re("p (r i) -> p r i", r=16), in1=eb.bc(1, [16, 16, 16]), op=ALU.mult)
                        yield
                        for r in range(16):
                            mm(out=B6[0:64, r * 16:(r + 1) * 16], lhsT=V3r[:, r, vs], rhs=pp_[:, r * 16:(r + 1) * 16],
                               start=True, stop=False)
                            mm(out=B6[0:64, r * 16:(r + 1) * 16], lhsT=V3c[0:16, r, vl], rhs=ppb[0:16, r * 16:(r + 1) * 16],
                               start=False, stop=True)
                        mm(out=B6[0:64, 256:512], lhsT=cb[:, CB_ONES:CB_ONES + 64], rhs=pp_[:, 0:256], start=True, stop=False)
                        mm(out=B6[0:64, 256:512], lhsT=cb[0:16, CB_ONES:CB_ONES + 64], rhs=ppb[0:16, :], start=False, stop=True)
                        yield
                        if phaseB:
                            vec("tensor_tensor", out=accO[:, :].re("p (i r) -> p r i", r=16),
                                in0=accO[:, :].re("p (i r) -> p r i", r=16),
                                in1=B6[0:64, 0:TT].re("p (r i) -> p r i", r=16), op=ALU.add)
                            vec("tensor_tensor", out=accD[:, :].re("p (i r) -> p r i", r=16),
                                in0=accD[:, :].re("p (i r) -> p r i", r=16),
                                in1=B6[0:64, 256:512].re("p (r i) -> p r i", r=16), op=ALU.add)
                            vec("reciprocal", out=accD[:, :], in_=accD[:, :])
                            vec("tensor_tensor", out=oT[:, h, :], in0=accO[:, :], in1=accD[:, :], op=ALU.mult)
                        if h % 2 == 1:
                            S.dma("gpsimd", out=V2r[j0_2 * 64:(j0_2 + 1) * 64, :, cc * 128:(cc + 1) * 128], in_=V2c[0:64, :, :])
                            S.dma("gpsimd", out=V3r[j0_3 * 16:(j0_3 + 1) * 16, :, cc * 128:(cc + 1) * 128], in_=V3c[0:16, :, :])
                    CK('attn')
                    vec("tensor_copy", out=K1[:, :, 0:128], in_=K1[:, :, TT:TT + 128])
                    vec("tensor_copy", out=V1[:, 0, :], in_=V1[:, 2, :])
                    vec("tensor_copy", out=K2r[:, :, :, j0_2 * 64:(j0_2 + 1) * 64],
                        in_=K2c[:].re("p c (i r) -> p c r i", r=4))
                    vec("tensor_copy", out=K3r[:, :, :, j0_3 * 16:(j0_3 + 1) * 16],
                        in_=K3c[:].re("p c (i r) -> p c r i", r=16))

                ths = [th_rwkv(), th_attn()]
                if not INTERLEAVE:
                    for t_ in ths:
                        for _ in t_:
                            pass
                    ths = []
                while ths:
                    for t_ in list(ths):
                        try:
                            next(t_)
                        except StopIteration:
                            ths.remove(t_)

                if not phaseB:
                    continue
                for cc in range(8):
                    wpb = wload(CH_PB + cc)
                    wv = wpb[:, :].re("p (a k c) -> p a k c", a=4, c=128)
                    pz = nextp()
                    for kc in range(8):
                        mm(out=pz[:, 0:TT], lhsT=wv[:, 0, kc, :], rhs=hT[:, kc, 1:TT + 1], start=(kc == 0), stop=(kc == 7))
                    sigmoid_to(sa[:, :], pz[:, 0:TT])
                    pz = nextp()
                    for kc in range(8):
                        mm(out=pz[:, 0:TT], lhsT=wv[:, 1, kc, :], rhs=hT[:, kc, 1:TT + 1], start=(kc == 0), stop=(kc == 7))
                    sigmoid_to(sbb[:, :], pz[:, 0:TT])
                    pz = nextp()
                    for kc in range(8):
                        mm(out=pz[:, 0:TT], lhsT=wv[:, 2, kc, :], rhs=yfin[:, kc, :], start=(kc == 0), stop=(kc == 7))
                    vec("tensor_tensor", out=sa[:, :], in0=sa[:, :], in1=pz[:, 0:TT], op=ALU.mult)
                    pz = nextp()
                    for hh_ in range(8):
                        mm(out=pz[:, 0:TT], lhsT=wv[0:64, 3, hh_, :], rhs=oT[:, hh_, :], start=(hh_ == 0), stop=(hh_ == 7))
                    vec("tensor_tensor", out=sbb[:, :], in0=sbb[:, :], in1=pz[:, 0:TT], op=ALU.mult)
                    vec("tensor_tensor", out=mixT[:, cc, :], in0=sa[:, :], in1=sbb[:, :], op=ALU.add)

                def norm_residual(ps_views, gb):
                    for hf in range(2):
                        act(out=junk[:, 0:512], in_=ps_views[hf], func=AF.Square, accum_out=st4[:, 8 + hf:9 + hf])
                    vec("tensor_tensor", out=st4[:, 10:11], in0=st4[:, 8:9], in1=st4[:, 9:10], op=ALU.add)
                    rsqrt_to(st4[:, 4:5], st4[:, 10:11], eps_r, 1.0 / D)
                    for hf in range(2):
                        for qq in range(2):
                            cs_ = slice(hf * 512 + qq * 256, hf * 512 + (qq + 1) * 256)
                            vec("scalar_tensor_tensor", out=utmp[:, :], in0=ps_views[hf][:, qq * 256:(qq + 1) * 256],
                                scalar=st4[:, 4:5], in1=gb[:, cs_], op0=ALU.mult, op1=ALU.mult)
                            vec("tensor_tensor", out=xm[:, sub, cs_], in0=xm[:, sub, cs_], in1=utmp[:, :], op=ALU.add)

                wo = [wload(CH_WOUT + 0), wload(CH_WOUT + 1)]
                for sub in range(NSUB):
                    for hf in range(2):
                        wv = wo[hf][:, :].re("p (k c) -> p k c", c=512)
                        for kc in range(8):
                            mm(out=(B1, B2)[hf][:, :], lhsT=mixT[:, kc, sub * 128:(sub + 1) * 128], rhs=wv[:, kc, :],
                               start=(kc == 0), stop=(kc == 7))
                    norm_residual([B1[:, :], B2[:, :]], gmb)
                rms_rstd(xm, 2)
                norm_transpose(xm, 2, h2T, PD_A2, lambda kc: modf[:, 24 + kc:25 + kc], 0)
                accs = [[B1[:, :], B2[:, :]], [B3[:, :], B5[:, :]]]
                for pg in range(3):
                    nk = 8 if pg < 2 else 6
                    for i4 in range(nk // 2):
                        i = pg * 4 + i4
                        wf_ = wload(CH_FF + i)
                        wv = wf_[:, :].re("p (k c) -> p k c", c=512)
                        for jj in range(2):
                            jl = i4 * 2 + jj
                            pg_ = nextp()
                            for kc in range(8):
                                mm(out=pg_[:, 0:TT], lhsT=wv[:, kc, jj * 128:(jj + 1) * 128], rhs=h2T[:, kc, :],
                                   start=(kc == 0), stop=(kc == 7))
                            act(out=sg[:, :], in_=pg_[:, 0:TT], func=AF.Silu)
                            pu = nextp()
                            for kc in range(8):
                                mm(out=pu[:, 0:TT], lhsT=wv[:, kc, 256 + jj * 128:256 + (jj + 1) * 128], rhs=h2T[:, kc, :],
                                   start=(kc == 0), stop=(kc == 7))
                            vec("tensor_tensor", out=actT[:, jl, :], in0=sg[:, :], in1=pu[:, 0:TT], op=ALU.mult)
                    for hf in range(2):
                        wf_ = wload(CH_FO + pg * 2 + hf)
                        wv = wf_[:, :].re("p (k c) -> p k c", c=512)
                        for sub in range(NSUB):
                            for kc in range(nk):
                                mm(out=accs[sub][hf], lhsT=actT[:, kc, sub * 128:(sub + 1) * 128], rhs=wv[:, kc, :],
                                   start=(pg == 0 and kc == 0), stop=(pg == 2 and kc == nk - 1))
                for sub in range(NSUB):
                    norm_residual(accs[sub], gfb)
                r0 = (m - PB0) * TT
                S.dma("gpsimd", out=y_d.v(y_d.t[r0:r0 + TT, :].rearrange("(s p) c -> p s c", p=128)), in_=xm[:])


        except _Stop:
            pass
        S.finish([y_d] + finals)
        S.emit()
    return nc


_CACHE = {}


def prep_inputs(x, c, w_mod, b_mod, g_pre_mix, g_post_mix, g_pre_ffn, g_post_ffn, w_in, mu_rkv, mu_lora,
           w0, w1, w2, a0, a1, a2, g1, g2, k_k, k_a, r_k, ln_x_w, ln_x_b, w_o_rwkv, w_o_attn, w_out,
           w_ffn_in, w_ffn_out):
    f = lambda a: np.asarray(a, np.float32)
    x = f(x); c = f(c)
    w_in = f(w_in)[0]; w_modm = f(w_mod)[0]
    bm = f(b_mod)[0].reshape(6, 1024)
    vecs = [bm[0], bm[1], bm[2], bm[3], bm[4], bm[5], f(g_pre_mix)[0], f(g_post_mix)[0], f(g_pre_ffn)[0],
            f(g_post_ffn)[0], f(mu_rkv)[0, 0], f(mu_rkv)[0, 1], f(mu_rkv)[0, 2], f(mu_lora)[0, 0], f(mu_lora)[0, 1],
            f(mu_lora)[0, 2], f(w0)[0], f(a0)[0], f(k_k)[0], f(k_a)[0], f(r_k)[0].reshape(-1), f(ln_x_w)[0],
            f(ln_x_b)[0]]
    wsrc = np.zeros((NCH, 128, 4096), np.float32)
    def put(i, arr3):
        P, K, C = arr3.shape
        v = wsrc[i].reshape(128, -1)
        tmp = np.zeros((128, K, 4096 // K if K in (8,) else C), np.float32) if False else None
        blk = np.zeros((128, K * C), np.float32)
        blk[:P] = arr3.reshape(P, K * C)
        v[:, :K * C] = blk
    for cch in range(8):
        a = np.zeros((128, 8, 512), np.float32)
        for j in range(3):
            a[:, :, j * 128:(j + 1) * 128] = _wchunk(w_in, slice(j * 1024 + cch * 128, j * 1024 + (cch + 1) * 128))
        put(CH_RW + cch, a)
    for g in range(3):
        for j in range(3):
            o = 3072 + j * 1536 + g * 512
            put(CH_AT + g * 3 + j, _wchunk(w_in, slice(o, o + 512)))
    wor = f(w_o_rwkv)[0]; woa = f(w_o_attn)[0]; wout = f(w_out)[0]
    for cc in range(8):
        cs_ = slice(cc * 128, (cc + 1) * 128)
        a = np.zeros((128, 4, 8, 128), np.float32)
        a[:, 0] = _wchunk(w_in, slice(7680 + cc * 128, 7680 + (cc + 1) * 128))
        a[:, 1] = _wchunk(w_in, slice(8704 + cc * 128, 8704 + (cc + 1) * 128))
        a[:, 2] = _wchunk(wor, cs_)
        a[0:64, 3] = woa[:, cs_].reshape(8, 64, 128).transpose(1, 0, 2)
        put(CH_PB + cc, a.reshape(128, 32, 128))
    for hf in range(2):
        put(CH_WOUT + hf, _wchunk(wout, slice(hf * 512, (hf + 1) * 512)))
    wfi = f(w_ffn_in)[0]; wfo = f(w_ffn_out)[0]
    for i in range(11):
        a = np.zeros((128, 8, 512), np.float32)
        a[:, :, 0:256] = _wchunk(wfi, slice(i * 256, (i + 1) * 256))
        a[:, :, 256:512] = _wchunk(wfi, slice(FH + i * 256, FH + (i + 1) * 256))
        put(CH_FF + i, a)
    for pg in range(3):
        nk = 8 if pg < 2 else 6
        for hf in range(2):
            blk = wfo[pg * 1024:pg * 1024 + nk * 128, hf * 512:(hf + 1) * 512]
            put(CH_FO + pg * 2 + hf, blk.reshape(nk, 128, 512).transpose(1, 0, 2))
    wmod = np.ascontiguousarray(
        w_modm.reshape(8, 128, 24, 256).transpose(2, 1, 0, 3).reshape(24, 128, 2048))
    l1 = np.concatenate([f(w1)[0], f(a1)[0], f(g1)[0]], 1)
    l1 = np.ascontiguousarray(l1.reshape(8, 128, 288).transpose(1, 0, 2).reshape(128, 8 * 288))
    l2 = np.zeros((128, 3, 1024), np.float32)
    l2[0:64, 0] = f(w2)[0]; l2[64:128, 0] = f(a2)[0]
    l2[:, 1] = f(g2)[0][0:128]; l2[0:32, 2] = f(g2)[0][128:160]
    l2 = l2.reshape(128, 3072)
    cbt = _host_consts()
    in_maps = []
    for core in range(8):
        b, hh = core // 2, core % 2
        pfm = np.concatenate([_fm(v) for v in vecs] + [_fm(c[b])], 1)
        if hh == 1:
            xvv = x[b]
        else:
            xvv = np.concatenate([np.zeros((T // 2, D), np.float32), x[b, :T // 2]], 0)
        in_maps.append({"xv": np.ascontiguousarray(xvv), "pfm": np.ascontiguousarray(pfm), "wmod": wmod,
                        "wsrc": wsrc, "l1": l1, "l2": l2, "cbt": cbt, "cft": _host_cf(hh)})
    return in_maps


def kernel(**inputs):
    in_maps = prep_inputs(**inputs)
    if "nc" not in _CACHE:
        _CACHE["nc"] = build()
    nc = _CACHE["nc"]
    res = run_bass_kernel_spmd(nc, in_maps, core_ids=list(range(8)))
    out = np.zeros((4, T, D), np.float32)
    for core in range(8):
        b, hh = core // 2, core % 2
        out[b, hh * (T // 2):(hh + 1) * (T // 2)] = res.results[core]["y"]
    return out
```

```python
import math
from contextlib import ExitStack

import numpy as np
import concourse.bass as bass
import concourse.mybir as mybir
from concourse.bass_utils import run_bass_kernel_spmd

F32 = mybir.dt.float32
BF16 = mybir.dt.bfloat16
AF = mybir.ActivationFunctionType
ALU = mybir.AluOpType
AX = mybir.AxisListType

ENGS = ("tensor", "vector", "scalar", "gpsimd", "sync")

T = 8192
D = 1024
TT = 256
NT = T // TT
PB0 = NT // 2
NSUB = TT // 128
FH = 2816
C0 = math.exp(-0.5)
GN_EPS = 64e-5
RMS_EPS = 1e-6
NSLOT = 4
import os
INTERLEAVE = os.environ.get('NOIL') is None


class Buf:
    def __init__(self, name, t):
        self.name = name
        self.t = t
        self.writer = None
        self.readers = []
        self.dsem = None
        self.dcnt = 0
        self.psum = False

    def __getitem__(self, idx):
        return View(self, self.t[idx])

    def v(self, ap):
        return View(self, ap)


class SubBuf:
    def __init__(self, buf, col0):
        self.buf = buf
        self.col0 = col0

    def __getitem__(self, idx):
        ps, cs = idx
        a = 0 if cs.start is None else cs.start
        assert cs.stop is not None
        return View(self.buf, self.buf.t[ps, self.col0 + a:self.col0 + cs.stop])


class View:
    def __init__(self, buf, ap):
        self.buf = buf
        self.ap = ap

    def __getitem__(self, idx):
        return View(self.buf, self.ap[idx])

    def re(self, pat, **kw):
        return View(self.buf, self.ap.rearrange(pat, **kw))

    def bc(self, axis, shape):
        return View(self.buf, self.ap.unsqueeze(axis).to_broadcast(list(shape)))


def _unw(x):
    return x.ap if isinstance(x, View) else x


class Sched:
    def __init__(self, nc, stack):
        self.nc = nc
        self.stack = stack
        self.q = {e: [] for e in ENGS}
        self.waited = {e: {} for e in ENGS}
        self.dma_sems = []

    def sb(self, name, shape, dt):
        t = self.stack.enter_context(self.nc.sbuf_tensor("s_" + name, list(shape), dt))
        return Buf(name, t)

    def ps(self, name, shape, dt=F32):
        t = self.stack.enter_context(self.nc.psum_tensor("p_" + name, list(shape), dt))
        return Buf(name, t)

    def dram(self, name, shape, dt, kind):
        t = self.nc.dram_tensor(name, list(shape), dt, kind=kind).ap()
        return Buf(name, t)

    def _deps(self, eng, reads, writes):
        deps = {}

        def add(tok):
            if tok is None:
                return
            k, v = tok
            if deps.get(k, 0) < v:
                deps[k] = v

        for b in reads:
            add(b.writer)
            if b.psum:
                for r in b.readers:
                    if r[0] != eng:
                        add(r)
        for b in writes:
            add(b.writer)
            for r in b.readers:
                add(r)
        waits = []
        for k, v in deps.items():
            if k == "tensor" and eng == "tensor":
                continue
            if self.waited[eng].get(k, 0) >= v:
                continue
            self.waited[eng][k] = v
            waits.append((k, v))
            if isinstance(k, str):
                self.q[k][v - 1][2] = True
        return waits

    def _commit(self, tok, reads, writes):
        for b in writes:
            b.writer = tok
            b.readers = []
        for b in reads:
            if b in writes:
                continue
            b.readers.append(tok)
            if len(b.readers) > 48:
                d = {}
                for k, v in b.readers:
                    if d.get(k, 0) < v:
                        d[k] = v
                b.readers = list(d.items())

    def op(self, eng, meth, **kw):
        writes, reads = [], []
        for k, v in kw.items():
            if isinstance(v, View):
                if k in ("out", "accum_out", "ap"):
                    if v.buf not in writes:
                        writes.append(v.buf)
                else:
                    if v.buf not in reads:
                        reads.append(v.buf)
        waits = self._deps(eng, reads, writes)
        if eng == "tensor":
            src = kw.get("lhsT", kw.get("in_"))
            lo = src.ap.base_partition()
            rows = (lo, lo + src.ap.partition_size())
            ob = kw["out"].buf
            prev = getattr(ob, "pe_rows", None)
            if prev is not None and ob.writer is not None and ob.writer[0] == "tensor" and \
                    (rows[1] <= prev[0] or prev[1] <= rows[0]):
                k, v = ob.writer
                if self.waited[eng].get(k, 0) < v:
                    self.waited[eng][k] = v
                    waits.append((k, v))
                    self.q[k][v - 1][2] = True
            ob.pe_rows = rows
        args = {k: _unw(v) for k, v in kw.items()}
        fn = lambda e, m=meth, a=args: getattr(e, m)(**a)
        self.q[eng].append([waits, fn, False, None])
        tok = (eng, len(self.q[eng]))
        self._commit(tok, reads, writes)
        return tok

    def dma(self, eng, out, in_, **kw):
        sb = out.buf
        if sb.dsem is None:
            sb.dsem = ("dma", len(self.dma_sems))
            self.dma_sems.append(sb.name)
        waits = self._deps(eng, [in_.buf], [out.buf])
        sb.dcnt += 16
        tok = (sb.dsem, sb.dcnt)
        a = dict(out=out.ap, in_=in_.ap, **kw)
        fn = lambda e, a=a: e.dma_start(**a)
        self.q[eng].append([waits, fn, False, sb.dsem])
        self._commit(tok, [in_.buf], [out.buf])
        return tok

    def finish(self, final_bufs):
        waits = self._deps("sync", final_bufs, [])
        self.q["sync"].append([waits, None, False, None])

    def emit(self):
        nc = self.nc
        st = self.stack
        esem = {e: st.enter_context(nc.semaphore("es_" + e)) for e in ENGS}
        dsem = [st.enter_context(nc.semaphore("ds%d" % i)) for i in range(len(self.dma_sems))]
        cum = {}
        for e in ENGS:
            c = 0
            arr = []
            for it in self.q[e]:
                if it[2]:
                    c += 1
                arr.append(c)
            cum[e] = arr

        def semval(k, v):
            if isinstance(k, str):
                return esem[k], cum[k][v - 1]
            return dsem[k[1]], v

        block = st.enter_context(nc.Block())

        def run(e, eng):
            for waits, fn, sig, dk in self.q[e]:
                for k, v in waits:
                    s, val = semval(k, v)
                    eng.wait_ge(s, val)
                if fn is None:
                    continue
                ins = fn(eng)
                if dk is not None:
                    ins.then_inc(dsem[dk[1]], 16)
                elif sig:
                    ins.then_inc(esem[e], 1)

        @block.tensor
        def _(eng):
            run("tensor", eng)

        @block.vector
        def _(eng):
            run("vector", eng)

        @block.scalar
        def _(eng):
            run("scalar", eng)

        @block.gpsimd
        def _(eng):
            run("gpsimd", eng)

        @block.sync
        def _(eng):
            run("sync", eng)


def _alibi_slopes(n):
    def pow2(m):
        start = 2.0 ** (-8.0 / m)
        return [start ** (i + 1) for i in range(m)]
    if math.log2(n).is_integer():
        s = pow2(n)
    else:
        p = 2 ** int(math.floor(math.log2(n)))
        s = pow2(p) + pow2(2 * p)[0::2][: n - p]
    return sorted(s, reverse=True)


(PV_SHM, PV_SCM, PV_GTM, PV_SHF, PV_SCF, PV_GTF, PV_GPM, PV_GQM, PV_GPF, PV_GQF,
 PV_MUR, PV_MUK, PV_MUV, PV_MUW, PV_MUA, PV_MUG, PV_W0, PV_A0, PV_KK, PV_KA, PV_RK,
 PV_LNW, PV_LNB, PV_C) = range(24)
NPV = 24

CB_ID = 0
CB_ONESBD = 128
CB_ONES = 256
CB_MT4 = 320
CB_ML4 = 832
CB_E1 = 1344
CB_EA2 = CB_E1 + 8 * 256
CB_EB2 = CB_EA2 + 2 * 8 * 64
CB_EA3 = CB_EB2 + 8 * 64
CB_EB3 = CB_EA3 + 8 * 8 * 16
NCB = CB_EB3 + 8 * 16
CF_MSK = 0
CF_VM = 256
CF_EPS = CF_VM + 128
CF_IDF = CF_EPS + 4
NCF = CF_IDF + 128

CH_RW = 0
CH_AT = 8
CH_PB = 17
CH_WOUT = 25
CH_FF = 27
CH_FO = 38
NCH = 44


def _host_consts():
    sl = np.asarray(_alibi_slopes(24), np.float64).reshape(3, 8)
    cb = np.zeros((128, NCB), np.float32)
    p = np.arange(128)
    cb[:, CB_ID:CB_ID + 128] = np.eye(128)
    cb[:, CB_ONESBD:CB_ONESBD + 128] = (p[:, None] // 64 == p[None, :] // 64)
    cb[:, CB_ONES:CB_ONES + 64] = 1.0
    same = (p[:, None] // 64 == p[None, :] // 64)
    su = same & (p[:, None] < p[None, :])
    iu = same & (p[:, None] <= p[None, :])
    slo = same & (p[:, None] > p[None, :])
    cb[:, CB_MT4:CB_MT4 + 512] = np.concatenate([su, iu, su, iu], 1)
    cb[:, CB_ML4:CB_ML4 + 512] = np.concatenate([slo] * 4, 1)
    k = p[:, None].astype(np.float64)
    q = p[None, :].astype(np.float64)
    for h in range(8):
        dpv = q - k + 128
        e_prev = np.where(dpv <= 128, np.exp(-sl[0, h] * dpv), 0.0)
        dcu = q - k
        e_cur = np.where(dcu >= 0, np.exp(-sl[0, h] * np.maximum(dcu, 0)), 0.0)
        cb[:, CB_E1 + h * 256: CB_E1 + h * 256 + 128] = e_prev
        cb[:, CB_E1 + h * 256 + 128: CB_E1 + h * 256 + 256] = e_cur
    i64 = np.arange(64)[None, :].astype(np.float64)
    for rot in range(2):
        for h in range(8):
            j = p // 64
            pp = (p % 64).astype(np.float64)
            a = ((rot - j - 1) % 2) + 1
            dl = 64.0 * a[:, None] + i64 - pp[:, None]
            e = np.where(dl <= 128, np.exp(-sl[1, h] * 4.0 * dl), 0.0)
            o = CB_EA2 + (rot * 8 + h) * 64
            cb[:, o:o + 64] = e
    for h in range(8):
        kk = np.arange(64)[:, None].astype(np.float64)
        dl = i64 - kk
        e = np.where(dl >= 0, np.exp(-sl[1, h] * 4.0 * np.maximum(dl, 0)), 0.0)
        o = CB_EB2 + h * 64
        cb[0:64, o:o + 64] = e
    i16 = np.arange(16)[None, :].astype(np.float64)
    for rot in range(8):
        for h in range(8):
            j = p // 16
            pp = (p % 16).astype(np.float64)
            a = ((rot - j - 1) % 8) + 1
            dl = 16.0 * a[:, None] + i16 - pp[:, None]
            e = np.where(dl <= 128, np.exp(-sl[2, h] * 16.0 * dl), 0.0)
            o = CB_EA3 + (rot * 8 + h) * 16
            cb[:, o:o + 16] = e
    for h in range(8):
        kk = np.arange(16)[:, None].astype(np.float64)
        dl = i16 - kk
        e = np.where(dl >= 0, np.exp(-sl[2, h] * 16.0 * np.maximum(dl, 0)), 0.0)
        o = CB_EB3 + h * 16
        cb[0:16, o:o + 16] = e
    return cb


def _host_cf(hh):
    cf = np.zeros((128, NCF), np.float32)
    m = np.ones((128, 256), np.float32)
    m[:, 0::64] = 0.0
    cf[:, CF_MSK:CF_MSK + 256] = m
    valid = lambda t: 0.0 if t < 0 else (1.0 if (hh == 1 or t >= PB0) else 0.0)
    p = np.arange(128)
    for t in range(NT):
        cf[:, CF_VM + t] = valid(t)
        cf[:, CF_VM + 32 + t] = valid(t - 1)
        j = p // 64
        a = ((t - j - 1) % 2) + 1
        cf[:, CF_VM + 64 + t] = [valid(t - aa) for aa in a]
        j = p // 16
        a = ((t - j - 1) % 8) + 1
        cf[:, CF_VM + 96 + t] = [valid(t - aa) for aa in a]
    cf[:, CF_EPS] = RMS_EPS
    cf[:, CF_EPS + 1] = GN_EPS
    cf[:, CF_EPS + 3] = 1.0
    cf[:, CF_IDF:CF_IDF + 128] = np.eye(128)
    return cf


def _fm(v):
    return np.ascontiguousarray(v.reshape(8, 128).T)


def _wchunk(w, cols):
    return w[:, cols].reshape(8, 128, -1).transpose(1, 0, 2)


class _Stop(Exception):
    pass


def build(nt=NT, dbg=None, dbg_tile=0, dbg_c=0, stop=None):
    nc = bass.Bass("TRN2", target_bir_lowering=False)
    with ExitStack() as st:
        S = Sched(nc, st)
        finals = []

        def CK(name):
            if stop == name:
                raise _Stop()

        def DBG(name, view, m=None, c=None):
            if not dbg or name not in dbg:
                return
            if m is not None and m != dbg_tile:
                return
            if c is not None and c != dbg_c:
                return
            shp = list(view.ap.shape)
            dd = S.dram("dbg_" + name, shp, view.ap.dtype, "ExternalOutput")
            S.dma("gpsimd", out=dd[:], in_=view)
            finals.append(dd)
        xv = S.dram("xv", [T, D], F32, "ExternalInput")
        pfm_d = S.dram("pfm", [128, NPV * 8], F32, "ExternalInput")
        wmod_d = S.dram("wmod", [24, 128, 2048], F32, "ExternalInput")
        wsrc = S.dram("wsrc", [NCH, 128, 4096], F32, "ExternalInput")
        l1_d = S.dram("l1", [128, 8 * 288], F32, "ExternalInput")
        l2_d = S.dram("l2", [128, 3 * 1024], F32, "ExternalInput")
        cb_d = S.dram("cbt", [128, NCB], F32, "ExternalInput")
        cf_d = S.dram("cft", [128, NCF], F32, "ExternalInput")
        y_d = S.dram("y", [T // 2, D], F32, "ExternalOutput")
        wscr = S.dram("wscr", [NCH, 128, 4096], BF16, "Internal")

        cb = S.sb("cb", [128, NCB], BF16)
        cf = S.sb("cf", [128, NCF], F32)
        pf = S.sb("pf", [128, NPV * 8], F32)
        pd = S.sb("pd", [128, 12 * 8], F32)
        gmb = S.sb("gmb", [128, 1024], BF16)
        gfb = S.sb("gfb", [128, 1024], BF16)
        l1a = S.sb("l1a", [128, 8, 288], BF16)
        l1b = S.sb("l1b", [128, 8, 288], BF16)
        l2 = S.sb("l2", [128, 3, 1024], BF16)
        ring = [S.sb("ring%d" % i, [128, 4096], BF16) for i in range(NSLOT)]
        xt1 = S.sb("xt", [128, NSUB, 1024], F32)
        xt = [xt1, xt1]
        nb = S.sb("nb", [128, NSUB, 1024], BF16)
        junk = nb[:, 0, :]
        st4 = S.sb("st4", [128, 16], F32)
        hT = S.sb("hT", [128, 8, TT + 1], BF16)
        h2T = S.sb("h2T", [128, 8, TT], BF16)
        mixT = h2T
        Zf = S.sb("Zf", [128, 8, 64], F32)
        Zb = S.sb("Zb", [128, 8, 2, 64], BF16)
        hal = S.sb("hal", [128, 8, 3], F32)
        tp = [S.sb("tp%d" % i, [128, TT + 1], F32) for i in range(12)]
        tv = lambda i: tp[i][:, 0:TT]
        pj = [tp[0], tp[0], tp[0]]
        tmpd = tv(1)
        rkv = [tv(2), tv(3), tv(4)]
        sw = tv(5); asig = tv(6); gg = tv(7); cs = tv(0); cm = tv(1)
        Ep = tv(8); En = tv(9); Em = tv(10); rinv = tv(1); kkb = tv(11); ff = tv(0)
        kmod = tv(5); bv = tv(1); bon = tv(6); yln = tv(9); ysq = tv(10)
        sqb = S.sb("sqb", [128, TT], BF16)
        ARs = [S.sb("AR%d" % i, [128, NSUB, 2, 128], BF16) for i in range(2)]
        Bt = S.sb("Bt", [128, TT], BF16)
        Kt = S.sb("Kt", [128, TT], BF16)
        vbf = S.sb("vbf", [128, TT], BF16)
        Bpads = [S.sb("Bpad%d" % i, [128, NSUB, 2, 128], BF16) for i in range(2)]
        Kpads = [S.sb("Kpad%d" % i, [128, NSUB, 2, 128], BF16) for i in range(2)]
        Vtms = [S.sb("Vtm%d" % i, [128, NSUB, 128], BF16) for i in range(2)]
        AMs = [S.sb("AM%d" % i, [128, 4, 512], BF16) for i in range(2)]
        TTfs = [S.sb("TTf%d" % i, [128, 4, 128], BF16) for i in range(2)]
        gbs = [S.sb("gb%d" % i, [128, TT], BF16) for i in range(2)]
        pcss = [S.sb("pcs%d" % i, [128, 4], F32) for i in range(2)]
        ysqB = S.sb("ysqB", [128, TT], F32)
        L0 = S.sb("L0", [128, 4, 128], BF16)
        LP = [S.sb("LP%d" % i, [128, 4, 128], BF16) for i in range(2)]
        LT = [S.sb("LT%d" % i, [128, 4, 128], BF16) for i in range(2)]
        SS = [S.sb("SS%d" % i, [128, 4, 128], BF16) for i in range(2)]
        Xb = S.sb("Xb", [128, 128], BF16)
        Ub = S.sb("Ub", [128, 128], BF16)
        ztmp = S.sb("ztmp", [128, 64], F32)
        Ytm = S.sb("Ytm", [128, NSUB, 128], F32)
        ynb = S.sb("ynb", [128, NSUB, 128], BF16)
        gst = S.sb("gst", [128, 32], F32)
        lw = S.sb("lw", [128, TT], BF16)
        lga = S.sb("lga", [128, TT], BF16)
        lgb = S.sb("lgb", [32, TT], BF16)
        yfin = S.sb("yfin", [128, 8, TT], BF16)
        _nbf = nb[:].re("p s c -> p (s c)")
        Qa = [h2T[:, 0:4, :], h2T[:, 4:8, :], _nbf[:, 0:1024].re("p (c t) -> p c t", t=TT)]
        K1 = S.sb("K1", [128, 4, 128 + TT], BF16)
        V1 = S.sb("V1", [128, 3, 512], BF16)
        K2c = S.sb("K2c", [128, 4, TT], BF16)
        K2r = S.sb("K2r", [128, 4, 4, 128], BF16)
        V2c = S.sb("V2c", [64, 4, 128], BF16)
        V2r = S.sb("V2r", [128, 4, 512], BF16)
        K3c = S.sb("K3c", [128, 4, TT], BF16)
        K3r = S.sb("K3r", [128, 4, 16, 128], BF16)
        V3c = S.sb("V3c", [16, 16, 128], BF16)
        V3r = S.sb("V3r", [128, 16, 512], BF16)
        VF = _nbf[:, 1024:2048].re("p (c t) -> p c t", t=TT)
        pe = S.sb("pe", [128, 512], BF16)
        pp_ = S.sb("pp", [128, 512], BF16)
        peb = S.sb("peb", [64, 256], BF16)
        ppb = S.sb("ppb", [64, 256], BF16)
        accO = S.sb("accO", [64, TT], F32)
        accD = S.sb("accD", [64, TT], F32)
        oT = S.sb("oT", [64, 8, TT], BF16)
        sa = tv(2); sbb = tv(3); sg = tv(4); utmp = tv(10)
        actT = S.sb("actT", [128, 8, TT], BF16)
        VF2 = actT[:, 0:4, :]
        wst = xt1[:].re("p s c -> p (s c)")

        _b0 = S.ps("b0", [128, 512])
        _b4 = S.ps("b4", [128, 512])
        _bS = S.ps("bS", [128, 1024])
        B3 = S.ps("b3", [128, 512])
        B5 = S.ps("b5", [128, 512])
        B6 = S.ps("b6", [128, 512])
        _pT = S.ps("pT", [128, 1024], BF16)
        R0 = SubBuf(_b0, 0); R1 = SubBuf(_b0, 256)
        Q0 = SubBuf(_b4, 0); Q1 = SubBuf(_b4, 256)
        B1 = Buf("B1", _bS.t[:, 0:512]); B2 = Buf("B2", _bS.t[:, 512:1024])
        pTr = SubBuf(_pT, 0); pTa = SubBuf(_pT, 512)
        for b_ in (_b0, _b4, B1, B2, B3, B5, B6, _pT):
            b_.psum = True
        pC = B3
        prot = {"r": [R0, R1], "a": [Q0, Q1], "x": [R0, R1, Q0, Q1]}
        prot_i = {"r": 0, "a": 0, "x": 0}

        def nextp(k="x"):
            prot_i[k] = (prot_i[k] + 1) % len(prot[k])
            return prot[k][prot_i[k]]

        mm = lambda **kw: S.op("tensor", "matmul", **kw)
        tr = lambda **kw: S.op("tensor", "transpose", **kw)
        act = lambda **kw: S.op("scalar", "activation", **kw)
        vec = lambda m, **kw: S.op("vector", m, **kw)
        gps = lambda m, **kw: S.op("gpsimd", m, **kw)

        def sigmoid_to(dst, src, nbias=None, scale=1.0):
            if nbias is None:
                act(out=dst, in_=src, func=AF.Exp, scale=-scale)
            else:
                act(out=dst, in_=src, func=AF.Exp, scale=-scale, bias=nbias)
            act(out=dst, in_=dst, func=AF.Ln, bias=one_c_for(dst))
            act(out=dst, in_=dst, func=AF.Exp, scale=-1.0)

        def one_c_for(v):
            lo = v.ap.base_partition()
            n = v.ap.partition_size()
            return cf[lo:lo + n, CF_EPS + 3:CF_EPS + 4]

        def rsqrt_to(dst, src, bias_ap, scale=1.0):
            act(out=dst, in_=src, func=AF.Ln, bias=bias_ap, scale=scale)
            act(out=dst, in_=dst, func=AF.Exp, scale=-0.5)

        ident = cb[:, CB_ID:CB_ID + 128]
        identf = cf[:, CF_IDF:CF_IDF + 128]
        onesbd = cb[:, CB_ONESBD:CB_ONESBD + 128]
        eps_r = cf[:, CF_EPS:CF_EPS + 1]
        eps_g = cf[:, CF_EPS + 1:CF_EPS + 2]
        zero_c = cf[:, CF_EPS + 2:CF_EPS + 3]
        one_c = cf[:, CF_EPS + 3:CF_EPS + 4]

        def pv(i, kc):
            return pf[:, i * 8 + kc: i * 8 + kc + 1]

        def pdv(i, kc):
            return pd[:, i * 8 + kc: i * 8 + kc + 1]
        PD_A1, PD_A2, PD_GM, PD_GF, PD_OMK, PD_OMR, PD_OMKm, PD_OMV = range(8)

        try:
            S.dma("gpsimd", out=cb[:, :], in_=cb_d[:, :])
            S.dma("sync", out=cf[:, :], in_=cf_d[:, :])
            S.dma("sync", out=pf[:, :], in_=pfm_d[:, :])
            for i in range(NCH):
                S.dma("gpsimd", out=wscr[i], in_=wsrc[i])
            CK('dma0')
            for b_ in (Zf, Zb, hal, Bpads[0], Bpads[1], Kpads[0], Kpads[1], K1, K2r, V2r, K3r, V3r, V1, hT, Xb, Ub, Vtms[0], Vtms[1]):
                gps("memset", ap=b_[:], constant=0.0)
            CK('memset')
            for half in range(2):
                S.dma("sync", out=wst[:, 0:4 * 288], in_=l1_d[:, half * 4 * 288:(half + 1) * 4 * 288])
                w1v = wst[:, 0:4 * 288].re("p (k c) -> p k c", c=288)
                for k4 in range(4):
                    kc = half * 4 + k4
                    for (lo, hi, mui) in ((0, 64, PV_MUW), (64, 128, PV_MUA), (128, 288, PV_MUG)):
                        vec("tensor_scalar", out=l1b[:, kc, lo:hi], in0=w1v[:, k4, lo:hi], scalar1=pv(mui, kc),
                            scalar2=None, op0=ALU.mult)
                        vec("tensor_tensor", out=l1a[:, kc, lo:hi], in0=w1v[:, k4, lo:hi], in1=l1b[:, kc, lo:hi],
                            op=ALU.subtract)
            for half in range(2):
                S.dma("sync", out=wst[:, 0:1536], in_=l2_d[:, half * 1536:(half + 1) * 1536])
                vec("tensor_copy", out=l2[:].re("p a c -> p (a c)")[:, half * 1536:(half + 1) * 1536], in_=wst[:, 0:1536])
            CK('lora0')
            for j in range(24):
                S.dma("sync", out=wst[:, 0:2048], in_=wmod_d[j])
                wv = wst[:, 0:2048].re("p (k c) -> p k c", c=256)
                for cc in range(2):
                    col = j * 2 + cc
                    for kc in range(8):
                        mm(out=B1[:, col:col + 1], lhsT=wv[:, kc, cc * 128:(cc + 1) * 128], rhs=pv(PV_C, kc),
                           start=(kc == 0), stop=(kc == 7))
            modf = S.sb("modf", [128, 48], F32)
            vec("tensor_tensor", out=modf[:, :], in0=B1[:, 0:48], in1=pf[:, 0:48], op=ALU.add)
            for kc in range(8):
                vec("scalar_tensor_tensor", out=pdv(PD_A1, kc), in0=modf[:, 8 + kc:9 + kc], scalar=1.0,
                    in1=pv(PV_GPM, kc), op0=ALU.add, op1=ALU.mult)
                vec("scalar_tensor_tensor", out=pdv(PD_A2, kc), in0=modf[:, 32 + kc:33 + kc], scalar=1.0,
                    in1=pv(PV_GPF, kc), op0=ALU.add, op1=ALU.mult)
                vec("tensor_tensor", out=pdv(PD_GM, kc), in0=modf[:, 16 + kc:17 + kc], in1=pv(PV_GQM, kc), op=ALU.mult)
                vec("tensor_tensor", out=pdv(PD_GF, kc), in0=modf[:, 40 + kc:41 + kc], in1=pv(PV_GQF, kc), op=ALU.mult)
                vec("tensor_scalar", out=pdv(PD_OMK, kc), in0=pv(PV_KA, kc), scalar1=-1.0, scalar2=1.0,
                    op0=ALU.mult, op1=ALU.add)
                vec("tensor_scalar", out=pdv(5, kc), in0=pv(PV_W0, kc), scalar1=-1.0, scalar2=None, op0=ALU.mult)
                vec("tensor_scalar", out=pdv(6, kc), in0=pv(PV_A0, kc), scalar1=-1.0, scalar2=None, op0=ALU.mult)
            dg = S.sb("dg", [128, 128], F32)
            onesf = S.sb("onesf", [128, 128], F32)
            gps("memset", ap=onesf[:, :], constant=1.0)
            for (pdi, dst) in ((PD_GM, gmb), (PD_GF, gfb)):
                for kc in range(8):
                    vec("tensor_scalar", out=dg[:, :], in0=identf, scalar1=pdv(pdi, kc), scalar2=None, op0=ALU.mult)
                    pz = nextp()
                    mm(out=pz[:, 0:128], lhsT=onesf[:, :], rhs=dg[:, :], start=True, stop=True)
                    act(out=dst[:, kc * 128:(kc + 1) * 128], in_=pz[:, 0:128], func=AF.Copy)

            CK('startup')
            ring_i = [0]

            ring_sets = {"r": ring[0:2], "a": ring[2:4], "x": ring}
            ring_k = {"r": 0, "a": 0, "x": 0}

            def wload(ch, k="x"):
                s = ring_sets[k][ring_k[k] % len(ring_sets[k])]
                ring_k[k] += 1
                S.dma("sync", out=s[:, :], in_=wscr[ch])
                return s

            def rms_rstd(src3, dst_cols, nsub=NSUB):
                for sub in range(nsub):
                    act(out=junk[:, :], in_=src3[:, sub, :], func=AF.Square,
                        accum_out=st4[:, 8 + sub:9 + sub])
                rsqrt_to(st4[:, dst_cols:dst_cols + nsub], st4[:, 8:8 + nsub], eps_r, 1.0 / D)

            def norm_transpose(xsrc, rcol, dstT, a_idx, b_view_fn, halo):
                for sub in range(NSUB):
                    vec("tensor_scalar", out=nb[:, sub, :], in0=xsrc[:, sub, :], scalar1=st4[:, rcol + sub:rcol + sub + 1],
                        scalar2=None, op0=ALU.mult)
                for kc in range(8):
                    for sub in range(NSUB):
                        tr(out=pTr[:, (kc % 2) * 256 + sub * 128:(kc % 2) * 256 + (sub + 1) * 128], in_=nb[:, sub, kc * 128:(kc + 1) * 128], identity=ident)
                    act(out=dstT[:, kc, halo:halo + TT], in_=pTr[:, (kc % 2) * 256:(kc % 2) * 256 + TT], func=AF.Identity,
                        scale=pdv(a_idx, kc), bias=b_view_fn(kc))

            for m in range(nt):
                phaseB = m >= PB0
                xm = xt[m % 2]
                vcur = cf[:, CF_VM + m:CF_VM + m + 1]
                vprev = cf[:, CF_VM + 32 + m:CF_VM + 33 + m]
                vr2 = cf[:, CF_VM + 64 + m:CF_VM + 65 + m]
                vr3 = cf[:, CF_VM + 96 + m:CF_VM + 97 + m]
                S.dma("gpsimd", out=xm[:], in_=xv.v(xv.t[m * TT:(m + 1) * TT, :].rearrange("(s p) c -> p s c", p=128)))
                if m > 0:
                    vec("tensor_scalar", out=hT[:, :, 0:1], in0=hT[:, :, TT:TT + 1],
                        scalar1=cf[:, CF_VM + m - 1:CF_VM + m], scalar2=None, op0=ALU.mult)
                rms_rstd(xm, 0)
                norm_transpose(xm, 0, hT, PD_A1, lambda kc: pf[:, PV_SHM * 8 + kc:PV_SHM * 8 + kc + 1]
                               if False else modf[:, kc:kc + 1], 1)

                CK('stage1')
                pz = nextp()
                for kc in range(8):
                    mm(out=pz[:, 0:TT], lhsT=l1a[:, kc, 0:128], rhs=hT[:, kc, 1:TT + 1], start=(kc == 0), stop=False)
                    mm(out=pz[:, 0:TT], lhsT=l1b[:, kc, 0:128], rhs=hT[:, kc, 0:TT], start=False, stop=(kc == 7))
                sigmoid_to(tp[11][0:64, 0:TT], pz[0:64, 0:TT], None, 2.0)
                vec("tensor_scalar", out=lw[0:64, :], in0=tp[11][0:64, 0:TT], scalar1=2.0, scalar2=-1.0, op0=ALU.mult, op1=ALU.add)
                act(out=lw[64:128, :], in_=pz[64:128, 0:TT], func=AF.Copy)
                if phaseB:
                    pz = nextp()
                    for kc in range(8):
                        mm(out=pz[:, 0:TT], lhsT=l1a[:, kc, 128:256], rhs=hT[:, kc, 1:TT + 1], start=(kc == 0), stop=False)
                        mm(out=pz[:, 0:TT], lhsT=l1b[:, kc, 128:256], rhs=hT[:, kc, 0:TT], start=False, stop=(kc == 7))
                    sigmoid_to(tp[11][:, 0:TT], pz[:, 0:TT])
                    act(out=lga[:, :], in_=tp[11][:, 0:TT], func=AF.Copy)
                    pz = nextp()
                    for kc in range(8):
                        mm(out=pz[0:32, 0:TT], lhsT=l1a[:, kc, 256:288], rhs=hT[:, kc, 1:TT + 1], start=(kc == 0), stop=False)
                        mm(out=pz[0:32, 0:TT], lhsT=l1b[:, kc, 256:288], rhs=hT[:, kc, 0:TT], start=False, stop=(kc == 7))
                    sigmoid_to(tp[11][0:32, 0:TT], pz[0:32, 0:TT])
                    act(out=lgb[0:32, :], in_=tp[11][0:32, 0:TT], func=AF.Copy)

                CK('lora1')
                def rw_front(c0):
                    for c in (c0,):
                        AR = ARs[c % 2]; AM = AMs[c % 2]; Vtm = Vtms[c % 2]; Bpad = Bpads[c % 2]; Kpad = Kpads[c % 2]
                        TTf = TTfs[c % 2]; gb = gbs[c % 2]; pcs = pcss[c % 2]
                        csl = slice(c * 128, (c + 1) * 128)
                        wr = wload(CH_RW + c, 'r')
                        wrv = wr[:, :].re("p (k c) -> p k c", c=512)
                        for j in ((0, 1, 2) if m >= PB0 - 1 else (1, 2)):
                            pz = nextp("r")
                            for kc in range(8):
                                mm(out=pz[:, 0:TT], lhsT=wrv[:, kc, j * 128:(j + 1) * 128], rhs=hT[:, kc, 1:TT + 1],
                                   start=(kc == 0), stop=(kc == 7))
                            vec("tensor_copy", out=pj[j][:, 0:1], in_=hal[:, c, j:j + 1])
                            act(out=pj[j][:, 1:TT + 1], in_=pz[:, 0:TT], func=AF.Copy)
                            vec("tensor_scalar", out=hal[:, c, j:j + 1], in0=pj[j][:, TT:TT + 1], scalar1=vcur,
                                scalar2=None, op0=ALU.mult)
                            vec("tensor_tensor", out=tmpd[:, :], in0=pj[j][:, 0:TT], in1=pj[j][:, 1:TT + 1], op=ALU.subtract)
                            vec("scalar_tensor_tensor", out=rkv[j][:, :], in0=tmpd[:, :], scalar=pv(PV_MUR + j, c),
                                in1=pj[j][:, 1:TT + 1], op0=ALU.mult, op1=ALU.add)
                            yield
                        r_, k_, v_ = rkv
                        vec("tensor_scalar", out=v_[:, :], in0=v_[:, :], scalar1=vcur, scalar2=None, op0=ALU.mult)
                        act(out=vbf[:, :], in_=v_[:, :], func=AF.Copy)
                        pz = nextp("r")
                        mm(out=pz[:, 0:TT], lhsT=l2[0:64, 0, csl], rhs=lw[0:64, :], start=True, stop=True)
                        sigmoid_to(sw[:, :], pz[:, 0:TT], pdv(5, c))
                        pz = nextp("r")
                        mm(out=pz[:, 0:TT], lhsT=l2[64:128, 0, csl], rhs=lw[64:128, :], start=True, stop=True)
                        sigmoid_to(asig[:, :], pz[:, 0:TT], pdv(6, c))
                        pz = nextp("r")
                        if phaseB:
                            mm(out=pz[:, 0:TT], lhsT=l2[:, 1, csl], rhs=lga[:, :], start=True, stop=False)
                            mm(out=pz[:, 0:TT], lhsT=l2[0:32, 2, csl], rhs=lgb[0:32, :], start=False, stop=True)
                            act(out=gb[:, :], in_=pz[:, 0:TT], func=AF.Copy)
                        yield
                        vec("tensor_tensor_scan", out=cs[:, :], data0=cf[:, CF_MSK:CF_MSK + TT], data1=sw[:, :],
                            initial=0.0, op0=ALU.mult, op1=ALU.add)
                        vec("tensor_tensor", out=cm[:, :], in0=cs[:, :], in1=sw[:, :], op=ALU.subtract)
                        act(out=Ep[:, :], in_=cs[:, :], func=AF.Exp, scale=-C0)
                        act(out=En[:, :], in_=cs[:, :], func=AF.Exp, scale=C0)
                        act(out=Em[:, :], in_=cm[:, :], func=AF.Exp, scale=-C0)
                        yield
                        act(out=sqb[:, :], in_=k_[:, :], func=AF.Square, scale=pv(PV_KK, c))
                        pz = nextp("r")
                        mm(out=pz[:, 0:TT], lhsT=onesbd, rhs=sqb[:, :], start=True, stop=True)
                        vec("tensor_scalar", out=rinv[:, :], in0=pz[:, 0:TT], scalar1=1e-18, scalar2=None, op0=ALU.max)
                        act(out=rinv[:, :], in_=rinv[:, :], func=AF.Ln)
                        act(out=rinv[:, :], in_=rinv[:, :], func=AF.Exp, scale=-0.5)
                        vec("scalar_tensor_tensor", out=kkb[:, :], in0=k_[:, :], scalar=pv(PV_KK, c), in1=rinv[:, :],
                            op0=ALU.mult, op1=ALU.mult)
                        yield
                        vec("tensor_scalar", out=ff[:, :], in0=asig[:, :], scalar1=pv(PV_KA, c), scalar2=pdv(PD_OMK, c),
                            op0=ALU.mult, op1=ALU.add)
                        vec("tensor_tensor", out=kmod[:, :], in0=k_[:, :], in1=ff[:, :], op=ALU.mult)
                        vec("tensor_tensor", out=bv[:, :], in0=kkb[:, :], in1=asig[:, :], op=ALU.mult)
                        vec("scalar_tensor_tensor", out=AR[:, :, 0, :], in0=kkb[:, :].re("p (s t) -> p s t", t=128), scalar=-1.0,
                            in1=Em[:, :].re("p (s t) -> p s t", t=128), op0=ALU.mult, op1=ALU.mult)
                        if phaseB:
                            vec("tensor_tensor", out=AR[:, :, 1, :], in0=r_[:, :].re("p (s t) -> p s t", t=128),
                                in1=Ep[:, :].re("p (s t) -> p s t", t=128), op=ALU.mult)
                        vec("tensor_tensor", out=Bt[:, :], in0=bv[:, :], in1=En[:, :], op=ALU.mult)
                        vec("tensor_tensor", out=Kt[:, :], in0=kmod[:, :], in1=En[:, :], op=ALU.mult)
                        yield
                        if phaseB:
                            vec("tensor_tensor", out=tmpd[:, :], in0=r_[:, :], in1=kmod[:, :], op=ALU.mult)
                            act(out=sqb[:, :], in_=tmpd[:, :], func=AF.Copy, scale=pv(PV_RK, c))
                            pz = nextp("r")
                            mm(out=pz[:, 0:TT], lhsT=onesbd, rhs=sqb[:, :], start=True, stop=True)
                            vec("tensor_tensor", out=bon[:, :], in0=pz[:, 0:TT], in1=v_[:, :], op=ALU.mult)
                            vec("tensor_tensor", out=yfin[:, c, :], in0=bon[:, :], in1=gb[:, :], op=ALU.mult)
                        yield
                        for qi, (src, dst) in enumerate(((Bt, Bpad), (Kt, Kpad), (vbf, None))):
                            for sub in range(NSUB):
                                tr(out=pTr[:, (qi % 2) * 256 + sub * 128: (qi % 2) * 256 + (sub + 1) * 128],
                                   in_=src[:, sub * 128:(sub + 1) * 128], identity=ident)
                            yield
                            srcv = pTr[:, (qi % 2) * 256:(qi % 2 + 1) * 256]
                            if dst is None:
                                act(out=Vtm[:].re("p s c -> p (s c)"), in_=srcv, func=AF.Copy)
                            else:
                                for h in range(2):
                                    act(out=dst[:, :, h, h * 64:(h + 1) * 64],
                                        in_=srcv.re("p (s c) -> p s c", c=128)[:, :, h * 64:(h + 1) * 64], func=AF.Copy)
                        CK('rwkv_a')
                        for h in range(2):
                            hs = slice(h * 64, (h + 1) * 64)
                            for sub in range(NSUB):
                                u = h * NSUB + sub
                                tsl = slice(sub * 128, (sub + 1) * 128)
                                pz = (B1, B2)[u % 2]
                                if phaseB:
                                    mm(out=pz[:, 0:256], lhsT=Bt[hs, tsl], rhs=AR[hs, sub, :, :].re("p a t -> p (a t)"),
                                       start=True, stop=True)
                                    mm(out=pz[:, 256:512], lhsT=Kt[hs, tsl], rhs=AR[hs, sub, :, :].re("p a t -> p (a t)"),
                                       start=True, stop=True)
                                    vec("tensor_tensor", out=AM[:, u, :], in0=pz[:, :], in1=cb[:, CB_MT4:CB_MT4 + 512], op=ALU.mult)
                                else:
                                    mm(out=pz[:, 0:128], lhsT=Bt[hs, tsl], rhs=AR[hs, sub, 0, :], start=True, stop=True)
                                    mm(out=pz[:, 256:384], lhsT=Kt[hs, tsl], rhs=AR[hs, sub, 0, :], start=True, stop=True)
                                    v4 = lambda ap_: ap_.re("p (a two b) -> p a two b", a=2, two=2)[:, :, 0, :]
                                    vec("tensor_tensor", out=v4(AM[:, u, :]), in0=v4(pz[:, :]),
                                        in1=v4(cb[:, CB_MT4:CB_MT4 + 512]), op=ALU.mult)
                                mm(out=_b0[:, u * 128:(u + 1) * 128], lhsT=AR[hs, sub, 0, :], rhs=Bt[hs, tsl],
                                   start=True, stop=True)
                                yield
                        vec("tensor_tensor", out=L0[:].re("p u t -> p (u t)"), in0=_b0[:, :], in1=cb[:, CB_ML4:CB_ML4 + 512],
                            op=ALU.mult)
                        yield
                        CK('rwkv_b')
                        vec("tensor_tensor", out=SS[0][:], in0=AM[:, :, 0:128], in1=ident.bc(1, [128, 4, 128]), op=ALU.add)
                        lt_prev = lambda u: AM[:, u, 0:128]
                        lp_prev = lambda u: L0[:, u, :]
                        scur = 0
                        for lev in range(1, 6):
                            lpn = LP[lev % 2]
                            ltn = LT[lev % 2]
                            for u in range(4):
                                mm(out=B1[:, u * 128:(u + 1) * 128], lhsT=lt_prev(u), rhs=lp_prev(u), start=True, stop=True)
                            if lev <= 4:
                                for u in range(4):
                                    mm(out=B2[:, u * 128:(u + 1) * 128], lhsT=lp_prev(u), rhs=lt_prev(u),
                                       start=True, stop=True)
                            act(out=lpn[:].re("p u t -> p (u t)"), in_=B1[:, :], func=AF.Copy)
                            if lev <= 4:
                                act(out=ltn[:].re("p u t -> p (u t)"), in_=B2[:, :], func=AF.Copy)
                            yield
                            for u in range(4):
                                mm(out=_b0[:, u * 128:(u + 1) * 128], lhsT=lpn[:, u, :], rhs=SS[scur][:, u, :], start=True, stop=True)
                            sdst = TTf if lev == 5 else SS[1 - scur]
                            vec("tensor_tensor", out=sdst[:].re("p u t -> p (u t)"), in0=_b0[:, :],
                                in1=SS[scur][:].re("p u t -> p (u t)"), op=ALU.add)
                            scur = 1 - scur
                            lt_prev = (lambda b: (lambda u: b[:, u, :]))(ltn)
                            yield
                            lp_prev = (lambda b: (lambda u: b[:, u, :]))(lpn)
                        vec("tensor_copy", out=pcs[:, 0:4], in_=Ep[:, :].re("p (q t) -> p q t", t=64)[:, :, 63])
                        yield

                def rw_back(c0):
                    for c in (c0,):
                        AR = ARs[c % 2]; AM = AMs[c % 2]; Vtm = Vtms[c % 2]; Bpad = Bpads[c % 2]; Kpad = Kpads[c % 2]
                        TTf = TTfs[c % 2]; gb = gbs[c % 2]; pcs = pcss[c % 2]
                        TTm = TTf
                        ysq = ysqB[:, :]
                        yln = ysqB[:, :]
                        CK('rwkv_c')
                        for q in range(2 * NSUB):
                            sub, half = q // 2, q % 2
                            ps_ = slice(half * 64, half * 64 + 64)
                            tsl = slice(sub * 128, (sub + 1) * 128)
                            zi = q % 2
                            for h in range(2):
                                hs = slice(h * 64, (h + 1) * 64)
                                u = h * NSUB + sub
                                mm(out=pC[:, hs], lhsT=AR[hs, sub, 0, :], rhs=Zb[hs, c, zi, :], start=True, stop=False)
                                mm(out=pC[:, hs], lhsT=AM[:, u, 256:384], rhs=Vtm[:, sub, hs], start=False, stop=True)
                            act(out=Xb[ps_, :], in_=pC[ps_, 0:128], func=AF.Copy)
                            yield
                            CK('c1')
                            for h in range(2):
                                hs = slice(h * 64, (h + 1) * 64)
                                u = h * NSUB + sub
                                mm(out=pC[:, 128 + h * 64:128 + (h + 1) * 64], lhsT=TTm[ps_, u, :], rhs=Xb[ps_, hs],
                                   start=True, stop=True)
                            vec("tensor_copy", out=Ub[ps_, :], in_=pC[ps_, 128:256])
                            yield
                            CK('c2')
                            if phaseB:
                                for h in range(2):
                                    hs = slice(h * 64, (h + 1) * 64)
                                    u = h * NSUB + sub
                                    o_ = slice(256 + h * 64, 256 + (h + 1) * 64)
                                    mm(out=pC[:, o_], lhsT=AR[hs, sub, 1, :], rhs=Zb[hs, c, zi, :], start=True, stop=False)
                                    mm(out=pC[:, o_], lhsT=AM[:, u, 128:256], rhs=Ub[:, hs], start=False, stop=False)
                                    mm(out=pC[:, o_], lhsT=AM[:, u, 384:512], rhs=Vtm[:, sub, hs], start=False, stop=True)
                                act(out=Ytm[ps_, sub, :], in_=pC[ps_, 256:384], func=AF.Copy)
                                yield
                            CK('c3')
                            for h in range(2):
                                hs = slice(h * 64, (h + 1) * 64)
                                mm(out=pC[:, 384:448], lhsT=Bpad[ps_, sub, h, :], rhs=Ub[ps_, hs], start=(h == 0), stop=False)
                                mm(out=pC[:, 384:448], lhsT=Kpad[ps_, sub, h, :], rhs=Vtm[ps_, sub, hs], start=False, stop=(h == 1))
                            CK('c4')
                            pcv = pcs[:, q:q + 1]
                            vec("tensor_scalar", out=ztmp[:, :], in0=Zf[:, c, :], scalar1=pcv, scalar2=None, op0=ALU.mult)
                            vec("scalar_tensor_tensor", out=Zf[:, c, :], in0=pC[:, 384:448], scalar=pcv, in1=ztmp[:, :],
                                op0=ALU.mult, op1=ALU.add)
                            act(out=Zb[:, c, 1 - zi, :], in_=Zf[:, c, :], func=AF.Copy)
                            yield
                        CK('rwkv_d')
                        if phaseB:
                            yv = Ytm[:].re("p s (h i) -> p (s h) i", i=64)
                            vec("tensor_reduce", out=gst[:, 0:4], in_=yv, axis=AX.X, op=ALU.add)
                            act(out=ysq[:, :], in_=Ytm[:].re("p s c -> p (s c)"), func=AF.Square)
                            vec("tensor_reduce", out=gst[:, 4:8], in_=ysq[:, :].re("p (g i) -> p g i", i=64), axis=AX.X, op=ALU.add)
                            vec("tensor_scalar", out=gst[:, 8:12], in0=gst[:, 0:4], scalar1=1.0 / 64, scalar2=None, op0=ALU.mult)
                            vec("tensor_tensor", out=gst[:, 12:16], in0=gst[:, 8:12], in1=gst[:, 8:12], op=ALU.mult)
                            vec("scalar_tensor_tensor", out=gst[:, 16:20], in0=gst[:, 4:8], scalar=1.0 / 64, in1=gst[:, 12:16],
                                op0=ALU.mult, op1=ALU.subtract)
                            rsqrt_to(gst[:, 24:28], gst[:, 16:20], eps_g, 1.0)
                            ysv = ysq[:, :].re("p (g i) -> p g i", i=64)
                            vec("tensor_tensor", out=ysv, in0=yv, in1=gst[:, 8:12].bc(2, [128, 4, 64]), op=ALU.subtract)
                            vec("tensor_tensor", out=ynb[:].re("p s (h i) -> p (s h) i", i=64), in0=ysv,
                                in1=gst[:, 24:28].bc(2, [128, 4, 64]), op=ALU.mult)
                            yield
                            for sub in range(NSUB):
                                tr(out=pTr[:, 256 + sub * 128:256 + (sub + 1) * 128], in_=ynb[:, sub, :], identity=ident)
                            act(out=yln[:, :], in_=pTr[:, 256:256 + TT], func=AF.Identity, scale=pv(PV_LNW, c), bias=pv(PV_LNB, c))
                            vec("tensor_tensor", out=yln[:, :], in0=yln[:, :], in1=gb[:, :], op=ALU.mult)
                            vec("tensor_tensor", out=yfin[:, c, :], in0=yln[:, :], in1=yfin[:, c, :], op=ALU.add)
                            yield


                def th_attn():
                    if m < PB0 - 8:
                        return
                    j0_2 = m % 2
                    j0_3 = m % 8
                    for g in range(3):
                        kdst = (K1, K2c, K3c)[g]
                        for j in ((0, 1, 2) if phaseB else (1, 2)):
                            wa = wload(CH_AT + g * 3 + j, 'a')
                            wav = wa[:, :].re("p (k c) -> p k c", c=512)
                            for cc in range(4):
                                pz = nextp("a")
                                for kc in range(8):
                                    mm(out=pz[:, 0:TT], lhsT=wav[:, kc, cc * 128:(cc + 1) * 128], rhs=hT[:, kc, 1:TT + 1],
                                       start=(kc == 0), stop=(kc == 7))
                                if j == 0:
                                    act(out=Qa[g][:, cc, :], in_=pz[:, 0:TT], func=AF.Copy, scale=0.125)
                                elif j == 1:
                                    if g == 0:
                                        act(out=K1[:, cc, 128:128 + TT], in_=pz[:, 0:TT], func=AF.Copy)
                                    else:
                                        act(out=kdst[:, cc, :], in_=pz[:, 0:TT], func=AF.Copy)
                                else:
                                    act(out=VF[:, cc, :], in_=pz[:, 0:TT], func=AF.Copy)
                            yield
                        if g == 0:
                            for blk in range(2):
                                for cc in range(4):
                                    tr(out=pTa[:, cc * 128:(cc + 1) * 128], in_=VF[:, cc, blk * 128:(blk + 1) * 128], identity=ident)
                                vec("tensor_copy", out=V1[:, 1 + blk, :], in_=pTa[:, 0:512])
                            yield
                        elif g == 1:
                            vec("tensor_copy", out=VF2[:], in_=VF[:])
                    CK('attn_proj')
                    for h in range(8):
                        cc, hp = h // 2, (h % 2) * 64
                        hs = slice(hp, hp + 64)
                        vs = slice(h * 64, (h + 1) * 64)
                        vl = slice(hp, hp + 64)
                        if h % 2 == 0:
                            for r in range(4):
                                tr(out=pTa[0:64, r * 128:(r + 1) * 128],
                                   in_=VF2[:, cc, :].re("p (i r) -> p r i", r=4)[:, r, :], identity=ident)
                            vec("tensor_copy", out=V2c[0:64, :, :].re("p r c -> p (r c)"), in_=pTa[0:64, 0:512])
                            for r in range(16):
                                tr(out=pTa[0:16, (r % 4) * 128:(r % 4 + 1) * 128],
                                   in_=VF[:, cc, :].re("p (i r) -> p r i", r=16)[:, r, :], identity=ident)
                                if r % 4 == 3:
                                    vec("tensor_copy", out=V3c[0:16, r - 3:r + 1, :].re("p a c -> p (a c)"), in_=pTa[0:16, 0:512])
                                yield
                        if not phaseB:
                            if h % 2 == 1:
                                S.dma("gpsimd", out=V2r[j0_2 * 64:(j0_2 + 1) * 64, :, cc * 128:(cc + 1) * 128], in_=V2c[0:64, :, :])
                                S.dma("gpsimd", out=V3r[j0_3 * 16:(j0_3 + 1) * 16, :, cc * 128:(cc + 1) * 128], in_=V3c[0:16, :, :])
                            continue
                        for blk in range(2):
                            qv = Qa[0][hs, cc, blk * 128:(blk + 1) * 128]
                            mm(out=B5[:, (blk * 2) * 128:(blk * 2 + 1) * 128], lhsT=K1[hs, cc, blk * 128:(blk + 1) * 128],
                               rhs=qv, start=True, stop=True)
                            mm(out=B5[:, (blk * 2 + 1) * 128:(blk * 2 + 2) * 128],
                               lhsT=K1[hs, cc, 128 + blk * 128:128 + (blk + 1) * 128], rhs=qv, start=True, stop=True)
                        act(out=pe[:, :], in_=B5[:, 0:512], func=AF.Exp)
                        yield
                        vec("tensor_tensor", out=pp_[:, :].re("p (b e) -> p b e", b=2), in0=pe[:, :].re("p (b e) -> p b e", b=2),
                            in1=cb[:, CB_E1 + h * 256:CB_E1 + (h + 1) * 256].bc(1, [128, 2, 256]), op=ALU.mult)
                        vec("tensor_scalar", out=pp_[:, 0:128], in0=pp_[:, 0:128], scalar1=vprev, scalar2=None, op0=ALU.mult)
                        yield
                        for blk in range(2):
                            mm(out=B6[0:64, blk * 128:(blk + 1) * 128], lhsT=V1[:, blk, vs],
                               rhs=pp_[:, (blk * 2) * 128:(blk * 2 + 1) * 128], start=True, stop=False)
                            mm(out=B6[0:64, blk * 128:(blk + 1) * 128], lhsT=V1[:, blk + 1, vs],
                               rhs=pp_[:, (blk * 2 + 1) * 128:(blk * 2 + 2) * 128], start=False, stop=True)
                        ppv = pp_[:, :].re("p (b c q) -> p b c q", b=2, c=2)
                        mm(out=B6[0:64, 256:512], lhsT=cb[:, CB_ONES:CB_ONES + 64], rhs=ppv[:, :, 0, :], start=True, stop=False)
                        mm(out=B6[0:64, 256:512], lhsT=cb[:, CB_ONES:CB_ONES + 64], rhs=ppv[:, :, 1, :], start=False, stop=True)
                        act(out=accO[:, :], in_=B6[0:64, 0:TT], func=AF.Copy)
                        act(out=accD[:, :], in_=B6[0:64, 256:512], func=AF.Copy)
                        yield
                        for r in range(4):
                            qv = Qa[1][hs, cc, :].re("p (i r) -> p r i", r=4)[:, r, :]
                            mm(out=B5[:, r * 64:(r + 1) * 64], lhsT=K2r[hs, cc, r, :], rhs=qv, start=True, stop=True)
                            mm(out=B5[0:64, 256 + r * 64:256 + (r + 1) * 64],
                               lhsT=K2c[hs, cc, :].re("p (i r) -> p r i", r=4)[:, r, :], rhs=qv, start=True, stop=True)
                        act(out=pe[:, 0:256], in_=B5[:, 0:256], func=AF.Exp)
                        act(out=peb[0:64, :], in_=B5[0:64, 256:512], func=AF.Exp)
                        yield
                        ea = cb[:, CB_EA2 + (j0_2 * 8 + h) * 64:CB_EA2 + (j0_2 * 8 + h + 1) * 64]
                        vec("scalar_tensor_tensor", out=pp_[:, 0:256].re("p (r i) -> p r i", r=4),
                            in0=pe[:, 0:256].re("p (r i) -> p r i", r=4), scalar=vr2, in1=ea.bc(1, [128, 4, 64]),
                            op0=ALU.mult, op1=ALU.mult)
                        eb = cb[0:64, CB_EB2 + h * 64:CB_EB2 + (h + 1) * 64]
                        vec("tensor_tensor", out=ppb[0:64, :].re("p (r i) -> p r i", r=4),
                            in0=peb[0:64, :].re("p (r i) -> p r i", r=4), in1=eb.bc(1, [64, 4, 64]), op=ALU.mult)
                        yield
                        for r in range(4):
                            mm(out=B6[0:64, r * 64:(r + 1) * 64], lhsT=V2r[:, r, vs], rhs=pp_[:, r * 64:(r + 1) * 64],
                               start=True, stop=False)
                            mm(out=B6[0:64, r * 64:(r + 1) * 64], lhsT=V2c[0:64, r, vl], rhs=ppb[0:64, r * 64:(r + 1) * 64],
                               start=False, stop=True)
                        mm(out=B6[0:64, 256:512], lhsT=cb[:, CB_ONES:CB_ONES + 64], rhs=pp_[:, 0:256], start=True, stop=False)
                        mm(out=B6[0:64, 256:512], lhsT=cb[0:64, CB_ONES:CB_ONES + 64], rhs=ppb[0:64, :], start=False, stop=True)
                        vec("tensor_tensor", out=accO[:, :].re("p (i r) -> p r i", r=4), in0=accO[:, :].re("p (i r) -> p r i", r=4),
                            in1=B6[0:64, 0:TT].re("p (r i) -> p r i", r=4), op=ALU.add)
                        vec("tensor_tensor", out=accD[:, :].re("p (i r) -> p r i", r=4), in0=accD[:, :].re("p (i r) -> p r i", r=4),
                            in1=B6[0:64, 256:512].re("p (r i) -> p r i", r=4), op=ALU.add)
                        yield
                        for r in range(16):
                            qv = Qa[2][hs, cc, :].re("p (i r) -> p r i", r=16)[:, r, :]
                            mm(out=B5[:, r * 16:(r + 1) * 16], lhsT=K3r[hs, cc, r, :], rhs=qv, start=True, stop=True)
                            mm(out=B5[0:16, 256 + r * 16:256 + (r + 1) * 16],
                               lhsT=K3c[hs, cc, :].re("p (i r) -> p r i", r=16)[:, r, :], rhs=qv, start=True, stop=True)
                        act(out=pe[:, 0:256], in_=B5[:, 0:256], func=AF.Exp)
                        act(out=peb[0:16, :], in_=B5[0:16, 256:512], func=AF.Exp)
                        yield
                        ea = cb[:, CB_EA3 + (j0_3 * 8 + h) * 16:CB_EA3 + (j0_3 * 8 + h + 1) * 16]
                        vec("scalar_tensor_tensor", out=pp_[:, 0:256].re("p (r i) -> p r i", r=16),
                            in0=pe[:, 0:256].re("p (r i) -> p r i", r=16), scalar=vr3, in1=ea.bc(1, [128, 16, 16]),
                            op0=ALU.mult, op1=ALU.mult)
                        eb = cb[0:16, CB_EB3 + h * 16:CB_EB3 + (h + 1) * 16]
                        vec("tensor_tensor", out=ppb[0:16, :].re("p (r i) -> p r i", r=16),
                            in0=peb[0:16, :].re("p (r i) -> p r i", r=16), in1=eb.bc(1, [16, 16, 16]), op=ALU.mult)
                        yield
                        for r in range(16):
                            mm(out=B6[0:64, r * 16:(r + 1) * 16], lhsT=V3r[:, r, vs], rhs=pp_[:, r * 16:(r + 1) * 16],
                               start=True, stop=False)
                            mm(out=B6[0:64, r * 16:(r + 1) * 16], lhsT=V3c[0:16, r, vl], rhs=ppb[0:16, r * 16:(r + 1) * 16],
                               start=False, stop=True)
                        mm(out=B6[0:64, 256:512], lhsT=cb[:, CB_ONES:CB_ONES + 64], rhs=pp_[:, 0:256], start=True, stop=False)
                        mm(out=B6[0:64, 256:512], lhsT=cb[0:16, CB_ONES:CB_ONES + 64], rhs=ppb[0:16, :], start=False, stop=True)
                        yield
                        if phaseB:
                            vec("tensor_tensor", out=accO[:, :].re("p (i r) -> p r i", r=16),
                                in0=accO[:, :].re("p (i r) -> p r i", r=16),
                                in1=B6[0:64, 0:TT].re("p (r i) -> p r i", r=16), op=ALU.add)
                            vec("tensor_tensor", out=accD[:, :].re("p (i r) -> p r i", r=16),
                                in0=accD[:, :].re("p (i r) -> p r i", r=16),
                                in1=B6[0:64, 256:512].re("p (r i) -> p r i", r=16), op=ALU.add)
                            vec("reciprocal", out=accD[:, :], in_=accD[:, :])
                            vec("tensor_tensor", out=oT[:, h, :], in0=accO[:, :], in1=accD[:, :], op=ALU.mult)
                        if h % 2 == 1:
                            S.dma("gpsimd", out=V2r[j0_2 * 64:(j0_2 + 1) * 64, :, cc * 128:(cc + 1) * 128], in_=V2c[0:64, :, :])
                            S.dma("gpsimd", out=V3r[j0_3 * 16:(j0_3 + 1) * 16, :, cc * 128:(cc + 1) * 128], in_=V3c[0:16, :, :])
                    CK('attn')
                    vec("tensor_copy", out=K1[:, :, 0:128], in_=K1[:, :, TT:TT + 128])
                    vec("tensor_copy", out=V1[:, 0, :], in_=V1[:, 2, :])
                    vec("tensor_copy", out=K2r[:, :, :, j0_2 * 64:(j0_2 + 1) * 64],
                        in_=K2c[:].re("p c (i r) -> p c r i", r=4))
                    vec("tensor_copy", out=K3r[:, :, :, j0_3 * 16:(j0_3 + 1) * 16],
                        in_=K3c[:].re("p c (i r) -> p c r i", r=16))

                ag = th_attn()
                ag_done = [False]

                def step_attn():
                    if ag_done[0]:
                        return
                    try:
                        next(ag)
                    except StopIteration:
                        ag_done[0] = True

                def run_rr(gens):
                    gens = list(gens)
                    while gens:
                        for g_ in list(gens):
                            try:
                                next(g_)
                            except StopIteration:
                                gens.remove(g_)
                        if INTERLEAVE:
                            step_attn()

                for k_ in range(9):
                    gens = []
                    if k_ < 8:
                        gens.append(rw_front(k_))
                    if k_ >= 1:
                        gens.append(rw_back(k_ - 1))
                    if INTERLEAVE:
                        run_rr(gens)
                    else:
                        for g_ in reversed(gens):
                            for _ in g_:
                                pass
                while not ag_done[0]:
                    step_attn()

                if not phaseB:
                    continue
                for cc in range(8):
                    wpb = wload(CH_PB + cc)
                    wv = wpb[:, :].re("p (a k c) -> p a k c", a=4, c=128)
                    pz = nextp()
                    for kc in range(8):
                        mm(out=pz[:, 0:TT], lhsT=wv[:, 0, kc, :], rhs=hT[:, kc, 1:TT + 1], start=(kc == 0), stop=(kc == 7))
                    sigmoid_to(sa[:, :], pz[:, 0:TT])
                    pz = nextp()
                    for kc in range(8):
                        mm(out=pz[:, 0:TT], lhsT=wv[:, 1, kc, :], rhs=hT[:, kc, 1:TT + 1], start=(kc == 0), stop=(kc == 7))
                    sigmoid_to(sbb[:, :], pz[:, 0:TT])
                    pz = nextp()
                    for kc in range(8):
                        mm(out=pz[:, 0:TT], lhsT=wv[:, 2, kc, :], rhs=yfin[:, kc, :], start=(kc == 0), stop=(kc == 7))
                    vec("tensor_tensor", out=sa[:, :], in0=sa[:, :], in1=pz[:, 0:TT], op=ALU.mult)
                    pz = nextp()
                    for hh_ in range(8):
                        mm(out=pz[:, 0:TT], lhsT=wv[0:64, 3, hh_, :], rhs=oT[:, hh_, :], start=(hh_ == 0), stop=(hh_ == 7))
                    vec("tensor_tensor", out=sbb[:, :], in0=sbb[:, :], in1=pz[:, 0:TT], op=ALU.mult)
                    vec("tensor_tensor", out=mixT[:, cc, :], in0=sa[:, :], in1=sbb[:, :], op=ALU.add)

                def norm_residual(ps_views, gb):
                    for hf in range(2):
                        act(out=junk[:, 0:512], in_=ps_views[hf], func=AF.Square, accum_out=st4[:, 8 + hf:9 + hf])
                    vec("tensor_tensor", out=st4[:, 10:11], in0=st4[:, 8:9], in1=st4[:, 9:10], op=ALU.add)
                    rsqrt_to(st4[:, 4:5], st4[:, 10:11], eps_r, 1.0 / D)
                    for hf in range(2):
                        for qq in range(2):
                            cs_ = slice(hf * 512 + qq * 256, hf * 512 + (qq + 1) * 256)
                            vec("scalar_tensor_tensor", out=utmp[:, :], in0=ps_views[hf][:, qq * 256:(qq + 1) * 256],
                                scalar=st4[:, 4:5], in1=gb[:, cs_], op0=ALU.mult, op1=ALU.mult)
                            vec("tensor_tensor", out=xm[:, sub, cs_], in0=xm[:, sub, cs_], in1=utmp[:, :], op=ALU.add)

                wo = [wload(CH_WOUT + 0), wload(CH_WOUT + 1)]
                for sub in range(NSUB):
                    for hf in range(2):
                        wv = wo[hf][:, :].re("p (k c) -> p k c", c=512)
                        for kc in range(8):
                            mm(out=(B1, B2)[hf][:, :], lhsT=mixT[:, kc, sub * 128:(sub + 1) * 128], rhs=wv[:, kc, :],
                               start=(kc == 0), stop=(kc == 7))
                    norm_residual([B1[:, :], B2[:, :]], gmb)
                rms_rstd(xm, 2)
                norm_transpose(xm, 2, h2T, PD_A2, lambda kc: modf[:, 24 + kc:25 + kc], 0)
                accs = [[B1[:, :], B2[:, :]], [B3[:, :], B5[:, :]]]
                for pg in range(3):
                    nk = 8 if pg < 2 else 6
                    for i4 in range(nk // 2):
                        i = pg * 4 + i4
                        wf_ = wload(CH_FF + i)
                        wv = wf_[:, :].re("p (k c) -> p k c", c=512)
                        for jj in range(2):
                            jl = i4 * 2 + jj
                            pg_ = nextp()
                            for kc in range(8):
                                mm(out=pg_[:, 0:TT], lhsT=wv[:, kc, jj * 128:(jj + 1) * 128], rhs=h2T[:, kc, :],
                                   start=(kc == 0), stop=(kc == 7))
                            act(out=sg[:, :], in_=pg_[:, 0:TT], func=AF.Silu)
                            pu = nextp()
                            for kc in range(8):
                                mm(out=pu[:, 0:TT], lhsT=wv[:, kc, 256 + jj * 128:256 + (jj + 1) * 128], rhs=h2T[:, kc, :],
                                   start=(kc == 0), stop=(kc == 7))
                            vec("tensor_tensor", out=actT[:, jl, :], in0=sg[:, :], in1=pu[:, 0:TT], op=ALU.mult)
                    for hf in range(2):
                        wf_ = wload(CH_FO + pg * 2 + hf)
                        wv = wf_[:, :].re("p (k c) -> p k c", c=512)
                        for sub in range(NSUB):
                            for kc in range(nk):
                                mm(out=accs[sub][hf], lhsT=actT[:, kc, sub * 128:(sub + 1) * 128], rhs=wv[:, kc, :],
                                   start=(pg == 0 and kc == 0), stop=(pg == 2 and kc == nk - 1))
                for sub in range(NSUB):
                    norm_residual(accs[sub], gfb)
                r0 = (m - PB0) * TT
                S.dma("gpsimd", out=y_d.v(y_d.t[r0:r0 + TT, :].rearrange("(s p) c -> p s c", p=128)), in_=xm[:])


        except _Stop:
            pass
        S.finish([y_d] + finals)
        S.emit()
    return nc


_CACHE = {}


def prep_inputs(x, c, w_mod, b_mod, g_pre_mix, g_post_mix, g_pre_ffn, g_post_ffn, w_in, mu_rkv, mu_lora,
           w0, w1, w2, a0, a1, a2, g1, g2, k_k, k_a, r_k, ln_x_w, ln_x_b, w_o_rwkv, w_o_attn, w_out,
           w_ffn_in, w_ffn_out):
    f = lambda a: np.asarray(a, np.float32)
    x = f(x); c = f(c)
    w_in = f(w_in)[0]; w_modm = f(w_mod)[0]
    bm = f(b_mod)[0].reshape(6, 1024)
    vecs = [bm[0], bm[1], bm[2], bm[3], bm[4], bm[5], f(g_pre_mix)[0], f(g_post_mix)[0], f(g_pre_ffn)[0],
            f(g_post_ffn)[0], f(mu_rkv)[0, 0], f(mu_rkv)[0, 1], f(mu_rkv)[0, 2], f(mu_lora)[0, 0], f(mu_lora)[0, 1],
            f(mu_lora)[0, 2], f(w0)[0], f(a0)[0], f(k_k)[0], f(k_a)[0], f(r_k)[0].reshape(-1), f(ln_x_w)[0],
            f(ln_x_b)[0]]
    wsrc = np.zeros((NCH, 128, 4096), np.float32)
    def put(i, arr3):
        P, K, C = arr3.shape
        v = wsrc[i].reshape(128, -1)
        tmp = np.zeros((128, K, 4096 // K if K in (8,) else C), np.float32) if False else None
        blk = np.zeros((128, K * C), np.float32)
        blk[:P] = arr3.reshape(P, K * C)
        v[:, :K * C] = blk
    for cch in range(8):
        a = np.zeros((128, 8, 512), np.float32)
        for j in range(3):
            a[:, :, j * 128:(j + 1) * 128] = _wchunk(w_in, slice(j * 1024 + cch * 128, j * 1024 + (cch + 1) * 128))
        put(CH_RW + cch, a)
    for g in range(3):
        for j in range(3):
            o = 3072 + j * 1536 + g * 512
            put(CH_AT + g * 3 + j, _wchunk(w_in, slice(o, o + 512)))
    wor = f(w_o_rwkv)[0]; woa = f(w_o_attn)[0]; wout = f(w_out)[0]
    for cc in range(8):
        cs_ = slice(cc * 128, (cc + 1) * 128)
        a = np.zeros((128, 4, 8, 128), np.float32)
        a[:, 0] = _wchunk(w_in, slice(7680 + cc * 128, 7680 + (cc + 1) * 128))
        a[:, 1] = _wchunk(w_in, slice(8704 + cc * 128, 8704 + (cc + 1) * 128))
        a[:, 2] = _wchunk(wor, cs_)
        a[0:64, 3] = woa[:, cs_].reshape(8, 64, 128).transpose(1, 0, 2)
        put(CH_PB + cc, a.reshape(128, 32, 128))
    for hf in range(2):
        put(CH_WOUT + hf, _wchunk(wout, slice(hf * 512, (hf + 1) * 512)))
    wfi = f(w_ffn_in)[0]; wfo = f(w_ffn_out)[0]
    for i in range(11):
        a = np.zeros((128, 8, 512), np.float32)
        a[:, :, 0:256] = _wchunk(wfi, slice(i * 256, (i + 1) * 256))
        a[:, :, 256:512] = _wchunk(wfi, slice(FH + i * 256, FH + (i + 1) * 256))
        put(CH_FF + i, a)
    for pg in range(3):
        nk = 8 if pg < 2 else 6
        for hf in range(2):
            blk = wfo[pg * 1024:pg * 1024 + nk * 128, hf * 512:(hf + 1) * 512]
            put(CH_FO + pg * 2 + hf, blk.reshape(nk, 128, 512).transpose(1, 0, 2))
    wmod = np.ascontiguousarray(
        w_modm.reshape(8, 128, 24, 256).transpose(2, 1, 0, 3).reshape(24, 128, 2048))
    l1 = np.concatenate([f(w1)[0], f(a1)[0], f(g1)[0]], 1)
    l1 = np.ascontiguousarray(l1.reshape(8, 128, 288).transpose(1, 0, 2).reshape(128, 8 * 288))
    l2 = np.zeros((128, 3, 1024), np.float32)
    l2[0:64, 0] = f(w2)[0]; l2[64:128, 0] = f(a2)[0]
    l2[:, 1] = f(g2)[0][0:128]; l2[0:32, 2] = f(g2)[0][128:160]
    l2 = l2.reshape(128, 3072)
    cbt = _host_consts()
    in_maps = []
    for core in range(8):
        b, hh = core // 2, core % 2
        pfm = np.concatenate([_fm(v) for v in vecs] + [_fm(c[b])], 1)
        if hh == 1:
            xvv = x[b]
        else:
            xvv = np.concatenate([np.zeros((T // 2, D), np.float32), x[b, :T // 2]], 0)
        in_maps.append({"xv": np.ascontiguousarray(xvv), "pfm": np.ascontiguousarray(pfm), "wmod": wmod,
                        "wsrc": wsrc, "l1": l1, "l2": l2, "cbt": cbt, "cft": _host_cf(hh)})
    return in_maps


def kernel(**inputs):
    in_maps = prep_inputs(**inputs)
    if "nc" not in _CACHE:
        _CACHE["nc"] = build()
    nc = _CACHE["nc"]
    res = run_bass_kernel_spmd(nc, in_maps, core_ids=list(range(8)))
    out = np.zeros((4, T, D), np.float32)
    for core in range(8):
        b, hh = core // 2, core % 2
        out[b, hh * (T // 2):(hh + 1) * (T // 2)] = res.results[core]["y"]
    return out
```

```python
import math
from contextlib import ExitStack

import numpy as np
import concourse.bass as bass
import concourse.mybir as mybir
from concourse.bass_utils import run_bass_kernel_spmd

F32 = mybir.dt.float32
BF16 = mybir.dt.bfloat16
AF = mybir.ActivationFunctionType
ALU = mybir.AluOpType
AX = mybir.AxisListType

ENGS = ("tensor", "vector", "scalar", "gpsimd", "sync")

T = 8192
D = 1024
TT = 256
NT = T // TT
PB0 = NT // 2
NSUB = TT // 128
FH = 2816
C0 = math.exp(-0.5)
GN_EPS = 64e-5
RMS_EPS = 1e-6
NSLOT = 4
import os
INTERLEAVE = os.environ.get('NOIL') is None


class Buf:
    def __init__(self, name, t):
        self.name = name
        self.t = t
        self.writer = None
        self.readers = []
        self.dsem = None
        self.dcnt = 0
        self.psum = False

    def __getitem__(self, idx):
        return View(self, self.t[idx])

    def v(self, ap):
        return View(self, ap)


class SubBuf:
    def __init__(self, buf, col0):
        self.buf = buf
        self.col0 = col0

    def __getitem__(self, idx):
        ps, cs = idx
        a = 0 if cs.start is None else cs.start
        assert cs.stop is not None
        return View(self.buf, self.buf.t[ps, self.col0 + a:self.col0 + cs.stop])


class View:
    def __init__(self, buf, ap):
        self.buf = buf
        self.ap = ap

    def __getitem__(self, idx):
        return View(self.buf, self.ap[idx])

    def re(self, pat, **kw):
        return View(self.buf, self.ap.rearrange(pat, **kw))

    def bc(self, axis, shape):
        return View(self.buf, self.ap.unsqueeze(axis).to_broadcast(list(shape)))


def _unw(x):
    return x.ap if isinstance(x, View) else x


class Sched:
    def __init__(self, nc, stack):
        self.nc = nc
        self.stack = stack
        self.q = {e: [] for e in ENGS}
        self.waited = {e: {} for e in ENGS}
        self.dma_sems = []

    def sb(self, name, shape, dt):
        t = self.stack.enter_context(self.nc.sbuf_tensor("s_" + name, list(shape), dt))
        return Buf(name, t)

    def ps(self, name, shape, dt=F32):
        t = self.stack.enter_context(self.nc.psum_tensor("p_" + name, list(shape), dt))
        return Buf(name, t)

    def dram(self, name, shape, dt, kind):
        t = self.nc.dram_tensor(name, list(shape), dt, kind=kind).ap()
        return Buf(name, t)

    def _deps(self, eng, reads, writes):
        deps = {}

        def add(tok):
            if tok is None:
                return
            k, v = tok
            if deps.get(k, 0) < v:
                deps[k] = v

        for b in reads:
            add(b.writer)
            if b.psum:
                for r in b.readers:
                    if r[0] != eng:
                        add(r)
        for b in writes:
            add(b.writer)
            for r in b.readers:
                add(r)
        waits = []
        for k, v in deps.items():
            if k == "tensor" and eng == "tensor":
                continue
            if self.waited[eng].get(k, 0) >= v:
                continue
            self.waited[eng][k] = v
            waits.append((k, v))
            if isinstance(k, str):
                self.q[k][v - 1][2] = True
        return waits

    def _commit(self, tok, reads, writes):
        for b in writes:
            b.writer = tok
            b.readers = []
        for b in reads:
            if b in writes:
                continue
            b.readers.append(tok)
            if len(b.readers) > 48:
                d = {}
                for k, v in b.readers:
                    if d.get(k, 0) < v:
                        d[k] = v
                b.readers = list(d.items())

    def op(self, eng, meth, **kw):
        writes, reads = [], []
        for k, v in kw.items():
            if isinstance(v, View):
                if k in ("out", "accum_out", "ap"):
                    if v.buf not in writes:
                        writes.append(v.buf)
                else:
                    if v.buf not in reads:
                        reads.append(v.buf)
        waits = self._deps(eng, reads, writes)
        if eng == "tensor":
            src = kw.get("lhsT", kw.get("in_"))
            lo = src.ap.base_partition()
            rows = (lo, lo + src.ap.partition_size())
            ob = kw["out"].buf
            prev = getattr(ob, "pe_rows", None)
            if prev is not None and ob.writer is not None and ob.writer[0] == "tensor" and \
                    (rows[1] <= prev[0] or prev[1] <= rows[0]):
                k, v = ob.writer
                if self.waited[eng].get(k, 0) < v:
                    self.waited[eng][k] = v
                    waits.append((k, v))
                    self.q[k][v - 1][2] = True
            ob.pe_rows = rows
        args = {k: _unw(v) for k, v in kw.items()}
        fn = lambda e, m=meth, a=args: getattr(e, m)(**a)
        self.q[eng].append([waits, fn, False, None])
        tok = (eng, len(self.q[eng]))
        self._commit(tok, reads, writes)
        return tok

    def dma(self, eng, out, in_, **kw):
        sb = out.buf
        if sb.dsem is None:
            sb.dsem = ("dma", len(self.dma_sems))
            self.dma_sems.append(sb.name)
        waits = self._deps(eng, [in_.buf], [out.buf])
        sb.dcnt += 16
        tok = (sb.dsem, sb.dcnt)
        a = dict(out=out.ap, in_=in_.ap, **kw)
        fn = lambda e, a=a: e.dma_start(**a)
        self.q[eng].append([waits, fn, False, sb.dsem])
        self._commit(tok, [in_.buf], [out.buf])
        return tok

    def finish(self, final_bufs):
        waits = self._deps("sync", final_bufs, [])
        self.q["sync"].append([waits, None, False, None])

    def emit(self):
        nc = self.nc
        st = self.stack
        esem = {e: st.enter_context(nc.semaphore("es_" + e)) for e in ENGS}
        dsem = [st.enter_context(nc.semaphore("ds%d" % i)) for i in range(len(self.dma_sems))]
        cum = {}
        for e in ENGS:
            c = 0
            arr = []
            for it in self.q[e]:
                if it[2]:
                    c += 1
                arr.append(c)
            cum[e] = arr

        def semval(k, v):
            if isinstance(k, str):
                return esem[k], cum[k][v - 1]
            return dsem[k[1]], v

        block = st.enter_context(nc.Block())

        def run(e, eng):
            for waits, fn, sig, dk in self.q[e]:
                for k, v in waits:
                    s, val = semval(k, v)
                    eng.wait_ge(s, val)
                if fn is None:
                    continue
                ins = fn(eng)
                if dk is not None:
                    ins.then_inc(dsem[dk[1]], 16)
                elif sig:
                    ins.then_inc(esem[e], 1)

        @block.tensor
        def _(eng):
            run("tensor", eng)

        @block.vector
        def _(eng):
            run("vector", eng)

        @block.scalar
        def _(eng):
            run("scalar", eng)

        @block.gpsimd
        def _(eng):
            run("gpsimd", eng)

        @block.sync
        def _(eng):
            run("sync", eng)


def _alibi_slopes(n):
    def pow2(m):
        start = 2.0 ** (-8.0 / m)
        return [start ** (i + 1) for i in range(m)]
    if math.log2(n).is_integer():
        s = pow2(n)
    else:
        p = 2 ** int(math.floor(math.log2(n)))
        s = pow2(p) + pow2(2 * p)[0::2][: n - p]
    return sorted(s, reverse=True)


(PV_SHM, PV_SCM, PV_GTM, PV_SHF, PV_SCF, PV_GTF, PV_GPM, PV_GQM, PV_GPF, PV_GQF,
 PV_MUR, PV_MUK, PV_MUV, PV_MUW, PV_MUA, PV_MUG, PV_W0, PV_A0, PV_KK, PV_KA, PV_RK,
 PV_LNW, PV_LNB, PV_C) = range(24)
NPV = 24

CB_ID = 0
CB_ONESBD = 128
CB_ONES = 256
CB_MT4 = 320
CB_ML4 = 832
CB_E1 = 1344
CB_EA2 = CB_E1 + 8 * 256
CB_EB2 = CB_EA2 + 2 * 8 * 64
CB_EA3 = CB_EB2 + 8 * 64
CB_EB3 = CB_EA3 + 8 * 8 * 16
NCB = CB_EB3 + 8 * 16
CF_MSK = 0
CF_VM = 256
CF_EPS = CF_VM + 128
CF_IDF = CF_EPS + 4
NCF = CF_IDF + 128

CH_RW = 0
CH_AT = 8
CH_PB = 17
CH_WOUT = 25
CH_FF = 27
CH_FO = 38
NCH = 44


def _host_consts():
    sl = np.asarray(_alibi_slopes(24), np.float64).reshape(3, 8)
    cb = np.zeros((128, NCB), np.float32)
    p = np.arange(128)
    cb[:, CB_ID:CB_ID + 128] = np.eye(128)
    cb[:, CB_ONESBD:CB_ONESBD + 128] = (p[:, None] // 64 == p[None, :] // 64)
    cb[:, CB_ONES:CB_ONES + 64] = 1.0
    same = (p[:, None] // 64 == p[None, :] // 64)
    su = same & (p[:, None] < p[None, :])
    iu = same & (p[:, None] <= p[None, :])
    slo = same & (p[:, None] > p[None, :])
    cb[:, CB_MT4:CB_MT4 + 512] = np.concatenate([su, iu, su, iu], 1)
    cb[:, CB_ML4:CB_ML4 + 512] = np.concatenate([slo] * 4, 1)
    k = p[:, None].astype(np.float64)
    q = p[None, :].astype(np.float64)
    for h in range(8):
        dpv = q - k + 128
        e_prev = np.where(dpv <= 128, np.exp(-sl[0, h] * dpv), 0.0)
        dcu = q - k
        e_cur = np.where(dcu >= 0, np.exp(-sl[0, h] * np.maximum(dcu, 0)), 0.0)
        cb[:, CB_E1 + h * 256: CB_E1 + h * 256 + 128] = e_prev
        cb[:, CB_E1 + h * 256 + 128: CB_E1 + h * 256 + 256] = e_cur
    i64 = np.arange(64)[None, :].astype(np.float64)
    for rot in range(2):
        for h in range(8):
            j = p // 64
            pp = (p % 64).astype(np.float64)
            a = ((rot - j - 1) % 2) + 1
            dl = 64.0 * a[:, None] + i64 - pp[:, None]
            e = np.where(dl <= 128, np.exp(-sl[1, h] * 4.0 * dl), 0.0)
            o = CB_EA2 + (rot * 8 + h) * 64
            cb[:, o:o + 64] = e
    for h in range(8):
        kk = np.arange(64)[:, None].astype(np.float64)
        dl = i64 - kk
        e = np.where(dl >= 0, np.exp(-sl[1, h] * 4.0 * np.maximum(dl, 0)), 0.0)
        o = CB_EB2 + h * 64
        cb[0:64, o:o + 64] = e
    i16 = np.arange(16)[None, :].astype(np.float64)
    for rot in range(8):
        for h in range(8):
            j = p // 16
            pp = (p % 16).astype(np.float64)
            a = ((rot - j - 1) % 8) + 1
            dl = 16.0 * a[:, None] + i16 - pp[:, None]
            e = np.where(dl <= 128, np.exp(-sl[2, h] * 16.0 * dl), 0.0)
            o = CB_EA3 + (rot * 8 + h) * 16
            cb[:, o:o + 16] = e
    for h in range(8):
        kk = np.arange(16)[:, None].astype(np.float64)
        dl = i16 - kk
        e = np.where(dl >= 0, np.exp(-sl[2, h] * 16.0 * np.maximum(dl, 0)), 0.0)
        o = CB_EB3 + h * 16
        cb[0:16, o:o + 16] = e
    return cb


def _host_cf(hh):
    cf = np.zeros((128, NCF), np.float32)
    m = np.ones((128, 256), np.float32)
    m[:, 0::64] = 0.0
    cf[:, CF_MSK:CF_MSK + 256] = m
    valid = lambda t: 0.0 if t < 0 else (1.0 if (hh == 1 or t >= PB0) else 0.0)
    p = np.arange(128)
    for t in range(NT):
        cf[:, CF_VM + t] = valid(t)
        cf[:, CF_VM + 32 + t] = valid(t - 1)
        j = p // 64
        a = ((t - j - 1) % 2) + 1
        cf[:, CF_VM + 64 + t] = [valid(t - aa) for aa in a]
        j = p // 16
        a = ((t - j - 1) % 8) + 1
        cf[:, CF_VM + 96 + t] = [valid(t - aa) for aa in a]
    cf[:, CF_EPS] = RMS_EPS
    cf[:, CF_EPS + 1] = GN_EPS
    cf[:, CF_EPS + 3] = 1.0
    cf[:, CF_IDF:CF_IDF + 128] = np.eye(128)
    return cf


def _fm(v):
    return np.ascontiguousarray(v.reshape(8, 128).T)


def _wchunk(w, cols):
    return w[:, cols].reshape(8, 128, -1).transpose(1, 0, 2)


class _Stop(Exception):
    pass


def build(nt=NT, dbg=None, dbg_tile=0, dbg_c=0, stop=None):
    nc = bass.Bass("TRN2", target_bir_lowering=False)
    with ExitStack() as st:
        S = Sched(nc, st)
        finals = []

        def CK(name):
            if stop == name:
                raise _Stop()

        def DBG(name, view, m=None, c=None):
            if not dbg or name not in dbg:
                return
            if m is not None and m != dbg_tile:
                return
            if c is not None and c != dbg_c:
                return
            shp = list(view.ap.shape)
            dd = S.dram("dbg_" + name, shp, view.ap.dtype, "ExternalOutput")
            S.dma("gpsimd", out=dd[:], in_=view)
            finals.append(dd)
        xv = S.dram("xv", [T, D], F32, "ExternalInput")
        pfm_d = S.dram("pfm", [128, NPV * 8], F32, "ExternalInput")
        wmod_d = S.dram("wmod", [24, 128, 2048], F32, "ExternalInput")
        wsrc = S.dram("wsrc", [NCH, 128, 4096], F32, "ExternalInput")
        l1_d = S.dram("l1", [128, 8 * 288], F32, "ExternalInput")
        l2_d = S.dram("l2", [128, 3 * 1024], F32, "ExternalInput")
        cb_d = S.dram("cbt", [128, NCB], F32, "ExternalInput")
        cf_d = S.dram("cft", [128, NCF], F32, "ExternalInput")
        y_d = S.dram("y", [T // 2, D], F32, "ExternalOutput")
        wscr = S.dram("wscr", [NCH, 128, 4096], BF16, "Internal")

        cb = S.sb("cb", [128, NCB], BF16)
        cf = S.sb("cf", [128, NCF], F32)
        pf = S.sb("pf", [128, NPV * 8], F32)
        pd = S.sb("pd", [128, 12 * 8], F32)
        gmb = S.sb("gmb", [128, 1024], BF16)
        gfb = S.sb("gfb", [128, 1024], BF16)
        l1a = S.sb("l1a", [128, 8, 288], BF16)
        l1b = S.sb("l1b", [128, 8, 288], BF16)
        l2 = S.sb("l2", [128, 3, 1024], BF16)
        ring = [S.sb("ring%d" % i, [128, 4096], BF16) for i in range(NSLOT)]
        xt1 = S.sb("xt", [128, NSUB, 1024], F32)
        xt = [xt1, xt1]
        nb = S.sb("nb", [128, NSUB, 1024], BF16)
        junk = nb[:, 0, :]
        st4 = S.sb("st4", [128, 16], F32)
        hT = S.sb("hT", [128, 8, TT + 1], BF16)
        h2T = S.sb("h2T", [128, 8, TT], BF16)
        mixT = h2T
        Zf = S.sb("Zf", [128, 8, 64], F32)
        Zb = S.sb("Zb", [128, 8, 2, 64], BF16)
        hal = S.sb("hal", [128, 8, 3], F32)
        tp = [S.sb("tp%d" % i, [128, TT + 1], F32) for i in range(12)]
        tv = lambda i: tp[i][:, 0:TT]
        pj = [tp[0], tp[0], tp[0]]
        tmpd = tv(1)
        rkv = [tv(2), tv(3), tv(4)]
        sw = tv(5); asig = tv(6); gg = tv(7); cs = tv(0); cm = tv(1)
        Ep = tv(8); En = tv(9); Em = tv(10); rinv = tv(1); kkb = tv(11); ff = tv(0)
        kmod = tv(5); bv = tv(1); bon = tv(6); yln = tv(9); ysq = tv(10)
        sqb = S.sb("sqb", [128, TT], BF16)
        ARs = [S.sb("AR%d" % i, [128, NSUB, 2, 128], BF16) for i in range(2)]
        Bt = S.sb("Bt", [128, TT], BF16)
        Kt = S.sb("Kt", [128, TT], BF16)
        vbf = S.sb("vbf", [128, TT], BF16)
        Bpads = [S.sb("Bpad%d" % i, [128, NSUB, 2, 128], BF16) for i in range(2)]
        Kpads = [S.sb("Kpad%d" % i, [128, NSUB, 2, 128], BF16) for i in range(2)]
        Vtms = [S.sb("Vtm%d" % i, [128, NSUB, 128], BF16) for i in range(2)]
        AMs = [S.sb("AM%d" % i, [128, 4, 512], BF16) for i in range(2)]
        TTfs = [S.sb("TTf%d" % i, [128, 4, 128], BF16) for i in range(2)]
        gbs = [S.sb("gb%d" % i, [128, TT], BF16) for i in range(2)]
        pcss = [S.sb("pcs%d" % i, [128, 4], F32) for i in range(2)]
        ysqB = S.sb("ysqB", [128, TT], F32)
        L0 = S.sb("L0", [128, 4, 128], BF16)
        LP = [S.sb("LP%d" % i, [128, 4, 128], BF16) for i in range(2)]
        LT = [S.sb("LT%d" % i, [128, 4, 128], BF16) for i in range(2)]
        SS = [S.sb("SS%d" % i, [128, 4, 128], BF16) for i in range(2)]
        Xb = S.sb("Xb", [128, 128], BF16)
        Ub = S.sb("Ub", [128, 128], BF16)
        ztmp = S.sb("ztmp", [128, 64], F32)
        Ytm = S.sb("Ytm", [128, NSUB, 128], F32)
        ynb = S.sb("ynb", [128, NSUB, 128], BF16)
        gst = S.sb("gst", [128, 32], F32)
        lw = S.sb("lw", [128, TT], BF16)
        lga = S.sb("lga", [128, TT], BF16)
        lgb = S.sb("lgb", [32, TT], BF16)
        yfin = S.sb("yfin", [128, 8, TT], BF16)
        _nbf = nb[:].re("p s c -> p (s c)")
        Qa = [h2T[:, 0:4, :], h2T[:, 4:8, :], _nbf[:, 0:1024].re("p (c t) -> p c t", t=TT)]
        K1 = S.sb("K1", [128, 4, 128 + TT], BF16)
        V1 = S.sb("V1", [128, 3, 512], BF16)
        K2c = S.sb("K2c", [128, 4, TT], BF16)
        K2r = S.sb("K2r", [128, 4, 4, 128], BF16)
        V2c = S.sb("V2c", [64, 4, 128], BF16)
        V2r = S.sb("V2r", [128, 4, 512], BF16)
        K3c = S.sb("K3c", [128, 4, TT], BF16)
        K3r = S.sb("K3r", [128, 4, 16, 128], BF16)
        V3c = S.sb("V3c", [16, 16, 128], BF16)
        V3r = S.sb("V3r", [128, 16, 512], BF16)
        VF = _nbf[:, 1024:2048].re("p (c t) -> p c t", t=TT)
        pe = S.sb("pe", [128, 512], BF16)
        pp_ = S.sb("pp", [128, 512], BF16)
        peb = S.sb("peb", [64, 256], BF16)
        ppb = S.sb("ppb", [64, 256], BF16)
        accO = S.sb("accO", [64, TT], F32)
        accD = S.sb("accD", [64, TT], F32)
        oT = S.sb("oT", [64, 8, TT], BF16)
        sa = tv(2); sbb = tv(3); sg = tv(4); utmp = tv(10)
        actT = S.sb("actT", [128, 8, TT], BF16)
        VF2 = actT[:, 0:4, :]
        wst = xt1[:].re("p s c -> p (s c)")

        _b0 = S.ps("b0", [128, 512])
        _b4 = S.ps("b4", [128, 512])
        _bS = S.ps("bS", [128, 1024])
        B3 = S.ps("b3", [128, 512])
        B5 = S.ps("b5", [128, 512])
        B6 = S.ps("b6", [128, 512])
        _pT = S.ps("pT", [128, 1024], BF16)
        R0 = SubBuf(_b0, 0); R1 = SubBuf(_b0, 256)
        Q0 = SubBuf(_b4, 0); Q1 = SubBuf(_b4, 256)
        B1 = Buf("B1", _bS.t[:, 0:512]); B2 = Buf("B2", _bS.t[:, 512:1024])
        pTr = SubBuf(_pT, 0); pTa = SubBuf(_pT, 512)
        for b_ in (_b0, _b4, B1, B2, B3, B5, B6, _pT):
            b_.psum = True
        pC = B3
        prot = {"r": [_b0, B1, B2], "a": [_b4, B5, B6], "x": [_b0, _b4, B1, B2, B3, B6], "f": [_b0, _b4, B6]}
        prot_i = {"r": 0, "a": 0, "x": 0, "f": 0}

        def nextp(k="x"):
            prot_i[k] = (prot_i[k] + 1) % len(prot[k])
            return prot[k][prot_i[k]]

        mm = lambda **kw: S.op("tensor", "matmul", **kw)
        tr = lambda **kw: S.op("tensor", "transpose", **kw)
        act = lambda **kw: S.op("scalar", "activation", **kw)
        vec = lambda m, **kw: S.op("vector", m, **kw)
        gps = lambda m, **kw: S.op("gpsimd", m, **kw)

        def sigmoid_to(dst, src, nbias=None, scale=1.0):
            if nbias is None:
                act(out=dst, in_=src, func=AF.Exp, scale=-scale)
            else:
                act(out=dst, in_=src, func=AF.Exp, scale=-scale, bias=nbias)
            act(out=dst, in_=dst, func=AF.Ln, bias=one_c_for(dst))
            act(out=dst, in_=dst, func=AF.Exp, scale=-1.0)

        def one_c_for(v):
            lo = v.ap.base_partition()
            n = v.ap.partition_size()
            return cf[lo:lo + n, CF_EPS + 3:CF_EPS + 4]

        def rsqrt_to(dst, src, bias_ap, scale=1.0):
            act(out=dst, in_=src, func=AF.Ln, bias=bias_ap, scale=scale)
            act(out=dst, in_=dst, func=AF.Exp, scale=-0.5)

        ident = cb[:, CB_ID:CB_ID + 128]
        identf = cf[:, CF_IDF:CF_IDF + 128]
        onesbd = cb[:, CB_ONESBD:CB_ONESBD + 128]
        eps_r = cf[:, CF_EPS:CF_EPS + 1]
        eps_g = cf[:, CF_EPS + 1:CF_EPS + 2]
        zero_c = cf[:, CF_EPS + 2:CF_EPS + 3]
        one_c = cf[:, CF_EPS + 3:CF_EPS + 4]

        def pv(i, kc):
            return pf[:, i * 8 + kc: i * 8 + kc + 1]

        def pdv(i, kc):
            return pd[:, i * 8 + kc: i * 8 + kc + 1]
        PD_A1, PD_A2, PD_GM, PD_GF, PD_OMK, PD_OMR, PD_OMKm, PD_OMV = range(8)

        try:
            S.dma("gpsimd", out=cb[:, :], in_=cb_d[:, :])
            S.dma("sync", out=cf[:, :], in_=cf_d[:, :])
            S.dma("sync", out=pf[:, :], in_=pfm_d[:, :])
            for i in range(NCH):
                S.dma("gpsimd", out=wscr[i], in_=wsrc[i])
            CK('dma0')
            for b_ in (Zf, Zb, hal, Bpads[0], Bpads[1], Kpads[0], Kpads[1], K1, K2r, V2r, K3r, V3r, V1, hT, Xb, Ub, Vtms[0], Vtms[1]):
                gps("memset", ap=b_[:], constant=0.0)
            CK('memset')
            for half in range(2):
                S.dma("sync", out=wst[:, 0:4 * 288], in_=l1_d[:, half * 4 * 288:(half + 1) * 4 * 288])
                w1v = wst[:, 0:4 * 288].re("p (k c) -> p k c", c=288)
                for k4 in range(4):
                    kc = half * 4 + k4
                    for (lo, hi, mui) in ((0, 64, PV_MUW), (64, 128, PV_MUA), (128, 288, PV_MUG)):
                        vec("tensor_scalar", out=l1b[:, kc, lo:hi], in0=w1v[:, k4, lo:hi], scalar1=pv(mui, kc),
                            scalar2=None, op0=ALU.mult)
                        vec("tensor_tensor", out=l1a[:, kc, lo:hi], in0=w1v[:, k4, lo:hi], in1=l1b[:, kc, lo:hi],
                            op=ALU.subtract)
            for half in range(2):
                S.dma("sync", out=wst[:, 0:1536], in_=l2_d[:, half * 1536:(half + 1) * 1536])
                vec("tensor_copy", out=l2[:].re("p a c -> p (a c)")[:, half * 1536:(half + 1) * 1536], in_=wst[:, 0:1536])
            CK('lora0')
            for j in range(24):
                S.dma("sync", out=wst[:, 0:2048], in_=wmod_d[j])
                wv = wst[:, 0:2048].re("p (k c) -> p k c", c=256)
                for cc in range(2):
                    col = j * 2 + cc
                    for kc in range(8):
                        mm(out=B1[:, col:col + 1], lhsT=wv[:, kc, cc * 128:(cc + 1) * 128], rhs=pv(PV_C, kc),
                           start=(kc == 0), stop=(kc == 7))
            modf = S.sb("modf", [128, 48], F32)
            vec("tensor_tensor", out=modf[:, :], in0=B1[:, 0:48], in1=pf[:, 0:48], op=ALU.add)
            for kc in range(8):
                vec("scalar_tensor_tensor", out=pdv(PD_A1, kc), in0=modf[:, 8 + kc:9 + kc], scalar=1.0,
                    in1=pv(PV_GPM, kc), op0=ALU.add, op1=ALU.mult)
                vec("scalar_tensor_tensor", out=pdv(PD_A2, kc), in0=modf[:, 32 + kc:33 + kc], scalar=1.0,
                    in1=pv(PV_GPF, kc), op0=ALU.add, op1=ALU.mult)
                vec("tensor_tensor", out=pdv(PD_GM, kc), in0=modf[:, 16 + kc:17 + kc], in1=pv(PV_GQM, kc), op=ALU.mult)
                vec("tensor_tensor", out=pdv(PD_GF, kc), in0=modf[:, 40 + kc:41 + kc], in1=pv(PV_GQF, kc), op=ALU.mult)
                vec("tensor_scalar", out=pdv(PD_OMK, kc), in0=pv(PV_KA, kc), scalar1=-1.0, scalar2=1.0,
                    op0=ALU.mult, op1=ALU.add)
                vec("tensor_scalar", out=pdv(5, kc), in0=pv(PV_W0, kc), scalar1=-1.0, scalar2=None, op0=ALU.mult)
                vec("tensor_scalar", out=pdv(6, kc), in0=pv(PV_A0, kc), scalar1=-1.0, scalar2=None, op0=ALU.mult)
            dg = S.sb("dg", [128, 128], F32)
            onesf = S.sb("onesf", [128, 128], F32)
            gps("memset", ap=onesf[:, :], constant=1.0)
            for (pdi, dst) in ((PD_GM, gmb), (PD_GF, gfb)):
                for kc in range(8):
                    vec("tensor_scalar", out=dg[:, :], in0=identf, scalar1=pdv(pdi, kc), scalar2=None, op0=ALU.mult)
                    pz = nextp()
                    mm(out=pz[:, 0:128], lhsT=onesf[:, :], rhs=dg[:, :], start=True, stop=True)
                    act(out=dst[:, kc * 128:(kc + 1) * 128], in_=pz[:, 0:128], func=AF.Copy)

            CK('startup')
            ring_i = [0]

            ring_sets = {"r": ring[0:2], "a": ring[2:4], "x": ring}
            ring_k = {"r": 0, "a": 0, "x": 0}

            def wload(ch, k="x"):
                s = ring_sets[k][ring_k[k] % len(ring_sets[k])]
                ring_k[k] += 1
                S.dma("sync", out=s[:, :], in_=wscr[ch])
                return s

            def rms_rstd(src3, dst_cols, nsub=NSUB):
                for sub in range(nsub):
                    act(out=junk[:, :], in_=src3[:, sub, :], func=AF.Square,
                        accum_out=st4[:, 8 + sub:9 + sub])
                rsqrt_to(st4[:, dst_cols:dst_cols + nsub], st4[:, 8:8 + nsub], eps_r, 1.0 / D)

            def norm_transpose(xsrc, rcol, dstT, a_idx, b_view_fn, halo):
                for sub in range(NSUB):
                    vec("tensor_scalar", out=nb[:, sub, :], in0=xsrc[:, sub, :], scalar1=st4[:, rcol + sub:rcol + sub + 1],
                        scalar2=None, op0=ALU.mult)
                for kc in range(8):
                    for sub in range(NSUB):
                        tr(out=pTr[:, (kc % 2) * 256 + sub * 128:(kc % 2) * 256 + (sub + 1) * 128], in_=nb[:, sub, kc * 128:(kc + 1) * 128], identity=ident)
                    act(out=dstT[:, kc, halo:halo + TT], in_=pTr[:, (kc % 2) * 256:(kc % 2) * 256 + TT], func=AF.Identity,
                        scale=pdv(a_idx, kc), bias=b_view_fn(kc))

            for m in range(nt):
                phaseB = m >= PB0
                xm = xt[m % 2]
                vcur = cf[:, CF_VM + m:CF_VM + m + 1]
                vprev = cf[:, CF_VM + 32 + m:CF_VM + 33 + m]
                vr2 = cf[:, CF_VM + 64 + m:CF_VM + 65 + m]
                vr3 = cf[:, CF_VM + 96 + m:CF_VM + 97 + m]
                S.dma("gpsimd", out=xm[:], in_=xv.v(xv.t[m * TT:(m + 1) * TT, :].rearrange("(s p) c -> p s c", p=128)))
                if m > 0:
                    vec("tensor_scalar", out=hT[:, :, 0:1], in0=hT[:, :, TT:TT + 1],
                        scalar1=cf[:, CF_VM + m - 1:CF_VM + m], scalar2=None, op0=ALU.mult)
                rms_rstd(xm, 0)
                norm_transpose(xm, 0, hT, PD_A1, lambda kc: pf[:, PV_SHM * 8 + kc:PV_SHM * 8 + kc + 1]
                               if False else modf[:, kc:kc + 1], 1)

                CK('stage1')
                pz = nextp()
                for kc in range(8):
                    mm(out=pz[:, 0:TT], lhsT=l1a[:, kc, 0:128], rhs=hT[:, kc, 1:TT + 1], start=(kc == 0), stop=False)
                    mm(out=pz[:, 0:TT], lhsT=l1b[:, kc, 0:128], rhs=hT[:, kc, 0:TT], start=False, stop=(kc == 7))
                sigmoid_to(tp[11][0:64, 0:TT], pz[0:64, 0:TT], None, 2.0)
                vec("tensor_scalar", out=lw[0:64, :], in0=tp[11][0:64, 0:TT], scalar1=2.0, scalar2=-1.0, op0=ALU.mult, op1=ALU.add)
                act(out=lw[64:128, :], in_=pz[64:128, 0:TT], func=AF.Copy)
                if phaseB:
                    pz = nextp()
                    for kc in range(8):
                        mm(out=pz[:, 0:TT], lhsT=l1a[:, kc, 128:256], rhs=hT[:, kc, 1:TT + 1], start=(kc == 0), stop=False)
                        mm(out=pz[:, 0:TT], lhsT=l1b[:, kc, 128:256], rhs=hT[:, kc, 0:TT], start=False, stop=(kc == 7))
                    sigmoid_to(tp[11][:, 0:TT], pz[:, 0:TT])
                    act(out=lga[:, :], in_=tp[11][:, 0:TT], func=AF.Copy)
                    pz = nextp()
                    for kc in range(8):
                        mm(out=pz[0:32, 0:TT], lhsT=l1a[:, kc, 256:288], rhs=hT[:, kc, 1:TT + 1], start=(kc == 0), stop=False)
                        mm(out=pz[0:32, 0:TT], lhsT=l1b[:, kc, 256:288], rhs=hT[:, kc, 0:TT], start=False, stop=(kc == 7))
                    sigmoid_to(tp[11][0:32, 0:TT], pz[0:32, 0:TT])
                    act(out=lgb[0:32, :], in_=tp[11][0:32, 0:TT], func=AF.Copy)

                CK('lora1')
                def rw_front(c0):
                    for c in (c0,):
                        AR = ARs[c % 2]; AM = AMs[c % 2]; Vtm = Vtms[c % 2]; Bpad = Bpads[c % 2]; Kpad = Kpads[c % 2]
                        TTf = TTfs[c % 2]; gb = gbs[c % 2]; pcs = pcss[c % 2]
                        csl = slice(c * 128, (c + 1) * 128)
                        wr = wload(CH_RW + c, 'r')
                        wrv = wr[:, :].re("p (k c) -> p k c", c=512)
                        for j in ((0, 1, 2) if m >= PB0 - 1 else (1, 2)):
                            pz = nextp("r")
                            for kc in range(8):
                                mm(out=pz[:, 0:TT], lhsT=wrv[:, kc, j * 128:(j + 1) * 128], rhs=hT[:, kc, 1:TT + 1],
                                   start=(kc == 0), stop=(kc == 7))
                            vec("tensor_copy", out=pj[j][:, 0:1], in_=hal[:, c, j:j + 1])
                            act(out=pj[j][:, 1:TT + 1], in_=pz[:, 0:TT], func=AF.Copy)
                            vec("tensor_scalar", out=hal[:, c, j:j + 1], in0=pj[j][:, TT:TT + 1], scalar1=vcur,
                                scalar2=None, op0=ALU.mult)
                            vec("tensor_tensor", out=tmpd[:, :], in0=pj[j][:, 0:TT], in1=pj[j][:, 1:TT + 1], op=ALU.subtract)
                            vec("scalar_tensor_tensor", out=rkv[j][:, :], in0=tmpd[:, :], scalar=pv(PV_MUR + j, c),
                                in1=pj[j][:, 1:TT + 1], op0=ALU.mult, op1=ALU.add)
                            yield
                        r_, k_, v_ = rkv
                        vec("tensor_scalar", out=v_[:, :], in0=v_[:, :], scalar1=vcur, scalar2=None, op0=ALU.mult)
                        act(out=vbf[:, :], in_=v_[:, :], func=AF.Copy)
                        pz = nextp("r")
                        mm(out=pz[:, 0:TT], lhsT=l2[0:64, 0, csl], rhs=lw[0:64, :], start=True, stop=True)
                        sigmoid_to(sw[:, :], pz[:, 0:TT], pdv(5, c))
                        pz = nextp("r")
                        mm(out=pz[:, 0:TT], lhsT=l2[64:128, 0, csl], rhs=lw[64:128, :], start=True, stop=True)
                        sigmoid_to(asig[:, :], pz[:, 0:TT], pdv(6, c))
                        pz = nextp("r")
                        if phaseB:
                            mm(out=pz[:, 0:TT], lhsT=l2[:, 1, csl], rhs=lga[:, :], start=True, stop=False)
                            mm(out=pz[:, 0:TT], lhsT=l2[0:32, 2, csl], rhs=lgb[0:32, :], start=False, stop=True)
                            act(out=gb[:, :], in_=pz[:, 0:TT], func=AF.Copy)
                        yield
                        vec("tensor_tensor_scan", out=cs[:, :], data0=cf[:, CF_MSK:CF_MSK + TT], data1=sw[:, :],
                            initial=0.0, op0=ALU.mult, op1=ALU.add)
                        vec("tensor_tensor", out=cm[:, :], in0=cs[:, :], in1=sw[:, :], op=ALU.subtract)
                        act(out=Ep[:, :], in_=cs[:, :], func=AF.Exp, scale=-C0)
                        act(out=En[:, :], in_=cs[:, :], func=AF.Exp, scale=C0)
                        act(out=Em[:, :], in_=cm[:, :], func=AF.Exp, scale=-C0)
                        yield
                        act(out=sqb[:, :], in_=k_[:, :], func=AF.Square, scale=pv(PV_KK, c))
                        pz = nextp("r")
                        mm(out=pz[:, 0:TT], lhsT=onesbd, rhs=sqb[:, :], start=True, stop=True)
                        vec("tensor_scalar", out=rinv[:, :], in0=pz[:, 0:TT], scalar1=1e-18, scalar2=None, op0=ALU.max)
                        act(out=rinv[:, :], in_=rinv[:, :], func=AF.Ln)
                        act(out=rinv[:, :], in_=rinv[:, :], func=AF.Exp, scale=-0.5)
                        vec("scalar_tensor_tensor", out=kkb[:, :], in0=k_[:, :], scalar=pv(PV_KK, c), in1=rinv[:, :],
                            op0=ALU.mult, op1=ALU.mult)
                        yield
                        vec("tensor_scalar", out=ff[:, :], in0=asig[:, :], scalar1=pv(PV_KA, c), scalar2=pdv(PD_OMK, c),
                            op0=ALU.mult, op1=ALU.add)
                        vec("tensor_tensor", out=kmod[:, :], in0=k_[:, :], in1=ff[:, :], op=ALU.mult)
                        vec("tensor_tensor", out=bv[:, :], in0=kkb[:, :], in1=asig[:, :], op=ALU.mult)
                        vec("scalar_tensor_tensor", out=AR[:, :, 0, :], in0=kkb[:, :].re("p (s t) -> p s t", t=128), scalar=-1.0,
                            in1=Em[:, :].re("p (s t) -> p s t", t=128), op0=ALU.mult, op1=ALU.mult)
                        if phaseB:
                            vec("tensor_tensor", out=AR[:, :, 1, :], in0=r_[:, :].re("p (s t) -> p s t", t=128),
                                in1=Ep[:, :].re("p (s t) -> p s t", t=128), op=ALU.mult)
                        vec("tensor_tensor", out=Bt[:, :], in0=bv[:, :], in1=En[:, :], op=ALU.mult)
                        vec("tensor_tensor", out=Kt[:, :], in0=kmod[:, :], in1=En[:, :], op=ALU.mult)
                        yield
                        if phaseB:
                            vec("tensor_tensor", out=tmpd[:, :], in0=r_[:, :], in1=kmod[:, :], op=ALU.mult)
                            act(out=sqb[:, :], in_=tmpd[:, :], func=AF.Copy, scale=pv(PV_RK, c))
                            pz = nextp("r")
                            mm(out=pz[:, 0:TT], lhsT=onesbd, rhs=sqb[:, :], start=True, stop=True)
                            vec("tensor_tensor", out=bon[:, :], in0=pz[:, 0:TT], in1=v_[:, :], op=ALU.mult)
                            vec("tensor_tensor", out=yfin[:, c, :], in0=bon[:, :], in1=gb[:, :], op=ALU.mult)
                        yield
                        for qi, (src, dst) in enumerate(((Bt, Bpad), (Kt, Kpad), (vbf, None))):
                            for sub in range(NSUB):
                                tr(out=pTr[:, (qi % 2) * 256 + sub * 128: (qi % 2) * 256 + (sub + 1) * 128],
                                   in_=src[:, sub * 128:(sub + 1) * 128], identity=ident)
                            yield
                            srcv = pTr[:, (qi % 2) * 256:(qi % 2 + 1) * 256]
                            if dst is None:
                                act(out=Vtm[:].re("p s c -> p (s c)"), in_=srcv, func=AF.Copy)
                            else:
                                for h in range(2):
                                    act(out=dst[:, :, h, h * 64:(h + 1) * 64],
                                        in_=srcv.re("p (s c) -> p s c", c=128)[:, :, h * 64:(h + 1) * 64], func=AF.Copy)
                        CK('rwkv_a')
                        for h in range(2):
                            hs = slice(h * 64, (h + 1) * 64)
                            for sub in range(NSUB):
                                u = h * NSUB + sub
                                tsl = slice(sub * 128, (sub + 1) * 128)
                                pz = (B1, B2)[u % 2]
                                if phaseB:
                                    mm(out=pz[:, 0:256], lhsT=Bt[hs, tsl], rhs=AR[hs, sub, :, :].re("p a t -> p (a t)"),
                                       start=True, stop=True)
                                    mm(out=pz[:, 256:512], lhsT=Kt[hs, tsl], rhs=AR[hs, sub, :, :].re("p a t -> p (a t)"),
                                       start=True, stop=True)
                                    vec("tensor_tensor", out=AM[:, u, :], in0=pz[:, :], in1=cb[:, CB_MT4:CB_MT4 + 512], op=ALU.mult)
                                else:
                                    mm(out=pz[:, 0:128], lhsT=Bt[hs, tsl], rhs=AR[hs, sub, 0, :], start=True, stop=True)
                                    mm(out=pz[:, 256:384], lhsT=Kt[hs, tsl], rhs=AR[hs, sub, 0, :], start=True, stop=True)
                                    v4 = lambda ap_: ap_.re("p (a two b) -> p a two b", a=2, two=2)[:, :, 0, :]
                                    vec("tensor_tensor", out=v4(AM[:, u, :]), in0=v4(pz[:, :]),
                                        in1=v4(cb[:, CB_MT4:CB_MT4 + 512]), op=ALU.mult)
                                mm(out=_b0[:, u * 128:(u + 1) * 128], lhsT=AR[hs, sub, 0, :], rhs=Bt[hs, tsl],
                                   start=True, stop=True)
                                yield
                        vec("tensor_tensor", out=L0[:].re("p u t -> p (u t)"), in0=_b0[:, :], in1=cb[:, CB_ML4:CB_ML4 + 512],
                            op=ALU.mult)
                        yield
                        CK('rwkv_b')
                        vec("tensor_tensor", out=SS[0][:], in0=AM[:, :, 0:128], in1=ident.bc(1, [128, 4, 128]), op=ALU.add)
                        lt_prev = lambda u: AM[:, u, 0:128]
                        lp_prev = lambda u: L0[:, u, :]
                        scur = 0
                        for lev in range(1, 6):
                            lpn = LP[lev % 2]
                            ltn = LT[lev % 2]
                            for u in range(4):
                                mm(out=B1[:, u * 128:(u + 1) * 128], lhsT=lt_prev(u), rhs=lp_prev(u), start=True, stop=True)
                            if lev <= 4:
                                for u in range(4):
                                    mm(out=B2[:, u * 128:(u + 1) * 128], lhsT=lp_prev(u), rhs=lt_prev(u),
                                       start=True, stop=True)
                            act(out=lpn[:].re("p u t -> p (u t)"), in_=B1[:, :], func=AF.Copy)
                            if lev <= 4:
                                act(out=ltn[:].re("p u t -> p (u t)"), in_=B2[:, :], func=AF.Copy)
                            yield
                            for u in range(4):
                                mm(out=_b0[:, u * 128:(u + 1) * 128], lhsT=lpn[:, u, :], rhs=SS[scur][:, u, :], start=True, stop=True)
                            sdst = TTf if lev == 5 else SS[1 - scur]
                            vec("tensor_tensor", out=sdst[:].re("p u t -> p (u t)"), in0=_b0[:, :],
                                in1=SS[scur][:].re("p u t -> p (u t)"), op=ALU.add)
                            scur = 1 - scur
                            lt_prev = (lambda b: (lambda u: b[:, u, :]))(ltn)
                            yield
                            lp_prev = (lambda b: (lambda u: b[:, u, :]))(lpn)
                        vec("tensor_copy", out=pcs[:, 0:4], in_=Ep[:, :].re("p (q t) -> p q t", t=64)[:, :, 63])
                        yield

                def rw_back(c0):
                    for c in (c0,):
                        AR = ARs[c % 2]; AM = AMs[c % 2]; Vtm = Vtms[c % 2]; Bpad = Bpads[c % 2]; Kpad = Kpads[c % 2]
                        TTf = TTfs[c % 2]; gb = gbs[c % 2]; pcs = pcss[c % 2]
                        TTm = TTf
                        ysq = ysqB[:, :]
                        yln = ysqB[:, :]
                        CK('rwkv_c')
                        for q in range(2 * NSUB):
                            sub, half = q // 2, q % 2
                            ps_ = slice(half * 64, half * 64 + 64)
                            tsl = slice(sub * 128, (sub + 1) * 128)
                            zi = q % 2
                            for h in range(2):
                                hs = slice(h * 64, (h + 1) * 64)
                                u = h * NSUB + sub
                                mm(out=pC[:, hs], lhsT=AR[hs, sub, 0, :], rhs=Zb[hs, c, zi, :], start=True, stop=False)
                                mm(out=pC[:, hs], lhsT=AM[:, u, 256:384], rhs=Vtm[:, sub, hs], start=False, stop=True)
                            act(out=Xb[ps_, :], in_=pC[ps_, 0:128], func=AF.Copy)
                            yield
                            CK('c1')
                            for h in range(2):
                                hs = slice(h * 64, (h + 1) * 64)
                                u = h * NSUB + sub
                                mm(out=pC[:, 128 + h * 64:128 + (h + 1) * 64], lhsT=TTm[ps_, u, :], rhs=Xb[ps_, hs],
                                   start=True, stop=True)
                            vec("tensor_copy", out=Ub[ps_, :], in_=pC[ps_, 128:256])
                            yield
                            CK('c2')
                            if phaseB:
                                for h in range(2):
                                    hs = slice(h * 64, (h + 1) * 64)
                                    u = h * NSUB + sub
                                    o_ = slice(256 + h * 64, 256 + (h + 1) * 64)
                                    mm(out=pC[:, o_], lhsT=AR[hs, sub, 1, :], rhs=Zb[hs, c, zi, :], start=True, stop=False)
                                    mm(out=pC[:, o_], lhsT=AM[:, u, 128:256], rhs=Ub[:, hs], start=False, stop=False)
                                    mm(out=pC[:, o_], lhsT=AM[:, u, 384:512], rhs=Vtm[:, sub, hs], start=False, stop=True)
                                act(out=Ytm[ps_, sub, :], in_=pC[ps_, 256:384], func=AF.Copy)
                                yield
                            CK('c3')
                            for h in range(2):
                                hs = slice(h * 64, (h + 1) * 64)
                                mm(out=pC[:, 384:448], lhsT=Bpad[ps_, sub, h, :], rhs=Ub[ps_, hs], start=(h == 0), stop=False)
                                mm(out=pC[:, 384:448], lhsT=Kpad[ps_, sub, h, :], rhs=Vtm[ps_, sub, hs], start=False, stop=(h == 1))
                            CK('c4')
                            pcv = pcs[:, q:q + 1]
                            vec("tensor_scalar", out=ztmp[:, :], in0=Zf[:, c, :], scalar1=pcv, scalar2=None, op0=ALU.mult)
                            vec("scalar_tensor_tensor", out=Zf[:, c, :], in0=pC[:, 384:448], scalar=pcv, in1=ztmp[:, :],
                                op0=ALU.mult, op1=ALU.add)
                            act(out=Zb[:, c, 1 - zi, :], in_=Zf[:, c, :], func=AF.Copy)
                            yield
                        CK('rwkv_d')
                        if phaseB:
                            yv = Ytm[:].re("p s (h i) -> p (s h) i", i=64)
                            vec("tensor_reduce", out=gst[:, 0:4], in_=yv, axis=AX.X, op=ALU.add)
                            act(out=ysq[:, :], in_=Ytm[:].re("p s c -> p (s c)"), func=AF.Square)
                            vec("tensor_reduce", out=gst[:, 4:8], in_=ysq[:, :].re("p (g i) -> p g i", i=64), axis=AX.X, op=ALU.add)
                            vec("tensor_scalar", out=gst[:, 8:12], in0=gst[:, 0:4], scalar1=1.0 / 64, scalar2=None, op0=ALU.mult)
                            vec("tensor_tensor", out=gst[:, 12:16], in0=gst[:, 8:12], in1=gst[:, 8:12], op=ALU.mult)
                            vec("scalar_tensor_tensor", out=gst[:, 16:20], in0=gst[:, 4:8], scalar=1.0 / 64, in1=gst[:, 12:16],
                                op0=ALU.mult, op1=ALU.subtract)
                            rsqrt_to(gst[:, 24:28], gst[:, 16:20], eps_g, 1.0)
                            ysv = ysq[:, :].re("p (g i) -> p g i", i=64)
                            vec("tensor_tensor", out=ysv, in0=yv, in1=gst[:, 8:12].bc(2, [128, 4, 64]), op=ALU.subtract)
                            vec("tensor_tensor", out=ynb[:].re("p s (h i) -> p (s h) i", i=64), in0=ysv,
                                in1=gst[:, 24:28].bc(2, [128, 4, 64]), op=ALU.mult)
                            yield
                            for sub in range(NSUB):
                                tr(out=pTr[:, 256 + sub * 128:256 + (sub + 1) * 128], in_=ynb[:, sub, :], identity=ident)
                            act(out=yln[:, :], in_=pTr[:, 256:256 + TT], func=AF.Identity, scale=pv(PV_LNW, c), bias=pv(PV_LNB, c))
                            vec("tensor_tensor", out=yln[:, :], in0=yln[:, :], in1=gb[:, :], op=ALU.mult)
                            vec("tensor_tensor", out=yfin[:, c, :], in0=yln[:, :], in1=yfin[:, c, :], op=ALU.add)
                            yield


                def th_attn():
                    if m < PB0 - 8:
                        return
                    j0_2 = m % 2
                    j0_3 = m % 8
                    for g in range(3):
                        kdst = (K1, K2c, K3c)[g]
                        for j in ((0, 1, 2) if phaseB else (1, 2)):
                            wa = wload(CH_AT + g * 3 + j, 'a')
                            wav = wa[:, :].re("p (k c) -> p k c", c=512)
                            for cc in range(4):
                                pz = nextp("a")
                                for kc in range(8):
                                    mm(out=pz[:, 0:TT], lhsT=wav[:, kc, cc * 128:(cc + 1) * 128], rhs=hT[:, kc, 1:TT + 1],
                                       start=(kc == 0), stop=(kc == 7))
                                if j == 0:
                                    act(out=Qa[g][:, cc, :], in_=pz[:, 0:TT], func=AF.Copy, scale=0.125)
                                elif j == 1:
                                    if g == 0:
                                        act(out=K1[:, cc, 128:128 + TT], in_=pz[:, 0:TT], func=AF.Copy)
                                    else:
                                        act(out=kdst[:, cc, :], in_=pz[:, 0:TT], func=AF.Copy)
                                else:
                                    act(out=VF[:, cc, :], in_=pz[:, 0:TT], func=AF.Copy)
                            yield
                        if g == 0:
                            for blk in range(2):
                                for cc in range(4):
                                    tr(out=pTa[:, cc * 128:(cc + 1) * 128], in_=VF[:, cc, blk * 128:(blk + 1) * 128], identity=ident)
                                vec("tensor_copy", out=V1[:, 1 + blk, :], in_=pTa[:, 0:512])
                            yield
                        elif g == 1:
                            vec("tensor_copy", out=VF2[:], in_=VF[:])
                    CK('attn_proj')
                    for h in range(8):
                        cc, hp = h // 2, (h % 2) * 64
                        hs = slice(hp, hp + 64)
                        vs = slice(h * 64, (h + 1) * 64)
                        vl = slice(hp, hp + 64)
                        if h % 2 == 0:
                            for r in range(4):
                                tr(out=pTa[0:64, r * 128:(r + 1) * 128],
                                   in_=VF2[:, cc, :].re("p (i r) -> p r i", r=4)[:, r, :], identity=ident)
                            vec("tensor_copy", out=V2c[0:64, :, :].re("p r c -> p (r c)"), in_=pTa[0:64, 0:512])
                            for r in range(16):
                                tr(out=pTa[0:16, (r % 4) * 128:(r % 4 + 1) * 128],
                                   in_=VF[:, cc, :].re("p (i r) -> p r i", r=16)[:, r, :], identity=ident)
                                if r % 4 == 3:
                                    vec("tensor_copy", out=V3c[0:16, r - 3:r + 1, :].re("p a c -> p (a c)"), in_=pTa[0:16, 0:512])
                                yield
                        if not phaseB:
                            if h % 2 == 1:
                                S.dma("gpsimd", out=V2r[j0_2 * 64:(j0_2 + 1) * 64, :, cc * 128:(cc + 1) * 128], in_=V2c[0:64, :, :])
                                S.dma("gpsimd", out=V3r[j0_3 * 16:(j0_3 + 1) * 16, :, cc * 128:(cc + 1) * 128], in_=V3c[0:16, :, :])
                            continue
                        for blk in range(2):
                            qv = Qa[0][hs, cc, blk * 128:(blk + 1) * 128]
                            mm(out=B5[:, (blk * 2) * 128:(blk * 2 + 1) * 128], lhsT=K1[hs, cc, blk * 128:(blk + 1) * 128],
                               rhs=qv, start=True, stop=True)
                            mm(out=B5[:, (blk * 2 + 1) * 128:(blk * 2 + 2) * 128],
                               lhsT=K1[hs, cc, 128 + blk * 128:128 + (blk + 1) * 128], rhs=qv, start=True, stop=True)
                        act(out=pe[:, :], in_=B5[:, 0:512], func=AF.Exp)
                        yield
                        vec("tensor_tensor", out=pp_[:, :].re("p (b e) -> p b e", b=2), in0=pe[:, :].re("p (b e) -> p b e", b=2),
                            in1=cb[:, CB_E1 + h * 256:CB_E1 + (h + 1) * 256].bc(1, [128, 2, 256]), op=ALU.mult)
                        vec("tensor_scalar", out=pp_[:, 0:128], in0=pp_[:, 0:128], scalar1=vprev, scalar2=None, op0=ALU.mult)
                        yield
                        for blk in range(2):
                            mm(out=B6[0:64, blk * 128:(blk + 1) * 128], lhsT=V1[:, blk, vs],
                               rhs=pp_[:, (blk * 2) * 128:(blk * 2 + 1) * 128], start=True, stop=False)
                            mm(out=B6[0:64, blk * 128:(blk + 1) * 128], lhsT=V1[:, blk + 1, vs],
                               rhs=pp_[:, (blk * 2 + 1) * 128:(blk * 2 + 2) * 128], start=False, stop=True)
                        ppv = pp_[:, :].re("p (b c q) -> p b c q", b=2, c=2)
                        mm(out=B6[0:64, 256:512], lhsT=cb[:, CB_ONES:CB_ONES + 64], rhs=ppv[:, :, 0, :], start=True, stop=False)
                        mm(out=B6[0:64, 256:512], lhsT=cb[:, CB_ONES:CB_ONES + 64], rhs=ppv[:, :, 1, :], start=False, stop=True)
                        act(out=accO[:, :], in_=B6[0:64, 0:TT], func=AF.Copy)
                        act(out=accD[:, :], in_=B6[0:64, 256:512], func=AF.Copy)
                        yield
                        for r in range(4):
                            qv = Qa[1][hs, cc, :].re("p (i r) -> p r i", r=4)[:, r, :]
                            mm(out=B5[:, r * 64:(r + 1) * 64], lhsT=K2r[hs, cc, r, :], rhs=qv, start=True, stop=True)
                            mm(out=B5[0:64, 256 + r * 64:256 + (r + 1) * 64],
                               lhsT=K2c[hs, cc, :].re("p (i r) -> p r i", r=4)[:, r, :], rhs=qv, start=True, stop=True)
                        act(out=pe[:, 0:256], in_=B5[:, 0:256], func=AF.Exp)
                        act(out=peb[0:64, :], in_=B5[0:64, 256:512], func=AF.Exp)
                        yield
                        ea = cb[:, CB_EA2 + (j0_2 * 8 + h) * 64:CB_EA2 + (j0_2 * 8 + h + 1) * 64]
                        vec("scalar_tensor_tensor", out=pp_[:, 0:256].re("p (r i) -> p r i", r=4),
                            in0=pe[:, 0:256].re("p (r i) -> p r i", r=4), scalar=vr2, in1=ea.bc(1, [128, 4, 64]),
                            op0=ALU.mult, op1=ALU.mult)
                        eb = cb[0:64, CB_EB2 + h * 64:CB_EB2 + (h + 1) * 64]
                        vec("tensor_tensor", out=ppb[0:64, :].re("p (r i) -> p r i", r=4),
                            in0=peb[0:64, :].re("p (r i) -> p r i", r=4), in1=eb.bc(1, [64, 4, 64]), op=ALU.mult)
                        yield
                        for r in range(4):
                            mm(out=B6[0:64, r * 64:(r + 1) * 64], lhsT=V2r[:, r, vs], rhs=pp_[:, r * 64:(r + 1) * 64],
                               start=True, stop=False)
                            mm(out=B6[0:64, r * 64:(r + 1) * 64], lhsT=V2c[0:64, r, vl], rhs=ppb[0:64, r * 64:(r + 1) * 64],
                               start=False, stop=True)
                        mm(out=B6[0:64, 256:512], lhsT=cb[:, CB_ONES:CB_ONES + 64], rhs=pp_[:, 0:256], start=True, stop=False)
                        mm(out=B6[0:64, 256:512], lhsT=cb[0:64, CB_ONES:CB_ONES + 64], rhs=ppb[0:64, :], start=False, stop=True)
                        vec("tensor_tensor", out=accO[:, :].re("p (i r) -> p r i", r=4), in0=accO[:, :].re("p (i r) -> p r i", r=4),
                            in1=B6[0:64, 0:TT].re("p (r i) -> p r i", r=4), op=ALU.add)
                        vec("tensor_tensor", out=accD[:, :].re("p (i r) -> p r i", r=4), in0=accD[:, :].re("p (i r) -> p r i", r=4),
                            in1=B6[0:64, 256:512].re("p (r i) -> p r i", r=4), op=ALU.add)
                        yield
                        for r in range(16):
                            qv = Qa[2][hs, cc, :].re("p (i r) -> p r i", r=16)[:, r, :]
                            mm(out=B5[:, r * 16:(r + 1) * 16], lhsT=K3r[hs, cc, r, :], rhs=qv, start=True, stop=True)
                            mm(out=B5[0:16, 256 + r * 16:256 + (r + 1) * 16],
                               lhsT=K3c[hs, cc, :].re("p (i r) -> p r i", r=16)[:, r, :], rhs=qv, start=True, stop=True)
                        act(out=pe[:, 0:256], in_=B5[:, 0:256], func=AF.Exp)
                        act(out=peb[0:16, :], in_=B5[0:16, 256:512], func=AF.Exp)
                        yield
                        ea = cb[:, CB_EA3 + (j0_3 * 8 + h) * 16:CB_EA3 + (j0_3 * 8 + h + 1) * 16]
                        vec("scalar_tensor_tensor", out=pp_[:, 0:256].re("p (r i) -> p r i", r=16),
                            in0=pe[:, 0:256].re("p (r i) -> p r i", r=16), scalar=vr3, in1=ea.bc(1, [128, 16, 16]),
                            op0=ALU.mult, op1=ALU.mult)
                        eb = cb[0:16, CB_EB3 + h * 16:CB_EB3 + (h + 1) * 16]
                        vec("tensor_tensor", out=ppb[0:16, :].re("p (r i) -> p r i", r=16),
                            in0=peb[0:16, :].re("p (r i) -> p r i", r=16), in1=eb.bc(1, [16, 16, 16]), op=ALU.mult)
                        yield
                        for r in range(16):
                            mm(out=B6[0:64, r * 16:(r + 1) * 16], lhsT=V3r[:, r, vs], rhs=pp_[:, r * 16:(r + 1) * 16],
                               start=True, stop=False)
                            mm(out=B6[0:64, r * 16:(r + 1) * 16], lhsT=V3c[0:16, r, vl], rhs=ppb[0:16, r * 16:(r + 1) * 16],
                               start=False, stop=True)
                        mm(out=B6[0:64, 256:512], lhsT=cb[:, CB_ONES:CB_ONES + 64], rhs=pp_[:, 0:256], start=True, stop=False)
                        mm(out=B6[0:64, 256:512], lhsT=cb[0:16, CB_ONES:CB_ONES + 64], rhs=ppb[0:16, :], start=False, stop=True)
                        yield
                        if phaseB:
                            vec("tensor_tensor", out=accO[:, :].re("p (i r) -> p r i", r=16),
                                in0=accO[:, :].re("p (i r) -> p r i", r=16),
                                in1=B6[0:64, 0:TT].re("p (r i) -> p r i", r=16), op=ALU.add)
                            vec("tensor_tensor", out=accD[:, :].re("p (i r) -> p r i", r=16),
                                in0=accD[:, :].re("p (i r) -> p r i", r=16),
                                in1=B6[0:64, 256:512].re("p (r i) -> p r i", r=16), op=ALU.add)
                            vec("reciprocal", out=accD[:, :], in_=accD[:, :])
                            vec("tensor_tensor", out=oT[:, h, :], in0=accO[:, :], in1=accD[:, :], op=ALU.mult)
                        if h % 2 == 1:
                            S.dma("gpsimd", out=V2r[j0_2 * 64:(j0_2 + 1) * 64, :, cc * 128:(cc + 1) * 128], in_=V2c[0:64, :, :])
                            S.dma("gpsimd", out=V3r[j0_3 * 16:(j0_3 + 1) * 16, :, cc * 128:(cc + 1) * 128], in_=V3c[0:16, :, :])
                    CK('attn')
                    vec("tensor_copy", out=K1[:, :, 0:128], in_=K1[:, :, TT:TT + 128])
                    vec("tensor_copy", out=V1[:, 0, :], in_=V1[:, 2, :])
                    vec("tensor_copy", out=K2r[:, :, :, j0_2 * 64:(j0_2 + 1) * 64],
                        in_=K2c[:].re("p c (i r) -> p c r i", r=4))
                    vec("tensor_copy", out=K3r[:, :, :, j0_3 * 16:(j0_3 + 1) * 16],
                        in_=K3c[:].re("p c (i r) -> p c r i", r=16))

                ag = th_attn()
                ag_done = [False]

                def step_attn():
                    if ag_done[0]:
                        return
                    try:
                        next(ag)
                    except StopIteration:
                        ag_done[0] = True

                def run_rr(gens):
                    gens = list(gens)
                    while gens:
                        for g_ in list(gens):
                            try:
                                next(g_)
                            except StopIteration:
                                gens.remove(g_)
                        if INTERLEAVE:
                            step_attn()

                for k_ in range(9):
                    gens = []
                    if k_ < 8:
                        gens.append(rw_front(k_))
                    if k_ >= 1:
                        gens.append(rw_back(k_ - 1))
                    if INTERLEAVE:
                        run_rr(gens)
                    else:
                        for g_ in reversed(gens):
                            for _ in g_:
                                pass
                while not ag_done[0]:
                    step_attn()

                if not phaseB:
                    continue
                for cc in range(8):
                    sa, sbb = ((tv(2), tv(3)), (tv(5), tv(6)))[cc % 2]
                    wpb = wload(CH_PB + cc)
                    wv = wpb[:, :].re("p (a k c) -> p a k c", a=4, c=128)
                    pz = nextp()
                    for kc in range(8):
                        mm(out=pz[:, 0:TT], lhsT=wv[:, 0, kc, :], rhs=hT[:, kc, 1:TT + 1], start=(kc == 0), stop=(kc == 7))
                    sigmoid_to(sa[:, :], pz[:, 0:TT])
                    pz = nextp()
                    for kc in range(8):
                        mm(out=pz[:, 0:TT], lhsT=wv[:, 1, kc, :], rhs=hT[:, kc, 1:TT + 1], start=(kc == 0), stop=(kc == 7))
                    sigmoid_to(sbb[:, :], pz[:, 0:TT])
                    pz = nextp()
                    for kc in range(8):
                        mm(out=pz[:, 0:TT], lhsT=wv[:, 2, kc, :], rhs=yfin[:, kc, :], start=(kc == 0), stop=(kc == 7))
                    vec("tensor_tensor", out=sa[:, :], in0=sa[:, :], in1=pz[:, 0:TT], op=ALU.mult)
                    pz = nextp()
                    for hh_ in range(8):
                        mm(out=pz[:, 0:TT], lhsT=wv[0:64, 3, hh_, :], rhs=oT[:, hh_, :], start=(hh_ == 0), stop=(hh_ == 7))
                    vec("tensor_tensor", out=sbb[:, :], in0=sbb[:, :], in1=pz[:, 0:TT], op=ALU.mult)
                    vec("tensor_tensor", out=mixT[:, cc, :], in0=sa[:, :], in1=sbb[:, :], op=ALU.add)

                def norm_residual(ps_views, gb):
                    for hf in range(2):
                        act(out=junk[:, 0:512], in_=ps_views[hf], func=AF.Square, accum_out=st4[:, 8 + hf:9 + hf])
                    vec("tensor_tensor", out=st4[:, 10:11], in0=st4[:, 8:9], in1=st4[:, 9:10], op=ALU.add)
                    rsqrt_to(st4[:, 4:5], st4[:, 10:11], eps_r, 1.0 / D)
                    for hf in range(2):
                        for qq in range(2):
                            cs_ = slice(hf * 512 + qq * 256, hf * 512 + (qq + 1) * 256)
                            vec("scalar_tensor_tensor", out=utmp[:, :], in0=ps_views[hf][:, qq * 256:(qq + 1) * 256],
                                scalar=st4[:, 4:5], in1=gb[:, cs_], op0=ALU.mult, op1=ALU.mult)
                            vec("tensor_tensor", out=xm[:, sub, cs_], in0=xm[:, sub, cs_], in1=utmp[:, :], op=ALU.add)

                wo = [wload(CH_WOUT + 0), wload(CH_WOUT + 1)]
                for sub in range(NSUB):
                    for hf in range(2):
                        wv = wo[hf][:, :].re("p (k c) -> p k c", c=512)
                        for kc in range(8):
                            mm(out=(B1, B2)[hf][:, :], lhsT=mixT[:, kc, sub * 128:(sub + 1) * 128], rhs=wv[:, kc, :],
                               start=(kc == 0), stop=(kc == 7))
                    norm_residual([B1[:, :], B2[:, :]], gmb)
                rms_rstd(xm, 2)
                norm_transpose(xm, 2, h2T, PD_A2, lambda kc: modf[:, 24 + kc:25 + kc], 0)
                accs = [[B1[:, :], B2[:, :]], [B3[:, :], B5[:, :]]]
                for pg in range(3):
                    nk = 8 if pg < 2 else 6
                    for i4 in range(nk // 2):
                        i = pg * 4 + i4
                        wf_ = wload(CH_FF + i)
                        wv = wf_[:, :].re("p (k c) -> p k c", c=512)
                        for jj in range(2):
                            jl = i4 * 2 + jj
                            pg_ = nextp("f")
                            for kc in range(8):
                                mm(out=pg_[:, 0:TT], lhsT=wv[:, kc, jj * 128:(jj + 1) * 128], rhs=h2T[:, kc, :],
                                   start=(kc == 0), stop=(kc == 7))
                            act(out=sg[:, :], in_=pg_[:, 0:TT], func=AF.Silu)
                            pu = nextp("f")
                            for kc in range(8):
                                mm(out=pu[:, 0:TT], lhsT=wv[:, kc, 256 + jj * 128:256 + (jj + 1) * 128], rhs=h2T[:, kc, :],
                                   start=(kc == 0), stop=(kc == 7))
                            vec("tensor_tensor", out=actT[:, jl, :], in0=sg[:, :], in1=pu[:, 0:TT], op=ALU.mult)
                    for hf in range(2):
                        wf_ = wload(CH_FO + pg * 2 + hf)
                        wv = wf_[:, :].re("p (k c) -> p k c", c=512)
                        for sub in range(NSUB):
                            for kc in range(nk):
                                mm(out=accs[sub][hf], lhsT=actT[:, kc, sub * 128:(sub + 1) * 128], rhs=wv[:, kc, :],
                                   start=(pg == 0 and kc == 0), stop=(pg == 2 and kc == nk - 1))
                for sub in range(NSUB):
                    norm_residual(accs[sub], gfb)
                r0 = (m - PB0) * TT
                S.dma("gpsimd", out=y_d.v(y_d.t[r0:r0 + TT, :].rearrange("(s p) c -> p s c", p=128)), in_=xm[:])


        except _Stop:
            pass
        S.finish([y_d] + finals)
        S.emit()
    return nc


_CACHE = {}


def prep_inputs(x, c, w_mod, b_mod, g_pre_mix, g_post_mix, g_pre_ffn, g_post_ffn, w_in, mu_rkv, mu_lora,
           w0, w1, w2, a0, a1, a2, g1, g2, k_k, k_a, r_k, ln_x_w, ln_x_b, w_o_rwkv, w_o_attn, w_out,
           w_ffn_in, w_ffn_out):
    f = lambda a: np.asarray(a, np.float32)
    x = f(x); c = f(c)
    w_in = f(w_in)[0]; w_modm = f(w_mod)[0]
    bm = f(b_mod)[0].reshape(6, 1024)
    vecs = [bm[0], bm[1], bm[2], bm[3], bm[4], bm[5], f(g_pre_mix)[0], f(g_post_mix)[0], f(g_pre_ffn)[0],
            f(g_post_ffn)[0], f(mu_rkv)[0, 0], f(mu_rkv)[0, 1], f(mu_rkv)[0, 2], f(mu_lora)[0, 0], f(mu_lora)[0, 1],
            f(mu_lora)[0, 2], f(w0)[0], f(a0)[0], f(k_k)[0], f(k_a)[0], f(r_k)[0].reshape(-1), f(ln_x_w)[0],
            f(ln_x_b)[0]]
    wsrc = np.zeros((NCH, 128, 4096), np.float32)
    def put(i, arr3):
        P, K, C = arr3.shape
        v = wsrc[i].reshape(128, -1)
        tmp = np.zeros((128, K, 4096 // K if K in (8,) else C), np.float32) if False else None
        blk = np.zeros((128, K * C), np.float32)
        blk[:P] = arr3.reshape(P, K * C)
        v[:, :K * C] = blk
    for cch in range(8):
        a = np.zeros((128, 8, 512), np.float32)
        for j in range(3):
            a[:, :, j * 128:(j + 1) * 128] = _wchunk(w_in, slice(j * 1024 + cch * 128, j * 1024 + (cch + 1) * 128))
        put(CH_RW + cch, a)
    for g in range(3):
        for j in range(3):
            o = 3072 + j * 1536 + g * 512
            put(CH_AT + g * 3 + j, _wchunk(w_in, slice(o, o + 512)))
    wor = f(w_o_rwkv)[0]; woa = f(w_o_attn)[0]; wout = f(w_out)[0]
    for cc in range(8):
        cs_ = slice(cc * 128, (cc + 1) * 128)
        a = np.zeros((128, 4, 8, 128), np.float32)
        a[:, 0] = _wchunk(w_in, slice(7680 + cc * 128, 7680 + (cc + 1) * 128))
        a[:, 1] = _wchunk(w_in, slice(8704 + cc * 128, 8704 + (cc + 1) * 128))
        a[:, 2] = _wchunk(wor, cs_)
        a[0:64, 3] = woa[:, cs_].reshape(8, 64, 128).transpose(1, 0, 2)
        put(CH_PB + cc, a.reshape(128, 32, 128))
    for hf in range(2):
        put(CH_WOUT + hf, _wchunk(wout, slice(hf * 512, (hf + 1) * 512)))
    wfi = f(w_ffn_in)[0]; wfo = f(w_ffn_out)[0]
    for i in range(11):
        a = np.zeros((128, 8, 512), np.float32)
        a[:, :, 0:256] = _wchunk(wfi, slice(i * 256, (i + 1) * 256))
        a[:, :, 256:512] = _wchunk(wfi, slice(FH + i * 256, FH + (i + 1) * 256))
        put(CH_FF + i, a)
    for pg in range(3):
        nk = 8 if pg < 2 else 6
        for hf in range(2):
            blk = wfo[pg * 1024:pg * 1024 + nk * 128, hf * 512:(hf + 1) * 512]
            put(CH_FO + pg * 2 + hf, blk.reshape(nk, 128, 512).transpose(1, 0, 2))
    wmod = np.ascontiguousarray(
        w_modm.reshape(8, 128, 24, 256).transpose(2, 1, 0, 3).reshape(24, 128, 2048))
    l1 = np.concatenate([f(w1)[0], f(a1)[0], f(g1)[0]], 1)
    l1 = np.ascontiguousarray(l1.reshape(8, 128, 288).transpose(1, 0, 2).reshape(128, 8 * 288))
    l2 = np.zeros((128, 3, 1024), np.float32)
    l2[0:64, 0] = f(w2)[0]; l2[64:128, 0] = f(a2)[0]
    l2[:, 1] = f(g2)[0][0:128]; l2[0:32, 2] = f(g2)[0][128:160]
    l2 = l2.reshape(128, 3072)
    cbt = _host_consts()
    in_maps = []
    for core in range(8):
        b, hh = core // 2, core % 2
        pfm = np.concatenate([_fm(v) for v in vecs] + [_fm(c[b])], 1)
        if hh == 1:
            xvv = x[b]
        else:
            xvv = np.concatenate([np.zeros((T // 2, D), np.float32), x[b, :T // 2]], 0)
        in_maps.append({"xv": np.ascontiguousarray(xvv), "pfm": np.ascontiguousarray(pfm), "wmod": wmod,
                        "wsrc": wsrc, "l1": l1, "l2": l2, "cbt": cbt, "cft": _host_cf(hh)})
    return in_maps


def kernel(**inputs):
    in_maps = prep_inputs(**inputs)
    if "nc" not in _CACHE:
        _CACHE["nc"] = build()
    nc = _CACHE["nc"]
    res = run_bass_kernel_spmd(nc, in_maps, core_ids=list(range(8)))
    out = np.zeros((4, T, D), np.float32)
    for core in range(8):
        b, hh = core // 2, core % 2
        out[b, hh * (T // 2):(hh + 1) * (T // 2)] = res.results[core]["y"]
    return out
```

```python
import math
from contextlib import ExitStack

import numpy as np
import concourse.bass as bass
import concourse.mybir as mybir
from concourse.bass_utils import run_bass_kernel_spmd

F32 = mybir.dt.float32
BF16 = mybir.dt.bfloat16
AF = mybir.ActivationFunctionType
ALU = mybir.AluOpType
AX = mybir.AxisListType

ENGS = ("tensor", "vector", "scalar", "gpsimd", "sync")

T = 8192
D = 1024
TT = 256
NT = T // TT
PB0 = NT // 2
NSUB = TT // 128
FH = 2816
C0 = math.exp(-0.5)
GN_EPS = 64e-5
RMS_EPS = 1e-6
NSLOT = 4
import os
INTERLEAVE = os.environ.get('NOIL') is None


class Buf:
    def __init__(self, name, t):
        self.name = name
        self.t = t
        self.writer = None
        self.readers = []
        self.dsem = None
        self.dcnt = 0
        self.psum = False

    def __getitem__(self, idx):
        return View(self, self.t[idx])

    def v(self, ap):
        return View(self, ap)


class SubBuf:
    def __init__(self, buf, col0):
        self.buf = buf
        self.col0 = col0

    def __getitem__(self, idx):
        ps, cs = idx
        a = 0 if cs.start is None else cs.start
        assert cs.stop is not None
        return View(self.buf, self.buf.t[ps, self.col0 + a:self.col0 + cs.stop])


class View:
    def __init__(self, buf, ap):
        self.buf = buf
        self.ap = ap

    def __getitem__(self, idx):
        return View(self.buf, self.ap[idx])

    def re(self, pat, **kw):
        return View(self.buf, self.ap.rearrange(pat, **kw))

    def bc(self, axis, shape):
        return View(self.buf, self.ap.unsqueeze(axis).to_broadcast(list(shape)))


def _unw(x):
    return x.ap if isinstance(x, View) else x


class Sched:
    def __init__(self, nc, stack):
        self.nc = nc
        self.stack = stack
        self.q = {e: [] for e in ENGS}
        self.waited = {e: {} for e in ENGS}
        self.dma_sems = []

    def sb(self, name, shape, dt):
        t = self.stack.enter_context(self.nc.sbuf_tensor("s_" + name, list(shape), dt))
        return Buf(name, t)

    def ps(self, name, shape, dt=F32):
        t = self.stack.enter_context(self.nc.psum_tensor("p_" + name, list(shape), dt))
        return Buf(name, t)

    def dram(self, name, shape, dt, kind):
        t = self.nc.dram_tensor(name, list(shape), dt, kind=kind).ap()
        return Buf(name, t)

    def _deps(self, eng, reads, writes):
        deps = {}

        def add(tok):
            if tok is None:
                return
            k, v = tok
            if deps.get(k, 0) < v:
                deps[k] = v

        for b in reads:
            add(b.writer)
            if b.psum:
                for r in b.readers:
                    if r[0] != eng:
                        add(r)
        for b in writes:
            add(b.writer)
            for r in b.readers:
                add(r)
        waits = []
        for k, v in deps.items():
            if k == "tensor" and eng == "tensor":
                continue
            if self.waited[eng].get(k, 0) >= v:
                continue
            self.waited[eng][k] = v
            waits.append((k, v))
            if isinstance(k, str):
                self.q[k][v - 1][2] = True
        return waits

    def _commit(self, tok, reads, writes):
        for b in writes:
            b.writer = tok
            b.readers = []
        for b in reads:
            if b in writes:
                continue
            b.readers.append(tok)
            if len(b.readers) > 48:
                d = {}
                for k, v in b.readers:
                    if d.get(k, 0) < v:
                        d[k] = v
                b.readers = list(d.items())

    def op(self, eng, meth, **kw):
        writes, reads = [], []
        for k, v in kw.items():
            if isinstance(v, View):
                if k in ("out", "accum_out", "ap"):
                    if v.buf not in writes:
                        writes.append(v.buf)
                else:
                    if v.buf not in reads:
                        reads.append(v.buf)
        waits = self._deps(eng, reads, writes)
        if eng == "tensor":
            src = kw.get("lhsT", kw.get("in_"))
            lo = src.ap.base_partition()
            rows = (lo, lo + src.ap.partition_size())
            ob = kw["out"].buf
            prev = getattr(ob, "pe_rows", None)
            if prev is not None and ob.writer is not None and ob.writer[0] == "tensor" and \
                    (rows[1] <= prev[0] or prev[1] <= rows[0]):
                k, v = ob.writer
                if self.waited[eng].get(k, 0) < v:
                    self.waited[eng][k] = v
                    waits.append((k, v))
                    self.q[k][v - 1][2] = True
            ob.pe_rows = rows
        args = {k: _unw(v) for k, v in kw.items()}
        fn = lambda e, m=meth, a=args: getattr(e, m)(**a)
        self.q[eng].append([waits, fn, False, None])
        tok = (eng, len(self.q[eng]))
        self._commit(tok, reads, writes)
        return tok

    def dma(self, eng, out, in_, **kw):
        sb = out.buf
        if sb.dsem is None:
            sb.dsem = ("dma", len(self.dma_sems))
            self.dma_sems.append(sb.name)
        waits = self._deps(eng, [in_.buf], [out.buf])
        sb.dcnt += 16
        tok = (sb.dsem, sb.dcnt)
        a = dict(out=out.ap, in_=in_.ap, **kw)
        fn = lambda e, a=a: e.dma_start(**a)
        self.q[eng].append([waits, fn, False, sb.dsem])
        self._commit(tok, [in_.buf], [out.buf])
        return tok

    def finish(self, final_bufs):
        waits = self._deps("sync", final_bufs, [])
        self.q["sync"].append([waits, None, False, None])

    def emit(self):
        nc = self.nc
        st = self.stack
        esem = {e: st.enter_context(nc.semaphore("es_" + e)) for e in ENGS}
        dsem = [st.enter_context(nc.semaphore("ds%d" % i)) for i in range(len(self.dma_sems))]
        cum = {}
        for e in ENGS:
            c = 0
            arr = []
            for it in self.q[e]:
                if it[2]:
                    c += 1
                arr.append(c)
            cum[e] = arr

        def semval(k, v):
            if isinstance(k, str):
                return esem[k], cum[k][v - 1]
            return dsem[k[1]], v

        block = st.enter_context(nc.Block())

        def run(e, eng):
            for waits, fn, sig, dk in self.q[e]:
                for k, v in waits:
                    s, val = semval(k, v)
                    eng.wait_ge(s, val)
                if fn is None:
                    continue
                ins = fn(eng)
                if dk is not None:
                    ins.then_inc(dsem[dk[1]], 16)
                elif sig:
                    ins.then_inc(esem[e], 1)

        @block.tensor
        def _(eng):
            run("tensor", eng)

        @block.vector
        def _(eng):
            run("vector", eng)

        @block.scalar
        def _(eng):
            run("scalar", eng)

        @block.gpsimd
        def _(eng):
            run("gpsimd", eng)

        @block.sync
        def _(eng):
            run("sync", eng)


def _alibi_slopes(n):
    def pow2(m):
        start = 2.0 ** (-8.0 / m)
        return [start ** (i + 1) for i in range(m)]
    if math.log2(n).is_integer():
        s = pow2(n)
    else:
        p = 2 ** int(math.floor(math.log2(n)))
        s = pow2(p) + pow2(2 * p)[0::2][: n - p]
    return sorted(s, reverse=True)


(PV_SHM, PV_SCM, PV_GTM, PV_SHF, PV_SCF, PV_GTF, PV_GPM, PV_GQM, PV_GPF, PV_GQF,
 PV_MUR, PV_MUK, PV_MUV, PV_MUW, PV_MUA, PV_MUG, PV_W0, PV_A0, PV_KK, PV_KA, PV_RK,
 PV_LNW, PV_LNB, PV_C) = range(24)
NPV = 24

CB_ID = 0
CB_ONESBD = 128
CB_ONES = 256
CB_MT4 = 320
CB_ML4 = 832
CB_E1 = 1344
CB_EA2 = CB_E1 + 8 * 256
CB_EB2 = CB_EA2 + 2 * 8 * 64
CB_EA3 = CB_EB2 + 8 * 64
CB_EB3 = CB_EA3 + 8 * 8 * 16
NCB = CB_EB3 + 8 * 16
CF_MSK = 0
CF_VM = 256
CF_EPS = CF_VM + 128
CF_IDF = CF_EPS + 4
NCF = CF_IDF + 128

CH_RW = 0
CH_AT = 8
CH_PB = 17
CH_WOUT = 25
CH_FF = 27
CH_FO = 38
NCH = 44


def _host_consts():
    sl = np.asarray(_alibi_slopes(24), np.float64).reshape(3, 8)
    cb = np.zeros((128, NCB), np.float32)
    p = np.arange(128)
    cb[:, CB_ID:CB_ID + 128] = np.eye(128)
    cb[:, CB_ONESBD:CB_ONESBD + 128] = (p[:, None] // 64 == p[None, :] // 64)
    cb[:, CB_ONES:CB_ONES + 64] = 1.0
    same = (p[:, None] // 64 == p[None, :] // 64)
    su = same & (p[:, None] < p[None, :])
    iu = same & (p[:, None] <= p[None, :])
    slo = same & (p[:, None] > p[None, :])
    cb[:, CB_MT4:CB_MT4 + 512] = np.concatenate([su, iu, su, iu], 1)
    cb[:, CB_ML4:CB_ML4 + 512] = np.concatenate([slo] * 4, 1)
    k = p[:, None].astype(np.float64)
    q = p[None, :].astype(np.float64)
    for h in range(8):
        dpv = q - k + 128
        e_prev = np.where(dpv <= 128, np.exp(-sl[0, h] * dpv), 0.0)
        dcu = q - k
        e_cur = np.where(dcu >= 0, np.exp(-sl[0, h] * np.maximum(dcu, 0)), 0.0)
        cb[:, CB_E1 + h * 256: CB_E1 + h * 256 + 128] = e_prev
        cb[:, CB_E1 + h * 256 + 128: CB_E1 + h * 256 + 256] = e_cur
    i64 = np.arange(64)[None, :].astype(np.float64)
    for rot in range(2):
        for h in range(8):
            j = p // 64
            pp = (p % 64).astype(np.float64)
            a = ((rot - j - 1) % 2) + 1
            dl = 64.0 * a[:, None] + i64 - pp[:, None]
            e = np.where(dl <= 128, np.exp(-sl[1, h] * 4.0 * dl), 0.0)
            o = CB_EA2 + (rot * 8 + h) * 64
            cb[:, o:o + 64] = e
    for h in range(8):
        kk = np.arange(64)[:, None].astype(np.float64)
        dl = i64 - kk
        e = np.where(dl >= 0, np.exp(-sl[1, h] * 4.0 * np.maximum(dl, 0)), 0.0)
        o = CB_EB2 + h * 64
        cb[0:64, o:o + 64] = e
    i16 = np.arange(16)[None, :].astype(np.float64)
    for rot in range(8):
        for h in range(8):
            j = p // 16
            pp = (p % 16).astype(np.float64)
            a = ((rot - j - 1) % 8) + 1
            dl = 16.0 * a[:, None] + i16 - pp[:, None]
            e = np.where(dl <= 128, np.exp(-sl[2, h] * 16.0 * dl), 0.0)
            o = CB_EA3 + (rot * 8 + h) * 16
            cb[:, o:o + 16] = e
    for h in range(8):
        kk = np.arange(16)[:, None].astype(np.float64)
        dl = i16 - kk
        e = np.where(dl >= 0, np.exp(-sl[2, h] * 16.0 * np.maximum(dl, 0)), 0.0)
        o = CB_EB3 + h * 16
        cb[0:16, o:o + 16] = e
    return cb


def _host_cf(hh):
    cf = np.zeros((128, NCF), np.float32)
    m = np.ones((128, 256), np.float32)
    m[:, 0::64] = 0.0
    cf[:, CF_MSK:CF_MSK + 256] = m
    valid = lambda t: 0.0 if t < 0 else (1.0 if (hh == 1 or t >= PB0) else 0.0)
    p = np.arange(128)
    for t in range(NT):
        cf[:, CF_VM + t] = valid(t)
        cf[:, CF_VM + 32 + t] = valid(t - 1)
        j = p // 64
        a = ((t - j - 1) % 2) + 1
        cf[:, CF_VM + 64 + t] = [valid(t - aa) for aa in a]
        j = p // 16
        a = ((t - j - 1) % 8) + 1
        cf[:, CF_VM + 96 + t] = [valid(t - aa) for aa in a]
    cf[:, CF_EPS] = RMS_EPS
    cf[:, CF_EPS + 1] = GN_EPS
    cf[:, CF_EPS + 3] = 1.0
    cf[:, CF_IDF:CF_IDF + 128] = np.eye(128)
    return cf


def _fm(v):
    return np.ascontiguousarray(v.reshape(8, 128).T)


def _wchunk(w, cols):
    return w[:, cols].reshape(8, 128, -1).transpose(1, 0, 2)


class _Stop(Exception):
    pass


def build(nt=NT, dbg=None, dbg_tile=0, dbg_c=0, stop=None):
    nc = bass.Bass("TRN2", target_bir_lowering=False)
    with ExitStack() as st:
        S = Sched(nc, st)
        finals = []

        def CK(name):
            if stop == name:
                raise _Stop()

        def DBG(name, view, m=None, c=None):
            if not dbg or name not in dbg:
                return
            if m is not None and m != dbg_tile:
                return
            if c is not None and c != dbg_c:
                return
            shp = list(view.ap.shape)
            dd = S.dram("dbg_" + name, shp, view.ap.dtype, "ExternalOutput")
            S.dma("gpsimd", out=dd[:], in_=view)
            finals.append(dd)
        xv = S.dram("xv", [T, D], F32, "ExternalInput")
        pfm_d = S.dram("pfm", [128, NPV * 8], F32, "ExternalInput")
        wmod_d = S.dram("wmod", [24, 128, 2048], F32, "ExternalInput")
        wsrc = S.dram("wsrc", [NCH, 128, 4096], F32, "ExternalInput")
        l1_d = S.dram("l1", [128, 8 * 288], F32, "ExternalInput")
        l2_d = S.dram("l2", [128, 3 * 1024], F32, "ExternalInput")
        cb_d = S.dram("cbt", [128, NCB], F32, "ExternalInput")
        cf_d = S.dram("cft", [128, NCF], F32, "ExternalInput")
        y_d = S.dram("y", [T // 2, D], F32, "ExternalOutput")
        wscr_all = S.dram("wscr", [NCH, 128, 4096], BF16, "Internal")
        wscr = [Buf("wscr%d" % i, wscr_all.t[i]) for i in range(NCH)]

        cb = S.sb("cb", [128, NCB], BF16)
        cf = S.sb("cf", [128, NCF], F32)
        pf = S.sb("pf", [128, NPV * 8], F32)
        pd = S.sb("pd", [128, 12 * 8], F32)
        gmb = S.sb("gmb", [128, 1024], BF16)
        gfb = S.sb("gfb", [128, 1024], BF16)
        l1a = S.sb("l1a", [128, 8, 288], BF16)
        l1b = S.sb("l1b", [128, 8, 288], BF16)
        l2 = S.sb("l2", [128, 3, 1024], BF16)
        ring = [S.sb("ring%d" % i, [128, 4096], BF16) for i in range(NSLOT)]
        xt1 = S.sb("xt", [128, NSUB, 1024], F32)
        xt = [xt1, xt1]
        nb = S.sb("nb", [128, NSUB, 1024], BF16)
        junk = nb[:, 0, :]
        st4 = S.sb("st4", [128, 16], F32)
        hT = S.sb("hT", [128, 8, TT + 1], BF16)
        h2T = S.sb("h2T", [128, 8, TT], BF16)
        mixT = h2T
        Zf = S.sb("Zf", [128, 8, 64], F32)
        Zb = S.sb("Zb", [128, 8, 2, 64], BF16)
        hal = S.sb("hal", [128, 8, 3], F32)
        tp = [S.sb("tp%d" % i, [128, TT + 1], F32) for i in range(12)]
        tv = lambda i: tp[i][:, 0:TT]
        pj = [tp[0], tp[0], tp[0]]
        tmpd = tv(1)
        rkv = [tv(2), tv(3), tv(4)]
        sw = tv(5); asig = tv(6); gg = tv(7); cs = tv(0); cm = tv(1)
        Ep = tv(8); En = tv(9); Em = tv(10); rinv = tv(1); kkb = tv(11); ff = tv(0)
        kmod = tv(5); bv = tv(1); bon = tv(6); yln = tv(9); ysq = tv(10)
        sqb = S.sb("sqb", [128, TT], BF16)
        ARs = [S.sb("AR%d" % i, [128, NSUB, 2, 128], BF16) for i in range(2)]
        Bt = S.sb("Bt", [128, TT], BF16)
        Kt = S.sb("Kt", [128, TT], BF16)
        vbf = S.sb("vbf", [128, TT], BF16)
        Bpads = [S.sb("Bpad%d" % i, [128, NSUB, 2, 128], BF16) for i in range(2)]
        Kpads = [S.sb("Kpad%d" % i, [128, NSUB, 2, 128], BF16) for i in range(2)]
        Vtms = [S.sb("Vtm%d" % i, [128, NSUB, 128], BF16) for i in range(2)]
        AMs = [S.sb("AM%d" % i, [128, 4, 512], BF16) for i in range(2)]
        TTfs = [S.sb("TTf%d" % i, [128, 4, 128], BF16) for i in range(2)]
        gbs = [S.sb("gb%d" % i, [128, TT], BF16) for i in range(2)]
        pcss = [S.sb("pcs%d" % i, [128, 4], F32) for i in range(2)]
        ysqB = S.sb("ysqB", [128, TT], F32)
        L0 = S.sb("L0", [128, 4, 128], BF16)
        LP = [S.sb("LP%d" % i, [128, 4, 128], BF16) for i in range(2)]
        LT = [S.sb("LT%d" % i, [128, 4, 128], BF16) for i in range(2)]
        SS = [S.sb("SS%d" % i, [128, 4, 128], BF16) for i in range(2)]
        Xb = S.sb("Xb", [128, 128], BF16)
        Ub = S.sb("Ub", [128, 128], BF16)
        ztmp = S.sb("ztmp", [128, 64], F32)
        Ytm = S.sb("Ytm", [128, NSUB, 128], F32)
        ynb = S.sb("ynb", [128, NSUB, 128], BF16)
        gst = S.sb("gst", [128, 32], F32)
        lw = S.sb("lw", [128, TT], BF16)
        lga = S.sb("lga", [128, TT], BF16)
        lgb = S.sb("lgb", [32, TT], BF16)
        yfin = S.sb("yfin", [128, 8, TT], BF16)
        _nbf = nb[:].re("p s c -> p (s c)")
        Qa = [h2T[:, 0:4, :], h2T[:, 4:8, :], _nbf[:, 0:1024].re("p (c t) -> p c t", t=TT)]
        K1 = S.sb("K1", [128, 4, 128 + TT], BF16)
        V1 = S.sb("V1", [128, 3, 512], BF16)
        K2c = S.sb("K2c", [128, 4, TT], BF16)
        K2r = S.sb("K2r", [128, 4, 4, 128], BF16)
        V2c = S.sb("V2c", [64, 4, 128], BF16)
        V2r = S.sb("V2r", [128, 4, 512], BF16)
        K3c = S.sb("K3c", [128, 4, TT], BF16)
        K3r = S.sb("K3r", [128, 4, 16, 128], BF16)
        V3c = S.sb("V3c", [16, 16, 128], BF16)
        V3r = S.sb("V3r", [128, 16, 512], BF16)
        VF = _nbf[:, 1024:2048].re("p (c t) -> p c t", t=TT)
        pe = S.sb("pe", [128, 512], BF16)
        pp_ = S.sb("pp", [128, 512], BF16)
        peb = S.sb("peb", [64, 256], BF16)
        ppb = S.sb("ppb", [64, 256], BF16)
        accO = S.sb("accO", [64, TT], F32)
        accD = S.sb("accD", [64, TT], F32)
        oT = S.sb("oT", [64, 8, TT], BF16)
        sa = tv(2); sbb = tv(3); sg = tv(4); utmp = tv(10)
        actT = S.sb("actT", [128, 8, TT], BF16)
        VF2 = actT[:, 0:4, :]
        wst = xt1[:].re("p s c -> p (s c)")

        _b0 = S.ps("b0", [128, 512])
        _b4 = S.ps("b4", [128, 512])
        _bS = S.ps("bS", [128, 1024])
        B3 = S.ps("b3", [128, 512])
        B5 = S.ps("b5", [128, 512])
        B6 = S.ps("b6", [128, 512])
        _pT = S.ps("pT", [128, 1024], BF16)
        R0 = SubBuf(_b0, 0); R1 = SubBuf(_b0, 256)
        Q0 = SubBuf(_b4, 0); Q1 = SubBuf(_b4, 256)
        B1 = Buf("B1", _bS.t[:, 0:512]); B2 = Buf("B2", _bS.t[:, 512:1024])
        pTr = SubBuf(_pT, 0); pTa = SubBuf(_pT, 512)
        for b_ in (_b0, _b4, B1, B2, B3, B5, B6, _pT):
            b_.psum = True
        pC = B3
        trot = [View(_pT, _pT.t[:, 0:512]), View(B6, B6.t[:, :].bitcast(BF16)), View(B5, B5.t[:, :].bitcast(BF16))]
        prot = {"r": [_b0, B1, B2], "a": [_b4, B5, B6], "x": [_b0, _b4, B1, B2, B3, B6], "f": [_b0, _b4, B6]}
        prot_i = {"r": 0, "a": 0, "x": 0, "f": 0}

        def nextp(k="x"):
            prot_i[k] = (prot_i[k] + 1) % len(prot[k])
            return prot[k][prot_i[k]]

        mm = lambda **kw: S.op("tensor", "matmul", **kw)
        tr = lambda **kw: S.op("tensor", "transpose", **kw)
        act = lambda **kw: S.op("scalar", "activation", **kw)
        vec = lambda m, **kw: S.op("vector", m, **kw)
        gps = lambda m, **kw: S.op("gpsimd", m, **kw)

        def sigmoid_to(dst, src, nbias=None, scale=1.0):
            if nbias is None:
                act(out=dst, in_=src, func=AF.Exp, scale=-scale)
            else:
                act(out=dst, in_=src, func=AF.Exp, scale=-scale, bias=nbias)
            act(out=dst, in_=dst, func=AF.Ln, bias=one_c_for(dst))
            act(out=dst, in_=dst, func=AF.Exp, scale=-1.0)

        def one_c_for(v):
            lo = v.ap.base_partition()
            n = v.ap.partition_size()
            return cf[lo:lo + n, CF_EPS + 3:CF_EPS + 4]

        def rsqrt_to(dst, src, bias_ap, scale=1.0):
            act(out=dst, in_=src, func=AF.Ln, bias=bias_ap, scale=scale)
            act(out=dst, in_=dst, func=AF.Exp, scale=-0.5)

        ident = cb[:, CB_ID:CB_ID + 128]
        identf = cf[:, CF_IDF:CF_IDF + 128]
        onesbd = cb[:, CB_ONESBD:CB_ONESBD + 128]
        eps_r = cf[:, CF_EPS:CF_EPS + 1]
        eps_g = cf[:, CF_EPS + 1:CF_EPS + 2]
        zero_c = cf[:, CF_EPS + 2:CF_EPS + 3]
        one_c = cf[:, CF_EPS + 3:CF_EPS + 4]

        def pv(i, kc):
            return pf[:, i * 8 + kc: i * 8 + kc + 1]

        def pdv(i, kc):
            return pd[:, i * 8 + kc: i * 8 + kc + 1]
        PD_A1, PD_A2, PD_GM, PD_GF, PD_OMK, PD_OMR, PD_OMKm, PD_OMV = range(8)

        try:
            S.dma("gpsimd", out=cb[:, :], in_=cb_d[:, :])
            S.dma("sync", out=cf[:, :], in_=cf_d[:, :])
            S.dma("sync", out=pf[:, :], in_=pfm_d[:, :])
            for i in range(NCH):
                S.dma("gpsimd", out=wscr[i][:, :], in_=wsrc[i])
            CK('dma0')
            for b_ in (Zf, Zb, hal, Bpads[0], Bpads[1], Kpads[0], Kpads[1], K1, K2r, V2r, K3r, V3r, V1, hT, Xb, Ub, Vtms[0], Vtms[1]):
                gps("memset", ap=b_[:], constant=0.0)
            CK('memset')
            for half in range(2):
                S.dma("sync", out=wst[:, 0:4 * 288], in_=l1_d[:, half * 4 * 288:(half + 1) * 4 * 288])
                w1v = wst[:, 0:4 * 288].re("p (k c) -> p k c", c=288)
                for k4 in range(4):
                    kc = half * 4 + k4
                    for (lo, hi, mui) in ((0, 64, PV_MUW), (64, 128, PV_MUA), (128, 288, PV_MUG)):
                        vec("tensor_scalar", out=l1b[:, kc, lo:hi], in0=w1v[:, k4, lo:hi], scalar1=pv(mui, kc),
                            scalar2=None, op0=ALU.mult)
                        vec("tensor_tensor", out=l1a[:, kc, lo:hi], in0=w1v[:, k4, lo:hi], in1=l1b[:, kc, lo:hi],
                            op=ALU.subtract)
            for half in range(2):
                S.dma("sync", out=wst[:, 0:1536], in_=l2_d[:, half * 1536:(half + 1) * 1536])
                vec("tensor_copy", out=l2[:].re("p a c -> p (a c)")[:, half * 1536:(half + 1) * 1536], in_=wst[:, 0:1536])
            CK('lora0')
            for j in range(24):
                S.dma("sync", out=wst[:, 0:2048], in_=wmod_d[j])
                wv = wst[:, 0:2048].re("p (k c) -> p k c", c=256)
                for cc in range(2):
                    col = j * 2 + cc
                    for kc in range(8):
                        mm(out=B1[:, col:col + 1], lhsT=wv[:, kc, cc * 128:(cc + 1) * 128], rhs=pv(PV_C, kc),
                           start=(kc == 0), stop=(kc == 7))
            modf = S.sb("modf", [128, 48], F32)
            vec("tensor_tensor", out=modf[:, :], in0=B1[:, 0:48], in1=pf[:, 0:48], op=ALU.add)
            for kc in range(8):
                vec("scalar_tensor_tensor", out=pdv(PD_A1, kc), in0=modf[:, 8 + kc:9 + kc], scalar=1.0,
                    in1=pv(PV_GPM, kc), op0=ALU.add, op1=ALU.mult)
                vec("scalar_tensor_tensor", out=pdv(PD_A2, kc), in0=modf[:, 32 + kc:33 + kc], scalar=1.0,
                    in1=pv(PV_GPF, kc), op0=ALU.add, op1=ALU.mult)
                vec("tensor_tensor", out=pdv(PD_GM, kc), in0=modf[:, 16 + kc:17 + kc], in1=pv(PV_GQM, kc), op=ALU.mult)
                vec("tensor_tensor", out=pdv(PD_GF, kc), in0=modf[:, 40 + kc:41 + kc], in1=pv(PV_GQF, kc), op=ALU.mult)
                vec("tensor_scalar", out=pdv(PD_OMK, kc), in0=pv(PV_KA, kc), scalar1=-1.0, scalar2=1.0,
                    op0=ALU.mult, op1=ALU.add)
                vec("tensor_scalar", out=pdv(5, kc), in0=pv(PV_W0, kc), scalar1=-1.0, scalar2=None, op0=ALU.mult)
                vec("tensor_scalar", out=pdv(6, kc), in0=pv(PV_A0, kc), scalar1=-1.0, scalar2=None, op0=ALU.mult)
            dg = S.sb("dg", [128, 128], F32)
            onesf = S.sb("onesf", [128, 128], F32)
            gps("memset", ap=onesf[:, :], constant=1.0)
            for (pdi, dst) in ((PD_GM, gmb), (PD_GF, gfb)):
                for kc in range(8):
                    vec("tensor_scalar", out=dg[:, :], in0=identf, scalar1=pdv(pdi, kc), scalar2=None, op0=ALU.mult)
                    pz = nextp()
                    mm(out=pz[:, 0:128], lhsT=onesf[:, :], rhs=dg[:, :], start=True, stop=True)
                    act(out=dst[:, kc * 128:(kc + 1) * 128], in_=pz[:, 0:128], func=AF.Copy)

            CK('startup')
            ring_i = [0]

            ring_sets = {"r": ring[0:2], "a": ring[2:4], "x": ring}
            ring_k = {"r": 0, "a": 0, "x": 0}

            def wload(ch, k="x"):
                s = ring_sets[k][ring_k[k] % len(ring_sets[k])]
                ring_k[k] += 1
                S.dma("sync", out=s[:, :], in_=wscr[ch][:, :])
                return s

            def rms_rstd(src3, dst_cols, nsub=NSUB):
                for sub in range(nsub):
                    act(out=junk[:, :], in_=src3[:, sub, :], func=AF.Square,
                        accum_out=st4[:, 8 + sub:9 + sub])
                rsqrt_to(st4[:, dst_cols:dst_cols + nsub], st4[:, 8:8 + nsub], eps_r, 1.0 / D)

            def norm_transpose(xsrc, rcol, dstT, a_idx, b_view_fn, halo):
                for sub in range(NSUB):
                    vec("tensor_scalar", out=nb[:, sub, :], in0=xsrc[:, sub, :], scalar1=st4[:, rcol + sub:rcol + sub + 1],
                        scalar2=None, op0=ALU.mult)
                for kc in range(8):
                    tgt = trot[kc % 3]
                    for sub in range(NSUB):
                        tr(out=tgt[:, sub * 128:(sub + 1) * 128], in_=nb[:, sub, kc * 128:(kc + 1) * 128], identity=ident)
                    act(out=dstT[:, kc, halo:halo + TT], in_=tgt[:, 0:TT], func=AF.Identity,
                        scale=pdv(a_idx, kc), bias=b_view_fn(kc))

            for m in range(nt):
                phaseB = m >= PB0
                xm = xt[m % 2]
                vcur = cf[:, CF_VM + m:CF_VM + m + 1]
                vprev = cf[:, CF_VM + 32 + m:CF_VM + 33 + m]
                vr2 = cf[:, CF_VM + 64 + m:CF_VM + 65 + m]
                vr3 = cf[:, CF_VM + 96 + m:CF_VM + 97 + m]
                S.dma("gpsimd", out=xm[:], in_=xv.v(xv.t[m * TT:(m + 1) * TT, :].rearrange("(s p) c -> p s c", p=128)))
                if m > 0:
                    vec("tensor_scalar", out=hT[:, :, 0:1], in0=hT[:, :, TT:TT + 1],
                        scalar1=cf[:, CF_VM + m - 1:CF_VM + m], scalar2=None, op0=ALU.mult)
                rms_rstd(xm, 0)
                norm_transpose(xm, 0, hT, PD_A1, lambda kc: pf[:, PV_SHM * 8 + kc:PV_SHM * 8 + kc + 1]
                               if False else modf[:, kc:kc + 1], 1)

                CK('stage1')
                pz = nextp()
                for kc in range(8):
                    mm(out=pz[:, 0:TT], lhsT=l1a[:, kc, 0:128], rhs=hT[:, kc, 1:TT + 1], start=(kc == 0), stop=False)
                    mm(out=pz[:, 0:TT], lhsT=l1b[:, kc, 0:128], rhs=hT[:, kc, 0:TT], start=False, stop=(kc == 7))
                sigmoid_to(tp[11][0:64, 0:TT], pz[0:64, 0:TT], None, 2.0)
                vec("tensor_scalar", out=lw[0:64, :], in0=tp[11][0:64, 0:TT], scalar1=2.0, scalar2=-1.0, op0=ALU.mult, op1=ALU.add)
                act(out=lw[64:128, :], in_=pz[64:128, 0:TT], func=AF.Copy)
                if phaseB:
                    pz = nextp()
                    for kc in range(8):
                        mm(out=pz[:, 0:TT], lhsT=l1a[:, kc, 128:256], rhs=hT[:, kc, 1:TT + 1], start=(kc == 0), stop=False)
                        mm(out=pz[:, 0:TT], lhsT=l1b[:, kc, 128:256], rhs=hT[:, kc, 0:TT], start=False, stop=(kc == 7))
                    sigmoid_to(tp[11][:, 0:TT], pz[:, 0:TT])
                    act(out=lga[:, :], in_=tp[11][:, 0:TT], func=AF.Copy)
                    pz = nextp()
                    for kc in range(8):
                        mm(out=pz[0:32, 0:TT], lhsT=l1a[:, kc, 256:288], rhs=hT[:, kc, 1:TT + 1], start=(kc == 0), stop=False)
                        mm(out=pz[0:32, 0:TT], lhsT=l1b[:, kc, 256:288], rhs=hT[:, kc, 0:TT], start=False, stop=(kc == 7))
                    sigmoid_to(tp[11][0:32, 0:TT], pz[0:32, 0:TT])
                    act(out=lgb[0:32, :], in_=tp[11][0:32, 0:TT], func=AF.Copy)

                CK('lora1')
                def rw_front(c0):
                    for c in (c0,):
                        AR = ARs[c % 2]; AM = AMs[c % 2]; Vtm = Vtms[c % 2]; Bpad = Bpads[c % 2]; Kpad = Kpads[c % 2]
                        TTf = TTfs[c % 2]; gb = gbs[c % 2]; pcs = pcss[c % 2]
                        csl = slice(c * 128, (c + 1) * 128)
                        wr = wload(CH_RW + c, 'r')
                        wrv = wr[:, :].re("p (k c) -> p k c", c=512)
                        for j in ((0, 1, 2) if m >= PB0 - 1 else (1, 2)):
                            pz = nextp("r")
                            for kc in range(8):
                                mm(out=pz[:, 0:TT], lhsT=wrv[:, kc, j * 128:(j + 1) * 128], rhs=hT[:, kc, 1:TT + 1],
                                   start=(kc == 0), stop=(kc == 7))
                            vec("tensor_copy", out=pj[j][:, 0:1], in_=hal[:, c, j:j + 1])
                            act(out=pj[j][:, 1:TT + 1], in_=pz[:, 0:TT], func=AF.Copy)
                            vec("tensor_scalar", out=hal[:, c, j:j + 1], in0=pj[j][:, TT:TT + 1], scalar1=vcur,
                                scalar2=None, op0=ALU.mult)
                            vec("tensor_tensor", out=tmpd[:, :], in0=pj[j][:, 0:TT], in1=pj[j][:, 1:TT + 1], op=ALU.subtract)
                            vec("scalar_tensor_tensor", out=rkv[j][:, :], in0=tmpd[:, :], scalar=pv(PV_MUR + j, c),
                                in1=pj[j][:, 1:TT + 1], op0=ALU.mult, op1=ALU.add)
                            yield
                        r_, k_, v_ = rkv
                        vec("tensor_scalar", out=v_[:, :], in0=v_[:, :], scalar1=vcur, scalar2=None, op0=ALU.mult)
                        act(out=vbf[:, :], in_=v_[:, :], func=AF.Copy)
                        pz = nextp("r")
                        mm(out=pz[:, 0:TT], lhsT=l2[0:64, 0, csl], rhs=lw[0:64, :], start=True, stop=True)
                        sigmoid_to(sw[:, :], pz[:, 0:TT], pdv(5, c))
                        pz = nextp("r")
                        mm(out=pz[:, 0:TT], lhsT=l2[64:128, 0, csl], rhs=lw[64:128, :], start=True, stop=True)
                        sigmoid_to(asig[:, :], pz[:, 0:TT], pdv(6, c))
                        pz = nextp("r")
                        if phaseB:
                            mm(out=pz[:, 0:TT], lhsT=l2[:, 1, csl], rhs=lga[:, :], start=True, stop=False)
                            mm(out=pz[:, 0:TT], lhsT=l2[0:32, 2, csl], rhs=lgb[0:32, :], start=False, stop=True)
                            act(out=gb[:, :], in_=pz[:, 0:TT], func=AF.Copy)
                        yield
                        vec("tensor_tensor_scan", out=cs[:, :], data0=cf[:, CF_MSK:CF_MSK + TT], data1=sw[:, :],
                            initial=0.0, op0=ALU.mult, op1=ALU.add)
                        vec("tensor_tensor", out=cm[:, :], in0=cs[:, :], in1=sw[:, :], op=ALU.subtract)
                        act(out=Ep[:, :], in_=cs[:, :], func=AF.Exp, scale=-C0)
                        act(out=En[:, :], in_=cs[:, :], func=AF.Exp, scale=C0)
                        act(out=Em[:, :], in_=cm[:, :], func=AF.Exp, scale=-C0)
                        yield
                        act(out=sqb[:, :], in_=k_[:, :], func=AF.Square, scale=pv(PV_KK, c))
                        pz = nextp("r")
                        mm(out=pz[:, 0:TT], lhsT=onesbd, rhs=sqb[:, :], start=True, stop=True)
                        vec("tensor_scalar", out=rinv[:, :], in0=pz[:, 0:TT], scalar1=1e-18, scalar2=None, op0=ALU.max)
                        act(out=rinv[:, :], in_=rinv[:, :], func=AF.Ln)
                        act(out=rinv[:, :], in_=rinv[:, :], func=AF.Exp, scale=-0.5)
                        vec("scalar_tensor_tensor", out=kkb[:, :], in0=k_[:, :], scalar=pv(PV_KK, c), in1=rinv[:, :],
                            op0=ALU.mult, op1=ALU.mult)
                        yield
                        vec("tensor_scalar", out=ff[:, :], in0=asig[:, :], scalar1=pv(PV_KA, c), scalar2=pdv(PD_OMK, c),
                            op0=ALU.mult, op1=ALU.add)
                        vec("tensor_tensor", out=kmod[:, :], in0=k_[:, :], in1=ff[:, :], op=ALU.mult)
                        vec("tensor_tensor", out=bv[:, :], in0=kkb[:, :], in1=asig[:, :], op=ALU.mult)
                        vec("scalar_tensor_tensor", out=AR[:, :, 0, :], in0=kkb[:, :].re("p (s t) -> p s t", t=128), scalar=-1.0,
                            in1=Em[:, :].re("p (s t) -> p s t", t=128), op0=ALU.mult, op1=ALU.mult)
                        if phaseB:
                            vec("tensor_tensor", out=AR[:, :, 1, :], in0=r_[:, :].re("p (s t) -> p s t", t=128),
                                in1=Ep[:, :].re("p (s t) -> p s t", t=128), op=ALU.mult)
                        vec("tensor_tensor", out=Bt[:, :], in0=bv[:, :], in1=En[:, :], op=ALU.mult)
                        vec("tensor_tensor", out=Kt[:, :], in0=kmod[:, :], in1=En[:, :], op=ALU.mult)
                        yield
                        if phaseB:
                            vec("tensor_tensor", out=tmpd[:, :], in0=r_[:, :], in1=kmod[:, :], op=ALU.mult)
                            act(out=sqb[:, :], in_=tmpd[:, :], func=AF.Copy, scale=pv(PV_RK, c))
                            pz = nextp("r")
                            mm(out=pz[:, 0:TT], lhsT=onesbd, rhs=sqb[:, :], start=True, stop=True)
                            vec("tensor_tensor", out=bon[:, :], in0=pz[:, 0:TT], in1=v_[:, :], op=ALU.mult)
                            vec("tensor_tensor", out=yfin[:, c, :], in0=bon[:, :], in1=gb[:, :], op=ALU.mult)
                        yield
                        for qi, (src, dst) in enumerate(((Bt, Bpad), (Kt, Kpad), (vbf, None))):
                            for sub in range(NSUB):
                                tr(out=pTr[:, (qi % 2) * 256 + sub * 128: (qi % 2) * 256 + (sub + 1) * 128],
                                   in_=src[:, sub * 128:(sub + 1) * 128], identity=ident)
                            yield
                            srcv = pTr[:, (qi % 2) * 256:(qi % 2 + 1) * 256]
                            if dst is None:
                                act(out=Vtm[:].re("p s c -> p (s c)"), in_=srcv, func=AF.Copy)
                            else:
                                for h in range(2):
                                    act(out=dst[:, :, h, h * 64:(h + 1) * 64],
                                        in_=srcv.re("p (s c) -> p s c", c=128)[:, :, h * 64:(h + 1) * 64], func=AF.Copy)
                        CK('rwkv_a')
                        for h in range(2):
                            hs = slice(h * 64, (h + 1) * 64)
                            for sub in range(NSUB):
                                u = h * NSUB + sub
                                tsl = slice(sub * 128, (sub + 1) * 128)
                                pz = (B1, B2)[u % 2]
                                if phaseB:
                                    mm(out=pz[:, 0:256], lhsT=Bt[hs, tsl], rhs=AR[hs, sub, :, :].re("p a t -> p (a t)"),
                                       start=True, stop=True)
                                    mm(out=pz[:, 256:512], lhsT=Kt[hs, tsl], rhs=AR[hs, sub, :, :].re("p a t -> p (a t)"),
                                       start=True, stop=True)
                                    vec("tensor_tensor", out=AM[:, u, :], in0=pz[:, :], in1=cb[:, CB_MT4:CB_MT4 + 512], op=ALU.mult)
                                else:
                                    mm(out=pz[:, 0:128], lhsT=Bt[hs, tsl], rhs=AR[hs, sub, 0, :], start=True, stop=True)
                                    mm(out=pz[:, 256:384], lhsT=Kt[hs, tsl], rhs=AR[hs, sub, 0, :], start=True, stop=True)
                                    v4 = lambda ap_: ap_.re("p (a two b) -> p a two b", a=2, two=2)[:, :, 0, :]
                                    vec("tensor_tensor", out=v4(AM[:, u, :]), in0=v4(pz[:, :]),
                                        in1=v4(cb[:, CB_MT4:CB_MT4 + 512]), op=ALU.mult)
                                mm(out=_b0[:, u * 128:(u + 1) * 128], lhsT=AR[hs, sub, 0, :], rhs=Bt[hs, tsl],
                                   start=True, stop=True)
                                yield
                        vec("tensor_tensor", out=L0[:].re("p u t -> p (u t)"), in0=_b0[:, :], in1=cb[:, CB_ML4:CB_ML4 + 512],
                            op=ALU.mult)
                        yield
                        CK('rwkv_b')
                        vec("tensor_tensor", out=SS[0][:], in0=AM[:, :, 0:128], in1=ident.bc(1, [128, 4, 128]), op=ALU.add)
                        lt_prev = lambda u: AM[:, u, 0:128]
                        lp_prev = lambda u: L0[:, u, :]
                        scur = 0
                        for lev in range(1, 6):
                            lpn = LP[lev % 2]
                            ltn = LT[lev % 2]
                            for u in range(4):
                                mm(out=B1[:, u * 128:(u + 1) * 128], lhsT=lt_prev(u), rhs=lp_prev(u), start=True, stop=True)
                            if lev <= 4:
                                for u in range(4):
                                    mm(out=B2[:, u * 128:(u + 1) * 128], lhsT=lp_prev(u), rhs=lt_prev(u),
                                       start=True, stop=True)
                            act(out=lpn[:].re("p u t -> p (u t)"), in_=B1[:, :], func=AF.Copy)
                            if lev <= 4:
                                act(out=ltn[:].re("p u t -> p (u t)"), in_=B2[:, :], func=AF.Copy)
                            yield
                            for u in range(4):
                                mm(out=_b0[:, u * 128:(u + 1) * 128], lhsT=lpn[:, u, :], rhs=SS[scur][:, u, :], start=True, stop=True)
                            sdst = TTf if lev == 5 else SS[1 - scur]
                            vec("tensor_tensor", out=sdst[:].re("p u t -> p (u t)"), in0=_b0[:, :],
                                in1=SS[scur][:].re("p u t -> p (u t)"), op=ALU.add)
                            scur = 1 - scur
                            lt_prev = (lambda b: (lambda u: b[:, u, :]))(ltn)
                            yield
                            lp_prev = (lambda b: (lambda u: b[:, u, :]))(lpn)
                        vec("tensor_copy", out=pcs[:, 0:4], in_=Ep[:, :].re("p (q t) -> p q t", t=64)[:, :, 63])
                        yield

                def rw_back(c0):
                    for c in (c0,):
                        AR = ARs[c % 2]; AM = AMs[c % 2]; Vtm = Vtms[c % 2]; Bpad = Bpads[c % 2]; Kpad = Kpads[c % 2]
                        TTf = TTfs[c % 2]; gb = gbs[c % 2]; pcs = pcss[c % 2]
                        TTm = TTf
                        ysq = ysqB[:, :]
                        yln = ysqB[:, :]
                        CK('rwkv_c')
                        for q in range(2 * NSUB):
                            sub, half = q // 2, q % 2
                            ps_ = slice(half * 64, half * 64 + 64)
                            tsl = slice(sub * 128, (sub + 1) * 128)
                            zi = q % 2
                            for h in range(2):
                                hs = slice(h * 64, (h + 1) * 64)
                                u = h * NSUB + sub
                                mm(out=pC[:, hs], lhsT=AR[hs, sub, 0, :], rhs=Zb[hs, c, zi, :], start=True, stop=False)
                                mm(out=pC[:, hs], lhsT=AM[:, u, 256:384], rhs=Vtm[:, sub, hs], start=False, stop=True)
                            act(out=Xb[ps_, :], in_=pC[ps_, 0:128], func=AF.Copy)
                            yield
                            CK('c1')
                            for h in range(2):
                                hs = slice(h * 64, (h + 1) * 64)
                                u = h * NSUB + sub
                                mm(out=pC[:, 128 + h * 64:128 + (h + 1) * 64], lhsT=TTm[ps_, u, :], rhs=Xb[ps_, hs],
                                   start=True, stop=True)
                            vec("tensor_copy", out=Ub[ps_, :], in_=pC[ps_, 128:256])
                            yield
                            CK('c2')
                            if phaseB:
                                for h in range(2):
                                    hs = slice(h * 64, (h + 1) * 64)
                                    u = h * NSUB + sub
                                    o_ = slice(256 + h * 64, 256 + (h + 1) * 64)
                                    mm(out=pC[:, o_], lhsT=AR[hs, sub, 1, :], rhs=Zb[hs, c, zi, :], start=True, stop=False)
                                    mm(out=pC[:, o_], lhsT=AM[:, u, 128:256], rhs=Ub[:, hs], start=False, stop=False)
                                    mm(out=pC[:, o_], lhsT=AM[:, u, 384:512], rhs=Vtm[:, sub, hs], start=False, stop=True)
                                act(out=Ytm[ps_, sub, :], in_=pC[ps_, 256:384], func=AF.Copy)
                                yield
                            CK('c3')
                            for h in range(2):
                                hs = slice(h * 64, (h + 1) * 64)
                                mm(out=pC[:, 384:448], lhsT=Bpad[ps_, sub, h, :], rhs=Ub[ps_, hs], start=(h == 0), stop=False)
                                mm(out=pC[:, 384:448], lhsT=Kpad[ps_, sub, h, :], rhs=Vtm[ps_, sub, hs], start=False, stop=(h == 1))
                            CK('c4')
                            pcv = pcs[:, q:q + 1]
                            vec("tensor_scalar", out=ztmp[:, :], in0=Zf[:, c, :], scalar1=pcv, scalar2=None, op0=ALU.mult)
                            vec("scalar_tensor_tensor", out=Zf[:, c, :], in0=pC[:, 384:448], scalar=pcv, in1=ztmp[:, :],
                                op0=ALU.mult, op1=ALU.add)
                            act(out=Zb[:, c, 1 - zi, :], in_=Zf[:, c, :], func=AF.Copy)
                            yield
                        CK('rwkv_d')
                        if phaseB:
                            yv = Ytm[:].re("p s (h i) -> p (s h) i", i=64)
                            vec("tensor_reduce", out=gst[:, 0:4], in_=yv, axis=AX.X, op=ALU.add)
                            act(out=ysq[:, :], in_=Ytm[:].re("p s c -> p (s c)"), func=AF.Square)
                            vec("tensor_reduce", out=gst[:, 4:8], in_=ysq[:, :].re("p (g i) -> p g i", i=64), axis=AX.X, op=ALU.add)
                            vec("tensor_scalar", out=gst[:, 8:12], in0=gst[:, 0:4], scalar1=1.0 / 64, scalar2=None, op0=ALU.mult)
                            vec("tensor_tensor", out=gst[:, 12:16], in0=gst[:, 8:12], in1=gst[:, 8:12], op=ALU.mult)
                            vec("scalar_tensor_tensor", out=gst[:, 16:20], in0=gst[:, 4:8], scalar=1.0 / 64, in1=gst[:, 12:16],
                                op0=ALU.mult, op1=ALU.subtract)
                            rsqrt_to(gst[:, 24:28], gst[:, 16:20], eps_g, 1.0)
                            ysv = ysq[:, :].re("p (g i) -> p g i", i=64)
                            vec("tensor_tensor", out=ysv, in0=yv, in1=gst[:, 8:12].bc(2, [128, 4, 64]), op=ALU.subtract)
                            vec("tensor_tensor", out=ynb[:].re("p s (h i) -> p (s h) i", i=64), in0=ysv,
                                in1=gst[:, 24:28].bc(2, [128, 4, 64]), op=ALU.mult)
                            yield
                            for sub in range(NSUB):
                                tr(out=pTr[:, 256 + sub * 128:256 + (sub + 1) * 128], in_=ynb[:, sub, :], identity=ident)
                            act(out=yln[:, :], in_=pTr[:, 256:256 + TT], func=AF.Identity, scale=pv(PV_LNW, c), bias=pv(PV_LNB, c))
                            vec("tensor_tensor", out=yln[:, :], in0=yln[:, :], in1=gb[:, :], op=ALU.mult)
                            vec("tensor_tensor", out=yfin[:, c, :], in0=yln[:, :], in1=yfin[:, c, :], op=ALU.add)
                            yield


                def th_attn():
                    if m < PB0 - 8:
                        return
                    j0_2 = m % 2
                    j0_3 = m % 8
                    for g in range(3):
                        kdst = (K1, K2c, K3c)[g]
                        for j in ((0, 1, 2) if phaseB else (1, 2)):
                            wa = wload(CH_AT + g * 3 + j, 'a')
                            wav = wa[:, :].re("p (k c) -> p k c", c=512)
                            for cc in range(4):
                                pz = nextp("a")
                                for kc in range(8):
                                    mm(out=pz[:, 0:TT], lhsT=wav[:, kc, cc * 128:(cc + 1) * 128], rhs=hT[:, kc, 1:TT + 1],
                                       start=(kc == 0), stop=(kc == 7))
                                if j == 0:
                                    act(out=Qa[g][:, cc, :], in_=pz[:, 0:TT], func=AF.Copy, scale=0.125)
                                elif j == 1:
                                    if g == 0:
                                        act(out=K1[:, cc, 128:128 + TT], in_=pz[:, 0:TT], func=AF.Copy)
                                    else:
                                        act(out=kdst[:, cc, :], in_=pz[:, 0:TT], func=AF.Copy)
                                else:
                                    act(out=VF[:, cc, :], in_=pz[:, 0:TT], func=AF.Copy)
                            yield
                        if g == 0:
                            for blk in range(2):
                                for cc in range(4):
                                    tr(out=pTa[:, cc * 128:(cc + 1) * 128], in_=VF[:, cc, blk * 128:(blk + 1) * 128], identity=ident)
                                vec("tensor_copy", out=V1[:, 1 + blk, :], in_=pTa[:, 0:512])
                            yield
                        elif g == 1:
                            vec("tensor_copy", out=VF2[:], in_=VF[:])
                    CK('attn_proj')
                    for h in range(8):
                        cc, hp = h // 2, (h % 2) * 64
                        hs = slice(hp, hp + 64)
                        vs = slice(h * 64, (h + 1) * 64)
                        vl = slice(hp, hp + 64)
                        if h % 2 == 0:
                            for r in range(4):
                                tr(out=pTa[0:64, r * 128:(r + 1) * 128],
                                   in_=VF2[:, cc, :].re("p (i r) -> p r i", r=4)[:, r, :], identity=ident)
                            vec("tensor_copy", out=V2c[0:64, :, :].re("p r c -> p (r c)"), in_=pTa[0:64, 0:512])
                            for r in range(16):
                                tr(out=pTa[0:16, (r % 4) * 128:(r % 4 + 1) * 128],
                                   in_=VF[:, cc, :].re("p (i r) -> p r i", r=16)[:, r, :], identity=ident)
                                if r % 4 == 3:
                                    vec("tensor_copy", out=V3c[0:16, r - 3:r + 1, :].re("p a c -> p (a c)"), in_=pTa[0:16, 0:512])
                                yield
                        if not phaseB:
                            if h % 2 == 1:
                                S.dma("gpsimd", out=V2r[j0_2 * 64:(j0_2 + 1) * 64, :, cc * 128:(cc + 1) * 128], in_=V2c[0:64, :, :])
                                S.dma("gpsimd", out=V3r[j0_3 * 16:(j0_3 + 1) * 16, :, cc * 128:(cc + 1) * 128], in_=V3c[0:16, :, :])
                            continue
                        for blk in range(2):
                            qv = Qa[0][hs, cc, blk * 128:(blk + 1) * 128]
                            mm(out=B5[:, (blk * 2) * 128:(blk * 2 + 1) * 128], lhsT=K1[hs, cc, blk * 128:(blk + 1) * 128],
                               rhs=qv, start=True, stop=True)
                            mm(out=B5[:, (blk * 2 + 1) * 128:(blk * 2 + 2) * 128],
                               lhsT=K1[hs, cc, 128 + blk * 128:128 + (blk + 1) * 128], rhs=qv, start=True, stop=True)
                        act(out=pe[:, :], in_=B5[:, 0:512], func=AF.Exp)
                        yield
                        vec("tensor_tensor", out=pp_[:, :].re("p (b e) -> p b e", b=2), in0=pe[:, :].re("p (b e) -> p b e", b=2),
                            in1=cb[:, CB_E1 + h * 256:CB_E1 + (h + 1) * 256].bc(1, [128, 2, 256]), op=ALU.mult)
                        vec("tensor_scalar", out=pp_[:, 0:128], in0=pp_[:, 0:128], scalar1=vprev, scalar2=None, op0=ALU.mult)
                        yield
                        for blk in range(2):
                            mm(out=B6[0:64, blk * 128:(blk + 1) * 128], lhsT=V1[:, blk, vs],
                               rhs=pp_[:, (blk * 2) * 128:(blk * 2 + 1) * 128], start=True, stop=False)
                            mm(out=B6[0:64, blk * 128:(blk + 1) * 128], lhsT=V1[:, blk + 1, vs],
                               rhs=pp_[:, (blk * 2 + 1) * 128:(blk * 2 + 2) * 128], start=False, stop=True)
                        ppv = pp_[:, :].re("p (b c q) -> p b c q", b=2, c=2)
                        mm(out=B6[0:64, 256:512], lhsT=cb[:, CB_ONES:CB_ONES + 64], rhs=ppv[:, :, 0, :], start=True, stop=False)
                        mm(out=B6[0:64, 256:512], lhsT=cb[:, CB_ONES:CB_ONES + 64], rhs=ppv[:, :, 1, :], start=False, stop=True)
                        act(out=accO[:, :], in_=B6[0:64, 0:TT], func=AF.Copy)
                        act(out=accD[:, :], in_=B6[0:64, 256:512], func=AF.Copy)
                        yield
                        for r in range(4):
                            qv = Qa[1][hs, cc, :].re("p (i r) -> p r i", r=4)[:, r, :]
                            mm(out=B5[:, r * 64:(r + 1) * 64], lhsT=K2r[hs, cc, r, :], rhs=qv, start=True, stop=True)
                            mm(out=B5[0:64, 256 + r * 64:256 + (r + 1) * 64],
                               lhsT=K2c[hs, cc, :].re("p (i r) -> p r i", r=4)[:, r, :], rhs=qv, start=True, stop=True)
                        act(out=pe[:, 0:256], in_=B5[:, 0:256], func=AF.Exp)
                        act(out=peb[0:64, :], in_=B5[0:64, 256:512], func=AF.Exp)
                        yield
                        ea = cb[:, CB_EA2 + (j0_2 * 8 + h) * 64:CB_EA2 + (j0_2 * 8 + h + 1) * 64]
                        vec("scalar_tensor_tensor", out=pp_[:, 0:256].re("p (r i) -> p r i", r=4),
                            in0=pe[:, 0:256].re("p (r i) -> p r i", r=4), scalar=vr2, in1=ea.bc(1, [128, 4, 64]),
                            op0=ALU.mult, op1=ALU.mult)
                        eb = cb[0:64, CB_EB2 + h * 64:CB_EB2 + (h + 1) * 64]
                        vec("tensor_tensor", out=ppb[0:64, :].re("p (r i) -> p r i", r=4),
                            in0=peb[0:64, :].re("p (r i) -> p r i", r=4), in1=eb.bc(1, [64, 4, 64]), op=ALU.mult)
                        yield
                        for r in range(4):
                            mm(out=B6[0:64, r * 64:(r + 1) * 64], lhsT=V2r[:, r, vs], rhs=pp_[:, r * 64:(r + 1) * 64],
                               start=True, stop=False)
                            mm(out=B6[0:64, r * 64:(r + 1) * 64], lhsT=V2c[0:64, r, vl], rhs=ppb[0:64, r * 64:(r + 1) * 64],
                               start=False, stop=True)
                        mm(out=B6[0:64, 256:512], lhsT=cb[:, CB_ONES:CB_ONES + 64], rhs=pp_[:, 0:256], start=True, stop=False)
                        mm(out=B6[0:64, 256:512], lhsT=cb[0:64, CB_ONES:CB_ONES + 64], rhs=ppb[0:64, :], start=False, stop=True)
                        vec("tensor_tensor", out=accO[:, :].re("p (i r) -> p r i", r=4), in0=accO[:, :].re("p (i r) -> p r i", r=4),
                            in1=B6[0:64, 0:TT].re("p (r i) -> p r i", r=4), op=ALU.add)
                        vec("tensor_tensor", out=accD[:, :].re("p (i r) -> p r i", r=4), in0=accD[:, :].re("p (i r) -> p r i", r=4),
                            in1=B6[0:64, 256:512].re("p (r i) -> p r i", r=4), op=ALU.add)
                        yield
                        for r in range(16):
                            qv = Qa[2][hs, cc, :].re("p (i r) -> p r i", r=16)[:, r, :]
                            mm(out=B5[:, r * 16:(r + 1) * 16], lhsT=K3r[hs, cc, r, :], rhs=qv, start=True, stop=True)
                            mm(out=B5[0:16, 256 + r * 16:256 + (r + 1) * 16],
                               lhsT=K3c[hs, cc, :].re("p (i r) -> p r i", r=16)[:, r, :], rhs=qv, start=True, stop=True)
                        act(out=pe[:, 0:256], in_=B5[:, 0:256], func=AF.Exp)
                        act(out=peb[0:16, :], in_=B5[0:16, 256:512], func=AF.Exp)
                        yield
                        ea = cb[:, CB_EA3 + (j0_3 * 8 + h) * 16:CB_EA3 + (j0_3 * 8 + h + 1) * 16]
                        vec("scalar_tensor_tensor", out=pp_[:, 0:256].re("p (r i) -> p r i", r=16),
                            in0=pe[:, 0:256].re("p (r i) -> p r i", r=16), scalar=vr3, in1=ea.bc(1, [128, 16, 16]),
                            op0=ALU.mult, op1=ALU.mult)
                        eb = cb[0:16, CB_EB3 + h * 16:CB_EB3 + (h + 1) * 16]
                        vec("tensor_tensor", out=ppb[0:16, :].re("p (r i) -> p r i", r=16),
                            in0=peb[0:16, :].re("p (r i) -> p r i", r=16), in1=eb.bc(1, [16, 16, 16]), op=ALU.mult)
                        yield
                        for r in range(16):
                            mm(out=B6[0:64, r * 16:(r + 1) * 16], lhsT=V3r[:, r, vs], rhs=pp_[:, r * 16:(r + 1) * 16],
                               start=True, stop=False)
                            mm(out=B6[0:64, r * 16:(r + 1) * 16], lhsT=V3c[0:16, r, vl], rhs=ppb[0:16, r * 16:(r + 1) * 16],
                               start=False, stop=True)
                        mm(out=B6[0:64, 256:512], lhsT=cb[:, CB_ONES:CB_ONES + 64], rhs=pp_[:, 0:256], start=True, stop=False)
                        mm(out=B6[0:64, 256:512], lhsT=cb[0:16, CB_ONES:CB_ONES + 64], rhs=ppb[0:16, :], start=False, stop=True)
                        yield
                        if phaseB:
                            vec("tensor_tensor", out=accO[:, :].re("p (i r) -> p r i", r=16),
                                in0=accO[:, :].re("p (i r) -> p r i", r=16),
                                in1=B6[0:64, 0:TT].re("p (r i) -> p r i", r=16), op=ALU.add)
                            vec("tensor_tensor", out=accD[:, :].re("p (i r) -> p r i", r=16),
                                in0=accD[:, :].re("p (i r) -> p r i", r=16),
                                in1=B6[0:64, 256:512].re("p (r i) -> p r i", r=16), op=ALU.add)
                            vec("reciprocal", out=accD[:, :], in_=accD[:, :])
                            vec("tensor_tensor", out=oT[:, h, :], in0=accO[:, :], in1=accD[:, :], op=ALU.mult)
                        if h % 2 == 1:
                            S.dma("gpsimd", out=V2r[j0_2 * 64:(j0_2 + 1) * 64, :, cc * 128:(cc + 1) * 128], in_=V2c[0:64, :, :])
                            S.dma("gpsimd", out=V3r[j0_3 * 16:(j0_3 + 1) * 16, :, cc * 128:(cc + 1) * 128], in_=V3c[0:16, :, :])
                    CK('attn')
                    vec("tensor_copy", out=K1[:, :, 0:128], in_=K1[:, :, TT:TT + 128])
                    vec("tensor_copy", out=V1[:, 0, :], in_=V1[:, 2, :])
                    vec("tensor_copy", out=K2r[:, :, :, j0_2 * 64:(j0_2 + 1) * 64],
                        in_=K2c[:].re("p c (i r) -> p c r i", r=4))
                    vec("tensor_copy", out=K3r[:, :, :, j0_3 * 16:(j0_3 + 1) * 16],
                        in_=K3c[:].re("p c (i r) -> p c r i", r=16))

                ag = th_attn()
                ag_done = [False]

                def step_attn():
                    if ag_done[0]:
                        return
                    try:
                        next(ag)
                    except StopIteration:
                        ag_done[0] = True

                def run_rr(gens):
                    gens = list(gens)
                    while gens:
                        for g_ in list(gens):
                            try:
                                next(g_)
                            except StopIteration:
                                gens.remove(g_)
                        if INTERLEAVE:
                            step_attn()

                for k_ in range(9):
                    gens = []
                    if k_ < 8:
                        gens.append(rw_front(k_))
                    if k_ >= 1:
                        gens.append(rw_back(k_ - 1))
                    if INTERLEAVE:
                        run_rr(gens)
                    else:
                        for g_ in reversed(gens):
                            for _ in g_:
                                pass
                while not ag_done[0]:
                    step_attn()

                if not phaseB:
                    continue
                for cc in range(8):
                    sa, sbb = ((tv(2), tv(3)), (tv(5), tv(6)))[cc % 2]
                    wpb = wload(CH_PB + cc)
                    wv = wpb[:, :].re("p (a k c) -> p a k c", a=4, c=128)
                    pz = nextp()
                    for kc in range(8):
                        mm(out=pz[:, 0:TT], lhsT=wv[:, 0, kc, :], rhs=hT[:, kc, 1:TT + 1], start=(kc == 0), stop=(kc == 7))
                    sigmoid_to(sa[:, :], pz[:, 0:TT])
                    pz = nextp()
                    for kc in range(8):
                        mm(out=pz[:, 0:TT], lhsT=wv[:, 1, kc, :], rhs=hT[:, kc, 1:TT + 1], start=(kc == 0), stop=(kc == 7))
                    sigmoid_to(sbb[:, :], pz[:, 0:TT])
                    pz = nextp()
                    for kc in range(8):
                        mm(out=pz[:, 0:TT], lhsT=wv[:, 2, kc, :], rhs=yfin[:, kc, :], start=(kc == 0), stop=(kc == 7))
                    vec("tensor_tensor", out=sa[:, :], in0=sa[:, :], in1=pz[:, 0:TT], op=ALU.mult)
                    pz = nextp()
                    for hh_ in range(8):
                        mm(out=pz[:, 0:TT], lhsT=wv[0:64, 3, hh_, :], rhs=oT[:, hh_, :], start=(hh_ == 0), stop=(hh_ == 7))
                    vec("tensor_tensor", out=sbb[:, :], in0=sbb[:, :], in1=pz[:, 0:TT], op=ALU.mult)
                    vec("tensor_tensor", out=mixT[:, cc, :], in0=sa[:, :], in1=sbb[:, :], op=ALU.add)

                def norm_residual(ps_views, gb):
                    for hf in range(2):
                        act(out=junk[:, 0:512], in_=ps_views[hf], func=AF.Square, accum_out=st4[:, 8 + hf:9 + hf])
                    vec("tensor_tensor", out=st4[:, 10:11], in0=st4[:, 8:9], in1=st4[:, 9:10], op=ALU.add)
                    rsqrt_to(st4[:, 4:5], st4[:, 10:11], eps_r, 1.0 / D)
                    for hf in range(2):
                        for qq in range(2):
                            cs_ = slice(hf * 512 + qq * 256, hf * 512 + (qq + 1) * 256)
                            vec("scalar_tensor_tensor", out=utmp[:, :], in0=ps_views[hf][:, qq * 256:(qq + 1) * 256],
                                scalar=st4[:, 4:5], in1=gb[:, cs_], op0=ALU.mult, op1=ALU.mult)
                            vec("tensor_tensor", out=xm[:, sub, cs_], in0=xm[:, sub, cs_], in1=utmp[:, :], op=ALU.add)

                wo = [wload(CH_WOUT + 0), wload(CH_WOUT + 1)]
                for sub in range(NSUB):
                    for hf in range(2):
                        wv = wo[hf][:, :].re("p (k c) -> p k c", c=512)
                        for kc in range(8):
                            mm(out=(B1, B2)[hf][:, :], lhsT=mixT[:, kc, sub * 128:(sub + 1) * 128], rhs=wv[:, kc, :],
                               start=(kc == 0), stop=(kc == 7))
                    norm_residual([B1[:, :], B2[:, :]], gmb)
                rms_rstd(xm, 2)
                norm_transpose(xm, 2, h2T, PD_A2, lambda kc: modf[:, 24 + kc:25 + kc], 0)
                accs = [[B1[:, :], B2[:, :]], [B3[:, :], B5[:, :]]]
                for pg in range(3):
                    nk = 8 if pg < 2 else 6
                    for i4 in range(nk // 2):
                        i = pg * 4 + i4
                        wf_ = wload(CH_FF + i)
                        wv = wf_[:, :].re("p (k c) -> p k c", c=512)
                        for jj in range(2):
                            jl = i4 * 2 + jj
                            pg_ = nextp("f")
                            for kc in range(8):
                                mm(out=pg_[:, 0:TT], lhsT=wv[:, kc, jj * 128:(jj + 1) * 128], rhs=h2T[:, kc, :],
                                   start=(kc == 0), stop=(kc == 7))
                            act(out=sg[:, :], in_=pg_[:, 0:TT], func=AF.Silu)
                            pu = nextp("f")
                            for kc in range(8):
                                mm(out=pu[:, 0:TT], lhsT=wv[:, kc, 256 + jj * 128:256 + (jj + 1) * 128], rhs=h2T[:, kc, :],
                                   start=(kc == 0), stop=(kc == 7))
                            vec("tensor_tensor", out=actT[:, jl, :], in0=sg[:, :], in1=pu[:, 0:TT], op=ALU.mult)
                    for hf in range(2):
                        wf_ = wload(CH_FO + pg * 2 + hf)
                        wv = wf_[:, :].re("p (k c) -> p k c", c=512)
                        for sub in range(NSUB):
                            for kc in range(nk):
                                mm(out=accs[sub][hf], lhsT=actT[:, kc, sub * 128:(sub + 1) * 128], rhs=wv[:, kc, :],
                                   start=(pg == 0 and kc == 0), stop=(pg == 2 and kc == nk - 1))
                for sub in range(NSUB):
                    norm_residual(accs[sub], gfb)
                r0 = (m - PB0) * TT
                S.dma("gpsimd", out=y_d.v(y_d.t[r0:r0 + TT, :].rearrange("(s p) c -> p s c", p=128)), in_=xm[:])


        except _Stop:
            pass
        S.finish([y_d] + finals)
        S.emit()
    return nc


_CACHE = {}


def prep_inputs(x, c, w_mod, b_mod, g_pre_mix, g_post_mix, g_pre_ffn, g_post_ffn, w_in, mu_rkv, mu_lora,
           w0, w1, w2, a0, a1, a2, g1, g2, k_k, k_a, r_k, ln_x_w, ln_x_b, w_o_rwkv, w_o_attn, w_out,
           w_ffn_in, w_ffn_out):
    f = lambda a: np.asarray(a, np.float32)
    x = f(x); c = f(c)
    w_in = f(w_in)[0]; w_modm = f(w_mod)[0]
    bm = f(b_mod)[0].reshape(6, 1024)
    vecs = [bm[0], bm[1], bm[2], bm[3], bm[4], bm[5], f(g_pre_mix)[0], f(g_post_mix)[0], f(g_pre_ffn)[0],
            f(g_post_ffn)[0], f(mu_rkv)[0, 0], f(mu_rkv)[0, 1], f(mu_rkv)[0, 2], f(mu_lora)[0, 0], f(mu_lora)[0, 1],
            f(mu_lora)[0, 2], f(w0)[0], f(a0)[0], f(k_k)[0], f(k_a)[0], f(r_k)[0].reshape(-1), f(ln_x_w)[0],
            f(ln_x_b)[0]]
    wsrc = np.zeros((NCH, 128, 4096), np.float32)
    def put(i, arr3):
        P, K, C = arr3.shape
        v = wsrc[i].reshape(128, -1)
        tmp = np.zeros((128, K, 4096 // K if K in (8,) else C), np.float32) if False else None
        blk = np.zeros((128, K * C), np.float32)
        blk[:P] = arr3.reshape(P, K * C)
        v[:, :K * C] = blk
    for cch in range(8):
        a = np.zeros((128, 8, 512), np.float32)
        for j in range(3):
            a[:, :, j * 128:(j + 1) * 128] = _wchunk(w_in, slice(j * 1024 + cch * 128, j * 1024 + (cch + 1) * 128))
        put(CH_RW + cch, a)
    for g in range(3):
        for j in range(3):
            o = 3072 + j * 1536 + g * 512
            put(CH_AT + g * 3 + j, _wchunk(w_in, slice(o, o + 512)))
    wor = f(w_o_rwkv)[0]; woa = f(w_o_attn)[0]; wout = f(w_out)[0]
    for cc in range(8):
        cs_ = slice(cc * 128, (cc + 1) * 128)
        a = np.zeros((128, 4, 8, 128), np.float32)
        a[:, 0] = _wchunk(w_in, slice(7680 + cc * 128, 7680 + (cc + 1) * 128))
        a[:, 1] = _wchunk(w_in, slice(8704 + cc * 128, 8704 + (cc + 1) * 128))
        a[:, 2] = _wchunk(wor, cs_)
        a[0:64, 3] = woa[:, cs_].reshape(8, 64, 128).transpose(1, 0, 2)
        put(CH_PB + cc, a.reshape(128, 32, 128))
    for hf in range(2):
        put(CH_WOUT + hf, _wchunk(wout, slice(hf * 512, (hf + 1) * 512)))
    wfi = f(w_ffn_in)[0]; wfo = f(w_ffn_out)[0]
    for i in range(11):
        a = np.zeros((128, 8, 512), np.float32)
        a[:, :, 0:256] = _wchunk(wfi, slice(i * 256, (i + 1) * 256))
        a[:, :, 256:512] = _wchunk(wfi, slice(FH + i * 256, FH + (i + 1) * 256))
        put(CH_FF + i, a)
    for pg in range(3):
        nk = 8 if pg < 2 else 6
        for hf in range(2):
            blk = wfo[pg * 1024:pg * 1024 + nk * 128, hf * 512:(hf + 1) * 512]
            put(CH_FO + pg * 2 + hf, blk.reshape(nk, 128, 512).transpose(1, 0, 2))
    wmod = np.ascontiguousarray(
        w_modm.reshape(8, 128, 24, 256).transpose(2, 1, 0, 3).reshape(24, 128, 2048))
    l1 = np.concatenate([f(w1)[0], f(a1)[0], f(g1)[0]], 1)
    l1 = np.ascontiguousarray(l1.reshape(8, 128, 288).transpose(1, 0, 2).reshape(128, 8 * 288))
    l2 = np.zeros((128, 3, 1024), np.float32)
    l2[0:64, 0] = f(w2)[0]; l2[64:128, 0] = f(a2)[0]
    l2[:, 1] = f(g2)[0][0:128]; l2[0:32, 2] = f(g2)[0][128:160]
    l2 = l2.reshape(128, 3072)
    cbt = _host_consts()
    in_maps = []
    for core in range(8):
        b, hh = core // 2, core % 2
        pfm = np.concatenate([_fm(v) for v in vecs] + [_fm(c[b])], 1)
        if hh == 1:
            xvv = x[b]
        else:
            xvv = np.concatenate([np.zeros((T // 2, D), np.float32), x[b, :T // 2]], 0)
        in_maps.append({"xv": np.ascontiguousarray(xvv), "pfm": np.ascontiguousarray(pfm), "wmod": wmod,
                        "wsrc": wsrc, "l1": l1, "l2": l2, "cbt": cbt, "cft": _host_cf(hh)})
    return in_maps


def kernel(**inputs):
    in_maps = prep_inputs(**inputs)
    if "nc" not in _CACHE:
        _CACHE["nc"] = build()
    nc = _CACHE["nc"]
    res = run_bass_kernel_spmd(nc, in_maps, core_ids=list(range(8)))
    out = np.zeros((4, T, D), np.float32)
    for core in range(8):
        b, hh = core // 2, core % 2
        out[b, hh * (T // 2):(hh + 1) * (T // 2)] = res.results[core]["y"]
    return out
```

```python
import math
from contextlib import ExitStack

import numpy as np
import concourse.bass as bass
import concourse.mybir as mybir
from concourse.bass_utils import run_bass_kernel_spmd

F32 = mybir.dt.float32
BF16 = mybir.dt.bfloat16
AF = mybir.ActivationFunctionType
ALU = mybir.AluOpType
AX = mybir.AxisListType

ENGS = ("tensor", "vector", "scalar", "gpsimd", "sync")

T = 8192
D = 1024
TT = 256
NT = T // TT
PB0 = NT // 2
NSUB = TT // 128
FH = 2816
C0 = math.exp(-0.5)
GN_EPS = 64e-5
RMS_EPS = 1e-6
NSLOT = 4
import os
INTERLEAVE = os.environ.get('NOIL') is None


class Buf:
    def __init__(self, name, t):
        self.name = name
        self.t = t
        self.writer = None
        self.readers = []
        self.dsem = None
        self.dcnt = 0
        self.psum = False

    def __getitem__(self, idx):
        return View(self, self.t[idx])

    def v(self, ap):
        return View(self, ap)


class SubBuf:
    def __init__(self, buf, col0, ncols=None):
        self.buf = buf
        self.col0 = col0
        self.ncols = ncols

    def __getitem__(self, idx):
        ps, cs = idx
        a = 0 if cs.start is None else cs.start
        e = cs.stop if cs.stop is not None else self.ncols
        assert e is not None
        return View(self.buf, self.buf.t[ps, self.col0 + a:self.col0 + e])


class View:
    def __init__(self, buf, ap):
        self.buf = buf
        self.ap = ap

    def __getitem__(self, idx):
        return View(self.buf, self.ap[idx])

    def re(self, pat, **kw):
        return View(self.buf, self.ap.rearrange(pat, **kw))

    def bc(self, axis, shape):
        return View(self.buf, self.ap.unsqueeze(axis).to_broadcast(list(shape)))


def _unw(x):
    return x.ap if isinstance(x, View) else x


class Sched:
    def __init__(self, nc, stack):
        self.nc = nc
        self.stack = stack
        self.q = {e: [] for e in ENGS}
        self.waited = {e: {} for e in ENGS}
        self.dma_sems = []

    def sb(self, name, shape, dt):
        t = self.stack.enter_context(self.nc.sbuf_tensor("s_" + name, list(shape), dt))
        return Buf(name, t)

    def ps(self, name, shape, dt=F32):
        t = self.stack.enter_context(self.nc.psum_tensor("p_" + name, list(shape), dt))
        return Buf(name, t)

    def dram(self, name, shape, dt, kind):
        t = self.nc.dram_tensor(name, list(shape), dt, kind=kind).ap()
        return Buf(name, t)

    def _deps(self, eng, reads, writes):
        deps = {}

        def add(tok):
            if tok is None:
                return
            k, v = tok
            if deps.get(k, 0) < v:
                deps[k] = v

        for b in reads:
            add(b.writer)
            if b.psum:
                for r in b.readers:
                    if r[0] != eng:
                        add(r)
        for b in writes:
            add(b.writer)
            for r in b.readers:
                add(r)
        waits = []
        for k, v in deps.items():
            if k == "tensor" and eng == "tensor":
                continue
            if self.waited[eng].get(k, 0) >= v:
                continue
            self.waited[eng][k] = v
            waits.append((k, v))
            if isinstance(k, str):
                self.q[k][v - 1][2] = True
        return waits

    def _commit(self, tok, reads, writes):
        for b in writes:
            b.writer = tok
            b.readers = []
        for b in reads:
            if b in writes:
                continue
            b.readers.append(tok)
            if len(b.readers) > 48:
                d = {}
                for k, v in b.readers:
                    if d.get(k, 0) < v:
                        d[k] = v
                b.readers = list(d.items())

    def op(self, eng, meth, **kw):
        writes, reads = [], []
        for k, v in kw.items():
            if isinstance(v, View):
                if k in ("out", "accum_out", "ap"):
                    if v.buf not in writes:
                        writes.append(v.buf)
                else:
                    if v.buf not in reads:
                        reads.append(v.buf)
        waits = self._deps(eng, reads, writes)
        if eng == "tensor":
            src = kw.get("lhsT", kw.get("in_"))
            lo = src.ap.base_partition()
            rows = (lo, lo + src.ap.partition_size())
            ob = kw["out"].buf
            prev = getattr(ob, "pe_rows", None)
            if prev is not None and ob.writer is not None and ob.writer[0] == "tensor" and \
                    (rows[1] <= prev[0] or prev[1] <= rows[0]):
                k, v = ob.writer
                if self.waited[eng].get(k, 0) < v:
                    self.waited[eng][k] = v
                    waits.append((k, v))
                    self.q[k][v - 1][2] = True
            ob.pe_rows = rows
        args = {k: _unw(v) for k, v in kw.items()}
        fn = lambda e, m=meth, a=args: getattr(e, m)(**a)
        self.q[eng].append([waits, fn, False, None])
        tok = (eng, len(self.q[eng]))
        self._commit(tok, reads, writes)
        return tok

    def dma(self, eng, out, in_, **kw):
        sb = out.buf
        if sb.dsem is None:
            sb.dsem = ("dma", len(self.dma_sems))
            self.dma_sems.append(sb.name)
        waits = self._deps(eng, [in_.buf], [out.buf])
        sb.dcnt += 16
        tok = (sb.dsem, sb.dcnt)
        a = dict(out=out.ap, in_=in_.ap, **kw)
        fn = lambda e, a=a: e.dma_start(**a)
        self.q[eng].append([waits, fn, False, sb.dsem])
        self._commit(tok, [in_.buf], [out.buf])
        return tok

    def finish(self, final_bufs):
        waits = self._deps("sync", final_bufs, [])
        self.q["sync"].append([waits, None, False, None])

    def emit(self):
        nc = self.nc
        st = self.stack
        esem = {e: st.enter_context(nc.semaphore("es_" + e)) for e in ENGS}
        dsem = [st.enter_context(nc.semaphore("ds%d" % i)) for i in range(len(self.dma_sems))]
        cum = {}
        for e in ENGS:
            c = 0
            arr = []
            for it in self.q[e]:
                if it[2]:
                    c += 1
                arr.append(c)
            cum[e] = arr

        def semval(k, v):
            if isinstance(k, str):
                return esem[k], cum[k][v - 1]
            return dsem[k[1]], v

        block = st.enter_context(nc.Block())

        def run(e, eng):
            for waits, fn, sig, dk in self.q[e]:
                for k, v in waits:
                    s, val = semval(k, v)
                    eng.wait_ge(s, val)
                if fn is None:
                    continue
                ins = fn(eng)
                if dk is not None:
                    ins.then_inc(dsem[dk[1]], 16)
                elif sig:
                    ins.then_inc(esem[e], 1)

        @block.tensor
        def _(eng):
            run("tensor", eng)

        @block.vector
        def _(eng):
            run("vector", eng)

        @block.scalar
        def _(eng):
            run("scalar", eng)

        @block.gpsimd
        def _(eng):
            run("gpsimd", eng)

        @block.sync
        def _(eng):
            run("sync", eng)


def _alibi_slopes(n):
    def pow2(m):
        start = 2.0 ** (-8.0 / m)
        return [start ** (i + 1) for i in range(m)]
    if math.log2(n).is_integer():
        s = pow2(n)
    else:
        p = 2 ** int(math.floor(math.log2(n)))
        s = pow2(p) + pow2(2 * p)[0::2][: n - p]
    return sorted(s, reverse=True)


(PV_SHM, PV_SCM, PV_GTM, PV_SHF, PV_SCF, PV_GTF, PV_GPM, PV_GQM, PV_GPF, PV_GQF,
 PV_MUR, PV_MUK, PV_MUV, PV_MUW, PV_MUA, PV_MUG, PV_W0, PV_A0, PV_KK, PV_KA, PV_RK,
 PV_LNW, PV_LNB, PV_C) = range(24)
NPV = 24

CB_ID = 0
CB_ONESBD = 128
CB_ONES = 256
CB_MT4 = 320
CB_ML4 = 832
CB_E1 = 1344
CB_EA2 = CB_E1 + 8 * 256
CB_EB2 = CB_EA2 + 2 * 8 * 64
CB_EA3 = CB_EB2 + 8 * 64
CB_EB3 = CB_EA3 + 8 * 8 * 16
NCB = CB_EB3 + 8 * 16
CF_MSK = 0
CF_VM = 256
CF_EPS = CF_VM + 128
CF_IDF = CF_EPS + 4
NCF = CF_IDF + 128

CH_RW = 0
CH_AT = 8
CH_PB = 17
CH_WOUT = 25
CH_FF = 27
CH_FO = 38
NCH = 44


def _host_consts():
    sl = np.asarray(_alibi_slopes(24), np.float64).reshape(3, 8)
    cb = np.zeros((128, NCB), np.float32)
    p = np.arange(128)
    cb[:, CB_ID:CB_ID + 128] = np.eye(128)
    cb[:, CB_ONESBD:CB_ONESBD + 128] = (p[:, None] // 64 == p[None, :] // 64)
    cb[:, CB_ONES:CB_ONES + 64] = 1.0
    same = (p[:, None] // 64 == p[None, :] // 64)
    su = same & (p[:, None] < p[None, :])
    iu = same & (p[:, None] <= p[None, :])
    slo = same & (p[:, None] > p[None, :])
    cb[:, CB_MT4:CB_MT4 + 512] = np.concatenate([su, iu, su, iu], 1)
    cb[:, CB_ML4:CB_ML4 + 512] = np.concatenate([slo] * 4, 1)
    k = p[:, None].astype(np.float64)
    q = p[None, :].astype(np.float64)
    for h in range(8):
        dpv = q - k + 128
        e_prev = np.where(dpv <= 128, np.exp(-sl[0, h] * dpv), 0.0)
        dcu = q - k
        e_cur = np.where(dcu >= 0, np.exp(-sl[0, h] * np.maximum(dcu, 0)), 0.0)
        cb[:, CB_E1 + h * 256: CB_E1 + h * 256 + 128] = e_prev
        cb[:, CB_E1 + h * 256 + 128: CB_E1 + h * 256 + 256] = e_cur
    i64 = np.arange(64)[None, :].astype(np.float64)
    for rot in range(2):
        for h in range(8):
            j = p // 64
            pp = (p % 64).astype(np.float64)
            a = ((rot - j - 1) % 2) + 1
            dl = 64.0 * a[:, None] + i64 - pp[:, None]
            e = np.where(dl <= 128, np.exp(-sl[1, h] * 4.0 * dl), 0.0)
            o = CB_EA2 + (rot * 8 + h) * 64
            cb[:, o:o + 64] = e
    for h in range(8):
        kk = np.arange(64)[:, None].astype(np.float64)
        dl = i64 - kk
        e = np.where(dl >= 0, np.exp(-sl[1, h] * 4.0 * np.maximum(dl, 0)), 0.0)
        o = CB_EB2 + h * 64
        cb[0:64, o:o + 64] = e
    i16 = np.arange(16)[None, :].astype(np.float64)
    for rot in range(8):
        for h in range(8):
            j = p // 16
            pp = (p % 16).astype(np.float64)
            a = ((rot - j - 1) % 8) + 1
            dl = 16.0 * a[:, None] + i16 - pp[:, None]
            e = np.where(dl <= 128, np.exp(-sl[2, h] * 16.0 * dl), 0.0)
            o = CB_EA3 + (rot * 8 + h) * 16
            cb[:, o:o + 16] = e
    for h in range(8):
        kk = np.arange(16)[:, None].astype(np.float64)
        dl = i16 - kk
        e = np.where(dl >= 0, np.exp(-sl[2, h] * 16.0 * np.maximum(dl, 0)), 0.0)
        o = CB_EB3 + h * 16
        cb[0:16, o:o + 16] = e
    return cb


def _host_cf(hh):
    cf = np.zeros((128, NCF), np.float32)
    m = np.ones((128, 256), np.float32)
    m[:, 0::64] = 0.0
    cf[:, CF_MSK:CF_MSK + 256] = m
    valid = lambda t: 0.0 if t < 0 else (1.0 if (hh == 1 or t >= PB0) else 0.0)
    p = np.arange(128)
    for t in range(NT):
        cf[:, CF_VM + t] = valid(t)
        cf[:, CF_VM + 32 + t] = valid(t - 1)
        j = p // 64
        a = ((t - j - 1) % 2) + 1
        cf[:, CF_VM + 64 + t] = [valid(t - aa) for aa in a]
        j = p // 16
        a = ((t - j - 1) % 8) + 1
        cf[:, CF_VM + 96 + t] = [valid(t - aa) for aa in a]
    cf[:, CF_EPS] = RMS_EPS
    cf[:, CF_EPS + 1] = GN_EPS
    cf[:, CF_EPS + 3] = 1.0
    cf[:, CF_IDF:CF_IDF + 128] = np.eye(128)
    return cf


def _fm(v):
    return np.ascontiguousarray(v.reshape(8, 128).T)


def _wchunk(w, cols):
    return w[:, cols].reshape(8, 128, -1).transpose(1, 0, 2)


class _Stop(Exception):
    pass


def build(nt=NT, dbg=None, dbg_tile=0, dbg_c=0, stop=None):
    nc = bass.Bass("TRN2", target_bir_lowering=False)
    with ExitStack() as st:
        S = Sched(nc, st)
        finals = []

        def CK(name):
            if stop == name:
                raise _Stop()

        def DBG(name, view, m=None, c=None):
            if not dbg or name not in dbg:
                return
            if m is not None and m != dbg_tile:
                return
            if c is not None and c != dbg_c:
                return
            shp = list(view.ap.shape)
            dd = S.dram("dbg_" + name, shp, view.ap.dtype, "ExternalOutput")
            S.dma("gpsimd", out=dd[:], in_=view)
            finals.append(dd)
        xv = S.dram("xv", [T, D], F32, "ExternalInput")
        pfm_d = S.dram("pfm", [128, NPV * 8], F32, "ExternalInput")
        wmod_d = S.dram("wmod", [24, 128, 2048], F32, "ExternalInput")
        wsrc = S.dram("wsrc", [NCH, 128, 4096], F32, "ExternalInput")
        l1_d = S.dram("l1", [128, 8 * 288], F32, "ExternalInput")
        l2_d = S.dram("l2", [128, 3 * 1024], F32, "ExternalInput")
        cb_d = S.dram("cbt", [128, NCB], F32, "ExternalInput")
        cf_d = S.dram("cft", [128, NCF], F32, "ExternalInput")
        y_d = S.dram("y", [T // 2, D], F32, "ExternalOutput")
        wscr_all = S.dram("wscr", [NCH, 128, 4096], BF16, "Internal")
        wscr = [Buf("wscr%d" % i, wscr_all.t[i]) for i in range(NCH)]

        cb = S.sb("cb", [128, NCB], BF16)
        cf = S.sb("cf", [128, NCF], F32)
        pf = S.sb("pf", [128, NPV * 8], F32)
        pd = S.sb("pd", [128, 12 * 8], F32)
        gmb = S.sb("gmb", [128, 1024], BF16)
        gfb = S.sb("gfb", [128, 1024], BF16)
        l1a = S.sb("l1a", [128, 8, 288], BF16)
        l1b = S.sb("l1b", [128, 8, 288], BF16)
        l2 = S.sb("l2", [128, 3, 1024], BF16)
        ring = [S.sb("ring%d" % i, [128, 4096], BF16) for i in range(NSLOT)]
        xt1 = S.sb("xt", [128, NSUB, 1024], F32)
        xt = [xt1, xt1]
        nb = S.sb("nb", [128, NSUB, 1024], BF16)
        junk = nb[:, 0, :]
        st4 = S.sb("st4", [128, 16], F32)
        hT = S.sb("hT", [128, 8, TT + 1], BF16)
        h2T = S.sb("h2T", [128, 8, TT], BF16)
        mixT = h2T
        Zf = S.sb("Zf", [128, 8, 64], F32)
        Zb = S.sb("Zb", [128, 8, 2, 64], BF16)
        hal = S.sb("hal", [128, 8, 3], F32)
        tp = [S.sb("tp%d" % i, [128, TT + 1], F32) if i != 7 else None for i in range(12)]
        tp[7] = tp[6]
        tv = lambda i: tp[i][:, 0:TT]
        pj = [tp[0], tp[0], tp[0]]
        tmpd = tv(1)
        rkv = [tv(2), tv(3), tv(4)]
        sw = tv(5); asig = tv(6); gg = tv(7); cs = tv(0); cm = tv(1)
        Ep = tv(8); En = tv(9); Em = tv(10); rinv = tv(1); kkb = tv(11); ff = tv(0)
        kmod = tv(5); bv = tv(1); bon = tv(6); yln = tv(9); ysq = tv(10)
        sqb = S.sb("sqb", [128, TT], BF16)
        ARs = [S.sb("AR%d" % i, [128, NSUB, 2, 128], BF16) for i in range(3)]
        Bts = [S.sb("Bt%d" % i, [128, TT], BF16) for i in range(2)]
        Kts = [S.sb("Kt%d" % i, [128, TT], BF16) for i in range(2)]
        vbfs = [S.sb("vbf%d" % i, [128, TT], BF16) for i in range(2)]
        Bpads = [S.sb("Bpad%d" % i, [128, NSUB, 2, 128], BF16) for i in range(2)]
        Kpads = [S.sb("Kpad%d" % i, [128, NSUB, 2, 128], BF16) for i in range(2)]
        Vtms = [S.sb("Vtm%d" % i, [128, NSUB, 128], BF16) for i in range(2)]
        AMs = [S.sb("AM%d" % i, [128, 4, 512], BF16) for i in range(2)]
        TTfs = [S.sb("TTf%d" % i, [128, 4, 128], BF16) for i in range(2)]
        gbs = [S.sb("gb%d" % i, [128, TT], BF16) for i in range(3)]
        pcss = [S.sb("pcs%d" % i, [128, 4], F32) for i in range(3)]
        ysqB = S.sb("ysqB", [128, TT], F32)
        L0 = S.sb("L0", [128, 4, 128], BF16)
        LP = [S.sb("LP%d" % i, [128, 4, 128], BF16) for i in range(2)]
        LT = [S.sb("LT%d" % i, [128, 4, 128], BF16) for i in range(2)]
        SS = [S.sb("SS%d" % i, [128, 4, 128], BF16) for i in range(2)]
        Xb = S.sb("Xb", [128, 128], BF16)
        Ub = S.sb("Ub", [128, 128], BF16)
        ztmp = S.sb("ztmp", [128, 64], F32)
        Ytm = S.sb("Ytm", [128, NSUB, 128], F32)
        ynb = S.sb("ynb", [128, NSUB, 128], BF16)
        gst = S.sb("gst", [128, 32], F32)
        lw = S.sb("lw", [128, TT], BF16)
        lga = S.sb("lga", [128, TT], BF16)
        lgb = S.sb("lgb", [32, TT], BF16)
        yfin = S.sb("yfin", [128, 8, TT], BF16)
        _nbf = nb[:].re("p s c -> p (s c)")
        Qa = [h2T[:, 0:4, :], h2T[:, 4:8, :], _nbf[:, 0:1024].re("p (c t) -> p c t", t=TT)]
        K1 = S.sb("K1", [128, 4, 128 + TT], BF16)
        V1 = S.sb("V1", [128, 3, 512], BF16)
        K2c = S.sb("K2c", [128, 4, TT], BF16)
        K2r = S.sb("K2r", [128, 4, 4, 128], BF16)
        V2c = S.sb("V2c", [64, 4, 128], BF16)
        V2r = S.sb("V2r", [128, 4, 512], BF16)
        K3c = S.sb("K3c", [128, 4, TT], BF16)
        K3r = S.sb("K3r", [128, 4, 16, 128], BF16)
        V3c = S.sb("V3c", [16, 16, 128], BF16)
        V3r = S.sb("V3r", [128, 16, 512], BF16)
        VF = _nbf[:, 1024:2048].re("p (c t) -> p c t", t=TT)
        pe = S.sb("pe", [128, 512], BF16)
        pp_ = S.sb("pp", [128, 512], BF16)
        peb = SubBuf(pe, 256, 256)
        ppb = SubBuf(pp_, 256, 256)
        accO = S.sb("accO", [64, TT], F32)
        accD = S.sb("accD", [64, TT], F32)
        oT = S.sb("oT", [64, 8, TT], BF16)
        sa = tv(2); sbb = tv(3); sg = tv(4); utmp = tv(10)
        actT = S.sb("actT", [128, 8, TT], BF16)
        VF2 = actT[:, 0:4, :]
        wst = xt1[:].re("p s c -> p (s c)")

        _b0 = S.ps("b0", [128, 512])
        _b4 = S.ps("b4", [128, 512])
        _bS = S.ps("bS", [128, 1024])
        B3 = S.ps("b3", [128, 512])
        B5 = S.ps("b5", [128, 512])
        B6 = S.ps("b6", [128, 512])
        _pT = S.ps("pT", [128, 1024], BF16)
        R0 = SubBuf(_b0, 0); R1 = SubBuf(_b0, 256)
        Q0 = SubBuf(_b4, 0); Q1 = SubBuf(_b4, 256)
        B1 = Buf("B1", _bS.t[:, 0:512]); B2 = Buf("B2", _bS.t[:, 512:1024])
        pTr = SubBuf(_pT, 0); pTa = SubBuf(_pT, 512)
        for b_ in (_b0, _b4, B1, B2, B3, B5, B6, _pT):
            b_.psum = True
        pC = B3
        trB = [View(B1, B1.t[:, :].bitcast(BF16)), View(B2, B2.t[:, :].bitcast(BF16))]
        trC = View(B3, B3.t[:, :].bitcast(BF16))
        trot = [View(_pT, _pT.t[:, 0:512]), View(B6, B6.t[:, :].bitcast(BF16)), View(B5, B5.t[:, :].bitcast(BF16))]
        prot = {"r": [_b0, B1, B2], "a": [_b4, B5, B6], "x": [_b0, _b4, B1, B2, B3, B6], "f": [_b0, _b4, B6]}
        prot_i = {"r": 0, "a": 0, "x": 0, "f": 0}

        def nextp(k="x"):
            prot_i[k] = (prot_i[k] + 1) % len(prot[k])
            return prot[k][prot_i[k]]

        mm = lambda **kw: S.op("tensor", "matmul", **kw)
        tr = lambda **kw: S.op("tensor", "transpose", **kw)
        act = lambda **kw: S.op("scalar", "activation", **kw)
        vec = lambda m, **kw: S.op("vector", m, **kw)
        gps = lambda m, **kw: S.op("gpsimd", m, **kw)

        def sigmoid_to(dst, src, nbias=None, scale=1.0):
            if nbias is None:
                act(out=dst, in_=src, func=AF.Exp, scale=-scale)
            else:
                act(out=dst, in_=src, func=AF.Exp, scale=-scale, bias=nbias)
            act(out=dst, in_=dst, func=AF.Ln, bias=one_c_for(dst))
            act(out=dst, in_=dst, func=AF.Exp, scale=-1.0)

        def one_c_for(v):
            lo = v.ap.base_partition()
            n = v.ap.partition_size()
            return cf[lo:lo + n, CF_EPS + 3:CF_EPS + 4]

        def rsqrt_to(dst, src, bias_ap, scale=1.0):
            act(out=dst, in_=src, func=AF.Ln, bias=bias_ap, scale=scale)
            act(out=dst, in_=dst, func=AF.Exp, scale=-0.5)

        ident = cb[:, CB_ID:CB_ID + 128]
        identf = cf[:, CF_IDF:CF_IDF + 128]
        onesbd = cb[:, CB_ONESBD:CB_ONESBD + 128]
        eps_r = cf[:, CF_EPS:CF_EPS + 1]
        eps_g = cf[:, CF_EPS + 1:CF_EPS + 2]
        zero_c = cf[:, CF_EPS + 2:CF_EPS + 3]
        one_c = cf[:, CF_EPS + 3:CF_EPS + 4]

        def pv(i, kc):
            return pf[:, i * 8 + kc: i * 8 + kc + 1]

        def pdv(i, kc):
            return pd[:, i * 8 + kc: i * 8 + kc + 1]
        PD_A1, PD_A2, PD_GM, PD_GF, PD_OMK, PD_OMR, PD_OMKm, PD_OMV = range(8)

        try:
            S.dma("gpsimd", out=cb[:, :], in_=cb_d[:, :])
            S.dma("sync", out=cf[:, :], in_=cf_d[:, :])
            S.dma("sync", out=pf[:, :], in_=pfm_d[:, :])
            for i in range(NCH):
                S.dma("gpsimd", out=wscr[i][:, :], in_=wsrc[i])
            CK('dma0')
            for b_ in (Zf, Zb, hal, Bpads[0], Bpads[1], Kpads[0], Kpads[1], K1, K2r, V2r, K3r, V3r, V1, hT, Xb, Ub, Vtms[0], Vtms[1]):
                gps("memset", ap=b_[:], constant=0.0)
            CK('memset')
            for half in range(2):
                S.dma("sync", out=wst[:, 0:4 * 288], in_=l1_d[:, half * 4 * 288:(half + 1) * 4 * 288])
                w1v = wst[:, 0:4 * 288].re("p (k c) -> p k c", c=288)
                for k4 in range(4):
                    kc = half * 4 + k4
                    for (lo, hi, mui) in ((0, 64, PV_MUW), (64, 128, PV_MUA), (128, 288, PV_MUG)):
                        vec("tensor_scalar", out=l1b[:, kc, lo:hi], in0=w1v[:, k4, lo:hi], scalar1=pv(mui, kc),
                            scalar2=None, op0=ALU.mult)
                        vec("tensor_tensor", out=l1a[:, kc, lo:hi], in0=w1v[:, k4, lo:hi], in1=l1b[:, kc, lo:hi],
                            op=ALU.subtract)
            for half in range(2):
                S.dma("sync", out=wst[:, 0:1536], in_=l2_d[:, half * 1536:(half + 1) * 1536])
                vec("tensor_copy", out=l2[:].re("p a c -> p (a c)")[:, half * 1536:(half + 1) * 1536], in_=wst[:, 0:1536])
            CK('lora0')
            for j in range(24):
                S.dma("sync", out=wst[:, 0:2048], in_=wmod_d[j])
                wv = wst[:, 0:2048].re("p (k c) -> p k c", c=256)
                for cc in range(2):
                    col = j * 2 + cc
                    for kc in range(8):
                        mm(out=B1[:, col:col + 1], lhsT=wv[:, kc, cc * 128:(cc + 1) * 128], rhs=pv(PV_C, kc),
                           start=(kc == 0), stop=(kc == 7))
            modf = S.sb("modf", [128, 48], F32)
            vec("tensor_tensor", out=modf[:, :], in0=B1[:, 0:48], in1=pf[:, 0:48], op=ALU.add)
            for kc in range(8):
                vec("scalar_tensor_tensor", out=pdv(PD_A1, kc), in0=modf[:, 8 + kc:9 + kc], scalar=1.0,
                    in1=pv(PV_GPM, kc), op0=ALU.add, op1=ALU.mult)
                vec("scalar_tensor_tensor", out=pdv(PD_A2, kc), in0=modf[:, 32 + kc:33 + kc], scalar=1.0,
                    in1=pv(PV_GPF, kc), op0=ALU.add, op1=ALU.mult)
                vec("tensor_tensor", out=pdv(PD_GM, kc), in0=modf[:, 16 + kc:17 + kc], in1=pv(PV_GQM, kc), op=ALU.mult)
                vec("tensor_tensor", out=pdv(PD_GF, kc), in0=modf[:, 40 + kc:41 + kc], in1=pv(PV_GQF, kc), op=ALU.mult)
                vec("tensor_scalar", out=pdv(PD_OMK, kc), in0=pv(PV_KA, kc), scalar1=-1.0, scalar2=1.0,
                    op0=ALU.mult, op1=ALU.add)
                vec("tensor_scalar", out=pdv(5, kc), in0=pv(PV_W0, kc), scalar1=-1.0, scalar2=None, op0=ALU.mult)
                vec("tensor_scalar", out=pdv(6, kc), in0=pv(PV_A0, kc), scalar1=-1.0, scalar2=None, op0=ALU.mult)
            dg = tp[0][:, 0:128]
            onesf = tp[1][:, 0:128]
            gps("memset", ap=onesf[:, :], constant=1.0)
            for (pdi, dst) in ((PD_GM, gmb), (PD_GF, gfb)):
                for kc in range(8):
                    vec("tensor_scalar", out=dg[:, :], in0=identf, scalar1=pdv(pdi, kc), scalar2=None, op0=ALU.mult)
                    pz = nextp()
                    mm(out=pz[:, 0:128], lhsT=onesf[:, :], rhs=dg[:, :], start=True, stop=True)
                    act(out=dst[:, kc * 128:(kc + 1) * 128], in_=pz[:, 0:128], func=AF.Copy)

            CK('startup')
            ring_i = [0]

            ring_sets = {"r": ring[0:2], "a": ring[2:4], "x": ring}
            ring_k = {"r": 0, "a": 0, "x": 0}

            def wload(ch, k="x"):
                s = ring_sets[k][ring_k[k] % len(ring_sets[k])]
                ring_k[k] += 1
                S.dma("sync", out=s[:, :], in_=wscr[ch][:, :])
                return s

            def rms_rstd(src3, dst_cols, nsub=NSUB):
                for sub in range(nsub):
                    act(out=junk[:, :], in_=src3[:, sub, :], func=AF.Square,
                        accum_out=st4[:, 8 + sub:9 + sub])
                rsqrt_to(st4[:, dst_cols:dst_cols + nsub], st4[:, 8:8 + nsub], eps_r, 1.0 / D)

            def norm_transpose(xsrc, rcol, dstT, a_idx, b_view_fn, halo):
                for sub in range(NSUB):
                    vec("tensor_scalar", out=nb[:, sub, :], in0=xsrc[:, sub, :], scalar1=st4[:, rcol + sub:rcol + sub + 1],
                        scalar2=None, op0=ALU.mult)
                for kc in range(8):
                    tgt = trot[kc % 3]
                    for sub in range(NSUB):
                        tr(out=tgt[:, sub * 128:(sub + 1) * 128], in_=nb[:, sub, kc * 128:(kc + 1) * 128], identity=ident)
                    act(out=dstT[:, kc, halo:halo + TT], in_=tgt[:, 0:TT], func=AF.Identity,
                        scale=pdv(a_idx, kc), bias=b_view_fn(kc))

            for m in range(nt):
                phaseB = m >= PB0
                prot["r"] = [_b0] if m >= PB0 - 8 else [_b0, _b4, B5, B6]
                xm = xt[m % 2]
                vcur = cf[:, CF_VM + m:CF_VM + m + 1]
                vprev = cf[:, CF_VM + 32 + m:CF_VM + 33 + m]
                vr2 = cf[:, CF_VM + 64 + m:CF_VM + 65 + m]
                vr3 = cf[:, CF_VM + 96 + m:CF_VM + 97 + m]
                S.dma("gpsimd", out=xm[:], in_=xv.v(xv.t[m * TT:(m + 1) * TT, :].rearrange("(s p) c -> p s c", p=128)))
                if m > 0:
                    vec("tensor_scalar", out=hT[:, :, 0:1], in0=hT[:, :, TT:TT + 1],
                        scalar1=cf[:, CF_VM + m - 1:CF_VM + m], scalar2=None, op0=ALU.mult)
                rms_rstd(xm, 0)
                norm_transpose(xm, 0, hT, PD_A1, lambda kc: pf[:, PV_SHM * 8 + kc:PV_SHM * 8 + kc + 1]
                               if False else modf[:, kc:kc + 1], 1)

                CK('stage1')
                pz = nextp()
                for kc in range(8):
                    mm(out=pz[:, 0:TT], lhsT=l1a[:, kc, 0:128], rhs=hT[:, kc, 1:TT + 1], start=(kc == 0), stop=False)
                    mm(out=pz[:, 0:TT], lhsT=l1b[:, kc, 0:128], rhs=hT[:, kc, 0:TT], start=False, stop=(kc == 7))
                sigmoid_to(tp[11][0:64, 0:TT], pz[0:64, 0:TT], None, 2.0)
                vec("tensor_scalar", out=lw[0:64, :], in0=tp[11][0:64, 0:TT], scalar1=2.0, scalar2=-1.0, op0=ALU.mult, op1=ALU.add)
                act(out=lw[64:128, :], in_=pz[64:128, 0:TT], func=AF.Copy)
                if phaseB:
                    pz = nextp()
                    for kc in range(8):
                        mm(out=pz[:, 0:TT], lhsT=l1a[:, kc, 128:256], rhs=hT[:, kc, 1:TT + 1], start=(kc == 0), stop=False)
                        mm(out=pz[:, 0:TT], lhsT=l1b[:, kc, 128:256], rhs=hT[:, kc, 0:TT], start=False, stop=(kc == 7))
                    sigmoid_to(tp[11][:, 0:TT], pz[:, 0:TT])
                    act(out=lga[:, :], in_=tp[11][:, 0:TT], func=AF.Copy)
                    pz = nextp()
                    for kc in range(8):
                        mm(out=pz[0:32, 0:TT], lhsT=l1a[:, kc, 256:288], rhs=hT[:, kc, 1:TT + 1], start=(kc == 0), stop=False)
                        mm(out=pz[0:32, 0:TT], lhsT=l1b[:, kc, 256:288], rhs=hT[:, kc, 0:TT], start=False, stop=(kc == 7))
                    sigmoid_to(tp[11][0:32, 0:TT], pz[0:32, 0:TT])
                    act(out=lgb[0:32, :], in_=tp[11][0:32, 0:TT], func=AF.Copy)

                CK('lora1')
                def rw_f1(c0):
                    for c in (c0,):
                        AR = ARs[c % 3]; AM = AMs[c % 2]; Vtm = Vtms[c % 2]; Bpad = Bpads[c % 2]; Kpad = Kpads[c % 2]
                        TTf = TTfs[c % 2]; gb = gbs[c % 3]; pcs = pcss[c % 3]
                        Bt = Bts[c % 2]; Kt = Kts[c % 2]; vbf = vbfs[c % 2]
                        csl = slice(c * 128, (c + 1) * 128)
                        wr = wload(CH_RW + c, 'r')
                        wrv = wr[:, :].re("p (k c) -> p k c", c=512)
                        for j in ((0, 1, 2) if m >= PB0 - 1 else (1, 2)):
                            pz = nextp("r")
                            for kc in range(8):
                                mm(out=pz[:, 0:TT], lhsT=wrv[:, kc, j * 128:(j + 1) * 128], rhs=hT[:, kc, 1:TT + 1],
                                   start=(kc == 0), stop=(kc == 7))
                            vec("tensor_copy", out=pj[j][:, 0:1], in_=hal[:, c, j:j + 1])
                            act(out=pj[j][:, 1:TT + 1], in_=pz[:, 0:TT], func=AF.Copy)
                            vec("tensor_scalar", out=hal[:, c, j:j + 1], in0=pj[j][:, TT:TT + 1], scalar1=vcur,
                                scalar2=None, op0=ALU.mult)
                            vec("tensor_tensor", out=tmpd[:, :], in0=pj[j][:, 0:TT], in1=pj[j][:, 1:TT + 1], op=ALU.subtract)
                            vec("scalar_tensor_tensor", out=rkv[j][:, :], in0=tmpd[:, :], scalar=pv(PV_MUR + j, c),
                                in1=pj[j][:, 1:TT + 1], op0=ALU.mult, op1=ALU.add)
                            yield
                        r_, k_, v_ = rkv
                        vec("tensor_scalar", out=v_[:, :], in0=v_[:, :], scalar1=vcur, scalar2=None, op0=ALU.mult)
                        act(out=vbf[:, :], in_=v_[:, :], func=AF.Copy)
                        pz = nextp("r")
                        mm(out=pz[:, 0:TT], lhsT=l2[0:64, 0, csl], rhs=lw[0:64, :], start=True, stop=True)
                        sigmoid_to(sw[:, :], pz[:, 0:TT], pdv(5, c))
                        pz = nextp("r")
                        mm(out=pz[:, 0:TT], lhsT=l2[64:128, 0, csl], rhs=lw[64:128, :], start=True, stop=True)
                        sigmoid_to(asig[:, :], pz[:, 0:TT], pdv(6, c))
                        pz = nextp("r")
                        if phaseB:
                            mm(out=pz[:, 0:TT], lhsT=l2[:, 1, csl], rhs=lga[:, :], start=True, stop=False)
                            mm(out=pz[:, 0:TT], lhsT=l2[0:32, 2, csl], rhs=lgb[0:32, :], start=False, stop=True)
                            act(out=gb[:, :], in_=pz[:, 0:TT], func=AF.Copy)
                        yield
                        vec("tensor_tensor_scan", out=cs[:, :], data0=cf[:, CF_MSK:CF_MSK + TT], data1=sw[:, :],
                            initial=0.0, op0=ALU.mult, op1=ALU.add)
                        vec("tensor_tensor", out=cm[:, :], in0=cs[:, :], in1=sw[:, :], op=ALU.subtract)
                        act(out=Ep[:, :], in_=cs[:, :], func=AF.Exp, scale=-C0)
                        act(out=En[:, :], in_=cs[:, :], func=AF.Exp, scale=C0)
                        act(out=Em[:, :], in_=cm[:, :], func=AF.Exp, scale=-C0)
                        yield
                        act(out=sqb[:, :], in_=k_[:, :], func=AF.Square, scale=pv(PV_KK, c))
                        pz = nextp("r")
                        mm(out=pz[:, 0:TT], lhsT=onesbd, rhs=sqb[:, :], start=True, stop=True)
                        vec("tensor_scalar", out=rinv[:, :], in0=pz[:, 0:TT], scalar1=1e-18, scalar2=None, op0=ALU.max)
                        act(out=rinv[:, :], in_=rinv[:, :], func=AF.Ln)
                        act(out=rinv[:, :], in_=rinv[:, :], func=AF.Exp, scale=-0.5)
                        vec("scalar_tensor_tensor", out=kkb[:, :], in0=k_[:, :], scalar=pv(PV_KK, c), in1=rinv[:, :],
                            op0=ALU.mult, op1=ALU.mult)
                        yield
                        vec("tensor_scalar", out=ff[:, :], in0=asig[:, :], scalar1=pv(PV_KA, c), scalar2=pdv(PD_OMK, c),
                            op0=ALU.mult, op1=ALU.add)
                        vec("tensor_tensor", out=kmod[:, :], in0=k_[:, :], in1=ff[:, :], op=ALU.mult)
                        vec("tensor_tensor", out=bv[:, :], in0=kkb[:, :], in1=asig[:, :], op=ALU.mult)
                        vec("scalar_tensor_tensor", out=AR[:, :, 0, :], in0=kkb[:, :].re("p (s t) -> p s t", t=128), scalar=-1.0,
                            in1=Em[:, :].re("p (s t) -> p s t", t=128), op0=ALU.mult, op1=ALU.mult)
                        if phaseB:
                            vec("tensor_tensor", out=AR[:, :, 1, :], in0=r_[:, :].re("p (s t) -> p s t", t=128),
                                in1=Ep[:, :].re("p (s t) -> p s t", t=128), op=ALU.mult)
                        vec("tensor_tensor", out=Bt[:, :], in0=bv[:, :], in1=En[:, :], op=ALU.mult)
                        vec("tensor_tensor", out=Kt[:, :], in0=kmod[:, :], in1=En[:, :], op=ALU.mult)
                        yield
                        if phaseB:
                            vec("tensor_tensor", out=tmpd[:, :], in0=r_[:, :], in1=kmod[:, :], op=ALU.mult)
                            act(out=sqb[:, :], in_=tmpd[:, :], func=AF.Copy, scale=pv(PV_RK, c))
                            pz = nextp("r")
                            mm(out=pz[:, 0:TT], lhsT=onesbd, rhs=sqb[:, :], start=True, stop=True)
                            vec("tensor_tensor", out=bon[:, :], in0=pz[:, 0:TT], in1=v_[:, :], op=ALU.mult)
                            vec("tensor_tensor", out=yfin[:, c, :], in0=bon[:, :], in1=gb[:, :], op=ALU.mult)
                        yield
                        vec("tensor_copy", out=pcs[:, 0:4], in_=Ep[:, :].re("p (q t) -> p q t", t=64)[:, :, 63])
                        yield

                def rw_f2(c0):
                    for c in (c0,):
                        AR = ARs[c % 3]; AM = AMs[c % 2]; Vtm = Vtms[c % 2]; Bpad = Bpads[c % 2]; Kpad = Kpads[c % 2]
                        TTf = TTfs[c % 2]; gb = gbs[c % 3]; pcs = pcss[c % 3]
                        Bt = Bts[c % 2]; Kt = Kts[c % 2]; vbf = vbfs[c % 2]
                        for qi, (src, dst) in enumerate(((Bt, Bpad), (Kt, Kpad), (vbf, None))):
                            for sub in range(NSUB):
                                tr(out=trB[qi % 2][:, sub * 128:(sub + 1) * 128],
                                   in_=src[:, sub * 128:(sub + 1) * 128], identity=ident)
                            yield
                            srcv = trB[qi % 2][:, 0:256]
                            if dst is None:
                                act(out=Vtm[:].re("p s c -> p (s c)"), in_=srcv, func=AF.Copy)
                            else:
                                for h in range(2):
                                    act(out=dst[:, :, h, h * 64:(h + 1) * 64],
                                        in_=srcv.re("p (s c) -> p s c", c=128)[:, :, h * 64:(h + 1) * 64], func=AF.Copy)
                        CK('rwkv_a')
                        for h in range(2):
                            hs = slice(h * 64, (h + 1) * 64)
                            for sub in range(NSUB):
                                u = h * NSUB + sub
                                tsl = slice(sub * 128, (sub + 1) * 128)
                                pz = (B1, B2)[u % 2]
                                if phaseB:
                                    mm(out=pz[:, 0:256], lhsT=Bt[hs, tsl], rhs=AR[hs, sub, :, :].re("p a t -> p (a t)"),
                                       start=True, stop=True)
                                    mm(out=pz[:, 256:512], lhsT=Kt[hs, tsl], rhs=AR[hs, sub, :, :].re("p a t -> p (a t)"),
                                       start=True, stop=True)
                                    vec("tensor_tensor", out=AM[:, u, :], in0=pz[:, :], in1=cb[:, CB_MT4:CB_MT4 + 512], op=ALU.mult)
                                else:
                                    mm(out=pz[:, 0:128], lhsT=Bt[hs, tsl], rhs=AR[hs, sub, 0, :], start=True, stop=True)
                                    mm(out=pz[:, 256:384], lhsT=Kt[hs, tsl], rhs=AR[hs, sub, 0, :], start=True, stop=True)
                                    v4 = lambda ap_: ap_.re("p (a two b) -> p a two b", a=2, two=2)[:, :, 0, :]
                                    vec("tensor_tensor", out=v4(AM[:, u, :]), in0=v4(pz[:, :]),
                                        in1=v4(cb[:, CB_MT4:CB_MT4 + 512]), op=ALU.mult)
                                yield
                        for h in range(2):
                            hs = slice(h * 64, (h + 1) * 64)
                            for sub in range(NSUB):
                                u = h * NSUB + sub
                                tsl = slice(sub * 128, (sub + 1) * 128)
                                mm(out=B1[:, u * 128:(u + 1) * 128], lhsT=AR[hs, sub, 0, :], rhs=Bt[hs, tsl],
                                   start=True, stop=True)
                        vec("tensor_tensor", out=L0[:].re("p u t -> p (u t)"), in0=B1[:, :], in1=cb[:, CB_ML4:CB_ML4 + 512],
                            op=ALU.mult)
                        yield
                        CK('rwkv_b')
                        vec("tensor_tensor", out=SS[0][:], in0=AM[:, :, 0:128], in1=ident.bc(1, [128, 4, 128]), op=ALU.add)
                        lt_prev = lambda u: AM[:, u, 0:128]
                        lp_prev = lambda u: L0[:, u, :]
                        scur = 0
                        for lev in range(1, 6):
                            lpn = LP[lev % 2]
                            ltn = LT[lev % 2]
                            for u in range(4):
                                mm(out=B1[:, u * 128:(u + 1) * 128], lhsT=lt_prev(u), rhs=lp_prev(u), start=True, stop=True)
                            if lev <= 4:
                                for u in range(4):
                                    mm(out=B2[:, u * 128:(u + 1) * 128], lhsT=lp_prev(u), rhs=lt_prev(u),
                                       start=True, stop=True)
                            act(out=lpn[:].re("p u t -> p (u t)"), in_=B1[:, :], func=AF.Copy)
                            if lev <= 4:
                                act(out=ltn[:].re("p u t -> p (u t)"), in_=B2[:, :], func=AF.Copy)
                            yield
                            for u in range(4):
                                mm(out=B1[:, u * 128:(u + 1) * 128], lhsT=lpn[:, u, :], rhs=SS[scur][:, u, :], start=True, stop=True)
                            sdst = TTf if lev == 5 else SS[1 - scur]
                            vec("tensor_tensor", out=sdst[:].re("p u t -> p (u t)"), in0=B1[:, :],
                                in1=SS[scur][:].re("p u t -> p (u t)"), op=ALU.add)
                            scur = 1 - scur
                            lt_prev = (lambda b: (lambda u: b[:, u, :]))(ltn)
                            yield
                            lp_prev = (lambda b: (lambda u: b[:, u, :]))(lpn)

                def rw_back(c0):
                    for c in (c0,):
                        AR = ARs[c % 3]; AM = AMs[c % 2]; Vtm = Vtms[c % 2]; Bpad = Bpads[c % 2]; Kpad = Kpads[c % 2]
                        TTf = TTfs[c % 2]; gb = gbs[c % 3]; pcs = pcss[c % 3]
                        Bt = Bts[c % 2]; Kt = Kts[c % 2]; vbf = vbfs[c % 2]
                        TTm = TTf
                        ysq = ysqB[:, :]
                        yln = ysqB[:, :]
                        CK('rwkv_c')
                        for q in range(2 * NSUB):
                            sub, half = q // 2, q % 2
                            ps_ = slice(half * 64, half * 64 + 64)
                            tsl = slice(sub * 128, (sub + 1) * 128)
                            zi = q % 2
                            for h in range(2):
                                hs = slice(h * 64, (h + 1) * 64)
                                u = h * NSUB + sub
                                mm(out=pC[:, hs], lhsT=AR[hs, sub, 0, :], rhs=Zb[hs, c, zi, :], start=True, stop=False)
                                mm(out=pC[:, hs], lhsT=AM[:, u, 256:384], rhs=Vtm[:, sub, hs], start=False, stop=True)
                            act(out=Xb[ps_, :], in_=pC[ps_, 0:128], func=AF.Copy)
                            yield
                            CK('c1')
                            for h in range(2):
                                hs = slice(h * 64, (h + 1) * 64)
                                u = h * NSUB + sub
                                mm(out=pC[:, 128 + h * 64:128 + (h + 1) * 64], lhsT=TTm[ps_, u, :], rhs=Xb[ps_, hs],
                                   start=True, stop=True)
                            vec("tensor_copy", out=Ub[ps_, :], in_=pC[ps_, 128:256])
                            yield
                            CK('c2')
                            if phaseB:
                                for h in range(2):
                                    hs = slice(h * 64, (h + 1) * 64)
                                    u = h * NSUB + sub
                                    o_ = slice(256 + h * 64, 256 + (h + 1) * 64)
                                    mm(out=pC[:, o_], lhsT=AR[hs, sub, 1, :], rhs=Zb[hs, c, zi, :], start=True, stop=False)
                                    mm(out=pC[:, o_], lhsT=AM[:, u, 128:256], rhs=Ub[:, hs], start=False, stop=False)
                                    mm(out=pC[:, o_], lhsT=AM[:, u, 384:512], rhs=Vtm[:, sub, hs], start=False, stop=True)
                                act(out=Ytm[ps_, sub, :], in_=pC[ps_, 256:384], func=AF.Copy)
                                yield
                            CK('c3')
                            for h in range(2):
                                hs = slice(h * 64, (h + 1) * 64)
                                mm(out=pC[:, 384:448], lhsT=Bpad[ps_, sub, h, :], rhs=Ub[ps_, hs], start=(h == 0), stop=False)
                                mm(out=pC[:, 384:448], lhsT=Kpad[ps_, sub, h, :], rhs=Vtm[ps_, sub, hs], start=False, stop=(h == 1))
                            CK('c4')
                            pcv = pcs[:, q:q + 1]
                            vec("tensor_scalar", out=ztmp[:, :], in0=Zf[:, c, :], scalar1=pcv, scalar2=None, op0=ALU.mult)
                            vec("scalar_tensor_tensor", out=Zf[:, c, :], in0=pC[:, 384:448], scalar=pcv, in1=ztmp[:, :],
                                op0=ALU.mult, op1=ALU.add)
                            act(out=Zb[:, c, 1 - zi, :], in_=Zf[:, c, :], func=AF.Copy)
                            yield
                        CK('rwkv_d')
                        if phaseB:
                            yv = Ytm[:].re("p s (h i) -> p (s h) i", i=64)
                            vec("tensor_reduce", out=gst[:, 0:4], in_=yv, axis=AX.X, op=ALU.add)
                            act(out=ysq[:, :], in_=Ytm[:].re("p s c -> p (s c)"), func=AF.Square)
                            vec("tensor_reduce", out=gst[:, 4:8], in_=ysq[:, :].re("p (g i) -> p g i", i=64), axis=AX.X, op=ALU.add)
                            vec("tensor_scalar", out=gst[:, 8:12], in0=gst[:, 0:4], scalar1=1.0 / 64, scalar2=None, op0=ALU.mult)
                            vec("tensor_tensor", out=gst[:, 12:16], in0=gst[:, 8:12], in1=gst[:, 8:12], op=ALU.mult)
                            vec("scalar_tensor_tensor", out=gst[:, 16:20], in0=gst[:, 4:8], scalar=1.0 / 64, in1=gst[:, 12:16],
                                op0=ALU.mult, op1=ALU.subtract)
                            rsqrt_to(gst[:, 24:28], gst[:, 16:20], eps_g, 1.0)
                            ysv = ysq[:, :].re("p (g i) -> p g i", i=64)
                            vec("tensor_tensor", out=ysv, in0=yv, in1=gst[:, 8:12].bc(2, [128, 4, 64]), op=ALU.subtract)
                            vec("tensor_tensor", out=ynb[:].re("p s (h i) -> p (s h) i", i=64), in0=ysv,
                                in1=gst[:, 24:28].bc(2, [128, 4, 64]), op=ALU.mult)
                            yield
                            for sub in range(NSUB):
                                tr(out=trC[:, sub * 128:(sub + 1) * 128], in_=ynb[:, sub, :], identity=ident)
                            act(out=yln[:, :], in_=trC[:, 0:TT], func=AF.Identity, scale=pv(PV_LNW, c), bias=pv(PV_LNB, c))
                            vec("tensor_tensor", out=yln[:, :], in0=yln[:, :], in1=gb[:, :], op=ALU.mult)
                            vec("tensor_tensor", out=yfin[:, c, :], in0=yln[:, :], in1=yfin[:, c, :], op=ALU.add)
                            yield


                def th_attn():
                    if m < PB0 - 8:
                        return
                    j0_2 = m % 2
                    j0_3 = m % 8
                    for g in range(3):
                        kdst = (K1, K2c, K3c)[g]
                        for j in ((0, 1, 2) if phaseB else (1, 2)):
                            wa = wload(CH_AT + g * 3 + j, 'a')
                            wav = wa[:, :].re("p (k c) -> p k c", c=512)
                            for cc in range(4):
                                pz = nextp("a")
                                for kc in range(8):
                                    mm(out=pz[:, 0:TT], lhsT=wav[:, kc, cc * 128:(cc + 1) * 128], rhs=hT[:, kc, 1:TT + 1],
                                       start=(kc == 0), stop=(kc == 7))
                                if j == 0:
                                    act(out=Qa[g][:, cc, :], in_=pz[:, 0:TT], func=AF.Copy, scale=0.125)
                                elif j == 1:
                                    if g == 0:
                                        act(out=K1[:, cc, 128:128 + TT], in_=pz[:, 0:TT], func=AF.Copy)
                                    else:
                                        act(out=kdst[:, cc, :], in_=pz[:, 0:TT], func=AF.Copy)
                                else:
                                    act(out=VF[:, cc, :], in_=pz[:, 0:TT], func=AF.Copy)
                            yield
                        if g == 0:
                            for blk in range(2):
                                for cc in range(4):
                                    tr(out=pTa[:, cc * 128:(cc + 1) * 128], in_=VF[:, cc, blk * 128:(blk + 1) * 128], identity=ident)
                                vec("tensor_copy", out=V1[:, 1 + blk, :], in_=pTa[:, 0:512])
                            yield
                        elif g == 1:
                            vec("tensor_copy", out=VF2[:], in_=VF[:])
                    CK('attn_proj')
                    for h in range(8):
                        cc, hp = h // 2, (h % 2) * 64
                        hs = slice(hp, hp + 64)
                        vs = slice(h * 64, (h + 1) * 64)
                        vl = slice(hp, hp + 64)
                        if h % 2 == 0:
                            for r in range(4):
                                tr(out=pTa[0:64, r * 128:(r + 1) * 128],
                                   in_=VF2[:, cc, :].re("p (i r) -> p r i", r=4)[:, r, :], identity=ident)
                            vec("tensor_copy", out=V2c[0:64, :, :].re("p r c -> p (r c)"), in_=pTa[0:64, 0:512])
                            for r in range(16):
                                tr(out=pTa[0:16, (r % 4) * 128:(r % 4 + 1) * 128],
                                   in_=VF[:, cc, :].re("p (i r) -> p r i", r=16)[:, r, :], identity=ident)
                                if r % 4 == 3:
                                    vec("tensor_copy", out=V3c[0:16, r - 3:r + 1, :].re("p a c -> p (a c)"), in_=pTa[0:16, 0:512])
                                yield
                        if not phaseB:
                            if h % 2 == 1:
                                S.dma("gpsimd", out=V2r[j0_2 * 64:(j0_2 + 1) * 64, :, cc * 128:(cc + 1) * 128], in_=V2c[0:64, :, :])
                                S.dma("gpsimd", out=V3r[j0_3 * 16:(j0_3 + 1) * 16, :, cc * 128:(cc + 1) * 128], in_=V3c[0:16, :, :])
                            continue
                        for blk in range(2):
                            qv = Qa[0][hs, cc, blk * 128:(blk + 1) * 128]
                            mm(out=B5[:, (blk * 2) * 128:(blk * 2 + 1) * 128], lhsT=K1[hs, cc, blk * 128:(blk + 1) * 128],
                               rhs=qv, start=True, stop=True)
                            mm(out=B5[:, (blk * 2 + 1) * 128:(blk * 2 + 2) * 128],
                               lhsT=K1[hs, cc, 128 + blk * 128:128 + (blk + 1) * 128], rhs=qv, start=True, stop=True)
                        act(out=pe[:, :], in_=B5[:, 0:512], func=AF.Exp)
                        yield
                        vec("tensor_tensor", out=pp_[:, :].re("p (b e) -> p b e", b=2), in0=pe[:, :].re("p (b e) -> p b e", b=2),
                            in1=cb[:, CB_E1 + h * 256:CB_E1 + (h + 1) * 256].bc(1, [128, 2, 256]), op=ALU.mult)
                        vec("tensor_scalar", out=pp_[:, 0:128], in0=pp_[:, 0:128], scalar1=vprev, scalar2=None, op0=ALU.mult)
                        yield
                        for blk in range(2):
                            mm(out=B6[0:64, blk * 128:(blk + 1) * 128], lhsT=V1[:, blk, vs],
                               rhs=pp_[:, (blk * 2) * 128:(blk * 2 + 1) * 128], start=True, stop=False)
                            mm(out=B6[0:64, blk * 128:(blk + 1) * 128], lhsT=V1[:, blk + 1, vs],
                               rhs=pp_[:, (blk * 2 + 1) * 128:(blk * 2 + 2) * 128], start=False, stop=True)
                        ppv = pp_[:, :].re("p (b c q) -> p b c q", b=2, c=2)
                        mm(out=B6[0:64, 256:512], lhsT=cb[:, CB_ONES:CB_ONES + 64], rhs=ppv[:, :, 0, :], start=True, stop=False)
                        mm(out=B6[0:64, 256:512], lhsT=cb[:, CB_ONES:CB_ONES + 64], rhs=ppv[:, :, 1, :], start=False, stop=True)
                        act(out=accO[:, :], in_=B6[0:64, 0:TT], func=AF.Copy)
                        act(out=accD[:, :], in_=B6[0:64, 256:512], func=AF.Copy)
                        yield
                        for r in range(4):
                            qv = Qa[1][hs, cc, :].re("p (i r) -> p r i", r=4)[:, r, :]
                            mm(out=B5[:, r * 64:(r + 1) * 64], lhsT=K2r[hs, cc, r, :], rhs=qv, start=True, stop=True)
                            mm(out=B5[0:64, 256 + r * 64:256 + (r + 1) * 64],
                               lhsT=K2c[hs, cc, :].re("p (i r) -> p r i", r=4)[:, r, :], rhs=qv, start=True, stop=True)
                        act(out=pe[:, 0:256], in_=B5[:, 0:256], func=AF.Exp)
                        act(out=peb[0:64, :], in_=B5[0:64, 256:512], func=AF.Exp)
                        yield
                        ea = cb[:, CB_EA2 + (j0_2 * 8 + h) * 64:CB_EA2 + (j0_2 * 8 + h + 1) * 64]
                        vec("scalar_tensor_tensor", out=pp_[:, 0:256].re("p (r i) -> p r i", r=4),
                            in0=pe[:, 0:256].re("p (r i) -> p r i", r=4), scalar=vr2, in1=ea.bc(1, [128, 4, 64]),
                            op0=ALU.mult, op1=ALU.mult)
                        eb = cb[0:64, CB_EB2 + h * 64:CB_EB2 + (h + 1) * 64]
                        vec("tensor_tensor", out=ppb[0:64, :].re("p (r i) -> p r i", r=4),
                            in0=peb[0:64, :].re("p (r i) -> p r i", r=4), in1=eb.bc(1, [64, 4, 64]), op=ALU.mult)
                        yield
                        for r in range(4):
                            mm(out=B6[0:64, r * 64:(r + 1) * 64], lhsT=V2r[:, r, vs], rhs=pp_[:, r * 64:(r + 1) * 64],
                               start=True, stop=False)
                            mm(out=B6[0:64, r * 64:(r + 1) * 64], lhsT=V2c[0:64, r, vl], rhs=ppb[0:64, r * 64:(r + 1) * 64],
                               start=False, stop=True)
                        mm(out=B6[0:64, 256:512], lhsT=cb[:, CB_ONES:CB_ONES + 64], rhs=pp_[:, 0:256], start=True, stop=False)
                        mm(out=B6[0:64, 256:512], lhsT=cb[0:64, CB_ONES:CB_ONES + 64], rhs=ppb[0:64, :], start=False, stop=True)
                        vec("tensor_tensor", out=accO[:, :].re("p (i r) -> p r i", r=4), in0=accO[:, :].re("p (i r) -> p r i", r=4),
                            in1=B6[0:64, 0:TT].re("p (r i) -> p r i", r=4), op=ALU.add)
                        vec("tensor_tensor", out=accD[:, :].re("p (i r) -> p r i", r=4), in0=accD[:, :].re("p (i r) -> p r i", r=4),
                            in1=B6[0:64, 256:512].re("p (r i) -> p r i", r=4), op=ALU.add)
                        yield
                        for r in range(16):
                            qv = Qa[2][hs, cc, :].re("p (i r) -> p r i", r=16)[:, r, :]
                            mm(out=B5[:, r * 16:(r + 1) * 16], lhsT=K3r[hs, cc, r, :], rhs=qv, start=True, stop=True)
                            mm(out=B5[0:16, 256 + r * 16:256 + (r + 1) * 16],
                               lhsT=K3c[hs, cc, :].re("p (i r) -> p r i", r=16)[:, r, :], rhs=qv, start=True, stop=True)
                        act(out=pe[:, 0:256], in_=B5[:, 0:256], func=AF.Exp)
                        act(out=peb[0:16, :], in_=B5[0:16, 256:512], func=AF.Exp)
                        yield
                        ea = cb[:, CB_EA3 + (j0_3 * 8 + h) * 16:CB_EA3 + (j0_3 * 8 + h + 1) * 16]
                        vec("scalar_tensor_tensor", out=pp_[:, 0:256].re("p (r i) -> p r i", r=16),
                            in0=pe[:, 0:256].re("p (r i) -> p r i", r=16), scalar=vr3, in1=ea.bc(1, [128, 16, 16]),
                            op0=ALU.mult, op1=ALU.mult)
                        eb = cb[0:16, CB_EB3 + h * 16:CB_EB3 + (h + 1) * 16]
                        vec("tensor_tensor", out=ppb[0:16, :].re("p (r i) -> p r i", r=16),
                            in0=peb[0:16, :].re("p (r i) -> p r i", r=16), in1=eb.bc(1, [16, 16, 16]), op=ALU.mult)
                        yield
                        for r in range(16):
                            mm(out=B6[0:64, r * 16:(r + 1) * 16], lhsT=V3r[:, r, vs], rhs=pp_[:, r * 16:(r + 1) * 16],
                               start=True, stop=False)
                            mm(out=B6[0:64, r * 16:(r + 1) * 16], lhsT=V3c[0:16, r, vl], rhs=ppb[0:16, r * 16:(r + 1) * 16],
                               start=False, stop=True)
                        mm(out=B6[0:64, 256:512], lhsT=cb[:, CB_ONES:CB_ONES + 64], rhs=pp_[:, 0:256], start=True, stop=False)
                        mm(out=B6[0:64, 256:512], lhsT=cb[0:16, CB_ONES:CB_ONES + 64], rhs=ppb[0:16, :], start=False, stop=True)
                        yield
                        if phaseB:
                            vec("tensor_tensor", out=accO[:, :].re("p (i r) -> p r i", r=16),
                                in0=accO[:, :].re("p (i r) -> p r i", r=16),
                                in1=B6[0:64, 0:TT].re("p (r i) -> p r i", r=16), op=ALU.add)
                            vec("tensor_tensor", out=accD[:, :].re("p (i r) -> p r i", r=16),
                                in0=accD[:, :].re("p (i r) -> p r i", r=16),
                                in1=B6[0:64, 256:512].re("p (r i) -> p r i", r=16), op=ALU.add)
                            vec("reciprocal", out=accD[:, :], in_=accD[:, :])
                            vec("tensor_tensor", out=oT[:, h, :], in0=accO[:, :], in1=accD[:, :], op=ALU.mult)
                        if h % 2 == 1:
                            S.dma("gpsimd", out=V2r[j0_2 * 64:(j0_2 + 1) * 64, :, cc * 128:(cc + 1) * 128], in_=V2c[0:64, :, :])
                            S.dma("gpsimd", out=V3r[j0_3 * 16:(j0_3 + 1) * 16, :, cc * 128:(cc + 1) * 128], in_=V3c[0:16, :, :])
                    CK('attn')
                    vec("tensor_copy", out=K1[:, :, 0:128], in_=K1[:, :, TT:TT + 128])
                    vec("tensor_copy", out=V1[:, 0, :], in_=V1[:, 2, :])
                    vec("tensor_copy", out=K2r[:, :, :, j0_2 * 64:(j0_2 + 1) * 64],
                        in_=K2c[:].re("p c (i r) -> p c r i", r=4))
                    vec("tensor_copy", out=K3r[:, :, :, j0_3 * 16:(j0_3 + 1) * 16],
                        in_=K3c[:].re("p c (i r) -> p c r i", r=16))

                ag = th_attn()
                ag_done = [False]

                def step_attn():
                    if ag_done[0]:
                        return
                    try:
                        next(ag)
                    except StopIteration:
                        ag_done[0] = True

                def run_rr(gens):
                    gens = list(gens)
                    while gens:
                        for g_ in list(gens):
                            try:
                                next(g_)
                            except StopIteration:
                                gens.remove(g_)
                        if INTERLEAVE:
                            step_attn()

                for k_ in range(10):
                    gens = []
                    if k_ < 8:
                        gens.append(rw_f1(k_))
                    if 1 <= k_ <= 8:
                        gens.append(rw_f2(k_ - 1))
                    if k_ >= 2:
                        gens.append(rw_back(k_ - 2))
                    if INTERLEAVE:
                        run_rr(gens)
                    else:
                        for g_ in reversed(gens):
                            for _ in g_:
                                pass
                while not ag_done[0]:
                    step_attn()

                if not phaseB:
                    continue
                for cc in range(8):
                    sa, sbb = ((tv(2), tv(3)), (tv(5), tv(6)))[cc % 2]
                    wpb = wload(CH_PB + cc)
                    wv = wpb[:, :].re("p (a k c) -> p a k c", a=4, c=128)
                    pz = nextp()
                    for kc in range(8):
                        mm(out=pz[:, 0:TT], lhsT=wv[:, 0, kc, :], rhs=hT[:, kc, 1:TT + 1], start=(kc == 0), stop=(kc == 7))
                    sigmoid_to(sa[:, :], pz[:, 0:TT])
                    pz = nextp()
                    for kc in range(8):
                        mm(out=pz[:, 0:TT], lhsT=wv[:, 1, kc, :], rhs=hT[:, kc, 1:TT + 1], start=(kc == 0), stop=(kc == 7))
                    sigmoid_to(sbb[:, :], pz[:, 0:TT])
                    pz = nextp()
                    for kc in range(8):
                        mm(out=pz[:, 0:TT], lhsT=wv[:, 2, kc, :], rhs=yfin[:, kc, :], start=(kc == 0), stop=(kc == 7))
                    vec("tensor_tensor", out=sa[:, :], in0=sa[:, :], in1=pz[:, 0:TT], op=ALU.mult)
                    pz = nextp()
                    for hh_ in range(8):
                        mm(out=pz[:, 0:TT], lhsT=wv[0:64, 3, hh_, :], rhs=oT[:, hh_, :], start=(hh_ == 0), stop=(hh_ == 7))
                    vec("tensor_tensor", out=sbb[:, :], in0=sbb[:, :], in1=pz[:, 0:TT], op=ALU.mult)
                    vec("tensor_tensor", out=mixT[:, cc, :], in0=sa[:, :], in1=sbb[:, :], op=ALU.add)

                def norm_residual(ps_views, gb):
                    for hf in range(2):
                        act(out=junk[:, 0:512], in_=ps_views[hf], func=AF.Square, accum_out=st4[:, 8 + hf:9 + hf])
                    vec("tensor_tensor", out=st4[:, 10:11], in0=st4[:, 8:9], in1=st4[:, 9:10], op=ALU.add)
                    rsqrt_to(st4[:, 4:5], st4[:, 10:11], eps_r, 1.0 / D)
                    for hf in range(2):
                        for qq in range(2):
                            cs_ = slice(hf * 512 + qq * 256, hf * 512 + (qq + 1) * 256)
                            vec("scalar_tensor_tensor", out=utmp[:, :], in0=ps_views[hf][:, qq * 256:(qq + 1) * 256],
                                scalar=st4[:, 4:5], in1=gb[:, cs_], op0=ALU.mult, op1=ALU.mult)
                            vec("tensor_tensor", out=xm[:, sub, cs_], in0=xm[:, sub, cs_], in1=utmp[:, :], op=ALU.add)

                wo = [wload(CH_WOUT + 0), wload(CH_WOUT + 1)]
                for sub in range(NSUB):
                    for hf in range(2):
                        wv = wo[hf][:, :].re("p (k c) -> p k c", c=512)
                        for kc in range(8):
                            mm(out=(B1, B2)[hf][:, :], lhsT=mixT[:, kc, sub * 128:(sub + 1) * 128], rhs=wv[:, kc, :],
                               start=(kc == 0), stop=(kc == 7))
                    norm_residual([B1[:, :], B2[:, :]], gmb)
                rms_rstd(xm, 2)
                norm_transpose(xm, 2, h2T, PD_A2, lambda kc: modf[:, 24 + kc:25 + kc], 0)
                accs = [[B1[:, :], B2[:, :]], [B3[:, :], B5[:, :]]]
                for pg in range(3):
                    nk = 8 if pg < 2 else 6
                    for i4 in range(nk // 2):
                        i = pg * 4 + i4
                        wf_ = wload(CH_FF + i)
                        wv = wf_[:, :].re("p (k c) -> p k c", c=512)
                        for jj in range(2):
                            jl = i4 * 2 + jj
                            pg_ = nextp("f")
                            for kc in range(8):
                                mm(out=pg_[:, 0:TT], lhsT=wv[:, kc, jj * 128:(jj + 1) * 128], rhs=h2T[:, kc, :],
                                   start=(kc == 0), stop=(kc == 7))
                            act(out=sg[:, :], in_=pg_[:, 0:TT], func=AF.Silu)
                            pu = nextp("f")
                            for kc in range(8):
                                mm(out=pu[:, 0:TT], lhsT=wv[:, kc, 256 + jj * 128:256 + (jj + 1) * 128], rhs=h2T[:, kc, :],
                                   start=(kc == 0), stop=(kc == 7))
                            vec("tensor_tensor", out=actT[:, jl, :], in0=sg[:, :], in1=pu[:, 0:TT], op=ALU.mult)
                    for hf in range(2):
                        wf_ = wload(CH_FO + pg * 2 + hf)
                        wv = wf_[:, :].re("p (k c) -> p k c", c=512)
                        for sub in range(NSUB):
                            for kc in range(nk):
                                mm(out=accs[sub][hf], lhsT=actT[:, kc, sub * 128:(sub + 1) * 128], rhs=wv[:, kc, :],
                                   start=(pg == 0 and kc == 0), stop=(pg == 2 and kc == nk - 1))
                for sub in range(NSUB):
                    norm_residual(accs[sub], gfb)
                r0 = (m - PB0) * TT
                S.dma("gpsimd", out=y_d.v(y_d.t[r0:r0 + TT, :].rearrange("(s p) c -> p s c", p=128)), in_=xm[:])


        except _Stop:
            pass
        S.finish([y_d] + finals)
        S.emit()
    return nc


_CACHE = {}


def prep_inputs(x, c, w_mod, b_mod, g_pre_mix, g_post_mix, g_pre_ffn, g_post_ffn, w_in, mu_rkv, mu_lora,
           w0, w1, w2, a0, a1, a2, g1, g2, k_k, k_a, r_k, ln_x_w, ln_x_b, w_o_rwkv, w_o_attn, w_out,
           w_ffn_in, w_ffn_out):
    f = lambda a: np.asarray(a, np.float32)
    x = f(x); c = f(c)
    w_in = f(w_in)[0]; w_modm = f(w_mod)[0]
    bm = f(b_mod)[0].reshape(6, 1024)
    vecs = [bm[0], bm[1], bm[2], bm[3], bm[4], bm[5], f(g_pre_mix)[0], f(g_post_mix)[0], f(g_pre_ffn)[0],
            f(g_post_ffn)[0], f(mu_rkv)[0, 0], f(mu_rkv)[0, 1], f(mu_rkv)[0, 2], f(mu_lora)[0, 0], f(mu_lora)[0, 1],
            f(mu_lora)[0, 2], f(w0)[0], f(a0)[0], f(k_k)[0], f(k_a)[0], f(r_k)[0].reshape(-1), f(ln_x_w)[0],
            f(ln_x_b)[0]]
    wsrc = np.zeros((NCH, 128, 4096), np.float32)
    def put(i, arr3):
        P, K, C = arr3.shape
        v = wsrc[i].reshape(128, -1)
        tmp = np.zeros((128, K, 4096 // K if K in (8,) else C), np.float32) if False else None
        blk = np.zeros((128, K * C), np.float32)
        blk[:P] = arr3.reshape(P, K * C)
        v[:, :K * C] = blk
    for cch in range(8):
        a = np.zeros((128, 8, 512), np.float32)
        for j in range(3):
            a[:, :, j * 128:(j + 1) * 128] = _wchunk(w_in, slice(j * 1024 + cch * 128, j * 1024 + (cch + 1) * 128))
        put(CH_RW + cch, a)
    for g in range(3):
        for j in range(3):
            o = 3072 + j * 1536 + g * 512
            put(CH_AT + g * 3 + j, _wchunk(w_in, slice(o, o + 512)))
    wor = f(w_o_rwkv)[0]; woa = f(w_o_attn)[0]; wout = f(w_out)[0]
    for cc in range(8):
        cs_ = slice(cc * 128, (cc + 1) * 128)
        a = np.zeros((128, 4, 8, 128), np.float32)
        a[:, 0] = _wchunk(w_in, slice(7680 + cc * 128, 7680 + (cc + 1) * 128))
        a[:, 1] = _wchunk(w_in, slice(8704 + cc * 128, 8704 + (cc + 1) * 128))
        a[:, 2] = _wchunk(wor, cs_)
        a[0:64, 3] = woa[:, cs_].reshape(8, 64, 128).transpose(1, 0, 2)
        put(CH_PB + cc, a.reshape(128, 32, 128))
    for hf in range(2):
        put(CH_WOUT + hf, _wchunk(wout, slice(hf * 512, (hf + 1) * 512)))
    wfi = f(w_ffn_in)[0]; wfo = f(w_ffn_out)[0]
    for i in range(11):
        a = np.zeros((128, 8, 512), np.float32)
        a[:, :, 0:256] = _wchunk(wfi, slice(i * 256, (i + 1) * 256))
        a[:, :, 256:512] = _wchunk(wfi, slice(FH + i * 256, FH + (i + 1) * 256))
        put(CH_FF + i, a)
    for pg in range(3):
        nk = 8 if pg < 2 else 6
        for hf in range(2):
            blk = wfo[pg * 1024:pg * 1024 + nk * 128, hf * 512:(hf + 1) * 512]
            put(CH_FO + pg * 2 + hf, blk.reshape(nk, 128, 512).transpose(1, 0, 2))
    wmod = np.ascontiguousarray(
        w_modm.reshape(8, 128, 24, 256).transpose(2, 1, 0, 3).reshape(24, 128, 2048))
    l1 = np.concatenate([f(w1)[0], f(a1)[0], f(g1)[0]], 1)
    l1 = np.ascontiguousarray(l1.reshape(8, 128, 288).transpose(1, 0, 2).reshape(128, 8 * 288))
    l2 = np.zeros((128, 3, 1024), np.float32)
    l2[0:64, 0] = f(w2)[0]; l2[64:128, 0] = f(a2)[0]
    l2[:, 1] = f(g2)[0][0:128]; l2[0:32, 2] = f(g2)[0][128:160]
    l2 = l2.reshape(128, 3072)
    cbt = _host_consts()
    in_maps = []
    for core in range(8):
        b, hh = core // 2, core % 2
        pfm = np.concatenate([_fm(v) for v in vecs] + [_fm(c[b])], 1)
        if hh == 1:
            xvv = x[b]
        else:
            xvv = np.concatenate([np.zeros((T // 2, D), np.float32), x[b, :T // 2]], 0)
        in_maps.append({"xv": np.ascontiguousarray(xvv), "pfm": np.ascontiguousarray(pfm), "wmod": wmod,
                        "wsrc": wsrc, "l1": l1, "l2": l2, "cbt": cbt, "cft": _host_cf(hh)})
    return in_maps


def kernel(**inputs):
    in_maps = prep_inputs(**inputs)
    if "nc" not in _CACHE:
        _CACHE["nc"] = build()
    nc = _CACHE["nc"]
    res = run_bass_kernel_spmd(nc, in_maps, core_ids=list(range(8)))
    out = np.zeros((4, T, D), np.float32)
    for core in range(8):
        b, hh = core // 2, core % 2
        out[b, hh * (T // 2):(hh + 1) * (T // 2)] = res.results[core]["y"]
    return out
```

```python
import math
from contextlib import ExitStack

import numpy as np
import concourse.bass as bass
import concourse.mybir as mybir
from concourse.bass_utils import run_bass_kernel_spmd

F32 = mybir.dt.float32
BF16 = mybir.dt.bfloat16
AF = mybir.ActivationFunctionType
ALU = mybir.AluOpType
AX = mybir.AxisListType

ENGS = ("tensor", "vector", "scalar", "gpsimd", "sync")

T = 8192
D = 1024
TT = 256
NT = T // TT
PB0 = NT // 2
NSUB = TT // 128
FH = 2816
C0 = math.exp(-0.5)
GN_EPS = 64e-5
RMS_EPS = 1e-6
NSLOT = 4
import os
INTERLEAVE = os.environ.get('NOIL') is None


class Buf:
    def __init__(self, name, t):
        self.name = name
        self.t = t
        self.writer = None
        self.readers = []
        self.dsem = None
        self.dcnt = 0
        self.psum = False

    def __getitem__(self, idx):
        return View(self, self.t[idx])

    def v(self, ap):
        return View(self, ap)


class SubBuf:
    def __init__(self, buf, col0, ncols=None):
        self.buf = buf
        self.col0 = col0
        self.ncols = ncols

    def __getitem__(self, idx):
        ps, cs = idx
        a = 0 if cs.start is None else cs.start
        e = cs.stop if cs.stop is not None else self.ncols
        assert e is not None
        return View(self.buf, self.buf.t[ps, self.col0 + a:self.col0 + e])


class View:
    def __init__(self, buf, ap):
        self.buf = buf
        self.ap = ap

    def __getitem__(self, idx):
        return View(self.buf, self.ap[idx])

    def re(self, pat, **kw):
        return View(self.buf, self.ap.rearrange(pat, **kw))

    def bc(self, axis, shape):
        return View(self.buf, self.ap.unsqueeze(axis).to_broadcast(list(shape)))


def _unw(x):
    return x.ap if isinstance(x, View) else x


class Sched:
    def __init__(self, nc, stack):
        self.nc = nc
        self.stack = stack
        self.q = {e: [] for e in ENGS}
        self.waited = {e: {} for e in ENGS}
        self.dma_sems = []

    def sb(self, name, shape, dt):
        t = self.stack.enter_context(self.nc.sbuf_tensor("s_" + name, list(shape), dt))
        return Buf(name, t)

    def ps(self, name, shape, dt=F32):
        t = self.stack.enter_context(self.nc.psum_tensor("p_" + name, list(shape), dt))
        return Buf(name, t)

    def dram(self, name, shape, dt, kind):
        t = self.nc.dram_tensor(name, list(shape), dt, kind=kind).ap()
        return Buf(name, t)

    def _deps(self, eng, reads, writes):
        deps = {}

        def add(tok):
            if tok is None:
                return
            k, v = tok
            if deps.get(k, 0) < v:
                deps[k] = v

        for b in reads:
            add(b.writer)
            if b.psum:
                for r in b.readers:
                    if r[0] != eng:
                        add(r)
        for b in writes:
            add(b.writer)
            for r in b.readers:
                add(r)
        waits = []
        for k, v in deps.items():
            if k == "tensor" and eng == "tensor":
                continue
            if self.waited[eng].get(k, 0) >= v:
                continue
            self.waited[eng][k] = v
            waits.append((k, v))
            if isinstance(k, str):
                self.q[k][v - 1][2] = True
        return waits

    def _commit(self, tok, reads, writes):
        for b in writes:
            b.writer = tok
            b.readers = []
        for b in reads:
            if b in writes:
                continue
            b.readers.append(tok)
            if len(b.readers) > 48:
                d = {}
                for k, v in b.readers:
                    if d.get(k, 0) < v:
                        d[k] = v
                b.readers = list(d.items())

    def op(self, eng, meth, **kw):
        writes, reads = [], []
        for k, v in kw.items():
            if isinstance(v, View):
                if k in ("out", "accum_out", "ap"):
                    if v.buf not in writes:
                        writes.append(v.buf)
                else:
                    if v.buf not in reads:
                        reads.append(v.buf)
        waits = self._deps(eng, reads, writes)
        if eng == "tensor":
            src = kw.get("lhsT", kw.get("in_"))
            lo = src.ap.base_partition()
            rows = (lo, lo + src.ap.partition_size())
            ob = kw["out"].buf
            prev = getattr(ob, "pe_rows", None)
            if prev is not None and ob.writer is not None and ob.writer[0] == "tensor" and \
                    (rows[1] <= prev[0] or prev[1] <= rows[0]):
                k, v = ob.writer
                if self.waited[eng].get(k, 0) < v:
                    self.waited[eng][k] = v
                    waits.append((k, v))
                    self.q[k][v - 1][2] = True
            ob.pe_rows = rows
        args = {k: _unw(v) for k, v in kw.items()}
        fn = lambda e, m=meth, a=args: getattr(e, m)(**a)
        self.q[eng].append([waits, fn, False, None])
        tok = (eng, len(self.q[eng]))
        self._commit(tok, reads, writes)
        return tok

    def dma(self, eng, out, in_, **kw):
        sb = out.buf
        if sb.dsem is None:
            sb.dsem = ("dma", len(self.dma_sems))
            self.dma_sems.append(sb.name)
        waits = self._deps(eng, [in_.buf], [out.buf])
        sb.dcnt += 16
        tok = (sb.dsem, sb.dcnt)
        a = dict(out=out.ap, in_=in_.ap, **kw)
        fn = lambda e, a=a: e.dma_start(**a)
        self.q[eng].append([waits, fn, False, sb.dsem])
        self._commit(tok, [in_.buf], [out.buf])
        return tok

    def finish(self, final_bufs):
        waits = self._deps("sync", final_bufs, [])
        self.q["sync"].append([waits, None, False, None])

    def emit(self):
        nc = self.nc
        st = self.stack
        esem = {e: st.enter_context(nc.semaphore("es_" + e)) for e in ENGS}
        dsem = [st.enter_context(nc.semaphore("ds%d" % i)) for i in range(len(self.dma_sems))]
        cum = {}
        for e in ENGS:
            c = 0
            arr = []
            for it in self.q[e]:
                if it[2]:
                    c += 1
                arr.append(c)
            cum[e] = arr

        def semval(k, v):
            if isinstance(k, str):
                return esem[k], cum[k][v - 1]
            return dsem[k[1]], v

        block = st.enter_context(nc.Block())

        def run(e, eng):
            for waits, fn, sig, dk in self.q[e]:
                for k, v in waits:
                    s, val = semval(k, v)
                    eng.wait_ge(s, val)
                if fn is None:
                    continue
                ins = fn(eng)
                if dk is not None:
                    ins.then_inc(dsem[dk[1]], 16)
                elif sig:
                    ins.then_inc(esem[e], 1)

        @block.tensor
        def _(eng):
            run("tensor", eng)

        @block.vector
        def _(eng):
            run("vector", eng)

        @block.scalar
        def _(eng):
            run("scalar", eng)

        @block.gpsimd
        def _(eng):
            run("gpsimd", eng)

        @block.sync
        def _(eng):
            run("sync", eng)


def _alibi_slopes(n):
    def pow2(m):
        start = 2.0 ** (-8.0 / m)
        return [start ** (i + 1) for i in range(m)]
    if math.log2(n).is_integer():
        s = pow2(n)
    else:
        p = 2 ** int(math.floor(math.log2(n)))
        s = pow2(p) + pow2(2 * p)[0::2][: n - p]
    return sorted(s, reverse=True)


(PV_SHM, PV_SCM, PV_GTM, PV_SHF, PV_SCF, PV_GTF, PV_GPM, PV_GQM, PV_GPF, PV_GQF,
 PV_MUR, PV_MUK, PV_MUV, PV_MUW, PV_MUA, PV_MUG, PV_W0, PV_A0, PV_KK, PV_KA, PV_RK,
 PV_LNW, PV_LNB, PV_C) = range(24)
NPV = 24

CB_ID = 0
CB_ONESBD = 128
CB_ONES = 256
CB_MT4 = 320
CB_ML4 = 832
CB_E1 = 1344
CB_EA2 = CB_E1 + 8 * 256
CB_EB2 = CB_EA2 + 2 * 8 * 64
CB_EA3 = CB_EB2 + 8 * 64
CB_EB3 = CB_EA3 + 8 * 8 * 16
NCB = CB_EB3 + 8 * 16
CF_MSK = 0
CF_VM = 256
CF_EPS = CF_VM + 128
CF_IDF = CF_EPS + 4
NCF = CF_IDF + 128

CH_RW = 0
CH_AT = 8
CH_PB = 17
CH_WOUT = 25
CH_FF = 27
CH_FO = 38
NCH = 44


def _host_consts():
    sl = np.asarray(_alibi_slopes(24), np.float64).reshape(3, 8)
    cb = np.zeros((128, NCB), np.float32)
    p = np.arange(128)
    cb[:, CB_ID:CB_ID + 128] = np.eye(128)
    cb[:, CB_ONESBD:CB_ONESBD + 128] = (p[:, None] // 64 == p[None, :] // 64)
    cb[:, CB_ONES:CB_ONES + 64] = 1.0
    same = (p[:, None] // 64 == p[None, :] // 64)
    su = same & (p[:, None] < p[None, :])
    iu = same & (p[:, None] <= p[None, :])
    slo = same & (p[:, None] > p[None, :])
    cb[:, CB_MT4:CB_MT4 + 512] = np.concatenate([su, iu, su, iu], 1)
    cb[:, CB_ML4:CB_ML4 + 512] = np.concatenate([slo] * 4, 1)
    k = p[:, None].astype(np.float64)
    q = p[None, :].astype(np.float64)
    for h in range(8):
        dpv = q - k + 128
        e_prev = np.where(dpv <= 128, np.exp(-sl[0, h] * dpv), 0.0)
        dcu = q - k
        e_cur = np.where(dcu >= 0, np.exp(-sl[0, h] * np.maximum(dcu, 0)), 0.0)
        cb[:, CB_E1 + h * 256: CB_E1 + h * 256 + 128] = e_prev
        cb[:, CB_E1 + h * 256 + 128: CB_E1 + h * 256 + 256] = e_cur
    i64 = np.arange(64)[None, :].astype(np.float64)
    for rot in range(2):
        for h in range(8):
            j = p // 64
            pp = (p % 64).astype(np.float64)
            a = ((rot - j - 1) % 2) + 1
            dl = 64.0 * a[:, None] + i64 - pp[:, None]
            e = np.where(dl <= 128, np.exp(-sl[1, h] * 4.0 * dl), 0.0)
            o = CB_EA2 + (rot * 8 + h) * 64
            cb[:, o:o + 64] = e
    for h in range(8):
        kk = np.arange(64)[:, None].astype(np.float64)
        dl = i64 - kk
        e = np.where(dl >= 0, np.exp(-sl[1, h] * 4.0 * np.maximum(dl, 0)), 0.0)
        o = CB_EB2 + h * 64
        cb[0:64, o:o + 64] = e
    i16 = np.arange(16)[None, :].astype(np.float64)
    for rot in range(8):
        for h in range(8):
            j = p // 16
            pp = (p % 16).astype(np.float64)
            a = ((rot - j - 1) % 8) + 1
            dl = 16.0 * a[:, None] + i16 - pp[:, None]
            e = np.where(dl <= 128, np.exp(-sl[2, h] * 16.0 * dl), 0.0)
            o = CB_EA3 + (rot * 8 + h) * 16
            cb[:, o:o + 16] = e
    for h in range(8):
        kk = np.arange(16)[:, None].astype(np.float64)
        dl = i16 - kk
        e = np.where(dl >= 0, np.exp(-sl[2, h] * 16.0 * np.maximum(dl, 0)), 0.0)
        o = CB_EB3 + h * 16
        cb[0:16, o:o + 16] = e
    return cb


def _host_cf(hh):
    cf = np.zeros((128, NCF), np.float32)
    m = np.ones((128, 256), np.float32)
    m[:, 0::64] = 0.0
    cf[:, CF_MSK:CF_MSK + 256] = m
    valid = lambda t: 0.0 if t < 0 else (1.0 if (hh == 1 or t >= PB0) else 0.0)
    p = np.arange(128)
    for t in range(NT):
        cf[:, CF_VM + t] = valid(t)
        cf[:, CF_VM + 32 + t] = valid(t - 1)
        j = p // 64
        a = ((t - j - 1) % 2) + 1
        cf[:, CF_VM + 64 + t] = [valid(t - aa) for aa in a]
        j = p // 16
        a = ((t - j - 1) % 8) + 1
        cf[:, CF_VM + 96 + t] = [valid(t - aa) for aa in a]
    cf[:, CF_EPS] = RMS_EPS
    cf[:, CF_EPS + 1] = GN_EPS
    cf[:, CF_EPS + 3] = 1.0
    cf[:, CF_IDF:CF_IDF + 128] = np.eye(128)
    return cf


def _fm(v):
    return np.ascontiguousarray(v.reshape(8, 128).T)


def _wchunk(w, cols):
    return w[:, cols].reshape(8, 128, -1).transpose(1, 0, 2)


class _Stop(Exception):
    pass


def build(nt=NT, dbg=None, dbg_tile=0, dbg_c=0, stop=None):
    nc = bass.Bass("TRN2", target_bir_lowering=False)
    with ExitStack() as st:
        S = Sched(nc, st)
        finals = []

        def CK(name):
            if stop == name:
                raise _Stop()

        def DBG(name, view, m=None, c=None):
            if not dbg or name not in dbg:
                return
            if m is not None and m != dbg_tile:
                return
            if c is not None and c != dbg_c:
                return
            shp = list(view.ap.shape)
            dd = S.dram("dbg_" + name, shp, view.ap.dtype, "ExternalOutput")
            S.dma("gpsimd", out=dd[:], in_=view)
            finals.append(dd)
        xv = S.dram("xv", [T, D], F32, "ExternalInput")
        pfm_d = S.dram("pfm", [128, NPV * 8], F32, "ExternalInput")
        wmod_d = S.dram("wmod", [24, 128, 2048], F32, "ExternalInput")
        wsrc = S.dram("wsrc", [NCH, 128, 4096], F32, "ExternalInput")
        l1_d = S.dram("l1", [128, 8 * 288], F32, "ExternalInput")
        l2_d = S.dram("l2", [128, 3 * 1024], F32, "ExternalInput")
        cb_d = S.dram("cbt", [128, NCB], F32, "ExternalInput")
        cf_d = S.dram("cft", [128, NCF], F32, "ExternalInput")
        y_d = S.dram("y", [T // 2, D], F32, "ExternalOutput")
        wscr_all = S.dram("wscr", [NCH, 128, 4096], BF16, "Internal")
        wscr = [Buf("wscr%d" % i, wscr_all.t[i]) for i in range(NCH)]

        cb = S.sb("cb", [128, NCB], BF16)
        cf = S.sb("cf", [128, NCF], F32)
        pf = S.sb("pf", [128, NPV * 8], F32)
        pd = S.sb("pd", [128, 12 * 8], F32)
        gmb = S.sb("gmb", [128, 1024], BF16)
        gfb = S.sb("gfb", [128, 1024], BF16)
        l1a = S.sb("l1a", [128, 8, 288], BF16)
        l1b = S.sb("l1b", [128, 8, 288], BF16)
        l2 = S.sb("l2", [128, 3, 1024], BF16)
        ring = [S.sb("ring%d" % i, [128, 4096], BF16) for i in range(NSLOT)]
        xt1 = S.sb("xt", [128, NSUB, 1024], F32)
        xt = [xt1, xt1]
        nb = S.sb("nb", [128, NSUB, 1024], BF16)
        junk = nb[:, 0, :]
        st4 = S.sb("st4", [128, 16], F32)
        hT = S.sb("hT", [128, 8, TT + 1], BF16)
        h2T = S.sb("h2T", [128, 8, TT], BF16)
        mixT = h2T
        Zf = S.sb("Zf", [128, 8, 64], F32)
        Zb = S.sb("Zb", [128, 8, 2, 64], BF16)
        hal = S.sb("hal", [128, 8, 3], F32)
        tp = [S.sb("tp%d" % i, [128, TT + 1], F32) if i != 7 else None for i in range(12)]
        tp[7] = tp[6]
        tv = lambda i: tp[i][:, 0:TT]
        pj = [tp[0], tp[0], tp[0]]
        tmpd = tv(1)
        rkv = [tv(2), tv(3), tv(4)]
        sw = tv(5); asig = tv(6); gg = tv(7); cs = tv(0); cm = tv(1)
        Ep = tv(8); En = tv(9); Em = tv(10); rinv = tv(1); kkb = tv(11); ff = tv(0)
        kmod = tv(5); bv = tv(1); bon = tv(6); yln = tv(9); ysq = tv(10)
        sqb = S.sb("sqb", [128, TT], BF16)
        ARs = [S.sb("AR%d" % i, [128, NSUB, 2, 128], BF16) for i in range(3)]
        Bts = [S.sb("Bt%d" % i, [128, TT], BF16) for i in range(2)]
        Kts = [S.sb("Kt%d" % i, [128, TT], BF16) for i in range(2)]
        vbfs = [S.sb("vbf%d" % i, [128, TT], BF16) for i in range(2)]
        Bpads = [S.sb("Bpad%d" % i, [128, NSUB, 2, 128], BF16) for i in range(2)]
        Kpads = [S.sb("Kpad%d" % i, [128, NSUB, 2, 128], BF16) for i in range(2)]
        Vtms = [S.sb("Vtm%d" % i, [128, NSUB, 128], BF16) for i in range(2)]
        AMs = [S.sb("AM%d" % i, [128, 4, 512], BF16) for i in range(2)]
        TTfs = [S.sb("TTf%d" % i, [128, 4, 128], BF16) for i in range(2)]
        gbs = [S.sb("gb%d" % i, [128, TT], BF16) for i in range(3)]
        pcss = [S.sb("pcs%d" % i, [128, 4], F32) for i in range(3)]
        ysqB = S.sb("ysqB", [128, TT], F32)
        L0 = S.sb("L0", [128, 4, 128], BF16)
        LP = [S.sb("LP%d" % i, [128, 4, 128], BF16) for i in range(2)]
        LT = [S.sb("LT%d" % i, [128, 4, 128], BF16) for i in range(2)]
        SS = [S.sb("SS%d" % i, [128, 4, 128], BF16) for i in range(2)]
        Xb = S.sb("Xb", [128, 128], BF16)
        Ub = S.sb("Ub", [128, 128], BF16)
        ztmp = S.sb("ztmp", [128, 64], F32)
        Ytm = S.sb("Ytm", [128, NSUB, 128], F32)
        ynb = S.sb("ynb", [128, NSUB, 128], BF16)
        gst = S.sb("gst", [128, 32], F32)
        lw = S.sb("lw", [128, TT], BF16)
        lga = S.sb("lga", [128, TT], BF16)
        lgb = S.sb("lgb", [32, TT], BF16)
        yfin = S.sb("yfin", [128, 8, TT], BF16)
        _nbf = nb[:].re("p s c -> p (s c)")
        Qa = [h2T[:, 0:4, :], h2T[:, 4:8, :], _nbf[:, 0:1024].re("p (c t) -> p c t", t=TT)]
        K1 = S.sb("K1", [128, 4, 128 + TT], BF16)
        V1 = S.sb("V1", [128, 3, 512], BF16)
        K2c = S.sb("K2c", [128, 4, TT], BF16)
        K2r = S.sb("K2r", [128, 4, 4, 128], BF16)
        V2c = S.sb("V2c", [64, 4, 128], BF16)
        V2r = S.sb("V2r", [128, 4, 512], BF16)
        K3c = S.sb("K3c", [128, 4, TT], BF16)
        K3r = S.sb("K3r", [128, 4, 16, 128], BF16)
        V3c = S.sb("V3c", [16, 16, 128], BF16)
        V3r = S.sb("V3r", [128, 16, 512], BF16)
        VF = _nbf[:, 1024:2048].re("p (c t) -> p c t", t=TT)
        pe = S.sb("pe", [128, 512], BF16)
        pp_ = S.sb("pp", [128, 512], BF16)
        peb = SubBuf(pe, 256, 256)
        ppb = SubBuf(pp_, 256, 256)
        accO = S.sb("accO", [64, TT], F32)
        accD = S.sb("accD", [64, TT], F32)
        oT = S.sb("oT", [64, 8, TT], BF16)
        sa = tv(2); sbb = tv(3); sg = tv(4); utmp = tv(10)
        actT = S.sb("actT", [128, 8, TT], BF16)
        VF2 = actT[:, 0:4, :]
        wst = xt1[:].re("p s c -> p (s c)")

        _b0 = S.ps("b0", [128, 512])
        _b4 = S.ps("b4", [128, 512])
        _bS = S.ps("bS", [128, 1024])
        B3 = S.ps("b3", [128, 512])
        B5 = S.ps("b5", [128, 512])
        B6 = S.ps("b6", [128, 512])
        _pT = S.ps("pT", [128, 1024], BF16)
        R0 = SubBuf(_b0, 0); R1 = SubBuf(_b0, 256)
        Q0 = SubBuf(_b4, 0); Q1 = SubBuf(_b4, 256)
        B1 = Buf("B1", _bS.t[:, 0:512]); B2 = Buf("B2", _bS.t[:, 512:1024])
        pTr = SubBuf(_pT, 0); pTa = SubBuf(_pT, 512)
        for b_ in (_b0, _b4, B1, B2, B3, B5, B6, _pT):
            b_.psum = True
        pC = B3
        trB = [View(B1, B1.t[:, :].bitcast(BF16)), View(B2, B2.t[:, :].bitcast(BF16))]
        trC = View(B3, B3.t[:, :].bitcast(BF16))
        trot = [View(_pT, _pT.t[:, 0:512]), View(B6, B6.t[:, :].bitcast(BF16)), View(B5, B5.t[:, :].bitcast(BF16))]
        prot = {"r": [_b0, B1, B2], "a": [_b4, B5, B6], "x": [_b0, _b4, B1, B2, B3, B6], "f": [_b0, _b4, B6]}
        prot_i = {"r": 0, "a": 0, "x": 0, "f": 0}

        def nextp(k="x"):
            prot_i[k] = (prot_i[k] + 1) % len(prot[k])
            return prot[k][prot_i[k]]

        mm = lambda **kw: S.op("tensor", "matmul", **kw)
        tr = lambda **kw: S.op("tensor", "transpose", **kw)
        act = lambda **kw: S.op("scalar", "activation", **kw)
        vec = lambda m, **kw: S.op("vector", m, **kw)
        gps = lambda m, **kw: S.op("gpsimd", m, **kw)

        def sigmoid_to(dst, src, nbias=None, scale=1.0):
            if nbias is None:
                act(out=dst, in_=src, func=AF.Exp, scale=-scale)
            else:
                act(out=dst, in_=src, func=AF.Exp, scale=-scale, bias=nbias)
            act(out=dst, in_=dst, func=AF.Ln, bias=one_c_for(dst))
            act(out=dst, in_=dst, func=AF.Exp, scale=-1.0)

        def one_c_for(v):
            lo = v.ap.base_partition()
            n = v.ap.partition_size()
            return cf[lo:lo + n, CF_EPS + 3:CF_EPS + 4]

        def rsqrt_to(dst, src, bias_ap, scale=1.0):
            act(out=dst, in_=src, func=AF.Ln, bias=bias_ap, scale=scale)
            act(out=dst, in_=dst, func=AF.Exp, scale=-0.5)

        ident = cb[:, CB_ID:CB_ID + 128]
        identf = cf[:, CF_IDF:CF_IDF + 128]
        onesbd = cb[:, CB_ONESBD:CB_ONESBD + 128]
        eps_r = cf[:, CF_EPS:CF_EPS + 1]
        eps_g = cf[:, CF_EPS + 1:CF_EPS + 2]
        zero_c = cf[:, CF_EPS + 2:CF_EPS + 3]
        one_c = cf[:, CF_EPS + 3:CF_EPS + 4]

        def pv(i, kc):
            return pf[:, i * 8 + kc: i * 8 + kc + 1]

        def pdv(i, kc):
            return pd[:, i * 8 + kc: i * 8 + kc + 1]
        PD_A1, PD_A2, PD_GM, PD_GF, PD_OMK, PD_OMR, PD_OMKm, PD_OMV = range(8)

        try:
            S.dma("gpsimd", out=cb[:, :], in_=cb_d[:, :])
            S.dma("sync", out=cf[:, :], in_=cf_d[:, :])
            S.dma("sync", out=pf[:, :], in_=pfm_d[:, :])
            for i in range(NCH):
                S.dma("gpsimd", out=wscr[i][:, :], in_=wsrc[i])
            CK('dma0')
            for b_ in (Zf, Zb, hal, Bpads[0], Bpads[1], Kpads[0], Kpads[1], K1, K2r, V2r, K3r, V3r, V1, hT, Xb, Ub, Vtms[0], Vtms[1]):
                gps("memset", ap=b_[:], constant=0.0)
            CK('memset')
            for half in range(2):
                S.dma("sync", out=wst[:, 0:4 * 288], in_=l1_d[:, half * 4 * 288:(half + 1) * 4 * 288])
                w1v = wst[:, 0:4 * 288].re("p (k c) -> p k c", c=288)
                for k4 in range(4):
                    kc = half * 4 + k4
                    for (lo, hi, mui) in ((0, 64, PV_MUW), (64, 128, PV_MUA), (128, 288, PV_MUG)):
                        vec("tensor_scalar", out=l1b[:, kc, lo:hi], in0=w1v[:, k4, lo:hi], scalar1=pv(mui, kc),
                            scalar2=None, op0=ALU.mult)
                        vec("tensor_tensor", out=l1a[:, kc, lo:hi], in0=w1v[:, k4, lo:hi], in1=l1b[:, kc, lo:hi],
                            op=ALU.subtract)
            for half in range(2):
                S.dma("sync", out=wst[:, 0:1536], in_=l2_d[:, half * 1536:(half + 1) * 1536])
                vec("tensor_copy", out=l2[:].re("p a c -> p (a c)")[:, half * 1536:(half + 1) * 1536], in_=wst[:, 0:1536])
            CK('lora0')
            for j in range(24):
                S.dma("sync", out=wst[:, 0:2048], in_=wmod_d[j])
                wv = wst[:, 0:2048].re("p (k c) -> p k c", c=256)
                for cc in range(2):
                    col = j * 2 + cc
                    for kc in range(8):
                        mm(out=B1[:, col:col + 1], lhsT=wv[:, kc, cc * 128:(cc + 1) * 128], rhs=pv(PV_C, kc),
                           start=(kc == 0), stop=(kc == 7))
            modf = S.sb("modf", [128, 48], F32)
            vec("tensor_tensor", out=modf[:, :], in0=B1[:, 0:48], in1=pf[:, 0:48], op=ALU.add)
            for kc in range(8):
                vec("scalar_tensor_tensor", out=pdv(PD_A1, kc), in0=modf[:, 8 + kc:9 + kc], scalar=1.0,
                    in1=pv(PV_GPM, kc), op0=ALU.add, op1=ALU.mult)
                vec("scalar_tensor_tensor", out=pdv(PD_A2, kc), in0=modf[:, 32 + kc:33 + kc], scalar=1.0,
                    in1=pv(PV_GPF, kc), op0=ALU.add, op1=ALU.mult)
                vec("tensor_tensor", out=pdv(PD_GM, kc), in0=modf[:, 16 + kc:17 + kc], in1=pv(PV_GQM, kc), op=ALU.mult)
                vec("tensor_tensor", out=pdv(PD_GF, kc), in0=modf[:, 40 + kc:41 + kc], in1=pv(PV_GQF, kc), op=ALU.mult)
                vec("tensor_scalar", out=pdv(PD_OMK, kc), in0=pv(PV_KA, kc), scalar1=-1.0, scalar2=1.0,
                    op0=ALU.mult, op1=ALU.add)
                vec("tensor_scalar", out=pdv(5, kc), in0=pv(PV_W0, kc), scalar1=-1.0, scalar2=None, op0=ALU.mult)
                vec("tensor_scalar", out=pdv(6, kc), in0=pv(PV_A0, kc), scalar1=-1.0, scalar2=None, op0=ALU.mult)
            dg = tp[0][:, 0:128]
            onesf = tp[1][:, 0:128]
            gps("memset", ap=onesf[:, :], constant=1.0)
            for (pdi, dst) in ((PD_GM, gmb), (PD_GF, gfb)):
                for kc in range(8):
                    vec("tensor_scalar", out=dg[:, :], in0=identf, scalar1=pdv(pdi, kc), scalar2=None, op0=ALU.mult)
                    pz = nextp()
                    mm(out=pz[:, 0:128], lhsT=onesf[:, :], rhs=dg[:, :], start=True, stop=True)
                    act(out=dst[:, kc * 128:(kc + 1) * 128], in_=pz[:, 0:128], func=AF.Copy)

            CK('startup')
            ring_i = [0]

            ring_sets = {"r": ring[0:2], "a": ring[2:4], "x": ring}
            ring_k = {"r": 0, "a": 0, "x": 0}

            def wload(ch, k="x"):
                s = ring_sets[k][ring_k[k] % len(ring_sets[k])]
                ring_k[k] += 1
                S.dma("sync", out=s[:, :], in_=wscr[ch][:, :])
                return s

            def rms_rstd(src3, dst_cols, nsub=NSUB):
                for sub in range(nsub):
                    act(out=junk[:, :], in_=src3[:, sub, :], func=AF.Square,
                        accum_out=st4[:, 8 + sub:9 + sub])
                rsqrt_to(st4[:, dst_cols:dst_cols + nsub], st4[:, 8:8 + nsub], eps_r, 1.0 / D)

            def norm_transpose(xsrc, rcol, dstT, a_idx, b_view_fn, halo):
                for sub in range(NSUB):
                    vec("tensor_scalar", out=nb[:, sub, :], in0=xsrc[:, sub, :], scalar1=st4[:, rcol + sub:rcol + sub + 1],
                        scalar2=None, op0=ALU.mult)
                for kc in range(8):
                    tgt = trot[kc % 3]
                    for sub in range(NSUB):
                        tr(out=tgt[:, sub * 128:(sub + 1) * 128], in_=nb[:, sub, kc * 128:(kc + 1) * 128], identity=ident)
                    act(out=dstT[:, kc, halo:halo + TT], in_=tgt[:, 0:TT], func=AF.Identity,
                        scale=pdv(a_idx, kc), bias=b_view_fn(kc))

            for m in range(nt):
                phaseB = m >= PB0
                prot["r"] = [_b0] if m >= PB0 - 8 else [_b0, _b4, B5, B6]
                xm = xt[m % 2]
                vcur = cf[:, CF_VM + m:CF_VM + m + 1]
                vprev = cf[:, CF_VM + 32 + m:CF_VM + 33 + m]
                vr2 = cf[:, CF_VM + 64 + m:CF_VM + 65 + m]
                vr3 = cf[:, CF_VM + 96 + m:CF_VM + 97 + m]
                S.dma("gpsimd", out=xm[:], in_=xv.v(xv.t[m * TT:(m + 1) * TT, :].rearrange("(s p) c -> p s c", p=128)))
                if m > 0:
                    vec("tensor_scalar", out=hT[:, :, 0:1], in0=hT[:, :, TT:TT + 1],
                        scalar1=cf[:, CF_VM + m - 1:CF_VM + m], scalar2=None, op0=ALU.mult)
                rms_rstd(xm, 0)
                norm_transpose(xm, 0, hT, PD_A1, lambda kc: pf[:, PV_SHM * 8 + kc:PV_SHM * 8 + kc + 1]
                               if False else modf[:, kc:kc + 1], 1)

                CK('stage1')
                pz = nextp()
                for kc in range(8):
                    mm(out=pz[:, 0:TT], lhsT=l1a[:, kc, 0:128], rhs=hT[:, kc, 1:TT + 1], start=(kc == 0), stop=False)
                    mm(out=pz[:, 0:TT], lhsT=l1b[:, kc, 0:128], rhs=hT[:, kc, 0:TT], start=False, stop=(kc == 7))
                sigmoid_to(tp[11][0:64, 0:TT], pz[0:64, 0:TT], None, 2.0)
                vec("tensor_scalar", out=lw[0:64, :], in0=tp[11][0:64, 0:TT], scalar1=2.0, scalar2=-1.0, op0=ALU.mult, op1=ALU.add)
                act(out=lw[64:128, :], in_=pz[64:128, 0:TT], func=AF.Copy)
                if phaseB:
                    pz = nextp()
                    for kc in range(8):
                        mm(out=pz[:, 0:TT], lhsT=l1a[:, kc, 128:256], rhs=hT[:, kc, 1:TT + 1], start=(kc == 0), stop=False)
                        mm(out=pz[:, 0:TT], lhsT=l1b[:, kc, 128:256], rhs=hT[:, kc, 0:TT], start=False, stop=(kc == 7))
                    sigmoid_to(tp[11][:, 0:TT], pz[:, 0:TT])
                    act(out=lga[:, :], in_=tp[11][:, 0:TT], func=AF.Copy)
                    pz = nextp()
                    for kc in range(8):
                        mm(out=pz[0:32, 0:TT], lhsT=l1a[:, kc, 256:288], rhs=hT[:, kc, 1:TT + 1], start=(kc == 0), stop=False)
                        mm(out=pz[0:32, 0:TT], lhsT=l1b[:, kc, 256:288], rhs=hT[:, kc, 0:TT], start=False, stop=(kc == 7))
                    sigmoid_to(tp[11][0:32, 0:TT], pz[0:32, 0:TT])
                    act(out=lgb[0:32, :], in_=tp[11][0:32, 0:TT], func=AF.Copy)

                CK('lora1')
                def rw_f1(c0):
                    for c in (c0,):
                        AR = ARs[c % 3]; AM = AMs[c % 2]; Vtm = Vtms[c % 2]; Bpad = Bpads[c % 2]; Kpad = Kpads[c % 2]
                        TTf = TTfs[c % 2]; gb = gbs[c % 3]; pcs = pcss[c % 3]
                        Bt = Bts[c % 2]; Kt = Kts[c % 2]; vbf = vbfs[c % 2]
                        csl = slice(c * 128, (c + 1) * 128)
                        wr = wload(CH_RW + c, 'r')
                        wrv = wr[:, :].re("p (k c) -> p k c", c=512)
                        for j in ((0, 1, 2) if m >= PB0 - 1 else (1, 2)):
                            pz = nextp("r")
                            for kc in range(8):
                                mm(out=pz[:, 0:TT], lhsT=wrv[:, kc, j * 128:(j + 1) * 128], rhs=hT[:, kc, 1:TT + 1],
                                   start=(kc == 0), stop=(kc == 7))
                            vec("tensor_copy", out=pj[j][:, 0:1], in_=hal[:, c, j:j + 1])
                            yield
                            act(out=pj[j][:, 1:TT + 1], in_=pz[:, 0:TT], func=AF.Copy)
                            yield
                            vec("tensor_scalar", out=hal[:, c, j:j + 1], in0=pj[j][:, TT:TT + 1], scalar1=vcur,
                                scalar2=None, op0=ALU.mult)
                            yield
                            vec("tensor_tensor", out=tmpd[:, :], in0=pj[j][:, 0:TT], in1=pj[j][:, 1:TT + 1], op=ALU.subtract)
                            yield
                            vec("scalar_tensor_tensor", out=rkv[j][:, :], in0=tmpd[:, :], scalar=pv(PV_MUR + j, c),
                                in1=pj[j][:, 1:TT + 1], op0=ALU.mult, op1=ALU.add)
                            yield
                        r_, k_, v_ = rkv
                        vec("tensor_scalar", out=v_[:, :], in0=v_[:, :], scalar1=vcur, scalar2=None, op0=ALU.mult)
                        yield
                        act(out=vbf[:, :], in_=v_[:, :], func=AF.Copy)
                        yield
                        pz = nextp("r")
                        mm(out=pz[:, 0:TT], lhsT=l2[0:64, 0, csl], rhs=lw[0:64, :], start=True, stop=True)
                        sigmoid_to(sw[:, :], pz[:, 0:TT], pdv(5, c))
                        yield
                        pz = nextp("r")
                        mm(out=pz[:, 0:TT], lhsT=l2[64:128, 0, csl], rhs=lw[64:128, :], start=True, stop=True)
                        sigmoid_to(asig[:, :], pz[:, 0:TT], pdv(6, c))
                        yield
                        pz = nextp("r")
                        if phaseB:
                            mm(out=pz[:, 0:TT], lhsT=l2[:, 1, csl], rhs=lga[:, :], start=True, stop=False)
                            mm(out=pz[:, 0:TT], lhsT=l2[0:32, 2, csl], rhs=lgb[0:32, :], start=False, stop=True)
                            act(out=gb[:, :], in_=pz[:, 0:TT], func=AF.Copy)
                        yield
                        vec("tensor_tensor_scan", out=cs[:, :], data0=cf[:, CF_MSK:CF_MSK + TT], data1=sw[:, :],
                            initial=0.0, op0=ALU.mult, op1=ALU.add)
                        yield
                        vec("tensor_tensor", out=cm[:, :], in0=cs[:, :], in1=sw[:, :], op=ALU.subtract)
                        yield
                        act(out=Ep[:, :], in_=cs[:, :], func=AF.Exp, scale=-C0)
                        yield
                        act(out=En[:, :], in_=cs[:, :], func=AF.Exp, scale=C0)
                        yield
                        act(out=Em[:, :], in_=cm[:, :], func=AF.Exp, scale=-C0)
                        yield
                        act(out=sqb[:, :], in_=k_[:, :], func=AF.Square, scale=pv(PV_KK, c))
                        yield
                        pz = nextp("r")
                        mm(out=pz[:, 0:TT], lhsT=onesbd, rhs=sqb[:, :], start=True, stop=True)
                        vec("tensor_scalar", out=rinv[:, :], in0=pz[:, 0:TT], scalar1=1e-18, scalar2=None, op0=ALU.max)
                        yield
                        act(out=rinv[:, :], in_=rinv[:, :], func=AF.Ln)
                        yield
                        act(out=rinv[:, :], in_=rinv[:, :], func=AF.Exp, scale=-0.5)
                        yield
                        vec("scalar_tensor_tensor", out=kkb[:, :], in0=k_[:, :], scalar=pv(PV_KK, c), in1=rinv[:, :],
                            op0=ALU.mult, op1=ALU.mult)
                        yield
                        vec("tensor_scalar", out=ff[:, :], in0=asig[:, :], scalar1=pv(PV_KA, c), scalar2=pdv(PD_OMK, c),
                            op0=ALU.mult, op1=ALU.add)
                        yield
                        vec("tensor_tensor", out=kmod[:, :], in0=k_[:, :], in1=ff[:, :], op=ALU.mult)
                        yield
                        vec("tensor_tensor", out=bv[:, :], in0=kkb[:, :], in1=asig[:, :], op=ALU.mult)
                        yield
                        vec("scalar_tensor_tensor", out=AR[:, :, 0, :], in0=kkb[:, :].re("p (s t) -> p s t", t=128), scalar=-1.0,
                            in1=Em[:, :].re("p (s t) -> p s t", t=128), op0=ALU.mult, op1=ALU.mult)
                        yield
                        if phaseB:
                            vec("tensor_tensor", out=AR[:, :, 1, :], in0=r_[:, :].re("p (s t) -> p s t", t=128),
                                in1=Ep[:, :].re("p (s t) -> p s t", t=128), op=ALU.mult)
                            yield
                        vec("tensor_tensor", out=Bt[:, :], in0=bv[:, :], in1=En[:, :], op=ALU.mult)
                        yield
                        vec("tensor_tensor", out=Kt[:, :], in0=kmod[:, :], in1=En[:, :], op=ALU.mult)
                        yield
                        if phaseB:
                            vec("tensor_tensor", out=tmpd[:, :], in0=r_[:, :], in1=kmod[:, :], op=ALU.mult)
                            yield
                            act(out=sqb[:, :], in_=tmpd[:, :], func=AF.Copy, scale=pv(PV_RK, c))
                            yield
                            pz = nextp("r")
                            mm(out=pz[:, 0:TT], lhsT=onesbd, rhs=sqb[:, :], start=True, stop=True)
                            vec("tensor_tensor", out=bon[:, :], in0=pz[:, 0:TT], in1=v_[:, :], op=ALU.mult)
                            yield
                            vec("tensor_tensor", out=yfin[:, c, :], in0=bon[:, :], in1=gb[:, :], op=ALU.mult)
                        yield
                        vec("tensor_copy", out=pcs[:, 0:4], in_=Ep[:, :].re("p (q t) -> p q t", t=64)[:, :, 63])
                        yield

                def rw_f2(c0):
                    for c in (c0,):
                        AR = ARs[c % 3]; AM = AMs[c % 2]; Vtm = Vtms[c % 2]; Bpad = Bpads[c % 2]; Kpad = Kpads[c % 2]
                        TTf = TTfs[c % 2]; gb = gbs[c % 3]; pcs = pcss[c % 3]
                        Bt = Bts[c % 2]; Kt = Kts[c % 2]; vbf = vbfs[c % 2]
                        for qi, (src, dst) in enumerate(((Bt, Bpad), (Kt, Kpad), (vbf, None))):
                            for sub in range(NSUB):
                                tr(out=trB[qi % 2][:, sub * 128:(sub + 1) * 128],
                                   in_=src[:, sub * 128:(sub + 1) * 128], identity=ident)
                            yield
                            srcv = trB[qi % 2][:, 0:256]
                            if dst is None:
                                act(out=Vtm[:].re("p s c -> p (s c)"), in_=srcv, func=AF.Copy)
                            else:
                                for h in range(2):
                                    act(out=dst[:, :, h, h * 64:(h + 1) * 64],
                                        in_=srcv.re("p (s c) -> p s c", c=128)[:, :, h * 64:(h + 1) * 64], func=AF.Copy)
                        CK('rwkv_a')
                        for h in range(2):
                            hs = slice(h * 64, (h + 1) * 64)
                            for sub in range(NSUB):
                                u = h * NSUB + sub
                                tsl = slice(sub * 128, (sub + 1) * 128)
                                pz = (B1, B2)[u % 2]
                                if phaseB:
                                    mm(out=pz[:, 0:256], lhsT=Bt[hs, tsl], rhs=AR[hs, sub, :, :].re("p a t -> p (a t)"),
                                       start=True, stop=True)
                                    mm(out=pz[:, 256:512], lhsT=Kt[hs, tsl], rhs=AR[hs, sub, :, :].re("p a t -> p (a t)"),
                                       start=True, stop=True)
                                    vec("tensor_tensor", out=AM[:, u, :], in0=pz[:, :], in1=cb[:, CB_MT4:CB_MT4 + 512], op=ALU.mult)
                                else:
                                    mm(out=pz[:, 0:128], lhsT=Bt[hs, tsl], rhs=AR[hs, sub, 0, :], start=True, stop=True)
                                    mm(out=pz[:, 256:384], lhsT=Kt[hs, tsl], rhs=AR[hs, sub, 0, :], start=True, stop=True)
                                    v4 = lambda ap_: ap_.re("p (a two b) -> p a two b", a=2, two=2)[:, :, 0, :]
                                    vec("tensor_tensor", out=v4(AM[:, u, :]), in0=v4(pz[:, :]),
                                        in1=v4(cb[:, CB_MT4:CB_MT4 + 512]), op=ALU.mult)
                                yield
                        for h in range(2):
                            hs = slice(h * 64, (h + 1) * 64)
                            for sub in range(NSUB):
                                u = h * NSUB + sub
                                tsl = slice(sub * 128, (sub + 1) * 128)
                                mm(out=B1[:, u * 128:(u + 1) * 128], lhsT=AR[hs, sub, 0, :], rhs=Bt[hs, tsl],
                                   start=True, stop=True)
                        vec("tensor_tensor", out=L0[:].re("p u t -> p (u t)"), in0=B1[:, :], in1=cb[:, CB_ML4:CB_ML4 + 512],
                            op=ALU.mult)
                        yield
                        CK('rwkv_b')
                        vec("tensor_tensor", out=SS[0][:], in0=AM[:, :, 0:128], in1=ident.bc(1, [128, 4, 128]), op=ALU.add)
                        lt_prev = lambda u: AM[:, u, 0:128]
                        lp_prev = lambda u: L0[:, u, :]
                        scur = 0
                        for lev in range(1, 6):
                            lpn = LP[lev % 2]
                            ltn = LT[lev % 2]
                            for u in range(4):
                                mm(out=B1[:, u * 128:(u + 1) * 128], lhsT=lt_prev(u), rhs=lp_prev(u), start=True, stop=True)
                            if lev <= 4:
                                for u in range(4):
                                    mm(out=B2[:, u * 128:(u + 1) * 128], lhsT=lp_prev(u), rhs=lt_prev(u),
                                       start=True, stop=True)
                            act(out=lpn[:].re("p u t -> p (u t)"), in_=B1[:, :], func=AF.Copy)
                            if lev <= 4:
                                act(out=ltn[:].re("p u t -> p (u t)"), in_=B2[:, :], func=AF.Copy)
                            yield
                            for u in range(4):
                                mm(out=B1[:, u * 128:(u + 1) * 128], lhsT=lpn[:, u, :], rhs=SS[scur][:, u, :], start=True, stop=True)
                            sdst = TTf if lev == 5 else SS[1 - scur]
                            vec("tensor_tensor", out=sdst[:].re("p u t -> p (u t)"), in0=B1[:, :],
                                in1=SS[scur][:].re("p u t -> p (u t)"), op=ALU.add)
                            scur = 1 - scur
                            lt_prev = (lambda b: (lambda u: b[:, u, :]))(ltn)
                            yield
                            lp_prev = (lambda b: (lambda u: b[:, u, :]))(lpn)

                def rw_back(c0):
                    for c in (c0,):
                        AR = ARs[c % 3]; AM = AMs[c % 2]; Vtm = Vtms[c % 2]; Bpad = Bpads[c % 2]; Kpad = Kpads[c % 2]
                        TTf = TTfs[c % 2]; gb = gbs[c % 3]; pcs = pcss[c % 3]
                        Bt = Bts[c % 2]; Kt = Kts[c % 2]; vbf = vbfs[c % 2]
                        TTm = TTf
                        ysq = ysqB[:, :]
                        yln = ysqB[:, :]
                        CK('rwkv_c')
                        for q in range(2 * NSUB):
                            sub, half = q // 2, q % 2
                            ps_ = slice(half * 64, half * 64 + 64)
                            tsl = slice(sub * 128, (sub + 1) * 128)
                            zi = q % 2
                            for h in range(2):
                                hs = slice(h * 64, (h + 1) * 64)
                                u = h * NSUB + sub
                                mm(out=pC[:, hs], lhsT=AR[hs, sub, 0, :], rhs=Zb[hs, c, zi, :], start=True, stop=False)
                                mm(out=pC[:, hs], lhsT=AM[:, u, 256:384], rhs=Vtm[:, sub, hs], start=False, stop=True)
                            act(out=Xb[ps_, :], in_=pC[ps_, 0:128], func=AF.Copy)
                            yield
                            CK('c1')
                            for h in range(2):
                                hs = slice(h * 64, (h + 1) * 64)
                                u = h * NSUB + sub
                                mm(out=pC[:, 128 + h * 64:128 + (h + 1) * 64], lhsT=TTm[ps_, u, :], rhs=Xb[ps_, hs],
                                   start=True, stop=True)
                            vec("tensor_copy", out=Ub[ps_, :], in_=pC[ps_, 128:256])
                            yield
                            CK('c2')
                            if phaseB:
                                for h in range(2):
                                    hs = slice(h * 64, (h + 1) * 64)
                                    u = h * NSUB + sub
                                    o_ = slice(256 + h * 64, 256 + (h + 1) * 64)
                                    mm(out=pC[:, o_], lhsT=AR[hs, sub, 1, :], rhs=Zb[hs, c, zi, :], start=True, stop=False)
                                    mm(out=pC[:, o_], lhsT=AM[:, u, 128:256], rhs=Ub[:, hs], start=False, stop=False)
                                    mm(out=pC[:, o_], lhsT=AM[:, u, 384:512], rhs=Vtm[:, sub, hs], start=False, stop=True)
                                act(out=Ytm[ps_, sub, :], in_=pC[ps_, 256:384], func=AF.Copy)
                                yield
                            CK('c3')
                            for h in range(2):
                                hs = slice(h * 64, (h + 1) * 64)
                                mm(out=pC[:, 384:448], lhsT=Bpad[ps_, sub, h, :], rhs=Ub[ps_, hs], start=(h == 0), stop=False)
                                mm(out=pC[:, 384:448], lhsT=Kpad[ps_, sub, h, :], rhs=Vtm[ps_, sub, hs], start=False, stop=(h == 1))
                            CK('c4')
                            pcv = pcs[:, q:q + 1]
                            vec("tensor_scalar", out=ztmp[:, :], in0=Zf[:, c, :], scalar1=pcv, scalar2=None, op0=ALU.mult)
                            vec("scalar_tensor_tensor", out=Zf[:, c, :], in0=pC[:, 384:448], scalar=pcv, in1=ztmp[:, :],
                                op0=ALU.mult, op1=ALU.add)
                            act(out=Zb[:, c, 1 - zi, :], in_=Zf[:, c, :], func=AF.Copy)
                            yield
                        CK('rwkv_d')
                        if phaseB:
                            yv = Ytm[:].re("p s (h i) -> p (s h) i", i=64)
                            vec("tensor_reduce", out=gst[:, 0:4], in_=yv, axis=AX.X, op=ALU.add)
                            act(out=ysq[:, :], in_=Ytm[:].re("p s c -> p (s c)"), func=AF.Square)
                            vec("tensor_reduce", out=gst[:, 4:8], in_=ysq[:, :].re("p (g i) -> p g i", i=64), axis=AX.X, op=ALU.add)
                            vec("tensor_scalar", out=gst[:, 8:12], in0=gst[:, 0:4], scalar1=1.0 / 64, scalar2=None, op0=ALU.mult)
                            vec("tensor_tensor", out=gst[:, 12:16], in0=gst[:, 8:12], in1=gst[:, 8:12], op=ALU.mult)
                            vec("scalar_tensor_tensor", out=gst[:, 16:20], in0=gst[:, 4:8], scalar=1.0 / 64, in1=gst[:, 12:16],
                                op0=ALU.mult, op1=ALU.subtract)
                            rsqrt_to(gst[:, 24:28], gst[:, 16:20], eps_g, 1.0)
                            ysv = ysq[:, :].re("p (g i) -> p g i", i=64)
                            vec("tensor_tensor", out=ysv, in0=yv, in1=gst[:, 8:12].bc(2, [128, 4, 64]), op=ALU.subtract)
                            vec("tensor_tensor", out=ynb[:].re("p s (h i) -> p (s h) i", i=64), in0=ysv,
                                in1=gst[:, 24:28].bc(2, [128, 4, 64]), op=ALU.mult)
                            yield
                            for sub in range(NSUB):
                                tr(out=trC[:, sub * 128:(sub + 1) * 128], in_=ynb[:, sub, :], identity=ident)
                            act(out=yln[:, :], in_=trC[:, 0:TT], func=AF.Identity, scale=pv(PV_LNW, c), bias=pv(PV_LNB, c))
                            vec("tensor_tensor", out=yln[:, :], in0=yln[:, :], in1=gb[:, :], op=ALU.mult)
                            vec("tensor_tensor", out=yfin[:, c, :], in0=yln[:, :], in1=yfin[:, c, :], op=ALU.add)
                            yield


                def th_attn():
                    if m < PB0 - 8:
                        return
                    j0_2 = m % 2
                    j0_3 = m % 8
                    for g in range(3):
                        kdst = (K1, K2c, K3c)[g]
                        for j in ((0, 1, 2) if phaseB else (1, 2)):
                            wa = wload(CH_AT + g * 3 + j, 'a')
                            wav = wa[:, :].re("p (k c) -> p k c", c=512)
                            for cc in range(4):
                                pz = nextp("a")
                                for kc in range(8):
                                    mm(out=pz[:, 0:TT], lhsT=wav[:, kc, cc * 128:(cc + 1) * 128], rhs=hT[:, kc, 1:TT + 1],
                                       start=(kc == 0), stop=(kc == 7))
                                if j == 0:
                                    act(out=Qa[g][:, cc, :], in_=pz[:, 0:TT], func=AF.Copy, scale=0.125)
                                elif j == 1:
                                    if g == 0:
                                        act(out=K1[:, cc, 128:128 + TT], in_=pz[:, 0:TT], func=AF.Copy)
                                    else:
                                        act(out=kdst[:, cc, :], in_=pz[:, 0:TT], func=AF.Copy)
                                else:
                                    act(out=VF[:, cc, :], in_=pz[:, 0:TT], func=AF.Copy)
                            yield
                        if g == 0:
                            for blk in range(2):
                                for cc in range(4):
                                    tr(out=pTa[:, cc * 128:(cc + 1) * 128], in_=VF[:, cc, blk * 128:(blk + 1) * 128], identity=ident)
                                vec("tensor_copy", out=V1[:, 1 + blk, :], in_=pTa[:, 0:512])
                            yield
                        elif g == 1:
                            vec("tensor_copy", out=VF2[:], in_=VF[:])
                    CK('attn_proj')
                    for h in range(8):
                        cc, hp = h // 2, (h % 2) * 64
                        hs = slice(hp, hp + 64)
                        vs = slice(h * 64, (h + 1) * 64)
                        vl = slice(hp, hp + 64)
                        if h % 2 == 0:
                            for r in range(4):
                                tr(out=pTa[0:64, r * 128:(r + 1) * 128],
                                   in_=VF2[:, cc, :].re("p (i r) -> p r i", r=4)[:, r, :], identity=ident)
                            vec("tensor_copy", out=V2c[0:64, :, :].re("p r c -> p (r c)"), in_=pTa[0:64, 0:512])
                            for r in range(16):
                                tr(out=pTa[0:16, (r % 4) * 128:(r % 4 + 1) * 128],
                                   in_=VF[:, cc, :].re("p (i r) -> p r i", r=16)[:, r, :], identity=ident)
                                if r % 4 == 3:
                                    vec("tensor_copy", out=V3c[0:16, r - 3:r + 1, :].re("p a c -> p (a c)"), in_=pTa[0:16, 0:512])
                                yield
                        if not phaseB:
                            if h % 2 == 1:
                                S.dma("gpsimd", out=V2r[j0_2 * 64:(j0_2 + 1) * 64, :, cc * 128:(cc + 1) * 128], in_=V2c[0:64, :, :])
                                S.dma("gpsimd", out=V3r[j0_3 * 16:(j0_3 + 1) * 16, :, cc * 128:(cc + 1) * 128], in_=V3c[0:16, :, :])
                            continue
                        for blk in range(2):
                            qv = Qa[0][hs, cc, blk * 128:(blk + 1) * 128]
                            mm(out=B5[:, (blk * 2) * 128:(blk * 2 + 1) * 128], lhsT=K1[hs, cc, blk * 128:(blk + 1) * 128],
                               rhs=qv, start=True, stop=True)
                            mm(out=B5[:, (blk * 2 + 1) * 128:(blk * 2 + 2) * 128],
                               lhsT=K1[hs, cc, 128 + blk * 128:128 + (blk + 1) * 128], rhs=qv, start=True, stop=True)
                        act(out=pe[:, :], in_=B5[:, 0:512], func=AF.Exp)
                        yield
                        vec("tensor_tensor", out=pp_[:, :].re("p (b e) -> p b e", b=2), in0=pe[:, :].re("p (b e) -> p b e", b=2),
                            in1=cb[:, CB_E1 + h * 256:CB_E1 + (h + 1) * 256].bc(1, [128, 2, 256]), op=ALU.mult)
                        vec("tensor_scalar", out=pp_[:, 0:128], in0=pp_[:, 0:128], scalar1=vprev, scalar2=None, op0=ALU.mult)
                        yield
                        for blk in range(2):
                            mm(out=B6[0:64, blk * 128:(blk + 1) * 128], lhsT=V1[:, blk, vs],
                               rhs=pp_[:, (blk * 2) * 128:(blk * 2 + 1) * 128], start=True, stop=False)
                            mm(out=B6[0:64, blk * 128:(blk + 1) * 128], lhsT=V1[:, blk + 1, vs],
                               rhs=pp_[:, (blk * 2 + 1) * 128:(blk * 2 + 2) * 128], start=False, stop=True)
                        ppv = pp_[:, :].re("p (b c q) -> p b c q", b=2, c=2)
                        mm(out=B6[0:64, 256:512], lhsT=cb[:, CB_ONES:CB_ONES + 64], rhs=ppv[:, :, 0, :], start=True, stop=False)
                        mm(out=B6[0:64, 256:512], lhsT=cb[:, CB_ONES:CB_ONES + 64], rhs=ppv[:, :, 1, :], start=False, stop=True)
                        act(out=accO[:, :], in_=B6[0:64, 0:TT], func=AF.Copy)
                        act(out=accD[:, :], in_=B6[0:64, 256:512], func=AF.Copy)
                        yield
                        for r in range(4):
                            qv = Qa[1][hs, cc, :].re("p (i r) -> p r i", r=4)[:, r, :]
                            mm(out=B5[:, r * 64:(r + 1) * 64], lhsT=K2r[hs, cc, r, :], rhs=qv, start=True, stop=True)
                            mm(out=B5[0:64, 256 + r * 64:256 + (r + 1) * 64],
                               lhsT=K2c[hs, cc, :].re("p (i r) -> p r i", r=4)[:, r, :], rhs=qv, start=True, stop=True)
                        act(out=pe[:, 0:256], in_=B5[:, 0:256], func=AF.Exp)
                        act(out=peb[0:64, :], in_=B5[0:64, 256:512], func=AF.Exp)
                        yield
                        ea = cb[:, CB_EA2 + (j0_2 * 8 + h) * 64:CB_EA2 + (j0_2 * 8 + h + 1) * 64]
                        vec("scalar_tensor_tensor", out=pp_[:, 0:256].re("p (r i) -> p r i", r=4),
                            in0=pe[:, 0:256].re("p (r i) -> p r i", r=4), scalar=vr2, in1=ea.bc(1, [128, 4, 64]),
                            op0=ALU.mult, op1=ALU.mult)
                        eb = cb[0:64, CB_EB2 + h * 64:CB_EB2 + (h + 1) * 64]
                        vec("tensor_tensor", out=ppb[0:64, :].re("p (r i) -> p r i", r=4),
                            in0=peb[0:64, :].re("p (r i) -> p r i", r=4), in1=eb.bc(1, [64, 4, 64]), op=ALU.mult)
                        yield
                        for r in range(4):
                            mm(out=B6[0:64, r * 64:(r + 1) * 64], lhsT=V2r[:, r, vs], rhs=pp_[:, r * 64:(r + 1) * 64],
                               start=True, stop=False)
                            mm(out=B6[0:64, r * 64:(r + 1) * 64], lhsT=V2c[0:64, r, vl], rhs=ppb[0:64, r * 64:(r + 1) * 64],
                               start=False, stop=True)
                        mm(out=B6[0:64, 256:512], lhsT=cb[:, CB_ONES:CB_ONES + 64], rhs=pp_[:, 0:256], start=True, stop=False)
                        mm(out=B6[0:64, 256:512], lhsT=cb[0:64, CB_ONES:CB_ONES + 64], rhs=ppb[0:64, :], start=False, stop=True)
                        vec("tensor_tensor", out=accO[:, :].re("p (i r) -> p r i", r=4), in0=accO[:, :].re("p (i r) -> p r i", r=4),
                            in1=B6[0:64, 0:TT].re("p (r i) -> p r i", r=4), op=ALU.add)
                        vec("tensor_tensor", out=accD[:, :].re("p (i r) -> p r i", r=4), in0=accD[:, :].re("p (i r) -> p r i", r=4),
                            in1=B6[0:64, 256:512].re("p (r i) -> p r i", r=4), op=ALU.add)
                        yield
                        for r in range(16):
                            qv = Qa[2][hs, cc, :].re("p (i r) -> p r i", r=16)[:, r, :]
                            mm(out=B5[:, r * 16:(r + 1) * 16], lhsT=K3r[hs, cc, r, :], rhs=qv, start=True, stop=True)
                            mm(out=B5[0:16, 256 + r * 16:256 + (r + 1) * 16],
                               lhsT=K3c[hs, cc, :].re("p (i r) -> p r i", r=16)[:, r, :], rhs=qv, start=True, stop=True)
                        act(out=pe[:, 0:256], in_=B5[:, 0:256], func=AF.Exp)
                        act(out=peb[0:16, :], in_=B5[0:16, 256:512], func=AF.Exp)
                        yield
                        ea = cb[:, CB_EA3 + (j0_3 * 8 + h) * 16:CB_EA3 + (j0_3 * 8 + h + 1) * 16]
                        vec("scalar_tensor_tensor", out=pp_[:, 0:256].re("p (r i) -> p r i", r=16),
                            in0=pe[:, 0:256].re("p (r i) -> p r i", r=16), scalar=vr3, in1=ea.bc(1, [128, 16, 16]),
                            op0=ALU.mult, op1=ALU.mult)
                        eb = cb[0:16, CB_EB3 + h * 16:CB_EB3 + (h + 1) * 16]
                        vec("tensor_tensor", out=ppb[0:16, :].re("p (r i) -> p r i", r=16),
                            in0=peb[0:16, :].re("p (r i) -> p r i", r=16), in1=eb.bc(1, [16, 16, 16]), op=ALU.mult)
                        yield
                        for r in range(16):
                            mm(out=B6[0:64, r * 16:(r + 1) * 16], lhsT=V3r[:, r, vs], rhs=pp_[:, r * 16:(r + 1) * 16],
                               start=True, stop=False)
                            mm(out=B6[0:64, r * 16:(r + 1) * 16], lhsT=V3c[0:16, r, vl], rhs=ppb[0:16, r * 16:(r + 1) * 16],
                               start=False, stop=True)
                        mm(out=B6[0:64, 256:512], lhsT=cb[:, CB_ONES:CB_ONES + 64], rhs=pp_[:, 0:256], start=True, stop=False)
                        mm(out=B6[0:64, 256:512], lhsT=cb[0:16, CB_ONES:CB_ONES + 64], rhs=ppb[0:16, :], start=False, stop=True)
                        yield
                        if phaseB:
                            vec("tensor_tensor", out=accO[:, :].re("p (i r) -> p r i", r=16),
                                in0=accO[:, :].re("p (i r) -> p r i", r=16),
                                in1=B6[0:64, 0:TT].re("p (r i) -> p r i", r=16), op=ALU.add)
                            vec("tensor_tensor", out=accD[:, :].re("p (i r) -> p r i", r=16),
                                in0=accD[:, :].re("p (i r) -> p r i", r=16),
                                in1=B6[0:64, 256:512].re("p (r i) -> p r i", r=16), op=ALU.add)
                            vec("reciprocal", out=accD[:, :], in_=accD[:, :])
                            vec("tensor_tensor", out=oT[:, h, :], in0=accO[:, :], in1=accD[:, :], op=ALU.mult)
                        if h % 2 == 1:
                            S.dma("gpsimd", out=V2r[j0_2 * 64:(j0_2 + 1) * 64, :, cc * 128:(cc + 1) * 128], in_=V2c[0:64, :, :])
                            S.dma("gpsimd", out=V3r[j0_3 * 16:(j0_3 + 1) * 16, :, cc * 128:(cc + 1) * 128], in_=V3c[0:16, :, :])
                    CK('attn')
                    vec("tensor_copy", out=K1[:, :, 0:128], in_=K1[:, :, TT:TT + 128])
                    vec("tensor_copy", out=V1[:, 0, :], in_=V1[:, 2, :])
                    vec("tensor_copy", out=K2r[:, :, :, j0_2 * 64:(j0_2 + 1) * 64],
                        in_=K2c[:].re("p c (i r) -> p c r i", r=4))
                    vec("tensor_copy", out=K3r[:, :, :, j0_3 * 16:(j0_3 + 1) * 16],
                        in_=K3c[:].re("p c (i r) -> p c r i", r=16))

                ag = th_attn()
                ag_done = [False]

                def step_attn():
                    if ag_done[0]:
                        return
                    try:
                        next(ag)
                    except StopIteration:
                        ag_done[0] = True

                def run_rr(gens):
                    gens = list(gens)
                    while gens:
                        for g_ in list(gens):
                            try:
                                next(g_)
                            except StopIteration:
                                gens.remove(g_)
                        if INTERLEAVE:
                            step_attn()

                for k_ in range(10):
                    gens = []
                    if k_ < 8:
                        gens.append(rw_f1(k_))
                    if 1 <= k_ <= 8:
                        gens.append(rw_f2(k_ - 1))
                    if k_ >= 2:
                        gens.append(rw_back(k_ - 2))
                    if INTERLEAVE:
                        run_rr(gens)
                    else:
                        for g_ in reversed(gens):
                            for _ in g_:
                                pass
                while not ag_done[0]:
                    step_attn()

                if not phaseB:
                    continue
                for cc in range(8):
                    sa, sbb = ((tv(2), tv(3)), (tv(5), tv(6)))[cc % 2]
                    wpb = wload(CH_PB + cc)
                    wv = wpb[:, :].re("p (a k c) -> p a k c", a=4, c=128)
                    pz = nextp()
                    for kc in range(8):
                        mm(out=pz[:, 0:TT], lhsT=wv[:, 0, kc, :], rhs=hT[:, kc, 1:TT + 1], start=(kc == 0), stop=(kc == 7))
                    sigmoid_to(sa[:, :], pz[:, 0:TT])
                    pz = nextp()
                    for kc in range(8):
                        mm(out=pz[:, 0:TT], lhsT=wv[:, 1, kc, :], rhs=hT[:, kc, 1:TT + 1], start=(kc == 0), stop=(kc == 7))
                    sigmoid_to(sbb[:, :], pz[:, 0:TT])
                    pz = nextp()
                    for kc in range(8):
                        mm(out=pz[:, 0:TT], lhsT=wv[:, 2, kc, :], rhs=yfin[:, kc, :], start=(kc == 0), stop=(kc == 7))
                    vec("tensor_tensor", out=sa[:, :], in0=sa[:, :], in1=pz[:, 0:TT], op=ALU.mult)
                    pz = nextp()
                    for hh_ in range(8):
                        mm(out=pz[:, 0:TT], lhsT=wv[0:64, 3, hh_, :], rhs=oT[:, hh_, :], start=(hh_ == 0), stop=(hh_ == 7))
                    vec("tensor_tensor", out=sbb[:, :], in0=sbb[:, :], in1=pz[:, 0:TT], op=ALU.mult)
                    vec("tensor_tensor", out=mixT[:, cc, :], in0=sa[:, :], in1=sbb[:, :], op=ALU.add)

                def norm_residual(ps_views, gb):
                    for hf in range(2):
                        act(out=junk[:, 0:512], in_=ps_views[hf], func=AF.Square, accum_out=st4[:, 8 + hf:9 + hf])
                    vec("tensor_tensor", out=st4[:, 10:11], in0=st4[:, 8:9], in1=st4[:, 9:10], op=ALU.add)
                    rsqrt_to(st4[:, 4:5], st4[:, 10:11], eps_r, 1.0 / D)
                    for hf in range(2):
                        for qq in range(2):
                            cs_ = slice(hf * 512 + qq * 256, hf * 512 + (qq + 1) * 256)
                            vec("scalar_tensor_tensor", out=utmp[:, :], in0=ps_views[hf][:, qq * 256:(qq + 1) * 256],
                                scalar=st4[:, 4:5], in1=gb[:, cs_], op0=ALU.mult, op1=ALU.mult)
                            vec("tensor_tensor", out=xm[:, sub, cs_], in0=xm[:, sub, cs_], in1=utmp[:, :], op=ALU.add)

                wo = [wload(CH_WOUT + 0), wload(CH_WOUT + 1)]
                for sub in range(NSUB):
                    for hf in range(2):
                        wv = wo[hf][:, :].re("p (k c) -> p k c", c=512)
                        for kc in range(8):
                            mm(out=(B1, B2)[hf][:, :], lhsT=mixT[:, kc, sub * 128:(sub + 1) * 128], rhs=wv[:, kc, :],
                               start=(kc == 0), stop=(kc == 7))
                    norm_residual([B1[:, :], B2[:, :]], gmb)
                rms_rstd(xm, 2)
                norm_transpose(xm, 2, h2T, PD_A2, lambda kc: modf[:, 24 + kc:25 + kc], 0)
                accs = [[B1[:, :], B2[:, :]], [B3[:, :], B5[:, :]]]
                for pg in range(3):
                    nk = 8 if pg < 2 else 6
                    for i4 in range(nk // 2):
                        i = pg * 4 + i4
                        wf_ = wload(CH_FF + i)
                        wv = wf_[:, :].re("p (k c) -> p k c", c=512)
                        for jj in range(2):
                            jl = i4 * 2 + jj
                            pg_ = nextp("f")
                            for kc in range(8):
                                mm(out=pg_[:, 0:TT], lhsT=wv[:, kc, jj * 128:(jj + 1) * 128], rhs=h2T[:, kc, :],
                                   start=(kc == 0), stop=(kc == 7))
                            act(out=sg[:, :], in_=pg_[:, 0:TT], func=AF.Silu)
                            pu = nextp("f")
                            for kc in range(8):
                                mm(out=pu[:, 0:TT], lhsT=wv[:, kc, 256 + jj * 128:256 + (jj + 1) * 128], rhs=h2T[:, kc, :],
                                   start=(kc == 0), stop=(kc == 7))
                            vec("tensor_tensor", out=actT[:, jl, :], in0=sg[:, :], in1=pu[:, 0:TT], op=ALU.mult)
                    for hf in range(2):
                        wf_ = wload(CH_FO + pg * 2 + hf)
                        wv = wf_[:, :].re("p (k c) -> p k c", c=512)
                        for sub in range(NSUB):
                            for kc in range(nk):
                                mm(out=accs[sub][hf], lhsT=actT[:, kc, sub * 128:(sub + 1) * 128], rhs=wv[:, kc, :],
                                   start=(pg == 0 and kc == 0), stop=(pg == 2 and kc == nk - 1))
                for sub in range(NSUB):
                    norm_residual(accs[sub], gfb)
                r0 = (m - PB0) * TT
                S.dma("gpsimd", out=y_d.v(y_d.t[r0:r0 + TT, :].rearrange("(s p) c -> p s c", p=128)), in_=xm[:])


        except _Stop:
            pass
        S.finish([y_d] + finals)
        S.emit()
    return nc


_CACHE = {}


def prep_inputs(x, c, w_mod, b_mod, g_pre_mix, g_post_mix, g_pre_ffn, g_post_ffn, w_in, mu_rkv, mu_lora,
           w0, w1, w2, a0, a1, a2, g1, g2, k_k, k_a, r_k, ln_x_w, ln_x_b, w_o_rwkv, w_o_attn, w_out,
           w_ffn_in, w_ffn_out):
    f = lambda a: np.asarray(a, np.float32)
    x = f(x); c = f(c)
    w_in = f(w_in)[0]; w_modm = f(w_mod)[0]
    bm = f(b_mod)[0].reshape(6, 1024)
    vecs = [bm[0], bm[1], bm[2], bm[3], bm[4], bm[5], f(g_pre_mix)[0], f(g_post_mix)[0], f(g_pre_ffn)[0],
            f(g_post_ffn)[0], f(mu_rkv)[0, 0], f(mu_rkv)[0, 1], f(mu_rkv)[0, 2], f(mu_lora)[0, 0], f(mu_lora)[0, 1],
            f(mu_lora)[0, 2], f(w0)[0], f(a0)[0], f(k_k)[0], f(k_a)[0], f(r_k)[0].reshape(-1), f(ln_x_w)[0],
            f(ln_x_b)[0]]
    wsrc = np.zeros((NCH, 128, 4096), np.float32)
    def put(i, arr3):
        P, K, C = arr3.shape
        v = wsrc[i].reshape(128, -1)
        tmp = np.zeros((128, K, 4096 // K if K in (8,) else C), np.float32) if False else None
        blk = np.zeros((128, K * C), np.float32)
        blk[:P] = arr3.reshape(P, K * C)
        v[:, :K * C] = blk
    for cch in range(8):
        a = np.zeros((128, 8, 512), np.float32)
        for j in range(3):
            a[:, :, j * 128:(j + 1) * 128] = _wchunk(w_in, slice(j * 1024 + cch * 128, j * 1024 + (cch + 1) * 128))
        put(CH_RW + cch, a)
    for g in range(3):
        for j in range(3):
            o = 3072 + j * 1536 + g * 512
            put(CH_AT + g * 3 + j, _wchunk(w_in, slice(o, o + 512)))
    wor = f(w_o_rwkv)[0]; woa = f(w_o_attn)[0]; wout = f(w_out)[0]
    for cc in range(8):
        cs_ = slice(cc * 128, (cc + 1) * 128)
        a = np.zeros((128, 4, 8, 128), np.float32)
        a[:, 0] = _wchunk(w_in, slice(7680 + cc * 128, 7680 + (cc + 1) * 128))
        a[:, 1] = _wchunk(w_in, slice(8704 + cc * 128, 8704 + (cc + 1) * 128))
        a[:, 2] = _wchunk(wor, cs_)
        a[0:64, 3] = woa[:, cs_].reshape(8, 64, 128).transpose(1, 0, 2)
        put(CH_PB + cc, a.reshape(128, 32, 128))
    for hf in range(2):
        put(CH_WOUT + hf, _wchunk(wout, slice(hf * 512, (hf + 1) * 512)))
    wfi = f(w_ffn_in)[0]; wfo = f(w_ffn_out)[0]
    for i in range(11):
        a = np.zeros((128, 8, 512), np.float32)
        a[:, :, 0:256] = _wchunk(wfi, slice(i * 256, (i + 1) * 256))
        a[:, :, 256:512] = _wchunk(wfi, slice(FH + i * 256, FH + (i + 1) * 256))
        put(CH_FF + i, a)
    for pg in range(3):
        nk = 8 if pg < 2 else 6
        for hf in range(2):
            blk = wfo[pg * 1024:pg * 1024 + nk * 128, hf * 512:(hf + 1) * 512]
            put(CH_FO + pg * 2 + hf, blk.reshape(nk, 128, 512).transpose(1, 0, 2))
    wmod = np.ascontiguousarray(
        w_modm.reshape(8, 128, 24, 256).transpose(2, 1, 0, 3).reshape(24, 128, 2048))
    l1 = np.concatenate([f(w1)[0], f(a1)[0], f(g1)[0]], 1)
    l1 = np.ascontiguousarray(l1.reshape(8, 128, 288).transpose(1, 0, 2).reshape(128, 8 * 288))
    l2 = np.zeros((128, 3, 1024), np.float32)
    l2[0:64, 0] = f(w2)[0]; l2[64:128, 0] = f(a2)[0]
    l2[:, 1] = f(g2)[0][0:128]; l2[0:32, 2] = f(g2)[0][128:160]
    l2 = l2.reshape(128, 3072)
    cbt = _host_consts()
    in_maps = []
    for core in range(8):
        b, hh = core // 2, core % 2
        pfm = np.concatenate([_fm(v) for v in vecs] + [_fm(c[b])], 1)
        if hh == 1:
            xvv = x[b]
        else:
            xvv = np.concatenate([np.zeros((T // 2, D), np.float32), x[b, :T // 2]], 0)
        in_maps.append({"xv": np.ascontiguousarray(xvv), "pfm": np.ascontiguousarray(pfm), "wmod": wmod,
                        "wsrc": wsrc, "l1": l1, "l2": l2, "cbt": cbt, "cft": _host_cf(hh)})
    return in_maps


def kernel(**inputs):
    in_maps = prep_inputs(**inputs)
    if "nc" not in _CACHE:
        _CACHE["nc"] = build()
    nc = _CACHE["nc"]
    res = run_bass_kernel_spmd(nc, in_maps, core_ids=list(range(8)))
    out = np.zeros((4, T, D), np.float32)
    for core in range(8):
        b, hh = core // 2, core % 2
        out[b, hh * (T // 2):(hh + 1) * (T // 2)] = res.results[core]["y"]
    return out
```

```python
import math
from contextlib import ExitStack

import numpy as np
import concourse.bass as bass
import concourse.mybir as mybir
from concourse.bass_utils import run_bass_kernel_spmd

F32 = mybir.dt.float32
BF16 = mybir.dt.bfloat16
AF = mybir.ActivationFunctionType
ALU = mybir.AluOpType
AX = mybir.AxisListType

ENGS = ("tensor", "vector", "scalar", "gpsimd", "sync")

T = 8192
D = 1024
TT = 256
NT = T // TT
PB0 = NT // 2
NSUB = TT // 128
FH = 2816
C0 = math.exp(-0.5)
GN_EPS = 64e-5
RMS_EPS = 1e-6
NSLOT = 4
import os
INTERLEAVE = os.environ.get('NOIL') is None
ATTN_EVERY = int(os.environ.get('ATTN_EVERY', '3'))
GPS_OFF = os.environ.get('GPS_OFF', '0') == '1'
LISTSCHED = os.environ.get('LISTSCHED', '1') == '1'


class Buf:
    def __init__(self, name, t):
        self.name = name
        self.t = t
        self.writer = None
        self.readers = []
        self.dsem = None
        self.dcnt = 0
        self.psum = False

    def __getitem__(self, idx):
        return View(self, self.t[idx])

    def v(self, ap):
        return View(self, ap)


class SubBuf:
    def __init__(self, buf, col0, ncols=None):
        self.buf = buf
        self.col0 = col0
        self.ncols = ncols

    def __getitem__(self, idx):
        ps, cs = idx
        a = 0 if cs.start is None else cs.start
        e = cs.stop if cs.stop is not None else self.ncols
        assert e is not None
        return View(self.buf, self.buf.t[ps, self.col0 + a:self.col0 + e])


class View:
    def __init__(self, buf, ap):
        self.buf = buf
        self.ap = ap

    def __getitem__(self, idx):
        return View(self.buf, self.ap[idx])

    def re(self, pat, **kw):
        return View(self.buf, self.ap.rearrange(pat, **kw))

    def bc(self, axis, shape):
        return View(self.buf, self.ap.unsqueeze(axis).to_broadcast(list(shape)))


def _unw(x):
    return x.ap if isinstance(x, View) else x


class Sched:
    def __init__(self, nc, stack):
        self.nc = nc
        self.stack = stack
        self.q = {e: [] for e in ENGS}
        self.waited = {e: {} for e in ENGS}
        self.dma_sems = []
        self.fin = {}
        self.eng_free = {e: 0.0 for e in ENGS}
        self.cur_fin = 0.0

    def _est(self, eng, tok, deps_tokens, dur):
        ready = 0.0
        for t_ in deps_tokens:
            f_ = self.fin.get(t_)
            if f_ is not None and f_ > ready:
                ready = f_
        start = max(ready + 150.0, self.eng_free[eng])
        fin = start + dur
        self.eng_free[eng] = fin if eng != "sync" and eng != "gpsimd" else start + 60.0
        self.fin[tok] = fin
        if len(self.fin) > 60000:
            ks = list(self.fin.keys())[:30000]
            for k_ in ks:
                del self.fin[k_]
        if fin > self.cur_fin:
            self.cur_fin = fin

    def sb(self, name, shape, dt):
        t = self.stack.enter_context(self.nc.sbuf_tensor("s_" + name, list(shape), dt))
        return Buf(name, t)

    def ps(self, name, shape, dt=F32):
        t = self.stack.enter_context(self.nc.psum_tensor("p_" + name, list(shape), dt))
        return Buf(name, t)

    def dram(self, name, shape, dt, kind):
        t = self.nc.dram_tensor(name, list(shape), dt, kind=kind).ap()
        return Buf(name, t)

    def _deps(self, eng, reads, writes):
        deps = {}

        def add(tok):
            if tok is None:
                return
            k, v = tok
            if deps.get(k, 0) < v:
                deps[k] = v

        for b in reads:
            add(b.writer)
            if b.psum:
                for r in b.readers:
                    if r[0] != eng:
                        add(r)
        for b in writes:
            add(b.writer)
            for r in b.readers:
                add(r)
        waits = []
        for k, v in deps.items():
            if k == "tensor" and eng == "tensor":
                continue
            if self.waited[eng].get(k, 0) >= v:
                continue
            self.waited[eng][k] = v
            waits.append((k, v))
            if isinstance(k, str):
                self.q[k][v - 1][2] = True
        return waits

    def _commit(self, tok, reads, writes):
        for b in writes:
            b.writer = tok
            b.readers = []
        for b in reads:
            if b in writes:
                continue
            b.readers.append(tok)
            if len(b.readers) > 48:
                d = {}
                for k, v in b.readers:
                    if d.get(k, 0) < v:
                        d[k] = v
                b.readers = list(d.items())

    def op(self, eng, meth, **kw):
        writes, reads = [], []
        for k, v in kw.items():
            if isinstance(v, View):
                if k in ("out", "accum_out", "ap"):
                    if v.buf not in writes:
                        writes.append(v.buf)
                else:
                    if v.buf not in reads:
                        reads.append(v.buf)
        dep_toks = [b_.writer for b_ in reads + writes if b_.writer is not None]
        for b_ in writes:
            dep_toks.extend(b_.readers)
        waits = self._deps(eng, reads, writes)
        if eng == "tensor":
            src = kw.get("lhsT", kw.get("in_"))
            lo = src.ap.base_partition()
            rows = (lo, lo + src.ap.partition_size())
            ob = kw["out"].buf
            prev = getattr(ob, "pe_rows", None)
            if prev is not None and ob.writer is not None and ob.writer[0] == "tensor" and \
                    (rows[1] <= prev[0] or prev[1] <= rows[0]):
                k, v = ob.writer
                if self.waited[eng].get(k, 0) < v:
                    self.waited[eng][k] = v
                    waits.append((k, v))
                    self.q[k][v - 1][2] = True
            ob.pe_rows = rows
        args = {k: _unw(v) for k, v in kw.items()}
        fn = lambda e, m=meth, a=args: getattr(e, m)(**a)
        self.q[eng].append([waits, fn, False, None])
        tok = (eng, len(self.q[eng]))
        o_ = kw.get("out", kw.get("ap"))
        try:
            fsz = o_.ap.free_size()
        except Exception:
            fsz = 256
        if eng == "tensor":
            n_ = kw["rhs"].ap.free_size() if "rhs" in kw else 128
            dur = max(n_, 64) / 1.2 + 30.0
        else:
            dur = (200.0 + 0.65 * fsz) if eng == "scalar" else (150.0 + 0.75 * fsz)
        self._est(eng, tok, dep_toks, dur)
        self._commit(tok, reads, writes)
        return tok

    def dma(self, eng, out, in_, **kw):
        sb = out.buf
        if sb.dsem is None:
            sb.dsem = ("dma", len(self.dma_sems))
            self.dma_sems.append(sb.name)
        dep_toks = [b_.writer for b_ in (in_.buf, out.buf) if b_.writer is not None] + list(out.buf.readers)
        waits = self._deps(eng, [in_.buf], [out.buf])
        sb.dcnt += 16
        tok = (sb.dsem, sb.dcnt)
        try:
            nbytes = out.ap.nbytes()
        except Exception:
            nbytes = 1 << 20
        self._est(eng, tok, dep_toks, 2500.0 + nbytes / 150.0)
        a = dict(out=out.ap, in_=in_.ap, **kw)
        fn = lambda e, a=a: e.dma_start(**a)
        self.q[eng].append([waits, fn, False, sb.dsem])
        self._commit(tok, [in_.buf], [out.buf])
        return tok

    def finish(self, final_bufs):
        waits = self._deps("sync", final_bufs, [])
        self.q["sync"].append([waits, None, False, None])

    def emit(self):
        nc = self.nc
        st = self.stack
        esem = {e: st.enter_context(nc.semaphore("es_" + e)) for e in ENGS}
        dsem = [st.enter_context(nc.semaphore("ds%d" % i)) for i in range(len(self.dma_sems))]
        cum = {}
        for e in ENGS:
            c = 0
            arr = []
            for it in self.q[e]:
                if it[2]:
                    c += 1
                arr.append(c)
            cum[e] = arr

        def semval(k, v):
            if isinstance(k, str):
                return esem[k], cum[k][v - 1]
            return dsem[k[1]], v

        block = st.enter_context(nc.Block())

        def run(e, eng):
            for waits, fn, sig, dk in self.q[e]:
                for k, v in waits:
                    s, val = semval(k, v)
                    eng.wait_ge(s, val)
                if fn is None:
                    continue
                ins = fn(eng)
                if dk is not None:
                    ins.then_inc(dsem[dk[1]], 16)
                elif sig:
                    ins.then_inc(esem[e], 1)

        @block.tensor
        def _(eng):
            run("tensor", eng)

        @block.vector
        def _(eng):
            run("vector", eng)

        @block.scalar
        def _(eng):
            run("scalar", eng)

        @block.gpsimd
        def _(eng):
            run("gpsimd", eng)

        @block.sync
        def _(eng):
            run("sync", eng)


def _alibi_slopes(n):
    def pow2(m):
        start = 2.0 ** (-8.0 / m)
        return [start ** (i + 1) for i in range(m)]
    if math.log2(n).is_integer():
        s = pow2(n)
    else:
        p = 2 ** int(math.floor(math.log2(n)))
        s = pow2(p) + pow2(2 * p)[0::2][: n - p]
    return sorted(s, reverse=True)


(PV_SHM, PV_SCM, PV_GTM, PV_SHF, PV_SCF, PV_GTF, PV_GPM, PV_GQM, PV_GPF, PV_GQF,
 PV_MUR, PV_MUK, PV_MUV, PV_MUW, PV_MUA, PV_MUG, PV_W0, PV_A0, PV_KK, PV_KA, PV_RK,
 PV_LNW, PV_LNB, PV_C) = range(24)
NPV = 24

CB_ID = 0
CB_ONESBD = 128
CB_ONES = 256
CB_MT4 = 320
CB_ML4 = 832
CB_E1 = 1344
CB_EA2 = CB_E1 + 8 * 256
CB_EB2 = CB_EA2 + 2 * 8 * 64
CB_EA3 = CB_EB2 + 8 * 64
CB_EB3 = CB_EA3 + 8 * 8 * 16
NCB = CB_EB3 + 8 * 16
CF_MSK = 0
CF_VM = 256
CF_EPS = CF_VM + 128
CF_IDF = CF_EPS + 4
NCF = CF_IDF + 128

CH_RW = 0
CH_AT = 8
CH_PB = 17
CH_WOUT = 25
CH_FF = 27
CH_FO = 38
NCH = 44


def _host_consts():
    sl = np.asarray(_alibi_slopes(24), np.float64).reshape(3, 8)
    cb = np.zeros((128, NCB), np.float32)
    p = np.arange(128)
    cb[:, CB_ID:CB_ID + 128] = np.eye(128)
    cb[:, CB_ONESBD:CB_ONESBD + 128] = (p[:, None] // 64 == p[None, :] // 64)
    cb[:, CB_ONES:CB_ONES + 64] = 1.0
    same = (p[:, None] // 64 == p[None, :] // 64)
    su = same & (p[:, None] < p[None, :])
    iu = same & (p[:, None] <= p[None, :])
    slo = same & (p[:, None] > p[None, :])
    cb[:, CB_MT4:CB_MT4 + 512] = np.concatenate([su, iu, su, iu], 1)
    cb[:, CB_ML4:CB_ML4 + 512] = np.concatenate([slo] * 4, 1)
    k = p[:, None].astype(np.float64)
    q = p[None, :].astype(np.float64)
    for h in range(8):
        dpv = q - k + 128
        e_prev = np.where(dpv <= 128, np.exp(-sl[0, h] * dpv), 0.0)
        dcu = q - k
        e_cur = np.where(dcu >= 0, np.exp(-sl[0, h] * np.maximum(dcu, 0)), 0.0)
        cb[:, CB_E1 + h * 256: CB_E1 + h * 256 + 128] = e_prev
        cb[:, CB_E1 + h * 256 + 128: CB_E1 + h * 256 + 256] = e_cur
    i64 = np.arange(64)[None, :].astype(np.float64)
    for rot in range(2):
        for h in range(8):
            j = p // 64
            pp = (p % 64).astype(np.float64)
            a = ((rot - j - 1) % 2) + 1
            dl = 64.0 * a[:, None] + i64 - pp[:, None]
            e = np.where(dl <= 128, np.exp(-sl[1, h] * 4.0 * dl), 0.0)
            o = CB_EA2 + (rot * 8 + h) * 64
            cb[:, o:o + 64] = e
    for h in range(8):
        kk = np.arange(64)[:, None].astype(np.float64)
        dl = i64 - kk
        e = np.where(dl >= 0, np.exp(-sl[1, h] * 4.0 * np.maximum(dl, 0)), 0.0)
        o = CB_EB2 + h * 64
        cb[0:64, o:o + 64] = e
    i16 = np.arange(16)[None, :].astype(np.float64)
    for rot in range(8):
        for h in range(8):
            j = p // 16
            pp = (p % 16).astype(np.float64)
            a = ((rot - j - 1) % 8) + 1
            dl = 16.0 * a[:, None] + i16 - pp[:, None]
            e = np.where(dl <= 128, np.exp(-sl[2, h] * 16.0 * dl), 0.0)
            o = CB_EA3 + (rot * 8 + h) * 16
            cb[:, o:o + 16] = e
    for h in range(8):
        kk = np.arange(16)[:, None].astype(np.float64)
        dl = i16 - kk
        e = np.where(dl >= 0, np.exp(-sl[2, h] * 16.0 * np.maximum(dl, 0)), 0.0)
        o = CB_EB3 + h * 16
        cb[0:16, o:o + 16] = e
    return cb


def _host_cf(hh):
    cf = np.zeros((128, NCF), np.float32)
    m = np.ones((128, 256), np.float32)
    m[:, 0::64] = 0.0
    cf[:, CF_MSK:CF_MSK + 256] = m
    valid = lambda t: 0.0 if t < 0 else (1.0 if (hh == 1 or t >= PB0) else 0.0)
    p = np.arange(128)
    for t in range(NT):
        cf[:, CF_VM + t] = valid(t)
        cf[:, CF_VM + 32 + t] = valid(t - 1)
        j = p // 64
        a = ((t - j - 1) % 2) + 1
        cf[:, CF_VM + 64 + t] = [valid(t - aa) for aa in a]
        j = p // 16
        a = ((t - j - 1) % 8) + 1
        cf[:, CF_VM + 96 + t] = [valid(t - aa) for aa in a]
    cf[:, CF_EPS] = RMS_EPS
    cf[:, CF_EPS + 1] = GN_EPS
    cf[:, CF_EPS + 3] = 1.0
    cf[:, CF_IDF:CF_IDF + 128] = np.eye(128)
    return cf


def _fm(v):
    return np.ascontiguousarray(v.reshape(8, 128).T)


def _wchunk(w, cols):
    return w[:, cols].reshape(8, 128, -1).transpose(1, 0, 2)


class _Stop(Exception):
    pass


def build(nt=NT, dbg=None, dbg_tile=0, dbg_c=0, stop=None):
    nc = bass.Bass("TRN2", target_bir_lowering=False)
    with ExitStack() as st:
        S = Sched(nc, st)
        finals = []

        def CK(name):
            if stop == name:
                raise _Stop()

        def DBG(name, view, m=None, c=None):
            if not dbg or name not in dbg:
                return
            if m is not None and m != dbg_tile:
                return
            if c is not None and c != dbg_c:
                return
            shp = list(view.ap.shape)
            dd = S.dram("dbg_" + name, shp, view.ap.dtype, "ExternalOutput")
            S.dma("gpsimd", out=dd[:], in_=view)
            finals.append(dd)
        xv = S.dram("xv", [T, D], F32, "ExternalInput")
        pfm_d = S.dram("pfm", [128, NPV * 8], F32, "ExternalInput")
        wmod_d = S.dram("wmod", [24, 128, 2048], F32, "ExternalInput")
        wsrc = S.dram("wsrc", [NCH, 128, 4096], F32, "ExternalInput")
        l1_d = S.dram("l1", [128, 8 * 288], F32, "ExternalInput")
        l2_d = S.dram("l2", [128, 3 * 1024], F32, "ExternalInput")
        cb_d = S.dram("cbt", [128, NCB], F32, "ExternalInput")
        cf_d = S.dram("cft", [128, NCF], F32, "ExternalInput")
        y_d = S.dram("y", [T // 2, D], F32, "ExternalOutput")
        wscr_all = S.dram("wscr", [NCH, 128, 4096], BF16, "Internal")
        wscr = [Buf("wscr%d" % i, wscr_all.t[i]) for i in range(NCH)]

        cb = S.sb("cb", [128, NCB], BF16)
        cf = S.sb("cf", [128, NCF], F32)
        pf = S.sb("pf", [128, NPV * 8], F32)
        pd = S.sb("pd", [128, 12 * 8], F32)
        gmb = S.sb("gmb", [128, 1024], BF16)
        gfb = S.sb("gfb", [128, 1024], BF16)
        l1a = S.sb("l1a", [128, 8, 288], BF16)
        l1b = S.sb("l1b", [128, 8, 288], BF16)
        l2 = S.sb("l2", [128, 3, 1024], BF16)
        ring = [S.sb("ring%d" % i, [128, 4096], BF16) for i in range(NSLOT)]
        xt1 = S.sb("xt", [128, NSUB, 1024], F32)
        xt = [xt1, xt1]
        nb = S.sb("nb", [128, NSUB, 1024], BF16)
        junk = nb[:, 0, :]
        st4 = S.sb("st4", [128, 16], F32)
        hT = S.sb("hT", [128, 8, TT + 1], BF16)
        h2T = S.sb("h2T", [128, 8, TT], BF16)
        mixT = h2T
        Zf = S.sb("Zf", [128, 8, 64], F32)
        Zb = S.sb("Zb", [128, 8, 2, 64], BF16)
        hal = S.sb("hal", [128, 8, 3], F32)
        tp = [S.sb("tp%d" % i, [128, TT + 1], F32) if i != 7 else None for i in range(12)]
        tp[7] = tp[6]
        tv = lambda i: tp[i][:, 0:TT]
        pj = [tp[0], tp[0], tp[0]]
        tmpd = tv(1)
        rkv = [tv(2), tv(3), tv(4)]
        sw = tv(5); asig = tv(6); gg = tv(7); cs = tv(0); cm = tv(1)
        Ep = tv(8); En = tv(9); Em = tv(10); rinv = tv(1); kkb = tv(11); ff = tv(0)
        kmod = tv(5); bv = tv(1); bon = tv(6); yln = tv(9); ysq = tv(10)
        sqb = S.sb("sqb", [128, TT], BF16)
        ARs = [S.sb("AR%d" % i, [128, NSUB, 2, 128], BF16) for i in range(3)]
        Bts = [S.sb("Bt%d" % i, [128, TT], BF16) for i in range(2)]
        Kts = [S.sb("Kt%d" % i, [128, TT], BF16) for i in range(2)]
        vbfs = [S.sb("vbf%d" % i, [128, TT], BF16) for i in range(2)]
        Bpads = [S.sb("Bpad%d" % i, [128, NSUB, 2, 128], BF16) for i in range(2)]
        Kpads = [S.sb("Kpad%d" % i, [128, NSUB, 2, 128], BF16) for i in range(2)]
        Vtms = [S.sb("Vtm%d" % i, [128, NSUB, 128], BF16) for i in range(2)]
        AMs = [S.sb("AM%d" % i, [128, 4, 512], BF16) for i in range(2)]
        TTfs = [S.sb("TTf%d" % i, [128, 4, 128], BF16) for i in range(2)]
        gbs = [S.sb("gb%d" % i, [128, TT], BF16) for i in range(3)]
        pcss = [S.sb("pcs%d" % i, [128, 4], F32) for i in range(3)]
        ysqB = S.sb("ysqB", [128, TT], F32)
        L0 = S.sb("L0", [128, 4, 128], BF16)
        LP = [S.sb("LP%d" % i, [128, 4, 128], BF16) for i in range(2)]
        LT = [S.sb("LT%d" % i, [128, 4, 128], BF16) for i in range(2)]
        SS = [S.sb("SS%d" % i, [128, 4, 128], BF16) for i in range(2)]
        Xb = S.sb("Xb", [128, 128], BF16)
        Ub = S.sb("Ub", [128, 128], BF16)
        ztmp = S.sb("ztmp", [128, 64], F32)
        Ytm = S.sb("Ytm", [128, NSUB, 128], F32)
        ynb = S.sb("ynb", [128, NSUB, 128], BF16)
        gst = S.sb("gst", [128, 32], F32)
        lw = S.sb("lw", [128, TT], BF16)
        lga = S.sb("lga", [128, TT], BF16)
        lgb = S.sb("lgb", [32, TT], BF16)
        yfin = S.sb("yfin", [128, 8, TT], BF16)
        _nbf = nb[:].re("p s c -> p (s c)")
        Qa = [h2T[:, 0:4, :], h2T[:, 4:8, :], _nbf[:, 0:1024].re("p (c t) -> p c t", t=TT)]
        K1 = S.sb("K1", [128, 4, 128 + TT], BF16)
        V1 = S.sb("V1", [128, 3, 512], BF16)
        K2c = S.sb("K2c", [128, 4, TT], BF16)
        K2r = S.sb("K2r", [128, 4, 4, 128], BF16)
        V2c = S.sb("V2c", [64, 4, 128], BF16)
        V2r = S.sb("V2r", [128, 4, 512], BF16)
        K3c = S.sb("K3c", [128, 4, TT], BF16)
        K3r = S.sb("K3r", [128, 4, 16, 128], BF16)
        V3c = S.sb("V3c", [16, 16, 128], BF16)
        V3r = S.sb("V3r", [128, 16, 512], BF16)
        VF = _nbf[:, 1024:2048].re("p (c t) -> p c t", t=TT)
        pe = S.sb("pe", [128, 512], BF16)
        pp_ = S.sb("pp", [128, 512], BF16)
        peb = SubBuf(pe, 256, 256)
        ppb = SubBuf(pp_, 256, 256)
        accO = S.sb("accO", [64, TT], F32)
        accD = S.sb("accD", [64, TT], F32)
        oT = S.sb("oT", [64, 8, TT], BF16)
        sa = tv(2); sbb = tv(3); sg = tv(4); utmp = tv(10)
        actT = S.sb("actT", [128, 8, TT], BF16)
        VF2 = actT[:, 0:4, :]
        wst = xt1[:].re("p s c -> p (s c)")

        _b0 = S.ps("b0", [128, 512])
        _b4 = S.ps("b4", [128, 512])
        _bS = S.ps("bS", [128, 1024])
        B3 = S.ps("b3", [128, 512])
        B5 = S.ps("b5", [128, 512])
        B6 = S.ps("b6", [128, 512])
        _pT = S.ps("pT", [128, 1024], BF16)
        R0 = SubBuf(_b0, 0); R1 = SubBuf(_b0, 256)
        Q0 = SubBuf(_b4, 0); Q1 = SubBuf(_b4, 256)
        B1 = Buf("B1", _bS.t[:, 0:512]); B2 = Buf("B2", _bS.t[:, 512:1024])
        pTr = SubBuf(_pT, 0); pTa = SubBuf(_pT, 512)
        for b_ in (_b0, _b4, B1, B2, B3, B5, B6, _pT):
            b_.psum = True
        pC = B3
        trB = [View(B1, B1.t[:, :].bitcast(BF16)), View(B2, B2.t[:, :].bitcast(BF16))]
        trC = View(B3, B3.t[:, :].bitcast(BF16))
        trot = [View(_pT, _pT.t[:, 0:512]), View(B6, B6.t[:, :].bitcast(BF16)), View(B5, B5.t[:, :].bitcast(BF16))]
        prot = {"r": [_b0, B1, B2], "a": [_b4, B5, B6], "x": [_b0, _b4, B1, B2, B3, B6], "f": [_b0, _b4, B6]}
        prot_i = {"r": 0, "a": 0, "x": 0, "f": 0}

        def nextp(k="x"):
            prot_i[k] = (prot_i[k] + 1) % len(prot[k])
            return prot[k][prot_i[k]]

        mm = lambda **kw: S.op("tensor", "matmul", **kw)
        tr = lambda **kw: S.op("tensor", "transpose", **kw)
        act = lambda **kw: S.op("scalar", "activation", **kw)
        vec = lambda m, **kw: S.op("vector", m, **kw)
        gps = lambda m, **kw: S.op("gpsimd", m, **kw)

        def sigmoid_to(dst, src, nbias=None, scale=1.0):
            if nbias is None:
                act(out=dst, in_=src, func=AF.Exp, scale=-scale)
            else:
                act(out=dst, in_=src, func=AF.Exp, scale=-scale, bias=nbias)
            act(out=dst, in_=dst, func=AF.Ln, bias=one_c_for(dst))
            act(out=dst, in_=dst, func=AF.Exp, scale=-1.0)

        def one_c_for(v):
            lo = v.ap.base_partition()
            n = v.ap.partition_size()
            return cf[lo:lo + n, CF_EPS + 3:CF_EPS + 4]

        def rsqrt_to(dst, src, bias_ap, scale=1.0):
            act(out=dst, in_=src, func=AF.Ln, bias=bias_ap, scale=scale)
            act(out=dst, in_=dst, func=AF.Exp, scale=-0.5)

        ident = cb[:, CB_ID:CB_ID + 128]
        identf = cf[:, CF_IDF:CF_IDF + 128]
        onesbd = cb[:, CB_ONESBD:CB_ONESBD + 128]
        eps_r = cf[:, CF_EPS:CF_EPS + 1]
        eps_g = cf[:, CF_EPS + 1:CF_EPS + 2]
        zero_c = cf[:, CF_EPS + 2:CF_EPS + 3]
        one_c = cf[:, CF_EPS + 3:CF_EPS + 4]

        def pv(i, kc):
            return pf[:, i * 8 + kc: i * 8 + kc + 1]

        def pdv(i, kc):
            return pd[:, i * 8 + kc: i * 8 + kc + 1]
        PD_A1, PD_A2, PD_GM, PD_GF, PD_OMK, PD_OMR, PD_OMKm, PD_OMV = range(8)

        try:
            S.dma("gpsimd", out=cb[:, :], in_=cb_d[:, :])
            S.dma("sync", out=cf[:, :], in_=cf_d[:, :])
            S.dma("sync", out=pf[:, :], in_=pfm_d[:, :])
            for i in range(NCH):
                S.dma("gpsimd", out=wscr[i][:, :], in_=wsrc[i])
            CK('dma0')
            for b_ in (Zf, Zb, hal, Bpads[0], Bpads[1], Kpads[0], Kpads[1], K1, K2r, V2r, K3r, V3r, V1, hT, Xb, Ub, Vtms[0], Vtms[1]):
                gps("memset", ap=b_[:], constant=0.0)
            CK('memset')
            for half in range(2):
                S.dma("sync", out=wst[:, 0:4 * 288], in_=l1_d[:, half * 4 * 288:(half + 1) * 4 * 288])
                w1v = wst[:, 0:4 * 288].re("p (k c) -> p k c", c=288)
                for k4 in range(4):
                    kc = half * 4 + k4
                    for (lo, hi, mui) in ((0, 64, PV_MUW), (64, 128, PV_MUA), (128, 288, PV_MUG)):
                        vec("tensor_scalar", out=l1b[:, kc, lo:hi], in0=w1v[:, k4, lo:hi], scalar1=pv(mui, kc),
                            scalar2=None, op0=ALU.mult)
                        vec("tensor_tensor", out=l1a[:, kc, lo:hi], in0=w1v[:, k4, lo:hi], in1=l1b[:, kc, lo:hi],
                            op=ALU.subtract)
            for half in range(2):
                S.dma("sync", out=wst[:, 0:1536], in_=l2_d[:, half * 1536:(half + 1) * 1536])
                vec("tensor_copy", out=l2[:].re("p a c -> p (a c)")[:, half * 1536:(half + 1) * 1536], in_=wst[:, 0:1536])
            CK('lora0')
            for j in range(24):
                S.dma("sync", out=wst[:, 0:2048], in_=wmod_d[j])
                wv = wst[:, 0:2048].re("p (k c) -> p k c", c=256)
                for cc in range(2):
                    col = j * 2 + cc
                    for kc in range(8):
                        mm(out=B1[:, col:col + 1], lhsT=wv[:, kc, cc * 128:(cc + 1) * 128], rhs=pv(PV_C, kc),
                           start=(kc == 0), stop=(kc == 7))
            modf = S.sb("modf", [128, 48], F32)
            vec("tensor_tensor", out=modf[:, :], in0=B1[:, 0:48], in1=pf[:, 0:48], op=ALU.add)
            for kc in range(8):
                vec("scalar_tensor_tensor", out=pdv(PD_A1, kc), in0=modf[:, 8 + kc:9 + kc], scalar=1.0,
                    in1=pv(PV_GPM, kc), op0=ALU.add, op1=ALU.mult)
                vec("scalar_tensor_tensor", out=pdv(PD_A2, kc), in0=modf[:, 32 + kc:33 + kc], scalar=1.0,
                    in1=pv(PV_GPF, kc), op0=ALU.add, op1=ALU.mult)
                vec("tensor_tensor", out=pdv(PD_GM, kc), in0=modf[:, 16 + kc:17 + kc], in1=pv(PV_GQM, kc), op=ALU.mult)
                vec("tensor_tensor", out=pdv(PD_GF, kc), in0=modf[:, 40 + kc:41 + kc], in1=pv(PV_GQF, kc), op=ALU.mult)
                vec("tensor_scalar", out=pdv(PD_OMK, kc), in0=pv(PV_KA, kc), scalar1=-1.0, scalar2=1.0,
                    op0=ALU.mult, op1=ALU.add)
                vec("tensor_scalar", out=pdv(5, kc), in0=pv(PV_W0, kc), scalar1=-1.0, scalar2=None, op0=ALU.mult)
                vec("tensor_scalar", out=pdv(6, kc), in0=pv(PV_A0, kc), scalar1=-1.0, scalar2=None, op0=ALU.mult)
            dg = tp[0][:, 0:128]
            onesf = tp[1][:, 0:128]
            gps("memset", ap=onesf[:, :], constant=1.0)
            for (pdi, dst) in ((PD_GM, gmb), (PD_GF, gfb)):
                for kc in range(8):
                    vec("tensor_scalar", out=dg[:, :], in0=identf, scalar1=pdv(pdi, kc), scalar2=None, op0=ALU.mult)
                    pz = nextp()
                    mm(out=pz[:, 0:128], lhsT=onesf[:, :], rhs=dg[:, :], start=True, stop=True)
                    act(out=dst[:, kc * 128:(kc + 1) * 128], in_=pz[:, 0:128], func=AF.Copy)

            CK('startup')
            ring_i = [0]

            ring_sets = {"r": ring[0:2], "a": ring[2:4], "x": ring}
            ring_k = {"r": 0, "a": 0, "x": 0}

            def wload(ch, k="x"):
                s = ring_sets[k][ring_k[k] % len(ring_sets[k])]
                ring_k[k] += 1
                S.dma("sync", out=s[:, :], in_=wscr[ch][:, :])
                return s

            def rms_rstd(src3, dst_cols, nsub=NSUB):
                for sub in range(nsub):
                    act(out=junk[:, :], in_=src3[:, sub, :], func=AF.Square,
                        accum_out=st4[:, 8 + sub:9 + sub])
                rsqrt_to(st4[:, dst_cols:dst_cols + nsub], st4[:, 8:8 + nsub], eps_r, 1.0 / D)

            def norm_transpose(xsrc, rcol, dstT, a_idx, b_view_fn, halo):
                for sub in range(NSUB):
                    vec("tensor_scalar", out=nb[:, sub, :], in0=xsrc[:, sub, :], scalar1=st4[:, rcol + sub:rcol + sub + 1],
                        scalar2=None, op0=ALU.mult)
                for kc in range(8):
                    tgt = trot[kc % 3]
                    for sub in range(NSUB):
                        tr(out=tgt[:, sub * 128:(sub + 1) * 128], in_=nb[:, sub, kc * 128:(kc + 1) * 128], identity=ident)
                    act(out=dstT[:, kc, halo:halo + TT], in_=tgt[:, 0:TT], func=AF.Identity,
                        scale=pdv(a_idx, kc), bias=b_view_fn(kc))

            for m in range(nt):
                phaseB = m >= PB0
                prot["r"] = [_b0] if m >= PB0 - 8 else [_b0, _b4, B5, B6]
                xm = xt[m % 2]
                vcur = cf[:, CF_VM + m:CF_VM + m + 1]
                vprev = cf[:, CF_VM + 32 + m:CF_VM + 33 + m]
                vr2 = cf[:, CF_VM + 64 + m:CF_VM + 65 + m]
                vr3 = cf[:, CF_VM + 96 + m:CF_VM + 97 + m]
                S.dma("gpsimd", out=xm[:], in_=xv.v(xv.t[m * TT:(m + 1) * TT, :].rearrange("(s p) c -> p s c", p=128)))
                if m > 0:
                    vec("tensor_scalar", out=hT[:, :, 0:1], in0=hT[:, :, TT:TT + 1],
                        scalar1=cf[:, CF_VM + m - 1:CF_VM + m], scalar2=None, op0=ALU.mult)
                rms_rstd(xm, 0)
                norm_transpose(xm, 0, hT, PD_A1, lambda kc: pf[:, PV_SHM * 8 + kc:PV_SHM * 8 + kc + 1]
                               if False else modf[:, kc:kc + 1], 1)

                CK('stage1')
                pz = nextp()
                for kc in range(8):
                    mm(out=pz[:, 0:TT], lhsT=l1a[:, kc, 0:128], rhs=hT[:, kc, 1:TT + 1], start=(kc == 0), stop=False)
                    mm(out=pz[:, 0:TT], lhsT=l1b[:, kc, 0:128], rhs=hT[:, kc, 0:TT], start=False, stop=(kc == 7))
                sigmoid_to(tp[11][0:64, 0:TT], pz[0:64, 0:TT], None, 2.0)
                vec("tensor_scalar", out=lw[0:64, :], in0=tp[11][0:64, 0:TT], scalar1=2.0, scalar2=-1.0, op0=ALU.mult, op1=ALU.add)
                act(out=lw[64:128, :], in_=pz[64:128, 0:TT], func=AF.Copy)
                if phaseB:
                    pz = nextp()
                    for kc in range(8):
                        mm(out=pz[:, 0:TT], lhsT=l1a[:, kc, 128:256], rhs=hT[:, kc, 1:TT + 1], start=(kc == 0), stop=False)
                        mm(out=pz[:, 0:TT], lhsT=l1b[:, kc, 128:256], rhs=hT[:, kc, 0:TT], start=False, stop=(kc == 7))
                    sigmoid_to(tp[11][:, 0:TT], pz[:, 0:TT])
                    act(out=lga[:, :], in_=tp[11][:, 0:TT], func=AF.Copy)
                    pz = nextp()
                    for kc in range(8):
                        mm(out=pz[0:32, 0:TT], lhsT=l1a[:, kc, 256:288], rhs=hT[:, kc, 1:TT + 1], start=(kc == 0), stop=False)
                        mm(out=pz[0:32, 0:TT], lhsT=l1b[:, kc, 256:288], rhs=hT[:, kc, 0:TT], start=False, stop=(kc == 7))
                    sigmoid_to(tp[11][0:32, 0:TT], pz[0:32, 0:TT])
                    act(out=lgb[0:32, :], in_=tp[11][0:32, 0:TT], func=AF.Copy)

                CK('lora1')
                def rw_f1(c0):
                    for c in (c0,):
                        AR = ARs[c % 3]; AM = AMs[c % 2]; Vtm = Vtms[c % 2]; Bpad = Bpads[c % 2]; Kpad = Kpads[c % 2]
                        TTf = TTfs[c % 2]; gb = gbs[c % 3]; pcs = pcss[c % 3]
                        Bt = Bts[c % 2]; Kt = Kts[c % 2]; vbf = vbfs[c % 2]
                        csl = slice(c * 128, (c + 1) * 128)
                        wr = wload(CH_RW + c, 'r')
                        wrv = wr[:, :].re("p (k c) -> p k c", c=512)
                        for j in ((0, 1, 2) if m >= PB0 - 1 else (1, 2)):
                            pz = nextp("r")
                            for kc in range(8):
                                mm(out=pz[:, 0:TT], lhsT=wrv[:, kc, j * 128:(j + 1) * 128], rhs=hT[:, kc, 1:TT + 1],
                                   start=(kc == 0), stop=(kc == 7))
                            vec("tensor_copy", out=pj[j][:, 0:1], in_=hal[:, c, j:j + 1])
                            yield
                            act(out=pj[j][:, 1:TT + 1], in_=pz[:, 0:TT], func=AF.Copy)
                            yield
                            vec("tensor_scalar", out=hal[:, c, j:j + 1], in0=pj[j][:, TT:TT + 1], scalar1=vcur,
                                scalar2=None, op0=ALU.mult)
                            yield
                            vec("tensor_tensor", out=tmpd[:, :], in0=pj[j][:, 0:TT], in1=pj[j][:, 1:TT + 1], op=ALU.subtract)
                            yield
                            vec("scalar_tensor_tensor", out=rkv[j][:, :], in0=tmpd[:, :], scalar=pv(PV_MUR + j, c),
                                in1=pj[j][:, 1:TT + 1], op0=ALU.mult, op1=ALU.add)
                            yield
                        r_, k_, v_ = rkv
                        vec("tensor_scalar", out=v_[:, :], in0=v_[:, :], scalar1=vcur, scalar2=None, op0=ALU.mult)
                        yield
                        act(out=vbf[:, :], in_=v_[:, :], func=AF.Copy)
                        yield
                        pz = nextp("r")
                        mm(out=pz[:, 0:TT], lhsT=l2[0:64, 0, csl], rhs=lw[0:64, :], start=True, stop=True)
                        sigmoid_to(sw[:, :], pz[:, 0:TT], pdv(5, c))
                        yield
                        pz = nextp("r")
                        mm(out=pz[:, 0:TT], lhsT=l2[64:128, 0, csl], rhs=lw[64:128, :], start=True, stop=True)
                        sigmoid_to(asig[:, :], pz[:, 0:TT], pdv(6, c))
                        yield
                        pz = nextp("r")
                        if phaseB:
                            mm(out=pz[:, 0:TT], lhsT=l2[:, 1, csl], rhs=lga[:, :], start=True, stop=False)
                            mm(out=pz[:, 0:TT], lhsT=l2[0:32, 2, csl], rhs=lgb[0:32, :], start=False, stop=True)
                            act(out=gb[:, :], in_=pz[:, 0:TT], func=AF.Copy)
                        yield
                        vec("tensor_tensor_scan", out=cs[:, :], data0=cf[:, CF_MSK:CF_MSK + TT], data1=sw[:, :],
                            initial=0.0, op0=ALU.mult, op1=ALU.add)
                        yield
                        vec("tensor_tensor", out=cm[:, :], in0=cs[:, :], in1=sw[:, :], op=ALU.subtract)
                        yield
                        act(out=Ep[:, :], in_=cs[:, :], func=AF.Exp, scale=-C0)
                        yield
                        act(out=En[:, :], in_=cs[:, :], func=AF.Exp, scale=C0)
                        yield
                        act(out=Em[:, :], in_=cm[:, :], func=AF.Exp, scale=-C0)
                        yield
                        act(out=sqb[:, :], in_=k_[:, :], func=AF.Square, scale=pv(PV_KK, c))
                        yield
                        pz = nextp("r")
                        mm(out=pz[:, 0:TT], lhsT=onesbd, rhs=sqb[:, :], start=True, stop=True)
                        vec("tensor_scalar", out=rinv[:, :], in0=pz[:, 0:TT], scalar1=1e-18, scalar2=None, op0=ALU.max)
                        yield
                        act(out=rinv[:, :], in_=rinv[:, :], func=AF.Ln)
                        yield
                        act(out=rinv[:, :], in_=rinv[:, :], func=AF.Exp, scale=-0.5)
                        yield
                        vec("scalar_tensor_tensor", out=kkb[:, :], in0=k_[:, :], scalar=pv(PV_KK, c), in1=rinv[:, :],
                            op0=ALU.mult, op1=ALU.mult)
                        yield
                        ev = gps if GPS_OFF else vec
                        ev("tensor_scalar", out=ff[:, :], in0=asig[:, :], scalar1=pv(PV_KA, c), scalar2=pdv(PD_OMK, c),
                            op0=ALU.mult, op1=ALU.add)
                        yield
                        ev("tensor_tensor", out=kmod[:, :], in0=k_[:, :], in1=ff[:, :], op=ALU.mult)
                        yield
                        ev("tensor_tensor", out=bv[:, :], in0=kkb[:, :], in1=asig[:, :], op=ALU.mult)
                        yield
                        vec("scalar_tensor_tensor", out=AR[:, :, 0, :], in0=kkb[:, :].re("p (s t) -> p s t", t=128), scalar=-1.0,
                            in1=Em[:, :].re("p (s t) -> p s t", t=128), op0=ALU.mult, op1=ALU.mult)
                        yield
                        if phaseB:
                            vec("tensor_tensor", out=AR[:, :, 1, :], in0=r_[:, :].re("p (s t) -> p s t", t=128),
                                in1=Ep[:, :].re("p (s t) -> p s t", t=128), op=ALU.mult)
                            yield
                        ev("tensor_tensor", out=Bt[:, :], in0=bv[:, :], in1=En[:, :], op=ALU.mult)
                        yield
                        ev("tensor_tensor", out=Kt[:, :], in0=kmod[:, :], in1=En[:, :], op=ALU.mult)
                        yield
                        if phaseB:
                            vec("tensor_tensor", out=tmpd[:, :], in0=r_[:, :], in1=kmod[:, :], op=ALU.mult)
                            yield
                            act(out=sqb[:, :], in_=tmpd[:, :], func=AF.Copy, scale=pv(PV_RK, c))
                            yield
                            pz = nextp("r")
                            mm(out=pz[:, 0:TT], lhsT=onesbd, rhs=sqb[:, :], start=True, stop=True)
                            vec("tensor_tensor", out=bon[:, :], in0=pz[:, 0:TT], in1=v_[:, :], op=ALU.mult)
                            yield
                            vec("tensor_tensor", out=yfin[:, c, :], in0=bon[:, :], in1=gb[:, :], op=ALU.mult)
                        yield
                        vec("tensor_copy", out=pcs[:, 0:4], in_=Ep[:, :].re("p (q t) -> p q t", t=64)[:, :, 63])
                        yield

                def rw_f2(c0):
                    for c in (c0,):
                        AR = ARs[c % 3]; AM = AMs[c % 2]; Vtm = Vtms[c % 2]; Bpad = Bpads[c % 2]; Kpad = Kpads[c % 2]
                        TTf = TTfs[c % 2]; gb = gbs[c % 3]; pcs = pcss[c % 3]
                        Bt = Bts[c % 2]; Kt = Kts[c % 2]; vbf = vbfs[c % 2]
                        for qi, (src, dst) in enumerate(((Bt, Bpad), (Kt, Kpad), (vbf, None))):
                            for sub in range(NSUB):
                                tr(out=trB[qi % 2][:, sub * 128:(sub + 1) * 128],
                                   in_=src[:, sub * 128:(sub + 1) * 128], identity=ident)
                            yield
                            srcv = trB[qi % 2][:, 0:256]
                            if dst is None:
                                act(out=Vtm[:].re("p s c -> p (s c)"), in_=srcv, func=AF.Copy)
                            else:
                                for h in range(2):
                                    act(out=dst[:, :, h, h * 64:(h + 1) * 64],
                                        in_=srcv.re("p (s c) -> p s c", c=128)[:, :, h * 64:(h + 1) * 64], func=AF.Copy)
                        CK('rwkv_a')
                        for h in range(2):
                            hs = slice(h * 64, (h + 1) * 64)
                            for sub in range(NSUB):
                                u = h * NSUB + sub
                                tsl = slice(sub * 128, (sub + 1) * 128)
                                pz = (B1, B2)[u % 2]
                                if phaseB:
                                    mm(out=pz[:, 0:256], lhsT=Bt[hs, tsl], rhs=AR[hs, sub, :, :].re("p a t -> p (a t)"),
                                       start=True, stop=True)
                                    mm(out=pz[:, 256:512], lhsT=Kt[hs, tsl], rhs=AR[hs, sub, :, :].re("p a t -> p (a t)"),
                                       start=True, stop=True)
                                    vec("tensor_tensor", out=AM[:, u, :], in0=pz[:, :], in1=cb[:, CB_MT4:CB_MT4 + 512], op=ALU.mult)
                                else:
                                    mm(out=pz[:, 0:128], lhsT=Bt[hs, tsl], rhs=AR[hs, sub, 0, :], start=True, stop=True)
                                    mm(out=pz[:, 256:384], lhsT=Kt[hs, tsl], rhs=AR[hs, sub, 0, :], start=True, stop=True)
                                    v4 = lambda ap_: ap_.re("p (a two b) -> p a two b", a=2, two=2)[:, :, 0, :]
                                    vec("tensor_tensor", out=v4(AM[:, u, :]), in0=v4(pz[:, :]),
                                        in1=v4(cb[:, CB_MT4:CB_MT4 + 512]), op=ALU.mult)
                                yield
                        for h in range(2):
                            hs = slice(h * 64, (h + 1) * 64)
                            for sub in range(NSUB):
                                u = h * NSUB + sub
                                tsl = slice(sub * 128, (sub + 1) * 128)
                                mm(out=B1[:, u * 128:(u + 1) * 128], lhsT=AR[hs, sub, 0, :], rhs=Bt[hs, tsl],
                                   start=True, stop=True)
                        vec("tensor_tensor", out=L0[:].re("p u t -> p (u t)"), in0=B1[:, :], in1=cb[:, CB_ML4:CB_ML4 + 512],
                            op=ALU.mult)
                        yield
                        CK('rwkv_b')
                        vec("tensor_tensor", out=SS[0][:], in0=AM[:, :, 0:128], in1=ident.bc(1, [128, 4, 128]), op=ALU.add)
                        lt_prev = lambda u: AM[:, u, 0:128]
                        lp_prev = lambda u: L0[:, u, :]
                        scur = 0
                        for lev in range(1, 6):
                            lpn = LP[lev % 2]
                            ltn = LT[lev % 2]
                            for u in range(4):
                                mm(out=B1[:, u * 128:(u + 1) * 128], lhsT=lt_prev(u), rhs=lp_prev(u), start=True, stop=True)
                            if lev <= 4:
                                for u in range(4):
                                    mm(out=B2[:, u * 128:(u + 1) * 128], lhsT=lp_prev(u), rhs=lt_prev(u),
                                       start=True, stop=True)
                            act(out=lpn[:].re("p u t -> p (u t)"), in_=B1[:, :], func=AF.Copy)
                            if lev <= 4:
                                act(out=ltn[:].re("p u t -> p (u t)"), in_=B2[:, :], func=AF.Copy)
                            yield
                            for u in range(4):
                                mm(out=B1[:, u * 128:(u + 1) * 128], lhsT=lpn[:, u, :], rhs=SS[scur][:, u, :], start=True, stop=True)
                            sdst = TTf if lev == 5 else SS[1 - scur]
                            vec("tensor_tensor", out=sdst[:].re("p u t -> p (u t)"), in0=B1[:, :],
                                in1=SS[scur][:].re("p u t -> p (u t)"), op=ALU.add)
                            scur = 1 - scur
                            lt_prev = (lambda b: (lambda u: b[:, u, :]))(ltn)
                            yield
                            lp_prev = (lambda b: (lambda u: b[:, u, :]))(lpn)

                def rw_back(c0):
                    for c in (c0,):
                        AR = ARs[c % 3]; AM = AMs[c % 2]; Vtm = Vtms[c % 2]; Bpad = Bpads[c % 2]; Kpad = Kpads[c % 2]
                        TTf = TTfs[c % 2]; gb = gbs[c % 3]; pcs = pcss[c % 3]
                        Bt = Bts[c % 2]; Kt = Kts[c % 2]; vbf = vbfs[c % 2]
                        TTm = TTf
                        ysq = ysqB[:, :]
                        yln = ysqB[:, :]
                        CK('rwkv_c')
                        for q in range(2 * NSUB):
                            sub, half = q // 2, q % 2
                            ps_ = slice(half * 64, half * 64 + 64)
                            tsl = slice(sub * 128, (sub + 1) * 128)
                            zi = q % 2
                            for h in range(2):
                                hs = slice(h * 64, (h + 1) * 64)
                                u = h * NSUB + sub
                                mm(out=pC[:, hs], lhsT=AR[hs, sub, 0, :], rhs=Zb[hs, c, zi, :], start=True, stop=False)
                                mm(out=pC[:, hs], lhsT=AM[:, u, 256:384], rhs=Vtm[:, sub, hs], start=False, stop=True)
                            act(out=Xb[ps_, :], in_=pC[ps_, 0:128], func=AF.Copy)
                            yield
                            CK('c1')
                            for h in range(2):
                                hs = slice(h * 64, (h + 1) * 64)
                                u = h * NSUB + sub
                                mm(out=pC[:, 128 + h * 64:128 + (h + 1) * 64], lhsT=TTm[ps_, u, :], rhs=Xb[ps_, hs],
                                   start=True, stop=True)
                            vec("tensor_copy", out=Ub[ps_, :], in_=pC[ps_, 128:256])
                            yield
                            CK('c2')
                            if phaseB:
                                for h in range(2):
                                    hs = slice(h * 64, (h + 1) * 64)
                                    u = h * NSUB + sub
                                    o_ = slice(256 + h * 64, 256 + (h + 1) * 64)
                                    mm(out=pC[:, o_], lhsT=AR[hs, sub, 1, :], rhs=Zb[hs, c, zi, :], start=True, stop=False)
                                    mm(out=pC[:, o_], lhsT=AM[:, u, 128:256], rhs=Ub[:, hs], start=False, stop=False)
                                    mm(out=pC[:, o_], lhsT=AM[:, u, 384:512], rhs=Vtm[:, sub, hs], start=False, stop=True)
                                act(out=Ytm[ps_, sub, :], in_=pC[ps_, 256:384], func=AF.Copy)
                                yield
                            CK('c3')
                            for h in range(2):
                                hs = slice(h * 64, (h + 1) * 64)
                                mm(out=pC[:, 384:448], lhsT=Bpad[ps_, sub, h, :], rhs=Ub[ps_, hs], start=(h == 0), stop=False)
                                mm(out=pC[:, 384:448], lhsT=Kpad[ps_, sub, h, :], rhs=Vtm[ps_, sub, hs], start=False, stop=(h == 1))
                            CK('c4')
                            pcv = pcs[:, q:q + 1]
                            vec("tensor_scalar", out=ztmp[:, :], in0=Zf[:, c, :], scalar1=pcv, scalar2=None, op0=ALU.mult)
                            vec("scalar_tensor_tensor", out=Zf[:, c, :], in0=pC[:, 384:448], scalar=pcv, in1=ztmp[:, :],
                                op0=ALU.mult, op1=ALU.add)
                            act(out=Zb[:, c, 1 - zi, :], in_=Zf[:, c, :], func=AF.Copy)
                            yield
                        CK('rwkv_d')
                        if phaseB:
                            yv = Ytm[:].re("p s (h i) -> p (s h) i", i=64)
                            vec("tensor_reduce", out=gst[:, 0:4], in_=yv, axis=AX.X, op=ALU.add)
                            act(out=ysq[:, :], in_=Ytm[:].re("p s c -> p (s c)"), func=AF.Square)
                            vec("tensor_reduce", out=gst[:, 4:8], in_=ysq[:, :].re("p (g i) -> p g i", i=64), axis=AX.X, op=ALU.add)
                            vec("tensor_scalar", out=gst[:, 8:12], in0=gst[:, 0:4], scalar1=1.0 / 64, scalar2=None, op0=ALU.mult)
                            vec("tensor_tensor", out=gst[:, 12:16], in0=gst[:, 8:12], in1=gst[:, 8:12], op=ALU.mult)
                            vec("scalar_tensor_tensor", out=gst[:, 16:20], in0=gst[:, 4:8], scalar=1.0 / 64, in1=gst[:, 12:16],
                                op0=ALU.mult, op1=ALU.subtract)
                            rsqrt_to(gst[:, 24:28], gst[:, 16:20], eps_g, 1.0)
                            ysv = ysq[:, :].re("p (g i) -> p g i", i=64)
                            vec("tensor_tensor", out=ysv, in0=yv, in1=gst[:, 8:12].bc(2, [128, 4, 64]), op=ALU.subtract)
                            vec("tensor_tensor", out=ynb[:].re("p s (h i) -> p (s h) i", i=64), in0=ysv,
                                in1=gst[:, 24:28].bc(2, [128, 4, 64]), op=ALU.mult)
                            yield
                            for sub in range(NSUB):
                                tr(out=trC[:, sub * 128:(sub + 1) * 128], in_=ynb[:, sub, :], identity=ident)
                            act(out=yln[:, :], in_=trC[:, 0:TT], func=AF.Identity, scale=pv(PV_LNW, c), bias=pv(PV_LNB, c))
                            vec("tensor_tensor", out=yln[:, :], in0=yln[:, :], in1=gb[:, :], op=ALU.mult)
                            vec("tensor_tensor", out=yfin[:, c, :], in0=yln[:, :], in1=yfin[:, c, :], op=ALU.add)
                            yield


                def th_attn():
                    if m < PB0 - 8:
                        return
                    j0_2 = m % 2
                    j0_3 = m % 8
                    for g in range(3):
                        kdst = (K1, K2c, K3c)[g]
                        for j in ((0, 1, 2) if phaseB else (1, 2)):
                            wa = wload(CH_AT + g * 3 + j, 'a')
                            wav = wa[:, :].re("p (k c) -> p k c", c=512)
                            for cc in range(4):
                                pz = nextp("a")
                                for kc in range(8):
                                    mm(out=pz[:, 0:TT], lhsT=wav[:, kc, cc * 128:(cc + 1) * 128], rhs=hT[:, kc, 1:TT + 1],
                                       start=(kc == 0), stop=(kc == 7))
                                if j == 0:
                                    act(out=Qa[g][:, cc, :], in_=pz[:, 0:TT], func=AF.Copy, scale=0.125)
                                elif j == 1:
                                    if g == 0:
                                        act(out=K1[:, cc, 128:128 + TT], in_=pz[:, 0:TT], func=AF.Copy)
                                    else:
                                        act(out=kdst[:, cc, :], in_=pz[:, 0:TT], func=AF.Copy)
                                else:
                                    act(out=VF[:, cc, :], in_=pz[:, 0:TT], func=AF.Copy)
                            yield
                        if g == 0:
                            for blk in range(2):
                                for cc in range(4):
                                    tr(out=pTa[:, cc * 128:(cc + 1) * 128], in_=VF[:, cc, blk * 128:(blk + 1) * 128], identity=ident)
                                vec("tensor_copy", out=V1[:, 1 + blk, :], in_=pTa[:, 0:512])
                            yield
                        elif g == 1:
                            vec("tensor_copy", out=VF2[:], in_=VF[:])
                    CK('attn_proj')
                    for h in range(8):
                        cc, hp = h // 2, (h % 2) * 64
                        hs = slice(hp, hp + 64)
                        vs = slice(h * 64, (h + 1) * 64)
                        vl = slice(hp, hp + 64)
                        if h % 2 == 0:
                            for r in range(4):
                                tr(out=pTa[0:64, r * 128:(r + 1) * 128],
                                   in_=VF2[:, cc, :].re("p (i r) -> p r i", r=4)[:, r, :], identity=ident)
                            vec("tensor_copy", out=V2c[0:64, :, :].re("p r c -> p (r c)"), in_=pTa[0:64, 0:512])
                            for r in range(16):
                                tr(out=pTa[0:16, (r % 4) * 128:(r % 4 + 1) * 128],
                                   in_=VF[:, cc, :].re("p (i r) -> p r i", r=16)[:, r, :], identity=ident)
                                if r % 4 == 3:
                                    vec("tensor_copy", out=V3c[0:16, r - 3:r + 1, :].re("p a c -> p (a c)"), in_=pTa[0:16, 0:512])
                                yield
                        if not phaseB:
                            if h % 2 == 1:
                                S.dma("gpsimd", out=V2r[j0_2 * 64:(j0_2 + 1) * 64, :, cc * 128:(cc + 1) * 128], in_=V2c[0:64, :, :])
                                S.dma("gpsimd", out=V3r[j0_3 * 16:(j0_3 + 1) * 16, :, cc * 128:(cc + 1) * 128], in_=V3c[0:16, :, :])
                            continue
                        for blk in range(2):
                            qv = Qa[0][hs, cc, blk * 128:(blk + 1) * 128]
                            mm(out=B5[:, (blk * 2) * 128:(blk * 2 + 1) * 128], lhsT=K1[hs, cc, blk * 128:(blk + 1) * 128],
                               rhs=qv, start=True, stop=True)
                            mm(out=B5[:, (blk * 2 + 1) * 128:(blk * 2 + 2) * 128],
                               lhsT=K1[hs, cc, 128 + blk * 128:128 + (blk + 1) * 128], rhs=qv, start=True, stop=True)
                        act(out=pe[:, :], in_=B5[:, 0:512], func=AF.Exp)
                        yield
                        vec("tensor_tensor", out=pp_[:, :].re("p (b e) -> p b e", b=2), in0=pe[:, :].re("p (b e) -> p b e", b=2),
                            in1=cb[:, CB_E1 + h * 256:CB_E1 + (h + 1) * 256].bc(1, [128, 2, 256]), op=ALU.mult)
                        vec("tensor_scalar", out=pp_[:, 0:128], in0=pp_[:, 0:128], scalar1=vprev, scalar2=None, op0=ALU.mult)
                        yield
                        for blk in range(2):
                            mm(out=B6[0:64, blk * 128:(blk + 1) * 128], lhsT=V1[:, blk, vs],
                               rhs=pp_[:, (blk * 2) * 128:(blk * 2 + 1) * 128], start=True, stop=False)
                            mm(out=B6[0:64, blk * 128:(blk + 1) * 128], lhsT=V1[:, blk + 1, vs],
                               rhs=pp_[:, (blk * 2 + 1) * 128:(blk * 2 + 2) * 128], start=False, stop=True)
                        ppv = pp_[:, :].re("p (b c q) -> p b c q", b=2, c=2)
                        mm(out=B6[0:64, 256:512], lhsT=cb[:, CB_ONES:CB_ONES + 64], rhs=ppv[:, :, 0, :], start=True, stop=False)
                        mm(out=B6[0:64, 256:512], lhsT=cb[:, CB_ONES:CB_ONES + 64], rhs=ppv[:, :, 1, :], start=False, stop=True)
                        act(out=accO[:, :], in_=B6[0:64, 0:TT], func=AF.Copy)
                        act(out=accD[:, :], in_=B6[0:64, 256:512], func=AF.Copy)
                        yield
                        for r in range(4):
                            qv = Qa[1][hs, cc, :].re("p (i r) -> p r i", r=4)[:, r, :]
                            mm(out=B5[:, r * 64:(r + 1) * 64], lhsT=K2r[hs, cc, r, :], rhs=qv, start=True, stop=True)
                            mm(out=B5[0:64, 256 + r * 64:256 + (r + 1) * 64],
                               lhsT=K2c[hs, cc, :].re("p (i r) -> p r i", r=4)[:, r, :], rhs=qv, start=True, stop=True)
                        act(out=pe[:, 0:256], in_=B5[:, 0:256], func=AF.Exp)
                        act(out=peb[0:64, :], in_=B5[0:64, 256:512], func=AF.Exp)
                        yield
                        ea = cb[:, CB_EA2 + (j0_2 * 8 + h) * 64:CB_EA2 + (j0_2 * 8 + h + 1) * 64]
                        vec("scalar_tensor_tensor", out=pp_[:, 0:256].re("p (r i) -> p r i", r=4),
                            in0=pe[:, 0:256].re("p (r i) -> p r i", r=4), scalar=vr2, in1=ea.bc(1, [128, 4, 64]),
                            op0=ALU.mult, op1=ALU.mult)
                        eb = cb[0:64, CB_EB2 + h * 64:CB_EB2 + (h + 1) * 64]
                        vec("tensor_tensor", out=ppb[0:64, :].re("p (r i) -> p r i", r=4),
                            in0=peb[0:64, :].re("p (r i) -> p r i", r=4), in1=eb.bc(1, [64, 4, 64]), op=ALU.mult)
                        yield
                        for r in range(4):
                            mm(out=B6[0:64, r * 64:(r + 1) * 64], lhsT=V2r[:, r, vs], rhs=pp_[:, r * 64:(r + 1) * 64],
                               start=True, stop=False)
                            mm(out=B6[0:64, r * 64:(r + 1) * 64], lhsT=V2c[0:64, r, vl], rhs=ppb[0:64, r * 64:(r + 1) * 64],
                               start=False, stop=True)
                        mm(out=B6[0:64, 256:512], lhsT=cb[:, CB_ONES:CB_ONES + 64], rhs=pp_[:, 0:256], start=True, stop=False)
                        mm(out=B6[0:64, 256:512], lhsT=cb[0:64, CB_ONES:CB_ONES + 64], rhs=ppb[0:64, :], start=False, stop=True)
                        vec("tensor_tensor", out=accO[:, :].re("p (i r) -> p r i", r=4), in0=accO[:, :].re("p (i r) -> p r i", r=4),
                            in1=B6[0:64, 0:TT].re("p (r i) -> p r i", r=4), op=ALU.add)
                        vec("tensor_tensor", out=accD[:, :].re("p (i r) -> p r i", r=4), in0=accD[:, :].re("p (i r) -> p r i", r=4),
                            in1=B6[0:64, 256:512].re("p (r i) -> p r i", r=4), op=ALU.add)
                        yield
                        for r in range(16):
                            qv = Qa[2][hs, cc, :].re("p (i r) -> p r i", r=16)[:, r, :]
                            mm(out=B5[:, r * 16:(r + 1) * 16], lhsT=K3r[hs, cc, r, :], rhs=qv, start=True, stop=True)
                            mm(out=B5[0:16, 256 + r * 16:256 + (r + 1) * 16],
                               lhsT=K3c[hs, cc, :].re("p (i r) -> p r i", r=16)[:, r, :], rhs=qv, start=True, stop=True)
                        act(out=pe[:, 0:256], in_=B5[:, 0:256], func=AF.Exp)
                        act(out=peb[0:16, :], in_=B5[0:16, 256:512], func=AF.Exp)
                        yield
                        ea = cb[:, CB_EA3 + (j0_3 * 8 + h) * 16:CB_EA3 + (j0_3 * 8 + h + 1) * 16]
                        vec("scalar_tensor_tensor", out=pp_[:, 0:256].re("p (r i) -> p r i", r=16),
                            in0=pe[:, 0:256].re("p (r i) -> p r i", r=16), scalar=vr3, in1=ea.bc(1, [128, 16, 16]),
                            op0=ALU.mult, op1=ALU.mult)
                        eb = cb[0:16, CB_EB3 + h * 16:CB_EB3 + (h + 1) * 16]
                        vec("tensor_tensor", out=ppb[0:16, :].re("p (r i) -> p r i", r=16),
                            in0=peb[0:16, :].re("p (r i) -> p r i", r=16), in1=eb.bc(1, [16, 16, 16]), op=ALU.mult)
                        yield
                        for r in range(16):
                            mm(out=B6[0:64, r * 16:(r + 1) * 16], lhsT=V3r[:, r, vs], rhs=pp_[:, r * 16:(r + 1) * 16],
                               start=True, stop=False)
                            mm(out=B6[0:64, r * 16:(r + 1) * 16], lhsT=V3c[0:16, r, vl], rhs=ppb[0:16, r * 16:(r + 1) * 16],
                               start=False, stop=True)
                        mm(out=B6[0:64, 256:512], lhsT=cb[:, CB_ONES:CB_ONES + 64], rhs=pp_[:, 0:256], start=True, stop=False)
                        mm(out=B6[0:64, 256:512], lhsT=cb[0:16, CB_ONES:CB_ONES + 64], rhs=ppb[0:16, :], start=False, stop=True)
                        yield
                        if phaseB:
                            vec("tensor_tensor", out=accO[:, :].re("p (i r) -> p r i", r=16),
                                in0=accO[:, :].re("p (i r) -> p r i", r=16),
                                in1=B6[0:64, 0:TT].re("p (r i) -> p r i", r=16), op=ALU.add)
                            vec("tensor_tensor", out=accD[:, :].re("p (i r) -> p r i", r=16),
                                in0=accD[:, :].re("p (i r) -> p r i", r=16),
                                in1=B6[0:64, 256:512].re("p (r i) -> p r i", r=16), op=ALU.add)
                            vec("reciprocal", out=accD[:, :], in_=accD[:, :])
                            vec("tensor_tensor", out=oT[:, h, :], in0=accO[:, :], in1=accD[:, :], op=ALU.mult)
                        if h % 2 == 1:
                            S.dma("gpsimd", out=V2r[j0_2 * 64:(j0_2 + 1) * 64, :, cc * 128:(cc + 1) * 128], in_=V2c[0:64, :, :])
                            S.dma("gpsimd", out=V3r[j0_3 * 16:(j0_3 + 1) * 16, :, cc * 128:(cc + 1) * 128], in_=V3c[0:16, :, :])
                    CK('attn')
                    vec("tensor_copy", out=K1[:, :, 0:128], in_=K1[:, :, TT:TT + 128])
                    vec("tensor_copy", out=V1[:, 0, :], in_=V1[:, 2, :])
                    vec("tensor_copy", out=K2r[:, :, :, j0_2 * 64:(j0_2 + 1) * 64],
                        in_=K2c[:].re("p c (i r) -> p c r i", r=4))
                    vec("tensor_copy", out=K3r[:, :, :, j0_3 * 16:(j0_3 + 1) * 16],
                        in_=K3c[:].re("p c (i r) -> p c r i", r=16))

                ag = th_attn()
                ag_done = [False]

                def step_attn():
                    if ag_done[0]:
                        return
                    try:
                        next(ag)
                    except StopIteration:
                        ag_done[0] = True

                rr_cnt = [0]
                clk = {}

                def advance(g_):
                    S.cur_fin = 0.0
                    try:
                        next(g_)
                    except StopIteration:
                        return False
                    if S.cur_fin > 0.0:
                        clk[id(g_)] = S.cur_fin
                    return True

                def run_ls(gens):
                    gens = list(gens)
                    for g_ in gens:
                        clk.setdefault(id(g_), 0.0)
                    clk.setdefault(id(ag), 0.0)
                    while gens:
                        cands = gens + ([] if ag_done[0] else [ag])
                        g_ = min(cands, key=lambda x: clk[id(x)])
                        if g_ is ag:
                            step_attn_ls()
                        elif not advance(g_):
                            gens.remove(g_)

                def step_attn_ls():
                    if not advance(ag):
                        ag_done[0] = True

                def run_rr(gens):
                    if LISTSCHED and INTERLEAVE:
                        return run_ls(gens)
                    gens = list(gens)
                    while gens:
                        for g_ in list(gens):
                            try:
                                next(g_)
                            except StopIteration:
                                gens.remove(g_)
                        rr_cnt[0] += 1
                        if INTERLEAVE and rr_cnt[0] % ATTN_EVERY == 0:
                            step_attn()

                if LISTSCHED and INTERLEAVE:
                    done = {"f1": 0, "f2": 0, "bk": 0}
                    mk = {"f1": rw_f1, "f2": rw_f2, "bk": rw_back}
                    cur = {"f1": None, "f2": None, "bk": None}
                    sclk = {"f1": 0.0, "f2": 0.0, "bk": 0.0, "at": clk.get(id(ag), 0.0)}

                    def can_start(k, c):
                        if k == "f1":
                            return done["f2"] >= c - 1 and done["bk"] >= c - 2
                        if k == "f2":
                            return done["f1"] >= c + 1 and done["bk"] >= c - 1
                        return done["f2"] >= c + 1

                    while True:
                        cands = []
                        for k in ("f1", "f2", "bk"):
                            if cur[k] is None and done[k] < 8 and can_start(k, done[k]):
                                cur[k] = mk[k](done[k])
                            if cur[k] is not None:
                                cands.append(k)
                        if not ag_done[0]:
                            cands.append("at")
                        if not cands:
                            break
                        k = min(cands, key=lambda x: sclk[x])
                        S.cur_fin = 0.0
                        if k == "at":
                            step_attn()
                        else:
                            try:
                                next(cur[k])
                            except StopIteration:
                                cur[k] = None
                                done[k] += 1
                        if S.cur_fin > 0.0:
                            sclk[k] = S.cur_fin
                    assert done == {"f1": 8, "f2": 8, "bk": 8}, done
                else:
                    for k_ in range(10):
                        gens = []
                        if k_ < 8:
                            gens.append(rw_f1(k_))
                        if 1 <= k_ <= 8:
                            gens.append(rw_f2(k_ - 1))
                        if k_ >= 2:
                            gens.append(rw_back(k_ - 2))
                        if INTERLEAVE:
                            run_rr(gens)
                        else:
                            for g_ in reversed(gens):
                                for _ in g_:
                                    pass
                while not ag_done[0]:
                    step_attn()

                if not phaseB:
                    continue
                for cc in range(8):
                    sa, sbb = ((tv(2), tv(3)), (tv(5), tv(6)))[cc % 2]
                    wpb = wload(CH_PB + cc)
                    wv = wpb[:, :].re("p (a k c) -> p a k c", a=4, c=128)
                    pz = nextp()
                    for kc in range(8):
                        mm(out=pz[:, 0:TT], lhsT=wv[:, 0, kc, :], rhs=hT[:, kc, 1:TT + 1], start=(kc == 0), stop=(kc == 7))
                    sigmoid_to(sa[:, :], pz[:, 0:TT])
                    pz = nextp()
                    for kc in range(8):
                        mm(out=pz[:, 0:TT], lhsT=wv[:, 1, kc, :], rhs=hT[:, kc, 1:TT + 1], start=(kc == 0), stop=(kc == 7))
                    sigmoid_to(sbb[:, :], pz[:, 0:TT])
                    pz = nextp()
                    for kc in range(8):
                        mm(out=pz[:, 0:TT], lhsT=wv[:, 2, kc, :], rhs=yfin[:, kc, :], start=(kc == 0), stop=(kc == 7))
                    vec("tensor_tensor", out=sa[:, :], in0=sa[:, :], in1=pz[:, 0:TT], op=ALU.mult)
                    pz = nextp()
                    for hh_ in range(8):
                        mm(out=pz[:, 0:TT], lhsT=wv[0:64, 3, hh_, :], rhs=oT[:, hh_, :], start=(hh_ == 0), stop=(hh_ == 7))
                    vec("tensor_tensor", out=sbb[:, :], in0=sbb[:, :], in1=pz[:, 0:TT], op=ALU.mult)
                    vec("tensor_tensor", out=mixT[:, cc, :], in0=sa[:, :], in1=sbb[:, :], op=ALU.add)

                def norm_residual(ps_views, gb):
                    for hf in range(2):
                        act(out=junk[:, 0:512], in_=ps_views[hf], func=AF.Square, accum_out=st4[:, 8 + hf:9 + hf])
                    vec("tensor_tensor", out=st4[:, 10:11], in0=st4[:, 8:9], in1=st4[:, 9:10], op=ALU.add)
                    rsqrt_to(st4[:, 4:5], st4[:, 10:11], eps_r, 1.0 / D)
                    for hf in range(2):
                        for qq in range(2):
                            cs_ = slice(hf * 512 + qq * 256, hf * 512 + (qq + 1) * 256)
                            vec("scalar_tensor_tensor", out=utmp[:, :], in0=ps_views[hf][:, qq * 256:(qq + 1) * 256],
                                scalar=st4[:, 4:5], in1=gb[:, cs_], op0=ALU.mult, op1=ALU.mult)
                            vec("tensor_tensor", out=xm[:, sub, cs_], in0=xm[:, sub, cs_], in1=utmp[:, :], op=ALU.add)

                wo = [wload(CH_WOUT + 0), wload(CH_WOUT + 1)]
                for sub in range(NSUB):
                    for hf in range(2):
                        wv = wo[hf][:, :].re("p (k c) -> p k c", c=512)
                        for kc in range(8):
                            mm(out=(B1, B2)[hf][:, :], lhsT=mixT[:, kc, sub * 128:(sub + 1) * 128], rhs=wv[:, kc, :],
                               start=(kc == 0), stop=(kc == 7))
                    norm_residual([B1[:, :], B2[:, :]], gmb)
                rms_rstd(xm, 2)
                norm_transpose(xm, 2, h2T, PD_A2, lambda kc: modf[:, 24 + kc:25 + kc], 0)
                accs = [[B1[:, :], B2[:, :]], [B3[:, :], B5[:, :]]]
                for pg in range(3):
                    nk = 8 if pg < 2 else 6
                    for i4 in range(nk // 2):
                        i = pg * 4 + i4
                        wf_ = wload(CH_FF + i)
                        wv = wf_[:, :].re("p (k c) -> p k c", c=512)
                        for jj in range(2):
                            jl = i4 * 2 + jj
                            pg_ = nextp("f")
                            for kc in range(8):
                                mm(out=pg_[:, 0:TT], lhsT=wv[:, kc, jj * 128:(jj + 1) * 128], rhs=h2T[:, kc, :],
                                   start=(kc == 0), stop=(kc == 7))
                            act(out=sg[:, :], in_=pg_[:, 0:TT], func=AF.Silu)
                            pu = nextp("f")
                            for kc in range(8):
                                mm(out=pu[:, 0:TT], lhsT=wv[:, kc, 256 + jj * 128:256 + (jj + 1) * 128], rhs=h2T[:, kc, :],
                                   start=(kc == 0), stop=(kc == 7))
                            vec("tensor_tensor", out=actT[:, jl, :], in0=sg[:, :], in1=pu[:, 0:TT], op=ALU.mult)
                    for hf in range(2):
                        wf_ = wload(CH_FO + pg * 2 + hf)
                        wv = wf_[:, :].re("p (k c) -> p k c", c=512)
                        for sub in range(NSUB):
                            for kc in range(nk):
                                mm(out=accs[sub][hf], lhsT=actT[:, kc, sub * 128:(sub + 1) * 128], rhs=wv[:, kc, :],
                                   start=(pg == 0 and kc == 0), stop=(pg == 2 and kc == nk - 1))
                for sub in range(NSUB):
                    norm_residual(accs[sub], gfb)
                r0 = (m - PB0) * TT
                S.dma("gpsimd", out=y_d.v(y_d.t[r0:r0 + TT, :].rearrange("(s p) c -> p s c", p=128)), in_=xm[:])


        except _Stop:
            pass
        S.finish([y_d] + finals)
        S.emit()
    return nc


_CACHE = {}


def prep_inputs(x, c, w_mod, b_mod, g_pre_mix, g_post_mix, g_pre_ffn, g_post_ffn, w_in, mu_rkv, mu_lora,
           w0, w1, w2, a0, a1, a2, g1, g2, k_k, k_a, r_k, ln_x_w, ln_x_b, w_o_rwkv, w_o_attn, w_out,
           w_ffn_in, w_ffn_out):
    f = lambda a: np.asarray(a, np.float32)
    x = f(x); c = f(c)
    w_in = f(w_in)[0]; w_modm = f(w_mod)[0]
    bm = f(b_mod)[0].reshape(6, 1024)
    vecs = [bm[0], bm[1], bm[2], bm[3], bm[4], bm[5], f(g_pre_mix)[0], f(g_post_mix)[0], f(g_pre_ffn)[0],
            f(g_post_ffn)[0], f(mu_rkv)[0, 0], f(mu_rkv)[0, 1], f(mu_rkv)[0, 2], f(mu_lora)[0, 0], f(mu_lora)[0, 1],
            f(mu_lora)[0, 2], f(w0)[0], f(a0)[0], f(k_k)[0], f(k_a)[0], f(r_k)[0].reshape(-1), f(ln_x_w)[0],
            f(ln_x_b)[0]]
    wsrc = np.zeros((NCH, 128, 4096), np.float32)
    def put(i, arr3):
        P, K, C = arr3.shape
        v = wsrc[i].reshape(128, -1)
        tmp = np.zeros((128, K, 4096 // K if K in (8,) else C), np.float32) if False else None
        blk = np.zeros((128, K * C), np.float32)
        blk[:P] = arr3.reshape(P, K * C)
        v[:, :K * C] = blk
    for cch in range(8):
        a = np.zeros((128, 8, 512), np.float32)
        for j in range(3):
            a[:, :, j * 128:(j + 1) * 128] = _wchunk(w_in, slice(j * 1024 + cch * 128, j * 1024 + (cch + 1) * 128))
        put(CH_RW + cch, a)
    for g in range(3):
        for j in range(3):
            o = 3072 + j * 1536 + g * 512
            put(CH_AT + g * 3 + j, _wchunk(w_in, slice(o, o + 512)))
    wor = f(w_o_rwkv)[0]; woa = f(w_o_attn)[0]; wout = f(w_out)[0]
    for cc in range(8):
        cs_ = slice(cc * 128, (cc + 1) * 128)
        a = np.zeros((128, 4, 8, 128), np.float32)
        a[:, 0] = _wchunk(w_in, slice(7680 + cc * 128, 7680 + (cc + 1) * 128))
        a[:, 1] = _wchunk(w_in, slice(8704 + cc * 128, 8704 + (cc + 1) * 128))
        a[:, 2] = _wchunk(wor, cs_)
        a[0:64, 3] = woa[:, cs_].reshape(8, 64, 128).transpose(1, 0, 2)
        put(CH_PB + cc, a.reshape(128, 32, 128))
    for hf in range(2):
        put(CH_WOUT + hf, _wchunk(wout, slice(hf * 512, (hf + 1) * 512)))
    wfi = f(w_ffn_in)[0]; wfo = f(w_ffn_out)[0]
    for i in range(11):
        a = np.zeros((128, 8, 512), np.float32)
        a[:, :, 0:256] = _wchunk(wfi, slice(i * 256, (i + 1) * 256))
        a[:, :, 256:512] = _wchunk(wfi, slice(FH + i * 256, FH + (i + 1) * 256))
        put(CH_FF + i, a)
    for pg in range(3):
        nk = 8 if pg < 2 else 6
        for hf in range(2):
            blk = wfo[pg * 1024:pg * 1024 + nk * 128, hf * 512:(hf + 1) * 512]
            put(CH_FO + pg * 2 + hf, blk.reshape(nk, 128, 512).transpose(1, 0, 2))
    wmod = np.ascontiguousarray(
        w_modm.reshape(8, 128, 24, 256).transpose(2, 1, 0, 3).reshape(24, 128, 2048))
    l1 = np.concatenate([f(w1)[0], f(a1)[0], f(g1)[0]], 1)
    l1 = np.ascontiguousarray(l1.reshape(8, 128, 288).transpose(1, 0, 2).reshape(128, 8 * 288))
    l2 = np.zeros((128, 3, 1024), np.float32)
    l2[0:64, 0] = f(w2)[0]; l2[64:128, 0] = f(a2)[0]
    l2[:, 1] = f(g2)[0][0:128]; l2[0:32, 2] = f(g2)[0][128:160]
    l2 = l2.reshape(128, 3072)
    cbt = _host_consts()
    in_maps = []
    for core in range(8):
        b, hh = core // 2, core % 2
        pfm = np.concatenate([_fm(v) for v in vecs] + [_fm(c[b])], 1)
        if hh == 1:
            xvv = x[b]
        else:
            xvv = np.concatenate([np.zeros((T // 2, D), np.float32), x[b, :T // 2]], 0)
        in_maps.append({"xv": np.ascontiguousarray(xvv), "pfm": np.ascontiguousarray(pfm), "wmod": wmod,
                        "wsrc": wsrc, "l1": l1, "l2": l2, "cbt": cbt, "cft": _host_cf(hh)})
    return in_maps


def kernel(**inputs):
    in_maps = prep_inputs(**inputs)
    if "nc" not in _CACHE:
        _CACHE["nc"] = build()
    nc = _CACHE["nc"]
    res = run_bass_kernel_spmd(nc, in_maps, core_ids=list(range(8)))
    out = np.zeros((4, T, D), np.float32)
    for core in range(8):
        b, hh = core // 2, core % 2
        out[b, hh * (T // 2):(hh + 1) * (T // 2)] = res.results[core]["y"]
    return out
```

```python
import math
from contextlib import ExitStack

import numpy as np
import concourse.bass as bass
import concourse.mybir as mybir
from concourse.bass_utils import run_bass_kernel_spmd

F32 = mybir.dt.float32
BF16 = mybir.dt.bfloat16
AF = mybir.ActivationFunctionType
ALU = mybir.AluOpType
AX = mybir.AxisListType

ENGS = ("tensor", "vector", "scalar", "gpsimd", "sync")

T = 8192
D = 1024
TT = 256
NT = T // TT
PB0 = NT // 2
NSUB = TT // 128
FH = 2816
C0 = math.exp(-0.5)
GN_EPS = 64e-5
RMS_EPS = 1e-6
NSLOT = 4
import os
INTERLEAVE = os.environ.get('NOIL') is None
ATTN_EVERY = int(os.environ.get('ATTN_EVERY', '3'))
GPS_OFF = os.environ.get('GPS_OFF', '0') == '1'
LISTSCHED = os.environ.get('LISTSCHED', '1') == '1'
SEM_LAT = float(os.environ.get('SEM_LAT', '300'))
PE_SCALE = float(os.environ.get('PE_SCALE', '1.0'))


class Buf:
    def __init__(self, name, t):
        self.name = name
        self.t = t
        self.writer = None
        self.readers = []
        self.dsem = None
        self.dcnt = 0
        self.psum = False

    def __getitem__(self, idx):
        return View(self, self.t[idx])

    def v(self, ap):
        return View(self, ap)


class SubBuf:
    def __init__(self, buf, col0, ncols=None):
        self.buf = buf
        self.col0 = col0
        self.ncols = ncols

    def __getitem__(self, idx):
        ps, cs = idx
        a = 0 if cs.start is None else cs.start
        e = cs.stop if cs.stop is not None else self.ncols
        assert e is not None
        return View(self.buf, self.buf.t[ps, self.col0 + a:self.col0 + e])


class View:
    def __init__(self, buf, ap):
        self.buf = buf
        self.ap = ap

    def __getitem__(self, idx):
        return View(self.buf, self.ap[idx])

    def re(self, pat, **kw):
        return View(self.buf, self.ap.rearrange(pat, **kw))

    def bc(self, axis, shape):
        return View(self.buf, self.ap.unsqueeze(axis).to_broadcast(list(shape)))


def _unw(x):
    return x.ap if isinstance(x, View) else x


class Sched:
    def __init__(self, nc, stack):
        self.nc = nc
        self.stack = stack
        self.q = {e: [] for e in ENGS}
        self.waited = {e: {} for e in ENGS}
        self.dma_sems = []
        self.fin = {}
        self.eng_free = {e: 0.0 for e in ENGS}
        self.cur_fin = 0.0

    def _est(self, eng, tok, deps_tokens, dur):
        ready = 0.0
        for t_ in deps_tokens:
            f_ = self.fin.get(t_)
            if f_ is not None and f_ > ready:
                ready = f_
        start = max(ready + SEM_LAT, self.eng_free[eng])
        fin = start + dur
        self.eng_free[eng] = fin if eng != "sync" and eng != "gpsimd" else start + 60.0
        self.fin[tok] = fin
        if len(self.fin) > 60000:
            ks = list(self.fin.keys())[:30000]
            for k_ in ks:
                del self.fin[k_]
        if fin > self.cur_fin:
            self.cur_fin = fin

    def sb(self, name, shape, dt):
        t = self.stack.enter_context(self.nc.sbuf_tensor("s_" + name, list(shape), dt))
        return Buf(name, t)

    def ps(self, name, shape, dt=F32):
        t = self.stack.enter_context(self.nc.psum_tensor("p_" + name, list(shape), dt))
        return Buf(name, t)

    def dram(self, name, shape, dt, kind):
        t = self.nc.dram_tensor(name, list(shape), dt, kind=kind).ap()
        return Buf(name, t)

    def _deps(self, eng, reads, writes):
        deps = {}

        def add(tok):
            if tok is None:
                return
            k, v = tok
            if deps.get(k, 0) < v:
                deps[k] = v

        for b in reads:
            add(b.writer)
            if b.psum:
                for r in b.readers:
                    if r[0] != eng:
                        add(r)
        for b in writes:
            add(b.writer)
            for r in b.readers:
                add(r)
        waits = []
        for k, v in deps.items():
            if k == "tensor" and eng == "tensor":
                continue
            if self.waited[eng].get(k, 0) >= v:
                continue
            self.waited[eng][k] = v
            waits.append((k, v))
            if isinstance(k, str):
                self.q[k][v - 1][2] = True
        return waits

    def _commit(self, tok, reads, writes):
        for b in writes:
            b.writer = tok
            b.readers = []
        for b in reads:
            if b in writes:
                continue
            b.readers.append(tok)
            if len(b.readers) > 48:
                d = {}
                for k, v in b.readers:
                    if d.get(k, 0) < v:
                        d[k] = v
                b.readers = list(d.items())

    def op(self, eng, meth, **kw):
        writes, reads = [], []
        for k, v in kw.items():
            if isinstance(v, View):
                if k in ("out", "accum_out", "ap"):
                    if v.buf not in writes:
                        writes.append(v.buf)
                else:
                    if v.buf not in reads:
                        reads.append(v.buf)
        dep_toks = [b_.writer for b_ in reads + writes if b_.writer is not None]
        for b_ in writes:
            dep_toks.extend(b_.readers)
        waits = self._deps(eng, reads, writes)
        if eng == "tensor":
            src = kw.get("lhsT", kw.get("in_"))
            lo = src.ap.base_partition()
            rows = (lo, lo + src.ap.partition_size())
            ob = kw["out"].buf
            prev = getattr(ob, "pe_rows", None)
            if prev is not None and ob.writer is not None and ob.writer[0] == "tensor" and \
                    (rows[1] <= prev[0] or prev[1] <= rows[0]):
                k, v = ob.writer
                if self.waited[eng].get(k, 0) < v:
                    self.waited[eng][k] = v
                    waits.append((k, v))
                    self.q[k][v - 1][2] = True
            ob.pe_rows = rows
        args = {k: _unw(v) for k, v in kw.items()}
        fn = lambda e, m=meth, a=args: getattr(e, m)(**a)
        self.q[eng].append([waits, fn, False, None])
        tok = (eng, len(self.q[eng]))
        o_ = kw.get("out", kw.get("ap"))
        try:
            fsz = o_.ap.free_size()
        except Exception:
            fsz = 256
        if eng == "tensor":
            n_ = kw["rhs"].ap.free_size() if "rhs" in kw else 128
            dur = (max(n_, 64) / 1.2 + 30.0) * PE_SCALE
        else:
            dur = (200.0 + 0.65 * fsz) if eng == "scalar" else (150.0 + 0.75 * fsz)
        self._est(eng, tok, dep_toks, dur)
        self._commit(tok, reads, writes)
        return tok

    def dma(self, eng, out, in_, **kw):
        sb = out.buf
        if sb.dsem is None:
            sb.dsem = ("dma", len(self.dma_sems))
            self.dma_sems.append(sb.name)
        dep_toks = [b_.writer for b_ in (in_.buf, out.buf) if b_.writer is not None] + list(out.buf.readers)
        waits = self._deps(eng, [in_.buf], [out.buf])
        sb.dcnt += 16
        tok = (sb.dsem, sb.dcnt)
        try:
            nbytes = out.ap.nbytes()
        except Exception:
            nbytes = 1 << 20
        self._est(eng, tok, dep_toks, 2500.0 + nbytes / 150.0)
        a = dict(out=out.ap, in_=in_.ap, **kw)
        fn = lambda e, a=a: e.dma_start(**a)
        self.q[eng].append([waits, fn, False, sb.dsem])
        self._commit(tok, [in_.buf], [out.buf])
        return tok

    def finish(self, final_bufs):
        waits = self._deps("sync", final_bufs, [])
        self.q["sync"].append([waits, None, False, None])

    def emit(self):
        nc = self.nc
        st = self.stack
        esem = {e: st.enter_context(nc.semaphore("es_" + e)) for e in ENGS}
        dsem = [st.enter_context(nc.semaphore("ds%d" % i)) for i in range(len(self.dma_sems))]
        cum = {}
        for e in ENGS:
            c = 0
            arr = []
            for it in self.q[e]:
                if it[2]:
                    c += 1
                arr.append(c)
            cum[e] = arr

        def semval(k, v):
            if isinstance(k, str):
                return esem[k], cum[k][v - 1]
            return dsem[k[1]], v

        block = st.enter_context(nc.Block())

        def run(e, eng):
            for waits, fn, sig, dk in self.q[e]:
                for k, v in waits:
                    s, val = semval(k, v)
                    eng.wait_ge(s, val)
                if fn is None:
                    continue
                ins = fn(eng)
                if dk is not None:
                    ins.then_inc(dsem[dk[1]], 16)
                elif sig:
                    ins.then_inc(esem[e], 1)

        @block.tensor
        def _(eng):
            run("tensor", eng)

        @block.vector
        def _(eng):
            run("vector", eng)

        @block.scalar
        def _(eng):
            run("scalar", eng)

        @block.gpsimd
        def _(eng):
            run("gpsimd", eng)

        @block.sync
        def _(eng):
            run("sync", eng)


def _alibi_slopes(n):
    def pow2(m):
        start = 2.0 ** (-8.0 / m)
        return [start ** (i + 1) for i in range(m)]
    if math.log2(n).is_integer():
        s = pow2(n)
    else:
        p = 2 ** int(math.floor(math.log2(n)))
        s = pow2(p) + pow2(2 * p)[0::2][: n - p]
    return sorted(s, reverse=True)


(PV_SHM, PV_SCM, PV_GTM, PV_SHF, PV_SCF, PV_GTF, PV_GPM, PV_GQM, PV_GPF, PV_GQF,
 PV_MUR, PV_MUK, PV_MUV, PV_MUW, PV_MUA, PV_MUG, PV_W0, PV_A0, PV_KK, PV_KA, PV_RK,
 PV_LNW, PV_LNB, PV_C) = range(24)
NPV = 24

CB_ID = 0
CB_ONESBD = 128
CB_ONES = 256
CB_MT4 = 320
CB_ML4 = 832
CB_E1 = 1344
CB_EA2 = CB_E1 + 8 * 256
CB_EB2 = CB_EA2 + 2 * 8 * 64
CB_EA3 = CB_EB2 + 8 * 64
CB_EB3 = CB_EA3 + 8 * 8 * 16
NCB = CB_EB3 + 8 * 16
CF_MSK = 0
CF_VM = 256
CF_EPS = CF_VM + 128
CF_IDF = CF_EPS + 4
NCF = CF_IDF + 128

CH_RW = 0
CH_AT = 8
CH_PB = 17
CH_WOUT = 25
CH_FF = 27
CH_FO = 38
NCH = 44


def _host_consts():
    sl = np.asarray(_alibi_slopes(24), np.float64).reshape(3, 8)
    cb = np.zeros((128, NCB), np.float32)
    p = np.arange(128)
    cb[:, CB_ID:CB_ID + 128] = np.eye(128)
    cb[:, CB_ONESBD:CB_ONESBD + 128] = (p[:, None] // 64 == p[None, :] // 64)
    cb[:, CB_ONES:CB_ONES + 64] = 1.0
    same = (p[:, None] // 64 == p[None, :] // 64)
    su = same & (p[:, None] < p[None, :])
    iu = same & (p[:, None] <= p[None, :])
    slo = same & (p[:, None] > p[None, :])
    cb[:, CB_MT4:CB_MT4 + 512] = np.concatenate([su, iu, su, iu], 1)
    cb[:, CB_ML4:CB_ML4 + 512] = np.concatenate([slo] * 4, 1)
    k = p[:, None].astype(np.float64)
    q = p[None, :].astype(np.float64)
    for h in range(8):
        dpv = q - k + 128
        e_prev = np.where(dpv <= 128, np.exp(-sl[0, h] * dpv), 0.0)
        dcu = q - k
        e_cur = np.where(dcu >= 0, np.exp(-sl[0, h] * np.maximum(dcu, 0)), 0.0)
        cb[:, CB_E1 + h * 256: CB_E1 + h * 256 + 128] = e_prev
        cb[:, CB_E1 + h * 256 + 128: CB_E1 + h * 256 + 256] = e_cur
    i64 = np.arange(64)[None, :].astype(np.float64)
    for rot in range(2):
        for h in range(8):
            j = p // 64
            pp = (p % 64).astype(np.float64)
            a = ((rot - j - 1) % 2) + 1
            dl = 64.0 * a[:, None] + i64 - pp[:, None]
            e = np.where(dl <= 128, np.exp(-sl[1, h] * 4.0 * dl), 0.0)
            o = CB_EA2 + (rot * 8 + h) * 64
            cb[:, o:o + 64] = e
    for h in range(8):
        kk = np.arange(64)[:, None].astype(np.float64)
        dl = i64 - kk
        e = np.where(dl >= 0, np.exp(-sl[1, h] * 4.0 * np.maximum(dl, 0)), 0.0)
        o = CB_EB2 + h * 64
        cb[0:64, o:o + 64] = e
    i16 = np.arange(16)[None, :].astype(np.float64)
    for rot in range(8):
        for h in range(8):
            j = p // 16
            pp = (p % 16).astype(np.float64)
            a = ((rot - j - 1) % 8) + 1
            dl = 16.0 * a[:, None] + i16 - pp[:, None]
            e = np.where(dl <= 128, np.exp(-sl[2, h] * 16.0 * dl), 0.0)
            o = CB_EA3 + (rot * 8 + h) * 16
            cb[:, o:o + 16] = e
    for h in range(8):
        kk = np.arange(16)[:, None].astype(np.float64)
        dl = i16 - kk
        e = np.where(dl >= 0, np.exp(-sl[2, h] * 16.0 * np.maximum(dl, 0)), 0.0)
        o = CB_EB3 + h * 16
        cb[0:16, o:o + 16] = e
    return cb


def _host_cf(hh):
    cf = np.zeros((128, NCF), np.float32)
    m = np.ones((128, 256), np.float32)
    m[:, 0::64] = 0.0
    cf[:, CF_MSK:CF_MSK + 256] = m
    valid = lambda t: 0.0 if t < 0 else (1.0 if (hh == 1 or t >= PB0) else 0.0)
    p = np.arange(128)
    for t in range(NT):
        cf[:, CF_VM + t] = valid(t)
        cf[:, CF_VM + 32 + t] = valid(t - 1)
        j = p // 64
        a = ((t - j - 1) % 2) + 1
        cf[:, CF_VM + 64 + t] = [valid(t - aa) for aa in a]
        j = p // 16
        a = ((t - j - 1) % 8) + 1
        cf[:, CF_VM + 96 + t] = [valid(t - aa) for aa in a]
    cf[:, CF_EPS] = RMS_EPS
    cf[:, CF_EPS + 1] = GN_EPS
    cf[:, CF_EPS + 3] = 1.0
    cf[:, CF_IDF:CF_IDF + 128] = np.eye(128)
    return cf


def _fm(v):
    return np.ascontiguousarray(v.reshape(8, 128).T)


def _wchunk(w, cols):
    return w[:, cols].reshape(8, 128, -1).transpose(1, 0, 2)


class _Stop(Exception):
    pass


def build(nt=NT, dbg=None, dbg_tile=0, dbg_c=0, stop=None):
    nc = bass.Bass("TRN2", target_bir_lowering=False)
    with ExitStack() as st:
        S = Sched(nc, st)
        finals = []

        def CK(name):
            if stop == name:
                raise _Stop()

        def DBG(name, view, m=None, c=None):
            if not dbg or name not in dbg:
                return
            if m is not None and m != dbg_tile:
                return
            if c is not None and c != dbg_c:
                return
            shp = list(view.ap.shape)
            dd = S.dram("dbg_" + name, shp, view.ap.dtype, "ExternalOutput")
            S.dma("gpsimd", out=dd[:], in_=view)
            finals.append(dd)
        xv = S.dram("xv", [T, D], F32, "ExternalInput")
        pfm_d = S.dram("pfm", [128, NPV * 8], F32, "ExternalInput")
        wmod_d = S.dram("wmod", [24, 128, 2048], F32, "ExternalInput")
        wsrc = S.dram("wsrc", [NCH, 128, 4096], F32, "ExternalInput")
        l1_d = S.dram("l1", [128, 8 * 288], F32, "ExternalInput")
        l2_d = S.dram("l2", [128, 3 * 1024], F32, "ExternalInput")
        cb_d = S.dram("cbt", [128, NCB], F32, "ExternalInput")
        cf_d = S.dram("cft", [128, NCF], F32, "ExternalInput")
        y_d = S.dram("y", [T // 2, D], F32, "ExternalOutput")
        wscr_all = S.dram("wscr", [NCH, 128, 4096], BF16, "Internal")
        wscr = [Buf("wscr%d" % i, wscr_all.t[i]) for i in range(NCH)]

        cb = S.sb("cb", [128, NCB], BF16)
        cf = S.sb("cf", [128, NCF], F32)
        pf = S.sb("pf", [128, NPV * 8], F32)
        pd = S.sb("pd", [128, 12 * 8], F32)
        gmb = S.sb("gmb", [128, 1024], BF16)
        gfb = S.sb("gfb", [128, 1024], BF16)
        l1a = S.sb("l1a", [128, 8, 288], BF16)
        l1b = S.sb("l1b", [128, 8, 288], BF16)
        l2 = S.sb("l2", [128, 3, 1024], BF16)
        ring = [S.sb("ring%d" % i, [128, 4096], BF16) for i in range(NSLOT)]
        xt1 = S.sb("xt", [128, NSUB, 1024], F32)
        xt = [xt1, xt1]
        nb = S.sb("nb", [128, NSUB, 1024], BF16)
        junk = nb[:, 0, :]
        st4 = S.sb("st4", [128, 16], F32)
        hT = S.sb("hT", [128, 8, TT + 1], BF16)
        h2T = S.sb("h2T", [128, 8, TT], BF16)
        mixT = h2T
        Zf = S.sb("Zf", [128, 8, 64], F32)
        Zb = S.sb("Zb", [128, 8, 2, 64], BF16)
        hal = S.sb("hal", [128, 8, 3], F32)
        tp = [S.sb("tp%d" % i, [128, TT + 1], F32) if i != 7 else None for i in range(12)]
        tp[7] = tp[6]
        tv = lambda i: tp[i][:, 0:TT]
        pj = [tp[0], tp[0], tp[0]]
        tmpd = tv(1)
        rkv = [tv(2), tv(3), tv(4)]
        sw = tv(5); asig = tv(6); gg = tv(7); cs = tv(0); cm = tv(1)
        Ep = tv(8); En = tv(9); Em = tv(10); rinv = tv(1); kkb = tv(11); ff = tv(0)
        kmod = tv(5); bv = tv(1); bon = tv(6); yln = tv(9); ysq = tv(10)
        sqb = S.sb("sqb", [128, TT], BF16)
        ARs = [S.sb("AR%d" % i, [128, NSUB, 2, 128], BF16) for i in range(3)]
        Bts = [S.sb("Bt%d" % i, [128, TT], BF16) for i in range(2)]
        Kts = [S.sb("Kt%d" % i, [128, TT], BF16) for i in range(2)]
        vbfs = [S.sb("vbf%d" % i, [128, TT], BF16) for i in range(2)]
        Bpads = [S.sb("Bpad%d" % i, [128, NSUB, 2, 128], BF16) for i in range(2)]
        Kpads = [S.sb("Kpad%d" % i, [128, NSUB, 2, 128], BF16) for i in range(2)]
        Vtms = [S.sb("Vtm%d" % i, [128, NSUB, 128], BF16) for i in range(2)]
        AMs = [S.sb("AM%d" % i, [128, 4, 512], BF16) for i in range(2)]
        TTfs = [S.sb("TTf%d" % i, [128, 4, 128], BF16) for i in range(2)]
        gbs = [S.sb("gb%d" % i, [128, TT], BF16) for i in range(3)]
        pcss = [S.sb("pcs%d" % i, [128, 4], F32) for i in range(3)]
        ysqB = S.sb("ysqB", [128, TT], F32)
        L0 = S.sb("L0", [128, 4, 128], BF16)
        LP = [S.sb("LP%d" % i, [128, 4, 128], BF16) for i in range(2)]
        LT = [S.sb("LT%d" % i, [128, 4, 128], BF16) for i in range(2)]
        SS = [S.sb("SS%d" % i, [128, 4, 128], BF16) for i in range(2)]
        Xb = S.sb("Xb", [128, 128], BF16)
        Ub = S.sb("Ub", [128, 128], BF16)
        ztmp = S.sb("ztmp", [128, 64], F32)
        Ytm = S.sb("Ytm", [128, NSUB, 128], F32)
        ynb = S.sb("ynb", [128, NSUB, 128], BF16)
        gst = S.sb("gst", [128, 32], F32)
        lw = S.sb("lw", [128, TT], BF16)
        lga = S.sb("lga", [128, TT], BF16)
        lgb = S.sb("lgb", [32, TT], BF16)
        yfin = S.sb("yfin", [128, 8, TT], BF16)
        _nbf = nb[:].re("p s c -> p (s c)")
        Qa = [h2T[:, 0:4, :], h2T[:, 4:8, :], _nbf[:, 0:1024].re("p (c t) -> p c t", t=TT)]
        K1 = S.sb("K1", [128, 4, 128 + TT], BF16)
        V1 = S.sb("V1", [128, 3, 512], BF16)
        K2c = S.sb("K2c", [128, 4, TT], BF16)
        K2r = S.sb("K2r", [128, 4, 4, 128], BF16)
        V2c = S.sb("V2c", [64, 4, 128], BF16)
        V2r = S.sb("V2r", [128, 4, 512], BF16)
        K3c = S.sb("K3c", [128, 4, TT], BF16)
        K3r = S.sb("K3r", [128, 4, 16, 128], BF16)
        V3c = S.sb("V3c", [16, 16, 128], BF16)
        V3r = S.sb("V3r", [128, 16, 512], BF16)
        VF = _nbf[:, 1024:2048].re("p (c t) -> p c t", t=TT)
        pe = S.sb("pe", [128, 512], BF16)
        pp_ = S.sb("pp", [128, 512], BF16)
        peb = SubBuf(pe, 256, 256)
        ppb = SubBuf(pp_, 256, 256)
        accO = S.sb("accO", [64, TT], F32)
        accD = S.sb("accD", [64, TT], F32)
        oT = S.sb("oT", [64, 8, TT], BF16)
        sa = tv(2); sbb = tv(3); sg = tv(4); utmp = tv(10)
        actT = S.sb("actT", [128, 8, TT], BF16)
        VF2 = actT[:, 0:4, :]
        wst = xt1[:].re("p s c -> p (s c)")

        _b0 = S.ps("b0", [128, 512])
        _b4 = S.ps("b4", [128, 512])
        _bS = S.ps("bS", [128, 1024])
        B3 = S.ps("b3", [128, 512])
        B5 = S.ps("b5", [128, 512])
        B6 = S.ps("b6", [128, 512])
        _pT = S.ps("pT", [128, 1024], BF16)
        R0 = SubBuf(_b0, 0); R1 = SubBuf(_b0, 256)
        Q0 = SubBuf(_b4, 0); Q1 = SubBuf(_b4, 256)
        B1 = Buf("B1", _bS.t[:, 0:512]); B2 = Buf("B2", _bS.t[:, 512:1024])
        pTr = SubBuf(_pT, 0); pTa = SubBuf(_pT, 512)
        for b_ in (_b0, _b4, B1, B2, B3, B5, B6, _pT):
            b_.psum = True
        pC = B3
        trB = [View(B1, B1.t[:, :].bitcast(BF16)), View(B2, B2.t[:, :].bitcast(BF16))]
        trC = View(B3, B3.t[:, :].bitcast(BF16))
        trot = [View(_pT, _pT.t[:, 0:512]), View(B6, B6.t[:, :].bitcast(BF16)), View(B5, B5.t[:, :].bitcast(BF16))]
        prot = {"r": [_b0, B1, B2], "a": [_b4, B5, B6], "x": [_b0, _b4, B1, B2, B3, B6], "f": [_b0, _b4, B6]}
        prot_i = {"r": 0, "a": 0, "x": 0, "f": 0}

        def nextp(k="x"):
            prot_i[k] = (prot_i[k] + 1) % len(prot[k])
            return prot[k][prot_i[k]]

        mm = lambda **kw: S.op("tensor", "matmul", **kw)
        tr = lambda **kw: S.op("tensor", "transpose", **kw)
        act = lambda **kw: S.op("scalar", "activation", **kw)
        vec = lambda m, **kw: S.op("vector", m, **kw)
        gps = lambda m, **kw: S.op("gpsimd", m, **kw)

        def sigmoid_to(dst, src, nbias=None, scale=1.0):
            if nbias is None:
                act(out=dst, in_=src, func=AF.Exp, scale=-scale)
            else:
                act(out=dst, in_=src, func=AF.Exp, scale=-scale, bias=nbias)
            act(out=dst, in_=dst, func=AF.Ln, bias=one_c_for(dst))
            act(out=dst, in_=dst, func=AF.Exp, scale=-1.0)

        def one_c_for(v):
            lo = v.ap.base_partition()
            n = v.ap.partition_size()
            return cf[lo:lo + n, CF_EPS + 3:CF_EPS + 4]

        def rsqrt_to(dst, src, bias_ap, scale=1.0):
            act(out=dst, in_=src, func=AF.Ln, bias=bias_ap, scale=scale)
            act(out=dst, in_=dst, func=AF.Exp, scale=-0.5)

        ident = cb[:, CB_ID:CB_ID + 128]
        identf = cf[:, CF_IDF:CF_IDF + 128]
        onesbd = cb[:, CB_ONESBD:CB_ONESBD + 128]
        eps_r = cf[:, CF_EPS:CF_EPS + 1]
        eps_g = cf[:, CF_EPS + 1:CF_EPS + 2]
        zero_c = cf[:, CF_EPS + 2:CF_EPS + 3]
        one_c = cf[:, CF_EPS + 3:CF_EPS + 4]

        def pv(i, kc):
            return pf[:, i * 8 + kc: i * 8 + kc + 1]

        def pdv(i, kc):
            return pd[:, i * 8 + kc: i * 8 + kc + 1]
        PD_A1, PD_A2, PD_GM, PD_GF, PD_OMK, PD_OMR, PD_OMKm, PD_OMV = range(8)

        try:
            S.dma("gpsimd", out=cb[:, :], in_=cb_d[:, :])
            S.dma("sync", out=cf[:, :], in_=cf_d[:, :])
            S.dma("sync", out=pf[:, :], in_=pfm_d[:, :])
            for i in range(NCH):
                S.dma("gpsimd", out=wscr[i][:, :], in_=wsrc[i])
            CK('dma0')
            for b_ in (Zf, Zb, hal, Bpads[0], Bpads[1], Kpads[0], Kpads[1], K1, K2r, V2r, K3r, V3r, V1, hT, Xb, Ub, Vtms[0], Vtms[1]):
                gps("memset", ap=b_[:], constant=0.0)
            CK('memset')
            for half in range(2):
                S.dma("sync", out=wst[:, 0:4 * 288], in_=l1_d[:, half * 4 * 288:(half + 1) * 4 * 288])
                w1v = wst[:, 0:4 * 288].re("p (k c) -> p k c", c=288)
                for k4 in range(4):
                    kc = half * 4 + k4
                    for (lo, hi, mui) in ((0, 64, PV_MUW), (64, 128, PV_MUA), (128, 288, PV_MUG)):
                        vec("tensor_scalar", out=l1b[:, kc, lo:hi], in0=w1v[:, k4, lo:hi], scalar1=pv(mui, kc),
                            scalar2=None, op0=ALU.mult)
                        vec("tensor_tensor", out=l1a[:, kc, lo:hi], in0=w1v[:, k4, lo:hi], in1=l1b[:, kc, lo:hi],
                            op=ALU.subtract)
            for half in range(2):
                S.dma("sync", out=wst[:, 0:1536], in_=l2_d[:, half * 1536:(half + 1) * 1536])
                vec("tensor_copy", out=l2[:].re("p a c -> p (a c)")[:, half * 1536:(half + 1) * 1536], in_=wst[:, 0:1536])
            CK('lora0')
            for j in range(24):
                S.dma("sync", out=wst[:, 0:2048], in_=wmod_d[j])
                wv = wst[:, 0:2048].re("p (k c) -> p k c", c=256)
                for cc in range(2):
                    col = j * 2 + cc
                    for kc in range(8):
                        mm(out=B1[:, col:col + 1], lhsT=wv[:, kc, cc * 128:(cc + 1) * 128], rhs=pv(PV_C, kc),
                           start=(kc == 0), stop=(kc == 7))
            modf = S.sb("modf", [128, 48], F32)
            vec("tensor_tensor", out=modf[:, :], in0=B1[:, 0:48], in1=pf[:, 0:48], op=ALU.add)
            for kc in range(8):
                vec("scalar_tensor_tensor", out=pdv(PD_A1, kc), in0=modf[:, 8 + kc:9 + kc], scalar=1.0,
                    in1=pv(PV_GPM, kc), op0=ALU.add, op1=ALU.mult)
                vec("scalar_tensor_tensor", out=pdv(PD_A2, kc), in0=modf[:, 32 + kc:33 + kc], scalar=1.0,
                    in1=pv(PV_GPF, kc), op0=ALU.add, op1=ALU.mult)
                vec("tensor_tensor", out=pdv(PD_GM, kc), in0=modf[:, 16 + kc:17 + kc], in1=pv(PV_GQM, kc), op=ALU.mult)
                vec("tensor_tensor", out=pdv(PD_GF, kc), in0=modf[:, 40 + kc:41 + kc], in1=pv(PV_GQF, kc), op=ALU.mult)
                vec("tensor_scalar", out=pdv(PD_OMK, kc), in0=pv(PV_KA, kc), scalar1=-1.0, scalar2=1.0,
                    op0=ALU.mult, op1=ALU.add)
                vec("tensor_scalar", out=pdv(5, kc), in0=pv(PV_W0, kc), scalar1=-1.0, scalar2=None, op0=ALU.mult)
                vec("tensor_scalar", out=pdv(6, kc), in0=pv(PV_A0, kc), scalar1=-1.0, scalar2=None, op0=ALU.mult)
            dg = tp[0][:, 0:128]
            onesf = tp[1][:, 0:128]
            gps("memset", ap=onesf[:, :], constant=1.0)
            for (pdi, dst) in ((PD_GM, gmb), (PD_GF, gfb)):
                for kc in range(8):
                    vec("tensor_scalar", out=dg[:, :], in0=identf, scalar1=pdv(pdi, kc), scalar2=None, op0=ALU.mult)
                    pz = nextp()
                    mm(out=pz[:, 0:128], lhsT=onesf[:, :], rhs=dg[:, :], start=True, stop=True)
                    act(out=dst[:, kc * 128:(kc + 1) * 128], in_=pz[:, 0:128], func=AF.Copy)

            CK('startup')
            ring_i = [0]

            ring_sets = {"r": ring[0:2], "a": ring[2:4], "x": ring}
            ring_k = {"r": 0, "a": 0, "x": 0}

            def wload(ch, k="x"):
                s = ring_sets[k][ring_k[k] % len(ring_sets[k])]
                ring_k[k] += 1
                S.dma("sync", out=s[:, :], in_=wscr[ch][:, :])
                return s

            def rms_rstd(src3, dst_cols, nsub=NSUB):
                for sub in range(nsub):
                    act(out=junk[:, :], in_=src3[:, sub, :], func=AF.Square,
                        accum_out=st4[:, 8 + sub:9 + sub])
                rsqrt_to(st4[:, dst_cols:dst_cols + nsub], st4[:, 8:8 + nsub], eps_r, 1.0 / D)

            def norm_transpose(xsrc, rcol, dstT, a_idx, b_view_fn, halo):
                for sub in range(NSUB):
                    vec("tensor_scalar", out=nb[:, sub, :], in0=xsrc[:, sub, :], scalar1=st4[:, rcol + sub:rcol + sub + 1],
                        scalar2=None, op0=ALU.mult)
                for kc in range(8):
                    tgt = trot[kc % 3]
                    for sub in range(NSUB):
                        tr(out=tgt[:, sub * 128:(sub + 1) * 128], in_=nb[:, sub, kc * 128:(kc + 1) * 128], identity=ident)
                    act(out=dstT[:, kc, halo:halo + TT], in_=tgt[:, 0:TT], func=AF.Identity,
                        scale=pdv(a_idx, kc), bias=b_view_fn(kc))

            for m in range(nt):
                phaseB = m >= PB0
                prot["r"] = [_b0] if m >= PB0 - 8 else [_b0, _b4, B5, B6]
                xm = xt[m % 2]
                vcur = cf[:, CF_VM + m:CF_VM + m + 1]
                vprev = cf[:, CF_VM + 32 + m:CF_VM + 33 + m]
                vr2 = cf[:, CF_VM + 64 + m:CF_VM + 65 + m]
                vr3 = cf[:, CF_VM + 96 + m:CF_VM + 97 + m]
                S.dma("gpsimd", out=xm[:], in_=xv.v(xv.t[m * TT:(m + 1) * TT, :].rearrange("(s p) c -> p s c", p=128)))
                if m > 0:
                    vec("tensor_scalar", out=hT[:, :, 0:1], in0=hT[:, :, TT:TT + 1],
                        scalar1=cf[:, CF_VM + m - 1:CF_VM + m], scalar2=None, op0=ALU.mult)
                rms_rstd(xm, 0)
                norm_transpose(xm, 0, hT, PD_A1, lambda kc: pf[:, PV_SHM * 8 + kc:PV_SHM * 8 + kc + 1]
                               if False else modf[:, kc:kc + 1], 1)

                CK('stage1')
                pz = nextp()
                for kc in range(8):
                    mm(out=pz[:, 0:TT], lhsT=l1a[:, kc, 0:128], rhs=hT[:, kc, 1:TT + 1], start=(kc == 0), stop=False)
                    mm(out=pz[:, 0:TT], lhsT=l1b[:, kc, 0:128], rhs=hT[:, kc, 0:TT], start=False, stop=(kc == 7))
                sigmoid_to(tp[11][0:64, 0:TT], pz[0:64, 0:TT], None, 2.0)
                vec("tensor_scalar", out=lw[0:64, :], in0=tp[11][0:64, 0:TT], scalar1=2.0, scalar2=-1.0, op0=ALU.mult, op1=ALU.add)
                act(out=lw[64:128, :], in_=pz[64:128, 0:TT], func=AF.Copy)
                if phaseB:
                    pz = nextp()
                    for kc in range(8):
                        mm(out=pz[:, 0:TT], lhsT=l1a[:, kc, 128:256], rhs=hT[:, kc, 1:TT + 1], start=(kc == 0), stop=False)
                        mm(out=pz[:, 0:TT], lhsT=l1b[:, kc, 128:256], rhs=hT[:, kc, 0:TT], start=False, stop=(kc == 7))
                    sigmoid_to(tp[11][:, 0:TT], pz[:, 0:TT])
                    act(out=lga[:, :], in_=tp[11][:, 0:TT], func=AF.Copy)
                    pz = nextp()
                    for kc in range(8):
                        mm(out=pz[0:32, 0:TT], lhsT=l1a[:, kc, 256:288], rhs=hT[:, kc, 1:TT + 1], start=(kc == 0), stop=False)
                        mm(out=pz[0:32, 0:TT], lhsT=l1b[:, kc, 256:288], rhs=hT[:, kc, 0:TT], start=False, stop=(kc == 7))
                    sigmoid_to(tp[11][0:32, 0:TT], pz[0:32, 0:TT])
                    act(out=lgb[0:32, :], in_=tp[11][0:32, 0:TT], func=AF.Copy)

                CK('lora1')
                def rw_f1(c0):
                    for c in (c0,):
                        AR = ARs[c % 3]; AM = AMs[c % 2]; Vtm = Vtms[c % 2]; Bpad = Bpads[c % 2]; Kpad = Kpads[c % 2]
                        TTf = TTfs[c % 2]; gb = gbs[c % 3]; pcs = pcss[c % 3]
                        Bt = Bts[c % 2]; Kt = Kts[c % 2]; vbf = vbfs[c % 2]
                        csl = slice(c * 128, (c + 1) * 128)
                        wr = wload(CH_RW + c, 'r')
                        wrv = wr[:, :].re("p (k c) -> p k c", c=512)
                        for j in ((0, 1, 2) if m >= PB0 - 1 else (1, 2)):
                            pz = nextp("r")
                            for kc in range(8):
                                mm(out=pz[:, 0:TT], lhsT=wrv[:, kc, j * 128:(j + 1) * 128], rhs=hT[:, kc, 1:TT + 1],
                                   start=(kc == 0), stop=(kc == 7))
                            vec("tensor_copy", out=pj[j][:, 0:1], in_=hal[:, c, j:j + 1])
                            yield
                            act(out=pj[j][:, 1:TT + 1], in_=pz[:, 0:TT], func=AF.Copy)
                            yield
                            vec("tensor_scalar", out=hal[:, c, j:j + 1], in0=pj[j][:, TT:TT + 1], scalar1=vcur,
                                scalar2=None, op0=ALU.mult)
                            yield
                            vec("tensor_tensor", out=tmpd[:, :], in0=pj[j][:, 0:TT], in1=pj[j][:, 1:TT + 1], op=ALU.subtract)
                            yield
                            vec("scalar_tensor_tensor", out=rkv[j][:, :], in0=tmpd[:, :], scalar=pv(PV_MUR + j, c),
                                in1=pj[j][:, 1:TT + 1], op0=ALU.mult, op1=ALU.add)
                            yield
                        r_, k_, v_ = rkv
                        vec("tensor_scalar", out=v_[:, :], in0=v_[:, :], scalar1=vcur, scalar2=None, op0=ALU.mult)
                        yield
                        act(out=vbf[:, :], in_=v_[:, :], func=AF.Copy)
                        yield
                        pz = nextp("r")
                        mm(out=pz[:, 0:TT], lhsT=l2[0:64, 0, csl], rhs=lw[0:64, :], start=True, stop=True)
                        sigmoid_to(sw[:, :], pz[:, 0:TT], pdv(5, c))
                        yield
                        pz = nextp("r")
                        mm(out=pz[:, 0:TT], lhsT=l2[64:128, 0, csl], rhs=lw[64:128, :], start=True, stop=True)
                        sigmoid_to(asig[:, :], pz[:, 0:TT], pdv(6, c))
                        yield
                        pz = nextp("r")
                        if phaseB:
                            mm(out=pz[:, 0:TT], lhsT=l2[:, 1, csl], rhs=lga[:, :], start=True, stop=False)
                            mm(out=pz[:, 0:TT], lhsT=l2[0:32, 2, csl], rhs=lgb[0:32, :], start=False, stop=True)
                            act(out=gb[:, :], in_=pz[:, 0:TT], func=AF.Copy)
                        yield
                        vec("tensor_tensor_scan", out=cs[:, :], data0=cf[:, CF_MSK:CF_MSK + TT], data1=sw[:, :],
                            initial=0.0, op0=ALU.mult, op1=ALU.add)
                        yield
                        vec("tensor_tensor", out=cm[:, :], in0=cs[:, :], in1=sw[:, :], op=ALU.subtract)
                        yield
                        act(out=Ep[:, :], in_=cs[:, :], func=AF.Exp, scale=-C0)
                        yield
                        act(out=En[:, :], in_=cs[:, :], func=AF.Exp, scale=C0)
                        yield
                        act(out=Em[:, :], in_=cm[:, :], func=AF.Exp, scale=-C0)
                        yield
                        act(out=sqb[:, :], in_=k_[:, :], func=AF.Square, scale=pv(PV_KK, c))
                        yield
                        pz = nextp("r")
                        mm(out=pz[:, 0:TT], lhsT=onesbd, rhs=sqb[:, :], start=True, stop=True)
                        vec("tensor_scalar", out=rinv[:, :], in0=pz[:, 0:TT], scalar1=1e-18, scalar2=None, op0=ALU.max)
                        yield
                        act(out=rinv[:, :], in_=rinv[:, :], func=AF.Ln)
                        yield
                        act(out=rinv[:, :], in_=rinv[:, :], func=AF.Exp, scale=-0.5)
                        yield
                        vec("scalar_tensor_tensor", out=kkb[:, :], in0=k_[:, :], scalar=pv(PV_KK, c), in1=rinv[:, :],
                            op0=ALU.mult, op1=ALU.mult)
                        yield
                        ev = gps if GPS_OFF else vec
                        ev("tensor_scalar", out=ff[:, :], in0=asig[:, :], scalar1=pv(PV_KA, c), scalar2=pdv(PD_OMK, c),
                            op0=ALU.mult, op1=ALU.add)
                        yield
                        ev("tensor_tensor", out=kmod[:, :], in0=k_[:, :], in1=ff[:, :], op=ALU.mult)
                        yield
                        ev("tensor_tensor", out=bv[:, :], in0=kkb[:, :], in1=asig[:, :], op=ALU.mult)
                        yield
                        vec("scalar_tensor_tensor", out=AR[:, :, 0, :], in0=kkb[:, :].re("p (s t) -> p s t", t=128), scalar=-1.0,
                            in1=Em[:, :].re("p (s t) -> p s t", t=128), op0=ALU.mult, op1=ALU.mult)
                        yield
                        if phaseB:
                            vec("tensor_tensor", out=AR[:, :, 1, :], in0=r_[:, :].re("p (s t) -> p s t", t=128),
                                in1=Ep[:, :].re("p (s t) -> p s t", t=128), op=ALU.mult)
                            yield
                        ev("tensor_tensor", out=Bt[:, :], in0=bv[:, :], in1=En[:, :], op=ALU.mult)
                        yield
                        ev("tensor_tensor", out=Kt[:, :], in0=kmod[:, :], in1=En[:, :], op=ALU.mult)
                        yield
                        if phaseB:
                            vec("tensor_tensor", out=tmpd[:, :], in0=r_[:, :], in1=kmod[:, :], op=ALU.mult)
                            yield
                            act(out=sqb[:, :], in_=tmpd[:, :], func=AF.Copy, scale=pv(PV_RK, c))
                            yield
                            pz = nextp("r")
                            mm(out=pz[:, 0:TT], lhsT=onesbd, rhs=sqb[:, :], start=True, stop=True)
                            vec("tensor_tensor", out=bon[:, :], in0=pz[:, 0:TT], in1=v_[:, :], op=ALU.mult)
                            yield
                            vec("tensor_tensor", out=yfin[:, c, :], in0=bon[:, :], in1=gb[:, :], op=ALU.mult)
                        yield
                        vec("tensor_copy", out=pcs[:, 0:4], in_=Ep[:, :].re("p (q t) -> p q t", t=64)[:, :, 63])
                        yield

                def rw_f2(c0):
                    for c in (c0,):
                        AR = ARs[c % 3]; AM = AMs[c % 2]; Vtm = Vtms[c % 2]; Bpad = Bpads[c % 2]; Kpad = Kpads[c % 2]
                        TTf = TTfs[c % 2]; gb = gbs[c % 3]; pcs = pcss[c % 3]
                        Bt = Bts[c % 2]; Kt = Kts[c % 2]; vbf = vbfs[c % 2]
                        for qi, (src, dst) in enumerate(((Bt, Bpad), (Kt, Kpad), (vbf, None))):
                            for sub in range(NSUB):
                                tr(out=trB[qi % 2][:, sub * 128:(sub + 1) * 128],
                                   in_=src[:, sub * 128:(sub + 1) * 128], identity=ident)
                            yield
                            srcv = trB[qi % 2][:, 0:256]
                            if dst is None:
                                act(out=Vtm[:].re("p s c -> p (s c)"), in_=srcv, func=AF.Copy)
                            else:
                                for h in range(2):
                                    act(out=dst[:, :, h, h * 64:(h + 1) * 64],
                                        in_=srcv.re("p (s c) -> p s c", c=128)[:, :, h * 64:(h + 1) * 64], func=AF.Copy)
                        CK('rwkv_a')
                        for h in range(2):
                            hs = slice(h * 64, (h + 1) * 64)
                            for sub in range(NSUB):
                                u = h * NSUB + sub
                                tsl = slice(sub * 128, (sub + 1) * 128)
                                pz = (B1, B2)[u % 2]
                                if phaseB:
                                    mm(out=pz[:, 0:256], lhsT=Bt[hs, tsl], rhs=AR[hs, sub, :, :].re("p a t -> p (a t)"),
                                       start=True, stop=True)
                                    mm(out=pz[:, 256:512], lhsT=Kt[hs, tsl], rhs=AR[hs, sub, :, :].re("p a t -> p (a t)"),
                                       start=True, stop=True)
                                    vec("tensor_tensor", out=AM[:, u, :], in0=pz[:, :], in1=cb[:, CB_MT4:CB_MT4 + 512], op=ALU.mult)
                                else:
                                    mm(out=pz[:, 0:128], lhsT=Bt[hs, tsl], rhs=AR[hs, sub, 0, :], start=True, stop=True)
                                    mm(out=pz[:, 256:384], lhsT=Kt[hs, tsl], rhs=AR[hs, sub, 0, :], start=True, stop=True)
                                    v4 = lambda ap_: ap_.re("p (a two b) -> p a two b", a=2, two=2)[:, :, 0, :]
                                    vec("tensor_tensor", out=v4(AM[:, u, :]), in0=v4(pz[:, :]),
                                        in1=v4(cb[:, CB_MT4:CB_MT4 + 512]), op=ALU.mult)
                                yield
                        for h in range(2):
                            hs = slice(h * 64, (h + 1) * 64)
                            for sub in range(NSUB):
                                u = h * NSUB + sub
                                tsl = slice(sub * 128, (sub + 1) * 128)
                                mm(out=B1[:, u * 128:(u + 1) * 128], lhsT=AR[hs, sub, 0, :], rhs=Bt[hs, tsl],
                                   start=True, stop=True)
                        vec("tensor_tensor", out=L0[:].re("p u t -> p (u t)"), in0=B1[:, :], in1=cb[:, CB_ML4:CB_ML4 + 512],
                            op=ALU.mult)
                        yield
                        CK('rwkv_b')
                        vec("tensor_tensor", out=SS[0][:], in0=AM[:, :, 0:128], in1=ident.bc(1, [128, 4, 128]), op=ALU.add)
                        lt_prev = lambda u: AM[:, u, 0:128]
                        lp_prev = lambda u: L0[:, u, :]
                        scur = 0
                        for lev in range(1, 6):
                            lpn = LP[lev % 2]
                            ltn = LT[lev % 2]
                            for u in range(4):
                                mm(out=B1[:, u * 128:(u + 1) * 128], lhsT=lt_prev(u), rhs=lp_prev(u), start=True, stop=True)
                            if lev <= 4:
                                for u in range(4):
                                    mm(out=B2[:, u * 128:(u + 1) * 128], lhsT=lp_prev(u), rhs=lt_prev(u),
                                       start=True, stop=True)
                            act(out=lpn[:].re("p u t -> p (u t)"), in_=B1[:, :], func=AF.Copy)
                            if lev <= 4:
                                act(out=ltn[:].re("p u t -> p (u t)"), in_=B2[:, :], func=AF.Copy)
                            yield
                            for u in range(4):
                                mm(out=B1[:, u * 128:(u + 1) * 128], lhsT=lpn[:, u, :], rhs=SS[scur][:, u, :], start=True, stop=True)
                            sdst = TTf if lev == 5 else SS[1 - scur]
                            vec("tensor_tensor", out=sdst[:].re("p u t -> p (u t)"), in0=B1[:, :],
                                in1=SS[scur][:].re("p u t -> p (u t)"), op=ALU.add)
                            scur = 1 - scur
                            lt_prev = (lambda b: (lambda u: b[:, u, :]))(ltn)
                            yield
                            lp_prev = (lambda b: (lambda u: b[:, u, :]))(lpn)

                def rw_back(c0):
                    for c in (c0,):
                        AR = ARs[c % 3]; AM = AMs[c % 2]; Vtm = Vtms[c % 2]; Bpad = Bpads[c % 2]; Kpad = Kpads[c % 2]
                        TTf = TTfs[c % 2]; gb = gbs[c % 3]; pcs = pcss[c % 3]
                        Bt = Bts[c % 2]; Kt = Kts[c % 2]; vbf = vbfs[c % 2]
                        TTm = TTf
                        ysq = ysqB[:, :]
                        yln = ysqB[:, :]
                        CK('rwkv_c')
                        for q in range(2 * NSUB):
                            sub, half = q // 2, q % 2
                            ps_ = slice(half * 64, half * 64 + 64)
                            tsl = slice(sub * 128, (sub + 1) * 128)
                            zi = q % 2
                            for h in range(2):
                                hs = slice(h * 64, (h + 1) * 64)
                                u = h * NSUB + sub
                                mm(out=pC[:, hs], lhsT=AR[hs, sub, 0, :], rhs=Zb[hs, c, zi, :], start=True, stop=False)
                                mm(out=pC[:, hs], lhsT=AM[:, u, 256:384], rhs=Vtm[:, sub, hs], start=False, stop=True)
                            act(out=Xb[ps_, :], in_=pC[ps_, 0:128], func=AF.Copy)
                            yield
                            CK('c1')
                            for h in range(2):
                                hs = slice(h * 64, (h + 1) * 64)
                                u = h * NSUB + sub
                                mm(out=pC[:, 128 + h * 64:128 + (h + 1) * 64], lhsT=TTm[ps_, u, :], rhs=Xb[ps_, hs],
                                   start=True, stop=True)
                            vec("tensor_copy", out=Ub[ps_, :], in_=pC[ps_, 128:256])
                            yield
                            CK('c2')
                            if phaseB:
                                for h in range(2):
                                    hs = slice(h * 64, (h + 1) * 64)
                                    u = h * NSUB + sub
                                    o_ = slice(256 + h * 64, 256 + (h + 1) * 64)
                                    mm(out=pC[:, o_], lhsT=AR[hs, sub, 1, :], rhs=Zb[hs, c, zi, :], start=True, stop=False)
                                    mm(out=pC[:, o_], lhsT=AM[:, u, 128:256], rhs=Ub[:, hs], start=False, stop=False)
                                    mm(out=pC[:, o_], lhsT=AM[:, u, 384:512], rhs=Vtm[:, sub, hs], start=False, stop=True)
                                act(out=Ytm[ps_, sub, :], in_=pC[ps_, 256:384], func=AF.Copy)
                                yield
                            CK('c3')
                            for h in range(2):
                                hs = slice(h * 64, (h + 1) * 64)
                                mm(out=pC[:, 384:448], lhsT=Bpad[ps_, sub, h, :], rhs=Ub[ps_, hs], start=(h == 0), stop=False)
                                mm(out=pC[:, 384:448], lhsT=Kpad[ps_, sub, h, :], rhs=Vtm[ps_, sub, hs], start=False, stop=(h == 1))
                            CK('c4')
                            pcv = pcs[:, q:q + 1]
                            vec("tensor_scalar", out=ztmp[:, :], in0=Zf[:, c, :], scalar1=pcv, scalar2=None, op0=ALU.mult)
                            vec("scalar_tensor_tensor", out=Zf[:, c, :], in0=pC[:, 384:448], scalar=pcv, in1=ztmp[:, :],
                                op0=ALU.mult, op1=ALU.add)
                            act(out=Zb[:, c, 1 - zi, :], in_=Zf[:, c, :], func=AF.Copy)
                            yield
                        CK('rwkv_d')
                        if phaseB:
                            yv = Ytm[:].re("p s (h i) -> p (s h) i", i=64)
                            vec("tensor_reduce", out=gst[:, 0:4], in_=yv, axis=AX.X, op=ALU.add)
                            act(out=ysq[:, :], in_=Ytm[:].re("p s c -> p (s c)"), func=AF.Square)
                            vec("tensor_reduce", out=gst[:, 4:8], in_=ysq[:, :].re("p (g i) -> p g i", i=64), axis=AX.X, op=ALU.add)
                            vec("tensor_scalar", out=gst[:, 8:12], in0=gst[:, 0:4], scalar1=1.0 / 64, scalar2=None, op0=ALU.mult)
                            vec("tensor_tensor", out=gst[:, 12:16], in0=gst[:, 8:12], in1=gst[:, 8:12], op=ALU.mult)
                            vec("scalar_tensor_tensor", out=gst[:, 16:20], in0=gst[:, 4:8], scalar=1.0 / 64, in1=gst[:, 12:16],
                                op0=ALU.mult, op1=ALU.subtract)
                            rsqrt_to(gst[:, 24:28], gst[:, 16:20], eps_g, 1.0)
                            ysv = ysq[:, :].re("p (g i) -> p g i", i=64)
                            vec("tensor_tensor", out=ysv, in0=yv, in1=gst[:, 8:12].bc(2, [128, 4, 64]), op=ALU.subtract)
                            vec("tensor_tensor", out=ynb[:].re("p s (h i) -> p (s h) i", i=64), in0=ysv,
                                in1=gst[:, 24:28].bc(2, [128, 4, 64]), op=ALU.mult)
                            yield
                            for sub in range(NSUB):
                                tr(out=trC[:, sub * 128:(sub + 1) * 128], in_=ynb[:, sub, :], identity=ident)
                            act(out=yln[:, :], in_=trC[:, 0:TT], func=AF.Identity, scale=pv(PV_LNW, c), bias=pv(PV_LNB, c))
                            vec("tensor_tensor", out=yln[:, :], in0=yln[:, :], in1=gb[:, :], op=ALU.mult)
                            vec("tensor_tensor", out=yfin[:, c, :], in0=yln[:, :], in1=yfin[:, c, :], op=ALU.add)
                            yield


                def th_attn():
                    if m < PB0 - 8:
                        return
                    j0_2 = m % 2
                    j0_3 = m % 8
                    for g in range(3):
                        kdst = (K1, K2c, K3c)[g]
                        for j in ((0, 1, 2) if phaseB else (1, 2)):
                            wa = wload(CH_AT + g * 3 + j, 'a')
                            wav = wa[:, :].re("p (k c) -> p k c", c=512)
                            for cc in range(4):
                                pz = nextp("a")
                                for kc in range(8):
                                    mm(out=pz[:, 0:TT], lhsT=wav[:, kc, cc * 128:(cc + 1) * 128], rhs=hT[:, kc, 1:TT + 1],
                                       start=(kc == 0), stop=(kc == 7))
                                if j == 0:
                                    act(out=Qa[g][:, cc, :], in_=pz[:, 0:TT], func=AF.Copy, scale=0.125)
                                elif j == 1:
                                    if g == 0:
                                        act(out=K1[:, cc, 128:128 + TT], in_=pz[:, 0:TT], func=AF.Copy)
                                    else:
                                        act(out=kdst[:, cc, :], in_=pz[:, 0:TT], func=AF.Copy)
                                else:
                                    act(out=VF[:, cc, :], in_=pz[:, 0:TT], func=AF.Copy)
                            yield
                        if g == 0:
                            for blk in range(2):
                                for cc in range(4):
                                    tr(out=pTa[:, cc * 128:(cc + 1) * 128], in_=VF[:, cc, blk * 128:(blk + 1) * 128], identity=ident)
                                vec("tensor_copy", out=V1[:, 1 + blk, :], in_=pTa[:, 0:512])
                            yield
                        elif g == 1:
                            vec("tensor_copy", out=VF2[:], in_=VF[:])
                    CK('attn_proj')
                    for h in range(8):
                        cc, hp = h // 2, (h % 2) * 64
                        hs = slice(hp, hp + 64)
                        vs = slice(h * 64, (h + 1) * 64)
                        vl = slice(hp, hp + 64)
                        if h % 2 == 0:
                            for r in range(4):
                                tr(out=pTa[0:64, r * 128:(r + 1) * 128],
                                   in_=VF2[:, cc, :].re("p (i r) -> p r i", r=4)[:, r, :], identity=ident)
                            vec("tensor_copy", out=V2c[0:64, :, :].re("p r c -> p (r c)"), in_=pTa[0:64, 0:512])
                            for r in range(16):
                                tr(out=pTa[0:16, (r % 4) * 128:(r % 4 + 1) * 128],
                                   in_=VF[:, cc, :].re("p (i r) -> p r i", r=16)[:, r, :], identity=ident)
                                if r % 4 == 3:
                                    vec("tensor_copy", out=V3c[0:16, r - 3:r + 1, :].re("p a c -> p (a c)"), in_=pTa[0:16, 0:512])
                                yield
                        if not phaseB:
                            if h % 2 == 1:
                                S.dma("gpsimd", out=V2r[j0_2 * 64:(j0_2 + 1) * 64, :, cc * 128:(cc + 1) * 128], in_=V2c[0:64, :, :])
                                S.dma("gpsimd", out=V3r[j0_3 * 16:(j0_3 + 1) * 16, :, cc * 128:(cc + 1) * 128], in_=V3c[0:16, :, :])
                            continue
                        for blk in range(2):
                            qv = Qa[0][hs, cc, blk * 128:(blk + 1) * 128]
                            mm(out=B5[:, (blk * 2) * 128:(blk * 2 + 1) * 128], lhsT=K1[hs, cc, blk * 128:(blk + 1) * 128],
                               rhs=qv, start=True, stop=True)
                            mm(out=B5[:, (blk * 2 + 1) * 128:(blk * 2 + 2) * 128],
                               lhsT=K1[hs, cc, 128 + blk * 128:128 + (blk + 1) * 128], rhs=qv, start=True, stop=True)
                        act(out=pe[:, :], in_=B5[:, 0:512], func=AF.Exp)
                        yield
                        vec("tensor_tensor", out=pp_[:, :].re("p (b e) -> p b e", b=2), in0=pe[:, :].re("p (b e) -> p b e", b=2),
                            in1=cb[:, CB_E1 + h * 256:CB_E1 + (h + 1) * 256].bc(1, [128, 2, 256]), op=ALU.mult)
                        vec("tensor_scalar", out=pp_[:, 0:128], in0=pp_[:, 0:128], scalar1=vprev, scalar2=None, op0=ALU.mult)
                        yield
                        for blk in range(2):
                            mm(out=B6[0:64, blk * 128:(blk + 1) * 128], lhsT=V1[:, blk, vs],
                               rhs=pp_[:, (blk * 2) * 128:(blk * 2 + 1) * 128], start=True, stop=False)
                            mm(out=B6[0:64, blk * 128:(blk + 1) * 128], lhsT=V1[:, blk + 1, vs],
                               rhs=pp_[:, (blk * 2 + 1) * 128:(blk * 2 + 2) * 128], start=False, stop=True)
                        ppv = pp_[:, :].re("p (b c q) -> p b c q", b=2, c=2)
                        mm(out=B6[0:64, 256:512], lhsT=cb[:, CB_ONES:CB_ONES + 64], rhs=ppv[:, :, 0, :], start=True, stop=False)
                        mm(out=B6[0:64, 256:512], lhsT=cb[:, CB_ONES:CB_ONES + 64], rhs=ppv[:, :, 1, :], start=False, stop=True)
                        act(out=accO[:, :], in_=B6[0:64, 0:TT], func=AF.Copy)
                        act(out=accD[:, :], in_=B6[0:64, 256:512], func=AF.Copy)
                        yield
                        for r in range(4):
                            qv = Qa[1][hs, cc, :].re("p (i r) -> p r i", r=4)[:, r, :]
                            mm(out=B5[:, r * 64:(r + 1) * 64], lhsT=K2r[hs, cc, r, :], rhs=qv, start=True, stop=True)
                            mm(out=B5[0:64, 256 + r * 64:256 + (r + 1) * 64],
                               lhsT=K2c[hs, cc, :].re("p (i r) -> p r i", r=4)[:, r, :], rhs=qv, start=True, stop=True)
                        act(out=pe[:, 0:256], in_=B5[:, 0:256], func=AF.Exp)
                        act(out=peb[0:64, :], in_=B5[0:64, 256:512], func=AF.Exp)
                        yield
                        ea = cb[:, CB_EA2 + (j0_2 * 8 + h) * 64:CB_EA2 + (j0_2 * 8 + h + 1) * 64]
                        vec("scalar_tensor_tensor", out=pp_[:, 0:256].re("p (r i) -> p r i", r=4),
                            in0=pe[:, 0:256].re("p (r i) -> p r i", r=4), scalar=vr2, in1=ea.bc(1, [128, 4, 64]),
                            op0=ALU.mult, op1=ALU.mult)
                        eb = cb[0:64, CB_EB2 + h * 64:CB_EB2 + (h + 1) * 64]
                        vec("tensor_tensor", out=ppb[0:64, :].re("p (r i) -> p r i", r=4),
                            in0=peb[0:64, :].re("p (r i) -> p r i", r=4), in1=eb.bc(1, [64, 4, 64]), op=ALU.mult)
                        yield
                        for r in range(4):
                            mm(out=B6[0:64, r * 64:(r + 1) * 64], lhsT=V2r[:, r, vs], rhs=pp_[:, r * 64:(r + 1) * 64],
                               start=True, stop=False)
                            mm(out=B6[0:64, r * 64:(r + 1) * 64], lhsT=V2c[0:64, r, vl], rhs=ppb[0:64, r * 64:(r + 1) * 64],
                               start=False, stop=True)
                        mm(out=B6[0:64, 256:512], lhsT=cb[:, CB_ONES:CB_ONES + 64], rhs=pp_[:, 0:256], start=True, stop=False)
                        mm(out=B6[0:64, 256:512], lhsT=cb[0:64, CB_ONES:CB_ONES + 64], rhs=ppb[0:64, :], start=False, stop=True)
                        vec("tensor_tensor", out=accO[:, :].re("p (i r) -> p r i", r=4), in0=accO[:, :].re("p (i r) -> p r i", r=4),
                            in1=B6[0:64, 0:TT].re("p (r i) -> p r i", r=4), op=ALU.add)
                        vec("tensor_tensor", out=accD[:, :].re("p (i r) -> p r i", r=4), in0=accD[:, :].re("p (i r) -> p r i", r=4),
                            in1=B6[0:64, 256:512].re("p (r i) -> p r i", r=4), op=ALU.add)
                        yield
                        for r in range(16):
                            qv = Qa[2][hs, cc, :].re("p (i r) -> p r i", r=16)[:, r, :]
                            mm(out=B5[:, r * 16:(r + 1) * 16], lhsT=K3r[hs, cc, r, :], rhs=qv, start=True, stop=True)
                            mm(out=B5[0:16, 256 + r * 16:256 + (r + 1) * 16],
                               lhsT=K3c[hs, cc, :].re("p (i r) -> p r i", r=16)[:, r, :], rhs=qv, start=True, stop=True)
                        act(out=pe[:, 0:256], in_=B5[:, 0:256], func=AF.Exp)
                        act(out=peb[0:16, :], in_=B5[0:16, 256:512], func=AF.Exp)
                        yield
                        ea = cb[:, CB_EA3 + (j0_3 * 8 + h) * 16:CB_EA3 + (j0_3 * 8 + h + 1) * 16]
                        vec("scalar_tensor_tensor", out=pp_[:, 0:256].re("p (r i) -> p r i", r=16),
                            in0=pe[:, 0:256].re("p (r i) -> p r i", r=16), scalar=vr3, in1=ea.bc(1, [128, 16, 16]),
                            op0=ALU.mult, op1=ALU.mult)
                        eb = cb[0:16, CB_EB3 + h * 16:CB_EB3 + (h + 1) * 16]
                        vec("tensor_tensor", out=ppb[0:16, :].re("p (r i) -> p r i", r=16),
                            in0=peb[0:16, :].re("p (r i) -> p r i", r=16), in1=eb.bc(1, [16, 16, 16]), op=ALU.mult)
                        yield
                        for r in range(16):
                            mm(out=B6[0:64, r * 16:(r + 1) * 16], lhsT=V3r[:, r, vs], rhs=pp_[:, r * 16:(r + 1) * 16],
                               start=True, stop=False)
                            mm(out=B6[0:64, r * 16:(r + 1) * 16], lhsT=V3c[0:16, r, vl], rhs=ppb[0:16, r * 16:(r + 1) * 16],
                               start=False, stop=True)
                        mm(out=B6[0:64, 256:512], lhsT=cb[:, CB_ONES:CB_ONES + 64], rhs=pp_[:, 0:256], start=True, stop=False)
                        mm(out=B6[0:64, 256:512], lhsT=cb[0:16, CB_ONES:CB_ONES + 64], rhs=ppb[0:16, :], start=False, stop=True)
                        yield
                        if phaseB:
                            vec("tensor_tensor", out=accO[:, :].re("p (i r) -> p r i", r=16),
                                in0=accO[:, :].re("p (i r) -> p r i", r=16),
                                in1=B6[0:64, 0:TT].re("p (r i) -> p r i", r=16), op=ALU.add)
                            vec("tensor_tensor", out=accD[:, :].re("p (i r) -> p r i", r=16),
                                in0=accD[:, :].re("p (i r) -> p r i", r=16),
                                in1=B6[0:64, 256:512].re("p (r i) -> p r i", r=16), op=ALU.add)
                            vec("reciprocal", out=accD[:, :], in_=accD[:, :])
                            vec("tensor_tensor", out=oT[:, h, :], in0=accO[:, :], in1=accD[:, :], op=ALU.mult)
                        if h % 2 == 1:
                            S.dma("gpsimd", out=V2r[j0_2 * 64:(j0_2 + 1) * 64, :, cc * 128:(cc + 1) * 128], in_=V2c[0:64, :, :])
                            S.dma("gpsimd", out=V3r[j0_3 * 16:(j0_3 + 1) * 16, :, cc * 128:(cc + 1) * 128], in_=V3c[0:16, :, :])
                    CK('attn')
                    vec("tensor_copy", out=K1[:, :, 0:128], in_=K1[:, :, TT:TT + 128])
                    vec("tensor_copy", out=V1[:, 0, :], in_=V1[:, 2, :])
                    vec("tensor_copy", out=K2r[:, :, :, j0_2 * 64:(j0_2 + 1) * 64],
                        in_=K2c[:].re("p c (i r) -> p c r i", r=4))
                    vec("tensor_copy", out=K3r[:, :, :, j0_3 * 16:(j0_3 + 1) * 16],
                        in_=K3c[:].re("p c (i r) -> p c r i", r=16))

                ag = th_attn()
                ag_done = [False]

                def step_attn():
                    if ag_done[0]:
                        return
                    try:
                        next(ag)
                    except StopIteration:
                        ag_done[0] = True

                rr_cnt = [0]
                clk = {}

                def advance(g_):
                    S.cur_fin = 0.0
                    try:
                        next(g_)
                    except StopIteration:
                        return False
                    if S.cur_fin > 0.0:
                        clk[id(g_)] = S.cur_fin
                    return True

                def run_ls(gens):
                    gens = list(gens)
                    for g_ in gens:
                        clk.setdefault(id(g_), 0.0)
                    clk.setdefault(id(ag), 0.0)
                    while gens:
                        cands = gens + ([] if ag_done[0] else [ag])
                        g_ = min(cands, key=lambda x: clk[id(x)])
                        if g_ is ag:
                            step_attn_ls()
                        elif not advance(g_):
                            gens.remove(g_)

                def step_attn_ls():
                    if not advance(ag):
                        ag_done[0] = True

                def run_rr(gens):
                    if LISTSCHED and INTERLEAVE:
                        return run_ls(gens)
                    gens = list(gens)
                    while gens:
                        for g_ in list(gens):
                            try:
                                next(g_)
                            except StopIteration:
                                gens.remove(g_)
                        rr_cnt[0] += 1
                        if INTERLEAVE and rr_cnt[0] % ATTN_EVERY == 0:
                            step_attn()

                if LISTSCHED and INTERLEAVE:
                    done = {"f1": 0, "f2": 0, "bk": 0}
                    mk = {"f1": rw_f1, "f2": rw_f2, "bk": rw_back}
                    cur = {"f1": None, "f2": None, "bk": None}
                    sclk = {"f1": 0.0, "f2": 0.0, "bk": 0.0, "at": clk.get(id(ag), 0.0)}

                    def can_start(k, c):
                        if k == "f1":
                            return done["f2"] >= c - 1 and done["bk"] >= c - 2
                        if k == "f2":
                            return done["f1"] >= c + 1 and done["bk"] >= c - 1
                        return done["f2"] >= c + 1

                    while True:
                        cands = []
                        for k in ("f1", "f2", "bk"):
                            if cur[k] is None and done[k] < 8 and can_start(k, done[k]):
                                cur[k] = mk[k](done[k])
                            if cur[k] is not None:
                                cands.append(k)
                        if not ag_done[0]:
                            cands.append("at")
                        if not cands:
                            break
                        k = min(cands, key=lambda x: sclk[x])
                        S.cur_fin = 0.0
                        if k == "at":
                            step_attn()
                        else:
                            try:
                                next(cur[k])
                            except StopIteration:
                                cur[k] = None
                                done[k] += 1
                        if S.cur_fin > 0.0:
                            sclk[k] = S.cur_fin
                    assert done == {"f1": 8, "f2": 8, "bk": 8}, done
                else:
                    for k_ in range(10):
                        gens = []
                        if k_ < 8:
                            gens.append(rw_f1(k_))
                        if 1 <= k_ <= 8:
                            gens.append(rw_f2(k_ - 1))
                        if k_ >= 2:
                            gens.append(rw_back(k_ - 2))
                        if INTERLEAVE:
                            run_rr(gens)
                        else:
                            for g_ in reversed(gens):
                                for _ in g_:
                                    pass
                while not ag_done[0]:
                    step_attn()

                if not phaseB:
                    continue
                for cc in range(8):
                    sa, sbb = ((tv(2), tv(3)), (tv(5), tv(6)))[cc % 2]
                    wpb = wload(CH_PB + cc)
                    wv = wpb[:, :].re("p (a k c) -> p a k c", a=4, c=128)
                    pz = nextp()
                    for kc in range(8):
                        mm(out=pz[:, 0:TT], lhsT=wv[:, 0, kc, :], rhs=hT[:, kc, 1:TT + 1], start=(kc == 0), stop=(kc == 7))
                    sigmoid_to(sa[:, :], pz[:, 0:TT])
                    pz = nextp()
                    for kc in range(8):
                        mm(out=pz[:, 0:TT], lhsT=wv[:, 1, kc, :], rhs=hT[:, kc, 1:TT + 1], start=(kc == 0), stop=(kc == 7))
                    sigmoid_to(sbb[:, :], pz[:, 0:TT])
                    pz = nextp()
                    for kc in range(8):
                        mm(out=pz[:, 0:TT], lhsT=wv[:, 2, kc, :], rhs=yfin[:, kc, :], start=(kc == 0), stop=(kc == 7))
                    vec("tensor_tensor", out=sa[:, :], in0=sa[:, :], in1=pz[:, 0:TT], op=ALU.mult)
                    pz = nextp()
                    for hh_ in range(8):
                        mm(out=pz[:, 0:TT], lhsT=wv[0:64, 3, hh_, :], rhs=oT[:, hh_, :], start=(hh_ == 0), stop=(hh_ == 7))
                    vec("tensor_tensor", out=sbb[:, :], in0=sbb[:, :], in1=pz[:, 0:TT], op=ALU.mult)
                    vec("tensor_tensor", out=mixT[:, cc, :], in0=sa[:, :], in1=sbb[:, :], op=ALU.add)

                def norm_residual(ps_views, gb):
                    for hf in range(2):
                        act(out=junk[:, 0:512], in_=ps_views[hf], func=AF.Square, accum_out=st4[:, 8 + hf:9 + hf])
                    vec("tensor_tensor", out=st4[:, 10:11], in0=st4[:, 8:9], in1=st4[:, 9:10], op=ALU.add)
                    rsqrt_to(st4[:, 4:5], st4[:, 10:11], eps_r, 1.0 / D)
                    for hf in range(2):
                        for qq in range(2):
                            cs_ = slice(hf * 512 + qq * 256, hf * 512 + (qq + 1) * 256)
                            vec("scalar_tensor_tensor", out=utmp[:, :], in0=ps_views[hf][:, qq * 256:(qq + 1) * 256],
                                scalar=st4[:, 4:5], in1=gb[:, cs_], op0=ALU.mult, op1=ALU.mult)
                            vec("tensor_tensor", out=xm[:, sub, cs_], in0=xm[:, sub, cs_], in1=utmp[:, :], op=ALU.add)

                wo = [wload(CH_WOUT + 0), wload(CH_WOUT + 1)]
                for sub in range(NSUB):
                    for hf in range(2):
                        wv = wo[hf][:, :].re("p (k c) -> p k c", c=512)
                        for kc in range(8):
                            mm(out=(B1, B2)[hf][:, :], lhsT=mixT[:, kc, sub * 128:(sub + 1) * 128], rhs=wv[:, kc, :],
                               start=(kc == 0), stop=(kc == 7))
                    norm_residual([B1[:, :], B2[:, :]], gmb)
                rms_rstd(xm, 2)
                norm_transpose(xm, 2, h2T, PD_A2, lambda kc: modf[:, 24 + kc:25 + kc], 0)
                accs = [[B1[:, :], B2[:, :]], [B3[:, :], B5[:, :]]]
                for pg in range(3):
                    nk = 8 if pg < 2 else 6
                    for i4 in range(nk // 2):
                        i = pg * 4 + i4
                        wf_ = wload(CH_FF + i)
                        wv = wf_[:, :].re("p (k c) -> p k c", c=512)
                        for jj in range(2):
                            jl = i4 * 2 + jj
                            pg_ = nextp("f")
                            for kc in range(8):
                                mm(out=pg_[:, 0:TT], lhsT=wv[:, kc, jj * 128:(jj + 1) * 128], rhs=h2T[:, kc, :],
                                   start=(kc == 0), stop=(kc == 7))
                            act(out=sg[:, :], in_=pg_[:, 0:TT], func=AF.Silu)
                            pu = nextp("f")
                            for kc in range(8):
                                mm(out=pu[:, 0:TT], lhsT=wv[:, kc, 256 + jj * 128:256 + (jj + 1) * 128], rhs=h2T[:, kc, :],
                                   start=(kc == 0), stop=(kc == 7))
                            vec("tensor_tensor", out=actT[:, jl, :], in0=sg[:, :], in1=pu[:, 0:TT], op=ALU.mult)
                    for hf in range(2):
                        wf_ = wload(CH_FO + pg * 2 + hf)
                        wv = wf_[:, :].re("p (k c) -> p k c", c=512)
                        for sub in range(NSUB):
                            for kc in range(nk):
                                mm(out=accs[sub][hf], lhsT=actT[:, kc, sub * 128:(sub + 1) * 128], rhs=wv[:, kc, :],
                                   start=(pg == 0 and kc == 0), stop=(pg == 2 and kc == nk - 1))
                for sub in range(NSUB):
                    norm_residual(accs[sub], gfb)
                r0 = (m - PB0) * TT
                S.dma("gpsimd", out=y_d.v(y_d.t[r0:r0 + TT, :].rearrange("(s p) c -> p s c", p=128)), in_=xm[:])


        except _Stop:
            pass
        S.finish([y_d] + finals)
        S.emit()
    return nc


_CACHE = {}


def prep_inputs(x, c, w_mod, b_mod, g_pre_mix, g_post_mix, g_pre_ffn, g_post_ffn, w_in, mu_rkv, mu_lora,
           w0, w1, w2, a0, a1, a2, g1, g2, k_k, k_a, r_k, ln_x_w, ln_x_b, w_o_rwkv, w_o_attn, w_out,
           w_ffn_in, w_ffn_out):
    f = lambda a: np.asarray(a, np.float32)
    x = f(x); c = f(c)
    w_in = f(w_in)[0]; w_modm = f(w_mod)[0]
    bm = f(b_mod)[0].reshape(6, 1024)
    vecs = [bm[0], bm[1], bm[2], bm[3], bm[4], bm[5], f(g_pre_mix)[0], f(g_post_mix)[0], f(g_pre_ffn)[0],
            f(g_post_ffn)[0], f(mu_rkv)[0, 0], f(mu_rkv)[0, 1], f(mu_rkv)[0, 2], f(mu_lora)[0, 0], f(mu_lora)[0, 1],
            f(mu_lora)[0, 2], f(w0)[0], f(a0)[0], f(k_k)[0], f(k_a)[0], f(r_k)[0].reshape(-1), f(ln_x_w)[0],
            f(ln_x_b)[0]]
    wsrc = np.zeros((NCH, 128, 4096), np.float32)
    def put(i, arr3):
        P, K, C = arr3.shape
        v = wsrc[i].reshape(128, -1)
        tmp = np.zeros((128, K, 4096 // K if K in (8,) else C), np.float32) if False else None
        blk = np.zeros((128, K * C), np.float32)
        blk[:P] = arr3.reshape(P, K * C)
        v[:, :K * C] = blk
    for cch in range(8):
        a = np.zeros((128, 8, 512), np.float32)
        for j in range(3):
            a[:, :, j * 128:(j + 1) * 128] = _wchunk(w_in, slice(j * 1024 + cch * 128, j * 1024 + (cch + 1) * 128))
        put(CH_RW + cch, a)
    for g in range(3):
        for j in range(3):
            o = 3072 + j * 1536 + g * 512
            put(CH_AT + g * 3 + j, _wchunk(w_in, slice(o, o + 512)))
    wor = f(w_o_rwkv)[0]; woa = f(w_o_attn)[0]; wout = f(w_out)[0]
    for cc in range(8):
        cs_ = slice(cc * 128, (cc + 1) * 128)
        a = np.zeros((128, 4, 8, 128), np.float32)
        a[:, 0] = _wchunk(w_in, slice(7680 + cc * 128, 7680 + (cc + 1) * 128))
        a[:, 1] = _wchunk(w_in, slice(8704 + cc * 128, 8704 + (cc + 1) * 128))
        a[:, 2] = _wchunk(wor, cs_)
        a[0:64, 3] = woa[:, cs_].reshape(8, 64, 128).transpose(1, 0, 2)
        put(CH_PB + cc, a.reshape(128, 32, 128))
    for hf in range(2):
        put(CH_WOUT + hf, _wchunk(wout, slice(hf * 512, (hf + 1) * 512)))
    wfi = f(w_ffn_in)[0]; wfo = f(w_ffn_out)[0]
    for i in range(11):
        a = np.zeros((128, 8, 512), np.float32)
        a[:, :, 0:256] = _wchunk(wfi, slice(i * 256, (i + 1) * 256))
        a[:, :, 256:512] = _wchunk(wfi, slice(FH + i * 256, FH + (i + 1) * 256))
        put(CH_FF + i, a)
    for pg in range(3):
        nk = 8 if pg < 2 else 6
        for hf in range(2):
            blk = wfo[pg * 1024:pg * 1024 + nk * 128, hf * 512:(hf + 1) * 512]
            put(CH_FO + pg * 2 + hf, blk.reshape(nk, 128, 512).transpose(1, 0, 2))
    wmod = np.ascontiguousarray(
        w_modm.reshape(8, 128, 24, 256).transpose(2, 1, 0, 3).reshape(24, 128, 2048))
    l1 = np.concatenate([f(w1)[0], f(a1)[0], f(g1)[0]], 1)
    l1 = np.ascontiguousarray(l1.reshape(8, 128, 288).transpose(1, 0, 2).reshape(128, 8 * 288))
    l2 = np.zeros((128, 3, 1024), np.float32)
    l2[0:64, 0] = f(w2)[0]; l2[64:128, 0] = f(a2)[0]
    l2[:, 1] = f(g2)[0][0:128]; l2[0:32, 2] = f(g2)[0][128:160]
    l2 = l2.reshape(128, 3072)
    cbt = _host_consts()
    in_maps = []
    for core in range(8):
        b, hh = core // 2, core % 2
        pfm = np.concatenate([_fm(v) for v in vecs] + [_fm(c[b])], 1)
        if hh == 1:
            xvv = x[b]
        else:
            xvv = np.concatenate([np.zeros((T // 2, D), np.float32), x[b, :T // 2]], 0)
        in_maps.append({"xv": np.ascontiguousarray(xvv), "pfm": np.ascontiguousarray(pfm), "wmod": wmod,
                        "wsrc": wsrc, "l1": l1, "l2": l2, "cbt": cbt, "cft": _host_cf(hh)})
    return in_maps


def kernel(**inputs):
    in_maps = prep_inputs(**inputs)
    if "nc" not in _CACHE:
        _CACHE["nc"] = build()
    nc = _CACHE["nc"]
    res = run_bass_kernel_spmd(nc, in_maps, core_ids=list(range(8)))
    out = np.zeros((4, T, D), np.float32)
    for core in range(8):
        b, hh = core // 2, core % 2
        out[b, hh * (T // 2):(hh + 1) * (T // 2)] = res.results[core]["y"]
    return out
```

```python
import math
from contextlib import ExitStack

import numpy as np
import concourse.bass as bass
import concourse.mybir as mybir
from concourse.bass_utils import run_bass_kernel_spmd

F32 = mybir.dt.float32
BF16 = mybir.dt.bfloat16
AF = mybir.ActivationFunctionType
ALU = mybir.AluOpType
AX = mybir.AxisListType

ENGS = ("tensor", "vector", "scalar", "gpsimd", "sync")

T = 8192
D = 1024
TT = 256
NT = T // TT
PB0 = NT // 2
NSUB = TT // 128
FH = 2816
C0 = math.exp(-0.5)
GN_EPS = 64e-5
RMS_EPS = 1e-6
NSLOT = 4
import os
INTERLEAVE = os.environ.get('NOIL') is None
ATTN_EVERY = int(os.environ.get('ATTN_EVERY', '3'))
GPS_OFF = os.environ.get('GPS_OFF', '0') == '1'
LISTSCHED = os.environ.get('LISTSCHED', '1') == '1'
DENSE_ATTN_PROJ = os.environ.get('DENSE_ATTN_PROJ', '0') == '1'
SEM_LAT = float(os.environ.get('SEM_LAT', '300'))
PE_SCALE = float(os.environ.get('PE_SCALE', '1.0'))
ACT_SCALE = float(os.environ.get('ACT_SCALE', '1.0'))
DVE_SCALE = float(os.environ.get('DVE_SCALE', '1.0'))


class Buf:
    def __init__(self, name, t):
        self.name = name
        self.t = t
        self.writer = None
        self.readers = []
        self.dsem = None
        self.dcnt = 0
        self.psum = False

    def __getitem__(self, idx):
        return View(self, self.t[idx])

    def v(self, ap):
        return View(self, ap)


class SubBuf:
    def __init__(self, buf, col0, ncols=None):
        self.buf = buf
        self.col0 = col0
        self.ncols = ncols

    def __getitem__(self, idx):
        ps, cs = idx
        a = 0 if cs.start is None else cs.start
        e = cs.stop if cs.stop is not None else self.ncols
        assert e is not None
        return View(self.buf, self.buf.t[ps, self.col0 + a:self.col0 + e])


class View:
    def __init__(self, buf, ap):
        self.buf = buf
        self.ap = ap

    def __getitem__(self, idx):
        return View(self.buf, self.ap[idx])

    def re(self, pat, **kw):
        return View(self.buf, self.ap.rearrange(pat, **kw))

    def bc(self, axis, shape):
        return View(self.buf, self.ap.unsqueeze(axis).to_broadcast(list(shape)))


def _unw(x):
    return x.ap if isinstance(x, View) else x


class Sched:
    def __init__(self, nc, stack):
        self.nc = nc
        self.stack = stack
        self.q = {e: [] for e in ENGS}
        self.waited = {e: {} for e in ENGS}
        self.dma_sems = []
        self.fin = {}
        self.eng_free = {e: 0.0 for e in ENGS}
        self.cur_fin = 0.0

    def _est(self, eng, tok, deps_tokens, dur):
        ready = 0.0
        for t_ in deps_tokens:
            f_ = self.fin.get(t_)
            if f_ is not None and f_ > ready:
                ready = f_
        start = max(ready + SEM_LAT, self.eng_free[eng])
        fin = start + dur
        self.eng_free[eng] = fin if eng != "sync" and eng != "gpsimd" else start + 60.0
        self.fin[tok] = fin
        if len(self.fin) > 60000:
            ks = list(self.fin.keys())[:30000]
            for k_ in ks:
                del self.fin[k_]
        if fin > self.cur_fin:
            self.cur_fin = fin

    def sb(self, name, shape, dt):
        t = self.stack.enter_context(self.nc.sbuf_tensor("s_" + name, list(shape), dt))
        return Buf(name, t)

    def ps(self, name, shape, dt=F32):
        t = self.stack.enter_context(self.nc.psum_tensor("p_" + name, list(shape), dt))
        return Buf(name, t)

    def dram(self, name, shape, dt, kind):
        t = self.nc.dram_tensor(name, list(shape), dt, kind=kind).ap()
        return Buf(name, t)

    def _deps(self, eng, reads, writes):
        deps = {}

        def add(tok):
            if tok is None:
                return
            k, v = tok
            if deps.get(k, 0) < v:
                deps[k] = v

        for b in reads:
            add(b.writer)
            if b.psum:
                for r in b.readers:
                    if r[0] != eng:
                        add(r)
        for b in writes:
            add(b.writer)
            for r in b.readers:
                add(r)
        waits = []
        for k, v in deps.items():
            if k == "tensor" and eng == "tensor":
                continue
            if self.waited[eng].get(k, 0) >= v:
                continue
            self.waited[eng][k] = v
            waits.append((k, v))
            if isinstance(k, str):
                self.q[k][v - 1][2] = True
        return waits

    def _commit(self, tok, reads, writes):
        for b in writes:
            b.writer = tok
            b.readers = []
        for b in reads:
            if b in writes:
                continue
            b.readers.append(tok)
            if len(b.readers) > 48:
                d = {}
                for k, v in b.readers:
                    if d.get(k, 0) < v:
                        d[k] = v
                b.readers = list(d.items())

    def op(self, eng, meth, **kw):
        writes, reads = [], []
        for k, v in kw.items():
            if isinstance(v, View):
                if k in ("out", "accum_out", "ap"):
                    if v.buf not in writes:
                        writes.append(v.buf)
                else:
                    if v.buf not in reads:
                        reads.append(v.buf)
        dep_toks = [b_.writer for b_ in reads + writes if b_.writer is not None]
        for b_ in writes:
            dep_toks.extend(b_.readers)
        waits = self._deps(eng, reads, writes)
        if eng == "tensor":
            src = kw.get("lhsT", kw.get("in_"))
            lo = src.ap.base_partition()
            rows = (lo, lo + src.ap.partition_size())
            ob = kw["out"].buf
            prev = getattr(ob, "pe_rows", None)
            if prev is not None and ob.writer is not None and ob.writer[0] == "tensor" and \
                    (rows[1] <= prev[0] or prev[1] <= rows[0]):
                k, v = ob.writer
                if self.waited[eng].get(k, 0) < v:
                    self.waited[eng][k] = v
                    waits.append((k, v))
                    self.q[k][v - 1][2] = True
            ob.pe_rows = rows
        args = {k: _unw(v) for k, v in kw.items()}
        fn = lambda e, m=meth, a=args: getattr(e, m)(**a)
        self.q[eng].append([waits, fn, False, None])
        tok = (eng, len(self.q[eng]))
        o_ = kw.get("out", kw.get("ap"))
        try:
            fsz = o_.ap.free_size()
        except Exception:
            fsz = 256
        if eng == "tensor":
            n_ = kw["rhs"].ap.free_size() if "rhs" in kw else 128
            dur = (max(n_, 64) / 1.2 + 30.0) * PE_SCALE
        else:
            dur = (200.0 + 0.65 * fsz) * ACT_SCALE if eng == "scalar" else (150.0 + 0.75 * fsz) * DVE_SCALE
        self._est(eng, tok, dep_toks, dur)
        self._commit(tok, reads, writes)
        return tok

    def dma(self, eng, out, in_, **kw):
        sb = out.buf
        if sb.dsem is None:
            sb.dsem = ("dma", len(self.dma_sems))
            self.dma_sems.append(sb.name)
        dep_toks = [b_.writer for b_ in (in_.buf, out.buf) if b_.writer is not None] + list(out.buf.readers)
        waits = self._deps(eng, [in_.buf], [out.buf])
        sb.dcnt += 16
        tok = (sb.dsem, sb.dcnt)
        try:
            nbytes = out.ap.nbytes()
        except Exception:
            nbytes = 1 << 20
        self._est(eng, tok, dep_toks, 2500.0 + nbytes / 150.0)
        a = dict(out=out.ap, in_=in_.ap, **kw)
        fn = lambda e, a=a: e.dma_start(**a)
        self.q[eng].append([waits, fn, False, sb.dsem])
        self._commit(tok, [in_.buf], [out.buf])
        return tok

    def finish(self, final_bufs):
        waits = self._deps("sync", final_bufs, [])
        self.q["sync"].append([waits, None, False, None])

    def emit(self):
        nc = self.nc
        st = self.stack
        esem = {e: st.enter_context(nc.semaphore("es_" + e)) for e in ENGS}
        dsem = [st.enter_context(nc.semaphore("ds%d" % i)) for i in range(len(self.dma_sems))]
        cum = {}
        for e in ENGS:
            c = 0
            arr = []
            for it in self.q[e]:
                if it[2]:
                    c += 1
                arr.append(c)
            cum[e] = arr

        def semval(k, v):
            if isinstance(k, str):
                return esem[k], cum[k][v - 1]
            return dsem[k[1]], v

        block = st.enter_context(nc.Block())

        def run(e, eng):
            for waits, fn, sig, dk in self.q[e]:
                for k, v in waits:
                    s, val = semval(k, v)
                    eng.wait_ge(s, val)
                if fn is None:
                    continue
                ins = fn(eng)
                if dk is not None:
                    ins.then_inc(dsem[dk[1]], 16)
                elif sig:
                    ins.then_inc(esem[e], 1)

        @block.tensor
        def _(eng):
            run("tensor", eng)

        @block.vector
        def _(eng):
            run("vector", eng)

        @block.scalar
        def _(eng):
            run("scalar", eng)

        @block.gpsimd
        def _(eng):
            run("gpsimd", eng)

        @block.sync
        def _(eng):
            run("sync", eng)


def _alibi_slopes(n):
    def pow2(m):
        start = 2.0 ** (-8.0 / m)
        return [start ** (i + 1) for i in range(m)]
    if math.log2(n).is_integer():
        s = pow2(n)
    else:
        p = 2 ** int(math.floor(math.log2(n)))
        s = pow2(p) + pow2(2 * p)[0::2][: n - p]
    return sorted(s, reverse=True)


(PV_SHM, PV_SCM, PV_GTM, PV_SHF, PV_SCF, PV_GTF, PV_GPM, PV_GQM, PV_GPF, PV_GQF,
 PV_MUR, PV_MUK, PV_MUV, PV_MUW, PV_MUA, PV_MUG, PV_W0, PV_A0, PV_KK, PV_KA, PV_RK,
 PV_LNW, PV_LNB, PV_C) = range(24)
NPV = 24

CB_ID = 0
CB_ONESBD = 128
CB_ONES = 256
CB_MT4 = 320
CB_ML4 = 832
CB_E1 = 1344
CB_EA2 = CB_E1 + 8 * 256
CB_EB2 = CB_EA2 + 2 * 8 * 64
CB_EA3 = CB_EB2 + 8 * 64
CB_EB3 = CB_EA3 + 8 * 8 * 16
NCB = CB_EB3 + 8 * 16
CF_MSK = 0
CF_VM = 256
CF_EPS = CF_VM + 128
CF_IDF = CF_EPS + 4
NCF = CF_IDF + 128

CH_RW = 0
CH_AT = 8
CH_PB = 17
CH_WOUT = 25
CH_FF = 27
CH_FO = 38
NCH = 44


def _host_consts():
    sl = np.asarray(_alibi_slopes(24), np.float64).reshape(3, 8)
    cb = np.zeros((128, NCB), np.float32)
    p = np.arange(128)
    cb[:, CB_ID:CB_ID + 128] = np.eye(128)
    cb[:, CB_ONESBD:CB_ONESBD + 128] = (p[:, None] // 64 == p[None, :] // 64)
    cb[:, CB_ONES:CB_ONES + 64] = 1.0
    same = (p[:, None] // 64 == p[None, :] // 64)
    su = same & (p[:, None] < p[None, :])
    iu = same & (p[:, None] <= p[None, :])
    slo = same & (p[:, None] > p[None, :])
    cb[:, CB_MT4:CB_MT4 + 512] = np.concatenate([su, iu, su, iu], 1)
    cb[:, CB_ML4:CB_ML4 + 512] = np.concatenate([slo] * 4, 1)
    k = p[:, None].astype(np.float64)
    q = p[None, :].astype(np.float64)
    for h in range(8):
        dpv = q - k + 128
        e_prev = np.where(dpv <= 128, np.exp(-sl[0, h] * dpv), 0.0)
        dcu = q - k
        e_cur = np.where(dcu >= 0, np.exp(-sl[0, h] * np.maximum(dcu, 0)), 0.0)
        cb[:, CB_E1 + h * 256: CB_E1 + h * 256 + 128] = e_prev
        cb[:, CB_E1 + h * 256 + 128: CB_E1 + h * 256 + 256] = e_cur
    i64 = np.arange(64)[None, :].astype(np.float64)
    for rot in range(2):
        for h in range(8):
            j = p // 64
            pp = (p % 64).astype(np.float64)
            a = ((rot - j - 1) % 2) + 1
            dl = 64.0 * a[:, None] + i64 - pp[:, None]
            e = np.where(dl <= 128, np.exp(-sl[1, h] * 4.0 * dl), 0.0)
            o = CB_EA2 + (rot * 8 + h) * 64
            cb[:, o:o + 64] = e
    for h in range(8):
        kk = np.arange(64)[:, None].astype(np.float64)
        dl = i64 - kk
        e = np.where(dl >= 0, np.exp(-sl[1, h] * 4.0 * np.maximum(dl, 0)), 0.0)
        o = CB_EB2 + h * 64
        cb[0:64, o:o + 64] = e
    i16 = np.arange(16)[None, :].astype(np.float64)
    for rot in range(8):
        for h in range(8):
            j = p // 16
            pp = (p % 16).astype(np.float64)
            a = ((rot - j - 1) % 8) + 1
            dl = 16.0 * a[:, None] + i16 - pp[:, None]
            e = np.where(dl <= 128, np.exp(-sl[2, h] * 16.0 * dl), 0.0)
            o = CB_EA3 + (rot * 8 + h) * 16
            cb[:, o:o + 16] = e
    for h in range(8):
        kk = np.arange(16)[:, None].astype(np.float64)
        dl = i16 - kk
        e = np.where(dl >= 0, np.exp(-sl[2, h] * 16.0 * np.maximum(dl, 0)), 0.0)
        o = CB_EB3 + h * 16
        cb[0:16, o:o + 16] = e
    return cb


def _host_cf(hh):
    cf = np.zeros((128, NCF), np.float32)
    m = np.ones((128, 256), np.float32)
    m[:, 0::64] = 0.0
    cf[:, CF_MSK:CF_MSK + 256] = m
    valid = lambda t: 0.0 if t < 0 else (1.0 if (hh == 1 or t >= PB0) else 0.0)
    p = np.arange(128)
    for t in range(NT):
        cf[:, CF_VM + t] = valid(t)
        cf[:, CF_VM + 32 + t] = valid(t - 1)
        j = p // 64
        a = ((t - j - 1) % 2) + 1
        cf[:, CF_VM + 64 + t] = [valid(t - aa) for aa in a]
        j = p // 16
        a = ((t - j - 1) % 8) + 1
        cf[:, CF_VM + 96 + t] = [valid(t - aa) for aa in a]
    cf[:, CF_EPS] = RMS_EPS
    cf[:, CF_EPS + 1] = GN_EPS
    cf[:, CF_EPS + 3] = 1.0
    cf[:, CF_IDF:CF_IDF + 128] = np.eye(128)
    return cf


def _fm(v):
    return np.ascontiguousarray(v.reshape(8, 128).T)


def _wchunk(w, cols):
    return w[:, cols].reshape(8, 128, -1).transpose(1, 0, 2)


class _Stop(Exception):
    pass


def build(nt=NT, dbg=None, dbg_tile=0, dbg_c=0, stop=None):
    nc = bass.Bass("TRN2", target_bir_lowering=False)
    with ExitStack() as st:
        S = Sched(nc, st)
        finals = []

        def CK(name):
            if stop == name:
                raise _Stop()

        def DBG(name, view, m=None, c=None):
            if not dbg or name not in dbg:
                return
            if m is not None and m != dbg_tile:
                return
            if c is not None and c != dbg_c:
                return
            shp = list(view.ap.shape)
            dd = S.dram("dbg_" + name, shp, view.ap.dtype, "ExternalOutput")
            S.dma("gpsimd", out=dd[:], in_=view)
            finals.append(dd)
        xv = S.dram("xv", [T, D], F32, "ExternalInput")
        pfm_d = S.dram("pfm", [128, NPV * 8], F32, "ExternalInput")
        wmod_d = S.dram("wmod", [24, 128, 2048], F32, "ExternalInput")
        wsrc = S.dram("wsrc", [NCH, 128, 4096], F32, "ExternalInput")
        l1_d = S.dram("l1", [128, 8 * 288], F32, "ExternalInput")
        l2_d = S.dram("l2", [128, 3 * 1024], F32, "ExternalInput")
        cb_d = S.dram("cbt", [128, NCB], F32, "ExternalInput")
        cf_d = S.dram("cft", [128, NCF], F32, "ExternalInput")
        y_d = S.dram("y", [T // 2, D], F32, "ExternalOutput")
        wscr_all = S.dram("wscr", [NCH, 128, 4096], BF16, "Internal")
        wscr = [Buf("wscr%d" % i, wscr_all.t[i]) for i in range(NCH)]

        cb = S.sb("cb", [128, NCB], BF16)
        cf = S.sb("cf", [128, NCF], F32)
        pf = S.sb("pf", [128, NPV * 8], F32)
        pd = S.sb("pd", [128, 12 * 8], F32)
        gmb = S.sb("gmb", [128, 1024], BF16)
        gfb = S.sb("gfb", [128, 1024], BF16)
        l1a = S.sb("l1a", [128, 8, 288], BF16)
        l1b = S.sb("l1b", [128, 8, 288], BF16)
        l2 = S.sb("l2", [128, 3, 1024], BF16)
        ring = [S.sb("ring%d" % i, [128, 4096], BF16) for i in range(NSLOT)]
        xt1 = S.sb("xt", [128, NSUB, 1024], F32)
        xt = [xt1, xt1]
        nb = S.sb("nb", [128, NSUB, 1024], BF16)
        junk = nb[:, 0, :]
        st4 = S.sb("st4", [128, 16], F32)
        hT = S.sb("hT", [128, 8, TT + 1], BF16)
        h2T = S.sb("h2T", [128, 8, TT], BF16)
        mixT = h2T
        Zf = S.sb("Zf", [128, 8, 64], F32)
        Zb = S.sb("Zb", [128, 8, 2, 64], BF16)
        hal = S.sb("hal", [128, 8, 3], F32)
        tp = [S.sb("tp%d" % i, [128, TT + 1], F32) if i != 7 else None for i in range(12)]
        tp[7] = tp[6]
        tv = lambda i: tp[i][:, 0:TT]
        pj = [tp[0], tp[0], tp[0]]
        tmpd = tv(1)
        rkv = [tv(2), tv(3), tv(4)]
        sw = tv(5); asig = tv(6); gg = tv(7); cs = tv(0); cm = tv(1)
        Ep = tv(8); En = tv(9); Em = tv(10); rinv = tv(1); kkb = tv(11); ff = tv(0)
        kmod = tv(5); bv = tv(1); bon = tv(6); yln = tv(9); ysq = tv(10)
        sqb = S.sb("sqb", [128, TT], BF16)
        ARs = [S.sb("AR%d" % i, [128, NSUB, 2, 128], BF16) for i in range(3)]
        Bts = [S.sb("Bt%d" % i, [128, TT], BF16) for i in range(2)]
        Kts = [S.sb("Kt%d" % i, [128, TT], BF16) for i in range(2)]
        vbfs = [S.sb("vbf%d" % i, [128, TT], BF16) for i in range(2)]
        Bpads = [S.sb("Bpad%d" % i, [128, NSUB, 2, 128], BF16) for i in range(2)]
        Kpads = [S.sb("Kpad%d" % i, [128, NSUB, 2, 128], BF16) for i in range(2)]
        Vtms = [S.sb("Vtm%d" % i, [128, NSUB, 128], BF16) for i in range(2)]
        AMs = [S.sb("AM%d" % i, [128, 4, 512], BF16) for i in range(2)]
        TTfs = [S.sb("TTf%d" % i, [128, 4, 128], BF16) for i in range(2)]
        gbs = [S.sb("gb%d" % i, [128, TT], BF16) for i in range(3)]
        pcss = [S.sb("pcs%d" % i, [128, 4], F32) for i in range(3)]
        ysqB = S.sb("ysqB", [128, TT], F32)
        L0 = S.sb("L0", [128, 4, 128], BF16)
        LP = [S.sb("LP%d" % i, [128, 4, 128], BF16) for i in range(2)]
        LT = [S.sb("LT%d" % i, [128, 4, 128], BF16) for i in range(2)]
        SS = [S.sb("SS%d" % i, [128, 4, 128], BF16) for i in range(2)]
        Xb = S.sb("Xb", [128, 128], BF16)
        Ub = S.sb("Ub", [128, 128], BF16)
        ztmp = S.sb("ztmp", [128, 64], F32)
        Ytm = S.sb("Ytm", [128, NSUB, 128], F32)
        ynb = S.sb("ynb", [128, NSUB, 128], BF16)
        gst = S.sb("gst", [128, 32], F32)
        lw = S.sb("lw", [128, TT], BF16)
        lga = S.sb("lga", [128, TT], BF16)
        lgb = S.sb("lgb", [32, TT], BF16)
        yfin = S.sb("yfin", [128, 8, TT], BF16)
        _nbf = nb[:].re("p s c -> p (s c)")
        Qa = [h2T[:, 0:4, :], h2T[:, 4:8, :], _nbf[:, 0:1024].re("p (c t) -> p c t", t=TT)]
        K1 = S.sb("K1", [128, 4, 128 + TT], BF16)
        V1 = S.sb("V1", [128, 3, 512], BF16)
        K2c = S.sb("K2c", [128, 4, TT], BF16)
        K2r = S.sb("K2r", [128, 4, 4, 128], BF16)
        V2c = S.sb("V2c", [64, 4, 128], BF16)
        V2r = S.sb("V2r", [128, 4, 512], BF16)
        K3c = S.sb("K3c", [128, 4, TT], BF16)
        K3r = S.sb("K3r", [128, 4, 16, 128], BF16)
        V3c = S.sb("V3c", [16, 16, 128], BF16)
        V3r = S.sb("V3r", [128, 16, 512], BF16)
        VF = _nbf[:, 1024:2048].re("p (c t) -> p c t", t=TT)
        pe = S.sb("pe", [128, 512], BF16)
        pp_ = S.sb("pp", [128, 512], BF16)
        peb = SubBuf(pe, 256, 256)
        ppb = SubBuf(pp_, 256, 256)
        accO = S.sb("accO", [64, TT], F32)
        accD = S.sb("accD", [64, TT], F32)
        oT = S.sb("oT", [64, 8, TT], BF16)
        sa = tv(2); sbb = tv(3); sg = tv(4); utmp = tv(10)
        actT = S.sb("actT", [128, 8, TT], BF16)
        VF2 = actT[:, 0:4, :]
        wst = xt1[:].re("p s c -> p (s c)")

        _b0 = S.ps("b0", [128, 512])
        _b4 = S.ps("b4", [128, 512])
        _bS = S.ps("bS", [128, 1024])
        B3 = S.ps("b3", [128, 512])
        B5 = S.ps("b5", [128, 512])
        B6 = S.ps("b6", [128, 512])
        _pT = S.ps("pT", [128, 1024], BF16)
        R0 = SubBuf(_b0, 0); R1 = SubBuf(_b0, 256)
        Q0 = SubBuf(_b4, 0); Q1 = SubBuf(_b4, 256)
        B1 = Buf("B1", _bS.t[:, 0:512]); B2 = Buf("B2", _bS.t[:, 512:1024])
        pTr = SubBuf(_pT, 0); pTa = SubBuf(_pT, 512)
        for b_ in (_b0, _b4, B1, B2, B3, B5, B6, _pT):
            b_.psum = True
        pC = B3
        trB = [View(B1, B1.t[:, :].bitcast(BF16)), View(B2, B2.t[:, :].bitcast(BF16))]
        trC = View(B3, B3.t[:, :].bitcast(BF16))
        trot = [View(_pT, _pT.t[:, 0:512]), View(B6, B6.t[:, :].bitcast(BF16)), View(B5, B5.t[:, :].bitcast(BF16))]
        prot = {"r": [_b0, B1, B2], "a": [_b4, B5, B6], "x": [_b0, _b4, B1, B2, B3, B6], "f": [_b0, _b4, B6]}
        prot_i = {"r": 0, "a": 0, "x": 0, "f": 0}

        def nextp(k="x"):
            prot_i[k] = (prot_i[k] + 1) % len(prot[k])
            return prot[k][prot_i[k]]

        mm = lambda **kw: S.op("tensor", "matmul", **kw)
        tr = lambda **kw: S.op("tensor", "transpose", **kw)
        act = lambda **kw: S.op("scalar", "activation", **kw)
        vec = lambda m, **kw: S.op("vector", m, **kw)
        gps = lambda m, **kw: S.op("gpsimd", m, **kw)

        def sigmoid_to(dst, src, nbias=None, scale=1.0):
            if nbias is None:
                act(out=dst, in_=src, func=AF.Exp, scale=-scale)
            else:
                act(out=dst, in_=src, func=AF.Exp, scale=-scale, bias=nbias)
            act(out=dst, in_=dst, func=AF.Ln, bias=one_c_for(dst))
            act(out=dst, in_=dst, func=AF.Exp, scale=-1.0)

        def one_c_for(v):
            lo = v.ap.base_partition()
            n = v.ap.partition_size()
            return cf[lo:lo + n, CF_EPS + 3:CF_EPS + 4]

        def rsqrt_to(dst, src, bias_ap, scale=1.0):
            act(out=dst, in_=src, func=AF.Ln, bias=bias_ap, scale=scale)
            act(out=dst, in_=dst, func=AF.Exp, scale=-0.5)

        ident = cb[:, CB_ID:CB_ID + 128]
        identf = cf[:, CF_IDF:CF_IDF + 128]
        onesbd = cb[:, CB_ONESBD:CB_ONESBD + 128]
        eps_r = cf[:, CF_EPS:CF_EPS + 1]
        eps_g = cf[:, CF_EPS + 1:CF_EPS + 2]
        zero_c = cf[:, CF_EPS + 2:CF_EPS + 3]
        one_c = cf[:, CF_EPS + 3:CF_EPS + 4]

        def pv(i, kc):
            return pf[:, i * 8 + kc: i * 8 + kc + 1]

        def pdv(i, kc):
            return pd[:, i * 8 + kc: i * 8 + kc + 1]
        PD_A1, PD_A2, PD_GM, PD_GF, PD_OMK, PD_OMR, PD_OMKm, PD_OMV = range(8)

        try:
            S.dma("gpsimd", out=cb[:, :], in_=cb_d[:, :])
            S.dma("sync", out=cf[:, :], in_=cf_d[:, :])
            S.dma("sync", out=pf[:, :], in_=pfm_d[:, :])
            for i in range(NCH):
                S.dma("gpsimd", out=wscr[i][:, :], in_=wsrc[i])
            CK('dma0')
            for b_ in (Zf, Zb, hal, Bpads[0], Bpads[1], Kpads[0], Kpads[1], K1, K2r, V2r, K3r, V3r, V1, hT, Xb, Ub, Vtms[0], Vtms[1]):
                gps("memset", ap=b_[:], constant=0.0)
            CK('memset')
            for half in range(2):
                S.dma("sync", out=wst[:, 0:4 * 288], in_=l1_d[:, half * 4 * 288:(half + 1) * 4 * 288])
                w1v = wst[:, 0:4 * 288].re("p (k c) -> p k c", c=288)
                for k4 in range(4):
                    kc = half * 4 + k4
                    for (lo, hi, mui) in ((0, 64, PV_MUW), (64, 128, PV_MUA), (128, 288, PV_MUG)):
                        vec("tensor_scalar", out=l1b[:, kc, lo:hi], in0=w1v[:, k4, lo:hi], scalar1=pv(mui, kc),
                            scalar2=None, op0=ALU.mult)
                        vec("tensor_tensor", out=l1a[:, kc, lo:hi], in0=w1v[:, k4, lo:hi], in1=l1b[:, kc, lo:hi],
                            op=ALU.subtract)
            for half in range(2):
                S.dma("sync", out=wst[:, 0:1536], in_=l2_d[:, half * 1536:(half + 1) * 1536])
                vec("tensor_copy", out=l2[:].re("p a c -> p (a c)")[:, half * 1536:(half + 1) * 1536], in_=wst[:, 0:1536])
            CK('lora0')
            for j in range(24):
                S.dma("sync", out=wst[:, 0:2048], in_=wmod_d[j])
                wv = wst[:, 0:2048].re("p (k c) -> p k c", c=256)
                for cc in range(2):
                    col = j * 2 + cc
                    for kc in range(8):
                        mm(out=B1[:, col:col + 1], lhsT=wv[:, kc, cc * 128:(cc + 1) * 128], rhs=pv(PV_C, kc),
                           start=(kc == 0), stop=(kc == 7))
            modf = S.sb("modf", [128, 48], F32)
            vec("tensor_tensor", out=modf[:, :], in0=B1[:, 0:48], in1=pf[:, 0:48], op=ALU.add)
            for kc in range(8):
                vec("scalar_tensor_tensor", out=pdv(PD_A1, kc), in0=modf[:, 8 + kc:9 + kc], scalar=1.0,
                    in1=pv(PV_GPM, kc), op0=ALU.add, op1=ALU.mult)
                vec("scalar_tensor_tensor", out=pdv(PD_A2, kc), in0=modf[:, 32 + kc:33 + kc], scalar=1.0,
                    in1=pv(PV_GPF, kc), op0=ALU.add, op1=ALU.mult)
                vec("tensor_tensor", out=pdv(PD_GM, kc), in0=modf[:, 16 + kc:17 + kc], in1=pv(PV_GQM, kc), op=ALU.mult)
                vec("tensor_tensor", out=pdv(PD_GF, kc), in0=modf[:, 40 + kc:41 + kc], in1=pv(PV_GQF, kc), op=ALU.mult)
                vec("tensor_scalar", out=pdv(PD_OMK, kc), in0=pv(PV_KA, kc), scalar1=-1.0, scalar2=1.0,
                    op0=ALU.mult, op1=ALU.add)
                vec("tensor_scalar", out=pdv(5, kc), in0=pv(PV_W0, kc), scalar1=-1.0, scalar2=None, op0=ALU.mult)
                vec("tensor_scalar", out=pdv(6, kc), in0=pv(PV_A0, kc), scalar1=-1.0, scalar2=None, op0=ALU.mult)
            dg = tp[0][:, 0:128]
            onesf = tp[1][:, 0:128]
            gps("memset", ap=onesf[:, :], constant=1.0)
            for (pdi, dst) in ((PD_GM, gmb), (PD_GF, gfb)):
                for kc in range(8):
                    vec("tensor_scalar", out=dg[:, :], in0=identf, scalar1=pdv(pdi, kc), scalar2=None, op0=ALU.mult)
                    pz = nextp()
                    mm(out=pz[:, 0:128], lhsT=onesf[:, :], rhs=dg[:, :], start=True, stop=True)
                    act(out=dst[:, kc * 128:(kc + 1) * 128], in_=pz[:, 0:128], func=AF.Copy)

            CK('startup')
            ring_i = [0]

            ring_sets = {"r": ring[0:2], "a": ring[2:4], "x": ring}
            ring_k = {"r": 0, "a": 0, "x": 0}

            def wload(ch, k="x"):
                s = ring_sets[k][ring_k[k] % len(ring_sets[k])]
                ring_k[k] += 1
                S.dma("sync", out=s[:, :], in_=wscr[ch][:, :])
                return s

            def rms_rstd(src3, dst_cols, nsub=NSUB):
                for sub in range(nsub):
                    act(out=junk[:, :], in_=src3[:, sub, :], func=AF.Square,
                        accum_out=st4[:, 8 + sub:9 + sub])
                rsqrt_to(st4[:, dst_cols:dst_cols + nsub], st4[:, 8:8 + nsub], eps_r, 1.0 / D)

            def norm_transpose(xsrc, rcol, dstT, a_idx, b_view_fn, halo):
                for sub in range(NSUB):
                    vec("tensor_scalar", out=nb[:, sub, :], in0=xsrc[:, sub, :], scalar1=st4[:, rcol + sub:rcol + sub + 1],
                        scalar2=None, op0=ALU.mult)
                for kc in range(8):
                    tgt = trot[kc % 3]
                    for sub in range(NSUB):
                        tr(out=tgt[:, sub * 128:(sub + 1) * 128], in_=nb[:, sub, kc * 128:(kc + 1) * 128], identity=ident)
                    act(out=dstT[:, kc, halo:halo + TT], in_=tgt[:, 0:TT], func=AF.Identity,
                        scale=pdv(a_idx, kc), bias=b_view_fn(kc))

            for m in range(nt):
                phaseB = m >= PB0
                prot["r"] = [_b0] if m >= PB0 - 8 else [_b0, _b4, B5, B6]
                xm = xt[m % 2]
                vcur = cf[:, CF_VM + m:CF_VM + m + 1]
                vprev = cf[:, CF_VM + 32 + m:CF_VM + 33 + m]
                vr2 = cf[:, CF_VM + 64 + m:CF_VM + 65 + m]
                vr3 = cf[:, CF_VM + 96 + m:CF_VM + 97 + m]
                S.dma("gpsimd", out=xm[:], in_=xv.v(xv.t[m * TT:(m + 1) * TT, :].rearrange("(s p) c -> p s c", p=128)))
                if m > 0:
                    vec("tensor_scalar", out=hT[:, :, 0:1], in0=hT[:, :, TT:TT + 1],
                        scalar1=cf[:, CF_VM + m - 1:CF_VM + m], scalar2=None, op0=ALU.mult)
                rms_rstd(xm, 0)
                norm_transpose(xm, 0, hT, PD_A1, lambda kc: pf[:, PV_SHM * 8 + kc:PV_SHM * 8 + kc + 1]
                               if False else modf[:, kc:kc + 1], 1)

                CK('stage1')
                pz = nextp()
                for kc in range(8):
                    mm(out=pz[:, 0:TT], lhsT=l1a[:, kc, 0:128], rhs=hT[:, kc, 1:TT + 1], start=(kc == 0), stop=False)
                    mm(out=pz[:, 0:TT], lhsT=l1b[:, kc, 0:128], rhs=hT[:, kc, 0:TT], start=False, stop=(kc == 7))
                sigmoid_to(tp[11][0:64, 0:TT], pz[0:64, 0:TT], None, 2.0)
                vec("tensor_scalar", out=lw[0:64, :], in0=tp[11][0:64, 0:TT], scalar1=2.0, scalar2=-1.0, op0=ALU.mult, op1=ALU.add)
                act(out=lw[64:128, :], in_=pz[64:128, 0:TT], func=AF.Copy)
                if phaseB:
                    pz = nextp()
                    for kc in range(8):
                        mm(out=pz[:, 0:TT], lhsT=l1a[:, kc, 128:256], rhs=hT[:, kc, 1:TT + 1], start=(kc == 0), stop=False)
                        mm(out=pz[:, 0:TT], lhsT=l1b[:, kc, 128:256], rhs=hT[:, kc, 0:TT], start=False, stop=(kc == 7))
                    sigmoid_to(tp[11][:, 0:TT], pz[:, 0:TT])
                    act(out=lga[:, :], in_=tp[11][:, 0:TT], func=AF.Copy)
                    pz = nextp()
                    for kc in range(8):
                        mm(out=pz[0:32, 0:TT], lhsT=l1a[:, kc, 256:288], rhs=hT[:, kc, 1:TT + 1], start=(kc == 0), stop=False)
                        mm(out=pz[0:32, 0:TT], lhsT=l1b[:, kc, 256:288], rhs=hT[:, kc, 0:TT], start=False, stop=(kc == 7))
                    sigmoid_to(tp[11][0:32, 0:TT], pz[0:32, 0:TT])
                    act(out=lgb[0:32, :], in_=tp[11][0:32, 0:TT], func=AF.Copy)

                CK('lora1')
                def rw_f1(c0):
                    for c in (c0,):
                        AR = ARs[c % 3]; AM = AMs[c % 2]; Vtm = Vtms[c % 2]; Bpad = Bpads[c % 2]; Kpad = Kpads[c % 2]
                        TTf = TTfs[c % 2]; gb = gbs[c % 3]; pcs = pcss[c % 3]
                        Bt = Bts[c % 2]; Kt = Kts[c % 2]; vbf = vbfs[c % 2]
                        csl = slice(c * 128, (c + 1) * 128)
                        wr = wload(CH_RW + c, 'r')
                        wrv = wr[:, :].re("p (k c) -> p k c", c=512)
                        for j in ((0, 1, 2) if m >= PB0 - 1 else (1, 2)):
                            pz = nextp("r")
                            for kc in range(8):
                                mm(out=pz[:, 0:TT], lhsT=wrv[:, kc, j * 128:(j + 1) * 128], rhs=hT[:, kc, 1:TT + 1],
                                   start=(kc == 0), stop=(kc == 7))
                            vec("tensor_copy", out=pj[j][:, 0:1], in_=hal[:, c, j:j + 1])
                            yield
                            act(out=pj[j][:, 1:TT + 1], in_=pz[:, 0:TT], func=AF.Copy)
                            yield
                            vec("tensor_scalar", out=hal[:, c, j:j + 1], in0=pj[j][:, TT:TT + 1], scalar1=vcur,
                                scalar2=None, op0=ALU.mult)
                            yield
                            vec("tensor_tensor", out=tmpd[:, :], in0=pj[j][:, 0:TT], in1=pj[j][:, 1:TT + 1], op=ALU.subtract)
                            yield
                            vec("scalar_tensor_tensor", out=rkv[j][:, :], in0=tmpd[:, :], scalar=pv(PV_MUR + j, c),
                                in1=pj[j][:, 1:TT + 1], op0=ALU.mult, op1=ALU.add)
                            yield
                        r_, k_, v_ = rkv
                        vec("tensor_scalar", out=v_[:, :], in0=v_[:, :], scalar1=vcur, scalar2=None, op0=ALU.mult)
                        yield
                        act(out=vbf[:, :], in_=v_[:, :], func=AF.Copy)
                        yield
                        pz = nextp("r")
                        mm(out=pz[:, 0:TT], lhsT=l2[0:64, 0, csl], rhs=lw[0:64, :], start=True, stop=True)
                        sigmoid_to(sw[:, :], pz[:, 0:TT], pdv(5, c))
                        yield
                        pz = nextp("r")
                        mm(out=pz[:, 0:TT], lhsT=l2[64:128, 0, csl], rhs=lw[64:128, :], start=True, stop=True)
                        sigmoid_to(asig[:, :], pz[:, 0:TT], pdv(6, c))
                        yield
                        pz = nextp("r")
                        if phaseB:
                            mm(out=pz[:, 0:TT], lhsT=l2[:, 1, csl], rhs=lga[:, :], start=True, stop=False)
                            mm(out=pz[:, 0:TT], lhsT=l2[0:32, 2, csl], rhs=lgb[0:32, :], start=False, stop=True)
                            act(out=gb[:, :], in_=pz[:, 0:TT], func=AF.Copy)
                        yield
                        vec("tensor_tensor_scan", out=cs[:, :], data0=cf[:, CF_MSK:CF_MSK + TT], data1=sw[:, :],
                            initial=0.0, op0=ALU.mult, op1=ALU.add)
                        yield
                        vec("tensor_tensor", out=cm[:, :], in0=cs[:, :], in1=sw[:, :], op=ALU.subtract)
                        yield
                        act(out=Ep[:, :], in_=cs[:, :], func=AF.Exp, scale=-C0)
                        yield
                        act(out=En[:, :], in_=cs[:, :], func=AF.Exp, scale=C0)
                        yield
                        act(out=Em[:, :], in_=cm[:, :], func=AF.Exp, scale=-C0)
                        yield
                        act(out=sqb[:, :], in_=k_[:, :], func=AF.Square, scale=pv(PV_KK, c))
                        yield
                        pz = nextp("r")
                        mm(out=pz[:, 0:TT], lhsT=onesbd, rhs=sqb[:, :], start=True, stop=True)
                        vec("tensor_scalar", out=rinv[:, :], in0=pz[:, 0:TT], scalar1=1e-18, scalar2=None, op0=ALU.max)
                        yield
                        act(out=rinv[:, :], in_=rinv[:, :], func=AF.Ln)
                        yield
                        act(out=rinv[:, :], in_=rinv[:, :], func=AF.Exp, scale=-0.5)
                        yield
                        vec("scalar_tensor_tensor", out=kkb[:, :], in0=k_[:, :], scalar=pv(PV_KK, c), in1=rinv[:, :],
                            op0=ALU.mult, op1=ALU.mult)
                        yield
                        ev = gps if GPS_OFF else vec
                        ev("tensor_scalar", out=ff[:, :], in0=asig[:, :], scalar1=pv(PV_KA, c), scalar2=pdv(PD_OMK, c),
                            op0=ALU.mult, op1=ALU.add)
                        yield
                        ev("tensor_tensor", out=kmod[:, :], in0=k_[:, :], in1=ff[:, :], op=ALU.mult)
                        yield
                        ev("tensor_tensor", out=bv[:, :], in0=kkb[:, :], in1=asig[:, :], op=ALU.mult)
                        yield
                        vec("scalar_tensor_tensor", out=AR[:, :, 0, :], in0=kkb[:, :].re("p (s t) -> p s t", t=128), scalar=-1.0,
                            in1=Em[:, :].re("p (s t) -> p s t", t=128), op0=ALU.mult, op1=ALU.mult)
                        yield
                        if phaseB:
                            vec("tensor_tensor", out=AR[:, :, 1, :], in0=r_[:, :].re("p (s t) -> p s t", t=128),
                                in1=Ep[:, :].re("p (s t) -> p s t", t=128), op=ALU.mult)
                            yield
                        ev("tensor_tensor", out=Bt[:, :], in0=bv[:, :], in1=En[:, :], op=ALU.mult)
                        yield
                        ev("tensor_tensor", out=Kt[:, :], in0=kmod[:, :], in1=En[:, :], op=ALU.mult)
                        yield
                        if phaseB:
                            vec("tensor_tensor", out=tmpd[:, :], in0=r_[:, :], in1=kmod[:, :], op=ALU.mult)
                            yield
                            act(out=sqb[:, :], in_=tmpd[:, :], func=AF.Copy, scale=pv(PV_RK, c))
                            yield
                            pz = nextp("r")
                            mm(out=pz[:, 0:TT], lhsT=onesbd, rhs=sqb[:, :], start=True, stop=True)
                            vec("tensor_tensor", out=bon[:, :], in0=pz[:, 0:TT], in1=v_[:, :], op=ALU.mult)
                            yield
                            vec("tensor_tensor", out=yfin[:, c, :], in0=bon[:, :], in1=gb[:, :], op=ALU.mult)
                        yield
                        vec("tensor_copy", out=pcs[:, 0:4], in_=Ep[:, :].re("p (q t) -> p q t", t=64)[:, :, 63])
                        yield

                def rw_f2(c0):
                    for c in (c0,):
                        AR = ARs[c % 3]; AM = AMs[c % 2]; Vtm = Vtms[c % 2]; Bpad = Bpads[c % 2]; Kpad = Kpads[c % 2]
                        TTf = TTfs[c % 2]; gb = gbs[c % 3]; pcs = pcss[c % 3]
                        Bt = Bts[c % 2]; Kt = Kts[c % 2]; vbf = vbfs[c % 2]
                        for qi, (src, dst) in enumerate(((Bt, Bpad), (Kt, Kpad), (vbf, None))):
                            for sub in range(NSUB):
                                tr(out=trB[qi % 2][:, sub * 128:(sub + 1) * 128],
                                   in_=src[:, sub * 128:(sub + 1) * 128], identity=ident)
                            yield
                            srcv = trB[qi % 2][:, 0:256]
                            if dst is None:
                                act(out=Vtm[:].re("p s c -> p (s c)"), in_=srcv, func=AF.Copy)
                                yield
                            else:
                                for h in range(2):
                                    act(out=dst[:, :, h, h * 64:(h + 1) * 64],
                                        in_=srcv.re("p (s c) -> p s c", c=128)[:, :, h * 64:(h + 1) * 64], func=AF.Copy)
                                    yield
                        CK('rwkv_a')
                        for h in range(2):
                            hs = slice(h * 64, (h + 1) * 64)
                            for sub in range(NSUB):
                                u = h * NSUB + sub
                                tsl = slice(sub * 128, (sub + 1) * 128)
                                pz = (B1, B2)[u % 2]
                                if phaseB:
                                    mm(out=pz[:, 0:256], lhsT=Bt[hs, tsl], rhs=AR[hs, sub, :, :].re("p a t -> p (a t)"),
                                       start=True, stop=True)
                                    mm(out=pz[:, 256:512], lhsT=Kt[hs, tsl], rhs=AR[hs, sub, :, :].re("p a t -> p (a t)"),
                                       start=True, stop=True)
                                    vec("tensor_tensor", out=AM[:, u, :], in0=pz[:, :], in1=cb[:, CB_MT4:CB_MT4 + 512], op=ALU.mult)
                                    yield
                                else:
                                    mm(out=pz[:, 0:128], lhsT=Bt[hs, tsl], rhs=AR[hs, sub, 0, :], start=True, stop=True)
                                    mm(out=pz[:, 256:384], lhsT=Kt[hs, tsl], rhs=AR[hs, sub, 0, :], start=True, stop=True)
                                    v4 = lambda ap_: ap_.re("p (a two b) -> p a two b", a=2, two=2)[:, :, 0, :]
                                    vec("tensor_tensor", out=v4(AM[:, u, :]), in0=v4(pz[:, :]),
                                        in1=v4(cb[:, CB_MT4:CB_MT4 + 512]), op=ALU.mult)
                                yield
                        for h in range(2):
                            hs = slice(h * 64, (h + 1) * 64)
                            for sub in range(NSUB):
                                u = h * NSUB + sub
                                tsl = slice(sub * 128, (sub + 1) * 128)
                                mm(out=B1[:, u * 128:(u + 1) * 128], lhsT=AR[hs, sub, 0, :], rhs=Bt[hs, tsl],
                                   start=True, stop=True)
                        vec("tensor_tensor", out=L0[:].re("p u t -> p (u t)"), in0=B1[:, :], in1=cb[:, CB_ML4:CB_ML4 + 512],
                            op=ALU.mult)
                        yield
                        CK('rwkv_b')
                        vec("tensor_tensor", out=SS[0][:], in0=AM[:, :, 0:128], in1=ident.bc(1, [128, 4, 128]), op=ALU.add)
                        yield
                        lt_prev = lambda u: AM[:, u, 0:128]
                        lp_prev = lambda u: L0[:, u, :]
                        scur = 0
                        for lev in range(1, 6):
                            lpn = LP[lev % 2]
                            ltn = LT[lev % 2]
                            for u in range(4):
                                mm(out=B1[:, u * 128:(u + 1) * 128], lhsT=lt_prev(u), rhs=lp_prev(u), start=True, stop=True)
                            if lev <= 4:
                                for u in range(4):
                                    mm(out=B2[:, u * 128:(u + 1) * 128], lhsT=lp_prev(u), rhs=lt_prev(u),
                                       start=True, stop=True)
                            act(out=lpn[:].re("p u t -> p (u t)"), in_=B1[:, :], func=AF.Copy)
                            yield
                            if lev <= 4:
                                act(out=ltn[:].re("p u t -> p (u t)"), in_=B2[:, :], func=AF.Copy)
                            yield
                            for u in range(4):
                                mm(out=B1[:, u * 128:(u + 1) * 128], lhsT=lpn[:, u, :], rhs=SS[scur][:, u, :], start=True, stop=True)
                            sdst = TTf if lev == 5 else SS[1 - scur]
                            vec("tensor_tensor", out=sdst[:].re("p u t -> p (u t)"), in0=B1[:, :],
                                in1=SS[scur][:].re("p u t -> p (u t)"), op=ALU.add)
                            yield
                            scur = 1 - scur
                            lt_prev = (lambda b: (lambda u: b[:, u, :]))(ltn)
                            yield
                            lp_prev = (lambda b: (lambda u: b[:, u, :]))(lpn)

                def rw_back(c0):
                    for c in (c0,):
                        AR = ARs[c % 3]; AM = AMs[c % 2]; Vtm = Vtms[c % 2]; Bpad = Bpads[c % 2]; Kpad = Kpads[c % 2]
                        TTf = TTfs[c % 2]; gb = gbs[c % 3]; pcs = pcss[c % 3]
                        Bt = Bts[c % 2]; Kt = Kts[c % 2]; vbf = vbfs[c % 2]
                        TTm = TTf
                        ysq = ysqB[:, :]
                        yln = ysqB[:, :]
                        CK('rwkv_c')
                        for q in range(2 * NSUB):
                            sub, half = q // 2, q % 2
                            ps_ = slice(half * 64, half * 64 + 64)
                            tsl = slice(sub * 128, (sub + 1) * 128)
                            zi = q % 2
                            for h in range(2):
                                hs = slice(h * 64, (h + 1) * 64)
                                u = h * NSUB + sub
                                mm(out=pC[:, hs], lhsT=AR[hs, sub, 0, :], rhs=Zb[hs, c, zi, :], start=True, stop=False)
                                mm(out=pC[:, hs], lhsT=AM[:, u, 256:384], rhs=Vtm[:, sub, hs], start=False, stop=True)
                            act(out=Xb[ps_, :], in_=pC[ps_, 0:128], func=AF.Copy)
                            yield
                            CK('c1')
                            for h in range(2):
                                hs = slice(h * 64, (h + 1) * 64)
                                u = h * NSUB + sub
                                mm(out=pC[:, 128 + h * 64:128 + (h + 1) * 64], lhsT=TTm[ps_, u, :], rhs=Xb[ps_, hs],
                                   start=True, stop=True)
                            vec("tensor_copy", out=Ub[ps_, :], in_=pC[ps_, 128:256])
                            yield
                            CK('c2')
                            if phaseB:
                                for h in range(2):
                                    hs = slice(h * 64, (h + 1) * 64)
                                    u = h * NSUB + sub
                                    o_ = slice(256 + h * 64, 256 + (h + 1) * 64)
                                    mm(out=pC[:, o_], lhsT=AR[hs, sub, 1, :], rhs=Zb[hs, c, zi, :], start=True, stop=False)
                                    mm(out=pC[:, o_], lhsT=AM[:, u, 128:256], rhs=Ub[:, hs], start=False, stop=False)
                                    mm(out=pC[:, o_], lhsT=AM[:, u, 384:512], rhs=Vtm[:, sub, hs], start=False, stop=True)
                                act(out=Ytm[ps_, sub, :], in_=pC[ps_, 256:384], func=AF.Copy)
                                yield
                            CK('c3')
                            for h in range(2):
                                hs = slice(h * 64, (h + 1) * 64)
                                mm(out=pC[:, 384:448], lhsT=Bpad[ps_, sub, h, :], rhs=Ub[ps_, hs], start=(h == 0), stop=False)
                                mm(out=pC[:, 384:448], lhsT=Kpad[ps_, sub, h, :], rhs=Vtm[ps_, sub, hs], start=False, stop=(h == 1))
                            CK('c4')
                            pcv = pcs[:, q:q + 1]
                            vec("tensor_scalar", out=ztmp[:, :], in0=Zf[:, c, :], scalar1=pcv, scalar2=None, op0=ALU.mult)
                            yield
                            vec("scalar_tensor_tensor", out=Zf[:, c, :], in0=pC[:, 384:448], scalar=pcv, in1=ztmp[:, :],
                                op0=ALU.mult, op1=ALU.add)
                            yield
                            act(out=Zb[:, c, 1 - zi, :], in_=Zf[:, c, :], func=AF.Copy)
                            yield
                        CK('rwkv_d')
                        if phaseB:
                            yv = Ytm[:].re("p s (h i) -> p (s h) i", i=64)
                            vec("tensor_reduce", out=gst[:, 0:4], in_=yv, axis=AX.X, op=ALU.add)
                            yield
                            act(out=ysq[:, :], in_=Ytm[:].re("p s c -> p (s c)"), func=AF.Square)
                            yield
                            vec("tensor_reduce", out=gst[:, 4:8], in_=ysq[:, :].re("p (g i) -> p g i", i=64), axis=AX.X, op=ALU.add)
                            yield
                            vec("tensor_scalar", out=gst[:, 8:12], in0=gst[:, 0:4], scalar1=1.0 / 64, scalar2=None, op0=ALU.mult)
                            yield
                            vec("tensor_tensor", out=gst[:, 12:16], in0=gst[:, 8:12], in1=gst[:, 8:12], op=ALU.mult)
                            yield
                            vec("scalar_tensor_tensor", out=gst[:, 16:20], in0=gst[:, 4:8], scalar=1.0 / 64, in1=gst[:, 12:16],
                                op0=ALU.mult, op1=ALU.subtract)
                            yield
                            rsqrt_to(gst[:, 24:28], gst[:, 16:20], eps_g, 1.0)
                            yield
                            ysv = ysq[:, :].re("p (g i) -> p g i", i=64)
                            vec("tensor_tensor", out=ysv, in0=yv, in1=gst[:, 8:12].bc(2, [128, 4, 64]), op=ALU.subtract)
                            yield
                            vec("tensor_tensor", out=ynb[:].re("p s (h i) -> p (s h) i", i=64), in0=ysv,
                                in1=gst[:, 24:28].bc(2, [128, 4, 64]), op=ALU.mult)
                            yield
                            for sub in range(NSUB):
                                tr(out=trC[:, sub * 128:(sub + 1) * 128], in_=ynb[:, sub, :], identity=ident)
                            act(out=yln[:, :], in_=trC[:, 0:TT], func=AF.Identity, scale=pv(PV_LNW, c), bias=pv(PV_LNB, c))
                            yield
                            vec("tensor_tensor", out=yln[:, :], in0=yln[:, :], in1=gb[:, :], op=ALU.mult)
                            yield
                            vec("tensor_tensor", out=yfin[:, c, :], in0=yln[:, :], in1=yfin[:, c, :], op=ALU.add)
                            yield


                def th_attn():
                    if m < PB0 - 8:
                        return
                    j0_2 = m % 2
                    j0_3 = m % 8
                    for g in range(3):
                        kdst = (K1, K2c, K3c)[g]
                        for j in ((0, 1, 2) if phaseB else (1, 2)):
                            wa = wload(CH_AT + g * 3 + j, 'a')
                            wav = wa[:, :].re("p (k c) -> p k c", c=512)
                            for cc in range(4):
                                pz = nextp("a")
                                for kc in range(8):
                                    mm(out=pz[:, 0:TT], lhsT=wav[:, kc, cc * 128:(cc + 1) * 128], rhs=hT[:, kc, 1:TT + 1],
                                       start=(kc == 0), stop=(kc == 7))
                                if j == 0:
                                    act(out=Qa[g][:, cc, :], in_=pz[:, 0:TT], func=AF.Copy, scale=0.125)
                                    yield
                                elif j == 1:
                                    if g == 0:
                                        act(out=K1[:, cc, 128:128 + TT], in_=pz[:, 0:TT], func=AF.Copy)
                                        yield
                                    else:
                                        act(out=kdst[:, cc, :], in_=pz[:, 0:TT], func=AF.Copy)
                                        yield
                                else:
                                    act(out=VF[:, cc, :], in_=pz[:, 0:TT], func=AF.Copy)
                                    yield
                            if not DENSE_ATTN_PROJ:
                                yield
                        if g == 0:
                            for blk in range(2):
                                for cc in range(4):
                                    tr(out=pTa[:, cc * 128:(cc + 1) * 128], in_=VF[:, cc, blk * 128:(blk + 1) * 128], identity=ident)
                                vec("tensor_copy", out=V1[:, 1 + blk, :], in_=pTa[:, 0:512])
                            yield
                        elif g == 1:
                            vec("tensor_copy", out=VF2[:], in_=VF[:])
                            yield
                    CK('attn_proj')
                    for h in range(8):
                        cc, hp = h // 2, (h % 2) * 64
                        hs = slice(hp, hp + 64)
                        vs = slice(h * 64, (h + 1) * 64)
                        vl = slice(hp, hp + 64)
                        if h % 2 == 0:
                            for r in range(4):
                                tr(out=pTa[0:64, r * 128:(r + 1) * 128],
                                   in_=VF2[:, cc, :].re("p (i r) -> p r i", r=4)[:, r, :], identity=ident)
                            vec("tensor_copy", out=V2c[0:64, :, :].re("p r c -> p (r c)"), in_=pTa[0:64, 0:512])
                            yield
                            for r in range(16):
                                tr(out=pTa[0:16, (r % 4) * 128:(r % 4 + 1) * 128],
                                   in_=VF[:, cc, :].re("p (i r) -> p r i", r=16)[:, r, :], identity=ident)
                                if r % 4 == 3:
                                    vec("tensor_copy", out=V3c[0:16, r - 3:r + 1, :].re("p a c -> p (a c)"), in_=pTa[0:16, 0:512])
                                yield
                        if not phaseB:
                            if h % 2 == 1:
                                S.dma("gpsimd", out=V2r[j0_2 * 64:(j0_2 + 1) * 64, :, cc * 128:(cc + 1) * 128], in_=V2c[0:64, :, :])
                                S.dma("gpsimd", out=V3r[j0_3 * 16:(j0_3 + 1) * 16, :, cc * 128:(cc + 1) * 128], in_=V3c[0:16, :, :])
                            continue
                        for blk in range(2):
                            qv = Qa[0][hs, cc, blk * 128:(blk + 1) * 128]
                            mm(out=B5[:, (blk * 2) * 128:(blk * 2 + 1) * 128], lhsT=K1[hs, cc, blk * 128:(blk + 1) * 128],
                               rhs=qv, start=True, stop=True)
                            mm(out=B5[:, (blk * 2 + 1) * 128:(blk * 2 + 2) * 128],
                               lhsT=K1[hs, cc, 128 + blk * 128:128 + (blk + 1) * 128], rhs=qv, start=True, stop=True)
                        act(out=pe[:, :], in_=B5[:, 0:512], func=AF.Exp)
                        yield
                        vec("tensor_tensor", out=pp_[:, :].re("p (b e) -> p b e", b=2), in0=pe[:, :].re("p (b e) -> p b e", b=2),
                            in1=cb[:, CB_E1 + h * 256:CB_E1 + (h + 1) * 256].bc(1, [128, 2, 256]), op=ALU.mult)
                        yield
                        vec("tensor_scalar", out=pp_[:, 0:128], in0=pp_[:, 0:128], scalar1=vprev, scalar2=None, op0=ALU.mult)
                        yield
                        for blk in range(2):
                            mm(out=B6[0:64, blk * 128:(blk + 1) * 128], lhsT=V1[:, blk, vs],
                               rhs=pp_[:, (blk * 2) * 128:(blk * 2 + 1) * 128], start=True, stop=False)
                            mm(out=B6[0:64, blk * 128:(blk + 1) * 128], lhsT=V1[:, blk + 1, vs],
                               rhs=pp_[:, (blk * 2 + 1) * 128:(blk * 2 + 2) * 128], start=False, stop=True)
                        ppv = pp_[:, :].re("p (b c q) -> p b c q", b=2, c=2)
                        mm(out=B6[0:64, 256:512], lhsT=cb[:, CB_ONES:CB_ONES + 64], rhs=ppv[:, :, 0, :], start=True, stop=False)
                        mm(out=B6[0:64, 256:512], lhsT=cb[:, CB_ONES:CB_ONES + 64], rhs=ppv[:, :, 1, :], start=False, stop=True)
                        act(out=accO[:, :], in_=B6[0:64, 0:TT], func=AF.Copy)
                        yield
                        act(out=accD[:, :], in_=B6[0:64, 256:512], func=AF.Copy)
                        yield
                        for r in range(4):
                            qv = Qa[1][hs, cc, :].re("p (i r) -> p r i", r=4)[:, r, :]
                            mm(out=B5[:, r * 64:(r + 1) * 64], lhsT=K2r[hs, cc, r, :], rhs=qv, start=True, stop=True)
                            mm(out=B5[0:64, 256 + r * 64:256 + (r + 1) * 64],
                               lhsT=K2c[hs, cc, :].re("p (i r) -> p r i", r=4)[:, r, :], rhs=qv, start=True, stop=True)
                        act(out=pe[:, 0:256], in_=B5[:, 0:256], func=AF.Exp)
                        yield
                        act(out=peb[0:64, :], in_=B5[0:64, 256:512], func=AF.Exp)
                        yield
                        ea = cb[:, CB_EA2 + (j0_2 * 8 + h) * 64:CB_EA2 + (j0_2 * 8 + h + 1) * 64]
                        vec("scalar_tensor_tensor", out=pp_[:, 0:256].re("p (r i) -> p r i", r=4),
                            in0=pe[:, 0:256].re("p (r i) -> p r i", r=4), scalar=vr2, in1=ea.bc(1, [128, 4, 64]),
                            op0=ALU.mult, op1=ALU.mult)
                        yield
                        eb = cb[0:64, CB_EB2 + h * 64:CB_EB2 + (h + 1) * 64]
                        vec("tensor_tensor", out=ppb[0:64, :].re("p (r i) -> p r i", r=4),
                            in0=peb[0:64, :].re("p (r i) -> p r i", r=4), in1=eb.bc(1, [64, 4, 64]), op=ALU.mult)
                        yield
                        for r in range(4):
                            mm(out=B6[0:64, r * 64:(r + 1) * 64], lhsT=V2r[:, r, vs], rhs=pp_[:, r * 64:(r + 1) * 64],
                               start=True, stop=False)
                            mm(out=B6[0:64, r * 64:(r + 1) * 64], lhsT=V2c[0:64, r, vl], rhs=ppb[0:64, r * 64:(r + 1) * 64],
                               start=False, stop=True)
                        mm(out=B6[0:64, 256:512], lhsT=cb[:, CB_ONES:CB_ONES + 64], rhs=pp_[:, 0:256], start=True, stop=False)
                        mm(out=B6[0:64, 256:512], lhsT=cb[0:64, CB_ONES:CB_ONES + 64], rhs=ppb[0:64, :], start=False, stop=True)
                        vec("tensor_tensor", out=accO[:, :].re("p (i r) -> p r i", r=4), in0=accO[:, :].re("p (i r) -> p r i", r=4),
                            in1=B6[0:64, 0:TT].re("p (r i) -> p r i", r=4), op=ALU.add)
                        yield
                        vec("tensor_tensor", out=accD[:, :].re("p (i r) -> p r i", r=4), in0=accD[:, :].re("p (i r) -> p r i", r=4),
                            in1=B6[0:64, 256:512].re("p (r i) -> p r i", r=4), op=ALU.add)
                        yield
                        for r in range(16):
                            qv = Qa[2][hs, cc, :].re("p (i r) -> p r i", r=16)[:, r, :]
                            mm(out=B5[:, r * 16:(r + 1) * 16], lhsT=K3r[hs, cc, r, :], rhs=qv, start=True, stop=True)
                            mm(out=B5[0:16, 256 + r * 16:256 + (r + 1) * 16],
                               lhsT=K3c[hs, cc, :].re("p (i r) -> p r i", r=16)[:, r, :], rhs=qv, start=True, stop=True)
                        act(out=pe[:, 0:256], in_=B5[:, 0:256], func=AF.Exp)
                        yield
                        act(out=peb[0:16, :], in_=B5[0:16, 256:512], func=AF.Exp)
                        yield
                        ea = cb[:, CB_EA3 + (j0_3 * 8 + h) * 16:CB_EA3 + (j0_3 * 8 + h + 1) * 16]
                        vec("scalar_tensor_tensor", out=pp_[:, 0:256].re("p (r i) -> p r i", r=16),
                            in0=pe[:, 0:256].re("p (r i) -> p r i", r=16), scalar=vr3, in1=ea.bc(1, [128, 16, 16]),
                            op0=ALU.mult, op1=ALU.mult)
                        yield
                        eb = cb[0:16, CB_EB3 + h * 16:CB_EB3 + (h + 1) * 16]
                        vec("tensor_tensor", out=ppb[0:16, :].re("p (r i) -> p r i", r=16),
                            in0=peb[0:16, :].re("p (r i) -> p r i", r=16), in1=eb.bc(1, [16, 16, 16]), op=ALU.mult)
                        yield
                        for r in range(16):
                            mm(out=B6[0:64, r * 16:(r + 1) * 16], lhsT=V3r[:, r, vs], rhs=pp_[:, r * 16:(r + 1) * 16],
                               start=True, stop=False)
                            mm(out=B6[0:64, r * 16:(r + 1) * 16], lhsT=V3c[0:16, r, vl], rhs=ppb[0:16, r * 16:(r + 1) * 16],
                               start=False, stop=True)
                        mm(out=B6[0:64, 256:512], lhsT=cb[:, CB_ONES:CB_ONES + 64], rhs=pp_[:, 0:256], start=True, stop=False)
                        mm(out=B6[0:64, 256:512], lhsT=cb[0:16, CB_ONES:CB_ONES + 64], rhs=ppb[0:16, :], start=False, stop=True)
                        yield
                        if phaseB:
                            vec("tensor_tensor", out=accO[:, :].re("p (i r) -> p r i", r=16),
                                in0=accO[:, :].re("p (i r) -> p r i", r=16),
                                in1=B6[0:64, 0:TT].re("p (r i) -> p r i", r=16), op=ALU.add)
                            yield
                            vec("tensor_tensor", out=accD[:, :].re("p (i r) -> p r i", r=16),
                                in0=accD[:, :].re("p (i r) -> p r i", r=16),
                                in1=B6[0:64, 256:512].re("p (r i) -> p r i", r=16), op=ALU.add)
                            yield
                            vec("reciprocal", out=accD[:, :], in_=accD[:, :])
                            yield
                            vec("tensor_tensor", out=oT[:, h, :], in0=accO[:, :], in1=accD[:, :], op=ALU.mult)
                            yield
                        if h % 2 == 1:
                            S.dma("gpsimd", out=V2r[j0_2 * 64:(j0_2 + 1) * 64, :, cc * 128:(cc + 1) * 128], in_=V2c[0:64, :, :])
                            S.dma("gpsimd", out=V3r[j0_3 * 16:(j0_3 + 1) * 16, :, cc * 128:(cc + 1) * 128], in_=V3c[0:16, :, :])
                    CK('attn')
                    vec("tensor_copy", out=K1[:, :, 0:128], in_=K1[:, :, TT:TT + 128])
                    yield
                    vec("tensor_copy", out=V1[:, 0, :], in_=V1[:, 2, :])
                    yield
                    vec("tensor_copy", out=K2r[:, :, :, j0_2 * 64:(j0_2 + 1) * 64],
                        in_=K2c[:].re("p c (i r) -> p c r i", r=4))
                    yield
                    vec("tensor_copy", out=K3r[:, :, :, j0_3 * 16:(j0_3 + 1) * 16],
                        in_=K3c[:].re("p c (i r) -> p c r i", r=16))

                ag = th_attn()
                ag_done = [False]

                def step_attn():
                    if ag_done[0]:
                        return
                    try:
                        next(ag)
                    except StopIteration:
                        ag_done[0] = True

                rr_cnt = [0]
                clk = {}

                def advance(g_):
                    S.cur_fin = 0.0
                    try:
                        next(g_)
                    except StopIteration:
                        return False
                    if S.cur_fin > 0.0:
                        clk[id(g_)] = S.cur_fin
                    return True

                def run_ls(gens):
                    gens = list(gens)
                    for g_ in gens:
                        clk.setdefault(id(g_), 0.0)
                    clk.setdefault(id(ag), 0.0)
                    while gens:
                        cands = gens + ([] if ag_done[0] else [ag])
                        g_ = min(cands, key=lambda x: clk[id(x)])
                        if g_ is ag:
                            step_attn_ls()
                        elif not advance(g_):
                            gens.remove(g_)

                def step_attn_ls():
                    if not advance(ag):
                        ag_done[0] = True

                def run_rr(gens):
                    if LISTSCHED and INTERLEAVE:
                        return run_ls(gens)
                    gens = list(gens)
                    while gens:
                        for g_ in list(gens):
                            try:
                                next(g_)
                            except StopIteration:
                                gens.remove(g_)
                        rr_cnt[0] += 1
                        if INTERLEAVE and rr_cnt[0] % ATTN_EVERY == 0:
                            step_attn()

                if LISTSCHED and INTERLEAVE:
                    done = {"f1": 0, "f2": 0, "bk": 0}
                    mk = {"f1": rw_f1, "f2": rw_f2, "bk": rw_back}
                    cur = {"f1": None, "f2": None, "bk": None}
                    sclk = {"f1": 0.0, "f2": 0.0, "bk": 0.0, "at": clk.get(id(ag), 0.0)}

                    def can_start(k, c):
                        if k == "f1":
                            return done["f2"] >= c - 1 and done["bk"] >= c - 2
                        if k == "f2":
                            return done["f1"] >= c + 1 and done["bk"] >= c - 1
                        return done["f2"] >= c + 1

                    while True:
                        cands = []
                        for k in ("f1", "f2", "bk"):
                            if cur[k] is None and done[k] < 8 and can_start(k, done[k]):
                                cur[k] = mk[k](done[k])
                            if cur[k] is not None:
                                cands.append(k)
                        if not ag_done[0]:
                            cands.append("at")
                        if not cands:
                            break
                        k = min(cands, key=lambda x: sclk[x])
                        S.cur_fin = 0.0
                        if k == "at":
                            step_attn()
                        else:
                            try:
                                next(cur[k])
                            except StopIteration:
                                cur[k] = None
                                done[k] += 1
                        if S.cur_fin > 0.0:
                            sclk[k] = S.cur_fin
                    assert done == {"f1": 8, "f2": 8, "bk": 8}, done
                else:
                    for k_ in range(10):
                        gens = []
                        if k_ < 8:
                            gens.append(rw_f1(k_))
                        if 1 <= k_ <= 8:
                            gens.append(rw_f2(k_ - 1))
                        if k_ >= 2:
                            gens.append(rw_back(k_ - 2))
                        if INTERLEAVE:
                            run_rr(gens)
                        else:
                            for g_ in reversed(gens):
                                for _ in g_:
                                    pass
                while not ag_done[0]:
                    step_attn()

                if not phaseB:
                    continue
                for cc in range(8):
                    sa, sbb = ((tv(2), tv(3)), (tv(5), tv(6)))[cc % 2]
                    wpb = wload(CH_PB + cc)
                    wv = wpb[:, :].re("p (a k c) -> p a k c", a=4, c=128)
                    pz = nextp()
                    for kc in range(8):
                        mm(out=pz[:, 0:TT], lhsT=wv[:, 0, kc, :], rhs=hT[:, kc, 1:TT + 1], start=(kc == 0), stop=(kc == 7))
                    sigmoid_to(sa[:, :], pz[:, 0:TT])
                    pz = nextp()
                    for kc in range(8):
                        mm(out=pz[:, 0:TT], lhsT=wv[:, 1, kc, :], rhs=hT[:, kc, 1:TT + 1], start=(kc == 0), stop=(kc == 7))
                    sigmoid_to(sbb[:, :], pz[:, 0:TT])
                    pz = nextp()
                    for kc in range(8):
                        mm(out=pz[:, 0:TT], lhsT=wv[:, 2, kc, :], rhs=yfin[:, kc, :], start=(kc == 0), stop=(kc == 7))
                    vec("tensor_tensor", out=sa[:, :], in0=sa[:, :], in1=pz[:, 0:TT], op=ALU.mult)
                    pz = nextp()
                    for hh_ in range(8):
                        mm(out=pz[:, 0:TT], lhsT=wv[0:64, 3, hh_, :], rhs=oT[:, hh_, :], start=(hh_ == 0), stop=(hh_ == 7))
                    vec("tensor_tensor", out=sbb[:, :], in0=sbb[:, :], in1=pz[:, 0:TT], op=ALU.mult)
                    vec("tensor_tensor", out=mixT[:, cc, :], in0=sa[:, :], in1=sbb[:, :], op=ALU.add)

                def norm_residual(ps_views, gb):
                    for hf in range(2):
                        act(out=junk[:, 0:512], in_=ps_views[hf], func=AF.Square, accum_out=st4[:, 8 + hf:9 + hf])
                    vec("tensor_tensor", out=st4[:, 10:11], in0=st4[:, 8:9], in1=st4[:, 9:10], op=ALU.add)
                    rsqrt_to(st4[:, 4:5], st4[:, 10:11], eps_r, 1.0 / D)
                    for hf in range(2):
                        for qq in range(2):
                            cs_ = slice(hf * 512 + qq * 256, hf * 512 + (qq + 1) * 256)
                            vec("scalar_tensor_tensor", out=utmp[:, :], in0=ps_views[hf][:, qq * 256:(qq + 1) * 256],
                                scalar=st4[:, 4:5], in1=gb[:, cs_], op0=ALU.mult, op1=ALU.mult)
                            vec("tensor_tensor", out=xm[:, sub, cs_], in0=xm[:, sub, cs_], in1=utmp[:, :], op=ALU.add)

                wo = [wload(CH_WOUT + 0), wload(CH_WOUT + 1)]
                for sub in range(NSUB):
                    for hf in range(2):
                        wv = wo[hf][:, :].re("p (k c) -> p k c", c=512)
                        for kc in range(8):
                            mm(out=(B1, B2)[hf][:, :], lhsT=mixT[:, kc, sub * 128:(sub + 1) * 128], rhs=wv[:, kc, :],
                               start=(kc == 0), stop=(kc == 7))
                    norm_residual([B1[:, :], B2[:, :]], gmb)
                rms_rstd(xm, 2)
                norm_transpose(xm, 2, h2T, PD_A2, lambda kc: modf[:, 24 + kc:25 + kc], 0)
                accs = [[B1[:, :], B2[:, :]], [B3[:, :], B5[:, :]]]
                for pg in range(3):
                    nk = 8 if pg < 2 else 6
                    for i4 in range(nk // 2):
                        i = pg * 4 + i4
                        wf_ = wload(CH_FF + i)
                        wv = wf_[:, :].re("p (k c) -> p k c", c=512)
                        for jj in range(2):
                            jl = i4 * 2 + jj
                            pg_ = nextp("f")
                            for kc in range(8):
                                mm(out=pg_[:, 0:TT], lhsT=wv[:, kc, jj * 128:(jj + 1) * 128], rhs=h2T[:, kc, :],
                                   start=(kc == 0), stop=(kc == 7))
                            act(out=sg[:, :], in_=pg_[:, 0:TT], func=AF.Silu)
                            pu = nextp("f")
                            for kc in range(8):
                                mm(out=pu[:, 0:TT], lhsT=wv[:, kc, 256 + jj * 128:256 + (jj + 1) * 128], rhs=h2T[:, kc, :],
                                   start=(kc == 0), stop=(kc == 7))
                            vec("tensor_tensor", out=actT[:, jl, :], in0=sg[:, :], in1=pu[:, 0:TT], op=ALU.mult)
                    for hf in range(2):
                        wf_ = wload(CH_FO + pg * 2 + hf)
                        wv = wf_[:, :].re("p (k c) -> p k c", c=512)
                        for sub in range(NSUB):
                            for kc in range(nk):
                                mm(out=accs[sub][hf], lhsT=actT[:, kc, sub * 128:(sub + 1) * 128], rhs=wv[:, kc, :],
                                   start=(pg == 0 and kc == 0), stop=(pg == 2 and kc == nk - 1))
                for sub in range(NSUB):
                    norm_residual(accs[sub], gfb)
                r0 = (m - PB0) * TT
                S.dma("gpsimd", out=y_d.v(y_d.t[r0:r0 + TT, :].rearrange("(s p) c -> p s c", p=128)), in_=xm[:])


        except _Stop:
            pass
        S.finish([y_d] + finals)
        S.emit()
    return nc


_CACHE = {}


def prep_inputs(x, c, w_mod, b_mod, g_pre_mix, g_post_mix, g_pre_ffn, g_post_ffn, w_in, mu_rkv, mu_lora,
           w0, w1, w2, a0, a1, a2, g1, g2, k_k, k_a, r_k, ln_x_w, ln_x_b, w_o_rwkv, w_o_attn, w_out,
           w_ffn_in, w_ffn_out):
    f = lambda a: np.asarray(a, np.float32)
    x = f(x); c = f(c)
    w_in = f(w_in)[0]; w_modm = f(w_mod)[0]
    bm = f(b_mod)[0].reshape(6, 1024)
    vecs = [bm[0], bm[1], bm[2], bm[3], bm[4], bm[5], f(g_pre_mix)[0], f(g_post_mix)[0], f(g_pre_ffn)[0],
            f(g_post_ffn)[0], f(mu_rkv)[0, 0], f(mu_rkv)[0, 1], f(mu_rkv)[0, 2], f(mu_lora)[0, 0], f(mu_lora)[0, 1],
            f(mu_lora)[0, 2], f(w0)[0], f(a0)[0], f(k_k)[0], f(k_a)[0], f(r_k)[0].reshape(-1), f(ln_x_w)[0],
            f(ln_x_b)[0]]
    wsrc = np.zeros((NCH, 128, 4096), np.float32)
    def put(i, arr3):
        P, K, C = arr3.shape
        v = wsrc[i].reshape(128, -1)
        tmp = np.zeros((128, K, 4096 // K if K in (8,) else C), np.float32) if False else None
        blk = np.zeros((128, K * C), np.float32)
        blk[:P] = arr3.reshape(P, K * C)
        v[:, :K * C] = blk
    for cch in range(8):
        a = np.zeros((128, 8, 512), np.float32)
        for j in range(3):
            a[:, :, j * 128:(j + 1) * 128] = _wchunk(w_in, slice(j * 1024 + cch * 128, j * 1024 + (cch + 1) * 128))
        put(CH_RW + cch, a)
    for g in range(3):
        for j in range(3):
            o = 3072 + j * 1536 + g * 512
            put(CH_AT + g * 3 + j, _wchunk(w_in, slice(o, o + 512)))
    wor = f(w_o_rwkv)[0]; woa = f(w_o_attn)[0]; wout = f(w_out)[0]
    for cc in range(8):
        cs_ = slice(cc * 128, (cc + 1) * 128)
        a = np.zeros((128, 4, 8, 128), np.float32)
        a[:, 0] = _wchunk(w_in, slice(7680 + cc * 128, 7680 + (cc + 1) * 128))
        a[:, 1] = _wchunk(w_in, slice(8704 + cc * 128, 8704 + (cc + 1) * 128))
        a[:, 2] = _wchunk(wor, cs_)
        a[0:64, 3] = woa[:, cs_].reshape(8, 64, 128).transpose(1, 0, 2)
        put(CH_PB + cc, a.reshape(128, 32, 128))
    for hf in range(2):
        put(CH_WOUT + hf, _wchunk(wout, slice(hf * 512, (hf + 1) * 512)))
    wfi = f(w_ffn_in)[0]; wfo = f(w_ffn_out)[0]
    for i in range(11):
        a = np.zeros((128, 8, 512), np.float32)
        a[:, :, 0:256] = _wchunk(wfi, slice(i * 256, (i + 1) * 256))
        a[:, :, 256:512] = _wchunk(wfi, slice(FH + i * 256, FH + (i + 1) * 256))
        put(CH_FF + i, a)
    for pg in range(3):
        nk = 8 if pg < 2 else 6
        for hf in range(2):
            blk = wfo[pg * 1024:pg * 1024 + nk * 128, hf * 512:(hf + 1) * 512]
            put(CH_FO + pg * 2 + hf, blk.reshape(nk, 128, 512).transpose(1, 0, 2))
    wmod = np.ascontiguousarray(
        w_modm.reshape(8, 128, 24, 256).transpose(2, 1, 0, 3).reshape(24, 128, 2048))
    l1 = np.concatenate([f(w1)[0], f(a1)[0], f(g1)[0]], 1)
    l1 = np.ascontiguousarray(l1.reshape(8, 128, 288).transpose(1, 0, 2).reshape(128, 8 * 288))
    l2 = np.zeros((128, 3, 1024), np.float32)
    l2[0:64, 0] = f(w2)[0]; l2[64:128, 0] = f(a2)[0]
    l2[:, 1] = f(g2)[0][0:128]; l2[0:32, 2] = f(g2)[0][128:160]
    l2 = l2.reshape(128, 3072)
    cbt = _host_consts()
    in_maps = []
    for core in range(8):
        b, hh = core // 2, core % 2
        pfm = np.concatenate([_fm(v) for v in vecs] + [_fm(c[b])], 1)
        if hh == 1:
            xvv = x[b]
        else:
            xvv = np.concatenate([np.zeros((T // 2, D), np.float32), x[b, :T // 2]], 0)
        in_maps.append({"xv": np.ascontiguousarray(xvv), "pfm": np.ascontiguousarray(pfm), "wmod": wmod,
                        "wsrc": wsrc, "l1": l1, "l2": l2, "cbt": cbt, "cft": _host_cf(hh)})
    return in_maps


def kernel(**inputs):
    in_maps = prep_inputs(**inputs)
    if "nc" not in _CACHE:
        _CACHE["nc"] = build()
    nc = _CACHE["nc"]
    res = run_bass_kernel_spmd(nc, in_maps, core_ids=list(range(8)))
    out = np.zeros((4, T, D), np.float32)
    for core in range(8):
        b, hh = core // 2, core % 2
        out[b, hh * (T // 2):(hh + 1) * (T // 2)] = res.results[core]["y"]
    return out
```

```python
import math
from contextlib import ExitStack

import numpy as np
import concourse.bass as bass
import concourse.mybir as mybir
from concourse.bass_utils import run_bass_kernel_spmd

F32 = mybir.dt.float32
BF16 = mybir.dt.bfloat16
AF = mybir.ActivationFunctionType
ALU = mybir.AluOpType
AX = mybir.AxisListType

ENGS = ("tensor", "vector", "scalar", "gpsimd", "sync")

T = 8192
D = 1024
TT = 256
NT = T // TT
PB0 = NT // 2
NSUB = TT // 128
FH = 2816
C0 = math.exp(-0.5)
GN_EPS = 64e-5
RMS_EPS = 1e-6
NSLOT = 4
import os
INTERLEAVE = os.environ.get('NOIL') is None
ATTN_EVERY = int(os.environ.get('ATTN_EVERY', '3'))
GPS_OFF = os.environ.get('GPS_OFF', '0') == '1'
LISTSCHED = os.environ.get('LISTSCHED', '1') == '1'
DENSE_ATTN_PROJ = os.environ.get('DENSE_ATTN_PROJ', '0') == '1'
S1_PREFETCH = os.environ.get('S1_PREFETCH', '1') == '1'
SEM_LAT = float(os.environ.get('SEM_LAT', '300'))
PE_SCALE = float(os.environ.get('PE_SCALE', '1.0'))
ACT_SCALE = float(os.environ.get('ACT_SCALE', '1.0'))
DVE_SCALE = float(os.environ.get('DVE_SCALE', '1.0'))


class Buf:
    def __init__(self, name, t):
        self.name = name
        self.t = t
        self.writer = None
        self.readers = []
        self.dsem = None
        self.dcnt = 0
        self.psum = False

    def __getitem__(self, idx):
        return View(self, self.t[idx])

    def v(self, ap):
        return View(self, ap)


class SubBuf:
    def __init__(self, buf, col0, ncols=None):
        self.buf = buf
        self.col0 = col0
        self.ncols = ncols

    def __getitem__(self, idx):
        ps, cs = idx
        a = 0 if cs.start is None else cs.start
        e = cs.stop if cs.stop is not None else self.ncols
        assert e is not None
        return View(self.buf, self.buf.t[ps, self.col0 + a:self.col0 + e])


class View:
    def __init__(self, buf, ap):
        self.buf = buf
        self.ap = ap

    def __getitem__(self, idx):
        return View(self.buf, self.ap[idx])

    def re(self, pat, **kw):
        return View(self.buf, self.ap.rearrange(pat, **kw))

    def bc(self, axis, shape):
        return View(self.buf, self.ap.unsqueeze(axis).to_broadcast(list(shape)))


def _unw(x):
    return x.ap if isinstance(x, View) else x


class Sched:
    def __init__(self, nc, stack):
        self.nc = nc
        self.stack = stack
        self.q = {e: [] for e in ENGS}
        self.waited = {e: {} for e in ENGS}
        self.dma_sems = []
        self.fin = {}
        self.eng_free = {e: 0.0 for e in ENGS}
        self.cur_fin = 0.0

    def _est(self, eng, tok, deps_tokens, dur):
        ready = 0.0
        for t_ in deps_tokens:
            f_ = self.fin.get(t_)
            if f_ is not None and f_ > ready:
                ready = f_
        start = max(ready + SEM_LAT, self.eng_free[eng])
        fin = start + dur
        self.eng_free[eng] = fin if eng != "sync" and eng != "gpsimd" else start + 60.0
        self.fin[tok] = fin
        if len(self.fin) > 60000:
            ks = list(self.fin.keys())[:30000]
            for k_ in ks:
                del self.fin[k_]
        if fin > self.cur_fin:
            self.cur_fin = fin

    def sb(self, name, shape, dt):
        t = self.stack.enter_context(self.nc.sbuf_tensor("s_" + name, list(shape), dt))
        return Buf(name, t)

    def ps(self, name, shape, dt=F32):
        t = self.stack.enter_context(self.nc.psum_tensor("p_" + name, list(shape), dt))
        return Buf(name, t)

    def dram(self, name, shape, dt, kind):
        t = self.nc.dram_tensor(name, list(shape), dt, kind=kind).ap()
        return Buf(name, t)

    def _deps(self, eng, reads, writes):
        deps = {}

        def add(tok):
            if tok is None:
                return
            k, v = tok
            if deps.get(k, 0) < v:
                deps[k] = v

        for b in reads:
            add(b.writer)
            if b.psum:
                for r in b.readers:
                    if r[0] != eng:
                        add(r)
        for b in writes:
            add(b.writer)
            for r in b.readers:
                add(r)
        waits = []
        for k, v in deps.items():
            if k == "tensor" and eng == "tensor":
                continue
            if self.waited[eng].get(k, 0) >= v:
                continue
            self.waited[eng][k] = v
            waits.append((k, v))
            if isinstance(k, str):
                self.q[k][v - 1][2] = True
        return waits

    def _commit(self, tok, reads, writes):
        for b in writes:
            b.writer = tok
            b.readers = []
        for b in reads:
            if b in writes:
                continue
            b.readers.append(tok)
            if len(b.readers) > 48:
                d = {}
                for k, v in b.readers:
                    if d.get(k, 0) < v:
                        d[k] = v
                b.readers = list(d.items())

    def op(self, eng, meth, **kw):
        writes, reads = [], []
        for k, v in kw.items():
            if isinstance(v, View):
                if k in ("out", "accum_out", "ap"):
                    if v.buf not in writes:
                        writes.append(v.buf)
                else:
                    if v.buf not in reads:
                        reads.append(v.buf)
        dep_toks = [b_.writer for b_ in reads + writes if b_.writer is not None]
        for b_ in writes:
            dep_toks.extend(b_.readers)
        waits = self._deps(eng, reads, writes)
        if eng == "tensor":
            src = kw.get("lhsT", kw.get("in_"))
            lo = src.ap.base_partition()
            rows = (lo, lo + src.ap.partition_size())
            ob = kw["out"].buf
            prev = getattr(ob, "pe_rows", None)
            if prev is not None and ob.writer is not None and ob.writer[0] == "tensor" and \
                    (rows[1] <= prev[0] or prev[1] <= rows[0]):
                k, v = ob.writer
                if self.waited[eng].get(k, 0) < v:
                    self.waited[eng][k] = v
                    waits.append((k, v))
                    self.q[k][v - 1][2] = True
            ob.pe_rows = rows
        args = {k: _unw(v) for k, v in kw.items()}
        fn = lambda e, m=meth, a=args: getattr(e, m)(**a)
        self.q[eng].append([waits, fn, False, None])
        tok = (eng, len(self.q[eng]))
        o_ = kw.get("out", kw.get("ap"))
        try:
            fsz = o_.ap.free_size()
        except Exception:
            fsz = 256
        if eng == "tensor":
            n_ = kw["rhs"].ap.free_size() if "rhs" in kw else 128
            dur = (max(n_, 64) / 1.2 + 30.0) * PE_SCALE
        else:
            dur = (200.0 + 0.65 * fsz) * ACT_SCALE if eng == "scalar" else (150.0 + 0.75 * fsz) * DVE_SCALE
        self._est(eng, tok, dep_toks, dur)
        self._commit(tok, reads, writes)
        return tok

    def dma(self, eng, out, in_, **kw):
        sb = out.buf
        if sb.dsem is None:
            sb.dsem = ("dma", len(self.dma_sems))
            self.dma_sems.append(sb.name)
        dep_toks = [b_.writer for b_ in (in_.buf, out.buf) if b_.writer is not None] + list(out.buf.readers)
        waits = self._deps(eng, [in_.buf], [out.buf])
        sb.dcnt += 16
        tok = (sb.dsem, sb.dcnt)
        try:
            nbytes = out.ap.nbytes()
        except Exception:
            nbytes = 1 << 20
        self._est(eng, tok, dep_toks, 2500.0 + nbytes / 150.0)
        a = dict(out=out.ap, in_=in_.ap, **kw)
        fn = lambda e, a=a: e.dma_start(**a)
        self.q[eng].append([waits, fn, False, sb.dsem])
        self._commit(tok, [in_.buf], [out.buf])
        return tok

    def finish(self, final_bufs):
        waits = self._deps("sync", final_bufs, [])
        self.q["sync"].append([waits, None, False, None])

    def emit(self):
        nc = self.nc
        st = self.stack
        esem = {e: st.enter_context(nc.semaphore("es_" + e)) for e in ENGS}
        dsem = [st.enter_context(nc.semaphore("ds%d" % i)) for i in range(len(self.dma_sems))]
        cum = {}
        for e in ENGS:
            c = 0
            arr = []
            for it in self.q[e]:
                if it[2]:
                    c += 1
                arr.append(c)
            cum[e] = arr

        def semval(k, v):
            if isinstance(k, str):
                return esem[k], cum[k][v - 1]
            return dsem[k[1]], v

        block = st.enter_context(nc.Block())

        def run(e, eng):
            for waits, fn, sig, dk in self.q[e]:
                for k, v in waits:
                    s, val = semval(k, v)
                    eng.wait_ge(s, val)
                if fn is None:
                    continue
                ins = fn(eng)
                if dk is not None:
                    ins.then_inc(dsem[dk[1]], 16)
                elif sig:
                    ins.then_inc(esem[e], 1)

        @block.tensor
        def _(eng):
            run("tensor", eng)

        @block.vector
        def _(eng):
            run("vector", eng)

        @block.scalar
        def _(eng):
            run("scalar", eng)

        @block.gpsimd
        def _(eng):
            run("gpsimd", eng)

        @block.sync
        def _(eng):
            run("sync", eng)


def _alibi_slopes(n):
    def pow2(m):
        start = 2.0 ** (-8.0 / m)
        return [start ** (i + 1) for i in range(m)]
    if math.log2(n).is_integer():
        s = pow2(n)
    else:
        p = 2 ** int(math.floor(math.log2(n)))
        s = pow2(p) + pow2(2 * p)[0::2][: n - p]
    return sorted(s, reverse=True)


(PV_SHM, PV_SCM, PV_GTM, PV_SHF, PV_SCF, PV_GTF, PV_GPM, PV_GQM, PV_GPF, PV_GQF,
 PV_MUR, PV_MUK, PV_MUV, PV_MUW, PV_MUA, PV_MUG, PV_W0, PV_A0, PV_KK, PV_KA, PV_RK,
 PV_LNW, PV_LNB, PV_C) = range(24)
NPV = 24

CB_ID = 0
CB_ONESBD = 128
CB_ONES = 256
CB_MT4 = 320
CB_ML4 = 832
CB_E1 = 1344
CB_EA2 = CB_E1 + 8 * 256
CB_EB2 = CB_EA2 + 2 * 8 * 64
CB_EA3 = CB_EB2 + 8 * 64
CB_EB3 = CB_EA3 + 8 * 8 * 16
NCB = CB_EB3 + 8 * 16
CF_MSK = 0
CF_VM = 256
CF_EPS = CF_VM + 128
CF_IDF = CF_EPS + 4
NCF = CF_IDF + 128

CH_RW = 0
CH_AT = 8
CH_PB = 17
CH_WOUT = 25
CH_FF = 27
CH_FO = 38
NCH = 44


def _host_consts():
    sl = np.asarray(_alibi_slopes(24), np.float64).reshape(3, 8)
    cb = np.zeros((128, NCB), np.float32)
    p = np.arange(128)
    cb[:, CB_ID:CB_ID + 128] = np.eye(128)
    cb[:, CB_ONESBD:CB_ONESBD + 128] = (p[:, None] // 64 == p[None, :] // 64)
    cb[:, CB_ONES:CB_ONES + 64] = 1.0
    same = (p[:, None] // 64 == p[None, :] // 64)
    su = same & (p[:, None] < p[None, :])
    iu = same & (p[:, None] <= p[None, :])
    slo = same & (p[:, None] > p[None, :])
    cb[:, CB_MT4:CB_MT4 + 512] = np.concatenate([su, iu, su, iu], 1)
    cb[:, CB_ML4:CB_ML4 + 512] = np.concatenate([slo] * 4, 1)
    k = p[:, None].astype(np.float64)
    q = p[None, :].astype(np.float64)
    for h in range(8):
        dpv = q - k + 128
        e_prev = np.where(dpv <= 128, np.exp(-sl[0, h] * dpv), 0.0)
        dcu = q - k
        e_cur = np.where(dcu >= 0, np.exp(-sl[0, h] * np.maximum(dcu, 0)), 0.0)
        cb[:, CB_E1 + h * 256: CB_E1 + h * 256 + 128] = e_prev
        cb[:, CB_E1 + h * 256 + 128: CB_E1 + h * 256 + 256] = e_cur
    i64 = np.arange(64)[None, :].astype(np.float64)
    for rot in range(2):
        for h in range(8):
            j = p // 64
            pp = (p % 64).astype(np.float64)
            a = ((rot - j - 1) % 2) + 1
            dl = 64.0 * a[:, None] + i64 - pp[:, None]
            e = np.where(dl <= 128, np.exp(-sl[1, h] * 4.0 * dl), 0.0)
            o = CB_EA2 + (rot * 8 + h) * 64
            cb[:, o:o + 64] = e
    for h in range(8):
        kk = np.arange(64)[:, None].astype(np.float64)
        dl = i64 - kk
        e = np.where(dl >= 0, np.exp(-sl[1, h] * 4.0 * np.maximum(dl, 0)), 0.0)
        o = CB_EB2 + h * 64
        cb[0:64, o:o + 64] = e
    i16 = np.arange(16)[None, :].astype(np.float64)
    for rot in range(8):
        for h in range(8):
            j = p // 16
            pp = (p % 16).astype(np.float64)
            a = ((rot - j - 1) % 8) + 1
            dl = 16.0 * a[:, None] + i16 - pp[:, None]
            e = np.where(dl <= 128, np.exp(-sl[2, h] * 16.0 * dl), 0.0)
            o = CB_EA3 + (rot * 8 + h) * 16
            cb[:, o:o + 16] = e
    for h in range(8):
        kk = np.arange(16)[:, None].astype(np.float64)
        dl = i16 - kk
        e = np.where(dl >= 0, np.exp(-sl[2, h] * 16.0 * np.maximum(dl, 0)), 0.0)
        o = CB_EB3 + h * 16
        cb[0:16, o:o + 16] = e
    return cb


def _host_cf(hh):
    cf = np.zeros((128, NCF), np.float32)
    m = np.ones((128, 256), np.float32)
    m[:, 0::64] = 0.0
    cf[:, CF_MSK:CF_MSK + 256] = m
    valid = lambda t: 0.0 if t < 0 else (1.0 if (hh == 1 or t >= PB0) else 0.0)
    p = np.arange(128)
    for t in range(NT):
        cf[:, CF_VM + t] = valid(t)
        cf[:, CF_VM + 32 + t] = valid(t - 1)
        j = p // 64
        a = ((t - j - 1) % 2) + 1
        cf[:, CF_VM + 64 + t] = [valid(t - aa) for aa in a]
        j = p // 16
        a = ((t - j - 1) % 8) + 1
        cf[:, CF_VM + 96 + t] = [valid(t - aa) for aa in a]
    cf[:, CF_EPS] = RMS_EPS
    cf[:, CF_EPS + 1] = GN_EPS
    cf[:, CF_EPS + 3] = 1.0
    cf[:, CF_IDF:CF_IDF + 128] = np.eye(128)
    return cf


def _fm(v):
    return np.ascontiguousarray(v.reshape(8, 128).T)


def _wchunk(w, cols):
    return w[:, cols].reshape(8, 128, -1).transpose(1, 0, 2)


class _Stop(Exception):
    pass


def build(nt=NT, dbg=None, dbg_tile=0, dbg_c=0, stop=None):
    nc = bass.Bass("TRN2", target_bir_lowering=False)
    with ExitStack() as st:
        S = Sched(nc, st)
        finals = []

        def CK(name):
            if stop == name:
                raise _Stop()

        def DBG(name, view, m=None, c=None):
            if not dbg or name not in dbg:
                return
            if m is not None and m != dbg_tile:
                return
            if c is not None and c != dbg_c:
                return
            shp = list(view.ap.shape)
            dd = S.dram("dbg_" + name, shp, view.ap.dtype, "ExternalOutput")
            S.dma("gpsimd", out=dd[:], in_=view)
            finals.append(dd)
        xv = S.dram("xv", [T, D], F32, "ExternalInput")
        pfm_d = S.dram("pfm", [128, NPV * 8], F32, "ExternalInput")
        wmod_d = S.dram("wmod", [24, 128, 2048], F32, "ExternalInput")
        wsrc = S.dram("wsrc", [NCH, 128, 4096], F32, "ExternalInput")
        l1_d = S.dram("l1", [128, 8 * 288], F32, "ExternalInput")
        l2_d = S.dram("l2", [128, 3 * 1024], F32, "ExternalInput")
        cb_d = S.dram("cbt", [128, NCB], F32, "ExternalInput")
        cf_d = S.dram("cft", [128, NCF], F32, "ExternalInput")
        y_d = S.dram("y", [T // 2, D], F32, "ExternalOutput")
        wscr_all = S.dram("wscr", [NCH, 128, 4096], BF16, "Internal")
        wscr = [Buf("wscr%d" % i, wscr_all.t[i]) for i in range(NCH)]

        cb = S.sb("cb", [128, NCB], BF16)
        cf = S.sb("cf", [128, NCF], F32)
        pf = S.sb("pf", [128, NPV * 8], F32)
        pd = S.sb("pd", [128, 12 * 8], F32)
        gmb = S.sb("gmb", [128, 1024], BF16)
        gfb = S.sb("gfb", [128, 1024], BF16)
        l1a = S.sb("l1a", [128, 8, 288], BF16)
        l1b = S.sb("l1b", [128, 8, 288], BF16)
        l2 = S.sb("l2", [128, 3, 1024], BF16)
        ring = [S.sb("ring%d" % i, [128, 4096], BF16) for i in range(NSLOT)]
        xt1 = S.sb("xt", [128, NSUB, 1024], F32)
        xt = [xt1, xt1]
        nb = S.sb("nb", [128, NSUB, 1024], BF16)
        junk = nb[:, 0, :]
        st4 = S.sb("st4", [128, 16], F32)
        hT = S.sb("hT", [128, 8, TT + 1], BF16)
        h2T = S.sb("h2T", [128, 8, TT], BF16)
        mixT = h2T
        Zf = S.sb("Zf", [128, 8, 64], F32)
        Zb = S.sb("Zb", [128, 8, 2, 64], BF16)
        hal = S.sb("hal", [128, 8, 3], F32)
        tp = [S.sb("tp%d" % i, [128, TT + 1], F32) if i != 7 else None for i in range(12)]
        tp[7] = tp[6]
        tv = lambda i: tp[i][:, 0:TT]
        pj = [tp[0], tp[0], tp[0]]
        tmpd = tv(1)
        rkv = [tv(2), tv(3), tv(4)]
        sw = tv(5); asig = tv(6); gg = tv(7); cs = tv(0); cm = tv(1)
        Ep = tv(8); En = tv(9); Em = tv(10); rinv = tv(1); kkb = tv(11); ff = tv(0)
        kmod = tv(5); bv = tv(1); bon = tv(6); yln = tv(9); ysq = tv(10)
        sqb = S.sb("sqb", [128, TT], BF16)
        ARs = [S.sb("AR%d" % i, [128, NSUB, 2, 128], BF16) for i in range(3)]
        Bts = [S.sb("Bt%d" % i, [128, TT], BF16) for i in range(2)]
        Kts = [S.sb("Kt%d" % i, [128, TT], BF16) for i in range(2)]
        vbfs = [S.sb("vbf%d" % i, [128, TT], BF16) for i in range(2)]
        Bpads = [S.sb("Bpad%d" % i, [128, NSUB, 2, 128], BF16) for i in range(2)]
        Kpads = [S.sb("Kpad%d" % i, [128, NSUB, 2, 128], BF16) for i in range(2)]
        Vtms = [S.sb("Vtm%d" % i, [128, NSUB, 128], BF16) for i in range(2)]
        AMs = [S.sb("AM%d" % i, [128, 4, 512], BF16) for i in range(2)]
        TTfs = [S.sb("TTf%d" % i, [128, 4, 128], BF16) for i in range(2)]
        gbs = [S.sb("gb%d" % i, [128, TT], BF16) for i in range(3)]
        pcss = [S.sb("pcs%d" % i, [128, 4], F32) for i in range(3)]
        ysqB = S.sb("ysqB", [128, TT], F32)
        L0 = S.sb("L0", [128, 4, 128], BF16)
        LP = [S.sb("LP%d" % i, [128, 4, 128], BF16) for i in range(2)]
        LT = [S.sb("LT%d" % i, [128, 4, 128], BF16) for i in range(2)]
        SS = [S.sb("SS%d" % i, [128, 4, 128], BF16) for i in range(2)]
        Xb = S.sb("Xb", [128, 128], BF16)
        Ub = S.sb("Ub", [128, 128], BF16)
        ztmp = S.sb("ztmp", [128, 64], F32)
        Ytm = S.sb("Ytm", [128, NSUB, 128], F32)
        ynb = S.sb("ynb", [128, NSUB, 128], BF16)
        gst = S.sb("gst", [128, 32], F32)
        lw = S.sb("lw", [128, TT], BF16)
        lga = S.sb("lga", [128, TT], BF16)
        lgb = S.sb("lgb", [32, TT], BF16)
        yfin = S.sb("yfin", [128, 8, TT], BF16)
        _nbf = nb[:].re("p s c -> p (s c)")
        Qa = [h2T[:, 0:4, :], h2T[:, 4:8, :], _nbf[:, 0:1024].re("p (c t) -> p c t", t=TT)]
        K1 = S.sb("K1", [128, 4, 128 + TT], BF16)
        V1 = S.sb("V1", [128, 3, 512], BF16)
        K2c = S.sb("K2c", [128, 4, TT], BF16)
        K2r = S.sb("K2r", [128, 4, 4, 128], BF16)
        V2c = S.sb("V2c", [64, 4, 128], BF16)
        V2r = S.sb("V2r", [128, 4, 512], BF16)
        K3c = S.sb("K3c", [128, 4, TT], BF16)
        K3r = S.sb("K3r", [128, 4, 16, 128], BF16)
        V3c = S.sb("V3c", [16, 16, 128], BF16)
        V3r = S.sb("V3r", [128, 16, 512], BF16)
        VF = _nbf[:, 1024:2048].re("p (c t) -> p c t", t=TT)
        pe = S.sb("pe", [128, 512], BF16)
        pp_ = S.sb("pp", [128, 512], BF16)
        peb = SubBuf(pe, 256, 256)
        ppb = SubBuf(pp_, 256, 256)
        accO = S.sb("accO", [64, TT], F32)
        accD = S.sb("accD", [64, TT], F32)
        oT = S.sb("oT", [64, 8, TT], BF16)
        sa = tv(2); sbb = tv(3); sg = tv(4); utmp = tv(10)
        actT = S.sb("actT", [128, 8, TT], BF16)
        VF2 = actT[:, 0:4, :]
        wst = xt1[:].re("p s c -> p (s c)")

        _b0 = S.ps("b0", [128, 512])
        _b4 = S.ps("b4", [128, 512])
        _bS = S.ps("bS", [128, 1024])
        B3 = S.ps("b3", [128, 512])
        B5 = S.ps("b5", [128, 512])
        B6 = S.ps("b6", [128, 512])
        _pT = S.ps("pT", [128, 1024], BF16)
        R0 = SubBuf(_b0, 0); R1 = SubBuf(_b0, 256)
        Q0 = SubBuf(_b4, 0); Q1 = SubBuf(_b4, 256)
        B1 = Buf("B1", _bS.t[:, 0:512]); B2 = Buf("B2", _bS.t[:, 512:1024])
        pTr = SubBuf(_pT, 0); pTa = SubBuf(_pT, 512)
        for b_ in (_b0, _b4, B1, B2, B3, B5, B6, _pT):
            b_.psum = True
        pC = B3
        trB = [View(B1, B1.t[:, :].bitcast(BF16)), View(B2, B2.t[:, :].bitcast(BF16))]
        trC = View(B3, B3.t[:, :].bitcast(BF16))
        trot = [View(_pT, _pT.t[:, 0:512]), View(B6, B6.t[:, :].bitcast(BF16)), View(B5, B5.t[:, :].bitcast(BF16))]
        prot = {"r": [_b0, B1, B2], "a": [_b4, B5, B6], "x": [_b0, _b4, B1, B2, B3, B6], "f": [_b0, _b4, B6], "s": [_b0, _b4, B1, B2, B3, B6]}
        prot_i = {"r": 0, "a": 0, "x": 0, "f": 0, "s": 0}

        def nextp(k="x"):
            prot_i[k] = (prot_i[k] + 1) % len(prot[k])
            return prot[k][prot_i[k]]

        mm = lambda **kw: S.op("tensor", "matmul", **kw)
        tr = lambda **kw: S.op("tensor", "transpose", **kw)
        act = lambda **kw: S.op("scalar", "activation", **kw)
        vec = lambda m, **kw: S.op("vector", m, **kw)
        gps = lambda m, **kw: S.op("gpsimd", m, **kw)

        def sigmoid_to(dst, src, nbias=None, scale=1.0):
            if nbias is None:
                act(out=dst, in_=src, func=AF.Exp, scale=-scale)
            else:
                act(out=dst, in_=src, func=AF.Exp, scale=-scale, bias=nbias)
            act(out=dst, in_=dst, func=AF.Ln, bias=one_c_for(dst))
            act(out=dst, in_=dst, func=AF.Exp, scale=-1.0)

        def one_c_for(v):
            lo = v.ap.base_partition()
            n = v.ap.partition_size()
            return cf[lo:lo + n, CF_EPS + 3:CF_EPS + 4]

        def rsqrt_to(dst, src, bias_ap, scale=1.0):
            act(out=dst, in_=src, func=AF.Ln, bias=bias_ap, scale=scale)
            act(out=dst, in_=dst, func=AF.Exp, scale=-0.5)

        ident = cb[:, CB_ID:CB_ID + 128]
        identf = cf[:, CF_IDF:CF_IDF + 128]
        onesbd = cb[:, CB_ONESBD:CB_ONESBD + 128]
        eps_r = cf[:, CF_EPS:CF_EPS + 1]
        eps_g = cf[:, CF_EPS + 1:CF_EPS + 2]
        zero_c = cf[:, CF_EPS + 2:CF_EPS + 3]
        one_c = cf[:, CF_EPS + 3:CF_EPS + 4]

        def pv(i, kc):
            return pf[:, i * 8 + kc: i * 8 + kc + 1]

        def pdv(i, kc):
            return pd[:, i * 8 + kc: i * 8 + kc + 1]
        PD_A1, PD_A2, PD_GM, PD_GF, PD_OMK, PD_OMR, PD_OMKm, PD_OMV = range(8)

        try:
            S.dma("gpsimd", out=cb[:, :], in_=cb_d[:, :])
            S.dma("sync", out=cf[:, :], in_=cf_d[:, :])
            S.dma("sync", out=pf[:, :], in_=pfm_d[:, :])
            for i in range(NCH):
                S.dma("gpsimd", out=wscr[i][:, :], in_=wsrc[i])
            CK('dma0')
            for b_ in (Zf, Zb, hal, Bpads[0], Bpads[1], Kpads[0], Kpads[1], K1, K2r, V2r, K3r, V3r, V1, hT, Xb, Ub, Vtms[0], Vtms[1]):
                gps("memset", ap=b_[:], constant=0.0)
            CK('memset')
            for half in range(2):
                S.dma("sync", out=wst[:, 0:4 * 288], in_=l1_d[:, half * 4 * 288:(half + 1) * 4 * 288])
                w1v = wst[:, 0:4 * 288].re("p (k c) -> p k c", c=288)
                for k4 in range(4):
                    kc = half * 4 + k4
                    for (lo, hi, mui) in ((0, 64, PV_MUW), (64, 128, PV_MUA), (128, 288, PV_MUG)):
                        vec("tensor_scalar", out=l1b[:, kc, lo:hi], in0=w1v[:, k4, lo:hi], scalar1=pv(mui, kc),
                            scalar2=None, op0=ALU.mult)
                        vec("tensor_tensor", out=l1a[:, kc, lo:hi], in0=w1v[:, k4, lo:hi], in1=l1b[:, kc, lo:hi],
                            op=ALU.subtract)
            for half in range(2):
                S.dma("sync", out=wst[:, 0:1536], in_=l2_d[:, half * 1536:(half + 1) * 1536])
                vec("tensor_copy", out=l2[:].re("p a c -> p (a c)")[:, half * 1536:(half + 1) * 1536], in_=wst[:, 0:1536])
            CK('lora0')
            for j in range(24):
                S.dma("sync", out=wst[:, 0:2048], in_=wmod_d[j])
                wv = wst[:, 0:2048].re("p (k c) -> p k c", c=256)
                for cc in range(2):
                    col = j * 2 + cc
                    for kc in range(8):
                        mm(out=B1[:, col:col + 1], lhsT=wv[:, kc, cc * 128:(cc + 1) * 128], rhs=pv(PV_C, kc),
                           start=(kc == 0), stop=(kc == 7))
            modf = S.sb("modf", [128, 48], F32)
            vec("tensor_tensor", out=modf[:, :], in0=B1[:, 0:48], in1=pf[:, 0:48], op=ALU.add)
            for kc in range(8):
                vec("scalar_tensor_tensor", out=pdv(PD_A1, kc), in0=modf[:, 8 + kc:9 + kc], scalar=1.0,
                    in1=pv(PV_GPM, kc), op0=ALU.add, op1=ALU.mult)
                vec("scalar_tensor_tensor", out=pdv(PD_A2, kc), in0=modf[:, 32 + kc:33 + kc], scalar=1.0,
                    in1=pv(PV_GPF, kc), op0=ALU.add, op1=ALU.mult)
                vec("tensor_tensor", out=pdv(PD_GM, kc), in0=modf[:, 16 + kc:17 + kc], in1=pv(PV_GQM, kc), op=ALU.mult)
                vec("tensor_tensor", out=pdv(PD_GF, kc), in0=modf[:, 40 + kc:41 + kc], in1=pv(PV_GQF, kc), op=ALU.mult)
                vec("tensor_scalar", out=pdv(PD_OMK, kc), in0=pv(PV_KA, kc), scalar1=-1.0, scalar2=1.0,
                    op0=ALU.mult, op1=ALU.add)
                vec("tensor_scalar", out=pdv(5, kc), in0=pv(PV_W0, kc), scalar1=-1.0, scalar2=None, op0=ALU.mult)
                vec("tensor_scalar", out=pdv(6, kc), in0=pv(PV_A0, kc), scalar1=-1.0, scalar2=None, op0=ALU.mult)
            dg = tp[0][:, 0:128]
            onesf = tp[1][:, 0:128]
            gps("memset", ap=onesf[:, :], constant=1.0)
            for (pdi, dst) in ((PD_GM, gmb), (PD_GF, gfb)):
                for kc in range(8):
                    vec("tensor_scalar", out=dg[:, :], in0=identf, scalar1=pdv(pdi, kc), scalar2=None, op0=ALU.mult)
                    pz = nextp()
                    mm(out=pz[:, 0:128], lhsT=onesf[:, :], rhs=dg[:, :], start=True, stop=True)
                    act(out=dst[:, kc * 128:(kc + 1) * 128], in_=pz[:, 0:128], func=AF.Copy)

            CK('startup')
            ring_i = [0]

            ring_sets = {"r": ring[0:2], "a": ring[2:4], "x": ring}
            ring_k = {"r": 0, "a": 0, "x": 0}

            def wload(ch, k="x"):
                s = ring_sets[k][ring_k[k] % len(ring_sets[k])]
                ring_k[k] += 1
                S.dma("sync", out=s[:, :], in_=wscr[ch][:, :])
                return s

            def rms_rstd(src3, dst_cols, nsub=NSUB):
                for sub in range(nsub):
                    act(out=junk[:, :], in_=src3[:, sub, :], func=AF.Square,
                        accum_out=st4[:, 8 + sub:9 + sub])
                rsqrt_to(st4[:, dst_cols:dst_cols + nsub], st4[:, 8:8 + nsub], eps_r, 1.0 / D)

            def norm_transpose(xsrc, rcol, dstT, a_idx, b_view_fn, halo):
                for sub in range(NSUB):
                    vec("tensor_scalar", out=nb[:, sub, :], in0=xsrc[:, sub, :], scalar1=st4[:, rcol + sub:rcol + sub + 1],
                        scalar2=None, op0=ALU.mult)
                for kc in range(8):
                    tgt = trot[kc % 3]
                    for sub in range(NSUB):
                        tr(out=tgt[:, sub * 128:(sub + 1) * 128], in_=nb[:, sub, kc * 128:(kc + 1) * 128], identity=ident)
                    act(out=dstT[:, kc, halo:halo + TT], in_=tgt[:, 0:TT], func=AF.Identity,
                        scale=pdv(a_idx, kc), bias=b_view_fn(kc))

            def norm_transpose_g(xsrc, rcol, dstT, a_idx, b_view_fn, halo):
                for sub in range(NSUB):
                    vec("tensor_scalar", out=nb[:, sub, :], in0=xsrc[:, sub, :], scalar1=st4[:, rcol + sub:rcol + sub + 1],
                        scalar2=None, op0=ALU.mult)
                    yield
                for kc in range(8):
                    tgt = trot[kc % 3]
                    for sub in range(NSUB):
                        tr(out=tgt[:, sub * 128:(sub + 1) * 128], in_=nb[:, sub, kc * 128:(kc + 1) * 128], identity=ident)
                    act(out=dstT[:, kc, halo:halo + TT], in_=tgt[:, 0:TT], func=AF.Identity,
                        scale=pdv(a_idx, kc), bias=b_view_fn(kc))
                    yield

            s1_done = set()

            for m in range(nt):
                phaseB = m >= PB0
                prot["r"] = [_b0] if m >= PB0 - 8 else [_b0, _b4, B5, B6]
                xm = xt[m % 2]
                vcur = cf[:, CF_VM + m:CF_VM + m + 1]
                vprev = cf[:, CF_VM + 32 + m:CF_VM + 33 + m]
                vr2 = cf[:, CF_VM + 64 + m:CF_VM + 65 + m]
                vr3 = cf[:, CF_VM + 96 + m:CF_VM + 97 + m]
                def stage1_gen(tix):
                    S.dma("gpsimd", out=xm[:], in_=xv.v(xv.t[tix * TT:(tix + 1) * TT, :].rearrange("(s p) c -> p s c", p=128)))
                    yield
                    if tix > 0:
                        vec("tensor_scalar", out=hT[:, :, 0:1], in0=hT[:, :, TT:TT + 1],
                            scalar1=cf[:, CF_VM + tix - 1:CF_VM + tix], scalar2=None, op0=ALU.mult)
                        yield
                    rms_rstd(xm, 0)
                    yield
                    yield from norm_transpose_g(xm, 0, hT, PD_A1, lambda kc: pf[:, PV_SHM * 8 + kc:PV_SHM * 8 + kc + 1]
                                   if False else modf[:, kc:kc + 1], 1)

                    pz = nextp("s")
                    for kc in range(8):
                        mm(out=pz[:, 0:TT], lhsT=l1a[:, kc, 0:128], rhs=hT[:, kc, 1:TT + 1], start=(kc == 0), stop=False)
                        mm(out=pz[:, 0:TT], lhsT=l1b[:, kc, 0:128], rhs=hT[:, kc, 0:TT], start=False, stop=(kc == 7))
                    sigmoid_to(tp[11][0:64, 0:TT], pz[0:64, 0:TT], None, 2.0)
                    yield
                    vec("tensor_scalar", out=lw[0:64, :], in0=tp[11][0:64, 0:TT], scalar1=2.0, scalar2=-1.0, op0=ALU.mult, op1=ALU.add)
                    yield
                    act(out=lw[64:128, :], in_=pz[64:128, 0:TT], func=AF.Copy)
                    yield
                    if tix >= PB0:
                        pz = nextp("s")
                        for kc in range(8):
                            mm(out=pz[:, 0:TT], lhsT=l1a[:, kc, 128:256], rhs=hT[:, kc, 1:TT + 1], start=(kc == 0), stop=False)
                            mm(out=pz[:, 0:TT], lhsT=l1b[:, kc, 128:256], rhs=hT[:, kc, 0:TT], start=False, stop=(kc == 7))
                        sigmoid_to(tp[11][:, 0:TT], pz[:, 0:TT])
                        yield
                        act(out=lga[:, :], in_=tp[11][:, 0:TT], func=AF.Copy)
                        yield
                        pz = nextp("s")
                        for kc in range(8):
                            mm(out=pz[0:32, 0:TT], lhsT=l1a[:, kc, 256:288], rhs=hT[:, kc, 1:TT + 1], start=(kc == 0), stop=False)
                            mm(out=pz[0:32, 0:TT], lhsT=l1b[:, kc, 256:288], rhs=hT[:, kc, 0:TT], start=False, stop=(kc == 7))
                        sigmoid_to(tp[11][0:32, 0:TT], pz[0:32, 0:TT])
                        yield
                        act(out=lgb[0:32, :], in_=tp[11][0:32, 0:TT], func=AF.Copy)
                        yield


                if m not in s1_done:
                    for _ in stage1_gen(m):
                        pass

                CK('lora1')
                def rw_f1(c0):
                    for c in (c0,):
                        AR = ARs[c % 3]; AM = AMs[c % 2]; Vtm = Vtms[c % 2]; Bpad = Bpads[c % 2]; Kpad = Kpads[c % 2]
                        TTf = TTfs[c % 2]; gb = gbs[c % 3]; pcs = pcss[c % 3]
                        Bt = Bts[c % 2]; Kt = Kts[c % 2]; vbf = vbfs[c % 2]
                        csl = slice(c * 128, (c + 1) * 128)
                        wr = wload(CH_RW + c, 'r')
                        wrv = wr[:, :].re("p (k c) -> p k c", c=512)
                        for j in ((0, 1, 2) if m >= PB0 - 1 else (1, 2)):
                            pz = nextp("r")
                            for kc in range(8):
                                mm(out=pz[:, 0:TT], lhsT=wrv[:, kc, j * 128:(j + 1) * 128], rhs=hT[:, kc, 1:TT + 1],
                                   start=(kc == 0), stop=(kc == 7))
                            vec("tensor_copy", out=pj[j][:, 0:1], in_=hal[:, c, j:j + 1])
                            yield
                            act(out=pj[j][:, 1:TT + 1], in_=pz[:, 0:TT], func=AF.Copy)
                            yield
                            vec("tensor_scalar", out=hal[:, c, j:j + 1], in0=pj[j][:, TT:TT + 1], scalar1=vcur,
                                scalar2=None, op0=ALU.mult)
                            yield
                            vec("tensor_tensor", out=tmpd[:, :], in0=pj[j][:, 0:TT], in1=pj[j][:, 1:TT + 1], op=ALU.subtract)
                            yield
                            vec("scalar_tensor_tensor", out=rkv[j][:, :], in0=tmpd[:, :], scalar=pv(PV_MUR + j, c),
                                in1=pj[j][:, 1:TT + 1], op0=ALU.mult, op1=ALU.add)
                            yield
                        r_, k_, v_ = rkv
                        vec("tensor_scalar", out=v_[:, :], in0=v_[:, :], scalar1=vcur, scalar2=None, op0=ALU.mult)
                        yield
                        act(out=vbf[:, :], in_=v_[:, :], func=AF.Copy)
                        yield
                        pz = nextp("r")
                        mm(out=pz[:, 0:TT], lhsT=l2[0:64, 0, csl], rhs=lw[0:64, :], start=True, stop=True)
                        sigmoid_to(sw[:, :], pz[:, 0:TT], pdv(5, c))
                        yield
                        pz = nextp("r")
                        mm(out=pz[:, 0:TT], lhsT=l2[64:128, 0, csl], rhs=lw[64:128, :], start=True, stop=True)
                        sigmoid_to(asig[:, :], pz[:, 0:TT], pdv(6, c))
                        yield
                        pz = nextp("r")
                        if phaseB:
                            mm(out=pz[:, 0:TT], lhsT=l2[:, 1, csl], rhs=lga[:, :], start=True, stop=False)
                            mm(out=pz[:, 0:TT], lhsT=l2[0:32, 2, csl], rhs=lgb[0:32, :], start=False, stop=True)
                            act(out=gb[:, :], in_=pz[:, 0:TT], func=AF.Copy)
                        yield
                        vec("tensor_tensor_scan", out=cs[:, :], data0=cf[:, CF_MSK:CF_MSK + TT], data1=sw[:, :],
                            initial=0.0, op0=ALU.mult, op1=ALU.add)
                        yield
                        vec("tensor_tensor", out=cm[:, :], in0=cs[:, :], in1=sw[:, :], op=ALU.subtract)
                        yield
                        act(out=Ep[:, :], in_=cs[:, :], func=AF.Exp, scale=-C0)
                        yield
                        act(out=En[:, :], in_=cs[:, :], func=AF.Exp, scale=C0)
                        yield
                        act(out=Em[:, :], in_=cm[:, :], func=AF.Exp, scale=-C0)
                        yield
                        act(out=sqb[:, :], in_=k_[:, :], func=AF.Square, scale=pv(PV_KK, c))
                        yield
                        pz = nextp("r")
                        mm(out=pz[:, 0:TT], lhsT=onesbd, rhs=sqb[:, :], start=True, stop=True)
                        vec("tensor_scalar", out=rinv[:, :], in0=pz[:, 0:TT], scalar1=1e-18, scalar2=None, op0=ALU.max)
                        yield
                        act(out=rinv[:, :], in_=rinv[:, :], func=AF.Ln)
                        yield
                        act(out=rinv[:, :], in_=rinv[:, :], func=AF.Exp, scale=-0.5)
                        yield
                        vec("scalar_tensor_tensor", out=kkb[:, :], in0=k_[:, :], scalar=pv(PV_KK, c), in1=rinv[:, :],
                            op0=ALU.mult, op1=ALU.mult)
                        yield
                        ev = gps if GPS_OFF else vec
                        ev("tensor_scalar", out=ff[:, :], in0=asig[:, :], scalar1=pv(PV_KA, c), scalar2=pdv(PD_OMK, c),
                            op0=ALU.mult, op1=ALU.add)
                        yield
                        ev("tensor_tensor", out=kmod[:, :], in0=k_[:, :], in1=ff[:, :], op=ALU.mult)
                        yield
                        ev("tensor_tensor", out=bv[:, :], in0=kkb[:, :], in1=asig[:, :], op=ALU.mult)
                        yield
                        vec("scalar_tensor_tensor", out=AR[:, :, 0, :], in0=kkb[:, :].re("p (s t) -> p s t", t=128), scalar=-1.0,
                            in1=Em[:, :].re("p (s t) -> p s t", t=128), op0=ALU.mult, op1=ALU.mult)
                        yield
                        if phaseB:
                            vec("tensor_tensor", out=AR[:, :, 1, :], in0=r_[:, :].re("p (s t) -> p s t", t=128),
                                in1=Ep[:, :].re("p (s t) -> p s t", t=128), op=ALU.mult)
                            yield
                        ev("tensor_tensor", out=Bt[:, :], in0=bv[:, :], in1=En[:, :], op=ALU.mult)
                        yield
                        ev("tensor_tensor", out=Kt[:, :], in0=kmod[:, :], in1=En[:, :], op=ALU.mult)
                        yield
                        if phaseB:
                            vec("tensor_tensor", out=tmpd[:, :], in0=r_[:, :], in1=kmod[:, :], op=ALU.mult)
                            yield
                            act(out=sqb[:, :], in_=tmpd[:, :], func=AF.Copy, scale=pv(PV_RK, c))
                            yield
                            pz = nextp("r")
                            mm(out=pz[:, 0:TT], lhsT=onesbd, rhs=sqb[:, :], start=True, stop=True)
                            vec("tensor_tensor", out=bon[:, :], in0=pz[:, 0:TT], in1=v_[:, :], op=ALU.mult)
                            yield
                            vec("tensor_tensor", out=yfin[:, c, :], in0=bon[:, :], in1=gb[:, :], op=ALU.mult)
                        yield
                        vec("tensor_copy", out=pcs[:, 0:4], in_=Ep[:, :].re("p (q t) -> p q t", t=64)[:, :, 63])
                        yield

                def rw_f2(c0):
                    for c in (c0,):
                        AR = ARs[c % 3]; AM = AMs[c % 2]; Vtm = Vtms[c % 2]; Bpad = Bpads[c % 2]; Kpad = Kpads[c % 2]
                        TTf = TTfs[c % 2]; gb = gbs[c % 3]; pcs = pcss[c % 3]
                        Bt = Bts[c % 2]; Kt = Kts[c % 2]; vbf = vbfs[c % 2]
                        for qi, (src, dst) in enumerate(((Bt, Bpad), (Kt, Kpad), (vbf, None))):
                            for sub in range(NSUB):
                                tr(out=trB[qi % 2][:, sub * 128:(sub + 1) * 128],
                                   in_=src[:, sub * 128:(sub + 1) * 128], identity=ident)
                            yield
                            srcv = trB[qi % 2][:, 0:256]
                            if dst is None:
                                act(out=Vtm[:].re("p s c -> p (s c)"), in_=srcv, func=AF.Copy)
                                yield
                            else:
                                for h in range(2):
                                    act(out=dst[:, :, h, h * 64:(h + 1) * 64],
                                        in_=srcv.re("p (s c) -> p s c", c=128)[:, :, h * 64:(h + 1) * 64], func=AF.Copy)
                                    yield
                        CK('rwkv_a')
                        for h in range(2):
                            hs = slice(h * 64, (h + 1) * 64)
                            for sub in range(NSUB):
                                u = h * NSUB + sub
                                tsl = slice(sub * 128, (sub + 1) * 128)
                                pz = (B1, B2)[u % 2]
                                if phaseB:
                                    mm(out=pz[:, 0:256], lhsT=Bt[hs, tsl], rhs=AR[hs, sub, :, :].re("p a t -> p (a t)"),
                                       start=True, stop=True)
                                    mm(out=pz[:, 256:512], lhsT=Kt[hs, tsl], rhs=AR[hs, sub, :, :].re("p a t -> p (a t)"),
                                       start=True, stop=True)
                                    vec("tensor_tensor", out=AM[:, u, :], in0=pz[:, :], in1=cb[:, CB_MT4:CB_MT4 + 512], op=ALU.mult)
                                    yield
                                else:
                                    mm(out=pz[:, 0:128], lhsT=Bt[hs, tsl], rhs=AR[hs, sub, 0, :], start=True, stop=True)
                                    mm(out=pz[:, 256:384], lhsT=Kt[hs, tsl], rhs=AR[hs, sub, 0, :], start=True, stop=True)
                                    v4 = lambda ap_: ap_.re("p (a two b) -> p a two b", a=2, two=2)[:, :, 0, :]
                                    vec("tensor_tensor", out=v4(AM[:, u, :]), in0=v4(pz[:, :]),
                                        in1=v4(cb[:, CB_MT4:CB_MT4 + 512]), op=ALU.mult)
                                yield
                        for h in range(2):
                            hs = slice(h * 64, (h + 1) * 64)
                            for sub in range(NSUB):
                                u = h * NSUB + sub
                                tsl = slice(sub * 128, (sub + 1) * 128)
                                mm(out=B1[:, u * 128:(u + 1) * 128], lhsT=AR[hs, sub, 0, :], rhs=Bt[hs, tsl],
                                   start=True, stop=True)
                        vec("tensor_tensor", out=L0[:].re("p u t -> p (u t)"), in0=B1[:, :], in1=cb[:, CB_ML4:CB_ML4 + 512],
                            op=ALU.mult)
                        yield
                        CK('rwkv_b')
                        vec("tensor_tensor", out=SS[0][:], in0=AM[:, :, 0:128], in1=ident.bc(1, [128, 4, 128]), op=ALU.add)
                        yield
                        lt_prev = lambda u: AM[:, u, 0:128]
                        lp_prev = lambda u: L0[:, u, :]
                        scur = 0
                        for lev in range(1, 6):
                            lpn = LP[lev % 2]
                            ltn = LT[lev % 2]
                            for u in range(4):
                                mm(out=B1[:, u * 128:(u + 1) * 128], lhsT=lt_prev(u), rhs=lp_prev(u), start=True, stop=True)
                            if lev <= 4:
                                for u in range(4):
                                    mm(out=B2[:, u * 128:(u + 1) * 128], lhsT=lp_prev(u), rhs=lt_prev(u),
                                       start=True, stop=True)
                            act(out=lpn[:].re("p u t -> p (u t)"), in_=B1[:, :], func=AF.Copy)
                            yield
                            if lev <= 4:
                                act(out=ltn[:].re("p u t -> p (u t)"), in_=B2[:, :], func=AF.Copy)
                            yield
                            for u in range(4):
                                mm(out=B1[:, u * 128:(u + 1) * 128], lhsT=lpn[:, u, :], rhs=SS[scur][:, u, :], start=True, stop=True)
                            sdst = TTf if lev == 5 else SS[1 - scur]
                            vec("tensor_tensor", out=sdst[:].re("p u t -> p (u t)"), in0=B1[:, :],
                                in1=SS[scur][:].re("p u t -> p (u t)"), op=ALU.add)
                            yield
                            scur = 1 - scur
                            lt_prev = (lambda b: (lambda u: b[:, u, :]))(ltn)
                            yield
                            lp_prev = (lambda b: (lambda u: b[:, u, :]))(lpn)

                def rw_back(c0):
                    for c in (c0,):
                        AR = ARs[c % 3]; AM = AMs[c % 2]; Vtm = Vtms[c % 2]; Bpad = Bpads[c % 2]; Kpad = Kpads[c % 2]
                        TTf = TTfs[c % 2]; gb = gbs[c % 3]; pcs = pcss[c % 3]
                        Bt = Bts[c % 2]; Kt = Kts[c % 2]; vbf = vbfs[c % 2]
                        TTm = TTf
                        ysq = ysqB[:, :]
                        yln = ysqB[:, :]
                        CK('rwkv_c')
                        for q in range(2 * NSUB):
                            sub, half = q // 2, q % 2
                            ps_ = slice(half * 64, half * 64 + 64)
                            tsl = slice(sub * 128, (sub + 1) * 128)
                            zi = q % 2
                            for h in range(2):
                                hs = slice(h * 64, (h + 1) * 64)
                                u = h * NSUB + sub
                                mm(out=pC[:, hs], lhsT=AR[hs, sub, 0, :], rhs=Zb[hs, c, zi, :], start=True, stop=False)
                                mm(out=pC[:, hs], lhsT=AM[:, u, 256:384], rhs=Vtm[:, sub, hs], start=False, stop=True)
                            act(out=Xb[ps_, :], in_=pC[ps_, 0:128], func=AF.Copy)
                            yield
                            CK('c1')
                            for h in range(2):
                                hs = slice(h * 64, (h + 1) * 64)
                                u = h * NSUB + sub
                                mm(out=pC[:, 128 + h * 64:128 + (h + 1) * 64], lhsT=TTm[ps_, u, :], rhs=Xb[ps_, hs],
                                   start=True, stop=True)
                            vec("tensor_copy", out=Ub[ps_, :], in_=pC[ps_, 128:256])
                            yield
                            CK('c2')
                            if phaseB:
                                for h in range(2):
                                    hs = slice(h * 64, (h + 1) * 64)
                                    u = h * NSUB + sub
                                    o_ = slice(256 + h * 64, 256 + (h + 1) * 64)
                                    mm(out=pC[:, o_], lhsT=AR[hs, sub, 1, :], rhs=Zb[hs, c, zi, :], start=True, stop=False)
                                    mm(out=pC[:, o_], lhsT=AM[:, u, 128:256], rhs=Ub[:, hs], start=False, stop=False)
                                    mm(out=pC[:, o_], lhsT=AM[:, u, 384:512], rhs=Vtm[:, sub, hs], start=False, stop=True)
                                act(out=Ytm[ps_, sub, :], in_=pC[ps_, 256:384], func=AF.Copy)
                                yield
                            CK('c3')
                            for h in range(2):
                                hs = slice(h * 64, (h + 1) * 64)
                                mm(out=pC[:, 384:448], lhsT=Bpad[ps_, sub, h, :], rhs=Ub[ps_, hs], start=(h == 0), stop=False)
                                mm(out=pC[:, 384:448], lhsT=Kpad[ps_, sub, h, :], rhs=Vtm[ps_, sub, hs], start=False, stop=(h == 1))
                            CK('c4')
                            pcv = pcs[:, q:q + 1]
                            vec("tensor_scalar", out=ztmp[:, :], in0=Zf[:, c, :], scalar1=pcv, scalar2=None, op0=ALU.mult)
                            yield
                            vec("scalar_tensor_tensor", out=Zf[:, c, :], in0=pC[:, 384:448], scalar=pcv, in1=ztmp[:, :],
                                op0=ALU.mult, op1=ALU.add)
                            yield
                            act(out=Zb[:, c, 1 - zi, :], in_=Zf[:, c, :], func=AF.Copy)
                            yield
                        CK('rwkv_d')
                        if phaseB:
                            yv = Ytm[:].re("p s (h i) -> p (s h) i", i=64)
                            vec("tensor_reduce", out=gst[:, 0:4], in_=yv, axis=AX.X, op=ALU.add)
                            yield
                            act(out=ysq[:, :], in_=Ytm[:].re("p s c -> p (s c)"), func=AF.Square)
                            yield
                            vec("tensor_reduce", out=gst[:, 4:8], in_=ysq[:, :].re("p (g i) -> p g i", i=64), axis=AX.X, op=ALU.add)
                            yield
                            vec("tensor_scalar", out=gst[:, 8:12], in0=gst[:, 0:4], scalar1=1.0 / 64, scalar2=None, op0=ALU.mult)
                            yield
                            vec("tensor_tensor", out=gst[:, 12:16], in0=gst[:, 8:12], in1=gst[:, 8:12], op=ALU.mult)
                            yield
                            vec("scalar_tensor_tensor", out=gst[:, 16:20], in0=gst[:, 4:8], scalar=1.0 / 64, in1=gst[:, 12:16],
                                op0=ALU.mult, op1=ALU.subtract)
                            yield
                            rsqrt_to(gst[:, 24:28], gst[:, 16:20], eps_g, 1.0)
                            yield
                            ysv = ysq[:, :].re("p (g i) -> p g i", i=64)
                            vec("tensor_tensor", out=ysv, in0=yv, in1=gst[:, 8:12].bc(2, [128, 4, 64]), op=ALU.subtract)
                            yield
                            vec("tensor_tensor", out=ynb[:].re("p s (h i) -> p (s h) i", i=64), in0=ysv,
                                in1=gst[:, 24:28].bc(2, [128, 4, 64]), op=ALU.mult)
                            yield
                            for sub in range(NSUB):
                                tr(out=trC[:, sub * 128:(sub + 1) * 128], in_=ynb[:, sub, :], identity=ident)
                            act(out=yln[:, :], in_=trC[:, 0:TT], func=AF.Identity, scale=pv(PV_LNW, c), bias=pv(PV_LNB, c))
                            yield
                            vec("tensor_tensor", out=yln[:, :], in0=yln[:, :], in1=gb[:, :], op=ALU.mult)
                            yield
                            vec("tensor_tensor", out=yfin[:, c, :], in0=yln[:, :], in1=yfin[:, c, :], op=ALU.add)
                            yield


                def th_attn():
                    if m < PB0 - 8:
                        return
                    j0_2 = m % 2
                    j0_3 = m % 8
                    for g in range(3):
                        kdst = (K1, K2c, K3c)[g]
                        for j in ((0, 1, 2) if phaseB else (1, 2)):
                            wa = wload(CH_AT + g * 3 + j, 'a')
                            wav = wa[:, :].re("p (k c) -> p k c", c=512)
                            for cc in range(4):
                                pz = nextp("a")
                                for kc in range(8):
                                    mm(out=pz[:, 0:TT], lhsT=wav[:, kc, cc * 128:(cc + 1) * 128], rhs=hT[:, kc, 1:TT + 1],
                                       start=(kc == 0), stop=(kc == 7))
                                if j == 0:
                                    act(out=Qa[g][:, cc, :], in_=pz[:, 0:TT], func=AF.Copy, scale=0.125)
                                    yield
                                elif j == 1:
                                    if g == 0:
                                        act(out=K1[:, cc, 128:128 + TT], in_=pz[:, 0:TT], func=AF.Copy)
                                        yield
                                    else:
                                        act(out=kdst[:, cc, :], in_=pz[:, 0:TT], func=AF.Copy)
                                        yield
                                else:
                                    act(out=VF[:, cc, :], in_=pz[:, 0:TT], func=AF.Copy)
                                    yield
                            if not DENSE_ATTN_PROJ:
                                yield
                        if g == 0:
                            for blk in range(2):
                                for cc in range(4):
                                    tr(out=pTa[:, cc * 128:(cc + 1) * 128], in_=VF[:, cc, blk * 128:(blk + 1) * 128], identity=ident)
                                vec("tensor_copy", out=V1[:, 1 + blk, :], in_=pTa[:, 0:512])
                            yield
                        elif g == 1:
                            vec("tensor_copy", out=VF2[:], in_=VF[:])
                            yield
                    CK('attn_proj')
                    for h in range(8):
                        cc, hp = h // 2, (h % 2) * 64
                        hs = slice(hp, hp + 64)
                        vs = slice(h * 64, (h + 1) * 64)
                        vl = slice(hp, hp + 64)
                        if h % 2 == 0:
                            for r in range(4):
                                tr(out=pTa[0:64, r * 128:(r + 1) * 128],
                                   in_=VF2[:, cc, :].re("p (i r) -> p r i", r=4)[:, r, :], identity=ident)
                            vec("tensor_copy", out=V2c[0:64, :, :].re("p r c -> p (r c)"), in_=pTa[0:64, 0:512])
                            yield
                            for r in range(16):
                                tr(out=pTa[0:16, (r % 4) * 128:(r % 4 + 1) * 128],
                                   in_=VF[:, cc, :].re("p (i r) -> p r i", r=16)[:, r, :], identity=ident)
                                if r % 4 == 3:
                                    vec("tensor_copy", out=V3c[0:16, r - 3:r + 1, :].re("p a c -> p (a c)"), in_=pTa[0:16, 0:512])
                                yield
                        if not phaseB:
                            if h % 2 == 1:
                                S.dma("gpsimd", out=V2r[j0_2 * 64:(j0_2 + 1) * 64, :, cc * 128:(cc + 1) * 128], in_=V2c[0:64, :, :])
                                S.dma("gpsimd", out=V3r[j0_3 * 16:(j0_3 + 1) * 16, :, cc * 128:(cc + 1) * 128], in_=V3c[0:16, :, :])
                            continue
                        for blk in range(2):
                            qv = Qa[0][hs, cc, blk * 128:(blk + 1) * 128]
                            mm(out=B5[:, (blk * 2) * 128:(blk * 2 + 1) * 128], lhsT=K1[hs, cc, blk * 128:(blk + 1) * 128],
                               rhs=qv, start=True, stop=True)
                            mm(out=B5[:, (blk * 2 + 1) * 128:(blk * 2 + 2) * 128],
                               lhsT=K1[hs, cc, 128 + blk * 128:128 + (blk + 1) * 128], rhs=qv, start=True, stop=True)
                        act(out=pe[:, :], in_=B5[:, 0:512], func=AF.Exp)
                        yield
                        vec("tensor_tensor", out=pp_[:, :].re("p (b e) -> p b e", b=2), in0=pe[:, :].re("p (b e) -> p b e", b=2),
                            in1=cb[:, CB_E1 + h * 256:CB_E1 + (h + 1) * 256].bc(1, [128, 2, 256]), op=ALU.mult)
                        yield
                        vec("tensor_scalar", out=pp_[:, 0:128], in0=pp_[:, 0:128], scalar1=vprev, scalar2=None, op0=ALU.mult)
                        yield
                        for blk in range(2):
                            mm(out=B6[0:64, blk * 128:(blk + 1) * 128], lhsT=V1[:, blk, vs],
                               rhs=pp_[:, (blk * 2) * 128:(blk * 2 + 1) * 128], start=True, stop=False)
                            mm(out=B6[0:64, blk * 128:(blk + 1) * 128], lhsT=V1[:, blk + 1, vs],
                               rhs=pp_[:, (blk * 2 + 1) * 128:(blk * 2 + 2) * 128], start=False, stop=True)
                        ppv = pp_[:, :].re("p (b c q) -> p b c q", b=2, c=2)
                        mm(out=B6[0:64, 256:512], lhsT=cb[:, CB_ONES:CB_ONES + 64], rhs=ppv[:, :, 0, :], start=True, stop=False)
                        mm(out=B6[0:64, 256:512], lhsT=cb[:, CB_ONES:CB_ONES + 64], rhs=ppv[:, :, 1, :], start=False, stop=True)
                        act(out=accO[:, :], in_=B6[0:64, 0:TT], func=AF.Copy)
                        yield
                        act(out=accD[:, :], in_=B6[0:64, 256:512], func=AF.Copy)
                        yield
                        for r in range(4):
                            qv = Qa[1][hs, cc, :].re("p (i r) -> p r i", r=4)[:, r, :]
                            mm(out=B5[:, r * 64:(r + 1) * 64], lhsT=K2r[hs, cc, r, :], rhs=qv, start=True, stop=True)
                            mm(out=B5[0:64, 256 + r * 64:256 + (r + 1) * 64],
                               lhsT=K2c[hs, cc, :].re("p (i r) -> p r i", r=4)[:, r, :], rhs=qv, start=True, stop=True)
                        act(out=pe[:, 0:256], in_=B5[:, 0:256], func=AF.Exp)
                        yield
                        act(out=peb[0:64, :], in_=B5[0:64, 256:512], func=AF.Exp)
                        yield
                        ea = cb[:, CB_EA2 + (j0_2 * 8 + h) * 64:CB_EA2 + (j0_2 * 8 + h + 1) * 64]
                        vec("scalar_tensor_tensor", out=pp_[:, 0:256].re("p (r i) -> p r i", r=4),
                            in0=pe[:, 0:256].re("p (r i) -> p r i", r=4), scalar=vr2, in1=ea.bc(1, [128, 4, 64]),
                            op0=ALU.mult, op1=ALU.mult)
                        yield
                        eb = cb[0:64, CB_EB2 + h * 64:CB_EB2 + (h + 1) * 64]
                        vec("tensor_tensor", out=ppb[0:64, :].re("p (r i) -> p r i", r=4),
                            in0=peb[0:64, :].re("p (r i) -> p r i", r=4), in1=eb.bc(1, [64, 4, 64]), op=ALU.mult)
                        yield
                        for r in range(4):
                            mm(out=B6[0:64, r * 64:(r + 1) * 64], lhsT=V2r[:, r, vs], rhs=pp_[:, r * 64:(r + 1) * 64],
                               start=True, stop=False)
                            mm(out=B6[0:64, r * 64:(r + 1) * 64], lhsT=V2c[0:64, r, vl], rhs=ppb[0:64, r * 64:(r + 1) * 64],
                               start=False, stop=True)
                        mm(out=B6[0:64, 256:512], lhsT=cb[:, CB_ONES:CB_ONES + 64], rhs=pp_[:, 0:256], start=True, stop=False)
                        mm(out=B6[0:64, 256:512], lhsT=cb[0:64, CB_ONES:CB_ONES + 64], rhs=ppb[0:64, :], start=False, stop=True)
                        vec("tensor_tensor", out=accO[:, :].re("p (i r) -> p r i", r=4), in0=accO[:, :].re("p (i r) -> p r i", r=4),
                            in1=B6[0:64, 0:TT].re("p (r i) -> p r i", r=4), op=ALU.add)
                        yield
                        vec("tensor_tensor", out=accD[:, :].re("p (i r) -> p r i", r=4), in0=accD[:, :].re("p (i r) -> p r i", r=4),
                            in1=B6[0:64, 256:512].re("p (r i) -> p r i", r=4), op=ALU.add)
                        yield
                        for r in range(16):
                            qv = Qa[2][hs, cc, :].re("p (i r) -> p r i", r=16)[:, r, :]
                            mm(out=B5[:, r * 16:(r + 1) * 16], lhsT=K3r[hs, cc, r, :], rhs=qv, start=True, stop=True)
                            mm(out=B5[0:16, 256 + r * 16:256 + (r + 1) * 16],
                               lhsT=K3c[hs, cc, :].re("p (i r) -> p r i", r=16)[:, r, :], rhs=qv, start=True, stop=True)
                        act(out=pe[:, 0:256], in_=B5[:, 0:256], func=AF.Exp)
                        yield
                        act(out=peb[0:16, :], in_=B5[0:16, 256:512], func=AF.Exp)
                        yield
                        ea = cb[:, CB_EA3 + (j0_3 * 8 + h) * 16:CB_EA3 + (j0_3 * 8 + h + 1) * 16]
                        vec("scalar_tensor_tensor", out=pp_[:, 0:256].re("p (r i) -> p r i", r=16),
                            in0=pe[:, 0:256].re("p (r i) -> p r i", r=16), scalar=vr3, in1=ea.bc(1, [128, 16, 16]),
                            op0=ALU.mult, op1=ALU.mult)
                        yield
                        eb = cb[0:16, CB_EB3 + h * 16:CB_EB3 + (h + 1) * 16]
                        vec("tensor_tensor", out=ppb[0:16, :].re("p (r i) -> p r i", r=16),
                            in0=peb[0:16, :].re("p (r i) -> p r i", r=16), in1=eb.bc(1, [16, 16, 16]), op=ALU.mult)
                        yield
                        for r in range(16):
                            mm(out=B6[0:64, r * 16:(r + 1) * 16], lhsT=V3r[:, r, vs], rhs=pp_[:, r * 16:(r + 1) * 16],
                               start=True, stop=False)
                            mm(out=B6[0:64, r * 16:(r + 1) * 16], lhsT=V3c[0:16, r, vl], rhs=ppb[0:16, r * 16:(r + 1) * 16],
                               start=False, stop=True)
                        mm(out=B6[0:64, 256:512], lhsT=cb[:, CB_ONES:CB_ONES + 64], rhs=pp_[:, 0:256], start=True, stop=False)
                        mm(out=B6[0:64, 256:512], lhsT=cb[0:16, CB_ONES:CB_ONES + 64], rhs=ppb[0:16, :], start=False, stop=True)
                        yield
                        if phaseB:
                            vec("tensor_tensor", out=accO[:, :].re("p (i r) -> p r i", r=16),
                                in0=accO[:, :].re("p (i r) -> p r i", r=16),
                                in1=B6[0:64, 0:TT].re("p (r i) -> p r i", r=16), op=ALU.add)
                            yield
                            vec("tensor_tensor", out=accD[:, :].re("p (i r) -> p r i", r=16),
                                in0=accD[:, :].re("p (i r) -> p r i", r=16),
                                in1=B6[0:64, 256:512].re("p (r i) -> p r i", r=16), op=ALU.add)
                            yield
                            vec("reciprocal", out=accD[:, :], in_=accD[:, :])
                            yield
                            vec("tensor_tensor", out=oT[:, h, :], in0=accO[:, :], in1=accD[:, :], op=ALU.mult)
                            yield
                        if h % 2 == 1:
                            S.dma("gpsimd", out=V2r[j0_2 * 64:(j0_2 + 1) * 64, :, cc * 128:(cc + 1) * 128], in_=V2c[0:64, :, :])
                            S.dma("gpsimd", out=V3r[j0_3 * 16:(j0_3 + 1) * 16, :, cc * 128:(cc + 1) * 128], in_=V3c[0:16, :, :])
                    CK('attn')
                    vec("tensor_copy", out=K1[:, :, 0:128], in_=K1[:, :, TT:TT + 128])
                    yield
                    vec("tensor_copy", out=V1[:, 0, :], in_=V1[:, 2, :])
                    yield
                    vec("tensor_copy", out=K2r[:, :, :, j0_2 * 64:(j0_2 + 1) * 64],
                        in_=K2c[:].re("p c (i r) -> p c r i", r=4))
                    yield
                    vec("tensor_copy", out=K3r[:, :, :, j0_3 * 16:(j0_3 + 1) * 16],
                        in_=K3c[:].re("p c (i r) -> p c r i", r=16))

                ag = th_attn()
                ag_done = [False]

                def step_attn():
                    if ag_done[0]:
                        return
                    try:
                        next(ag)
                    except StopIteration:
                        ag_done[0] = True

                rr_cnt = [0]
                clk = {}

                def advance(g_):
                    S.cur_fin = 0.0
                    try:
                        next(g_)
                    except StopIteration:
                        return False
                    if S.cur_fin > 0.0:
                        clk[id(g_)] = S.cur_fin
                    return True

                def run_ls(gens):
                    gens = list(gens)
                    for g_ in gens:
                        clk.setdefault(id(g_), 0.0)
                    clk.setdefault(id(ag), 0.0)
                    while gens:
                        cands = gens + ([] if ag_done[0] else [ag])
                        g_ = min(cands, key=lambda x: clk[id(x)])
                        if g_ is ag:
                            step_attn_ls()
                        elif not advance(g_):
                            gens.remove(g_)

                def step_attn_ls():
                    if not advance(ag):
                        ag_done[0] = True

                def run_rr(gens):
                    if LISTSCHED and INTERLEAVE:
                        return run_ls(gens)
                    gens = list(gens)
                    while gens:
                        for g_ in list(gens):
                            try:
                                next(g_)
                            except StopIteration:
                                gens.remove(g_)
                        rr_cnt[0] += 1
                        if INTERLEAVE and rr_cnt[0] % ATTN_EVERY == 0:
                            step_attn()

                if LISTSCHED and INTERLEAVE:
                    done = {"f1": 0, "f2": 0, "bk": 0}
                    mk = {"f1": rw_f1, "f2": rw_f2, "bk": rw_back}
                    cur = {"f1": None, "f2": None, "bk": None}
                    sclk = {"f1": 0.0, "f2": 0.0, "bk": 0.0, "at": clk.get(id(ag), 0.0), "s1": 0.0}
                    s1g = [None, False]
                    want_s1 = S1_PREFETCH and (not phaseB) and (m + 1 < nt)

                    def can_start(k, c):
                        if k == "f1":
                            return done["f2"] >= c - 1 and done["bk"] >= c - 2
                        if k == "f2":
                            return done["f1"] >= c + 1 and done["bk"] >= c - 1
                        return done["f2"] >= c + 1

                    while True:
                        cands = []
                        for k in ("f1", "f2", "bk"):
                            if cur[k] is None and done[k] < 8 and can_start(k, done[k]):
                                cur[k] = mk[k](done[k])
                            if cur[k] is not None:
                                cands.append(k)
                        if not ag_done[0]:
                            cands.append("at")
                        if want_s1 and not s1g[1] and done["f1"] == 8 and ag_done[0]:
                            if s1g[0] is None:
                                prot["s"] = [_b0, _b4, B5, B6]
                                s1g[0] = stage1_gen(m + 1)
                                s1_done.add(m + 1)
                                sclk["s1"] = max(sclk["f1"], sclk["at"])
                            cands.append("s1")
                        if not cands:
                            break
                        k = min(cands, key=lambda x: sclk[x])
                        S.cur_fin = 0.0
                        if k == "at":
                            step_attn()
                        elif k == "s1":
                            try:
                                next(s1g[0])
                            except StopIteration:
                                s1g[1] = True
                                prot["s"] = [_b0, _b4, B1, B2, B3, B6]
                        else:
                            try:
                                next(cur[k])
                            except StopIteration:
                                cur[k] = None
                                done[k] += 1
                        if S.cur_fin > 0.0:
                            sclk[k] = S.cur_fin
                    assert done == {"f1": 8, "f2": 8, "bk": 8}, done
                else:
                    for k_ in range(10):
                        gens = []
                        if k_ < 8:
                            gens.append(rw_f1(k_))
                        if 1 <= k_ <= 8:
                            gens.append(rw_f2(k_ - 1))
                        if k_ >= 2:
                            gens.append(rw_back(k_ - 2))
                        if INTERLEAVE:
                            run_rr(gens)
                        else:
                            for g_ in reversed(gens):
                                for _ in g_:
                                    pass
                while not ag_done[0]:
                    step_attn()

                if not phaseB:
                    continue
                for cc in range(8):
                    sa, sbb = ((tv(2), tv(3)), (tv(5), tv(6)))[cc % 2]
                    wpb = wload(CH_PB + cc)
                    wv = wpb[:, :].re("p (a k c) -> p a k c", a=4, c=128)
                    pz = nextp()
                    for kc in range(8):
                        mm(out=pz[:, 0:TT], lhsT=wv[:, 0, kc, :], rhs=hT[:, kc, 1:TT + 1], start=(kc == 0), stop=(kc == 7))
                    sigmoid_to(sa[:, :], pz[:, 0:TT])
                    pz = nextp()
                    for kc in range(8):
                        mm(out=pz[:, 0:TT], lhsT=wv[:, 1, kc, :], rhs=hT[:, kc, 1:TT + 1], start=(kc == 0), stop=(kc == 7))
                    sigmoid_to(sbb[:, :], pz[:, 0:TT])
                    pz = nextp()
                    for kc in range(8):
                        mm(out=pz[:, 0:TT], lhsT=wv[:, 2, kc, :], rhs=yfin[:, kc, :], start=(kc == 0), stop=(kc == 7))
                    vec("tensor_tensor", out=sa[:, :], in0=sa[:, :], in1=pz[:, 0:TT], op=ALU.mult)
                    pz = nextp()
                    for hh_ in range(8):
                        mm(out=pz[:, 0:TT], lhsT=wv[0:64, 3, hh_, :], rhs=oT[:, hh_, :], start=(hh_ == 0), stop=(hh_ == 7))
                    vec("tensor_tensor", out=sbb[:, :], in0=sbb[:, :], in1=pz[:, 0:TT], op=ALU.mult)
                    vec("tensor_tensor", out=mixT[:, cc, :], in0=sa[:, :], in1=sbb[:, :], op=ALU.add)

                def norm_residual(ps_views, gb):
                    for hf in range(2):
                        act(out=junk[:, 0:512], in_=ps_views[hf], func=AF.Square, accum_out=st4[:, 8 + hf:9 + hf])
                    vec("tensor_tensor", out=st4[:, 10:11], in0=st4[:, 8:9], in1=st4[:, 9:10], op=ALU.add)
                    rsqrt_to(st4[:, 4:5], st4[:, 10:11], eps_r, 1.0 / D)
                    for hf in range(2):
                        for qq in range(2):
                            cs_ = slice(hf * 512 + qq * 256, hf * 512 + (qq + 1) * 256)
                            vec("scalar_tensor_tensor", out=utmp[:, :], in0=ps_views[hf][:, qq * 256:(qq + 1) * 256],
                                scalar=st4[:, 4:5], in1=gb[:, cs_], op0=ALU.mult, op1=ALU.mult)
                            vec("tensor_tensor", out=xm[:, sub, cs_], in0=xm[:, sub, cs_], in1=utmp[:, :], op=ALU.add)

                wo = [wload(CH_WOUT + 0), wload(CH_WOUT + 1)]
                for sub in range(NSUB):
                    for hf in range(2):
                        wv = wo[hf][:, :].re("p (k c) -> p k c", c=512)
                        for kc in range(8):
                            mm(out=(B1, B2)[hf][:, :], lhsT=mixT[:, kc, sub * 128:(sub + 1) * 128], rhs=wv[:, kc, :],
                               start=(kc == 0), stop=(kc == 7))
                    norm_residual([B1[:, :], B2[:, :]], gmb)
                rms_rstd(xm, 2)
                norm_transpose(xm, 2, h2T, PD_A2, lambda kc: modf[:, 24 + kc:25 + kc], 0)
                accs = [[B1[:, :], B2[:, :]], [B3[:, :], B5[:, :]]]
                for pg in range(3):
                    nk = 8 if pg < 2 else 6
                    for i4 in range(nk // 2):
                        i = pg * 4 + i4
                        wf_ = wload(CH_FF + i)
                        wv = wf_[:, :].re("p (k c) -> p k c", c=512)
                        for jj in range(2):
                            jl = i4 * 2 + jj
                            pg_ = nextp("f")
                            for kc in range(8):
                                mm(out=pg_[:, 0:TT], lhsT=wv[:, kc, jj * 128:(jj + 1) * 128], rhs=h2T[:, kc, :],
                                   start=(kc == 0), stop=(kc == 7))
                            act(out=sg[:, :], in_=pg_[:, 0:TT], func=AF.Silu)
                            pu = nextp("f")
                            for kc in range(8):
                                mm(out=pu[:, 0:TT], lhsT=wv[:, kc, 256 + jj * 128:256 + (jj + 1) * 128], rhs=h2T[:, kc, :],
                                   start=(kc == 0), stop=(kc == 7))
                            vec("tensor_tensor", out=actT[:, jl, :], in0=sg[:, :], in1=pu[:, 0:TT], op=ALU.mult)
                    for hf in range(2):
                        wf_ = wload(CH_FO + pg * 2 + hf)
                        wv = wf_[:, :].re("p (k c) -> p k c", c=512)
                        for sub in range(NSUB):
                            for kc in range(nk):
                                mm(out=accs[sub][hf], lhsT=actT[:, kc, sub * 128:(sub + 1) * 128], rhs=wv[:, kc, :],
                                   start=(pg == 0 and kc == 0), stop=(pg == 2 and kc == nk - 1))
                for sub in range(NSUB):
                    norm_residual(accs[sub], gfb)
                r0 = (m - PB0) * TT
                S.dma("gpsimd", out=y_d.v(y_d.t[r0:r0 + TT, :].rearrange("(s p) c -> p s c", p=128)), in_=xm[:])


        except _Stop:
            pass
        S.finish([y_d] + finals)
        S.emit()
    return nc


_CACHE = {}


def prep_inputs(x, c, w_mod, b_mod, g_pre_mix, g_post_mix, g_pre_ffn, g_post_ffn, w_in, mu_rkv, mu_lora,
           w0, w1, w2, a0, a1, a2, g1, g2, k_k, k_a, r_k, ln_x_w, ln_x_b, w_o_rwkv, w_o_attn, w_out,
           w_ffn_in, w_ffn_out):
    f = lambda a: np.asarray(a, np.float32)
    x = f(x); c = f(c)
    w_in = f(w_in)[0]; w_modm = f(w_mod)[0]
    bm = f(b_mod)[0].reshape(6, 1024)
    vecs = [bm[0], bm[1], bm[2], bm[3], bm[4], bm[5], f(g_pre_mix)[0], f(g_post_mix)[0], f(g_pre_ffn)[0],
            f(g_post_ffn)[0], f(mu_rkv)[0, 0], f(mu_rkv)[0, 1], f(mu_rkv)[0, 2], f(mu_lora)[0, 0], f(mu_lora)[0, 1],
            f(mu_lora)[0, 2], f(w0)[0], f(a0)[0], f(k_k)[0], f(k_a)[0], f(r_k)[0].reshape(-1), f(ln_x_w)[0],
            f(ln_x_b)[0]]
    wsrc = np.zeros((NCH, 128, 4096), np.float32)
    def put(i, arr3):
        P, K, C = arr3.shape
        v = wsrc[i].reshape(128, -1)
        tmp = np.zeros((128, K, 4096 // K if K in (8,) else C), np.float32) if False else None
        blk = np.zeros((128, K * C), np.float32)
        blk[:P] = arr3.reshape(P, K * C)
        v[:, :K * C] = blk
    for cch in range(8):
        a = np.zeros((128, 8, 512), np.float32)
        for j in range(3):
            a[:, :, j * 128:(j + 1) * 128] = _wchunk(w_in, slice(j * 1024 + cch * 128, j * 1024 + (cch + 1) * 128))
        put(CH_RW + cch, a)
    for g in range(3):
        for j in range(3):
            o = 3072 + j * 1536 + g * 512
            put(CH_AT + g * 3 + j, _wchunk(w_in, slice(o, o + 512)))
    wor = f(w_o_rwkv)[0]; woa = f(w_o_attn)[0]; wout = f(w_out)[0]
    for cc in range(8):
        cs_ = slice(cc * 128, (cc + 1) * 128)
        a = np.zeros((128, 4, 8, 128), np.float32)
        a[:, 0] = _wchunk(w_in, slice(7680 + cc * 128, 7680 + (cc + 1) * 128))
        a[:, 1] = _wchunk(w_in, slice(8704 + cc * 128, 8704 + (cc + 1) * 128))
        a[:, 2] = _wchunk(wor, cs_)
        a[0:64, 3] = woa[:, cs_].reshape(8, 64, 128).transpose(1, 0, 2)
        put(CH_PB + cc, a.reshape(128, 32, 128))
    for hf in range(2):
        put(CH_WOUT + hf, _wchunk(wout, slice(hf * 512, (hf + 1) * 512)))
    wfi = f(w_ffn_in)[0]; wfo = f(w_ffn_out)[0]
    for i in range(11):
        a = np.zeros((128, 8, 512), np.float32)
        a[:, :, 0:256] = _wchunk(wfi, slice(i * 256, (i + 1) * 256))
        a[:, :, 256:512] = _wchunk(wfi, slice(FH + i * 256, FH + (i + 1) * 256))
        put(CH_FF + i, a)
    for pg in range(3):
        nk = 8 if pg < 2 else 6
        for hf in range(2):
            blk = wfo[pg * 1024:pg * 1024 + nk * 128, hf * 512:(hf + 1) * 512]
            put(CH_FO + pg * 2 + hf, blk.reshape(nk, 128, 512).transpose(1, 0, 2))
    wmod = np.ascontiguousarray(
        w_modm.reshape(8, 128, 24, 256).transpose(2, 1, 0, 3).reshape(24, 128, 2048))
    l1 = np.concatenate([f(w1)[0], f(a1)[0], f(g1)[0]], 1)
    l1 = np.ascontiguousarray(l1.reshape(8, 128, 288).transpose(1, 0, 2).reshape(128, 8 * 288))
    l2 = np.zeros((128, 3, 1024), np.float32)
    l2[0:64, 0] = f(w2)[0]; l2[64:128, 0] = f(a2)[0]
    l2[:, 1] = f(g2)[0][0:128]; l2[0:32, 2] = f(g2)[0][128:160]
    l2 = l2.reshape(128, 3072)
    cbt = _host_consts()
    in_maps = []
    for core in range(8):
        b, hh = core // 2, core % 2
        pfm = np.concatenate([_fm(v) for v in vecs] + [_fm(c[b])], 1)
        if hh == 1:
            xvv = x[b]
        else:
            xvv = np.concatenate([np.zeros((T // 2, D), np.float32), x[b, :T // 2]], 0)
        in_maps.append({"xv": np.ascontiguousarray(xvv), "pfm": np.ascontiguousarray(pfm), "wmod": wmod,
                        "wsrc": wsrc, "l1": l1, "l2": l2, "cbt": cbt, "cft": _host_cf(hh)})
    return in_maps


def kernel(**inputs):
    in_maps = prep_inputs(**inputs)
    if "nc" not in _CACHE:
        _CACHE["nc"] = build()
    nc = _CACHE["nc"]
    res = run_bass_kernel_spmd(nc, in_maps, core_ids=list(range(8)))
    out = np.zeros((4, T, D), np.float32)
    for core in range(8):
        b, hh = core // 2, core % 2
        out[b, hh * (T // 2):(hh + 1) * (T // 2)] = res.results[core]["y"]
    return out
```

```python
import math
from contextlib import ExitStack

import numpy as np
import concourse.bass as bass
import concourse.mybir as mybir
from concourse.bass_utils import run_bass_kernel_spmd

F32 = mybir.dt.float32
BF16 = mybir.dt.bfloat16
AF = mybir.ActivationFunctionType
ALU = mybir.AluOpType
AX = mybir.AxisListType

ENGS = ("tensor", "vector", "scalar", "gpsimd", "sync")

T = 8192
D = 1024
TT = 256
NT = T // TT
PB0 = NT // 2
NSUB = TT // 128
FH = 2816
C0 = math.exp(-0.5)
GN_EPS = 64e-5
RMS_EPS = 1e-6
NSLOT = 4
import os
INTERLEAVE = os.environ.get('NOIL') is None
ATTN_EVERY = int(os.environ.get('ATTN_EVERY', '3'))
GPS_OFF = os.environ.get('GPS_OFF', '0') == '1'
LISTSCHED = os.environ.get('LISTSCHED', '1') == '1'
DENSE_ATTN_PROJ = os.environ.get('DENSE_ATTN_PROJ', '0') == '1'
S1_PREFETCH = os.environ.get('S1_PREFETCH', '1') == '1'
DYN_COPY = os.environ.get('DYN_COPY', '1') == '1'
SEM_LAT = float(os.environ.get('SEM_LAT', '300'))
PE_SCALE = float(os.environ.get('PE_SCALE', '1.0'))
ACT_SCALE = float(os.environ.get('ACT_SCALE', '1.0'))
DVE_SCALE = float(os.environ.get('DVE_SCALE', '1.0'))


class Buf:
    def __init__(self, name, t):
        self.name = name
        self.t = t
        self.writer = None
        self.readers = []
        self.dsem = None
        self.dcnt = 0
        self.psum = False

    def __getitem__(self, idx):
        return View(self, self.t[idx])

    def v(self, ap):
        return View(self, ap)


class SubBuf:
    def __init__(self, buf, col0, ncols=None):
        self.buf = buf
        self.col0 = col0
        self.ncols = ncols

    def __getitem__(self, idx):
        ps, cs = idx
        a = 0 if cs.start is None else cs.start
        e = cs.stop if cs.stop is not None else self.ncols
        assert e is not None
        return View(self.buf, self.buf.t[ps, self.col0 + a:self.col0 + e])


class View:
    def __init__(self, buf, ap):
        self.buf = buf
        self.ap = ap

    def __getitem__(self, idx):
        return View(self.buf, self.ap[idx])

    def re(self, pat, **kw):
        return View(self.buf, self.ap.rearrange(pat, **kw))

    def bc(self, axis, shape):
        return View(self.buf, self.ap.unsqueeze(axis).to_broadcast(list(shape)))


def _unw(x):
    return x.ap if isinstance(x, View) else x


class Sched:
    def __init__(self, nc, stack):
        self.nc = nc
        self.stack = stack
        self.q = {e: [] for e in ENGS}
        self.waited = {e: {} for e in ENGS}
        self.dma_sems = []
        self.fin = {}
        self.eng_free = {e: 0.0 for e in ENGS}
        self.cur_fin = 0.0

    def _est(self, eng, tok, deps_tokens, dur):
        ready = 0.0
        for t_ in deps_tokens:
            f_ = self.fin.get(t_)
            if f_ is not None and f_ > ready:
                ready = f_
        start = max(ready + SEM_LAT, self.eng_free[eng])
        fin = start + dur
        self.eng_free[eng] = fin if eng != "sync" and eng != "gpsimd" else start + 60.0
        self.fin[tok] = fin
        if len(self.fin) > 60000:
            ks = list(self.fin.keys())[:30000]
            for k_ in ks:
                del self.fin[k_]
        if fin > self.cur_fin:
            self.cur_fin = fin

    def sb(self, name, shape, dt):
        t = self.stack.enter_context(self.nc.sbuf_tensor("s_" + name, list(shape), dt))
        return Buf(name, t)

    def ps(self, name, shape, dt=F32):
        t = self.stack.enter_context(self.nc.psum_tensor("p_" + name, list(shape), dt))
        return Buf(name, t)

    def dram(self, name, shape, dt, kind):
        t = self.nc.dram_tensor(name, list(shape), dt, kind=kind).ap()
        return Buf(name, t)

    def _deps(self, eng, reads, writes):
        deps = {}

        def add(tok):
            if tok is None:
                return
            k, v = tok
            if deps.get(k, 0) < v:
                deps[k] = v

        for b in reads:
            add(b.writer)
            if b.psum:
                for r in b.readers:
                    if r[0] != eng:
                        add(r)
        for b in writes:
            add(b.writer)
            for r in b.readers:
                add(r)
        waits = []
        for k, v in deps.items():
            if k == "tensor" and eng == "tensor":
                continue
            if self.waited[eng].get(k, 0) >= v:
                continue
            self.waited[eng][k] = v
            waits.append((k, v))
            if isinstance(k, str):
                self.q[k][v - 1][2] = True
        return waits

    def _commit(self, tok, reads, writes):
        for b in writes:
            b.writer = tok
            b.readers = []
        for b in reads:
            if b in writes:
                continue
            b.readers.append(tok)
            if len(b.readers) > 48:
                d = {}
                for k, v in b.readers:
                    if d.get(k, 0) < v:
                        d[k] = v
                b.readers = list(d.items())

    def op(self, eng, meth, **kw):
        writes, reads = [], []
        for k, v in kw.items():
            if isinstance(v, View):
                if k in ("out", "accum_out", "ap"):
                    if v.buf not in writes:
                        writes.append(v.buf)
                else:
                    if v.buf not in reads:
                        reads.append(v.buf)
        dep_toks = [b_.writer for b_ in reads + writes if b_.writer is not None]
        for b_ in writes:
            dep_toks.extend(b_.readers)
        waits = self._deps(eng, reads, writes)
        if eng == "tensor":
            src = kw.get("lhsT", kw.get("in_"))
            lo = src.ap.base_partition()
            rows = (lo, lo + src.ap.partition_size())
            ob = kw["out"].buf
            prev = getattr(ob, "pe_rows", None)
            if prev is not None and ob.writer is not None and ob.writer[0] == "tensor" and \
                    (rows[1] <= prev[0] or prev[1] <= rows[0]):
                k, v = ob.writer
                if self.waited[eng].get(k, 0) < v:
                    self.waited[eng][k] = v
                    waits.append((k, v))
                    self.q[k][v - 1][2] = True
            ob.pe_rows = rows
        args = {k: _unw(v) for k, v in kw.items()}
        fn = lambda e, m=meth, a=args: getattr(e, m)(**a)
        self.q[eng].append([waits, fn, False, None])
        tok = (eng, len(self.q[eng]))
        o_ = kw.get("out", kw.get("ap"))
        try:
            fsz = o_.ap.free_size()
        except Exception:
            fsz = 256
        if eng == "tensor":
            n_ = kw["rhs"].ap.free_size() if "rhs" in kw else 128
            dur = (max(n_, 64) / 1.2 + 30.0) * PE_SCALE
        else:
            dur = (200.0 + 0.65 * fsz) * ACT_SCALE if eng == "scalar" else (150.0 + 0.75 * fsz) * DVE_SCALE
        self._est(eng, tok, dep_toks, dur)
        self._commit(tok, reads, writes)
        return tok

    def dma(self, eng, out, in_, **kw):
        sb = out.buf
        if sb.dsem is None:
            sb.dsem = ("dma", len(self.dma_sems))
            self.dma_sems.append(sb.name)
        dep_toks = [b_.writer for b_ in (in_.buf, out.buf) if b_.writer is not None] + list(out.buf.readers)
        waits = self._deps(eng, [in_.buf], [out.buf])
        sb.dcnt += 16
        tok = (sb.dsem, sb.dcnt)
        try:
            nbytes = out.ap.nbytes()
        except Exception:
            nbytes = 1 << 20
        self._est(eng, tok, dep_toks, 2500.0 + nbytes / 150.0)
        a = dict(out=out.ap, in_=in_.ap, **kw)
        fn = lambda e, a=a: e.dma_start(**a)
        self.q[eng].append([waits, fn, False, sb.dsem])
        self._commit(tok, [in_.buf], [out.buf])
        return tok

    def finish(self, final_bufs):
        waits = self._deps("sync", final_bufs, [])
        self.q["sync"].append([waits, None, False, None])

    def emit(self):
        nc = self.nc
        st = self.stack
        esem = {e: st.enter_context(nc.semaphore("es_" + e)) for e in ENGS}
        dsem = [st.enter_context(nc.semaphore("ds%d" % i)) for i in range(len(self.dma_sems))]
        cum = {}
        for e in ENGS:
            c = 0
            arr = []
            for it in self.q[e]:
                if it[2]:
                    c += 1
                arr.append(c)
            cum[e] = arr

        def semval(k, v):
            if isinstance(k, str):
                return esem[k], cum[k][v - 1]
            return dsem[k[1]], v

        block = st.enter_context(nc.Block())

        def run(e, eng):
            for waits, fn, sig, dk in self.q[e]:
                for k, v in waits:
                    s, val = semval(k, v)
                    eng.wait_ge(s, val)
                if fn is None:
                    continue
                ins = fn(eng)
                if dk is not None:
                    ins.then_inc(dsem[dk[1]], 16)
                elif sig:
                    ins.then_inc(esem[e], 1)

        @block.tensor
        def _(eng):
            run("tensor", eng)

        @block.vector
        def _(eng):
            run("vector", eng)

        @block.scalar
        def _(eng):
            run("scalar", eng)

        @block.gpsimd
        def _(eng):
            run("gpsimd", eng)

        @block.sync
        def _(eng):
            run("sync", eng)


def _alibi_slopes(n):
    def pow2(m):
        start = 2.0 ** (-8.0 / m)
        return [start ** (i + 1) for i in range(m)]
    if math.log2(n).is_integer():
        s = pow2(n)
    else:
        p = 2 ** int(math.floor(math.log2(n)))
        s = pow2(p) + pow2(2 * p)[0::2][: n - p]
    return sorted(s, reverse=True)


(PV_SHM, PV_SCM, PV_GTM, PV_SHF, PV_SCF, PV_GTF, PV_GPM, PV_GQM, PV_GPF, PV_GQF,
 PV_MUR, PV_MUK, PV_MUV, PV_MUW, PV_MUA, PV_MUG, PV_W0, PV_A0, PV_KK, PV_KA, PV_RK,
 PV_LNW, PV_LNB, PV_C) = range(24)
NPV = 24

CB_ID = 0
CB_ONESBD = 128
CB_ONES = 256
CB_MT4 = 320
CB_ML4 = 832
CB_E1 = 1344
CB_EA2 = CB_E1 + 8 * 256
CB_EB2 = CB_EA2 + 2 * 8 * 64
CB_EA3 = CB_EB2 + 8 * 64
CB_EB3 = CB_EA3 + 8 * 8 * 16
NCB = CB_EB3 + 8 * 16
CF_MSK = 0
CF_VM = 256
CF_EPS = CF_VM + 128
CF_IDF = CF_EPS + 4
NCF = CF_IDF + 128

CH_RW = 0
CH_AT = 8
CH_PB = 17
CH_WOUT = 25
CH_FF = 27
CH_FO = 38
NCH = 44


def _host_consts():
    sl = np.asarray(_alibi_slopes(24), np.float64).reshape(3, 8)
    cb = np.zeros((128, NCB), np.float32)
    p = np.arange(128)
    cb[:, CB_ID:CB_ID + 128] = np.eye(128)
    cb[:, CB_ONESBD:CB_ONESBD + 128] = (p[:, None] // 64 == p[None, :] // 64)
    cb[:, CB_ONES:CB_ONES + 64] = 1.0
    same = (p[:, None] // 64 == p[None, :] // 64)
    su = same & (p[:, None] < p[None, :])
    iu = same & (p[:, None] <= p[None, :])
    slo = same & (p[:, None] > p[None, :])
    cb[:, CB_MT4:CB_MT4 + 512] = np.concatenate([su, iu, su, iu], 1)
    cb[:, CB_ML4:CB_ML4 + 512] = np.concatenate([slo] * 4, 1)
    k = p[:, None].astype(np.float64)
    q = p[None, :].astype(np.float64)
    for h in range(8):
        dpv = q - k + 128
        e_prev = np.where(dpv <= 128, np.exp(-sl[0, h] * dpv), 0.0)
        dcu = q - k
        e_cur = np.where(dcu >= 0, np.exp(-sl[0, h] * np.maximum(dcu, 0)), 0.0)
        cb[:, CB_E1 + h * 256: CB_E1 + h * 256 + 128] = e_prev
        cb[:, CB_E1 + h * 256 + 128: CB_E1 + h * 256 + 256] = e_cur
    i64 = np.arange(64)[None, :].astype(np.float64)
    for rot in range(2):
        for h in range(8):
            j = p // 64
            pp = (p % 64).astype(np.float64)
            a = ((rot - j - 1) % 2) + 1
            dl = 64.0 * a[:, None] + i64 - pp[:, None]
            e = np.where(dl <= 128, np.exp(-sl[1, h] * 4.0 * dl), 0.0)
            o = CB_EA2 + (rot * 8 + h) * 64
            cb[:, o:o + 64] = e
    for h in range(8):
        kk = np.arange(64)[:, None].astype(np.float64)
        dl = i64 - kk
        e = np.where(dl >= 0, np.exp(-sl[1, h] * 4.0 * np.maximum(dl, 0)), 0.0)
        o = CB_EB2 + h * 64
        cb[0:64, o:o + 64] = e
    i16 = np.arange(16)[None, :].astype(np.float64)
    for rot in range(8):
        for h in range(8):
            j = p // 16
            pp = (p % 16).astype(np.float64)
            a = ((rot - j - 1) % 8) + 1
            dl = 16.0 * a[:, None] + i16 - pp[:, None]
            e = np.where(dl <= 128, np.exp(-sl[2, h] * 16.0 * dl), 0.0)
            o = CB_EA3 + (rot * 8 + h) * 16
            cb[:, o:o + 16] = e
    for h in range(8):
        kk = np.arange(16)[:, None].astype(np.float64)
        dl = i16 - kk
        e = np.where(dl >= 0, np.exp(-sl[2, h] * 16.0 * np.maximum(dl, 0)), 0.0)
        o = CB_EB3 + h * 16
        cb[0:16, o:o + 16] = e
    return cb


def _host_cf(hh):
    cf = np.zeros((128, NCF), np.float32)
    m = np.ones((128, 256), np.float32)
    m[:, 0::64] = 0.0
    cf[:, CF_MSK:CF_MSK + 256] = m
    valid = lambda t: 0.0 if t < 0 else (1.0 if (hh == 1 or t >= PB0) else 0.0)
    p = np.arange(128)
    for t in range(NT):
        cf[:, CF_VM + t] = valid(t)
        cf[:, CF_VM + 32 + t] = valid(t - 1)
        j = p // 64
        a = ((t - j - 1) % 2) + 1
        cf[:, CF_VM + 64 + t] = [valid(t - aa) for aa in a]
        j = p // 16
        a = ((t - j - 1) % 8) + 1
        cf[:, CF_VM + 96 + t] = [valid(t - aa) for aa in a]
    cf[:, CF_EPS] = RMS_EPS
    cf[:, CF_EPS + 1] = GN_EPS
    cf[:, CF_EPS + 3] = 1.0
    cf[:, CF_IDF:CF_IDF + 128] = np.eye(128)
    return cf


def _fm(v):
    return np.ascontiguousarray(v.reshape(8, 128).T)


def _wchunk(w, cols):
    return w[:, cols].reshape(8, 128, -1).transpose(1, 0, 2)


class _Stop(Exception):
    pass


def build(nt=NT, dbg=None, dbg_tile=0, dbg_c=0, stop=None):
    nc = bass.Bass("TRN2", target_bir_lowering=False)
    with ExitStack() as st:
        S = Sched(nc, st)
        finals = []

        def CK(name):
            if stop == name:
                raise _Stop()

        def DBG(name, view, m=None, c=None):
            if not dbg or name not in dbg:
                return
            if m is not None and m != dbg_tile:
                return
            if c is not None and c != dbg_c:
                return
            shp = list(view.ap.shape)
            dd = S.dram("dbg_" + name, shp, view.ap.dtype, "ExternalOutput")
            S.dma("gpsimd", out=dd[:], in_=view)
            finals.append(dd)
        xv = S.dram("xv", [T, D], F32, "ExternalInput")
        pfm_d = S.dram("pfm", [128, NPV * 8], F32, "ExternalInput")
        wmod_d = S.dram("wmod", [24, 128, 2048], F32, "ExternalInput")
        wsrc = S.dram("wsrc", [NCH, 128, 4096], F32, "ExternalInput")
        l1_d = S.dram("l1", [128, 8 * 288], F32, "ExternalInput")
        l2_d = S.dram("l2", [128, 3 * 1024], F32, "ExternalInput")
        cb_d = S.dram("cbt", [128, NCB], F32, "ExternalInput")
        cf_d = S.dram("cft", [128, NCF], F32, "ExternalInput")
        y_d = S.dram("y", [T // 2, D], F32, "ExternalOutput")
        wscr_all = S.dram("wscr", [NCH, 128, 4096], BF16, "Internal")
        wscr = [Buf("wscr%d" % i, wscr_all.t[i]) for i in range(NCH)]

        cb = S.sb("cb", [128, NCB], BF16)
        cf = S.sb("cf", [128, NCF], F32)
        pf = S.sb("pf", [128, NPV * 8], F32)
        pd = S.sb("pd", [128, 12 * 8], F32)
        gmb = S.sb("gmb", [128, 1024], BF16)
        gfb = S.sb("gfb", [128, 1024], BF16)
        l1a = S.sb("l1a", [128, 8, 288], BF16)
        l1b = S.sb("l1b", [128, 8, 288], BF16)
        l2 = S.sb("l2", [128, 3, 1024], BF16)
        ring = [S.sb("ring%d" % i, [128, 4096], BF16) for i in range(NSLOT)]
        xt1 = S.sb("xt", [128, NSUB, 1024], F32)
        xt = [xt1, xt1]
        nb = S.sb("nb", [128, NSUB, 1024], BF16)
        junk = nb[:, 0, :]
        st4 = S.sb("st4", [128, 16], F32)
        hT = S.sb("hT", [128, 8, TT + 1], BF16)
        h2T = S.sb("h2T", [128, 8, TT], BF16)
        mixT = h2T
        Zf = S.sb("Zf", [128, 8, 64], F32)
        Zb = S.sb("Zb", [128, 8, 2, 64], BF16)
        hal = S.sb("hal", [128, 8, 3], F32)
        tp = [S.sb("tp%d" % i, [128, TT + 1], F32) if i != 7 else None for i in range(12)]
        tp[7] = tp[6]
        tv = lambda i: tp[i][:, 0:TT]
        pj = [tp[0], tp[0], tp[0]]
        tmpd = tv(1)
        rkv = [tv(2), tv(3), tv(4)]
        sw = tv(5); asig = tv(6); gg = tv(7); cs = tv(0); cm = tv(1)
        Ep = tv(8); En = tv(9); Em = tv(10); rinv = tv(1); kkb = tv(11); ff = tv(0)
        kmod = tv(5); bv = tv(1); bon = tv(6); yln = tv(9); ysq = tv(10)
        sqb = S.sb("sqb", [128, TT], BF16)
        ARs = [S.sb("AR%d" % i, [128, NSUB, 2, 128], BF16) for i in range(3)]
        Bts = [S.sb("Bt%d" % i, [128, TT], BF16) for i in range(2)]
        Kts = [S.sb("Kt%d" % i, [128, TT], BF16) for i in range(2)]
        vbfs = [S.sb("vbf%d" % i, [128, TT], BF16) for i in range(2)]
        Bpads = [S.sb("Bpad%d" % i, [128, NSUB, 2, 128], BF16) for i in range(2)]
        Kpads = [S.sb("Kpad%d" % i, [128, NSUB, 2, 128], BF16) for i in range(2)]
        Vtms = [S.sb("Vtm%d" % i, [128, NSUB, 128], BF16) for i in range(2)]
        AMs = [S.sb("AM%d" % i, [128, 4, 512], BF16) for i in range(2)]
        TTfs = [S.sb("TTf%d" % i, [128, 4, 128], BF16) for i in range(2)]
        gbs = [S.sb("gb%d" % i, [128, TT], BF16) for i in range(3)]
        pcss = [S.sb("pcs%d" % i, [128, 4], F32) for i in range(3)]
        ysqB = S.sb("ysqB", [128, TT], F32)
        L0 = S.sb("L0", [128, 4, 128], BF16)
        LP = [S.sb("LP%d" % i, [128, 4, 128], BF16) for i in range(2)]
        LT = [S.sb("LT%d" % i, [128, 4, 128], BF16) for i in range(2)]
        SS = [S.sb("SS%d" % i, [128, 4, 128], BF16) for i in range(2)]
        Xb = S.sb("Xb", [128, 128], BF16)
        Ub = S.sb("Ub", [128, 128], BF16)
        ztmp = S.sb("ztmp", [128, 64], F32)
        Ytm = S.sb("Ytm", [128, NSUB, 128], F32)
        ynb = S.sb("ynb", [128, NSUB, 128], BF16)
        gst = S.sb("gst", [128, 32], F32)
        lw = S.sb("lw", [128, TT], BF16)
        lga = S.sb("lga", [128, TT], BF16)
        lgb = S.sb("lgb", [32, TT], BF16)
        yfin = S.sb("yfin", [128, 8, TT], BF16)
        _nbf = nb[:].re("p s c -> p (s c)")
        Qa = [h2T[:, 0:4, :], h2T[:, 4:8, :], _nbf[:, 0:1024].re("p (c t) -> p c t", t=TT)]
        K1 = S.sb("K1", [128, 4, 128 + TT], BF16)
        V1 = S.sb("V1", [128, 3, 512], BF16)
        K2c = S.sb("K2c", [128, 4, TT], BF16)
        K2r = S.sb("K2r", [128, 4, 4, 128], BF16)
        V2c = S.sb("V2c", [64, 4, 128], BF16)
        V2r = S.sb("V2r", [128, 4, 512], BF16)
        K3c = S.sb("K3c", [128, 4, TT], BF16)
        K3r = S.sb("K3r", [128, 4, 16, 128], BF16)
        V3c = S.sb("V3c", [16, 16, 128], BF16)
        V3r = S.sb("V3r", [128, 16, 512], BF16)
        VF = _nbf[:, 1024:2048].re("p (c t) -> p c t", t=TT)
        pe = S.sb("pe", [128, 512], BF16)
        pp_ = S.sb("pp", [128, 512], BF16)
        peb = SubBuf(pe, 256, 256)
        ppb = SubBuf(pp_, 256, 256)
        accO = S.sb("accO", [64, TT], F32)
        accD = S.sb("accD", [64, TT], F32)
        oT = S.sb("oT", [64, 8, TT], BF16)
        sa = tv(2); sbb = tv(3); sg = tv(4); utmp = tv(10)
        actT = S.sb("actT", [128, 8, TT], BF16)
        VF2 = actT[:, 0:4, :]
        wst = xt1[:].re("p s c -> p (s c)")

        _b0 = S.ps("b0", [128, 512])
        _b4 = S.ps("b4", [128, 512])
        _bS = S.ps("bS", [128, 1024])
        B3 = S.ps("b3", [128, 512])
        B5 = S.ps("b5", [128, 512])
        B6 = S.ps("b6", [128, 512])
        _pT = S.ps("pT", [128, 1024], BF16)
        R0 = SubBuf(_b0, 0); R1 = SubBuf(_b0, 256)
        Q0 = SubBuf(_b4, 0); Q1 = SubBuf(_b4, 256)
        B1 = Buf("B1", _bS.t[:, 0:512]); B2 = Buf("B2", _bS.t[:, 512:1024])
        pTr = SubBuf(_pT, 0); pTa = SubBuf(_pT, 512)
        for b_ in (_b0, _b4, B1, B2, B3, B5, B6, _pT):
            b_.psum = True
        pC = B3
        trB = [View(B1, B1.t[:, :].bitcast(BF16)), View(B2, B2.t[:, :].bitcast(BF16))]
        trC = View(B3, B3.t[:, :].bitcast(BF16))
        trot = [View(_pT, _pT.t[:, 0:512]), View(B6, B6.t[:, :].bitcast(BF16)), View(B5, B5.t[:, :].bitcast(BF16))]
        prot = {"r": [_b0, B1, B2], "a": [_b4, B5, B6], "x": [_b0, _b4, B1, B2, B3, B6], "f": [_b0, _b4, B6], "s": [_b0, _b4, B1, B2, B3, B6]}
        prot_i = {"r": 0, "a": 0, "x": 0, "f": 0, "s": 0}

        def nextp(k="x"):
            prot_i[k] = (prot_i[k] + 1) % len(prot[k])
            return prot[k][prot_i[k]]

        mm = lambda **kw: S.op("tensor", "matmul", **kw)
        tr = lambda **kw: S.op("tensor", "transpose", **kw)
        act = lambda **kw: S.op("scalar", "activation", **kw)
        vec = lambda m, **kw: S.op("vector", m, **kw)
        gps = lambda m, **kw: S.op("gpsimd", m, **kw)

        def cpy(out, in_):
            if (not DYN_COPY) or S.eng_free["scalar"] <= S.eng_free["vector"]:
                act(out=out, in_=in_, func=AF.Copy)
            else:
                vec("tensor_copy", out=out, in_=in_)

        def sigmoid_to(dst, src, nbias=None, scale=1.0):
            if nbias is None:
                act(out=dst, in_=src, func=AF.Exp, scale=-scale)
            else:
                act(out=dst, in_=src, func=AF.Exp, scale=-scale, bias=nbias)
            act(out=dst, in_=dst, func=AF.Ln, bias=one_c_for(dst))
            act(out=dst, in_=dst, func=AF.Exp, scale=-1.0)

        def one_c_for(v):
            lo = v.ap.base_partition()
            n = v.ap.partition_size()
            return cf[lo:lo + n, CF_EPS + 3:CF_EPS + 4]

        def rsqrt_to(dst, src, bias_ap, scale=1.0):
            act(out=dst, in_=src, func=AF.Ln, bias=bias_ap, scale=scale)
            act(out=dst, in_=dst, func=AF.Exp, scale=-0.5)

        ident = cb[:, CB_ID:CB_ID + 128]
        identf = cf[:, CF_IDF:CF_IDF + 128]
        onesbd = cb[:, CB_ONESBD:CB_ONESBD + 128]
        eps_r = cf[:, CF_EPS:CF_EPS + 1]
        eps_g = cf[:, CF_EPS + 1:CF_EPS + 2]
        zero_c = cf[:, CF_EPS + 2:CF_EPS + 3]
        one_c = cf[:, CF_EPS + 3:CF_EPS + 4]

        def pv(i, kc):
            return pf[:, i * 8 + kc: i * 8 + kc + 1]

        def pdv(i, kc):
            return pd[:, i * 8 + kc: i * 8 + kc + 1]
        PD_A1, PD_A2, PD_GM, PD_GF, PD_OMK, PD_OMR, PD_OMKm, PD_OMV = range(8)

        try:
            S.dma("gpsimd", out=cb[:, :], in_=cb_d[:, :])
            S.dma("sync", out=cf[:, :], in_=cf_d[:, :])
            S.dma("sync", out=pf[:, :], in_=pfm_d[:, :])
            for i in range(NCH):
                S.dma("gpsimd", out=wscr[i][:, :], in_=wsrc[i])
            CK('dma0')
            for b_ in (Zf, Zb, hal, Bpads[0], Bpads[1], Kpads[0], Kpads[1], K1, K2r, V2r, K3r, V3r, V1, hT, Xb, Ub, Vtms[0], Vtms[1]):
                gps("memset", ap=b_[:], constant=0.0)
            CK('memset')
            for half in range(2):
                S.dma("sync", out=wst[:, 0:4 * 288], in_=l1_d[:, half * 4 * 288:(half + 1) * 4 * 288])
                w1v = wst[:, 0:4 * 288].re("p (k c) -> p k c", c=288)
                for k4 in range(4):
                    kc = half * 4 + k4
                    for (lo, hi, mui) in ((0, 64, PV_MUW), (64, 128, PV_MUA), (128, 288, PV_MUG)):
                        vec("tensor_scalar", out=l1b[:, kc, lo:hi], in0=w1v[:, k4, lo:hi], scalar1=pv(mui, kc),
                            scalar2=None, op0=ALU.mult)
                        vec("tensor_tensor", out=l1a[:, kc, lo:hi], in0=w1v[:, k4, lo:hi], in1=l1b[:, kc, lo:hi],
                            op=ALU.subtract)
            for half in range(2):
                S.dma("sync", out=wst[:, 0:1536], in_=l2_d[:, half * 1536:(half + 1) * 1536])
                cpy(out=l2[:].re("p a c -> p (a c)")[:, half * 1536:(half + 1) * 1536], in_=wst[:, 0:1536])
            CK('lora0')
            for j in range(24):
                S.dma("sync", out=wst[:, 0:2048], in_=wmod_d[j])
                wv = wst[:, 0:2048].re("p (k c) -> p k c", c=256)
                for cc in range(2):
                    col = j * 2 + cc
                    for kc in range(8):
                        mm(out=B1[:, col:col + 1], lhsT=wv[:, kc, cc * 128:(cc + 1) * 128], rhs=pv(PV_C, kc),
                           start=(kc == 0), stop=(kc == 7))
            modf = S.sb("modf", [128, 48], F32)
            vec("tensor_tensor", out=modf[:, :], in0=B1[:, 0:48], in1=pf[:, 0:48], op=ALU.add)
            for kc in range(8):
                vec("scalar_tensor_tensor", out=pdv(PD_A1, kc), in0=modf[:, 8 + kc:9 + kc], scalar=1.0,
                    in1=pv(PV_GPM, kc), op0=ALU.add, op1=ALU.mult)
                vec("scalar_tensor_tensor", out=pdv(PD_A2, kc), in0=modf[:, 32 + kc:33 + kc], scalar=1.0,
                    in1=pv(PV_GPF, kc), op0=ALU.add, op1=ALU.mult)
                vec("tensor_tensor", out=pdv(PD_GM, kc), in0=modf[:, 16 + kc:17 + kc], in1=pv(PV_GQM, kc), op=ALU.mult)
                vec("tensor_tensor", out=pdv(PD_GF, kc), in0=modf[:, 40 + kc:41 + kc], in1=pv(PV_GQF, kc), op=ALU.mult)
                vec("tensor_scalar", out=pdv(PD_OMK, kc), in0=pv(PV_KA, kc), scalar1=-1.0, scalar2=1.0,
                    op0=ALU.mult, op1=ALU.add)
                vec("tensor_scalar", out=pdv(5, kc), in0=pv(PV_W0, kc), scalar1=-1.0, scalar2=None, op0=ALU.mult)
                vec("tensor_scalar", out=pdv(6, kc), in0=pv(PV_A0, kc), scalar1=-1.0, scalar2=None, op0=ALU.mult)
            dg = tp[0][:, 0:128]
            onesf = tp[1][:, 0:128]
            gps("memset", ap=onesf[:, :], constant=1.0)
            for (pdi, dst) in ((PD_GM, gmb), (PD_GF, gfb)):
                for kc in range(8):
                    vec("tensor_scalar", out=dg[:, :], in0=identf, scalar1=pdv(pdi, kc), scalar2=None, op0=ALU.mult)
                    pz = nextp()
                    mm(out=pz[:, 0:128], lhsT=onesf[:, :], rhs=dg[:, :], start=True, stop=True)
                    cpy(out=dst[:, kc * 128:(kc + 1) * 128], in_=pz[:, 0:128])

            CK('startup')
            ring_i = [0]

            ring_sets = {"r": ring[0:2], "a": ring[2:4], "x": ring}
            ring_k = {"r": 0, "a": 0, "x": 0}

            def wload(ch, k="x"):
                s = ring_sets[k][ring_k[k] % len(ring_sets[k])]
                ring_k[k] += 1
                S.dma("sync", out=s[:, :], in_=wscr[ch][:, :])
                return s

            def rms_rstd(src3, dst_cols, nsub=NSUB):
                for sub in range(nsub):
                    act(out=junk[:, :], in_=src3[:, sub, :], func=AF.Square,
                        accum_out=st4[:, 8 + sub:9 + sub])
                rsqrt_to(st4[:, dst_cols:dst_cols + nsub], st4[:, 8:8 + nsub], eps_r, 1.0 / D)

            def norm_transpose(xsrc, rcol, dstT, a_idx, b_view_fn, halo):
                for sub in range(NSUB):
                    vec("tensor_scalar", out=nb[:, sub, :], in0=xsrc[:, sub, :], scalar1=st4[:, rcol + sub:rcol + sub + 1],
                        scalar2=None, op0=ALU.mult)
                for kc in range(8):
                    tgt = trot[kc % 3]
                    for sub in range(NSUB):
                        tr(out=tgt[:, sub * 128:(sub + 1) * 128], in_=nb[:, sub, kc * 128:(kc + 1) * 128], identity=ident)
                    act(out=dstT[:, kc, halo:halo + TT], in_=tgt[:, 0:TT], func=AF.Identity,
                        scale=pdv(a_idx, kc), bias=b_view_fn(kc))

            def norm_transpose_g(xsrc, rcol, dstT, a_idx, b_view_fn, halo):
                for sub in range(NSUB):
                    vec("tensor_scalar", out=nb[:, sub, :], in0=xsrc[:, sub, :], scalar1=st4[:, rcol + sub:rcol + sub + 1],
                        scalar2=None, op0=ALU.mult)
                    yield
                for kc in range(8):
                    tgt = trot[kc % 3]
                    for sub in range(NSUB):
                        tr(out=tgt[:, sub * 128:(sub + 1) * 128], in_=nb[:, sub, kc * 128:(kc + 1) * 128], identity=ident)
                    act(out=dstT[:, kc, halo:halo + TT], in_=tgt[:, 0:TT], func=AF.Identity,
                        scale=pdv(a_idx, kc), bias=b_view_fn(kc))
                    yield

            s1_done = set()

            for m in range(nt):
                phaseB = m >= PB0
                prot["r"] = [_b0] if m >= PB0 - 8 else [_b0, _b4, B5, B6]
                xm = xt[m % 2]
                vcur = cf[:, CF_VM + m:CF_VM + m + 1]
                vprev = cf[:, CF_VM + 32 + m:CF_VM + 33 + m]
                vr2 = cf[:, CF_VM + 64 + m:CF_VM + 65 + m]
                vr3 = cf[:, CF_VM + 96 + m:CF_VM + 97 + m]
                def stage1_gen(tix):
                    S.dma("gpsimd", out=xm[:], in_=xv.v(xv.t[tix * TT:(tix + 1) * TT, :].rearrange("(s p) c -> p s c", p=128)))
                    yield
                    if tix > 0:
                        vec("tensor_scalar", out=hT[:, :, 0:1], in0=hT[:, :, TT:TT + 1],
                            scalar1=cf[:, CF_VM + tix - 1:CF_VM + tix], scalar2=None, op0=ALU.mult)
                        yield
                    rms_rstd(xm, 0)
                    yield
                    yield from norm_transpose_g(xm, 0, hT, PD_A1, lambda kc: pf[:, PV_SHM * 8 + kc:PV_SHM * 8 + kc + 1]
                                   if False else modf[:, kc:kc + 1], 1)

                    pz = nextp("s")
                    for kc in range(8):
                        mm(out=pz[:, 0:TT], lhsT=l1a[:, kc, 0:128], rhs=hT[:, kc, 1:TT + 1], start=(kc == 0), stop=False)
                        mm(out=pz[:, 0:TT], lhsT=l1b[:, kc, 0:128], rhs=hT[:, kc, 0:TT], start=False, stop=(kc == 7))
                    sigmoid_to(tp[11][0:64, 0:TT], pz[0:64, 0:TT], None, 2.0)
                    yield
                    vec("tensor_scalar", out=lw[0:64, :], in0=tp[11][0:64, 0:TT], scalar1=2.0, scalar2=-1.0, op0=ALU.mult, op1=ALU.add)
                    yield
                    cpy(out=lw[64:128, :], in_=pz[64:128, 0:TT])
                    yield
                    if tix >= PB0:
                        pz = nextp("s")
                        for kc in range(8):
                            mm(out=pz[:, 0:TT], lhsT=l1a[:, kc, 128:256], rhs=hT[:, kc, 1:TT + 1], start=(kc == 0), stop=False)
                            mm(out=pz[:, 0:TT], lhsT=l1b[:, kc, 128:256], rhs=hT[:, kc, 0:TT], start=False, stop=(kc == 7))
                        sigmoid_to(tp[11][:, 0:TT], pz[:, 0:TT])
                        yield
                        cpy(out=lga[:, :], in_=tp[11][:, 0:TT])
                        yield
                        pz = nextp("s")
                        for kc in range(8):
                            mm(out=pz[0:32, 0:TT], lhsT=l1a[:, kc, 256:288], rhs=hT[:, kc, 1:TT + 1], start=(kc == 0), stop=False)
                            mm(out=pz[0:32, 0:TT], lhsT=l1b[:, kc, 256:288], rhs=hT[:, kc, 0:TT], start=False, stop=(kc == 7))
                        sigmoid_to(tp[11][0:32, 0:TT], pz[0:32, 0:TT])
                        yield
                        cpy(out=lgb[0:32, :], in_=tp[11][0:32, 0:TT])
                        yield


                if m not in s1_done:
                    for _ in stage1_gen(m):
                        pass

                CK('lora1')
                def rw_f1(c0):
                    for c in (c0,):
                        AR = ARs[c % 3]; AM = AMs[c % 2]; Vtm = Vtms[c % 2]; Bpad = Bpads[c % 2]; Kpad = Kpads[c % 2]
                        TTf = TTfs[c % 2]; gb = gbs[c % 3]; pcs = pcss[c % 3]
                        Bt = Bts[c % 2]; Kt = Kts[c % 2]; vbf = vbfs[c % 2]
                        csl = slice(c * 128, (c + 1) * 128)
                        wr = wload(CH_RW + c, 'r')
                        wrv = wr[:, :].re("p (k c) -> p k c", c=512)
                        for j in ((0, 1, 2) if m >= PB0 - 1 else (1, 2)):
                            pz = nextp("r")
                            for kc in range(8):
                                mm(out=pz[:, 0:TT], lhsT=wrv[:, kc, j * 128:(j + 1) * 128], rhs=hT[:, kc, 1:TT + 1],
                                   start=(kc == 0), stop=(kc == 7))
                            cpy(out=pj[j][:, 0:1], in_=hal[:, c, j:j + 1])
                            yield
                            cpy(out=pj[j][:, 1:TT + 1], in_=pz[:, 0:TT])
                            yield
                            vec("tensor_scalar", out=hal[:, c, j:j + 1], in0=pj[j][:, TT:TT + 1], scalar1=vcur,
                                scalar2=None, op0=ALU.mult)
                            yield
                            vec("tensor_tensor", out=tmpd[:, :], in0=pj[j][:, 0:TT], in1=pj[j][:, 1:TT + 1], op=ALU.subtract)
                            yield
                            vec("scalar_tensor_tensor", out=rkv[j][:, :], in0=tmpd[:, :], scalar=pv(PV_MUR + j, c),
                                in1=pj[j][:, 1:TT + 1], op0=ALU.mult, op1=ALU.add)
                            yield
                        r_, k_, v_ = rkv
                        vec("tensor_scalar", out=v_[:, :], in0=v_[:, :], scalar1=vcur, scalar2=None, op0=ALU.mult)
                        yield
                        cpy(out=vbf[:, :], in_=v_[:, :])
                        yield
                        pz = nextp("r")
                        mm(out=pz[:, 0:TT], lhsT=l2[0:64, 0, csl], rhs=lw[0:64, :], start=True, stop=True)
                        sigmoid_to(sw[:, :], pz[:, 0:TT], pdv(5, c))
                        yield
                        pz = nextp("r")
                        mm(out=pz[:, 0:TT], lhsT=l2[64:128, 0, csl], rhs=lw[64:128, :], start=True, stop=True)
                        sigmoid_to(asig[:, :], pz[:, 0:TT], pdv(6, c))
                        yield
                        pz = nextp("r")
                        if phaseB:
                            mm(out=pz[:, 0:TT], lhsT=l2[:, 1, csl], rhs=lga[:, :], start=True, stop=False)
                            mm(out=pz[:, 0:TT], lhsT=l2[0:32, 2, csl], rhs=lgb[0:32, :], start=False, stop=True)
                            cpy(out=gb[:, :], in_=pz[:, 0:TT])
                        yield
                        vec("tensor_tensor_scan", out=cs[:, :], data0=cf[:, CF_MSK:CF_MSK + TT], data1=sw[:, :],
                            initial=0.0, op0=ALU.mult, op1=ALU.add)
                        yield
                        vec("tensor_tensor", out=cm[:, :], in0=cs[:, :], in1=sw[:, :], op=ALU.subtract)
                        yield
                        act(out=Ep[:, :], in_=cs[:, :], func=AF.Exp, scale=-C0)
                        yield
                        act(out=En[:, :], in_=cs[:, :], func=AF.Exp, scale=C0)
                        yield
                        act(out=Em[:, :], in_=cm[:, :], func=AF.Exp, scale=-C0)
                        yield
                        act(out=sqb[:, :], in_=k_[:, :], func=AF.Square, scale=pv(PV_KK, c))
                        yield
                        pz = nextp("r")
                        mm(out=pz[:, 0:TT], lhsT=onesbd, rhs=sqb[:, :], start=True, stop=True)
                        vec("tensor_scalar", out=rinv[:, :], in0=pz[:, 0:TT], scalar1=1e-18, scalar2=None, op0=ALU.max)
                        yield
                        act(out=rinv[:, :], in_=rinv[:, :], func=AF.Ln)
                        yield
                        act(out=rinv[:, :], in_=rinv[:, :], func=AF.Exp, scale=-0.5)
                        yield
                        vec("scalar_tensor_tensor", out=kkb[:, :], in0=k_[:, :], scalar=pv(PV_KK, c), in1=rinv[:, :],
                            op0=ALU.mult, op1=ALU.mult)
                        yield
                        ev = gps if GPS_OFF else vec
                        ev("tensor_scalar", out=ff[:, :], in0=asig[:, :], scalar1=pv(PV_KA, c), scalar2=pdv(PD_OMK, c),
                            op0=ALU.mult, op1=ALU.add)
                        yield
                        ev("tensor_tensor", out=kmod[:, :], in0=k_[:, :], in1=ff[:, :], op=ALU.mult)
                        yield
                        ev("tensor_tensor", out=bv[:, :], in0=kkb[:, :], in1=asig[:, :], op=ALU.mult)
                        yield
                        vec("scalar_tensor_tensor", out=AR[:, :, 0, :], in0=kkb[:, :].re("p (s t) -> p s t", t=128), scalar=-1.0,
                            in1=Em[:, :].re("p (s t) -> p s t", t=128), op0=ALU.mult, op1=ALU.mult)
                        yield
                        if phaseB:
                            vec("tensor_tensor", out=AR[:, :, 1, :], in0=r_[:, :].re("p (s t) -> p s t", t=128),
                                in1=Ep[:, :].re("p (s t) -> p s t", t=128), op=ALU.mult)
                            yield
                        ev("tensor_tensor", out=Bt[:, :], in0=bv[:, :], in1=En[:, :], op=ALU.mult)
                        yield
                        ev("tensor_tensor", out=Kt[:, :], in0=kmod[:, :], in1=En[:, :], op=ALU.mult)
                        yield
                        if phaseB:
                            vec("tensor_tensor", out=tmpd[:, :], in0=r_[:, :], in1=kmod[:, :], op=ALU.mult)
                            yield
                            act(out=sqb[:, :], in_=tmpd[:, :], func=AF.Copy, scale=pv(PV_RK, c))
                            yield
                            pz = nextp("r")
                            mm(out=pz[:, 0:TT], lhsT=onesbd, rhs=sqb[:, :], start=True, stop=True)
                            vec("tensor_tensor", out=bon[:, :], in0=pz[:, 0:TT], in1=v_[:, :], op=ALU.mult)
                            yield
                            vec("tensor_tensor", out=yfin[:, c, :], in0=bon[:, :], in1=gb[:, :], op=ALU.mult)
                        yield
                        cpy(out=pcs[:, 0:4], in_=Ep[:, :].re("p (q t) -> p q t", t=64)[:, :, 63])
                        yield

                def rw_f2(c0):
                    for c in (c0,):
                        AR = ARs[c % 3]; AM = AMs[c % 2]; Vtm = Vtms[c % 2]; Bpad = Bpads[c % 2]; Kpad = Kpads[c % 2]
                        TTf = TTfs[c % 2]; gb = gbs[c % 3]; pcs = pcss[c % 3]
                        Bt = Bts[c % 2]; Kt = Kts[c % 2]; vbf = vbfs[c % 2]
                        for qi, (src, dst) in enumerate(((Bt, Bpad), (Kt, Kpad), (vbf, None))):
                            for sub in range(NSUB):
                                tr(out=trB[qi % 2][:, sub * 128:(sub + 1) * 128],
                                   in_=src[:, sub * 128:(sub + 1) * 128], identity=ident)
                            yield
                            srcv = trB[qi % 2][:, 0:256]
                            if dst is None:
                                cpy(out=Vtm[:].re("p s c -> p (s c)"), in_=srcv)
                                yield
                            else:
                                for h in range(2):
                                    cpy(out=dst[:, :, h, h * 64:(h + 1) * 64],
                                        in_=srcv.re("p (s c) -> p s c", c=128)[:, :, h * 64:(h + 1) * 64])
                                    yield
                        CK('rwkv_a')
                        for h in range(2):
                            hs = slice(h * 64, (h + 1) * 64)
                            for sub in range(NSUB):
                                u = h * NSUB + sub
                                tsl = slice(sub * 128, (sub + 1) * 128)
                                pz = (B1, B2)[u % 2]
                                if phaseB:
                                    mm(out=pz[:, 0:256], lhsT=Bt[hs, tsl], rhs=AR[hs, sub, :, :].re("p a t -> p (a t)"),
                                       start=True, stop=True)
                                    mm(out=pz[:, 256:512], lhsT=Kt[hs, tsl], rhs=AR[hs, sub, :, :].re("p a t -> p (a t)"),
                                       start=True, stop=True)
                                    vec("tensor_tensor", out=AM[:, u, :], in0=pz[:, :], in1=cb[:, CB_MT4:CB_MT4 + 512], op=ALU.mult)
                                    yield
                                else:
                                    mm(out=pz[:, 0:128], lhsT=Bt[hs, tsl], rhs=AR[hs, sub, 0, :], start=True, stop=True)
                                    mm(out=pz[:, 256:384], lhsT=Kt[hs, tsl], rhs=AR[hs, sub, 0, :], start=True, stop=True)
                                    v4 = lambda ap_: ap_.re("p (a two b) -> p a two b", a=2, two=2)[:, :, 0, :]
                                    vec("tensor_tensor", out=v4(AM[:, u, :]), in0=v4(pz[:, :]),
                                        in1=v4(cb[:, CB_MT4:CB_MT4 + 512]), op=ALU.mult)
                                yield
                        for h in range(2):
                            hs = slice(h * 64, (h + 1) * 64)
                            for sub in range(NSUB):
                                u = h * NSUB + sub
                                tsl = slice(sub * 128, (sub + 1) * 128)
                                mm(out=B1[:, u * 128:(u + 1) * 128], lhsT=AR[hs, sub, 0, :], rhs=Bt[hs, tsl],
                                   start=True, stop=True)
                        vec("tensor_tensor", out=L0[:].re("p u t -> p (u t)"), in0=B1[:, :], in1=cb[:, CB_ML4:CB_ML4 + 512],
                            op=ALU.mult)
                        yield
                        CK('rwkv_b')
                        vec("tensor_tensor", out=SS[0][:], in0=AM[:, :, 0:128], in1=ident.bc(1, [128, 4, 128]), op=ALU.add)
                        yield
                        lt_prev = lambda u: AM[:, u, 0:128]
                        lp_prev = lambda u: L0[:, u, :]
                        scur = 0
                        for lev in range(1, 6):
                            lpn = LP[lev % 2]
                            ltn = LT[lev % 2]
                            for u in range(4):
                                mm(out=B1[:, u * 128:(u + 1) * 128], lhsT=lt_prev(u), rhs=lp_prev(u), start=True, stop=True)
                            if lev <= 4:
                                for u in range(4):
                                    mm(out=B2[:, u * 128:(u + 1) * 128], lhsT=lp_prev(u), rhs=lt_prev(u),
                                       start=True, stop=True)
                            cpy(out=lpn[:].re("p u t -> p (u t)"), in_=B1[:, :])
                            yield
                            if lev <= 4:
                                cpy(out=ltn[:].re("p u t -> p (u t)"), in_=B2[:, :])
                            yield
                            for u in range(4):
                                mm(out=B1[:, u * 128:(u + 1) * 128], lhsT=lpn[:, u, :], rhs=SS[scur][:, u, :], start=True, stop=True)
                            sdst = TTf if lev == 5 else SS[1 - scur]
                            vec("tensor_tensor", out=sdst[:].re("p u t -> p (u t)"), in0=B1[:, :],
                                in1=SS[scur][:].re("p u t -> p (u t)"), op=ALU.add)
                            yield
                            scur = 1 - scur
                            lt_prev = (lambda b: (lambda u: b[:, u, :]))(ltn)
                            yield
                            lp_prev = (lambda b: (lambda u: b[:, u, :]))(lpn)

                def rw_back(c0):
                    for c in (c0,):
                        AR = ARs[c % 3]; AM = AMs[c % 2]; Vtm = Vtms[c % 2]; Bpad = Bpads[c % 2]; Kpad = Kpads[c % 2]
                        TTf = TTfs[c % 2]; gb = gbs[c % 3]; pcs = pcss[c % 3]
                        Bt = Bts[c % 2]; Kt = Kts[c % 2]; vbf = vbfs[c % 2]
                        TTm = TTf
                        ysq = ysqB[:, :]
                        yln = ysqB[:, :]
                        CK('rwkv_c')
                        for q in range(2 * NSUB):
                            sub, half = q // 2, q % 2
                            ps_ = slice(half * 64, half * 64 + 64)
                            tsl = slice(sub * 128, (sub + 1) * 128)
                            zi = q % 2
                            for h in range(2):
                                hs = slice(h * 64, (h + 1) * 64)
                                u = h * NSUB + sub
                                mm(out=pC[:, hs], lhsT=AR[hs, sub, 0, :], rhs=Zb[hs, c, zi, :], start=True, stop=False)
                                mm(out=pC[:, hs], lhsT=AM[:, u, 256:384], rhs=Vtm[:, sub, hs], start=False, stop=True)
                            cpy(out=Xb[ps_, :], in_=pC[ps_, 0:128])
                            yield
                            CK('c1')
                            for h in range(2):
                                hs = slice(h * 64, (h + 1) * 64)
                                u = h * NSUB + sub
                                mm(out=pC[:, 128 + h * 64:128 + (h + 1) * 64], lhsT=TTm[ps_, u, :], rhs=Xb[ps_, hs],
                                   start=True, stop=True)
                            cpy(out=Ub[ps_, :], in_=pC[ps_, 128:256])
                            yield
                            CK('c2')
                            if phaseB:
                                for h in range(2):
                                    hs = slice(h * 64, (h + 1) * 64)
                                    u = h * NSUB + sub
                                    o_ = slice(256 + h * 64, 256 + (h + 1) * 64)
                                    mm(out=pC[:, o_], lhsT=AR[hs, sub, 1, :], rhs=Zb[hs, c, zi, :], start=True, stop=False)
                                    mm(out=pC[:, o_], lhsT=AM[:, u, 128:256], rhs=Ub[:, hs], start=False, stop=False)
                                    mm(out=pC[:, o_], lhsT=AM[:, u, 384:512], rhs=Vtm[:, sub, hs], start=False, stop=True)
                                cpy(out=Ytm[ps_, sub, :], in_=pC[ps_, 256:384])
                                yield
                            CK('c3')
                            for h in range(2):
                                hs = slice(h * 64, (h + 1) * 64)
                                mm(out=pC[:, 384:448], lhsT=Bpad[ps_, sub, h, :], rhs=Ub[ps_, hs], start=(h == 0), stop=False)
                                mm(out=pC[:, 384:448], lhsT=Kpad[ps_, sub, h, :], rhs=Vtm[ps_, sub, hs], start=False, stop=(h == 1))
                            CK('c4')
                            pcv = pcs[:, q:q + 1]
                            vec("tensor_scalar", out=ztmp[:, :], in0=Zf[:, c, :], scalar1=pcv, scalar2=None, op0=ALU.mult)
                            yield
                            vec("scalar_tensor_tensor", out=Zf[:, c, :], in0=pC[:, 384:448], scalar=pcv, in1=ztmp[:, :],
                                op0=ALU.mult, op1=ALU.add)
                            yield
                            cpy(out=Zb[:, c, 1 - zi, :], in_=Zf[:, c, :])
                            yield
                        CK('rwkv_d')
                        if phaseB:
                            yv = Ytm[:].re("p s (h i) -> p (s h) i", i=64)
                            vec("tensor_reduce", out=gst[:, 0:4], in_=yv, axis=AX.X, op=ALU.add)
                            yield
                            act(out=ysq[:, :], in_=Ytm[:].re("p s c -> p (s c)"), func=AF.Square)
                            yield
                            vec("tensor_reduce", out=gst[:, 4:8], in_=ysq[:, :].re("p (g i) -> p g i", i=64), axis=AX.X, op=ALU.add)
                            yield
                            vec("tensor_scalar", out=gst[:, 8:12], in0=gst[:, 0:4], scalar1=1.0 / 64, scalar2=None, op0=ALU.mult)
                            yield
                            vec("tensor_tensor", out=gst[:, 12:16], in0=gst[:, 8:12], in1=gst[:, 8:12], op=ALU.mult)
                            yield
                            vec("scalar_tensor_tensor", out=gst[:, 16:20], in0=gst[:, 4:8], scalar=1.0 / 64, in1=gst[:, 12:16],
                                op0=ALU.mult, op1=ALU.subtract)
                            yield
                            rsqrt_to(gst[:, 24:28], gst[:, 16:20], eps_g, 1.0)
                            yield
                            ysv = ysq[:, :].re("p (g i) -> p g i", i=64)
                            vec("tensor_tensor", out=ysv, in0=yv, in1=gst[:, 8:12].bc(2, [128, 4, 64]), op=ALU.subtract)
                            yield
                            vec("tensor_tensor", out=ynb[:].re("p s (h i) -> p (s h) i", i=64), in0=ysv,
                                in1=gst[:, 24:28].bc(2, [128, 4, 64]), op=ALU.mult)
                            yield
                            for sub in range(NSUB):
                                tr(out=trC[:, sub * 128:(sub + 1) * 128], in_=ynb[:, sub, :], identity=ident)
                            act(out=yln[:, :], in_=trC[:, 0:TT], func=AF.Identity, scale=pv(PV_LNW, c), bias=pv(PV_LNB, c))
                            yield
                            vec("tensor_tensor", out=yln[:, :], in0=yln[:, :], in1=gb[:, :], op=ALU.mult)
                            yield
                            vec("tensor_tensor", out=yfin[:, c, :], in0=yln[:, :], in1=yfin[:, c, :], op=ALU.add)
                            yield


                def th_attn():
                    if m < PB0 - 8:
                        return
                    j0_2 = m % 2
                    j0_3 = m % 8
                    for g in range(3):
                        kdst = (K1, K2c, K3c)[g]
                        for j in ((0, 1, 2) if phaseB else (1, 2)):
                            wa = wload(CH_AT + g * 3 + j, 'a')
                            wav = wa[:, :].re("p (k c) -> p k c", c=512)
                            for cc in range(4):
                                pz = nextp("a")
                                for kc in range(8):
                                    mm(out=pz[:, 0:TT], lhsT=wav[:, kc, cc * 128:(cc + 1) * 128], rhs=hT[:, kc, 1:TT + 1],
                                       start=(kc == 0), stop=(kc == 7))
                                if j == 0:
                                    act(out=Qa[g][:, cc, :], in_=pz[:, 0:TT], func=AF.Copy, scale=0.125)
                                    yield
                                elif j == 1:
                                    if g == 0:
                                        cpy(out=K1[:, cc, 128:128 + TT], in_=pz[:, 0:TT])
                                        yield
                                    else:
                                        cpy(out=kdst[:, cc, :], in_=pz[:, 0:TT])
                                        yield
                                else:
                                    cpy(out=VF[:, cc, :], in_=pz[:, 0:TT])
                                    yield
                            if not DENSE_ATTN_PROJ:
                                yield
                        if g == 0:
                            for blk in range(2):
                                for cc in range(4):
                                    tr(out=pTa[:, cc * 128:(cc + 1) * 128], in_=VF[:, cc, blk * 128:(blk + 1) * 128], identity=ident)
                                cpy(out=V1[:, 1 + blk, :], in_=pTa[:, 0:512])
                            yield
                        elif g == 1:
                            cpy(out=VF2[:], in_=VF[:])
                            yield
                    CK('attn_proj')
                    for h in range(8):
                        cc, hp = h // 2, (h % 2) * 64
                        hs = slice(hp, hp + 64)
                        vs = slice(h * 64, (h + 1) * 64)
                        vl = slice(hp, hp + 64)
                        if h % 2 == 0:
                            for r in range(4):
                                tr(out=pTa[0:64, r * 128:(r + 1) * 128],
                                   in_=VF2[:, cc, :].re("p (i r) -> p r i", r=4)[:, r, :], identity=ident)
                            cpy(out=V2c[0:64, :, :].re("p r c -> p (r c)"), in_=pTa[0:64, 0:512])
                            yield
                            for r in range(16):
                                tr(out=pTa[0:16, (r % 4) * 128:(r % 4 + 1) * 128],
                                   in_=VF[:, cc, :].re("p (i r) -> p r i", r=16)[:, r, :], identity=ident)
                                if r % 4 == 3:
                                    cpy(out=V3c[0:16, r - 3:r + 1, :].re("p a c -> p (a c)"), in_=pTa[0:16, 0:512])
                                yield
                        if not phaseB:
                            if h % 2 == 1:
                                S.dma("gpsimd", out=V2r[j0_2 * 64:(j0_2 + 1) * 64, :, cc * 128:(cc + 1) * 128], in_=V2c[0:64, :, :])
                                S.dma("gpsimd", out=V3r[j0_3 * 16:(j0_3 + 1) * 16, :, cc * 128:(cc + 1) * 128], in_=V3c[0:16, :, :])
                            continue
                        for blk in range(2):
                            qv = Qa[0][hs, cc, blk * 128:(blk + 1) * 128]
                            mm(out=B5[:, (blk * 2) * 128:(blk * 2 + 1) * 128], lhsT=K1[hs, cc, blk * 128:(blk + 1) * 128],
                               rhs=qv, start=True, stop=True)
                            mm(out=B5[:, (blk * 2 + 1) * 128:(blk * 2 + 2) * 128],
                               lhsT=K1[hs, cc, 128 + blk * 128:128 + (blk + 1) * 128], rhs=qv, start=True, stop=True)
                        act(out=pe[:, :], in_=B5[:, 0:512], func=AF.Exp)
                        yield
                        vec("tensor_tensor", out=pp_[:, :].re("p (b e) -> p b e", b=2), in0=pe[:, :].re("p (b e) -> p b e", b=2),
                            in1=cb[:, CB_E1 + h * 256:CB_E1 + (h + 1) * 256].bc(1, [128, 2, 256]), op=ALU.mult)
                        yield
                        vec("tensor_scalar", out=pp_[:, 0:128], in0=pp_[:, 0:128], scalar1=vprev, scalar2=None, op0=ALU.mult)
                        yield
                        for blk in range(2):
                            mm(out=B6[0:64, blk * 128:(blk + 1) * 128], lhsT=V1[:, blk, vs],
                               rhs=pp_[:, (blk * 2) * 128:(blk * 2 + 1) * 128], start=True, stop=False)
                            mm(out=B6[0:64, blk * 128:(blk + 1) * 128], lhsT=V1[:, blk + 1, vs],
                               rhs=pp_[:, (blk * 2 + 1) * 128:(blk * 2 + 2) * 128], start=False, stop=True)
                        ppv = pp_[:, :].re("p (b c q) -> p b c q", b=2, c=2)
                        mm(out=B6[0:64, 256:512], lhsT=cb[:, CB_ONES:CB_ONES + 64], rhs=ppv[:, :, 0, :], start=True, stop=False)
                        mm(out=B6[0:64, 256:512], lhsT=cb[:, CB_ONES:CB_ONES + 64], rhs=ppv[:, :, 1, :], start=False, stop=True)
                        cpy(out=accO[:, :], in_=B6[0:64, 0:TT])
                        yield
                        cpy(out=accD[:, :], in_=B6[0:64, 256:512])
                        yield
                        for r in range(4):
                            qv = Qa[1][hs, cc, :].re("p (i r) -> p r i", r=4)[:, r, :]
                            mm(out=B5[:, r * 64:(r + 1) * 64], lhsT=K2r[hs, cc, r, :], rhs=qv, start=True, stop=True)
                            mm(out=B5[0:64, 256 + r * 64:256 + (r + 1) * 64],
                               lhsT=K2c[hs, cc, :].re("p (i r) -> p r i", r=4)[:, r, :], rhs=qv, start=True, stop=True)
                        act(out=pe[:, 0:256], in_=B5[:, 0:256], func=AF.Exp)
                        yield
                        act(out=peb[0:64, :], in_=B5[0:64, 256:512], func=AF.Exp)
                        yield
                        ea = cb[:, CB_EA2 + (j0_2 * 8 + h) * 64:CB_EA2 + (j0_2 * 8 + h + 1) * 64]
                        vec("scalar_tensor_tensor", out=pp_[:, 0:256].re("p (r i) -> p r i", r=4),
                            in0=pe[:, 0:256].re("p (r i) -> p r i", r=4), scalar=vr2, in1=ea.bc(1, [128, 4, 64]),
                            op0=ALU.mult, op1=ALU.mult)
                        yield
                        eb = cb[0:64, CB_EB2 + h * 64:CB_EB2 + (h + 1) * 64]
                        vec("tensor_tensor", out=ppb[0:64, :].re("p (r i) -> p r i", r=4),
                            in0=peb[0:64, :].re("p (r i) -> p r i", r=4), in1=eb.bc(1, [64, 4, 64]), op=ALU.mult)
                        yield
                        for r in range(4):
                            mm(out=B6[0:64, r * 64:(r + 1) * 64], lhsT=V2r[:, r, vs], rhs=pp_[:, r * 64:(r + 1) * 64],
                               start=True, stop=False)
                            mm(out=B6[0:64, r * 64:(r + 1) * 64], lhsT=V2c[0:64, r, vl], rhs=ppb[0:64, r * 64:(r + 1) * 64],
                               start=False, stop=True)
                        mm(out=B6[0:64, 256:512], lhsT=cb[:, CB_ONES:CB_ONES + 64], rhs=pp_[:, 0:256], start=True, stop=False)
                        mm(out=B6[0:64, 256:512], lhsT=cb[0:64, CB_ONES:CB_ONES + 64], rhs=ppb[0:64, :], start=False, stop=True)
                        vec("tensor_tensor", out=accO[:, :].re("p (i r) -> p r i", r=4), in0=accO[:, :].re("p (i r) -> p r i", r=4),
                            in1=B6[0:64, 0:TT].re("p (r i) -> p r i", r=4), op=ALU.add)
                        yield
                        vec("tensor_tensor", out=accD[:, :].re("p (i r) -> p r i", r=4), in0=accD[:, :].re("p (i r) -> p r i", r=4),
                            in1=B6[0:64, 256:512].re("p (r i) -> p r i", r=4), op=ALU.add)
                        yield
                        for r in range(16):
                            qv = Qa[2][hs, cc, :].re("p (i r) -> p r i", r=16)[:, r, :]
                            mm(out=B5[:, r * 16:(r + 1) * 16], lhsT=K3r[hs, cc, r, :], rhs=qv, start=True, stop=True)
                            mm(out=B5[0:16, 256 + r * 16:256 + (r + 1) * 16],
                               lhsT=K3c[hs, cc, :].re("p (i r) -> p r i", r=16)[:, r, :], rhs=qv, start=True, stop=True)
                        act(out=pe[:, 0:256], in_=B5[:, 0:256], func=AF.Exp)
                        yield
                        act(out=peb[0:16, :], in_=B5[0:16, 256:512], func=AF.Exp)
                        yield
                        ea = cb[:, CB_EA3 + (j0_3 * 8 + h) * 16:CB_EA3 + (j0_3 * 8 + h + 1) * 16]
                        vec("scalar_tensor_tensor", out=pp_[:, 0:256].re("p (r i) -> p r i", r=16),
                            in0=pe[:, 0:256].re("p (r i) -> p r i", r=16), scalar=vr3, in1=ea.bc(1, [128, 16, 16]),
                            op0=ALU.mult, op1=ALU.mult)
                        yield
                        eb = cb[0:16, CB_EB3 + h * 16:CB_EB3 + (h + 1) * 16]
                        vec("tensor_tensor", out=ppb[0:16, :].re("p (r i) -> p r i", r=16),
                            in0=peb[0:16, :].re("p (r i) -> p r i", r=16), in1=eb.bc(1, [16, 16, 16]), op=ALU.mult)
                        yield
                        for r in range(16):
                            mm(out=B6[0:64, r * 16:(r + 1) * 16], lhsT=V3r[:, r, vs], rhs=pp_[:, r * 16:(r + 1) * 16],
                               start=True, stop=False)
                            mm(out=B6[0:64, r * 16:(r + 1) * 16], lhsT=V3c[0:16, r, vl], rhs=ppb[0:16, r * 16:(r + 1) * 16],
                               start=False, stop=True)
                        mm(out=B6[0:64, 256:512], lhsT=cb[:, CB_ONES:CB_ONES + 64], rhs=pp_[:, 0:256], start=True, stop=False)
                        mm(out=B6[0:64, 256:512], lhsT=cb[0:16, CB_ONES:CB_ONES + 64], rhs=ppb[0:16, :], start=False, stop=True)
                        yield
                        if phaseB:
                            vec("tensor_tensor", out=accO[:, :].re("p (i r) -> p r i", r=16),
                                in0=accO[:, :].re("p (i r) -> p r i", r=16),
                                in1=B6[0:64, 0:TT].re("p (r i) -> p r i", r=16), op=ALU.add)
                            yield
                            vec("tensor_tensor", out=accD[:, :].re("p (i r) -> p r i", r=16),
                                in0=accD[:, :].re("p (i r) -> p r i", r=16),
                                in1=B6[0:64, 256:512].re("p (r i) -> p r i", r=16), op=ALU.add)
                            yield
                            vec("reciprocal", out=accD[:, :], in_=accD[:, :])
                            yield
                            vec("tensor_tensor", out=oT[:, h, :], in0=accO[:, :], in1=accD[:, :], op=ALU.mult)
                            yield
                        if h % 2 == 1:
                            S.dma("gpsimd", out=V2r[j0_2 * 64:(j0_2 + 1) * 64, :, cc * 128:(cc + 1) * 128], in_=V2c[0:64, :, :])
                            S.dma("gpsimd", out=V3r[j0_3 * 16:(j0_3 + 1) * 16, :, cc * 128:(cc + 1) * 128], in_=V3c[0:16, :, :])
                    CK('attn')
                    cpy(out=K1[:, :, 0:128], in_=K1[:, :, TT:TT + 128])
                    yield
                    cpy(out=V1[:, 0, :], in_=V1[:, 2, :])
                    yield
                    cpy(out=K2r[:, :, :, j0_2 * 64:(j0_2 + 1) * 64],
                        in_=K2c[:].re("p c (i r) -> p c r i", r=4))
                    yield
                    cpy(out=K3r[:, :, :, j0_3 * 16:(j0_3 + 1) * 16],
                        in_=K3c[:].re("p c (i r) -> p c r i", r=16))

                ag = th_attn()
                ag_done = [False]

                def step_attn():
                    if ag_done[0]:
                        return
                    try:
                        next(ag)
                    except StopIteration:
                        ag_done[0] = True

                rr_cnt = [0]
                clk = {}

                def advance(g_):
                    S.cur_fin = 0.0
                    try:
                        next(g_)
                    except StopIteration:
                        return False
                    if S.cur_fin > 0.0:
                        clk[id(g_)] = S.cur_fin
                    return True

                def run_ls(gens):
                    gens = list(gens)
                    for g_ in gens:
                        clk.setdefault(id(g_), 0.0)
                    clk.setdefault(id(ag), 0.0)
                    while gens:
                        cands = gens + ([] if ag_done[0] else [ag])
                        g_ = min(cands, key=lambda x: clk[id(x)])
                        if g_ is ag:
                            step_attn_ls()
                        elif not advance(g_):
                            gens.remove(g_)

                def step_attn_ls():
                    if not advance(ag):
                        ag_done[0] = True

                def run_rr(gens):
                    if LISTSCHED and INTERLEAVE:
                        return run_ls(gens)
                    gens = list(gens)
                    while gens:
                        for g_ in list(gens):
                            try:
                                next(g_)
                            except StopIteration:
                                gens.remove(g_)
                        rr_cnt[0] += 1
                        if INTERLEAVE and rr_cnt[0] % ATTN_EVERY == 0:
                            step_attn()

                if LISTSCHED and INTERLEAVE:
                    done = {"f1": 0, "f2": 0, "bk": 0}
                    mk = {"f1": rw_f1, "f2": rw_f2, "bk": rw_back}
                    cur = {"f1": None, "f2": None, "bk": None}
                    sclk = {"f1": 0.0, "f2": 0.0, "bk": 0.0, "at": clk.get(id(ag), 0.0), "s1": 0.0}
                    s1g = [None, False]
                    want_s1 = S1_PREFETCH and (not phaseB) and (m + 1 < nt)

                    def can_start(k, c):
                        if k == "f1":
                            return done["f2"] >= c - 1 and done["bk"] >= c - 2
                        if k == "f2":
                            return done["f1"] >= c + 1 and done["bk"] >= c - 1
                        return done["f2"] >= c + 1

                    while True:
                        cands = []
                        for k in ("f1", "f2", "bk"):
                            if cur[k] is None and done[k] < 8 and can_start(k, done[k]):
                                cur[k] = mk[k](done[k])
                            if cur[k] is not None:
                                cands.append(k)
                        if not ag_done[0]:
                            cands.append("at")
                        if want_s1 and not s1g[1] and done["f1"] == 8 and ag_done[0]:
                            if s1g[0] is None:
                                prot["s"] = [_b0, _b4, B5, B6]
                                s1g[0] = stage1_gen(m + 1)
                                s1_done.add(m + 1)
                                sclk["s1"] = max(sclk["f1"], sclk["at"])
                            cands.append("s1")
                        if not cands:
                            break
                        k = min(cands, key=lambda x: sclk[x])
                        S.cur_fin = 0.0
                        if k == "at":
                            step_attn()
                        elif k == "s1":
                            try:
                                next(s1g[0])
                            except StopIteration:
                                s1g[1] = True
                                prot["s"] = [_b0, _b4, B1, B2, B3, B6]
                        else:
                            try:
                                next(cur[k])
                            except StopIteration:
                                cur[k] = None
                                done[k] += 1
                        if S.cur_fin > 0.0:
                            sclk[k] = S.cur_fin
                    assert done == {"f1": 8, "f2": 8, "bk": 8}, done
                else:
                    for k_ in range(10):
                        gens = []
                        if k_ < 8:
                            gens.append(rw_f1(k_))
                        if 1 <= k_ <= 8:
                            gens.append(rw_f2(k_ - 1))
                        if k_ >= 2:
                            gens.append(rw_back(k_ - 2))
                        if INTERLEAVE:
                            run_rr(gens)
                        else:
                            for g_ in reversed(gens):
                                for _ in g_:
                                    pass
                while not ag_done[0]:
                    step_attn()

                if not phaseB:
                    continue
                for cc in range(8):
                    sa, sbb = ((tv(2), tv(3)), (tv(5), tv(6)))[cc % 2]
                    wpb = wload(CH_PB + cc)
                    wv = wpb[:, :].re("p (a k c) -> p a k c", a=4, c=128)
                    pz = nextp()
                    for kc in range(8):
                        mm(out=pz[:, 0:TT], lhsT=wv[:, 0, kc, :], rhs=hT[:, kc, 1:TT + 1], start=(kc == 0), stop=(kc == 7))
                    sigmoid_to(sa[:, :], pz[:, 0:TT])
                    pz = nextp()
                    for kc in range(8):
                        mm(out=pz[:, 0:TT], lhsT=wv[:, 1, kc, :], rhs=hT[:, kc, 1:TT + 1], start=(kc == 0), stop=(kc == 7))
                    sigmoid_to(sbb[:, :], pz[:, 0:TT])
                    pz = nextp()
                    for kc in range(8):
                        mm(out=pz[:, 0:TT], lhsT=wv[:, 2, kc, :], rhs=yfin[:, kc, :], start=(kc == 0), stop=(kc == 7))
                    vec("tensor_tensor", out=sa[:, :], in0=sa[:, :], in1=pz[:, 0:TT], op=ALU.mult)
                    pz = nextp()
                    for hh_ in range(8):
                        mm(out=pz[:, 0:TT], lhsT=wv[0:64, 3, hh_, :], rhs=oT[:, hh_, :], start=(hh_ == 0), stop=(hh_ == 7))
                    vec("tensor_tensor", out=sbb[:, :], in0=sbb[:, :], in1=pz[:, 0:TT], op=ALU.mult)
                    vec("tensor_tensor", out=mixT[:, cc, :], in0=sa[:, :], in1=sbb[:, :], op=ALU.add)

                def norm_residual(ps_views, gb):
                    for hf in range(2):
                        act(out=junk[:, 0:512], in_=ps_views[hf], func=AF.Square, accum_out=st4[:, 8 + hf:9 + hf])
                    vec("tensor_tensor", out=st4[:, 10:11], in0=st4[:, 8:9], in1=st4[:, 9:10], op=ALU.add)
                    rsqrt_to(st4[:, 4:5], st4[:, 10:11], eps_r, 1.0 / D)
                    for hf in range(2):
                        for qq in range(2):
                            cs_ = slice(hf * 512 + qq * 256, hf * 512 + (qq + 1) * 256)
                            vec("scalar_tensor_tensor", out=utmp[:, :], in0=ps_views[hf][:, qq * 256:(qq + 1) * 256],
                                scalar=st4[:, 4:5], in1=gb[:, cs_], op0=ALU.mult, op1=ALU.mult)
                            vec("tensor_tensor", out=xm[:, sub, cs_], in0=xm[:, sub, cs_], in1=utmp[:, :], op=ALU.add)

                wo = [wload(CH_WOUT + 0), wload(CH_WOUT + 1)]
                for sub in range(NSUB):
                    for hf in range(2):
                        wv = wo[hf][:, :].re("p (k c) -> p k c", c=512)
                        for kc in range(8):
                            mm(out=(B1, B2)[hf][:, :], lhsT=mixT[:, kc, sub * 128:(sub + 1) * 128], rhs=wv[:, kc, :],
                               start=(kc == 0), stop=(kc == 7))
                    norm_residual([B1[:, :], B2[:, :]], gmb)
                rms_rstd(xm, 2)
                norm_transpose(xm, 2, h2T, PD_A2, lambda kc: modf[:, 24 + kc:25 + kc], 0)
                accs = [[B1[:, :], B2[:, :]], [B3[:, :], B5[:, :]]]
                for pg in range(3):
                    nk = 8 if pg < 2 else 6
                    for i4 in range(nk // 2):
                        i = pg * 4 + i4
                        wf_ = wload(CH_FF + i)
                        wv = wf_[:, :].re("p (k c) -> p k c", c=512)
                        for jj in range(2):
                            jl = i4 * 2 + jj
                            pg_ = nextp("f")
                            for kc in range(8):
                                mm(out=pg_[:, 0:TT], lhsT=wv[:, kc, jj * 128:(jj + 1) * 128], rhs=h2T[:, kc, :],
                                   start=(kc == 0), stop=(kc == 7))
                            act(out=sg[:, :], in_=pg_[:, 0:TT], func=AF.Silu)
                            pu = nextp("f")
                            for kc in range(8):
                                mm(out=pu[:, 0:TT], lhsT=wv[:, kc, 256 + jj * 128:256 + (jj + 1) * 128], rhs=h2T[:, kc, :],
                                   start=(kc == 0), stop=(kc == 7))
                            vec("tensor_tensor", out=actT[:, jl, :], in0=sg[:, :], in1=pu[:, 0:TT], op=ALU.mult)
                    for hf in range(2):
                        wf_ = wload(CH_FO + pg * 2 + hf)
                        wv = wf_[:, :].re("p (k c) -> p k c", c=512)
                        for sub in range(NSUB):
                            for kc in range(nk):
                                mm(out=accs[sub][hf], lhsT=actT[:, kc, sub * 128:(sub + 1) * 128], rhs=wv[:, kc, :],
                                   start=(pg == 0 and kc == 0), stop=(pg == 2 and kc == nk - 1))
                for sub in range(NSUB):
                    norm_residual(accs[sub], gfb)
                r0 = (m - PB0) * TT
                S.dma("gpsimd", out=y_d.v(y_d.t[r0:r0 + TT, :].rearrange("(s p) c -> p s c", p=128)), in_=xm[:])


        except _Stop:
            pass
        S.finish([y_d] + finals)
        S.emit()
    return nc


_CACHE = {}


def prep_inputs(x, c, w_mod, b_mod, g_pre_mix, g_post_mix, g_pre_ffn, g_post_ffn, w_in, mu_rkv, mu_lora,
           w0, w1, w2, a0, a1, a2, g1, g2, k_k, k_a, r_k, ln_x_w, ln_x_b, w_o_rwkv, w_o_attn, w_out,
           w_ffn_in, w_ffn_out):
    f = lambda a: np.asarray(a, np.float32)
    x = f(x); c = f(c)
    w_in = f(w_in)[0]; w_modm = f(w_mod)[0]
    bm = f(b_mod)[0].reshape(6, 1024)
    vecs = [bm[0], bm[1], bm[2], bm[3], bm[4], bm[5], f(g_pre_mix)[0], f(g_post_mix)[0], f(g_pre_ffn)[0],
            f(g_post_ffn)[0], f(mu_rkv)[0, 0], f(mu_rkv)[0, 1], f(mu_rkv)[0, 2], f(mu_lora)[0, 0], f(mu_lora)[0, 1],
            f(mu_lora)[0, 2], f(w0)[0], f(a0)[0], f(k_k)[0], f(k_a)[0], f(r_k)[0].reshape(-1), f(ln_x_w)[0],
            f(ln_x_b)[0]]
    wsrc = np.zeros((NCH, 128, 4096), np.float32)
    def put(i, arr3):
        P, K, C = arr3.shape
        v = wsrc[i].reshape(128, -1)
        tmp = np.zeros((128, K, 4096 // K if K in (8,) else C), np.float32) if False else None
        blk = np.zeros((128, K * C), np.float32)
        blk[:P] = arr3.reshape(P, K * C)
        v[:, :K * C] = blk
    for cch in range(8):
        a = np.zeros((128, 8, 512), np.float32)
        for j in range(3):
            a[:, :, j * 128:(j + 1) * 128] = _wchunk(w_in, slice(j * 1024 + cch * 128, j * 1024 + (cch + 1) * 128))
        put(CH_RW + cch, a)
    for g in range(3):
        for j in range(3):
            o = 3072 + j * 1536 + g * 512
            put(CH_AT + g * 3 + j, _wchunk(w_in, slice(o, o + 512)))
    wor = f(w_o_rwkv)[0]; woa = f(w_o_attn)[0]; wout = f(w_out)[0]
    for cc in range(8):
        cs_ = slice(cc * 128, (cc + 1) * 128)
        a = np.zeros((128, 4, 8, 128), np.float32)
        a[:, 0] = _wchunk(w_in, slice(7680 + cc * 128, 7680 + (cc + 1) * 128))
        a[:, 1] = _wchunk(w_in, slice(8704 + cc * 128, 8704 + (cc + 1) * 128))
        a[:, 2] = _wchunk(wor, cs_)
        a[0:64, 3] = woa[:, cs_].reshape(8, 64, 128).transpose(1, 0, 2)
        put(CH_PB + cc, a.reshape(128, 32, 128))
    for hf in range(2):
        put(CH_WOUT + hf, _wchunk(wout, slice(hf * 512, (hf + 1) * 512)))
    wfi = f(w_ffn_in)[0]; wfo = f(w_ffn_out)[0]
    for i in range(11):
        a = np.zeros((128, 8, 512), np.float32)
        a[:, :, 0:256] = _wchunk(wfi, slice(i * 256, (i + 1) * 256))
        a[:, :, 256:512] = _wchunk(wfi, slice(FH + i * 256, FH + (i + 1) * 256))
        put(CH_FF + i, a)
    for pg in range(3):
        nk = 8 if pg < 2 else 6
        for hf in range(2):
            blk = wfo[pg * 1024:pg * 1024 + nk * 128, hf * 512:(hf + 1) * 512]
            put(CH_FO + pg * 2 + hf, blk.reshape(nk, 128, 512).transpose(1, 0, 2))
    wmod = np.ascontiguousarray(
        w_modm.reshape(8, 128, 24, 256).transpose(2, 1, 0, 3).reshape(24, 128, 2048))
    l1 = np.concatenate([f(w1)[0], f(a1)[0], f(g1)[0]], 1)
    l1 = np.ascontiguousarray(l1.reshape(8, 128, 288).transpose(1, 0, 2).reshape(128, 8 * 288))
    l2 = np.zeros((128, 3, 1024), np.float32)
    l2[0:64, 0] = f(w2)[0]; l2[64:128, 0] = f(a2)[0]
    l2[:, 1] = f(g2)[0][0:128]; l2[0:32, 2] = f(g2)[0][128:160]
    l2 = l2.reshape(128, 3072)
    cbt = _host_consts()
    in_maps = []
    for core in range(8):
        b, hh = core // 2, core % 2
        pfm = np.concatenate([_fm(v) for v in vecs] + [_fm(c[b])], 1)
        if hh == 1:
            xvv = x[b]
        else:
            xvv = np.concatenate([np.zeros((T // 2, D), np.float32), x[b, :T // 2]], 0)
        in_maps.append({"xv": np.ascontiguousarray(xvv), "pfm": np.ascontiguousarray(pfm), "wmod": wmod,
                        "wsrc": wsrc, "l1": l1, "l2": l2, "cbt": cbt, "cft": _host_cf(hh)})
    return in_maps


def kernel(**inputs):
    in_maps = prep_inputs(**inputs)
    if "nc" not in _CACHE:
        _CACHE["nc"] = build()
    nc = _CACHE["nc"]
    res = run_bass_kernel_spmd(nc, in_maps, core_ids=list(range(8)))
    out = np.zeros((4, T, D), np.float32)
    for core in range(8):
        b, hh = core // 2, core % 2
        out[b, hh * (T // 2):(hh + 1) * (T // 2)] = res.results[core]["y"]
    return out
```

```python
import math
from contextlib import ExitStack

import numpy as np
import concourse.bass as bass
import concourse.mybir as mybir
from concourse.bass_utils import run_bass_kernel_spmd

F32 = mybir.dt.float32
BF16 = mybir.dt.bfloat16
AF = mybir.ActivationFunctionType
ALU = mybir.AluOpType
AX = mybir.AxisListType

ENGS = ("tensor", "vector", "scalar", "gpsimd", "sync")

T = 8192
D = 1024
TT = 256
NT = T // TT
PB0 = NT // 2
NSUB = TT // 128
FH = 2816
C0 = math.exp(-0.5)
GN_EPS = 64e-5
RMS_EPS = 1e-6
NSLOT = 4
import os
INTERLEAVE = os.environ.get('NOIL') is None
ATTN_EVERY = int(os.environ.get('ATTN_EVERY', '3'))
GPS_OFF = os.environ.get('GPS_OFF', '0') == '1'
LISTSCHED = os.environ.get('LISTSCHED', '1') == '1'
DENSE_ATTN_PROJ = os.environ.get('DENSE_ATTN_PROJ', '0') == '1'
S1_PREFETCH = os.environ.get('S1_PREFETCH', '1') == '1'
DYN_COPY = os.environ.get('DYN_COPY', '1') == '1'
SEM_LAT = float(os.environ.get('SEM_LAT', '300'))
PE_SCALE = float(os.environ.get('PE_SCALE', '1.0'))
ACT_SCALE = float(os.environ.get('ACT_SCALE', '0.8'))
DVE_SCALE = float(os.environ.get('DVE_SCALE', '1.25'))


class Buf:
    def __init__(self, name, t):
        self.name = name
        self.t = t
        self.writer = None
        self.readers = []
        self.dsem = None
        self.dcnt = 0
        self.psum = False

    def __getitem__(self, idx):
        return View(self, self.t[idx])

    def v(self, ap):
        return View(self, ap)


class SubBuf:
    def __init__(self, buf, col0, ncols=None):
        self.buf = buf
        self.col0 = col0
        self.ncols = ncols

    def __getitem__(self, idx):
        ps, cs = idx
        a = 0 if cs.start is None else cs.start
        e = cs.stop if cs.stop is not None else self.ncols
        assert e is not None
        return View(self.buf, self.buf.t[ps, self.col0 + a:self.col0 + e])


class View:
    def __init__(self, buf, ap):
        self.buf = buf
        self.ap = ap

    def __getitem__(self, idx):
        return View(self.buf, self.ap[idx])

    def re(self, pat, **kw):
        return View(self.buf, self.ap.rearrange(pat, **kw))

    def bc(self, axis, shape):
        return View(self.buf, self.ap.unsqueeze(axis).to_broadcast(list(shape)))


def _unw(x):
    return x.ap if isinstance(x, View) else x


class Sched:
    def __init__(self, nc, stack):
        self.nc = nc
        self.stack = stack
        self.q = {e: [] for e in ENGS}
        self.waited = {e: {} for e in ENGS}
        self.dma_sems = []
        self.fin = {}
        self.eng_free = {e: 0.0 for e in ENGS}
        self.cur_fin = 0.0

    def _est(self, eng, tok, deps_tokens, dur):
        ready = 0.0
        for t_ in deps_tokens:
            f_ = self.fin.get(t_)
            if f_ is not None and f_ > ready:
                ready = f_
        start = max(ready + SEM_LAT, self.eng_free[eng])
        fin = start + dur
        self.eng_free[eng] = fin if eng != "sync" and eng != "gpsimd" else start + 60.0
        self.fin[tok] = fin
        if len(self.fin) > 60000:
            ks = list(self.fin.keys())[:30000]
            for k_ in ks:
                del self.fin[k_]
        if fin > self.cur_fin:
            self.cur_fin = fin

    def sb(self, name, shape, dt):
        t = self.stack.enter_context(self.nc.sbuf_tensor("s_" + name, list(shape), dt))
        return Buf(name, t)

    def ps(self, name, shape, dt=F32):
        t = self.stack.enter_context(self.nc.psum_tensor("p_" + name, list(shape), dt))
        return Buf(name, t)

    def dram(self, name, shape, dt, kind):
        t = self.nc.dram_tensor(name, list(shape), dt, kind=kind).ap()
        return Buf(name, t)

    def _deps(self, eng, reads, writes):
        deps = {}

        def add(tok):
            if tok is None:
                return
            k, v = tok
            if deps.get(k, 0) < v:
                deps[k] = v

        for b in reads:
            add(b.writer)
            if b.psum:
                for r in b.readers:
                    if r[0] != eng:
                        add(r)
        for b in writes:
            add(b.writer)
            for r in b.readers:
                add(r)
        waits = []
        for k, v in deps.items():
            if k == "tensor" and eng == "tensor":
                continue
            if self.waited[eng].get(k, 0) >= v:
                continue
            self.waited[eng][k] = v
            waits.append((k, v))
            if isinstance(k, str):
                self.q[k][v - 1][2] = True
        return waits

    def _commit(self, tok, reads, writes):
        for b in writes:
            b.writer = tok
            b.readers = []
        for b in reads:
            if b in writes:
                continue
            b.readers.append(tok)
            if len(b.readers) > 48:
                d = {}
                for k, v in b.readers:
                    if d.get(k, 0) < v:
                        d[k] = v
                b.readers = list(d.items())

    def op(self, eng, meth, **kw):
        writes, reads = [], []
        for k, v in kw.items():
            if isinstance(v, View):
                if k in ("out", "accum_out", "ap"):
                    if v.buf not in writes:
                        writes.append(v.buf)
                else:
                    if v.buf not in reads:
                        reads.append(v.buf)
        dep_toks = [b_.writer for b_ in reads + writes if b_.writer is not None]
        for b_ in writes:
            dep_toks.extend(b_.readers)
        waits = self._deps(eng, reads, writes)
        if eng == "tensor":
            src = kw.get("lhsT", kw.get("in_"))
            lo = src.ap.base_partition()
            rows = (lo, lo + src.ap.partition_size())
            ob = kw["out"].buf
            prev = getattr(ob, "pe_rows", None)
            if prev is not None and ob.writer is not None and ob.writer[0] == "tensor" and \
                    (rows[1] <= prev[0] or prev[1] <= rows[0]):
                k, v = ob.writer
                if self.waited[eng].get(k, 0) < v:
                    self.waited[eng][k] = v
                    waits.append((k, v))
                    self.q[k][v - 1][2] = True
            ob.pe_rows = rows
        args = {k: _unw(v) for k, v in kw.items()}
        fn = lambda e, m=meth, a=args: getattr(e, m)(**a)
        self.q[eng].append([waits, fn, False, None])
        tok = (eng, len(self.q[eng]))
        o_ = kw.get("out", kw.get("ap"))
        try:
            fsz = o_.ap.free_size()
        except Exception:
            fsz = 256
        if eng == "tensor":
            n_ = kw["rhs"].ap.free_size() if "rhs" in kw else 128
            dur = (max(n_, 64) / 1.2 + 30.0) * PE_SCALE
        else:
            dur = (200.0 + 0.65 * fsz) * ACT_SCALE if eng == "scalar" else (150.0 + 0.75 * fsz) * DVE_SCALE
        self._est(eng, tok, dep_toks, dur)
        self._commit(tok, reads, writes)
        return tok

    def dma(self, eng, out, in_, **kw):
        sb = out.buf
        if sb.dsem is None:
            sb.dsem = ("dma", len(self.dma_sems))
            self.dma_sems.append(sb.name)
        dep_toks = [b_.writer for b_ in (in_.buf, out.buf) if b_.writer is not None] + list(out.buf.readers)
        waits = self._deps(eng, [in_.buf], [out.buf])
        sb.dcnt += 16
        tok = (sb.dsem, sb.dcnt)
        try:
            nbytes = out.ap.nbytes()
        except Exception:
            nbytes = 1 << 20
        self._est(eng, tok, dep_toks, 2500.0 + nbytes / 150.0)
        a = dict(out=out.ap, in_=in_.ap, **kw)
        fn = lambda e, a=a: e.dma_start(**a)
        self.q[eng].append([waits, fn, False, sb.dsem])
        self._commit(tok, [in_.buf], [out.buf])
        return tok

    def finish(self, final_bufs):
        waits = self._deps("sync", final_bufs, [])
        self.q["sync"].append([waits, None, False, None])

    def emit(self):
        nc = self.nc
        st = self.stack
        esem = {e: st.enter_context(nc.semaphore("es_" + e)) for e in ENGS}
        dsem = [st.enter_context(nc.semaphore("ds%d" % i)) for i in range(len(self.dma_sems))]
        cum = {}
        for e in ENGS:
            c = 0
            arr = []
            for it in self.q[e]:
                if it[2]:
                    c += 1
                arr.append(c)
            cum[e] = arr

        def semval(k, v):
            if isinstance(k, str):
                return esem[k], cum[k][v - 1]
            return dsem[k[1]], v

        block = st.enter_context(nc.Block())

        def run(e, eng):
            for waits, fn, sig, dk in self.q[e]:
                for k, v in waits:
                    s, val = semval(k, v)
                    eng.wait_ge(s, val)
                if fn is None:
                    continue
                ins = fn(eng)
                if dk is not None:
                    ins.then_inc(dsem[dk[1]], 16)
                elif sig:
                    ins.then_inc(esem[e], 1)

        @block.tensor
        def _(eng):
            run("tensor", eng)

        @block.vector
        def _(eng):
            run("vector", eng)

        @block.scalar
        def _(eng):
            run("scalar", eng)

        @block.gpsimd
        def _(eng):
            run("gpsimd", eng)

        @block.sync
        def _(eng):
            run("sync", eng)


def _alibi_slopes(n):
    def pow2(m):
        start = 2.0 ** (-8.0 / m)
        return [start ** (i + 1) for i in range(m)]
    if math.log2(n).is_integer():
        s = pow2(n)
    else:
        p = 2 ** int(math.floor(math.log2(n)))
        s = pow2(p) + pow2(2 * p)[0::2][: n - p]
    return sorted(s, reverse=True)


(PV_SHM, PV_SCM, PV_GTM, PV_SHF, PV_SCF, PV_GTF, PV_GPM, PV_GQM, PV_GPF, PV_GQF,
 PV_MUR, PV_MUK, PV_MUV, PV_MUW, PV_MUA, PV_MUG, PV_W0, PV_A0, PV_KK, PV_KA, PV_RK,
 PV_LNW, PV_LNB, PV_C) = range(24)
NPV = 24

CB_ID = 0
CB_ONESBD = 128
CB_ONES = 256
CB_MT4 = 320
CB_ML4 = 832
CB_E1 = 1344
CB_EA2 = CB_E1 + 8 * 256
CB_EB2 = CB_EA2 + 2 * 8 * 64
CB_EA3 = CB_EB2 + 8 * 64
CB_EB3 = CB_EA3 + 8 * 8 * 16
NCB = CB_EB3 + 8 * 16
CF_MSK = 0
CF_VM = 256
CF_EPS = CF_VM + 128
CF_IDF = CF_EPS + 4
NCF = CF_IDF + 128

CH_RW = 0
CH_AT = 8
CH_PB = 17
CH_WOUT = 25
CH_FF = 27
CH_FO = 38
NCH = 44


def _host_consts():
    sl = np.asarray(_alibi_slopes(24), np.float64).reshape(3, 8)
    cb = np.zeros((128, NCB), np.float32)
    p = np.arange(128)
    cb[:, CB_ID:CB_ID + 128] = np.eye(128)
    cb[:, CB_ONESBD:CB_ONESBD + 128] = (p[:, None] // 64 == p[None, :] // 64)
    cb[:, CB_ONES:CB_ONES + 64] = 1.0
    same = (p[:, None] // 64 == p[None, :] // 64)
    su = same & (p[:, None] < p[None, :])
    iu = same & (p[:, None] <= p[None, :])
    slo = same & (p[:, None] > p[None, :])
    cb[:, CB_MT4:CB_MT4 + 512] = np.concatenate([su, iu, su, iu], 1)
    cb[:, CB_ML4:CB_ML4 + 512] = np.concatenate([slo] * 4, 1)
    k = p[:, None].astype(np.float64)
    q = p[None, :].astype(np.float64)
    for h in range(8):
        dpv = q - k + 128
        e_prev = np.where(dpv <= 128, np.exp(-sl[0, h] * dpv), 0.0)
        dcu = q - k
        e_cur = np.where(dcu >= 0, np.exp(-sl[0, h] * np.maximum(dcu, 0)), 0.0)
        cb[:, CB_E1 + h * 256: CB_E1 + h * 256 + 128] = e_prev
        cb[:, CB_E1 + h * 256 + 128: CB_E1 + h * 256 + 256] = e_cur
    i64 = np.arange(64)[None, :].astype(np.float64)
    for rot in range(2):
        for h in range(8):
            j = p // 64
            pp = (p % 64).astype(np.float64)
            a = ((rot - j - 1) % 2) + 1
            dl = 64.0 * a[:, None] + i64 - pp[:, None]
            e = np.where(dl <= 128, np.exp(-sl[1, h] * 4.0 * dl), 0.0)
            o = CB_EA2 + (rot * 8 + h) * 64
            cb[:, o:o + 64] = e
    for h in range(8):
        kk = np.arange(64)[:, None].astype(np.float64)
        dl = i64 - kk
        e = np.where(dl >= 0, np.exp(-sl[1, h] * 4.0 * np.maximum(dl, 0)), 0.0)
        o = CB_EB2 + h * 64
        cb[0:64, o:o + 64] = e
    i16 = np.arange(16)[None, :].astype(np.float64)
    for rot in range(8):
        for h in range(8):
            j = p // 16
            pp = (p % 16).astype(np.float64)
            a = ((rot - j - 1) % 8) + 1
            dl = 16.0 * a[:, None] + i16 - pp[:, None]
            e = np.where(dl <= 128, np.exp(-sl[2, h] * 16.0 * dl), 0.0)
            o = CB_EA3 + (rot * 8 + h) * 16
            cb[:, o:o + 16] = e
    for h in range(8):
        kk = np.arange(16)[:, None].astype(np.float64)
        dl = i16 - kk
        e = np.where(dl >= 0, np.exp(-sl[2, h] * 16.0 * np.maximum(dl, 0)), 0.0)
        o = CB_EB3 + h * 16
        cb[0:16, o:o + 16] = e
    return cb


def _host_cf(hh):
    cf = np.zeros((128, NCF), np.float32)
    m = np.ones((128, 256), np.float32)
    m[:, 0::64] = 0.0
    cf[:, CF_MSK:CF_MSK + 256] = m
    valid = lambda t: 0.0 if t < 0 else (1.0 if (hh == 1 or t >= PB0) else 0.0)
    p = np.arange(128)
    for t in range(NT):
        cf[:, CF_VM + t] = valid(t)
        cf[:, CF_VM + 32 + t] = valid(t - 1)
        j = p // 64
        a = ((t - j - 1) % 2) + 1
        cf[:, CF_VM + 64 + t] = [valid(t - aa) for aa in a]
        j = p // 16
        a = ((t - j - 1) % 8) + 1
        cf[:, CF_VM + 96 + t] = [valid(t - aa) for aa in a]
    cf[:, CF_EPS] = RMS_EPS
    cf[:, CF_EPS + 1] = GN_EPS
    cf[:, CF_EPS + 3] = 1.0
    cf[:, CF_IDF:CF_IDF + 128] = np.eye(128)
    return cf


def _fm(v):
    return np.ascontiguousarray(v.reshape(8, 128).T)


def _wchunk(w, cols):
    return w[:, cols].reshape(8, 128, -1).transpose(1, 0, 2)


class _Stop(Exception):
    pass


def build(nt=NT, dbg=None, dbg_tile=0, dbg_c=0, stop=None):
    nc = bass.Bass("TRN2", target_bir_lowering=False)
    with ExitStack() as st:
        S = Sched(nc, st)
        finals = []

        def CK(name):
            if stop == name:
                raise _Stop()

        def DBG(name, view, m=None, c=None):
            if not dbg or name not in dbg:
                return
            if m is not None and m != dbg_tile:
                return
            if c is not None and c != dbg_c:
                return
            shp = list(view.ap.shape)
            dd = S.dram("dbg_" + name, shp, view.ap.dtype, "ExternalOutput")
            S.dma("gpsimd", out=dd[:], in_=view)
            finals.append(dd)
        xv = S.dram("xv", [T, D], F32, "ExternalInput")
        pfm_d = S.dram("pfm", [128, NPV * 8], F32, "ExternalInput")
        wmod_d = S.dram("wmod", [24, 128, 2048], F32, "ExternalInput")
        wsrc = S.dram("wsrc", [NCH, 128, 4096], F32, "ExternalInput")
        l1_d = S.dram("l1", [128, 8 * 288], F32, "ExternalInput")
        l2_d = S.dram("l2", [128, 3 * 1024], F32, "ExternalInput")
        cb_d = S.dram("cbt", [128, NCB], F32, "ExternalInput")
        cf_d = S.dram("cft", [128, NCF], F32, "ExternalInput")
        y_d = S.dram("y", [T // 2, D], F32, "ExternalOutput")
        wscr_all = S.dram("wscr", [NCH, 128, 4096], BF16, "Internal")
        wscr = [Buf("wscr%d" % i, wscr_all.t[i]) for i in range(NCH)]

        cb = S.sb("cb", [128, NCB], BF16)
        cf = S.sb("cf", [128, NCF], F32)
        pf = S.sb("pf", [128, NPV * 8], F32)
        pd = S.sb("pd", [128, 12 * 8], F32)
        gmb = S.sb("gmb", [128, 1024], BF16)
        gfb = S.sb("gfb", [128, 1024], BF16)
        l1a = S.sb("l1a", [128, 8, 288], BF16)
        l1b = S.sb("l1b", [128, 8, 288], BF16)
        l2 = S.sb("l2", [128, 3, 1024], BF16)
        ring = [S.sb("ring%d" % i, [128, 4096], BF16) for i in range(NSLOT)]
        xt1 = S.sb("xt", [128, NSUB, 1024], F32)
        xt = [xt1, xt1]
        nb = S.sb("nb", [128, NSUB, 1024], BF16)
        junk = nb[:, 0, :]
        st4 = S.sb("st4", [128, 16], F32)
        hT = S.sb("hT", [128, 8, TT + 1], BF16)
        h2T = S.sb("h2T", [128, 8, TT], BF16)
        mixT = h2T
        Zf = S.sb("Zf", [128, 8, 64], F32)
        Zb = S.sb("Zb", [128, 8, 2, 64], BF16)
        hal = S.sb("hal", [128, 8, 3], F32)
        tp = [S.sb("tp%d" % i, [128, TT + 1], F32) if i != 7 else None for i in range(12)]
        tp[7] = tp[6]
        tv = lambda i: tp[i][:, 0:TT]
        pj = [tp[0], tp[0], tp[0]]
        tmpd = tv(1)
        rkv = [tv(2), tv(3), tv(4)]
        sw = tv(5); asig = tv(6); gg = tv(7); cs = tv(0); cm = tv(1)
        Ep = tv(8); En = tv(9); Em = tv(10); rinv = tv(1); kkb = tv(11); ff = tv(0)
        kmod = tv(5); bv = tv(1); bon = tv(6); yln = tv(9); ysq = tv(10)
        sqb = S.sb("sqb", [128, TT], BF16)
        ARs = [S.sb("AR%d" % i, [128, NSUB, 2, 128], BF16) for i in range(3)]
        Bts = [S.sb("Bt%d" % i, [128, TT], BF16) for i in range(2)]
        Kts = [S.sb("Kt%d" % i, [128, TT], BF16) for i in range(2)]
        vbfs = [S.sb("vbf%d" % i, [128, TT], BF16) for i in range(2)]
        Bpads = [S.sb("Bpad%d" % i, [128, NSUB, 2, 128], BF16) for i in range(2)]
        Kpads = [S.sb("Kpad%d" % i, [128, NSUB, 2, 128], BF16) for i in range(2)]
        Vtms = [S.sb("Vtm%d" % i, [128, NSUB, 128], BF16) for i in range(2)]
        AMs = [S.sb("AM%d" % i, [128, 4, 512], BF16) for i in range(2)]
        TTfs = [S.sb("TTf%d" % i, [128, 4, 128], BF16) for i in range(2)]
        gbs = [S.sb("gb%d" % i, [128, TT], BF16) for i in range(3)]
        pcss = [S.sb("pcs%d" % i, [128, 4], F32) for i in range(3)]
        ysqB = S.sb("ysqB", [128, TT], F32)
        L0 = S.sb("L0", [128, 4, 128], BF16)
        LP = [S.sb("LP%d" % i, [128, 4, 128], BF16) for i in range(2)]
        LT = [S.sb("LT%d" % i, [128, 4, 128], BF16) for i in range(2)]
        SS = [S.sb("SS%d" % i, [128, 4, 128], BF16) for i in range(2)]
        Xb = S.sb("Xb", [128, 128], BF16)
        Ub = S.sb("Ub", [128, 128], BF16)
        ztmp = S.sb("ztmp", [128, 64], F32)
        Ytm = S.sb("Ytm", [128, NSUB, 128], F32)
        ynb = S.sb("ynb", [128, NSUB, 128], BF16)
        gst = S.sb("gst", [128, 32], F32)
        lw = S.sb("lw", [128, TT], BF16)
        lga = S.sb("lga", [128, TT], BF16)
        lgb = S.sb("lgb", [32, TT], BF16)
        yfin = S.sb("yfin", [128, 8, TT], BF16)
        _nbf = nb[:].re("p s c -> p (s c)")
        Qa = [h2T[:, 0:4, :], h2T[:, 4:8, :], _nbf[:, 0:1024].re("p (c t) -> p c t", t=TT)]
        K1 = S.sb("K1", [128, 4, 128 + TT], BF16)
        V1 = S.sb("V1", [128, 3, 512], BF16)
        K2c = S.sb("K2c", [128, 4, TT], BF16)
        K2r = S.sb("K2r", [128, 4, 4, 128], BF16)
        V2c = S.sb("V2c", [64, 4, 128], BF16)
        V2r = S.sb("V2r", [128, 4, 512], BF16)
        K3c = S.sb("K3c", [128, 4, TT], BF16)
        K3r = S.sb("K3r", [128, 4, 16, 128], BF16)
        V3c = S.sb("V3c", [16, 16, 128], BF16)
        V3r = S.sb("V3r", [128, 16, 512], BF16)
        VF = _nbf[:, 1024:2048].re("p (c t) -> p c t", t=TT)
        pe = S.sb("pe", [128, 512], BF16)
        pp_ = S.sb("pp", [128, 512], BF16)
        peb = SubBuf(pe, 256, 256)
        ppb = SubBuf(pp_, 256, 256)
        accO = S.sb("accO", [64, TT], F32)
        accD = S.sb("accD", [64, TT], F32)
        oT = S.sb("oT", [64, 8, TT], BF16)
        sa = tv(2); sbb = tv(3); sg = tv(4); utmp = tv(10)
        actT = S.sb("actT", [128, 8, TT], BF16)
        VF2 = actT[:, 0:4, :]
        wst = xt1[:].re("p s c -> p (s c)")

        _b0 = S.ps("b0", [128, 512])
        _b4 = S.ps("b4", [128, 512])
        _bS = S.ps("bS", [128, 1024])
        B3 = S.ps("b3", [128, 512])
        B5 = S.ps("b5", [128, 512])
        B6 = S.ps("b6", [128, 512])
        _pT = S.ps("pT", [128, 1024], BF16)
        R0 = SubBuf(_b0, 0); R1 = SubBuf(_b0, 256)
        Q0 = SubBuf(_b4, 0); Q1 = SubBuf(_b4, 256)
        B1 = Buf("B1", _bS.t[:, 0:512]); B2 = Buf("B2", _bS.t[:, 512:1024])
        pTr = SubBuf(_pT, 0); pTa = SubBuf(_pT, 512)
        for b_ in (_b0, _b4, B1, B2, B3, B5, B6, _pT):
            b_.psum = True
        pC = B3
        trB = [View(B1, B1.t[:, :].bitcast(BF16)), View(B2, B2.t[:, :].bitcast(BF16))]
        trC = View(B3, B3.t[:, :].bitcast(BF16))
        trot = [View(_pT, _pT.t[:, 0:512]), View(B6, B6.t[:, :].bitcast(BF16)), View(B5, B5.t[:, :].bitcast(BF16))]
        prot = {"r": [_b0, B1, B2], "a": [_b4, B5, B6], "x": [_b0, _b4, B1, B2, B3, B6], "f": [_b0, _b4, B6], "s": [_b0, _b4, B1, B2, B3, B6]}
        prot_i = {"r": 0, "a": 0, "x": 0, "f": 0, "s": 0}

        def nextp(k="x"):
            prot_i[k] = (prot_i[k] + 1) % len(prot[k])
            return prot[k][prot_i[k]]

        mm = lambda **kw: S.op("tensor", "matmul", **kw)
        tr = lambda **kw: S.op("tensor", "transpose", **kw)
        act = lambda **kw: S.op("scalar", "activation", **kw)
        vec = lambda m, **kw: S.op("vector", m, **kw)
        gps = lambda m, **kw: S.op("gpsimd", m, **kw)

        def cpy(out, in_):
            if (not DYN_COPY) or S.eng_free["scalar"] <= S.eng_free["vector"]:
                act(out=out, in_=in_, func=AF.Copy)
            else:
                vec("tensor_copy", out=out, in_=in_)

        def sigmoid_to(dst, src, nbias=None, scale=1.0):
            if nbias is None:
                act(out=dst, in_=src, func=AF.Exp, scale=-scale)
            else:
                act(out=dst, in_=src, func=AF.Exp, scale=-scale, bias=nbias)
            act(out=dst, in_=dst, func=AF.Ln, bias=one_c_for(dst))
            act(out=dst, in_=dst, func=AF.Exp, scale=-1.0)

        def one_c_for(v):
            lo = v.ap.base_partition()
            n = v.ap.partition_size()
            return cf[lo:lo + n, CF_EPS + 3:CF_EPS + 4]

        def rsqrt_to(dst, src, bias_ap, scale=1.0):
            act(out=dst, in_=src, func=AF.Ln, bias=bias_ap, scale=scale)
            act(out=dst, in_=dst, func=AF.Exp, scale=-0.5)

        ident = cb[:, CB_ID:CB_ID + 128]
        identf = cf[:, CF_IDF:CF_IDF + 128]
        onesbd = cb[:, CB_ONESBD:CB_ONESBD + 128]
        eps_r = cf[:, CF_EPS:CF_EPS + 1]
        eps_g = cf[:, CF_EPS + 1:CF_EPS + 2]
        zero_c = cf[:, CF_EPS + 2:CF_EPS + 3]
        one_c = cf[:, CF_EPS + 3:CF_EPS + 4]

        def pv(i, kc):
            return pf[:, i * 8 + kc: i * 8 + kc + 1]

        def pdv(i, kc):
            return pd[:, i * 8 + kc: i * 8 + kc + 1]
        PD_A1, PD_A2, PD_GM, PD_GF, PD_OMK, PD_OMR, PD_OMKm, PD_OMV = range(8)

        try:
            S.dma("gpsimd", out=cb[:, :], in_=cb_d[:, :])
            S.dma("sync", out=cf[:, :], in_=cf_d[:, :])
            S.dma("sync", out=pf[:, :], in_=pfm_d[:, :])
            for i in range(NCH):
                S.dma("gpsimd", out=wscr[i][:, :], in_=wsrc[i])
            CK('dma0')
            for b_ in (Zf, Zb, hal, Bpads[0], Bpads[1], Kpads[0], Kpads[1], K1, K2r, V2r, K3r, V3r, V1, hT, Xb, Ub, Vtms[0], Vtms[1]):
                gps("memset", ap=b_[:], constant=0.0)
            CK('memset')
            for half in range(2):
                S.dma("sync", out=wst[:, 0:4 * 288], in_=l1_d[:, half * 4 * 288:(half + 1) * 4 * 288])
                w1v = wst[:, 0:4 * 288].re("p (k c) -> p k c", c=288)
                for k4 in range(4):
                    kc = half * 4 + k4
                    for (lo, hi, mui) in ((0, 64, PV_MUW), (64, 128, PV_MUA), (128, 288, PV_MUG)):
                        vec("tensor_scalar", out=l1b[:, kc, lo:hi], in0=w1v[:, k4, lo:hi], scalar1=pv(mui, kc),
                            scalar2=None, op0=ALU.mult)
                        vec("tensor_tensor", out=l1a[:, kc, lo:hi], in0=w1v[:, k4, lo:hi], in1=l1b[:, kc, lo:hi],
                            op=ALU.subtract)
            for half in range(2):
                S.dma("sync", out=wst[:, 0:1536], in_=l2_d[:, half * 1536:(half + 1) * 1536])
                cpy(out=l2[:].re("p a c -> p (a c)")[:, half * 1536:(half + 1) * 1536], in_=wst[:, 0:1536])
            CK('lora0')
            for j in range(24):
                S.dma("sync", out=wst[:, 0:2048], in_=wmod_d[j])
                wv = wst[:, 0:2048].re("p (k c) -> p k c", c=256)
                for cc in range(2):
                    col = j * 2 + cc
                    for kc in range(8):
                        mm(out=B1[:, col:col + 1], lhsT=wv[:, kc, cc * 128:(cc + 1) * 128], rhs=pv(PV_C, kc),
                           start=(kc == 0), stop=(kc == 7))
            modf = S.sb("modf", [128, 48], F32)
            vec("tensor_tensor", out=modf[:, :], in0=B1[:, 0:48], in1=pf[:, 0:48], op=ALU.add)
            for kc in range(8):
                vec("scalar_tensor_tensor", out=pdv(PD_A1, kc), in0=modf[:, 8 + kc:9 + kc], scalar=1.0,
                    in1=pv(PV_GPM, kc), op0=ALU.add, op1=ALU.mult)
                vec("scalar_tensor_tensor", out=pdv(PD_A2, kc), in0=modf[:, 32 + kc:33 + kc], scalar=1.0,
                    in1=pv(PV_GPF, kc), op0=ALU.add, op1=ALU.mult)
                vec("tensor_tensor", out=pdv(PD_GM, kc), in0=modf[:, 16 + kc:17 + kc], in1=pv(PV_GQM, kc), op=ALU.mult)
                vec("tensor_tensor", out=pdv(PD_GF, kc), in0=modf[:, 40 + kc:41 + kc], in1=pv(PV_GQF, kc), op=ALU.mult)
                vec("tensor_scalar", out=pdv(PD_OMK, kc), in0=pv(PV_KA, kc), scalar1=-1.0, scalar2=1.0,
                    op0=ALU.mult, op1=ALU.add)
                vec("tensor_scalar", out=pdv(5, kc), in0=pv(PV_W0, kc), scalar1=-1.0, scalar2=None, op0=ALU.mult)
                vec("tensor_scalar", out=pdv(6, kc), in0=pv(PV_A0, kc), scalar1=-1.0, scalar2=None, op0=ALU.mult)
            dg = tp[0][:, 0:128]
            onesf = tp[1][:, 0:128]
            gps("memset", ap=onesf[:, :], constant=1.0)
            for (pdi, dst) in ((PD_GM, gmb), (PD_GF, gfb)):
                for kc in range(8):
                    vec("tensor_scalar", out=dg[:, :], in0=identf, scalar1=pdv(pdi, kc), scalar2=None, op0=ALU.mult)
                    pz = nextp()
                    mm(out=pz[:, 0:128], lhsT=onesf[:, :], rhs=dg[:, :], start=True, stop=True)
                    cpy(out=dst[:, kc * 128:(kc + 1) * 128], in_=pz[:, 0:128])

            CK('startup')
            ring_i = [0]

            ring_sets = {"r": ring[0:2], "a": ring[2:4], "x": ring}
            ring_k = {"r": 0, "a": 0, "x": 0}

            def wload(ch, k="x"):
                s = ring_sets[k][ring_k[k] % len(ring_sets[k])]
                ring_k[k] += 1
                S.dma("sync", out=s[:, :], in_=wscr[ch][:, :])
                return s

            def rms_rstd(src3, dst_cols, nsub=NSUB):
                for sub in range(nsub):
                    act(out=junk[:, :], in_=src3[:, sub, :], func=AF.Square,
                        accum_out=st4[:, 8 + sub:9 + sub])
                rsqrt_to(st4[:, dst_cols:dst_cols + nsub], st4[:, 8:8 + nsub], eps_r, 1.0 / D)

            def norm_transpose(xsrc, rcol, dstT, a_idx, b_view_fn, halo):
                for sub in range(NSUB):
                    vec("tensor_scalar", out=nb[:, sub, :], in0=xsrc[:, sub, :], scalar1=st4[:, rcol + sub:rcol + sub + 1],
                        scalar2=None, op0=ALU.mult)
                for kc in range(8):
                    tgt = trot[kc % 3]
                    for sub in range(NSUB):
                        tr(out=tgt[:, sub * 128:(sub + 1) * 128], in_=nb[:, sub, kc * 128:(kc + 1) * 128], identity=ident)
                    act(out=dstT[:, kc, halo:halo + TT], in_=tgt[:, 0:TT], func=AF.Identity,
                        scale=pdv(a_idx, kc), bias=b_view_fn(kc))

            def norm_transpose_g(xsrc, rcol, dstT, a_idx, b_view_fn, halo):
                for sub in range(NSUB):
                    vec("tensor_scalar", out=nb[:, sub, :], in0=xsrc[:, sub, :], scalar1=st4[:, rcol + sub:rcol + sub + 1],
                        scalar2=None, op0=ALU.mult)
                    yield
                for kc in range(8):
                    tgt = trot[kc % 3]
                    for sub in range(NSUB):
                        tr(out=tgt[:, sub * 128:(sub + 1) * 128], in_=nb[:, sub, kc * 128:(kc + 1) * 128], identity=ident)
                    act(out=dstT[:, kc, halo:halo + TT], in_=tgt[:, 0:TT], func=AF.Identity,
                        scale=pdv(a_idx, kc), bias=b_view_fn(kc))
                    yield

            s1_done = set()

            for m in range(nt):
                phaseB = m >= PB0
                prot["r"] = [_b0] if m >= PB0 - 8 else [_b0, _b4, B5, B6]
                xm = xt[m % 2]
                vcur = cf[:, CF_VM + m:CF_VM + m + 1]
                vprev = cf[:, CF_VM + 32 + m:CF_VM + 33 + m]
                vr2 = cf[:, CF_VM + 64 + m:CF_VM + 65 + m]
                vr3 = cf[:, CF_VM + 96 + m:CF_VM + 97 + m]
                def stage1_gen(tix):
                    S.dma("gpsimd", out=xm[:], in_=xv.v(xv.t[tix * TT:(tix + 1) * TT, :].rearrange("(s p) c -> p s c", p=128)))
                    yield
                    if tix > 0:
                        vec("tensor_scalar", out=hT[:, :, 0:1], in0=hT[:, :, TT:TT + 1],
                            scalar1=cf[:, CF_VM + tix - 1:CF_VM + tix], scalar2=None, op0=ALU.mult)
                        yield
                    rms_rstd(xm, 0)
                    yield
                    yield from norm_transpose_g(xm, 0, hT, PD_A1, lambda kc: pf[:, PV_SHM * 8 + kc:PV_SHM * 8 + kc + 1]
                                   if False else modf[:, kc:kc + 1], 1)

                    pz = nextp("s")
                    for kc in range(8):
                        mm(out=pz[:, 0:TT], lhsT=l1a[:, kc, 0:128], rhs=hT[:, kc, 1:TT + 1], start=(kc == 0), stop=False)
                        mm(out=pz[:, 0:TT], lhsT=l1b[:, kc, 0:128], rhs=hT[:, kc, 0:TT], start=False, stop=(kc == 7))
                    sigmoid_to(tp[11][0:64, 0:TT], pz[0:64, 0:TT], None, 2.0)
                    yield
                    vec("tensor_scalar", out=lw[0:64, :], in0=tp[11][0:64, 0:TT], scalar1=2.0, scalar2=-1.0, op0=ALU.mult, op1=ALU.add)
                    yield
                    cpy(out=lw[64:128, :], in_=pz[64:128, 0:TT])
                    yield
                    if tix >= PB0:
                        pz = nextp("s")
                        for kc in range(8):
                            mm(out=pz[:, 0:TT], lhsT=l1a[:, kc, 128:256], rhs=hT[:, kc, 1:TT + 1], start=(kc == 0), stop=False)
                            mm(out=pz[:, 0:TT], lhsT=l1b[:, kc, 128:256], rhs=hT[:, kc, 0:TT], start=False, stop=(kc == 7))
                        sigmoid_to(tp[11][:, 0:TT], pz[:, 0:TT])
                        yield
                        cpy(out=lga[:, :], in_=tp[11][:, 0:TT])
                        yield
                        pz = nextp("s")
                        for kc in range(8):
                            mm(out=pz[0:32, 0:TT], lhsT=l1a[:, kc, 256:288], rhs=hT[:, kc, 1:TT + 1], start=(kc == 0), stop=False)
                            mm(out=pz[0:32, 0:TT], lhsT=l1b[:, kc, 256:288], rhs=hT[:, kc, 0:TT], start=False, stop=(kc == 7))
                        sigmoid_to(tp[11][0:32, 0:TT], pz[0:32, 0:TT])
                        yield
                        cpy(out=lgb[0:32, :], in_=tp[11][0:32, 0:TT])
                        yield


                if m not in s1_done:
                    for _ in stage1_gen(m):
                        pass

                CK('lora1')
                def rw_f1(c0):
                    for c in (c0,):
                        AR = ARs[c % 3]; AM = AMs[c % 2]; Vtm = Vtms[c % 2]; Bpad = Bpads[c % 2]; Kpad = Kpads[c % 2]
                        TTf = TTfs[c % 2]; gb = gbs[c % 3]; pcs = pcss[c % 3]
                        Bt = Bts[c % 2]; Kt = Kts[c % 2]; vbf = vbfs[c % 2]
                        csl = slice(c * 128, (c + 1) * 128)
                        wr = wload(CH_RW + c, 'r')
                        wrv = wr[:, :].re("p (k c) -> p k c", c=512)
                        for j in ((0, 1, 2) if m >= PB0 - 1 else (1, 2)):
                            pz = nextp("r")
                            for kc in range(8):
                                mm(out=pz[:, 0:TT], lhsT=wrv[:, kc, j * 128:(j + 1) * 128], rhs=hT[:, kc, 1:TT + 1],
                                   start=(kc == 0), stop=(kc == 7))
                            cpy(out=pj[j][:, 0:1], in_=hal[:, c, j:j + 1])
                            yield
                            cpy(out=pj[j][:, 1:TT + 1], in_=pz[:, 0:TT])
                            yield
                            vec("tensor_scalar", out=hal[:, c, j:j + 1], in0=pj[j][:, TT:TT + 1], scalar1=vcur,
                                scalar2=None, op0=ALU.mult)
                            yield
                            vec("tensor_tensor", out=tmpd[:, :], in0=pj[j][:, 0:TT], in1=pj[j][:, 1:TT + 1], op=ALU.subtract)
                            yield
                            vec("scalar_tensor_tensor", out=rkv[j][:, :], in0=tmpd[:, :], scalar=pv(PV_MUR + j, c),
                                in1=pj[j][:, 1:TT + 1], op0=ALU.mult, op1=ALU.add)
                            yield
                        r_, k_, v_ = rkv
                        vec("tensor_scalar", out=v_[:, :], in0=v_[:, :], scalar1=vcur, scalar2=None, op0=ALU.mult)
                        yield
                        cpy(out=vbf[:, :], in_=v_[:, :])
                        yield
                        pz = nextp("r")
                        mm(out=pz[:, 0:TT], lhsT=l2[0:64, 0, csl], rhs=lw[0:64, :], start=True, stop=True)
                        sigmoid_to(sw[:, :], pz[:, 0:TT], pdv(5, c))
                        yield
                        pz = nextp("r")
                        mm(out=pz[:, 0:TT], lhsT=l2[64:128, 0, csl], rhs=lw[64:128, :], start=True, stop=True)
                        sigmoid_to(asig[:, :], pz[:, 0:TT], pdv(6, c))
                        yield
                        pz = nextp("r")
                        if phaseB:
                            mm(out=pz[:, 0:TT], lhsT=l2[:, 1, csl], rhs=lga[:, :], start=True, stop=False)
                            mm(out=pz[:, 0:TT], lhsT=l2[0:32, 2, csl], rhs=lgb[0:32, :], start=False, stop=True)
                            cpy(out=gb[:, :], in_=pz[:, 0:TT])
                        yield
                        vec("tensor_tensor_scan", out=cs[:, :], data0=cf[:, CF_MSK:CF_MSK + TT], data1=sw[:, :],
                            initial=0.0, op0=ALU.mult, op1=ALU.add)
                        yield
                        vec("tensor_tensor", out=cm[:, :], in0=cs[:, :], in1=sw[:, :], op=ALU.subtract)
                        yield
                        act(out=Ep[:, :], in_=cs[:, :], func=AF.Exp, scale=-C0)
                        yield
                        act(out=En[:, :], in_=cs[:, :], func=AF.Exp, scale=C0)
                        yield
                        act(out=Em[:, :], in_=cm[:, :], func=AF.Exp, scale=-C0)
                        yield
                        act(out=sqb[:, :], in_=k_[:, :], func=AF.Square, scale=pv(PV_KK, c))
                        yield
                        pz = nextp("r")
                        mm(out=pz[:, 0:TT], lhsT=onesbd, rhs=sqb[:, :], start=True, stop=True)
                        vec("tensor_scalar", out=rinv[:, :], in0=pz[:, 0:TT], scalar1=1e-18, scalar2=None, op0=ALU.max)
                        yield
                        act(out=rinv[:, :], in_=rinv[:, :], func=AF.Ln)
                        yield
                        act(out=rinv[:, :], in_=rinv[:, :], func=AF.Exp, scale=-0.5)
                        yield
                        vec("scalar_tensor_tensor", out=kkb[:, :], in0=k_[:, :], scalar=pv(PV_KK, c), in1=rinv[:, :],
                            op0=ALU.mult, op1=ALU.mult)
                        yield
                        ev = gps if GPS_OFF else vec
                        ev("tensor_scalar", out=ff[:, :], in0=asig[:, :], scalar1=pv(PV_KA, c), scalar2=pdv(PD_OMK, c),
                            op0=ALU.mult, op1=ALU.add)
                        yield
                        ev("tensor_tensor", out=kmod[:, :], in0=k_[:, :], in1=ff[:, :], op=ALU.mult)
                        yield
                        ev("tensor_tensor", out=bv[:, :], in0=kkb[:, :], in1=asig[:, :], op=ALU.mult)
                        yield
                        vec("scalar_tensor_tensor", out=AR[:, :, 0, :], in0=kkb[:, :].re("p (s t) -> p s t", t=128), scalar=-1.0,
                            in1=Em[:, :].re("p (s t) -> p s t", t=128), op0=ALU.mult, op1=ALU.mult)
                        yield
                        if phaseB:
                            vec("tensor_tensor", out=AR[:, :, 1, :], in0=r_[:, :].re("p (s t) -> p s t", t=128),
                                in1=Ep[:, :].re("p (s t) -> p s t", t=128), op=ALU.mult)
                            yield
                        ev("tensor_tensor", out=Bt[:, :], in0=bv[:, :], in1=En[:, :], op=ALU.mult)
                        yield
                        ev("tensor_tensor", out=Kt[:, :], in0=kmod[:, :], in1=En[:, :], op=ALU.mult)
                        yield
                        if phaseB:
                            vec("tensor_tensor", out=tmpd[:, :], in0=r_[:, :], in1=kmod[:, :], op=ALU.mult)
                            yield
                            act(out=sqb[:, :], in_=tmpd[:, :], func=AF.Copy, scale=pv(PV_RK, c))
                            yield
                            pz = nextp("r")
                            mm(out=pz[:, 0:TT], lhsT=onesbd, rhs=sqb[:, :], start=True, stop=True)
                            vec("tensor_tensor", out=bon[:, :], in0=pz[:, 0:TT], in1=v_[:, :], op=ALU.mult)
                            yield
                            vec("tensor_tensor", out=yfin[:, c, :], in0=bon[:, :], in1=gb[:, :], op=ALU.mult)
                        yield
                        cpy(out=pcs[:, 0:4], in_=Ep[:, :].re("p (q t) -> p q t", t=64)[:, :, 63])
                        yield

                def rw_f2(c0):
                    for c in (c0,):
                        AR = ARs[c % 3]; AM = AMs[c % 2]; Vtm = Vtms[c % 2]; Bpad = Bpads[c % 2]; Kpad = Kpads[c % 2]
                        TTf = TTfs[c % 2]; gb = gbs[c % 3]; pcs = pcss[c % 3]
                        Bt = Bts[c % 2]; Kt = Kts[c % 2]; vbf = vbfs[c % 2]
                        for qi, (src, dst) in enumerate(((Bt, Bpad), (Kt, Kpad), (vbf, None))):
                            for sub in range(NSUB):
                                tr(out=trB[qi % 2][:, sub * 128:(sub + 1) * 128],
                                   in_=src[:, sub * 128:(sub + 1) * 128], identity=ident)
                            yield
                            srcv = trB[qi % 2][:, 0:256]
                            if dst is None:
                                cpy(out=Vtm[:].re("p s c -> p (s c)"), in_=srcv)
                                yield
                            else:
                                for h in range(2):
                                    cpy(out=dst[:, :, h, h * 64:(h + 1) * 64],
                                        in_=srcv.re("p (s c) -> p s c", c=128)[:, :, h * 64:(h + 1) * 64])
                                    yield
                        CK('rwkv_a')
                        for h in range(2):
                            hs = slice(h * 64, (h + 1) * 64)
                            for sub in range(NSUB):
                                u = h * NSUB + sub
                                tsl = slice(sub * 128, (sub + 1) * 128)
                                pz = (B1, B2)[u % 2]
                                if phaseB:
                                    mm(out=pz[:, 0:256], lhsT=Bt[hs, tsl], rhs=AR[hs, sub, :, :].re("p a t -> p (a t)"),
                                       start=True, stop=True)
                                    mm(out=pz[:, 256:512], lhsT=Kt[hs, tsl], rhs=AR[hs, sub, :, :].re("p a t -> p (a t)"),
                                       start=True, stop=True)
                                    vec("tensor_tensor", out=AM[:, u, :], in0=pz[:, :], in1=cb[:, CB_MT4:CB_MT4 + 512], op=ALU.mult)
                                    yield
                                else:
                                    mm(out=pz[:, 0:128], lhsT=Bt[hs, tsl], rhs=AR[hs, sub, 0, :], start=True, stop=True)
                                    mm(out=pz[:, 256:384], lhsT=Kt[hs, tsl], rhs=AR[hs, sub, 0, :], start=True, stop=True)
                                    v4 = lambda ap_: ap_.re("p (a two b) -> p a two b", a=2, two=2)[:, :, 0, :]
                                    vec("tensor_tensor", out=v4(AM[:, u, :]), in0=v4(pz[:, :]),
                                        in1=v4(cb[:, CB_MT4:CB_MT4 + 512]), op=ALU.mult)
                                yield
                        for h in range(2):
                            hs = slice(h * 64, (h + 1) * 64)
                            for sub in range(NSUB):
                                u = h * NSUB + sub
                                tsl = slice(sub * 128, (sub + 1) * 128)
                                mm(out=B1[:, u * 128:(u + 1) * 128], lhsT=AR[hs, sub, 0, :], rhs=Bt[hs, tsl],
                                   start=True, stop=True)
                        vec("tensor_tensor", out=L0[:].re("p u t -> p (u t)"), in0=B1[:, :], in1=cb[:, CB_ML4:CB_ML4 + 512],
                            op=ALU.mult)
                        yield
                        CK('rwkv_b')
                        vec("tensor_tensor", out=SS[0][:], in0=AM[:, :, 0:128], in1=ident.bc(1, [128, 4, 128]), op=ALU.add)
                        yield
                        lt_prev = lambda u: AM[:, u, 0:128]
                        lp_prev = lambda u: L0[:, u, :]
                        scur = 0
                        for lev in range(1, 6):
                            lpn = LP[lev % 2]
                            ltn = LT[lev % 2]
                            for u in range(4):
                                mm(out=B1[:, u * 128:(u + 1) * 128], lhsT=lt_prev(u), rhs=lp_prev(u), start=True, stop=True)
                            if lev <= 4:
                                for u in range(4):
                                    mm(out=B2[:, u * 128:(u + 1) * 128], lhsT=lp_prev(u), rhs=lt_prev(u),
                                       start=True, stop=True)
                            cpy(out=lpn[:].re("p u t -> p (u t)"), in_=B1[:, :])
                            yield
                            if lev <= 4:
                                cpy(out=ltn[:].re("p u t -> p (u t)"), in_=B2[:, :])
                            yield
                            for u in range(4):
                                mm(out=B1[:, u * 128:(u + 1) * 128], lhsT=lpn[:, u, :], rhs=SS[scur][:, u, :], start=True, stop=True)
                            sdst = TTf if lev == 5 else SS[1 - scur]
                            vec("tensor_tensor", out=sdst[:].re("p u t -> p (u t)"), in0=B1[:, :],
                                in1=SS[scur][:].re("p u t -> p (u t)"), op=ALU.add)
                            yield
                            scur = 1 - scur
                            lt_prev = (lambda b: (lambda u: b[:, u, :]))(ltn)
                            yield
                            lp_prev = (lambda b: (lambda u: b[:, u, :]))(lpn)

                def rw_back(c0):
                    for c in (c0,):
                        AR = ARs[c % 3]; AM = AMs[c % 2]; Vtm = Vtms[c % 2]; Bpad = Bpads[c % 2]; Kpad = Kpads[c % 2]
                        TTf = TTfs[c % 2]; gb = gbs[c % 3]; pcs = pcss[c % 3]
                        Bt = Bts[c % 2]; Kt = Kts[c % 2]; vbf = vbfs[c % 2]
                        TTm = TTf
                        ysq = ysqB[:, :]
                        yln = ysqB[:, :]
                        CK('rwkv_c')
                        for q in range(2 * NSUB):
                            sub, half = q // 2, q % 2
                            ps_ = slice(half * 64, half * 64 + 64)
                            tsl = slice(sub * 128, (sub + 1) * 128)
                            zi = q % 2
                            for h in range(2):
                                hs = slice(h * 64, (h + 1) * 64)
                                u = h * NSUB + sub
                                mm(out=pC[:, hs], lhsT=AR[hs, sub, 0, :], rhs=Zb[hs, c, zi, :], start=True, stop=False)
                                mm(out=pC[:, hs], lhsT=AM[:, u, 256:384], rhs=Vtm[:, sub, hs], start=False, stop=True)
                            cpy(out=Xb[ps_, :], in_=pC[ps_, 0:128])
                            yield
                            CK('c1')
                            for h in range(2):
                                hs = slice(h * 64, (h + 1) * 64)
                                u = h * NSUB + sub
                                mm(out=pC[:, 128 + h * 64:128 + (h + 1) * 64], lhsT=TTm[ps_, u, :], rhs=Xb[ps_, hs],
                                   start=True, stop=True)
                            cpy(out=Ub[ps_, :], in_=pC[ps_, 128:256])
                            yield
                            CK('c2')
                            if phaseB:
                                for h in range(2):
                                    hs = slice(h * 64, (h + 1) * 64)
                                    u = h * NSUB + sub
                                    o_ = slice(256 + h * 64, 256 + (h + 1) * 64)
                                    mm(out=pC[:, o_], lhsT=AR[hs, sub, 1, :], rhs=Zb[hs, c, zi, :], start=True, stop=False)
                                    mm(out=pC[:, o_], lhsT=AM[:, u, 128:256], rhs=Ub[:, hs], start=False, stop=False)
                                    mm(out=pC[:, o_], lhsT=AM[:, u, 384:512], rhs=Vtm[:, sub, hs], start=False, stop=True)
                                cpy(out=Ytm[ps_, sub, :], in_=pC[ps_, 256:384])
                                yield
                            CK('c3')
                            for h in range(2):
                                hs = slice(h * 64, (h + 1) * 64)
                                mm(out=pC[:, 384:448], lhsT=Bpad[ps_, sub, h, :], rhs=Ub[ps_, hs], start=(h == 0), stop=False)
                                mm(out=pC[:, 384:448], lhsT=Kpad[ps_, sub, h, :], rhs=Vtm[ps_, sub, hs], start=False, stop=(h == 1))
                            CK('c4')
                            pcv = pcs[:, q:q + 1]
                            vec("tensor_scalar", out=ztmp[:, :], in0=Zf[:, c, :], scalar1=pcv, scalar2=None, op0=ALU.mult)
                            yield
                            vec("scalar_tensor_tensor", out=Zf[:, c, :], in0=pC[:, 384:448], scalar=pcv, in1=ztmp[:, :],
                                op0=ALU.mult, op1=ALU.add)
                            yield
                            cpy(out=Zb[:, c, 1 - zi, :], in_=Zf[:, c, :])
                            yield
                        CK('rwkv_d')
                        if phaseB:
                            yv = Ytm[:].re("p s (h i) -> p (s h) i", i=64)
                            vec("tensor_reduce", out=gst[:, 0:4], in_=yv, axis=AX.X, op=ALU.add)
                            yield
                            act(out=ysq[:, :], in_=Ytm[:].re("p s c -> p (s c)"), func=AF.Square)
                            yield
                            vec("tensor_reduce", out=gst[:, 4:8], in_=ysq[:, :].re("p (g i) -> p g i", i=64), axis=AX.X, op=ALU.add)
                            yield
                            vec("tensor_scalar", out=gst[:, 8:12], in0=gst[:, 0:4], scalar1=1.0 / 64, scalar2=None, op0=ALU.mult)
                            yield
                            vec("tensor_tensor", out=gst[:, 12:16], in0=gst[:, 8:12], in1=gst[:, 8:12], op=ALU.mult)
                            yield
                            vec("scalar_tensor_tensor", out=gst[:, 16:20], in0=gst[:, 4:8], scalar=1.0 / 64, in1=gst[:, 12:16],
                                op0=ALU.mult, op1=ALU.subtract)
                            yield
                            rsqrt_to(gst[:, 24:28], gst[:, 16:20], eps_g, 1.0)
                            yield
                            ysv = ysq[:, :].re("p (g i) -> p g i", i=64)
                            vec("tensor_tensor", out=ysv, in0=yv, in1=gst[:, 8:12].bc(2, [128, 4, 64]), op=ALU.subtract)
                            yield
                            vec("tensor_tensor", out=ynb[:].re("p s (h i) -> p (s h) i", i=64), in0=ysv,
                                in1=gst[:, 24:28].bc(2, [128, 4, 64]), op=ALU.mult)
                            yield
                            for sub in range(NSUB):
                                tr(out=trC[:, sub * 128:(sub + 1) * 128], in_=ynb[:, sub, :], identity=ident)
                            act(out=yln[:, :], in_=trC[:, 0:TT], func=AF.Identity, scale=pv(PV_LNW, c), bias=pv(PV_LNB, c))
                            yield
                            vec("tensor_tensor", out=yln[:, :], in0=yln[:, :], in1=gb[:, :], op=ALU.mult)
                            yield
                            vec("tensor_tensor", out=yfin[:, c, :], in0=yln[:, :], in1=yfin[:, c, :], op=ALU.add)
                            yield


                def th_attn():
                    if m < PB0 - 8:
                        return
                    j0_2 = m % 2
                    j0_3 = m % 8
                    for g in range(3):
                        kdst = (K1, K2c, K3c)[g]
                        for j in ((0, 1, 2) if phaseB else (1, 2)):
                            wa = wload(CH_AT + g * 3 + j, 'a')
                            wav = wa[:, :].re("p (k c) -> p k c", c=512)
                            for cc in range(4):
                                pz = nextp("a")
                                for kc in range(8):
                                    mm(out=pz[:, 0:TT], lhsT=wav[:, kc, cc * 128:(cc + 1) * 128], rhs=hT[:, kc, 1:TT + 1],
                                       start=(kc == 0), stop=(kc == 7))
                                if j == 0:
                                    act(out=Qa[g][:, cc, :], in_=pz[:, 0:TT], func=AF.Copy, scale=0.125)
                                    yield
                                elif j == 1:
                                    if g == 0:
                                        cpy(out=K1[:, cc, 128:128 + TT], in_=pz[:, 0:TT])
                                        yield
                                    else:
                                        cpy(out=kdst[:, cc, :], in_=pz[:, 0:TT])
                                        yield
                                else:
                                    cpy(out=VF[:, cc, :], in_=pz[:, 0:TT])
                                    yield
                            if not DENSE_ATTN_PROJ:
                                yield
                        if g == 0:
                            for blk in range(2):
                                for cc in range(4):
                                    tr(out=pTa[:, cc * 128:(cc + 1) * 128], in_=VF[:, cc, blk * 128:(blk + 1) * 128], identity=ident)
                                cpy(out=V1[:, 1 + blk, :], in_=pTa[:, 0:512])
                            yield
                        elif g == 1:
                            cpy(out=VF2[:], in_=VF[:])
                            yield
                    CK('attn_proj')
                    for h in range(8):
                        cc, hp = h // 2, (h % 2) * 64
                        hs = slice(hp, hp + 64)
                        vs = slice(h * 64, (h + 1) * 64)
                        vl = slice(hp, hp + 64)
                        if h % 2 == 0:
                            for r in range(4):
                                tr(out=pTa[0:64, r * 128:(r + 1) * 128],
                                   in_=VF2[:, cc, :].re("p (i r) -> p r i", r=4)[:, r, :], identity=ident)
                            cpy(out=V2c[0:64, :, :].re("p r c -> p (r c)"), in_=pTa[0:64, 0:512])
                            yield
                            for r in range(16):
                                tr(out=pTa[0:16, (r % 4) * 128:(r % 4 + 1) * 128],
                                   in_=VF[:, cc, :].re("p (i r) -> p r i", r=16)[:, r, :], identity=ident)
                                if r % 4 == 3:
                                    cpy(out=V3c[0:16, r - 3:r + 1, :].re("p a c -> p (a c)"), in_=pTa[0:16, 0:512])
                                yield
                        if not phaseB:
                            if h % 2 == 1:
                                S.dma("gpsimd", out=V2r[j0_2 * 64:(j0_2 + 1) * 64, :, cc * 128:(cc + 1) * 128], in_=V2c[0:64, :, :])
                                S.dma("gpsimd", out=V3r[j0_3 * 16:(j0_3 + 1) * 16, :, cc * 128:(cc + 1) * 128], in_=V3c[0:16, :, :])
                            continue
                        for blk in range(2):
                            qv = Qa[0][hs, cc, blk * 128:(blk + 1) * 128]
                            mm(out=B5[:, (blk * 2) * 128:(blk * 2 + 1) * 128], lhsT=K1[hs, cc, blk * 128:(blk + 1) * 128],
                               rhs=qv, start=True, stop=True)
                            mm(out=B5[:, (blk * 2 + 1) * 128:(blk * 2 + 2) * 128],
                               lhsT=K1[hs, cc, 128 + blk * 128:128 + (blk + 1) * 128], rhs=qv, start=True, stop=True)
                        act(out=pe[:, :], in_=B5[:, 0:512], func=AF.Exp)
                        yield
                        vec("tensor_tensor", out=pp_[:, :].re("p (b e) -> p b e", b=2), in0=pe[:, :].re("p (b e) -> p b e", b=2),
                            in1=cb[:, CB_E1 + h * 256:CB_E1 + (h + 1) * 256].bc(1, [128, 2, 256]), op=ALU.mult)
                        yield
                        vec("tensor_scalar", out=pp_[:, 0:128], in0=pp_[:, 0:128], scalar1=vprev, scalar2=None, op0=ALU.mult)
                        yield
                        for blk in range(2):
                            mm(out=B6[0:64, blk * 128:(blk + 1) * 128], lhsT=V1[:, blk, vs],
                               rhs=pp_[:, (blk * 2) * 128:(blk * 2 + 1) * 128], start=True, stop=False)
                            mm(out=B6[0:64, blk * 128:(blk + 1) * 128], lhsT=V1[:, blk + 1, vs],
                               rhs=pp_[:, (blk * 2 + 1) * 128:(blk * 2 + 2) * 128], start=False, stop=True)
                        ppv = pp_[:, :].re("p (b c q) -> p b c q", b=2, c=2)
                        mm(out=B6[0:64, 256:512], lhsT=cb[:, CB_ONES:CB_ONES + 64], rhs=ppv[:, :, 0, :], start=True, stop=False)
                        mm(out=B6[0:64, 256:512], lhsT=cb[:, CB_ONES:CB_ONES + 64], rhs=ppv[:, :, 1, :], start=False, stop=True)
                        cpy(out=accO[:, :], in_=B6[0:64, 0:TT])
                        yield
                        cpy(out=accD[:, :], in_=B6[0:64, 256:512])
                        yield
                        for r in range(4):
                            qv = Qa[1][hs, cc, :].re("p (i r) -> p r i", r=4)[:, r, :]
                            mm(out=B5[:, r * 64:(r + 1) * 64], lhsT=K2r[hs, cc, r, :], rhs=qv, start=True, stop=True)
                            mm(out=B5[0:64, 256 + r * 64:256 + (r + 1) * 64],
                               lhsT=K2c[hs, cc, :].re("p (i r) -> p r i", r=4)[:, r, :], rhs=qv, start=True, stop=True)
                        act(out=pe[:, 0:256], in_=B5[:, 0:256], func=AF.Exp)
                        yield
                        act(out=peb[0:64, :], in_=B5[0:64, 256:512], func=AF.Exp)
                        yield
                        ea = cb[:, CB_EA2 + (j0_2 * 8 + h) * 64:CB_EA2 + (j0_2 * 8 + h + 1) * 64]
                        vec("scalar_tensor_tensor", out=pp_[:, 0:256].re("p (r i) -> p r i", r=4),
                            in0=pe[:, 0:256].re("p (r i) -> p r i", r=4), scalar=vr2, in1=ea.bc(1, [128, 4, 64]),
                            op0=ALU.mult, op1=ALU.mult)
                        yield
                        eb = cb[0:64, CB_EB2 + h * 64:CB_EB2 + (h + 1) * 64]
                        vec("tensor_tensor", out=ppb[0:64, :].re("p (r i) -> p r i", r=4),
                            in0=peb[0:64, :].re("p (r i) -> p r i", r=4), in1=eb.bc(1, [64, 4, 64]), op=ALU.mult)
                        yield
                        for r in range(4):
                            mm(out=B6[0:64, r * 64:(r + 1) * 64], lhsT=V2r[:, r, vs], rhs=pp_[:, r * 64:(r + 1) * 64],
                               start=True, stop=False)
                            mm(out=B6[0:64, r * 64:(r + 1) * 64], lhsT=V2c[0:64, r, vl], rhs=ppb[0:64, r * 64:(r + 1) * 64],
                               start=False, stop=True)
                        mm(out=B6[0:64, 256:512], lhsT=cb[:, CB_ONES:CB_ONES + 64], rhs=pp_[:, 0:256], start=True, stop=False)
                        mm(out=B6[0:64, 256:512], lhsT=cb[0:64, CB_ONES:CB_ONES + 64], rhs=ppb[0:64, :], start=False, stop=True)
                        vec("tensor_tensor", out=accO[:, :].re("p (i r) -> p r i", r=4), in0=accO[:, :].re("p (i r) -> p r i", r=4),
                            in1=B6[0:64, 0:TT].re("p (r i) -> p r i", r=4), op=ALU.add)
                        yield
                        vec("tensor_tensor", out=accD[:, :].re("p (i r) -> p r i", r=4), in0=accD[:, :].re("p (i r) -> p r i", r=4),
                            in1=B6[0:64, 256:512].re("p (r i) -> p r i", r=4), op=ALU.add)
                        yield
                        for r in range(16):
                            qv = Qa[2][hs, cc, :].re("p (i r) -> p r i", r=16)[:, r, :]
                            mm(out=B5[:, r * 16:(r + 1) * 16], lhsT=K3r[hs, cc, r, :], rhs=qv, start=True, stop=True)
                            mm(out=B5[0:16, 256 + r * 16:256 + (r + 1) * 16],
                               lhsT=K3c[hs, cc, :].re("p (i r) -> p r i", r=16)[:, r, :], rhs=qv, start=True, stop=True)
                        act(out=pe[:, 0:256], in_=B5[:, 0:256], func=AF.Exp)
                        yield
                        act(out=peb[0:16, :], in_=B5[0:16, 256:512], func=AF.Exp)
                        yield
                        ea = cb[:, CB_EA3 + (j0_3 * 8 + h) * 16:CB_EA3 + (j0_3 * 8 + h + 1) * 16]
                        vec("scalar_tensor_tensor", out=pp_[:, 0:256].re("p (r i) -> p r i", r=16),
                            in0=pe[:, 0:256].re("p (r i) -> p r i", r=16), scalar=vr3, in1=ea.bc(1, [128, 16, 16]),
                            op0=ALU.mult, op1=ALU.mult)
                        yield
                        eb = cb[0:16, CB_EB3 + h * 16:CB_EB3 + (h + 1) * 16]
                        vec("tensor_tensor", out=ppb[0:16, :].re("p (r i) -> p r i", r=16),
                            in0=peb[0:16, :].re("p (r i) -> p r i", r=16), in1=eb.bc(1, [16, 16, 16]), op=ALU.mult)
                        yield
                        for r in range(16):
                            mm(out=B6[0:64, r * 16:(r + 1) * 16], lhsT=V3r[:, r, vs], rhs=pp_[:, r * 16:(r + 1) * 16],
                               start=True, stop=False)
                            mm(out=B6[0:64, r * 16:(r + 1) * 16], lhsT=V3c[0:16, r, vl], rhs=ppb[0:16, r * 16:(r + 1) * 16],
                               start=False, stop=True)
                        mm(out=B6[0:64, 256:512], lhsT=cb[:, CB_ONES:CB_ONES + 64], rhs=pp_[:, 0:256], start=True, stop=False)
                        mm(out=B6[0:64, 256:512], lhsT=cb[0:16, CB_ONES:CB_ONES + 64], rhs=ppb[0:16, :], start=False, stop=True)
                        yield
                        if phaseB:
                            vec("tensor_tensor", out=accO[:, :].re("p (i r) -> p r i", r=16),
                                in0=accO[:, :].re("p (i r) -> p r i", r=16),
                                in1=B6[0:64, 0:TT].re("p (r i) -> p r i", r=16), op=ALU.add)
                            yield
                            vec("tensor_tensor", out=accD[:, :].re("p (i r) -> p r i", r=16),
                                in0=accD[:, :].re("p (i r) -> p r i", r=16),
                                in1=B6[0:64, 256:512].re("p (r i) -> p r i", r=16), op=ALU.add)
                            yield
                            vec("reciprocal", out=accD[:, :], in_=accD[:, :])
                            yield
                            vec("tensor_tensor", out=oT[:, h, :], in0=accO[:, :], in1=accD[:, :], op=ALU.mult)
                            yield
                        if h % 2 == 1:
                            S.dma("gpsimd", out=V2r[j0_2 * 64:(j0_2 + 1) * 64, :, cc * 128:(cc + 1) * 128], in_=V2c[0:64, :, :])
                            S.dma("gpsimd", out=V3r[j0_3 * 16:(j0_3 + 1) * 16, :, cc * 128:(cc + 1) * 128], in_=V3c[0:16, :, :])
                    CK('attn')
                    cpy(out=K1[:, :, 0:128], in_=K1[:, :, TT:TT + 128])
                    yield
                    cpy(out=V1[:, 0, :], in_=V1[:, 2, :])
                    yield
                    cpy(out=K2r[:, :, :, j0_2 * 64:(j0_2 + 1) * 64],
                        in_=K2c[:].re("p c (i r) -> p c r i", r=4))
                    yield
                    cpy(out=K3r[:, :, :, j0_3 * 16:(j0_3 + 1) * 16],
                        in_=K3c[:].re("p c (i r) -> p c r i", r=16))

                ag = th_attn()
                ag_done = [False]

                def step_attn():
                    if ag_done[0]:
                        return
                    try:
                        next(ag)
                    except StopIteration:
                        ag_done[0] = True

                rr_cnt = [0]
                clk = {}

                def advance(g_):
                    S.cur_fin = 0.0
                    try:
                        next(g_)
                    except StopIteration:
                        return False
                    if S.cur_fin > 0.0:
                        clk[id(g_)] = S.cur_fin
                    return True

                def run_ls(gens):
                    gens = list(gens)
                    for g_ in gens:
                        clk.setdefault(id(g_), 0.0)
                    clk.setdefault(id(ag), 0.0)
                    while gens:
                        cands = gens + ([] if ag_done[0] else [ag])
                        g_ = min(cands, key=lambda x: clk[id(x)])
                        if g_ is ag:
                            step_attn_ls()
                        elif not advance(g_):
                            gens.remove(g_)

                def step_attn_ls():
                    if not advance(ag):
                        ag_done[0] = True

                def run_rr(gens):
                    if LISTSCHED and INTERLEAVE:
                        return run_ls(gens)
                    gens = list(gens)
                    while gens:
                        for g_ in list(gens):
                            try:
                                next(g_)
                            except StopIteration:
                                gens.remove(g_)
                        rr_cnt[0] += 1
                        if INTERLEAVE and rr_cnt[0] % ATTN_EVERY == 0:
                            step_attn()

                if LISTSCHED and INTERLEAVE:
                    done = {"f1": 0, "f2": 0, "bk": 0}
                    mk = {"f1": rw_f1, "f2": rw_f2, "bk": rw_back}
                    cur = {"f1": None, "f2": None, "bk": None}
                    sclk = {"f1": 0.0, "f2": 0.0, "bk": 0.0, "at": clk.get(id(ag), 0.0), "s1": 0.0}
                    s1g = [None, False]
                    want_s1 = S1_PREFETCH and (not phaseB) and (m + 1 < nt)

                    def can_start(k, c):
                        if k == "f1":
                            return done["f2"] >= c - 1 and done["bk"] >= c - 2
                        if k == "f2":
                            return done["f1"] >= c + 1 and done["bk"] >= c - 1
                        return done["f2"] >= c + 1

                    while True:
                        cands = []
                        for k in ("f1", "f2", "bk"):
                            if cur[k] is None and done[k] < 8 and can_start(k, done[k]):
                                cur[k] = mk[k](done[k])
                            if cur[k] is not None:
                                cands.append(k)
                        if not ag_done[0]:
                            cands.append("at")
                        if want_s1 and not s1g[1] and done["f1"] == 8 and ag_done[0]:
                            if s1g[0] is None:
                                prot["s"] = [_b0, _b4, B5, B6]
                                s1g[0] = stage1_gen(m + 1)
                                s1_done.add(m + 1)
                                sclk["s1"] = max(sclk["f1"], sclk["at"])
                            cands.append("s1")
                        if not cands:
                            break
                        k = min(cands, key=lambda x: sclk[x])
                        S.cur_fin = 0.0
                        if k == "at":
                            step_attn()
                        elif k == "s1":
                            try:
                                next(s1g[0])
                            except StopIteration:
                                s1g[1] = True
                                prot["s"] = [_b0, _b4, B1, B2, B3, B6]
                        else:
                            try:
                                next(cur[k])
                            except StopIteration:
                                cur[k] = None
                                done[k] += 1
                        if S.cur_fin > 0.0:
                            sclk[k] = S.cur_fin
                    assert done == {"f1": 8, "f2": 8, "bk": 8}, done
                else:
                    for k_ in range(10):
                        gens = []
                        if k_ < 8:
                            gens.append(rw_f1(k_))
                        if 1 <= k_ <= 8:
                            gens.append(rw_f2(k_ - 1))
                        if k_ >= 2:
                            gens.append(rw_back(k_ - 2))
                        if INTERLEAVE:
                            run_rr(gens)
                        else:
                            for g_ in reversed(gens):
                                for _ in g_:
                                    pass
                while not ag_done[0]:
                    step_attn()

                if not phaseB:
                    continue
                for cc in range(8):
                    sa, sbb = ((tv(2), tv(3)), (tv(5), tv(6)))[cc % 2]
                    wpb = wload(CH_PB + cc)
                    wv = wpb[:, :].re("p (a k c) -> p a k c", a=4, c=128)
                    pz = nextp()
                    for kc in range(8):
                        mm(out=pz[:, 0:TT], lhsT=wv[:, 0, kc, :], rhs=hT[:, kc, 1:TT + 1], start=(kc == 0), stop=(kc == 7))
                    sigmoid_to(sa[:, :], pz[:, 0:TT])
                    pz = nextp()
                    for kc in range(8):
                        mm(out=pz[:, 0:TT], lhsT=wv[:, 1, kc, :], rhs=hT[:, kc, 1:TT + 1], start=(kc == 0), stop=(kc == 7))
                    sigmoid_to(sbb[:, :], pz[:, 0:TT])
                    pz = nextp()
                    for kc in range(8):
                        mm(out=pz[:, 0:TT], lhsT=wv[:, 2, kc, :], rhs=yfin[:, kc, :], start=(kc == 0), stop=(kc == 7))
                    vec("tensor_tensor", out=sa[:, :], in0=sa[:, :], in1=pz[:, 0:TT], op=ALU.mult)
                    pz = nextp()
                    for hh_ in range(8):
                        mm(out=pz[:, 0:TT], lhsT=wv[0:64, 3, hh_, :], rhs=oT[:, hh_, :], start=(hh_ == 0), stop=(hh_ == 7))
                    vec("tensor_tensor", out=sbb[:, :], in0=sbb[:, :], in1=pz[:, 0:TT], op=ALU.mult)
                    vec("tensor_tensor", out=mixT[:, cc, :], in0=sa[:, :], in1=sbb[:, :], op=ALU.add)

                def norm_residual(ps_views, gb):
                    for hf in range(2):
                        act(out=junk[:, 0:512], in_=ps_views[hf], func=AF.Square, accum_out=st4[:, 8 + hf:9 + hf])
                    vec("tensor_tensor", out=st4[:, 10:11], in0=st4[:, 8:9], in1=st4[:, 9:10], op=ALU.add)
                    rsqrt_to(st4[:, 4:5], st4[:, 10:11], eps_r, 1.0 / D)
                    for hf in range(2):
                        for qq in range(2):
                            cs_ = slice(hf * 512 + qq * 256, hf * 512 + (qq + 1) * 256)
                            vec("scalar_tensor_tensor", out=utmp[:, :], in0=ps_views[hf][:, qq * 256:(qq + 1) * 256],
                                scalar=st4[:, 4:5], in1=gb[:, cs_], op0=ALU.mult, op1=ALU.mult)
                            vec("tensor_tensor", out=xm[:, sub, cs_], in0=xm[:, sub, cs_], in1=utmp[:, :], op=ALU.add)

                wo = [wload(CH_WOUT + 0), wload(CH_WOUT + 1)]
                for sub in range(NSUB):
                    for hf in range(2):
                        wv = wo[hf][:, :].re("p (k c) -> p k c", c=512)
                        for kc in range(8):
                            mm(out=(B1, B2)[hf][:, :], lhsT=mixT[:, kc, sub * 128:(sub + 1) * 128], rhs=wv[:, kc, :],
                               start=(kc == 0), stop=(kc == 7))
                    norm_residual([B1[:, :], B2[:, :]], gmb)
                rms_rstd(xm, 2)
                norm_transpose(xm, 2, h2T, PD_A2, lambda kc: modf[:, 24 + kc:25 + kc], 0)
                accs = [[B1[:, :], B2[:, :]], [B3[:, :], B5[:, :]]]
                for pg in range(3):
                    nk = 8 if pg < 2 else 6
                    for i4 in range(nk // 2):
                        i = pg * 4 + i4
                        wf_ = wload(CH_FF + i)
                        wv = wf_[:, :].re("p (k c) -> p k c", c=512)
                        for jj in range(2):
                            jl = i4 * 2 + jj
                            pg_ = nextp("f")
                            for kc in range(8):
                                mm(out=pg_[:, 0:TT], lhsT=wv[:, kc, jj * 128:(jj + 1) * 128], rhs=h2T[:, kc, :],
                                   start=(kc == 0), stop=(kc == 7))
                            act(out=sg[:, :], in_=pg_[:, 0:TT], func=AF.Silu)
                            pu = nextp("f")
                            for kc in range(8):
                                mm(out=pu[:, 0:TT], lhsT=wv[:, kc, 256 + jj * 128:256 + (jj + 1) * 128], rhs=h2T[:, kc, :],
                                   start=(kc == 0), stop=(kc == 7))
                            vec("tensor_tensor", out=actT[:, jl, :], in0=sg[:, :], in1=pu[:, 0:TT], op=ALU.mult)
                    for hf in range(2):
                        wf_ = wload(CH_FO + pg * 2 + hf)
                        wv = wf_[:, :].re("p (k c) -> p k c", c=512)
                        for sub in range(NSUB):
                            for kc in range(nk):
                                mm(out=accs[sub][hf], lhsT=actT[:, kc, sub * 128:(sub + 1) * 128], rhs=wv[:, kc, :],
                                   start=(pg == 0 and kc == 0), stop=(pg == 2 and kc == nk - 1))
                for sub in range(NSUB):
                    norm_residual(accs[sub], gfb)
                r0 = (m - PB0) * TT
                S.dma("gpsimd", out=y_d.v(y_d.t[r0:r0 + TT, :].rearrange("(s p) c -> p s c", p=128)), in_=xm[:])


        except _Stop:
            pass
        S.finish([y_d] + finals)
        S.emit()
    return nc


_CACHE = {}


def prep_inputs(x, c, w_mod, b_mod, g_pre_mix, g_post_mix, g_pre_ffn, g_post_ffn, w_in, mu_rkv, mu_lora,
           w0, w1, w2, a0, a1, a2, g1, g2, k_k, k_a, r_k, ln_x_w, ln_x_b, w_o_rwkv, w_o_attn, w_out,
           w_ffn_in, w_ffn_out):
    f = lambda a: np.asarray(a, np.float32)
    x = f(x); c = f(c)
    w_in = f(w_in)[0]; w_modm = f(w_mod)[0]
    bm = f(b_mod)[0].reshape(6, 1024)
    vecs = [bm[0], bm[1], bm[2], bm[3], bm[4], bm[5], f(g_pre_mix)[0], f(g_post_mix)[0], f(g_pre_ffn)[0],
            f(g_post_ffn)[0], f(mu_rkv)[0, 0], f(mu_rkv)[0, 1], f(mu_rkv)[0, 2], f(mu_lora)[0, 0], f(mu_lora)[0, 1],
            f(mu_lora)[0, 2], f(w0)[0], f(a0)[0], f(k_k)[0], f(k_a)[0], f(r_k)[0].reshape(-1), f(ln_x_w)[0],
            f(ln_x_b)[0]]
    wsrc = np.zeros((NCH, 128, 4096), np.float32)
    def put(i, arr3):
        P, K, C = arr3.shape
        v = wsrc[i].reshape(128, -1)
        tmp = np.zeros((128, K, 4096 // K if K in (8,) else C), np.float32) if False else None
        blk = np.zeros((128, K * C), np.float32)
        blk[:P] = arr3.reshape(P, K * C)
        v[:, :K * C] = blk
    for cch in range(8):
        a = np.zeros((128, 8, 512), np.float32)
        for j in range(3):
            a[:, :, j * 128:(j + 1) * 128] = _wchunk(w_in, slice(j * 1024 + cch * 128, j * 1024 + (cch + 1) * 128))
        put(CH_RW + cch, a)
    for g in range(3):
        for j in range(3):
            o = 3072 + j * 1536 + g * 512
            put(CH_AT + g * 3 + j, _wchunk(w_in, slice(o, o + 512)))
    wor = f(w_o_rwkv)[0]; woa = f(w_o_attn)[0]; wout = f(w_out)[0]
    for cc in range(8):
        cs_ = slice(cc * 128, (cc + 1) * 128)
        a = np.zeros((128, 4, 8, 128), np.float32)
        a[:, 0] = _wchunk(w_in, slice(7680 + cc * 128, 7680 + (cc + 1) * 128))
        a[:, 1] = _wchunk(w_in, slice(8704 + cc * 128, 8704 + (cc + 1) * 128))
        a[:, 2] = _wchunk(wor, cs_)
        a[0:64, 3] = woa[:, cs_].reshape(8, 64, 128).transpose(1, 0, 2)
        put(CH_PB + cc, a.reshape(128, 32, 128))
    for hf in range(2):
        put(CH_WOUT + hf, _wchunk(wout, slice(hf * 512, (hf + 1) * 512)))
    wfi = f(w_ffn_in)[0]; wfo = f(w_ffn_out)[0]
    for i in range(11):
        a = np.zeros((128, 8, 512), np.float32)
        a[:, :, 0:256] = _wchunk(wfi, slice(i * 256, (i + 1) * 256))
        a[:, :, 256:512] = _wchunk(wfi, slice(FH + i * 256, FH + (i + 1) * 256))
        put(CH_FF + i, a)
    for pg in range(3):
        nk = 8 if pg < 2 else 6
        for hf in range(2):
            blk = wfo[pg * 1024:pg * 1024 + nk * 128, hf * 512:(hf + 1) * 512]
            put(CH_FO + pg * 2 + hf, blk.reshape(nk, 128, 512).transpose(1, 0, 2))
    wmod = np.ascontiguousarray(
        w_modm.reshape(8, 128, 24, 256).transpose(2, 1, 0, 3).reshape(24, 128, 2048))
    l1 = np.concatenate([f(w1)[0], f(a1)[0], f(g1)[0]], 1)
    l1 = np.ascontiguousarray(l1.reshape(8, 128, 288).transpose(1, 0, 2).reshape(128, 8 * 288))
    l2 = np.zeros((128, 3, 1024), np.float32)
    l2[0:64, 0] = f(w2)[0]; l2[64:128, 0] = f(a2)[0]
    l2[:, 1] = f(g2)[0][0:128]; l2[0:32, 2] = f(g2)[0][128:160]
    l2 = l2.reshape(128, 3072)
    cbt = _host_consts()
    in_maps = []
    for core in range(8):
        b, hh = core // 2, core % 2
        pfm = np.concatenate([_fm(v) for v in vecs] + [_fm(c[b])], 1)
        if hh == 1:
            xvv = x[b]
        else:
            xvv = np.concatenate([np.zeros((T // 2, D), np.float32), x[b, :T // 2]], 0)
        in_maps.append({"xv": np.ascontiguousarray(xvv), "pfm": np.ascontiguousarray(pfm), "wmod": wmod,
                        "wsrc": wsrc, "l1": l1, "l2": l2, "cbt": cbt, "cft": _host_cf(hh)})
    return in_maps


def kernel(**inputs):
    in_maps = prep_inputs(**inputs)
    if "nc" not in _CACHE:
        _CACHE["nc"] = build()
    nc = _CACHE["nc"]
    res = run_bass_kernel_spmd(nc, in_maps, core_ids=list(range(8)))
    out = np.zeros((4, T, D), np.float32)
    for core in range(8):
        b, hh = core // 2, core % 2
        out[b, hh * (T // 2):(hh + 1) * (T // 2)] = res.results[core]["y"]
    return out
```
